# Optimizing a Trainium2 kernel written in Bass

```python
import jax, jax.numpy as jnp
from jax import lax
import numpy as np

D_MODEL = 1024
BATCH = 8
SEQ = 2048
DEPTH = 2

MEM_LEN = 256
GROUP_WIDTH = D_MODEL // 2
D_MIX = 3 * GROUP_WIDTH
MOBA_HEAD_DIM = 64
MOBA_HEADS = GROUP_WIDTH // MOBA_HEAD_DIM
MOBA_BLOCK = 256
MOBA_TOPK = 3
MOBA_Q_CHUNK = 16
HGRN_HEAD_DIM = 128
HGRN_HEADS = GROUP_WIDTH // HGRN_HEAD_DIM
HGRN_CHUNK = 64
MEM_HEAD_DIM = 128
MEM_HEADS = GROUP_WIDTH // MEM_HEAD_DIM
ROPE_THETA = 500000.0
ROPE_DIM = MOBA_HEAD_DIM // 4
NORM_EPS = 1e-6
IN_COLS = 3 * GROUP_WIDTH + 3 * GROUP_WIDTH + GROUP_WIDTH + D_MIX

kernel_name = "hymba_moba_hgrn2_memxattn_block"


def rms_norm(x, g):
    xf = x.astype(jnp.float32)
    y = xf * lax.rsqrt(jnp.mean(xf * xf, axis=-1, keepdims=True) + NORM_EPS)
    return (y * g.astype(jnp.float32)).astype(x.dtype)


def partial_rope(x, positions):
    half = ROPE_DIM // 2
    inv_freq = ROPE_THETA ** (-jnp.arange(half, dtype=jnp.float32) / half)
    ang = positions.astype(jnp.float32)[..., None] * inv_freq
    cos = jnp.cos(ang)[:, :, None, :]
    sin = jnp.sin(ang)[:, :, None, :]
    xr = x[..., :ROPE_DIM].astype(jnp.float32)
    x1, x2 = xr[..., :half], xr[..., half:]
    rot = jnp.concatenate([x1 * cos - x2 * sin, x2 * cos + x1 * sin], axis=-1).astype(x.dtype)
    return jnp.concatenate([rot, x[..., ROPE_DIM:]], axis=-1)


def moba_attention(q, k, v):
    B, S, H, D = q.shape
    BLK, QC = MOBA_BLOCK, MOBA_Q_CHUNK
    nb = -(-S // BLK)
    pad = nb * BLK - S
    topk = min(MOBA_TOPK, nb)
    kbh = jnp.pad(k, ((0, 0), (0, pad), (0, 0), (0, 0))).reshape(B, nb, BLK, H, D).transpose(0, 3, 1, 2, 4)
    vbh = jnp.pad(v, ((0, 0), (0, pad), (0, 0), (0, 0))).reshape(B, nb, BLK, H, D).transpose(0, 3, 1, 2, 4)
    kmean = jnp.mean(kbh.astype(jnp.float32), axis=3)
    scale = D ** -0.5
    nqc = S // QC
    qch = q.reshape(B, nqc, QC, H, D).transpose(1, 0, 3, 2, 4)
    bi = jnp.arange(B)[:, None, None, None]
    hi = jnp.arange(H)[None, :, None, None]
    blk_ids = jnp.arange(nb)

    def one_chunk(args):
        ci, qi = args
        start = ci * QC
        own = start // BLK
        qpos = start + jnp.arange(QC)
        gate = jnp.einsum('bhqd,bhnd->bhqn', qi.astype(jnp.float32), kmean)
        gate = jnp.where(blk_ids < own, gate, -jnp.inf)
        _, sel = lax.top_k(gate, topk)
        sel_ok = sel < own
        k_sel = kbh[bi, hi, sel]
        v_sel = vbh[bi, hi, sel]
        s_sel = jnp.einsum('bhqd,bhqkld->bhqkl', qi, k_sel).astype(jnp.float32) * scale
        s_sel = jnp.where(sel_ok[..., None], s_sel, -jnp.inf).reshape(B, H, QC, topk * BLK)
        k_own = lax.dynamic_index_in_dim(kbh, own, axis=2, keepdims=False)
        v_own = lax.dynamic_index_in_dim(vbh, own, axis=2, keepdims=False)
        kpos = own * BLK + jnp.arange(BLK)
        s_own = jnp.einsum('bhqd,bhld->bhql', qi, k_own).astype(jnp.float32) * scale
        s_own = jnp.where(kpos[None, :] <= qpos[:, None], s_own, -jnp.inf)
        p = jax.nn.softmax(jnp.concatenate([s_sel, s_own], axis=-1), axis=-1).astype(v.dtype)
        p_sel = p[..., :topk * BLK].reshape(B, H, QC, topk, BLK)
        p_own = p[..., topk * BLK:]
        return (jnp.einsum('bhqkl,bhqkld->bhqd', p_sel, v_sel)
                + jnp.einsum('bhql,bhld->bhqd', p_own, v_own))

    o = lax.map(one_chunk, (jnp.arange(nqc), qch))
    return o.transpose(1, 0, 3, 2, 4).reshape(B, S, H, D)


def hgrn2_chunkwise(q, f_logit, i, lb):
    B, S, H, DK = q.shape
    DV = i.shape[-1]
    C = HGRN_CHUNK
    NC = S // C
    fl = f_logit.astype(jnp.float32)
    qf = jax.nn.silu(q.astype(jnp.float32))
    log_f = jnp.logaddexp(jnp.log(lb), jnp.log1p(-lb) + jax.nn.log_sigmoid(fl))
    kf = (1.0 - lb) * jax.nn.sigmoid(-fl)
    vf = i.astype(jnp.float32)

    def to_chunks(t):
        return t.reshape(B, NC, C, H, t.shape[-1]).transpose(1, 0, 3, 2, 4)

    causal = jnp.tril(jnp.ones((C, C), dtype=bool))

    def step(state, inp):
        qc, lfc, kc, vc = inp
        A = jnp.cumsum(lfc, axis=2)
        diff = A[:, :, :, None, :] - A[:, :, None, :, :]
        decay = jnp.exp(jnp.where(causal[:, :, None], diff, -jnp.inf))
        scores = jnp.einsum('bhtd,bhtsd,bhsd->bhts', qc, decay, kc)
        o = (jnp.einsum('bhts,bhsv->bhtv', scores, vc)
             + jnp.einsum('bhtd,bhdv->bhtv', qc * jnp.exp(A), state))
        A_end = A[:, :, -1:, :]
        state = (jnp.exp(A_end[:, :, 0, :])[..., None] * state
                 + jnp.einsum('bhsd,bhsv->bhdv', kc * jnp.exp(A_end - A), vc))
        return state, o

    s0 = jnp.zeros((B, H, DK, DV), jnp.float32)
    _, o = lax.scan(step, s0, (to_chunks(qf), to_chunks(log_f), to_chunks(kf), to_chunks(vf)))
    return o.transpose(1, 0, 3, 2, 4).reshape(B, S, H, DV)


def memory_attention(q, k, v):
    scale = q.shape[-1] ** -0.5
    s = jnp.einsum('bshd,bmhd->bhsm', q, k).astype(jnp.float32) * scale
    p = jax.nn.softmax(s, axis=-1).astype(v.dtype)
    return jnp.einsum('bhsm,bmhd->bshd', p, v)


def setup_inputs(seed: int = 0) -> dict:
    key = jax.random.key(seed)
    ks = jax.random.split(key, 14)
    f32 = jnp.float32

    def gain(k, shape):
        return 1.0 + 0.02 * jax.random.normal(k, shape, f32)

    return {
        "x": jax.random.normal(ks[0], (BATCH, SEQ, D_MODEL), f32),
        "mem": jax.random.normal(ks[1], (BATCH, MEM_LEN, D_MODEL), f32),
        "positions": jnp.tile(jnp.arange(SEQ, dtype=jnp.int32)[None, :], (BATCH, 1)),
        "norm_g": gain(ks[2], (DEPTH, D_MODEL)),
        "w_in": jax.random.normal(ks[3], (DEPTH, D_MODEL, IN_COLS), f32) * D_MODEL ** -0.5,
        "w_out": jax.random.normal(ks[4], (DEPTH, D_MIX, D_MODEL), f32) * D_MIX ** -0.5,
        "moba_q_norm": gain(ks[5], (DEPTH, MOBA_HEAD_DIM)),
        "moba_k_norm": gain(ks[6], (DEPTH, MOBA_HEAD_DIM)),
        "hgrn_lb_logits": 0.5 * jax.random.normal(ks[7], (DEPTH, GROUP_WIDTH), f32),
        "hgrn_o_norm": gain(ks[8], (DEPTH, HGRN_HEAD_DIM)),
        "mem_norm_g": gain(ks[9], (DEPTH, D_MODEL)),
        "w_mem_kv": jax.random.normal(ks[10], (DEPTH, D_MODEL, 2 * GROUP_WIDTH), f32) * D_MODEL ** -0.5,
        "mem_q_norm": gain(ks[11], (DEPTH, MEM_HEAD_DIM)),
        "mem_k_norm": gain(ks[12], (DEPTH, MEM_HEAD_DIM)),
    }


def reference(x, mem, positions, norm_g, w_in, w_out, moba_q_norm, moba_k_norm, hgrn_lb_logits,
              hgrn_o_norm, mem_norm_g, w_mem_kv, mem_q_norm, mem_k_norm):
    B, S, _ = x.shape
    M = mem.shape[1]
    G = GROUP_WIDTH
    lb_all = jnp.cumsum(jax.nn.softmax(hgrn_lb_logits.astype(jnp.float32), axis=0), axis=0)
    lb_all = lb_all - lb_all[0:1]
    split_points = [G * n for n in range(1, 8)]
    for l in range(DEPTH):
        h = rms_norm(x, norm_g[l])
        proj = h @ w_in[l]
        q_a, k_a, v_a, q_h, f_h, i_h, q_m, z = jnp.split(proj, split_points, axis=-1)

        qa = partial_rope(rms_norm(q_a.reshape(B, S, MOBA_HEADS, MOBA_HEAD_DIM), moba_q_norm[l]), positions)
        ka = partial_rope(rms_norm(k_a.reshape(B, S, MOBA_HEADS, MOBA_HEAD_DIM), moba_k_norm[l]), positions)
        va = v_a.reshape(B, S, MOBA_HEADS, MOBA_HEAD_DIM)
        o_a = moba_attention(qa, ka, va).reshape(B, S, G)

        o_h = hgrn2_chunkwise(q_h.reshape(B, S, HGRN_HEADS, HGRN_HEAD_DIM),
                              f_h.reshape(B, S, HGRN_HEADS, HGRN_HEAD_DIM),
                              i_h.reshape(B, S, HGRN_HEADS, HGRN_HEAD_DIM),
                              lb_all[l].reshape(HGRN_HEADS, HGRN_HEAD_DIM))
        o_h = rms_norm(o_h, hgrn_o_norm[l]).astype(x.dtype).reshape(B, S, G)

        kv_m = rms_norm(mem, mem_norm_g[l]) @ w_mem_kv[l]
        k_m, v_m = jnp.split(kv_m, 2, axis=-1)
        km = rms_norm(k_m.reshape(B, M, MEM_HEADS, MEM_HEAD_DIM), mem_k_norm[l])
        vm = v_m.reshape(B, M, MEM_HEADS, MEM_HEAD_DIM)
        qm = rms_norm(q_m.reshape(B, S, MEM_HEADS, MEM_HEAD_DIM), mem_q_norm[l])
        o_m = memory_attention(qm, km, vm).reshape(B, S, G)

        y = jnp.concatenate([o_a, o_h, o_m], axis=-1) * jax.nn.silu(z)
        x = x + y @ w_out[l]
    return x
```

```python
import numpy as np
from contextlib import ExitStack
import concourse.bass as bass
import concourse.mybir as mybir
from concourse.bass_utils import run_bass_kernel_spmd

F32 = mybir.dt.float32
BF16 = mybir.dt.bfloat16
I32 = mybir.dt.int32
AF = mybir.ActivationFunctionType
ALU = mybir.AluOpType
AX = mybir.AxisListType

S = 2048
D = 1024
NT = 16
EPS = 1e-6
NCST = 576
import os as _os0
ALL_GROUPS = tuple(_os0.environ.get("ORDER", "A0,A1,H0,H1,M0,M1").split(","))
import os as _os
SCHED = _os.environ.get("SCHED", "1") == "1"
PAR1 = int(_os.environ.get("PAR1", "1"))
PAR2 = int(_os.environ.get("PAR2", "1"))
GPAR = int(_os.environ.get("GPAR", "1"))
PS3 = int(_os.environ.get("PS3", "3"))
MASKV = int(_os.environ.get("MASKV", "1"))
OPB = int(_os.environ.get("OPB", "1"))
SCHED2 = int(_os.environ.get("SCHED2", "1"))
SDELTA = float(_os.environ.get("SDELTA", "100"))
LATX = float(_os.environ.get("LATX", "180"))
PEK = float(_os.environ.get("PEK", "0.65"))
ACTK = float(_os.environ.get("ACTK", "1.0"))
DVEK = float(_os.environ.get("DVEK", "1.0"))
POOLK = float(_os.environ.get("POOLK", "1.0"))
LATS = float(_os.environ.get("LATS", "60"))
XQ = int(_os.environ.get("XQ", "0"))
GOFF = int(_os.environ.get("GOFF", "0"))


class Res:
    __slots__ = ("name", "w", "r", "excl")

    def __init__(self, name, excl=False):
        self.name = name
        self.w = None
        self.r = []
        self.excl = excl


class _Rec:
    def __init__(self):
        self.name = None
        self.args = ()
        self.kw = {}

    def __getattr__(self, name):
        def f(*a, **k):
            self.name, self.args, self.kw = name, a, k
            return self
        return f


def _free_size(ap):
    n = 1
    for d in list(ap.shape)[1:]:
        n *= int(d)
    return n


class Op:
    __slots__ = ("eng", "fn", "deps", "sdeps", "sig", "idx", "dma", "sem", "val", "i", "cost", "start")

    def __init__(self, eng, fn, deps, sdeps, dma):
        self.eng = eng
        self.fn = fn
        self.deps = deps
        self.sdeps = sdeps
        self.sig = False
        self.idx = 0
        self.dma = dma
        self.sem = None
        self.val = 0
        self.i = 0
        self.start = 0.0
        rec = _Rec()
        fn(rec)
        out = rec.kw.get("out", rec.args[0] if rec.args else None)
        n = _free_size(out) if out is not None else 64
        if dma:
            c = 2000.0 + n * int(out.shape[0]) * 4 / 120.0
        elif eng == "tensor":
            if rec.name == "transpose":
                c = 110.0
            else:
                lhsT = rec.kw.get("lhsT")
                f32 = lhsT is not None and lhsT.dtype == F32
                c = PEK * (64.0 + max(n, 64) / 2.0) * (4.0 if f32 else 1.0)
        elif eng == "scalar":
            c = ACTK * (200.0 + n / 1.2)
        elif eng == "vector":
            c = DVEK * (120.0 + n / 0.96 * (8.0 if rec.name == "reciprocal" else 1.0))
        else:
            c = POOLK * (300.0 + n / 0.5)
        self.cost = c


class Prog:
    ENGS = ["tensor", "vector", "scalar", "gpsimd", "sync"]

    def __init__(self, nc):
        self.nc = nc
        self.ops = []

    phase = None
    tok = None
    tokset = ()

    def op(self, eng, fn, reads=(), writes=(), dma=False):
        if self.phase == "gate" and self.tok is not None:
            writes = list(writes) + [self.tok]
        elif self.phase == "chain" and eng in self.tokset:
            reads = list(reads) + [self.tok]
        deps, sdeps = {}, {}

        def add(d):
            if d.dma or dma or d.eng != eng or eng != "tensor":
                deps[id(d)] = d
            else:
                sdeps[id(d)] = d
        for r in reads:
            if r.w is not None:
                add(r.w)
            if r.excl:
                for d in r.r:
                    if d.eng != eng:
                        add(d)
        for w in writes:
            if w.w is not None:
                add(w.w)
            for d in w.r:
                add(d)
        o = Op(eng, fn, list(deps.values()), list(sdeps.values()), dma)
        for r in reads:
            r.r.append(o)
        for w in writes:
            w.w = o
            w.r = []
        self.ops.append(o)
        return o

    def schedule(self):
        import heapq
        ops = self.ops
        for i, o in enumerate(ops):
            o.i = i
        succs = [[] for _ in ops]
        npred = [0] * len(ops)
        for o in ops:
            ds = o.deps + o.sdeps
            npred[o.i] = len(ds)
            for d in ds:
                succs[d.i].append(o)
        ready = [0.0] * len(ops)
        free = {e: 0.0 for e in self.ENGS}
        heap = [(0.0, o.i) for o in ops if npred[o.i] == 0]
        heapq.heapify(heap)
        done = 0
        while heap:
            t, i = heapq.heappop(heap)
            o = ops[i]
            st = max(ready[i], free[o.eng])
            if st > t + 1e-9:
                heapq.heappush(heap, (st, i))
                continue
            o.start = st
            if o.dma:
                free[o.eng] = st + 150.0
            else:
                free[o.eng] = st + o.cost
            fin = st + o.cost
            done += 1
            for sc in succs[i]:
                lat = 60.0 if (sc.eng == o.eng and not o.dma) else 180.0
                if fin + lat > ready[sc.i]:
                    ready[sc.i] = fin + lat
                npred[sc.i] -= 1
                if npred[sc.i] == 0:
                    heapq.heappush(heap, (max(ready[sc.i], free[sc.eng]), sc.i))
        assert done == len(ops), (done, len(ops))
        self.ops = sorted(ops, key=lambda o: (o.start, o.i))
        self.est_ns = max(o.start + o.cost for o in ops)

    def schedule2(self, delta=120.0):
        ops = self.ops
        n = len(ops)
        for i, o in enumerate(ops):
            o.i = i
        succs = [[] for _ in ops]
        npred = [0] * n
        for o in ops:
            ds = o.deps + o.sdeps
            npred[o.i] = len(ds)
            for d in ds:
                succs[d.i].append(o)
        blev = [0.0] * n
        for o in reversed(ops):
            b = 0.0
            for sc in succs[o.i]:
                lat = LATS if (sc.eng == o.eng and not o.dma) else LATX
                v = lat + blev[sc.i]
                if v > b:
                    b = v
            blev[o.i] = b + o.cost
        ready = [0.0] * n
        free = {e: 0.0 for e in self.ENGS}
        rsets = {e: [] for e in self.ENGS}
        for o in ops:
            if npred[o.i] == 0:
                rsets[o.eng].append(o.i)
        done = 0
        while done < n:
            best_e, best_t = None, 1e30
            for e in self.ENGS:
                rs = rsets[e]
                if not rs:
                    continue
                t = min(ready[i] for i in rs)
                if t < free[e]:
                    t = free[e]
                if t < best_t:
                    best_t, best_e = t, e
            e = best_e
            rs = rsets[e]
            lim = best_t + delta
            pick, pb = -1, -1.0
            for i in rs:
                if ready[i] <= lim and blev[i] > pb:
                    pb, pick = blev[i], i
            rs.remove(pick)
            o = ops[pick]
            st = max(ready[pick], free[e])
            o.start = st
            free[e] = st + (150.0 if o.dma else o.cost)
            fin = st + o.cost
            done += 1
            for sc in succs[pick]:
                lat = LATS if (sc.eng == o.eng and not o.dma) else LATX
                if fin + lat > ready[sc.i]:
                    ready[sc.i] = fin + lat
                npred[sc.i] -= 1
                if npred[sc.i] == 0:
                    rsets[sc.eng].append(sc.i)
        self.ops = sorted(ops, key=lambda o: (o.start, o.i))
        self.est_ns = max(o.start + o.cost for o in ops)

    def emit(self, stack, final_deps, ndma_sems=8):
        nc = self.nc
        for o in self.ops:
            for d in o.deps:
                d.sig = True
        for d in final_deps:
            d.sig = True
        sems = {e: stack.enter_context(nc.semaphore("s_" + e)) for e in self.ENGS}
        cnt = {e: 0 for e in self.ENGS}
        pools, pool_i, pre_wait = {}, {}, {}
        for o in self.ops:
            if o.dma:
                if o.eng not in pools:
                    pools[o.eng] = [[stack.enter_context(nc.semaphore("d_%s_%d" % (o.eng, i))), 0]
                                    for i in range(ndma_sems)]
                    pool_i[o.eng] = 0
                p = pools[o.eng][pool_i[o.eng] % ndma_sems]
                pool_i[o.eng] += 1
                if p[1] > 0:
                    pre_wait[id(o)] = (p[0], p[1])
                p[1] += 16
                o.sem = p[0]
                o.val = p[1]
            elif o.sig:
                cnt[o.eng] += 1
                o.idx = cnt[o.eng]
        per = {e: [o for o in self.ops if o.eng == e] for e in self.ENGS}
        self.stats = {e: len(per[e]) for e in self.ENGS}
        block = stack.enter_context(nc.Block())

        def mk(e):
            def body(engobj):
                known = {}

                def wait(sem, val):
                    k = id(sem)
                    if known.get(k, 0) < val:
                        engobj.wait_ge(sem, val)
                        known[k] = val

                for o in per[e]:
                    for d in o.deps:
                        if d.dma:
                            wait(d.sem, d.val)
                        else:
                            wait(sems[d.eng], d.idx)
                    if o.dma:
                        pw = pre_wait.get(id(o))
                        if pw:
                            wait(pw[0], pw[1])
                        o.fn(engobj).then_inc(o.sem, 16)
                    else:
                        ins = o.fn(engobj)
                        if o.sig:
                            ins.then_inc(sems[e], 1)
                if e == "sync":
                    for d in final_deps:
                        if d.dma:
                            wait(d.sem, d.val)
                        else:
                            wait(sems[d.eng], d.idx)
            return body

        block.tensor(mk("tensor"))
        block.vector(mk("vector"))
        block.scalar(mk("scalar"))
        block.gpsimd(mk("gpsimd"))
        block.sync(mk("sync"))


def make_consts():
    c = np.zeros((128, NCST), np.float32)
    i = np.arange(128)
    c[:, 0:128] = np.eye(128)
    c[:, 128:256] = (i[None, :] >= i[:, None])
    same = (i[:, None] // 64) == (i[None, :] // 64)
    c[:, 256:384] = same & (i[:, None] <= i[None, :])
    c[:, 384:512] = same & (i[:, None] > i[None, :])
    c[:, 512] = i < 64
    c[:, 513] = i >= 64
    c[:, 514] = 1.0
    f64 = 500000.0 ** (-np.arange(8, dtype=np.float64) / 8.0)
    f = f64.astype(np.float32)
    flo = (f64 - f.astype(np.float64)).astype(np.float32)
    c[:, 515:523] = f[None, :]
    c[:, 523:531] = f[None, :]
    c[:, 547:555] = flo[None, :]
    c[:, 555:563] = flo[None, :]
    c[:, 531:539] = 0.0
    c[:, 539:547] = np.pi / 2
    return c


def build_nc(layers=(0, 1), groups=ALL_GROUPS):
    nc = bass.Bass("TRN2", target_bir_lowering=False)

    def din(name, shape, d=F32):
        return nc.dram_tensor(name, shape, d, kind="ExternalInput").ap()

    x_d = din("x", [S, D])
    mem_d = din("mem", [256, D])
    pos_d = din("pos", [16, 128], I32)
    ng_d = din("norm_g", [2, D])
    win_d = din("w_in", [2, D, 5120])
    wout_d = din("w_out", [2, 1536, D])
    gq_d = din("moba_q_norm", [2, 64])
    gk_d = din("moba_k_norm", [2, 64])
    lbl_d = din("hgrn_lb_logits", [2, 512])
    go_d = din("hgrn_o_norm", [2, 128])
    mng_d = din("mem_norm_g", [2, D])
    wkv_d = din("w_mem_kv", [2, D, 1024])
    gmq_d = din("mem_q_norm", [2, 128])
    gmk_d = din("mem_k_norm", [2, 128])
    cst_d = din("cst", [128, NCST])
    out_d = nc.dram_tensor("out", [S, D], F32, kind="ExternalOutput").ap()

    P = Prog(nc)
    if _os.environ.get("TOKR"):
        P.tok = Res("tok")
        P.tokset = tuple(_os.environ["TOKR"].split(","))
    with ExitStack() as st:
        def sb(name, shape, dt=F32):
            return st.enter_context(nc.sbuf_tensor("sb_" + name, shape, dt))

        def ps(name, shape, dt=F32):
            return st.enter_context(nc.psum_tensor("pp_" + name, shape, dt))

        def T(fn, r=(), w=()):
            return P.op("tensor", fn, r, w)

        def V(fn, r=(), w=()):
            return P.op("vector", fn, r, w)

        def A(fn, r=(), w=()):
            return P.op("scalar", fn, r, w)

        def G(fn, r=(), w=()):
            return P.op("gpsimd", fn, r, w)

        def DMA(q, fn, r=(), w=()):
            return P.op(q, fn, r, w, dma=True)

        x_tok = sb("x_tok", [128, NT, D]); rx = [Res("x%d" % i) for i in range(NT)]
        hT = sb("hT", [128, 8, S], BF16); rhT = [Res("hT%d" % i) for i in range(NT)]
        cst = sb("cst", [128, NCST]); rcst = Res("cst")
        idb = sb("idb", [128, 128], BF16); ridb = Res("idb")
        trib = sb("trib", [128, 128], BF16); rtrib = Res("trib")
        hgmb = sb("hgmb", [128, 128], BF16); rhgmb = Res("hgmb")
        gq_bc = sb("gq_bc", [128, 64]); gk_bc = sb("gk_bc", [128, 64]); go_bc = sb("go_bc", [128, 128])
        gmq_bc = sb("gmq_bc", [128, 128]); gmk_bc = sb("gmk_bc", [128, 128]); rgains = Res("gains")
        cs = sb("cs", [128, NT, 16]); sn = sb("sn", [128, NT, 16]); rrope = Res("rope")
        wb = [sb("wb%d" % i, [128, 8, 512], BF16) for i in range(3)]; rwb = [[Res("wb%da" % i), Res("wb%db" % i)] for i in range(3)]
        wo = sb("wo", [128, 2, D], BF16); rwo = Res("wo")
        Fp = [sb("F%d" % i, [128, 256]) for i in range(9)]; rF = [Res("F%d" % i) for i in range(9)]
        Bp = [sb("B%d" % i, [128, 256], BF16) for i in range(7)]; rB = [Res("B%d" % i) for i in range(7)]
        szs = [sb("sz%d" % k, [128, 4, 256]) for k in range(2)]; rszs = [[Res("sz%d_%d" % (k, i)) for i in range(4)] for k in range(2)]
        y_tok = sb("y_tok", [128, 4, 256], BF16); ry = [Res("y%d" % i) for i in range(4)]
        yT = [sb("yT%d" % i, [128, 256], BF16) for i in range(2)]; ryT = [Res("yT%d" % i) for i in range(2)]
        pT = [sb("pT%d" % i, [128, 512], BF16) for i in range(4)]; rpT = [Res("pT%d" % i) for i in range(4)]
        hbs = [sb("hb%d" % i, [128, D], BF16) for i in range(2)]; rhbs = [Res("hb0"), Res("hb1")]
        small = sb("small", [128, 128]); rsm = {}

        def sm(name, a, n):
            rsm[name] = Res("sm_" + name)
            return small[:, a:a + n], rsm[name]
        ss16, r_ss16 = sm("ss16", 0, 16)
        t16, r_t16 = sm("t16", 16, 16)
        rstd16, r_rstd16 = sm("rstd16", 32, 16)
        ss4, r_ss4 = sm("ss4", 48, 4)
        t4, r_t4 = sm("t4", 52, 4)
        rs4, r_rs4 = sm("rs4", 56, 4)
        rden, r_rden = sm("rden", 60, 4)
        dec4, r_dec4 = sm("dec4", 64, 4)
        sso, r_sso = sm("sso", 68, 2)
        to2, r_to2 = sm("to2", 70, 2)
        rso, r_rso = sm("rso", 72, 2)
        gm = sb("gm", [128, 4, 8]); rgm = Res("gm")
        top8 = sb("top8", [128, 4, 8]); rtop8 = Res("top8")
        selb = sb("selb", [128, 4, 8]); rselb = Res("selb")
        kT = sb("kT", [128, 4, S], BF16); rkT = [Res("kT%d" % i) for i in range(NT)]
        v_flat = sb("v_aug", [128, NT * 4 * 65], BF16); rv = [Res("v%d" % i) for i in range(NT)]
        v_aug = v_flat[:].rearrange("p (a b c) -> p a b c", a=NT, b=4)
        stage = v_flat[:, 0:2048].bitcast(F32); rstage = Res("stage")
        g_bcv = v_flat[:, 2048:4096].bitcast(F32); rg_bc = Res("g_bc")
        qTs = [sb("qT%d" % i, [128, 2048], BF16) for i in range(2)]; rqTs = [Res("qT0"), Res("qT1")]
        lbv = qTs[1][:, 0:2048].bitcast(F32); rlb = rqTs[1]
        lb_g = lbv[:, 0:256]; oml_g = lbv[:, 256:512]
        k_aug = sb("k_aug", [128, 4, 72], BF16); rk_aug = Res("k_aug")
        q_aug = sb("q_aug", [128, 4, 72], BF16); rq_aug = Res("q_aug")
        kmT32 = sb("kmT32", [128, 2, 2, 8]); rkm = Res("kmT32")
        S32 = sb("S32", [128, 2, 128]); rS32 = [Res("S32_0"), Res("S32_1")]
        Sbf = sb("Sbf", [128, 2, 2, 128], BF16); rSbf = [[Res("Sbf00"), Res("Sbf01")], [Res("Sbf10"), Res("Sbf11")]]
        qTA = sb("qTA", [128, 2, 128], BF16); qTB = sb("qTB", [128, 2, 128], BF16)
        rqTA = Res("qTA"); rqTB = Res("qTB")
        memT = sb("memT", [128, 8, 256], BF16); rmemT = Res("memT")
        kmT = sb("kmT", [128, 2, 256], BF16); rkmT = Res("kmT")
        vm_aug = sb("vm_aug", [128, 2, 2, 129], BF16); rvm = Res("vm")
        psA = [ps("psA%d" % i, [128, 512]) for i in range(2)]; rpsA = [Res("psA0", True), Res("psA1", True)]
        psT = ps("psT", [128, 1024], BF16); rpsT = Res("psT", True)
        psG = ps("psG", [128, 512]); rpsG = Res("psG", True)
        psS = [ps("psS%d" % i, [128, 512]) for i in range(2)]; rpsS = [Res("psS0", True), Res("psS1", True)]
        psO = [ps("psO%d" % i, [128, 512]) for i in range(2)]; rpsO = [Res("psO0", True), Res("psO1", True)]

        ctr = {"pa": 0, "w": 0, "ev": 0, "ps": 0, "pt": 0, "yt": 0, "mk": 0}

        def nxt(k, n):
            v = ctr[k] % n
            ctr[k] += 1
            return v

        def evac(fn_v, fn_a, r, w):
            if nxt("ev", 2) == 0:
                return A(fn_a, r, w)
            return V(fn_v, r, w)

        DMA("sync", lambda e: e.dma_start(out=cst[:], in_=cst_d), w=[rcst])
        V(lambda e: e.tensor_copy(out=idb[:], in_=cst[:, 0:128]), [rcst], [ridb])
        V(lambda e: e.tensor_copy(out=trib[:], in_=cst[:, 128:256]), [rcst], [rtrib])
        V(lambda e: e.tensor_copy(out=hgmb[:], in_=cst[:, 256:384]), [rcst], [rhgmb])
        ident32 = cst[:, 0:128]
        G(lambda e: e.memset(vm_aug[:, :, :, 128:129], 1.0), w=[rvm])
        G(lambda e: e.memset(qTA[:], 0.0), w=[rqTA])
        G(lambda e: e.memset(qTB[:], 0.0), w=[rqTB])
        nI = sb("nI", [128, 256], I32); rnI = Res("nI")
        posi = nI[0:16, 0:128]; rposi = rnI
        posf = Fp[8][0:16, 0:128]; rposf = rF[8]
        DMA("sync", lambda e: e.dma_start(out=posi, in_=pos_d), w=[rposi])
        V(lambda e: e.tensor_copy(out=posf, in_=posi), [rposi], [rposf])
        T(lambda e: e.matmul(psG[:, 0:16], lhsT=posf, rhs=cst[0:16, 0:16], start=True, stop=True), [rposf, rcst], [rpsG])
        post, r_post = sm("post", 80, 16)
        V(lambda e: e.tensor_copy(out=post, in_=psG[:, 0:16]), [rpsG], [r_post])
        ang = Fp[0][:, 0:256].rearrange("p (i j) -> p i j", i=NT)
        V(lambda e: e.tensor_tensor(out=ang, in0=post.unsqueeze(2).to_broadcast([128, NT, 16]),
                                    in1=cst[:, 515:531].unsqueeze(1).to_broadcast([128, NT, 16]), op=ALU.mult),
          [r_post, rcst], [rF[0]])
        ang_lo = Fp[1][:, 0:256].rearrange("p (i j) -> p i j", i=NT)
        V(lambda e: e.tensor_tensor(out=ang_lo, in0=post.unsqueeze(2).to_broadcast([128, NT, 16]),
                                    in1=cst[:, 547:563].unsqueeze(1).to_broadcast([128, NT, 16]), op=ALU.mult),
          [r_post, rcst], [rF[1]])
        V(lambda e: e.tensor_tensor(out=ang, in0=ang, in1=ang_lo, op=ALU.add), [rF[0], rF[1]], [rF[0]])
        V(lambda e: e.tensor_tensor(out=ang, in0=ang, in1=cst[:, 531:547].unsqueeze(1).to_broadcast([128, NT, 16]),
                                    op=ALU.add), [rF[0], rcst], [rF[0]])
        V(lambda e: e.tensor_scalar(out=Fp[1][:], in0=Fp[0][:], scalar1=float(1.0 / (2 * np.pi)), scalar2=None,
                                    op0=ALU.mult), [rF[0]], [rF[1]])
        V(lambda e: e.tensor_copy(out=nI[:], in_=Fp[1][:]), [rF[1]], [rnI])
        V(lambda e: e.tensor_copy(out=Fp[1][:], in_=nI[:]), [rnI], [rF[1]])
        C1 = 6.28125
        C2 = float(2 * np.pi - 6.28125)
        V(lambda e: e.scalar_tensor_tensor(out=Fp[2][:], in0=Fp[1][:], scalar=-C1, in1=Fp[0][:],
                                           op0=ALU.mult, op1=ALU.add), [rF[1], rF[0]], [rF[2]])
        V(lambda e: e.scalar_tensor_tensor(out=Fp[2][:], in0=Fp[1][:], scalar=-C2, in1=Fp[2][:],
                                           op0=ALU.mult, op1=ALU.add), [rF[1], rF[2]], [rF[2]])
        V(lambda e: e.tensor_scalar(out=Fp[2][:], in0=Fp[2][:], scalar1=float(np.pi), scalar2=float(-np.pi),
                                    op0=ALU.min, op1=ALU.max), [rF[2]], [rF[2]])
        A(lambda e: e.activation(out=Fp[3][:], in_=Fp[2][:], func=AF.Sin), [rF[2]], [rF[3]])
        sc = Fp[3][:, 0:256].rearrange("p (i j) -> p i j", i=NT)
        V(lambda e: e.tensor_copy(out=cs[:, :, 0:8], in_=sc[:, :, 8:16]), [rF[3]], [rrope])
        V(lambda e: e.tensor_copy(out=cs[:, :, 8:16], in_=sc[:, :, 8:16]), [rF[3]], [rrope])
        V(lambda e: e.tensor_scalar(out=sn[:, :, 0:8], in0=sc[:, :, 0:8], scalar1=-1.0, scalar2=None, op0=ALU.mult),
          [rF[3]], [rrope])
        V(lambda e: e.tensor_copy(out=sn[:, :, 8:16], in_=sc[:, :, 0:8]), [rF[3]], [rrope])
        def load_w(dst, rdst, src_ap):
            return DMA("gpsimd", lambda e: e.dma_start(out=dst, in_=src_ap), w=[rdst])

        def w_in_cols(l, c0, n):
            return win_d[l].rearrange("(c p) n -> p c n", p=128)[:, :, c0:c0 + n]

        psGb = psG[:].bitcast(BF16)

        def rms_to_T(src_tile, rsrc, gain, rgain, dstT, rdst, col0, ssc, r_ssc, tsc, r_tsc, rsc, r_rsc, k):
            hb = hbs[k % 2]; rhb = rhbs[k % 2]
            pst, rpst = (psT, rpsT) if k % 2 == 0 else (psGb, rpsG)
            A(lambda e: e.activation(out=hb[:], in_=src_tile, func=AF.Square, accum_out=ssc[:, k:k + 1]),
              [rsrc], [rhb, r_ssc])
            A(lambda e: e.activation(out=tsc[:, k:k + 1], in_=ssc[:, k:k + 1], func=AF.Ln, scale=1.0 / D, bias=EPS),
              [r_ssc], [r_tsc])
            A(lambda e: e.activation(out=rsc[:, k:k + 1], in_=tsc[:, k:k + 1], func=AF.Exp, scale=-0.5),
              [r_tsc], [r_rsc])
            V(lambda e: e.scalar_tensor_tensor(out=hb[:], in0=src_tile, scalar=rsc[:, k:k + 1], in1=gain,
                                               op0=ALU.mult, op1=ALU.mult), [rsrc, r_rsc, rgain], [rhb])
            for c in range(8):
                T(lambda e, c=c: e.transpose(out=pst[:, c * 128:(c + 1) * 128], in_=hb[:, c * 128:(c + 1) * 128],
                                             identity=idb[:]), [rhb, ridb], [rpst])
            src3 = pst[:, 0:1024].rearrange("p (c t) -> p c t", c=8)
            evac(lambda e: e.tensor_copy(out=dstT[:, :, col0:col0 + 128], in_=src3),
                 lambda e: e.copy(out=dstT[:, :, col0:col0 + 128], in_=src3), [rpst], [rdst])

        def proj(lhs_cols, rlhs, wt, rwt, ncols=512, lhsT_src=None):
            b = nxt("pa", 2)
            src = hT if lhsT_src is None else lhsT_src
            for c in range(8):
                T(lambda e, c=c: e.matmul(psA[b][:, 0:ncols], lhsT=src[:, c, lhs_cols:lhs_cols + 128],
                                          rhs=wt[:, c, 0:ncols], start=(c == 0), stop=(c == 7)),
                  [rlhs] + list(rwt), [rpsA[b]])
            return psA[b], rpsA[b]

        def headnorm(src_ps, rps, H, Dh, gain, outF, routF, tmpA, rtmpA, tmpB, rtmpB):
            n = H * Dh
            A(lambda e: e.activation(out=tmpA[:, 0:n], in_=src_ps, func=AF.Square), [rps], [rtmpA])
            V(lambda e: e.tensor_reduce(out=ss4[:, 0:H], in_=tmpA[:, 0:n].rearrange("p (h d) -> p h d", h=H),
                                        axis=AX.X, op=ALU.add), [rtmpA], [r_ss4])
            A(lambda e: e.activation(out=t4[:, 0:H], in_=ss4[:, 0:H], func=AF.Ln, scale=1.0 / Dh, bias=EPS),
              [r_ss4], [r_t4])
            A(lambda e: e.activation(out=rs4[:, 0:H], in_=t4[:, 0:H], func=AF.Exp, scale=-0.5), [r_t4], [r_rs4])
            V(lambda e: e.tensor_tensor(out=tmpB[:, 0:n].rearrange("p (h d) -> p h d", h=H),
                                        in0=src_ps.rearrange("p (h d) -> p h d", h=H),
                                        in1=rs4[:, 0:H].unsqueeze(2).to_broadcast([128, H, Dh]), op=ALU.mult),
              [rps, r_rs4], [rtmpB])
            (G if GOFF else V)(lambda e: e.tensor_tensor(out=outF[:, 0:n].rearrange("p (h d) -> p h d", h=H),
                                                         in0=tmpB[:, 0:n].rearrange("p (h d) -> p h d", h=H),
                                                         in1=gain.unsqueeze(1).to_broadcast([128, H, Dh]), op=ALU.mult),
                               [rtmpB, rgains], [routF])

        def rope(Fx, rFx, i, tR, rtR):
            x3 = Fx[:, 0:256].rearrange("p (h d) -> p h d", h=4)
            a3 = tR[:, 0:64].rearrange("p (h d) -> p h d", h=4)
            b3 = tR[:, 64:128].rearrange("p (h d) -> p h d", h=4)
            rtA = rtR
            rtB = rtR
            G(lambda e: e.tensor_tensor(out=a3, in0=x3[:, :, 0:16], in1=cs[:, i, :].unsqueeze(1).to_broadcast([128, 4, 16]),
                                        op=ALU.mult), [rFx, rrope], [rtA])
            G(lambda e: e.tensor_tensor(out=b3[:, :, 0:8], in0=x3[:, :, 8:16],
                                        in1=sn[:, i, 0:8].unsqueeze(1).to_broadcast([128, 4, 8]), op=ALU.mult),
              [rFx, rrope], [rtB])
            G(lambda e: e.tensor_tensor(out=b3[:, :, 8:16], in0=x3[:, :, 0:8],
                                        in1=sn[:, i, 8:16].unsqueeze(1).to_broadcast([128, 4, 8]), op=ALU.mult),
              [rFx, rrope], [rtB])
            G(lambda e: e.tensor_tensor(out=x3[:, :, 0:16], in0=a3, in1=b3, op=ALU.add), [rtA, rtB], [rFx])

        def silu_ps(dst, rdst, src_ps, rps):
            A(lambda e: e.activation(out=dst, in_=src_ps, func=AF.Exp, scale=-1.0), [rps], [rdst])
            A(lambda e: e.activation(out=dst, in_=dst, func=AF.Ln, bias=1.0), [rdst], [rdst])
            A(lambda e: e.activation(out=dst, in_=dst, func=AF.Exp, scale=-1.0), [rdst], [rdst])
            V(lambda e: e.tensor_tensor(out=dst, in0=src_ps, in1=dst, op=ALU.mult), [rps, rdst], [rdst])

        SC = [((Fp[0], rF[0]), (Fp[1], rF[1]), (Fp[2], rF[2]), (Fp[3], rF[3])),
              ((Fp[4], rF[4]), (Fp[6], rF[6]), (Fp[7], rF[7]), (Fp[8], rF[8]))]
        AUG = [(k_aug, rk_aug), (q_aug, rq_aug)]
        if _os.environ.get("NOPAR", "0") == "1":
            SC[1] = SC[0]
        if _os.environ.get("NOAUG", "0") == "1":
            AUG[1] = AUG[0]

        dec8, _r = sm("dec8", 96, 8)
        r_dec8 = [Res("dec8a"), Res("dec8b")]
        sso4, _r2 = sm("sso4", 104, 4)
        r_sso2 = [Res("ssoA"), Res("ssoB")]
        dummy = sb("dummy", [128, 8])
        kflat = kT[:].rearrange("p h t -> p (h t)")
        kf32 = kflat[:, 0:4608].bitcast(F32)
        Fq = [kf32[:, k * 256:(k + 1) * 256] for k in range(9)]
        rFq = [Res("Fq%d" % k) for k in range(9)]
        Bq = [kflat[:, 4608 + k * 256:4608 + (k + 1) * 256] for k in range(7)]
        rBq = [Res("Bq%d" % k) for k in range(7)]
        HSETS = [([t[:] for t in Fp], rF, [t[:] for t in Bp], rB), (Fq, rFq, Bq, rBq)]

        def barrier():
            G(lambda e: e.memset(dummy[:], 0.0), w=list(rkT) + rFq + rBq + list(rv) + [rstage, rg_bc])

        rrow = Res("pe_rowfence")

        out_dmas = []

        def outproj_tile(i, r, last, obanks=None):
            yb = nxt("yt", 2)
            for pp in range(2):
                T(lambda e, pp=pp: e.transpose(out=psT[:, pp * 128:(pp + 1) * 128], in_=y_tok[:, r, pp * 128:(pp + 1) * 128],
                                               identity=idb[:]), [ry[r], ridb], [rpsT])
            evac(lambda e: e.tensor_copy(out=yT[yb][:], in_=psT[:, 0:256]),
                 lambda e: e.copy(out=yT[yb][:], in_=psT[:, 0:256]), [rpsT], [ryT[yb]])
            for half in range(2):
                if obanks is None:
                    b = nxt("pa", 2)
                    pso, rpso = psA[b], rpsA[b]
                else:
                    pso, rpso = obanks[half]
                for pp in range(2):
                    T(lambda e, pp=pp, half=half, pso=pso: e.matmul(pso[:, 0:512], lhsT=yT[yb][:, pp * 128:(pp + 1) * 128],
                                                                    rhs=wo[:, pp, half * 512:(half + 1) * 512],
                                                                    start=(pp == 0), stop=(pp == 1)),
                      [ryT[yb], rwo], [rpso])
                V(lambda e, half=half, pso=pso: e.tensor_tensor(out=x_tok[:, i, half * 512:(half + 1) * 512], in0=pso[:, 0:512],
                                                                in1=x_tok[:, i, half * 512:(half + 1) * 512], op=ALU.add),
                  [rpso, rx[i]], [rx[i]])
            if last:
                out_dmas.append(DMA("sync", lambda e: e.dma_start(out=out_d[i * 128:(i + 1) * 128, :], in_=x_tok[:, i, :]),
                                    r=[rx[i]]))

        def attn_chunk(Q, H, Dh, KP, scale, key_tiles, causal, kTsrc, rkTsrc, vsrc, rvsrc, qview, rqT, sz, rsz):
            DA = Dh + 1
            for h in range(H):
                if Dh == 64:
                    o_b = h % 2
                    banks = [o_b, o_b, o_b, o_b]
                    offs = [0, DA, 2 * DA, 3 * DA]
                else:
                    banks = [0, 0, 1, 1]
                    offs = [0, DA, 0, DA]
                started = set()
                kts = key_tiles(Q)
                for kt in kts:
                    j = kt - 4 * Q if causal else -1
                    q0 = max(j, 0) * 128
                    N = 512 - q0
                    sbk = nxt("ps", PS3)
                    pss, rpss = ((psS[0], rpsS[0]), (psS[1], rpsS[1]), (psG, rpsG))[sbk]
                    T(lambda e, kt=kt, h=h, q0=q0, N=N, pss=pss: e.matmul(
                        pss[:, 0:N], lhsT=kTsrc(h, kt), rhs=qview(h)[:, q0:512], start=True, stop=True),
                      [rkTsrc(kt), rqT], [rpss])
                    pb = nxt("pt", 4)
                    A(lambda e, N=N, pss=pss, pb=pb: e.activation(out=pT[pb][:, 0:N], in_=pss[:, 0:N], func=AF.Exp,
                                                                  scale=scale), [rpss], [rpT[pb]])
                    if j >= 0:
                        (V if (MASKV and nxt("mk", 2) == 0) else G)(
                            lambda e, pb=pb: e.tensor_tensor(out=pT[pb][:, 0:128], in0=pT[pb][:, 0:128], in1=trib[:],
                                                             op=ALU.mult), [rpT[pb], rtrib], [rpT[pb]])
                    for r in range(max(j, 0), 4):
                        bk = banks[r]
                        first = bk not in started
                        started.add(bk)
                        T(lambda e, r=r, kt=kt, h=h, q0=q0, pb=pb, bk=bk, first=first: e.matmul(
                            psO[bk][:, offs[r]:offs[r] + DA], lhsT=pT[pb][:, r * 128 - q0:r * 128 - q0 + 128],
                            rhs=vsrc(h, kt), start=first, stop=False, skip_group_check=True),
                          [rpT[pb], rvsrc(kt)], [rpsO[bk]])
                for r in range(4):
                    bk = banks[r]
                    V(lambda e, r=r, bk=bk: e.reciprocal(out=rden[:, r:r + 1], in_=psO[bk][:, offs[r] + Dh:offs[r] + DA]),
                      [rpsO[bk]], [r_rden])
                    V(lambda e, r=r, bk=bk, h=h: e.scalar_tensor_tensor(
                        out=y_tok[:, r, h * Dh:(h + 1) * Dh], in0=psO[bk][:, offs[r]:offs[r] + Dh], scalar=rden[:, r:r + 1],
                        in1=sz[:, r, h * Dh:(h + 1) * Dh], op0=ALU.mult, op1=ALU.mult),
                      [rpsO[bk], r_rden, rsz[r]], [ry[r]])

        for li, l in enumerate(layers):
            last_layer = (li == len(layers) - 1)
            DMA("sync", lambda e, l=l: e.dma_start(out=g_bcv, in_=ng_d[l:l + 1, :].partition_broadcast(128)), w=[rg_bc])
            for dst, src in ((gq_bc, gq_d), (gk_bc, gk_d), (go_bc, go_d), (gmq_bc, gmq_d), (gmk_bc, gmk_d)):
                DMA("sync", lambda e, l=l, dst=dst, src=src: e.dma_start(out=dst[:], in_=src[l:l + 1, :].partition_broadcast(128)),
                    w=[rgains])
            for i in range(NT):
                if li == 0:
                    DMA(("scalar" if (XQ and i % 2 == 1) else "sync"), lambda e, i=i: e.dma_start(out=x_tok[:, i, :], in_=x_d[i * 128:(i + 1) * 128, :]), w=[rx[i]])
                rms_to_T(x_tok[:, i, :], rx[i], g_bcv, rg_bc, hT, rhT[i], i * 128, ss16, r_ss16, t16, r_t16,
                         rstd16, r_rstd16, i)
            glist = [g for g in ALL_GROUPS if g in groups]
            for gi, gname in enumerate(glist):
                last = last_layer and gi == len(glist) - 1
                kind = gname[0]
                g = int(gname[1])
                if kind == "A":
                    w1 = nxt("w", 3); w2 = nxt("w", 3)
                    load_w(wb[w1][:, :, 0:256], rwb[w1][0], w_in_cols(l, 512 + 256 * g, 256))
                    load_w(wb[w1][:, :, 256:512], rwb[w1][1], w_in_cols(l, 1024 + 256 * g, 256))
                    load_w(wb[w2][:, :, 0:256], rwb[w2][0], w_in_cols(l, 256 * g, 256))
                    load_w(wb[w2][:, :, 256:512], rwb[w2][1], w_in_cols(l, 3584 + 256 * g, 256))
                    load_w(wo[:], rwo, wout_d[l, 256 * g:256 * g + 256, :].rearrange("(c p) n -> p c n", p=128))
                    barrier()
                    G(lambda e: e.memset(v_aug[:, :, :, 64:65], 1.0), w=rv)
                    V(lambda e: e.memset(psG[:, 0:16], 0.0), w=[rpsG])
                    for i in range(NT):
                        n_blk = i // 2
                        (tA, rtA), (tB, rtB), (FO, rFO), (tR, rtR) = SC[(i % 2) * PAR1]
                        ka, rka = AUG[(i % 2) * PAR1]
                        pa, rpa = proj(i * 128, rhT[i], wb[w1], rwb[w1])
                        headnorm(pa[:, 0:256], rpa, 4, 64, gk_bc[:], FO, rFO, tA, rtA, tB, rtB)
                        evac(lambda e, i=i, pa=pa: e.tensor_copy(out=v_aug[:, i, :, 0:64],
                                                                 in_=pa[:, 256:512].rearrange("p (h d) -> p h d", h=4)),
                             lambda e, i=i, pa=pa: e.copy(out=v_aug[:, i, :, 0:64],
                                                          in_=pa[:, 256:512].rearrange("p (h d) -> p h d", h=4)),
                             [rpa], [rv[i]])
                        rope(FO, rFO, i, tR, rtR)
                        G(lambda e, ka=ka, FO=FO: e.tensor_copy(out=ka[:, :, 0:64], in_=FO[:].rearrange("p (h d) -> p h d", h=4)),
                          [rFO], [rka])
                        G(lambda e, ka=ka: e.memset(ka[:, :, 64:72], 0.0), w=[rka])
                        G(lambda e, ka=ka, n_blk=n_blk: e.memset(ka[:, :, 64 + n_blk:65 + n_blk], 1.0), w=[rka])
                        for pp in range(2):
                            T(lambda e, pp=pp, n_blk=n_blk, FO=FO: e.matmul(psG[:, pp * 8 + n_blk:pp * 8 + n_blk + 1],
                                                                            lhsT=FO[:, pp * 128:(pp + 1) * 128], rhs=cst[:, 514:515],
                                                                            start=False, stop=False, skip_group_check=True),
                              [rFO, rcst], [rpsG])
                        for h in range(4):
                            T(lambda e, h=h, ka=ka: e.transpose(out=psT[0:72, h * 128:(h + 1) * 128], in_=ka[:, h, :], identity=idb[:]),
                              [rka, ridb], [rpsT])
                        src3 = psT[0:72, 0:512].rearrange("p (h t) -> p h t", h=4)
                        evac(lambda e, i=i, src3=src3: e.tensor_copy(out=kT[0:72, :, i * 128:(i + 1) * 128], in_=src3),
                             lambda e, i=i, src3=src3: e.copy(out=kT[0:72, :, i * 128:(i + 1) * 128], in_=src3),
                             [rpsT], [rkT[i]])
                    G(lambda e: e.memset(kmT32[:], 0.0), w=[rkm])
                    A(lambda e: e.copy(out=kmT32[0:64, :, 0, :], in_=psG[0:64, 0:16].rearrange("p (a n) -> p a n", a=2)), [rpsG], [rkm])
                    A(lambda e: e.copy(out=kmT32[64:128, :, 1, :], in_=psG[64:128, 0:16].rearrange("p (a n) -> p a n", a=2)), [rpsG], [rkm])
                    for Q in range(4):
                        qT3 = qTs[Q % 2][:, 0:2048].rearrange("p (h t) -> p h t", h=4)
                        rqT = rqTs[Q % 2]
                        sz = szs[Q % 2]; rsz = rszs[Q % 2]
                        for r in range(4):
                            i = 4 * Q + r
                            own = i // 2
                            P.phase = "chain"
                            par = (i % 2) * PAR2 * (1 if (own < 4 or GPAR) else 0)
                            (tA, rtA), (tB, rtB), (FO, rFO), (tR, rtR) = SC[par]
                            qa, rqa = AUG[par]
                            pa, rpa = proj(i * 128, rhT[i], wb[w2], rwb[w2])
                            silu_ps(sz[:, r, :], rsz[r], pa[:, 256:512], rpa)
                            headnorm(pa[:, 0:256], rpa, 4, 64, gq_bc[:], FO, rFO, tA, rtA, tB, rtB)
                            rope(FO, rFO, i, tR, rtR)
                            G(lambda e, qa=qa, FO=FO: e.tensor_copy(out=qa[:, :, 0:64], in_=FO[:].rearrange("p (h d) -> p h d", h=4)),
                              [rFO], [rqa])
                            if own >= 4:
                                P.phase = "gate"
                                for pp in range(2):
                                    T(lambda e, pp=pp, FO=FO: e.matmul(psG[:, pp * 128:(pp + 1) * 128],
                                                                       lhsT=FO[:, pp * 128:(pp + 1) * 128], rhs=ident32,
                                                                       start=True, stop=True),
                                      [rFO, rcst], [rpsG])
                                A(lambda e: e.copy(out=Fp[5][:], in_=psG[:, 0:256]), [rpsG], [rF[5]])
                                for h in range(4):
                                    T(lambda e, h=h, own=own: e.matmul(
                                        psG[:, 256 + h * 8:256 + h * 8 + own],
                                        lhsT=Fp[5][:, (h // 2) * 128:(h // 2) * 128 + 128],
                                        rhs=kmT32[:, h // 2, h % 2, 0:own], start=True, stop=True),
                                      [rF[5], rkm], [rpsG])
                                V(lambda e: e.memset(gm[:], -1.0e30), w=[rgm])
                                V(lambda e, own=own: e.tensor_copy(
                                    out=gm[:, :, 0:own], in_=psG[:, 256:288].rearrange("p (h n) -> p h n", h=4)[:, :, 0:own]),
                                  [rpsG], [rgm])
                                for h in range(4):
                                    V(lambda e, h=h: e.max(out=top8[:, h, :], in_=gm[:, h, :]), [rgm], [rtop8])
                                V(lambda e: e.tensor_tensor(out=selb[:], in0=gm[:], in1=top8[:, :, 2:3].to_broadcast([128, 4, 8]),
                                                            op=ALU.is_ge), [rgm, rtop8], [rselb])
                                V(lambda e, qa=qa: e.tensor_scalar(out=qa[:, :, 64:72], in0=selb[:], scalar1=30000.0,
                                                                   scalar2=-30000.0, op0=ALU.mult, op1=ALU.add), [rselb], [rqa])
                                V(lambda e, qa=qa, own=own: e.memset(qa[:, :, 64 + own:65 + own], 0.0), w=[rqa])
                            else:
                                G(lambda e, qa=qa: e.memset(qa[:, :, 64:72], 0.0), w=[rqa])
                            P.phase = None
                            for h in range(4):
                                T(lambda e, h=h, qa=qa: e.transpose(out=psT[0:72, h * 128:(h + 1) * 128], in_=qa[:, h, :],
                                                                    identity=idb[:]), [rqa, ridb], [rpsT])
                            src3 = psT[0:72, 0:512].rearrange("p (h t) -> p h t", h=4)
                            evac(lambda e, r=r, src3=src3, qT3=qT3: e.tensor_copy(out=qT3[0:72, :, r * 128:(r + 1) * 128], in_=src3),
                                 lambda e, r=r, src3=src3, qT3=qT3: e.copy(out=qT3[0:72, :, r * 128:(r + 1) * 128], in_=src3),
                                 [rpsT], [rqT])
                        attn_chunk(Q, 4, 64, 72, 0.125, lambda Q: list(range(4 * Q + 4)), True,
                                   lambda h, kt: kT[0:72, h, kt * 128:(kt + 1) * 128], lambda kt: rkT[kt],
                                   lambda h, kt: v_aug[:, kt, h, :], lambda kt: rv[kt],
                                   lambda h, qT3=qT3: qT3[0:72, h, :], rqT, sz, rsz)
                        for r in range(4):
                            outproj_tile(4 * Q + r, r, last, obanks=([(psO[0], rpsO[0]), (psO[1], rpsO[1])] if OPB else None))
                elif kind == "M":
                    w1 = nxt("w", 3); w2 = nxt("w", 3)
                    wkvv = wkv_d[l].rearrange("(c p) n -> p c n", p=128)
                    load_w(wb[w1][:, :, 0:256], rwb[w1][0], wkvv[:, :, 256 * g:256 * g + 256])
                    load_w(wb[w1][:, :, 256:512], rwb[w1][1], wkvv[:, :, 512 + 256 * g:512 + 256 * g + 256])
                    load_w(wb[w2][:, :, 0:256], rwb[w2][0], w_in_cols(l, 3072 + 256 * g, 256))
                    load_w(wb[w2][:, :, 256:512], rwb[w2][1], w_in_cols(l, 4608 + 256 * g, 256))
                    load_w(wo[:], rwo, wout_d[l, 1024 + 256 * g:1024 + 256 * g + 256, :].rearrange("(c p) n -> p c n", p=128))
                    if g == 0 or ("M0" not in groups):
                        barrier()
                        DMA("sync", lambda e, l=l: e.dma_start(out=g_bcv, in_=mng_d[l:l + 1, :].partition_broadcast(128)),
                            w=[rg_bc])
                        for mt in range(2):
                            DMA("sync", lambda e, mt=mt: e.dma_start(out=stage, in_=mem_d[mt * 128:(mt + 1) * 128, :]),
                                w=[rstage])
                            rms_to_T(stage, rstage, g_bcv, rg_bc, memT, rmemT, mt * 128, ss16, r_ss16, t16, r_t16,
                                     rstd16, r_rstd16, mt)
                    for mt in range(2):
                        (tA, rtA), (tB, rtB), (FO, rFO), (tR, rtR) = SC[mt % 2]
                        pa, rpa = proj(mt * 128, rmemT, wb[w1], rwb[w1], lhsT_src=memT)
                        headnorm(pa[:, 0:256], rpa, 2, 128, gmk_bc[:], FO, rFO, tA, rtA, tB, rtB)
                        evac(lambda e, mt=mt, pa=pa: e.tensor_copy(out=vm_aug[:, mt, :, 0:128],
                                                                   in_=pa[:, 256:512].rearrange("p (h d) -> p h d", h=2)),
                             lambda e, mt=mt, pa=pa: e.copy(out=vm_aug[:, mt, :, 0:128],
                                                            in_=pa[:, 256:512].rearrange("p (h d) -> p h d", h=2)),
                             [rpa], [rvm])
                        G(lambda e, mt=mt, FO=FO: e.tensor_copy(out=Bp[mt % 2][:], in_=FO[:]), [rFO], [rB[mt % 2]])
                        for hh in range(2):
                            T(lambda e, hh=hh, mt=mt: e.transpose(out=psT[:, hh * 128:(hh + 1) * 128],
                                                                  in_=Bp[mt % 2][:, hh * 128:(hh + 1) * 128],
                                                                  identity=idb[:]), [rB[mt % 2], ridb], [rpsT])
                        src3 = psT[:, 0:256].rearrange("p (h t) -> p h t", h=2)
                        evac(lambda e, mt=mt, src3=src3: e.tensor_copy(out=kmT[:, :, mt * 128:(mt + 1) * 128], in_=src3),
                             lambda e, mt=mt, src3=src3: e.copy(out=kmT[:, :, mt * 128:(mt + 1) * 128], in_=src3),
                             [rpsT], [rkmT])
                    for Q in range(4):
                        qm3 = qTs[Q % 2][:, 0:1024].rearrange("p (h t) -> p h t", h=2)
                        rqT = rqTs[Q % 2]
                        sz = szs[Q % 2]; rsz = rszs[Q % 2]
                        for r in range(4):
                            i = 4 * Q + r
                            (tA, rtA), (tB, rtB), (FO, rFO), (tR, rtR) = SC[i % 2]
                            pa, rpa = proj(i * 128, rhT[i], wb[w2], rwb[w2])
                            silu_ps(sz[:, r, :], rsz[r], pa[:, 256:512], rpa)
                            headnorm(pa[:, 0:256], rpa, 2, 128, gmq_bc[:], FO, rFO, tA, rtA, tB, rtB)
                            G(lambda e, i=i, FO=FO: e.tensor_copy(out=Bp[i % 2][:], in_=FO[:]), [rFO], [rB[i % 2]])
                            for hh in range(2):
                                T(lambda e, hh=hh, i=i: e.transpose(out=psT[:, hh * 128:(hh + 1) * 128],
                                                                    in_=Bp[i % 2][:, hh * 128:(hh + 1) * 128], identity=idb[:]),
                                  [rB[i % 2], ridb], [rpsT])
                            src3 = psT[:, 0:256].rearrange("p (h t) -> p h t", h=2)
                            evac(lambda e, r=r, src3=src3, qm3=qm3: e.tensor_copy(out=qm3[:, :, r * 128:(r + 1) * 128], in_=src3),
                                 lambda e, r=r, src3=src3, qm3=qm3: e.copy(out=qm3[:, :, r * 128:(r + 1) * 128], in_=src3),
                                 [rpsT], [rqT])
                        attn_chunk(Q, 2, 128, 128, float(128 ** -0.5), lambda Q: [0, 1], False,
                                   lambda h, kt: kmT[:, h, kt * 128:(kt + 1) * 128], lambda kt: rkmT,
                                   lambda h, kt: vm_aug[:, kt, h, :], lambda kt: rvm,
                                   lambda h, qm3=qm3: qm3[:, h, :], rqT, sz, rsz)
                        for r in range(4):
                            outproj_tile(4 * Q + r, r, last, obanks=([(psO[0], rpsO[0]), (psO[1], rpsO[1])] if OPB else None))
                else:
                    w1 = nxt("w", 3); w2 = nxt("w", 3)
                    load_w(wb[w1][:, :, 0:256], rwb[w1][0], w_in_cols(l, 1536 + 256 * g, 256))
                    load_w(wb[w1][:, :, 256:512], rwb[w1][1], w_in_cols(l, 2048 + 256 * g, 256))
                    load_w(wb[w2][:, :, 0:256], rwb[w2][0], w_in_cols(l, 2560 + 256 * g, 256))
                    load_w(wb[w2][:, :, 256:512], rwb[w2][1], w_in_cols(l, 4096 + 256 * g, 256))
                    load_w(wo[:], rwo, wout_d[l, 512 + 256 * g:512 + 256 * g + 256, :].rearrange("(c p) n -> p c n", p=128))
                    barrier()
                    if l != 0:
                        DMA("sync", lambda e, g=g: e.dma_start(out=lb_g, in_=lbl_d[1:2, 256 * g:256 * g + 256].partition_broadcast(128)), w=[rlb])
                        DMA("sync", lambda e, g=g: e.dma_start(out=oml_g, in_=lbl_d[0:1, 256 * g:256 * g + 256].partition_broadcast(128)), w=[rlb])
                        V(lambda e: e.tensor_tensor(out=lb_g, in0=lb_g, in1=oml_g, op=ALU.subtract), [rlb], [rlb])
                        A(lambda e: e.activation(out=lb_g, in_=lb_g, func=AF.Exp, scale=-1.0), [rlb], [rlb])
                        A(lambda e: e.activation(out=lb_g, in_=lb_g, func=AF.Ln, bias=1.0), [rlb], [rlb])
                        A(lambda e: e.activation(out=lb_g, in_=lb_g, func=AF.Exp, scale=-1.0), [rlb], [rlb])
                        V(lambda e: e.tensor_scalar(out=oml_g, in0=lb_g, scalar1=-1.0, scalar2=1.0, op0=ALU.mult, op1=ALU.add),
                          [rlb], [rlb])
                    for hh in range(2):
                        G(lambda e, hh=hh: e.memset(S32[:, hh, :], 0.0), w=[rS32[hh]])
                        G(lambda e, hh=hh: e.memset(Sbf[:, 0, hh, :], 0.0), w=[rSbf[0][hh]])
                    Tri32 = cst[:, 256:384]
                    TriE32 = cst[:, 384:512]
                    for i in range(NT):
                        Fs, rFs, Bs, rBs = HSETS[i % 2]
                        sl = i % 4
                        sz = szs[(i // 4) % 2]; rsz = rszs[(i // 4) % 2]
                        pq, rpq = proj(i * 128, rhT[i], wb[w1], rwb[w1])
                        silu_ps(Fs[0], rFs[0], pq[:, 0:256], rpq)
                        A(lambda e, pq=pq, Fs=Fs: e.activation(out=Fs[1], in_=pq[:, 256:512], func=AF.Exp, scale=-1.0), [rpq], [rFs[1]])
                        A(lambda e, Fs=Fs: e.activation(out=Fs[1], in_=Fs[1], func=AF.Ln, bias=1.0), [rFs[1]], [rFs[1]])
                        pi_, rpi = proj(i * 128, rhT[i], wb[w2], rwb[w2])
                        silu_ps(sz[:, sl, :], rsz[sl], pi_[:, 256:512], rpi)
                        V(lambda e, pi_=pi_, Bs=Bs: e.tensor_copy(out=Bs[0], in_=pi_[:, 0:256]), [rpi], [rBs[0]])
                        if l == 0:
                            A(lambda e, Fs=Fs: e.activation(out=Fs[2], in_=Fs[1], func=AF.Copy, scale=-1.0), [rFs[1]], [rFs[2]])
                            A(lambda e, Fs=Fs: e.activation(out=Fs[1], in_=Fs[1], func=AF.Exp, scale=-1.0), [rFs[1]], [rFs[1]])
                        else:
                            A(lambda e, Fs=Fs: e.activation(out=Fs[1], in_=Fs[1], func=AF.Exp, scale=-1.0), [rFs[1]], [rFs[1]])
                            V(lambda e, g=g, Fs=Fs: e.tensor_tensor(out=Fs[1], in0=Fs[1], in1=oml_g,
                                                                    op=ALU.mult), [rFs[1], rlb], [rFs[1]])
                            V(lambda e, g=g, Fs=Fs: e.tensor_tensor(out=Fs[1], in0=Fs[1], in1=lb_g,
                                                                    op=ALU.add), [rFs[1], rlb], [rFs[1]])
                            A(lambda e, Fs=Fs: e.activation(out=Fs[2], in_=Fs[1], func=AF.Ln), [rFs[1]], [rFs[2]])
                        (G if GOFF else V)(lambda e, Fs=Fs: e.tensor_scalar(out=Fs[3], in0=Fs[1], scalar1=-1.0, scalar2=1.0, op0=ALU.mult,
                                                                            op1=ALU.add), [rFs[1]], [rFs[3]])
                        T(lambda e, Fs=Fs: e.matmul(psG[:, 0:256], lhsT=Tri32, rhs=Fs[2], start=True, stop=True),
                          [rcst, rFs[2]], [rpsG])
                        T(lambda e, Fs=Fs: e.matmul(psG[:, 256:512], lhsT=TriE32, rhs=Fs[2], start=True, stop=True),
                          [rcst, rFs[2]], [rpsG])
                        for hh in range(2):
                            T(lambda e, hh=hh, Fs=Fs: e.matmul(psS[1][:, 256 + 2 * hh:256 + 2 * hh + 2],
                                                               lhsT=Fs[2][:, hh * 128:(hh + 1) * 128],
                                                               rhs=cst[:, 512:514], start=True, stop=True), [rFs[2], rcst], [rpsS[1]])
                        dsl = dec8[:, 4 * (i % 2):4 * (i % 2) + 4]
                        rds = r_dec8[i % 2]
                        A(lambda e, dsl=dsl: e.activation(out=dsl, in_=psS[1][:, 256:260], func=AF.Exp), [rpsS[1]], [rds])
                        A(lambda e, Fs=Fs: e.activation(out=Fs[4], in_=psG[:, 0:256], func=AF.Exp), [rpsG], [rFs[4]])
                        A(lambda e, Fs=Fs: e.activation(out=Fs[5], in_=psG[:, 0:256], func=AF.Exp, scale=-1.0), [rpsG], [rFs[5]])
                        A(lambda e, Fs=Fs: e.activation(out=Fs[6], in_=psG[:, 256:512], func=AF.Exp), [rpsG], [rFs[6]])
                        V(lambda e, Fs=Fs, Bs=Bs: e.tensor_tensor(out=Bs[1], in0=Fs[0], in1=Fs[4], op=ALU.mult), [rFs[0], rFs[4]], [rBs[1]])
                        G(lambda e, Fs=Fs, Bs=Bs: e.tensor_tensor(out=Bs[2], in0=Fs[3], in1=Fs[5], op=ALU.mult), [rFs[3], rFs[5]], [rBs[2]])
                        G(lambda e, Fs=Fs, Bs=Bs: e.tensor_tensor(out=Bs[3], in0=Fs[3], in1=Fs[6], op=ALU.mult), [rFs[3], rFs[6]], [rBs[3]])
                        for hh in range(2):
                            T(lambda e, hh=hh, Bs=Bs: e.transpose(out=psT[:, hh * 128:(hh + 1) * 128], in_=Bs[1][:, hh * 128:(hh + 1) * 128],
                                                                  identity=idb[:]), [rBs[1], ridb], [rpsT])
                            T(lambda e, hh=hh, Bs=Bs: e.transpose(out=psT[:, 256 + hh * 128:256 + (hh + 1) * 128],
                                                                  in_=Bs[2][:, hh * 128:(hh + 1) * 128], identity=idb[:]),
                              [rBs[2], ridb], [rpsT])
                        pq3 = psT[:, 0:256].rearrange("p (h t) -> p h t", h=2)
                        A(lambda e, Bs=Bs: e.copy(out=Bs[4], in_=psT[:, 0:256]), [rpsT], [rBs[4]])
                        V(lambda e, Bs=Bs: e.tensor_copy(out=Bs[5], in_=psT[:, 256:512]), [rpsT], [rBs[5]])
                        A(lambda e, pq3=pq3: e.copy(out=qTA[:, :, 0:64], in_=pq3[:, :, 0:64]), [rpsT], [rqTA])
                        V(lambda e, pq3=pq3: e.tensor_copy(out=qTB[:, :, 64:128], in_=pq3[:, :, 64:128]), [rpsT], [rqTB])
                        cur = i % 2
                        nxtb = 1 - cur
                        for hh in range(2):
                            hs = slice(hh * 128, (hh + 1) * 128)
                            T(lambda e, hs=hs, Bs=Bs: e.matmul(psS[1][:, hs], lhsT=Bs[5][:, hs], rhs=Bs[4][:, hs], start=True, stop=True),
                              [rBs[5], rBs[4]], [rpsS[1]])
                        for hh in range(2):
                            hs = slice(hh * 128, (hh + 1) * 128)
                            V(lambda e, hs=hs, Bs=Bs: e.tensor_tensor(out=Bs[6][:, hs], in0=psS[1][:, hs], in1=hgmb[:], op=ALU.mult),
                              [rpsS[1], rhgmb], [rBs[6]])
                        for hh in range(2):
                            hs = slice(hh * 128, (hh + 1) * 128)
                            T(lambda e, hs=hs, hh=hh, Bs=Bs: e.matmul(psO[hh][:, 0:128], lhsT=Bs[6][:, hs], rhs=Bs[0][:, hs],
                                                                      start=True, stop=False), [rBs[6], rBs[0]], [rpsO[hh]])
                            T(lambda e, hh=hh, cur=cur: e.matmul(psO[hh][:, 0:128], lhsT=qTA[:, hh, :], rhs=Sbf[:, cur, hh, :],
                                                                 start=False, stop=False), [rqTA, rSbf[cur][hh]], [rpsO[hh]])
                        for hh in range(2):
                            hs = slice(hh * 128, (hh + 1) * 128)
                            T(lambda e, hs=hs, Bs=Bs: e.matmul(psS[0][:, hs], lhsT=Bs[3][0:64, hs], rhs=Bs[0][0:64, hs],
                                                               start=True, stop=True), [rBs[3], rBs[0]], [rpsS[0], rrow])
                        for hh in range(2):
                            hs = slice(hh * 128, (hh + 1) * 128)
                            V(lambda e, hs=hs, hh=hh, dsl=dsl: e.scalar_tensor_tensor(out=S32[:, hh, :], in0=S32[:, hh, :],
                                                                                      scalar=dsl[:, 2 * hh:2 * hh + 1], in1=psS[0][:, hs],
                                                                                      op0=ALU.mult, op1=ALU.add),
                              [rS32[hh], rds, rpsS[0]], [rS32[hh]])
                            G(lambda e, hh=hh, nxtb=nxtb: e.tensor_copy(out=Sbf[:, nxtb, hh, :], in_=S32[:, hh, :]),
                              [rS32[hh]], [rSbf[nxtb][hh]])
                        for hh in range(2):
                            T(lambda e, hh=hh, nxtb=nxtb: e.matmul(psO[hh][:, 0:128], lhsT=qTB[:, hh, :], rhs=Sbf[:, nxtb, hh, :],
                                                                   start=False, stop=True), [rqTB, rSbf[nxtb][hh]], [rpsO[hh], rrow])
                        for hh in range(2):
                            hs = slice(hh * 128, (hh + 1) * 128)
                            T(lambda e, hs=hs, Bs=Bs: e.matmul(psS[0][:, hs], lhsT=Bs[3][64:128, hs], rhs=Bs[0][64:128, hs],
                                                               start=True, stop=True), [rBs[3], rBs[0]], [rpsS[0], rrow])
                        for hh in range(2):
                            hs = slice(hh * 128, (hh + 1) * 128)
                            V(lambda e, hs=hs, hh=hh, dsl=dsl: e.scalar_tensor_tensor(out=S32[:, hh, :], in0=S32[:, hh, :],
                                                                                      scalar=dsl[:, 2 * hh + 1:2 * hh + 2], in1=psS[0][:, hs],
                                                                                      op0=ALU.mult, op1=ALU.add),
                              [rS32[hh], rds, rpsS[0]], [rS32[hh]])
                        for hh in range(2):
                            G(lambda e, hh=hh, nxtb=nxtb: e.tensor_copy(out=Sbf[:, nxtb, hh, :], in_=S32[:, hh, :]),
                              [rS32[hh]], [rSbf[nxtb][hh]])
                        ssl = sso4[:, 2 * (i % 2):2 * (i % 2) + 2]
                        r_sso = r_sso2[i % 2]
                        for hh in range(2):
                            A(lambda e, hh=hh, Fs=Fs, ssl=ssl: e.activation(out=Fs[7][:, 0:128], in_=psO[hh][:, 0:128], func=AF.Square,
                                                                            accum_out=ssl[:, hh:hh + 1]), [rpsO[hh]], [rFs[7], r_sso])
                        A(lambda e, ssl=ssl: e.activation(out=ssl, in_=ssl, func=AF.Ln, scale=1.0 / 128, bias=EPS), [r_sso], [r_sso])
                        A(lambda e, ssl=ssl: e.activation(out=ssl, in_=ssl, func=AF.Exp, scale=-0.5), [r_sso], [r_sso])
                        for hh in range(2):
                            hs = slice(hh * 128, (hh + 1) * 128)
                            V(lambda e, hh=hh, hs=hs, Fs=Fs, ssl=ssl: e.scalar_tensor_tensor(out=Fs[8][:, hs], in0=psO[hh][:, 0:128],
                                                                                             scalar=ssl[:, hh:hh + 1], in1=go_bc[:],
                                                                                             op0=ALU.mult, op1=ALU.mult),
                              [rpsO[hh], r_sso, rgains], [rFs[8]])
                        G(lambda e, Fs=Fs, sl=sl, sz=sz: e.tensor_tensor(out=y_tok[:, sl, :], in0=Fs[8], in1=sz[:, sl, :], op=ALU.mult),
                          [rFs[8], rsz[sl]], [ry[sl]])
                        outproj_tile(i, sl, last, obanks=[(psS[0], rpsS[0]), (psS[0], rpsS[0])])
            if not glist and last_layer:
                for i in range(NT):
                    out_dmas.append(DMA("sync", lambda e, i=i: e.dma_start(out=out_d[i * 128:(i + 1) * 128, :], in_=x_tok[:, i, :]),
                                        r=[rx[i]]))
        if SCHED:
            if SCHED2:
                P.schedule2(SDELTA)
            else:
                P.schedule()
        P.emit(st, out_dmas)
    build_nc.stats = P.stats
    return nc


_CACHE = {}


def _get_nc(layers, groups):
    key = (tuple(layers), tuple(groups))
    if key not in _CACHE:
        _CACHE[key] = build_nc(layers, groups)
    return _CACHE[key]


def run(inputs, layers=(0, 1), groups=ALL_GROUPS, cores=8):
    nc = _get_nc(layers, groups)
    f = lambda a: np.ascontiguousarray(np.asarray(a))
    cst = make_consts()
    shared = {k: f(inputs[k]).astype(np.float32, copy=False) for k in
              ("norm_g", "w_in", "w_out", "moba_q_norm", "moba_k_norm", "hgrn_lb_logits", "hgrn_o_norm",
               "mem_norm_g", "w_mem_kv", "mem_q_norm", "mem_k_norm")}
    x = f(inputs["x"]); mem = f(inputs["mem"]); pos = f(inputs["positions"]).astype(np.int32, copy=False)
    in_maps = []
    for b in range(cores):
        m = dict(shared)
        m["x"] = x[b]
        m["mem"] = mem[b]
        m["pos"] = pos[b].reshape(16, 128)
        m["cst"] = cst
        in_maps.append(m)
    res = run_bass_kernel_spmd(nc, in_maps, core_ids=list(range(cores)))
    return np.stack([np.asarray(r["out"]) for r in res.results], axis=0)


def kernel(x, mem, positions, norm_g, w_in, w_out, moba_q_norm, moba_k_norm, hgrn_lb_logits,
           hgrn_o_norm, mem_norm_g, w_mem_kv, mem_q_norm, mem_k_norm):
    inputs = dict(x=x, mem=mem, positions=positions, norm_g=norm_g, w_in=w_in, w_out=w_out,
                  moba_q_norm=moba_q_norm, moba_k_norm=moba_k_norm, hgrn_lb_logits=hgrn_lb_logits,
                  hgrn_o_norm=hgrn_o_norm, mem_norm_g=mem_norm_g, w_mem_kv=w_mem_kv,
                  mem_q_norm=mem_q_norm, mem_k_norm=mem_k_norm)
    return run(inputs).astype(np.float32, copy=False)
```

```python
import numpy as np
from contextlib import ExitStack
import concourse.bass as bass
import concourse.mybir as mybir
from concourse.bass_utils import run_bass_kernel_spmd

F32 = mybir.dt.float32
BF16 = mybir.dt.bfloat16
I32 = mybir.dt.int32
AF = mybir.ActivationFunctionType
ALU = mybir.AluOpType
AX = mybir.AxisListType

S = 2048
D = 1024
NT = 16
EPS = 1e-6
NCST = 576
import os as _os0
ALL_GROUPS = tuple(_os0.environ.get("ORDER", "A0,A1,H0,H1,M0,M1").split(","))
import os as _os
SCHED = _os.environ.get("SCHED", "1") == "1"
PAR1 = int(_os.environ.get("PAR1", "1"))
PAR2 = int(_os.environ.get("PAR2", "1"))
GPAR = int(_os.environ.get("GPAR", "1"))
PS3 = int(_os.environ.get("PS3", "3"))
MASKV = int(_os.environ.get("MASKV", "1"))
OPB = int(_os.environ.get("OPB", "1"))
SCHED2 = int(_os.environ.get("SCHED2", "1"))
SDELTA = float(_os.environ.get("SDELTA", "100"))
LATX = float(_os.environ.get("LATX", "180"))
PEK = float(_os.environ.get("PEK", "0.65"))
ACTK = float(_os.environ.get("ACTK", "1.0"))
DVEK = float(_os.environ.get("DVEK", "1.0"))
POOLK = float(_os.environ.get("POOLK", "1.0"))
LATS = float(_os.environ.get("LATS", "60"))
XQ = int(_os.environ.get("XQ", "0"))
GOFF = int(_os.environ.get("GOFF", "0"))
TRANS = int(_os.environ.get("TRANS", "1"))


class Res:
    __slots__ = ("name", "w", "r", "excl")

    def __init__(self, name, excl=False):
        self.name = name
        self.w = None
        self.r = []
        self.excl = excl


class _Rec:
    def __init__(self):
        self.name = None
        self.args = ()
        self.kw = {}

    def __getattr__(self, name):
        def f(*a, **k):
            self.name, self.args, self.kw = name, a, k
            return self
        return f


def _free_size(ap):
    n = 1
    for d in list(ap.shape)[1:]:
        n *= int(d)
    return n


class Op:
    __slots__ = ("eng", "fn", "deps", "sdeps", "sig", "idx", "dma", "sem", "val", "i", "cost", "start")

    def __init__(self, eng, fn, deps, sdeps, dma):
        self.eng = eng
        self.fn = fn
        self.deps = deps
        self.sdeps = sdeps
        self.sig = False
        self.idx = 0
        self.dma = dma
        self.sem = None
        self.val = 0
        self.i = 0
        self.start = 0.0
        rec = _Rec()
        fn(rec)
        out = rec.kw.get("out", rec.args[0] if rec.args else None)
        n = _free_size(out) if out is not None else 64
        if dma:
            c = 2000.0 + n * int(out.shape[0]) * 4 / 120.0
        elif eng == "tensor":
            if rec.name == "transpose":
                c = 110.0
            else:
                lhsT = rec.kw.get("lhsT")
                f32 = lhsT is not None and lhsT.dtype == F32
                c = PEK * (64.0 + max(n, 64) / 2.0) * (4.0 if f32 else 1.0)
        elif eng == "scalar":
            c = ACTK * (200.0 + n / 1.2)
        elif eng == "vector":
            c = DVEK * (120.0 + n / 0.96 * (8.0 if rec.name == "reciprocal" else 1.0))
        else:
            c = POOLK * (300.0 + n / 0.5)
        self.cost = c


class Prog:
    ENGS = ["tensor", "vector", "scalar", "gpsimd", "sync"]

    def __init__(self, nc):
        self.nc = nc
        self.ops = []

    phase = None
    tok = None
    tokset = ()

    def op(self, eng, fn, reads=(), writes=(), dma=False):
        if self.phase == "gate" and self.tok is not None:
            writes = list(writes) + [self.tok]
        elif self.phase == "chain" and eng in self.tokset:
            reads = list(reads) + [self.tok]
        deps, sdeps = {}, {}

        def add(d):
            if d.dma or dma or d.eng != eng or eng != "tensor":
                deps[id(d)] = d
            else:
                sdeps[id(d)] = d
        for r in reads:
            if r.w is not None:
                add(r.w)
            if r.excl:
                for d in r.r:
                    if d.eng != eng:
                        add(d)
        for w in writes:
            if w.w is not None:
                add(w.w)
            for d in w.r:
                add(d)
        o = Op(eng, fn, list(deps.values()), list(sdeps.values()), dma)
        for r in reads:
            r.r.append(o)
        for w in writes:
            w.w = o
            w.r = []
        self.ops.append(o)
        return o

    def schedule(self):
        import heapq
        ops = self.ops
        for i, o in enumerate(ops):
            o.i = i
        succs = [[] for _ in ops]
        npred = [0] * len(ops)
        for o in ops:
            ds = o.deps + o.sdeps
            npred[o.i] = len(ds)
            for d in ds:
                succs[d.i].append(o)
        ready = [0.0] * len(ops)
        free = {e: 0.0 for e in self.ENGS}
        heap = [(0.0, o.i) for o in ops if npred[o.i] == 0]
        heapq.heapify(heap)
        done = 0
        while heap:
            t, i = heapq.heappop(heap)
            o = ops[i]
            st = max(ready[i], free[o.eng])
            if st > t + 1e-9:
                heapq.heappush(heap, (st, i))
                continue
            o.start = st
            if o.dma:
                free[o.eng] = st + 150.0
            else:
                free[o.eng] = st + o.cost
            fin = st + o.cost
            done += 1
            for sc in succs[i]:
                lat = 60.0 if (sc.eng == o.eng and not o.dma) else 180.0
                if fin + lat > ready[sc.i]:
                    ready[sc.i] = fin + lat
                npred[sc.i] -= 1
                if npred[sc.i] == 0:
                    heapq.heappush(heap, (max(ready[sc.i], free[sc.eng]), sc.i))
        assert done == len(ops), (done, len(ops))
        self.ops = sorted(ops, key=lambda o: (o.start, o.i))
        self.est_ns = max(o.start + o.cost for o in ops)

    def schedule2(self, delta=120.0):
        ops = self.ops
        n = len(ops)
        for i, o in enumerate(ops):
            o.i = i
        succs = [[] for _ in ops]
        npred = [0] * n
        for o in ops:
            ds = o.deps + o.sdeps
            npred[o.i] = len(ds)
            for d in ds:
                succs[d.i].append(o)
        blev = [0.0] * n
        for o in reversed(ops):
            b = 0.0
            for sc in succs[o.i]:
                lat = LATS if (sc.eng == o.eng and not o.dma) else LATX
                v = lat + blev[sc.i]
                if v > b:
                    b = v
            blev[o.i] = b + o.cost
        ready = [0.0] * n
        free = {e: 0.0 for e in self.ENGS}
        rsets = {e: [] for e in self.ENGS}
        for o in ops:
            if npred[o.i] == 0:
                rsets[o.eng].append(o.i)
        done = 0
        while done < n:
            best_e, best_t = None, 1e30
            for e in self.ENGS:
                rs = rsets[e]
                if not rs:
                    continue
                t = min(ready[i] for i in rs)
                if t < free[e]:
                    t = free[e]
                if t < best_t:
                    best_t, best_e = t, e
            e = best_e
            rs = rsets[e]
            lim = best_t + delta
            pick, pb = -1, -1.0
            for i in rs:
                if ready[i] <= lim and blev[i] > pb:
                    pb, pick = blev[i], i
            rs.remove(pick)
            o = ops[pick]
            st = max(ready[pick], free[e])
            o.start = st
            free[e] = st + (150.0 if o.dma else o.cost)
            fin = st + o.cost
            done += 1
            for sc in succs[pick]:
                lat = LATS if (sc.eng == o.eng and not o.dma) else LATX
                if fin + lat > ready[sc.i]:
                    ready[sc.i] = fin + lat
                npred[sc.i] -= 1
                if npred[sc.i] == 0:
                    rsets[sc.eng].append(sc.i)
        self.ops = sorted(ops, key=lambda o: (o.start, o.i))
        self.est_ns = max(o.start + o.cost for o in ops)

    def emit(self, stack, final_deps, ndma_sems=8):
        nc = self.nc
        for o in self.ops:
            for d in o.deps:
                d.sig = True
        for d in final_deps:
            d.sig = True
        sems = {e: stack.enter_context(nc.semaphore("s_" + e)) for e in self.ENGS}
        cnt = {e: 0 for e in self.ENGS}
        pools, pool_i, pre_wait = {}, {}, {}
        for o in self.ops:
            if o.dma:
                if o.eng not in pools:
                    pools[o.eng] = [[stack.enter_context(nc.semaphore("d_%s_%d" % (o.eng, i))), 0]
                                    for i in range(ndma_sems)]
                    pool_i[o.eng] = 0
                p = pools[o.eng][pool_i[o.eng] % ndma_sems]
                pool_i[o.eng] += 1
                if p[1] > 0:
                    pre_wait[id(o)] = (p[0], p[1])
                p[1] += 16
                o.sem = p[0]
                o.val = p[1]
            elif o.sig:
                cnt[o.eng] += 1
                o.idx = cnt[o.eng]
        per = {e: [o for o in self.ops if o.eng == e] for e in self.ENGS}
        self.stats = {e: len(per[e]) for e in self.ENGS}
        known = {e: {} for e in self.ENGS}
        kn = {}
        plan = {}
        nw = 0

        def semkey(d):
            return (d.sem, d.val) if d.dma else (sems[d.eng], d.idx)

        for o in self.ops:
            kd = known[o.eng]
            ws = []
            for d in o.deps:
                sm, val = semkey(d)
                if kd.get(id(sm), (None, 0))[1] < val:
                    ws.append((sm, val))
                    kd[id(sm)] = (sm, val)
                if TRANS:
                    for k2, (s2, v2) in kn[id(d)].items():
                        if kd.get(k2, (None, 0))[1] < v2:
                            kd[k2] = (s2, v2)
            if o.dma:
                pw = pre_wait.get(id(o))
                if pw and kd.get(id(pw[0]), (None, 0))[1] < pw[1]:
                    ws.append(pw)
                    kd[id(pw[0])] = pw
            plan[id(o)] = ws
            nw += len(ws)
            if o.dma or o.sig:
                mine = dict(kd)
                sm, val = semkey(o)
                mine[id(sm)] = (sm, val)
                kn[id(o)] = mine
        fin_w = []
        kd = known["sync"]
        for d in final_deps:
            sm, val = semkey(d)
            if kd.get(id(sm), (None, 0))[1] < val:
                fin_w.append((sm, val))
                kd[id(sm)] = (sm, val)
        self.stats["waits"] = nw
        block = stack.enter_context(nc.Block())

        def mk(e):
            def body(engobj):
                for o in per[e]:
                    for sm, val in plan[id(o)]:
                        engobj.wait_ge(sm, val)
                    if o.dma:
                        o.fn(engobj).then_inc(o.sem, 16)
                    else:
                        ins = o.fn(engobj)
                        if o.sig:
                            ins.then_inc(sems[e], 1)
                if e == "sync":
                    for sm, val in fin_w:
                        engobj.wait_ge(sm, val)
            return body

        block.tensor(mk("tensor"))
        block.vector(mk("vector"))
        block.scalar(mk("scalar"))
        block.gpsimd(mk("gpsimd"))
        block.sync(mk("sync"))


def make_consts():
    c = np.zeros((128, NCST), np.float32)
    i = np.arange(128)
    c[:, 0:128] = np.eye(128)
    c[:, 128:256] = (i[None, :] >= i[:, None])
    same = (i[:, None] // 64) == (i[None, :] // 64)
    c[:, 256:384] = same & (i[:, None] <= i[None, :])
    c[:, 384:512] = same & (i[:, None] > i[None, :])
    c[:, 512] = i < 64
    c[:, 513] = i >= 64
    c[:, 514] = 1.0
    f64 = 500000.0 ** (-np.arange(8, dtype=np.float64) / 8.0)
    f = f64.astype(np.float32)
    flo = (f64 - f.astype(np.float64)).astype(np.float32)
    c[:, 515:523] = f[None, :]
    c[:, 523:531] = f[None, :]
    c[:, 547:555] = flo[None, :]
    c[:, 555:563] = flo[None, :]
    c[:, 531:539] = 0.0
    c[:, 539:547] = np.pi / 2
    return c


def build_nc(layers=(0, 1), groups=ALL_GROUPS):
    nc = bass.Bass("TRN2", target_bir_lowering=False)

    def din(name, shape, d=F32):
        return nc.dram_tensor(name, shape, d, kind="ExternalInput").ap()

    x_d = din("x", [S, D])
    mem_d = din("mem", [256, D])
    pos_d = din("pos", [16, 128], I32)
    ng_d = din("norm_g", [2, D])
    win_d = din("w_in", [2, D, 5120])
    wout_d = din("w_out", [2, 1536, D])
    gq_d = din("moba_q_norm", [2, 64])
    gk_d = din("moba_k_norm", [2, 64])
    lbl_d = din("hgrn_lb_logits", [2, 512])
    go_d = din("hgrn_o_norm", [2, 128])
    mng_d = din("mem_norm_g", [2, D])
    wkv_d = din("w_mem_kv", [2, D, 1024])
    gmq_d = din("mem_q_norm", [2, 128])
    gmk_d = din("mem_k_norm", [2, 128])
    cst_d = din("cst", [128, NCST])
    out_d = nc.dram_tensor("out", [S, D], F32, kind="ExternalOutput").ap()

    P = Prog(nc)
    if _os.environ.get("TOKR"):
        P.tok = Res("tok")
        P.tokset = tuple(_os.environ["TOKR"].split(","))
    with ExitStack() as st:
        def sb(name, shape, dt=F32):
            return st.enter_context(nc.sbuf_tensor("sb_" + name, shape, dt))

        def ps(name, shape, dt=F32):
            return st.enter_context(nc.psum_tensor("pp_" + name, shape, dt))

        def T(fn, r=(), w=()):
            return P.op("tensor", fn, r, w)

        def V(fn, r=(), w=()):
            return P.op("vector", fn, r, w)

        def A(fn, r=(), w=()):
            return P.op("scalar", fn, r, w)

        def G(fn, r=(), w=()):
            return P.op("gpsimd", fn, r, w)

        def DMA(q, fn, r=(), w=()):
            return P.op(q, fn, r, w, dma=True)

        x_tok = sb("x_tok", [128, NT, D]); rx = [Res("x%d" % i) for i in range(NT)]
        hT = sb("hT", [128, 8, S], BF16); rhT = [Res("hT%d" % i) for i in range(NT)]
        cst = sb("cst", [128, NCST]); rcst = Res("cst")
        idb = sb("idb", [128, 128], BF16); ridb = Res("idb")
        trib = sb("trib", [128, 128], BF16); rtrib = Res("trib")
        hgmb = sb("hgmb", [128, 128], BF16); rhgmb = Res("hgmb")
        gq_bc = sb("gq_bc", [128, 64]); gk_bc = sb("gk_bc", [128, 64]); go_bc = sb("go_bc", [128, 128])
        gmq_bc = sb("gmq_bc", [128, 128]); gmk_bc = sb("gmk_bc", [128, 128]); rgains = Res("gains")
        cs = sb("cs", [128, NT, 16]); sn = sb("sn", [128, NT, 16]); rrope = Res("rope")
        wb = [sb("wb%d" % i, [128, 8, 512], BF16) for i in range(3)]; rwb = [[Res("wb%da" % i), Res("wb%db" % i)] for i in range(3)]
        wo = sb("wo", [128, 2, D], BF16); rwo = Res("wo")
        Fp = [sb("F%d" % i, [128, 256]) for i in range(9)]; rF = [Res("F%d" % i) for i in range(9)]
        Bp = [sb("B%d" % i, [128, 256], BF16) for i in range(7)]; rB = [Res("B%d" % i) for i in range(7)]
        szs = [sb("sz%d" % k, [128, 4, 256]) for k in range(2)]; rszs = [[Res("sz%d_%d" % (k, i)) for i in range(4)] for k in range(2)]
        y_tok = sb("y_tok", [128, 4, 256], BF16); ry = [Res("y%d" % i) for i in range(4)]
        yT = [sb("yT%d" % i, [128, 256], BF16) for i in range(2)]; ryT = [Res("yT%d" % i) for i in range(2)]
        pT = [sb("pT%d" % i, [128, 512], BF16) for i in range(4)]; rpT = [Res("pT%d" % i) for i in range(4)]
        hbs = [sb("hb%d" % i, [128, D], BF16) for i in range(2)]; rhbs = [Res("hb0"), Res("hb1")]
        small = sb("small", [128, 128]); rsm = {}

        def sm(name, a, n):
            rsm[name] = Res("sm_" + name)
            return small[:, a:a + n], rsm[name]
        ss16, r_ss16 = sm("ss16", 0, 16)
        t16, r_t16 = sm("t16", 16, 16)
        rstd16, r_rstd16 = sm("rstd16", 32, 16)
        ss4, r_ss4 = sm("ss4", 48, 4)
        t4, r_t4 = sm("t4", 52, 4)
        rs4, r_rs4 = sm("rs4", 56, 4)
        rden, r_rden = sm("rden", 60, 4)
        dec4, r_dec4 = sm("dec4", 64, 4)
        sso, r_sso = sm("sso", 68, 2)
        to2, r_to2 = sm("to2", 70, 2)
        rso, r_rso = sm("rso", 72, 2)
        gm = sb("gm", [128, 4, 8]); rgm = Res("gm")
        top8 = sb("top8", [128, 4, 8]); rtop8 = Res("top8")
        selb = sb("selb", [128, 4, 8]); rselb = Res("selb")
        kT = sb("kT", [128, 4, S], BF16); rkT = [Res("kT%d" % i) for i in range(NT)]
        v_flat = sb("v_aug", [128, NT * 4 * 65], BF16); rv = [Res("v%d" % i) for i in range(NT)]
        v_aug = v_flat[:].rearrange("p (a b c) -> p a b c", a=NT, b=4)
        stage = v_flat[:, 0:2048].bitcast(F32); rstage = Res("stage")
        g_bcv = v_flat[:, 2048:4096].bitcast(F32); rg_bc = Res("g_bc")
        qTs = [sb("qT%d" % i, [128, 2048], BF16) for i in range(2)]; rqTs = [Res("qT0"), Res("qT1")]
        lbv = qTs[1][:, 0:2048].bitcast(F32); rlb = rqTs[1]
        lb_g = lbv[:, 0:256]; oml_g = lbv[:, 256:512]
        k_aug = sb("k_aug", [128, 4, 72], BF16); rk_aug = Res("k_aug")
        q_aug = sb("q_aug", [128, 4, 72], BF16); rq_aug = Res("q_aug")
        kmT32 = sb("kmT32", [128, 2, 2, 8]); rkm = Res("kmT32")
        S32 = sb("S32", [128, 2, 128]); rS32 = [Res("S32_0"), Res("S32_1")]
        Sbf = sb("Sbf", [128, 2, 2, 128], BF16); rSbf = [[Res("Sbf00"), Res("Sbf01")], [Res("Sbf10"), Res("Sbf11")]]
        qTA = sb("qTA", [128, 2, 128], BF16); qTB = sb("qTB", [128, 2, 128], BF16)
        rqTA = Res("qTA"); rqTB = Res("qTB")
        memT = sb("memT", [128, 8, 256], BF16); rmemT = Res("memT")
        kmT = sb("kmT", [128, 2, 256], BF16); rkmT = Res("kmT")
        vm_aug = sb("vm_aug", [128, 2, 2, 129], BF16); rvm = Res("vm")
        psA = [ps("psA%d" % i, [128, 512]) for i in range(2)]; rpsA = [Res("psA0", True), Res("psA1", True)]
        psT = ps("psT", [128, 1024], BF16); rpsT = Res("psT", True)
        psG = ps("psG", [128, 512]); rpsG = Res("psG", True)
        psS = [ps("psS%d" % i, [128, 512]) for i in range(2)]; rpsS = [Res("psS0", True), Res("psS1", True)]
        psO = [ps("psO%d" % i, [128, 512]) for i in range(2)]; rpsO = [Res("psO0", True), Res("psO1", True)]

        ctr = {"pa": 0, "w": 0, "ev": 0, "ps": 0, "pt": 0, "yt": 0, "mk": 0}

        def nxt(k, n):
            v = ctr[k] % n
            ctr[k] += 1
            return v

        def evac(fn_v, fn_a, r, w):
            if nxt("ev", 2) == 0:
                return A(fn_a, r, w)
            return V(fn_v, r, w)

        DMA("sync", lambda e: e.dma_start(out=cst[:], in_=cst_d), w=[rcst])
        V(lambda e: e.tensor_copy(out=idb[:], in_=cst[:, 0:128]), [rcst], [ridb])
        V(lambda e: e.tensor_copy(out=trib[:], in_=cst[:, 128:256]), [rcst], [rtrib])
        V(lambda e: e.tensor_copy(out=hgmb[:], in_=cst[:, 256:384]), [rcst], [rhgmb])
        ident32 = cst[:, 0:128]
        G(lambda e: e.memset(vm_aug[:, :, :, 128:129], 1.0), w=[rvm])
        G(lambda e: e.memset(qTA[:], 0.0), w=[rqTA])
        G(lambda e: e.memset(qTB[:], 0.0), w=[rqTB])
        nI = sb("nI", [128, 256], I32); rnI = Res("nI")
        posi = nI[0:16, 0:128]; rposi = rnI
        posf = Fp[8][0:16, 0:128]; rposf = rF[8]
        DMA("sync", lambda e: e.dma_start(out=posi, in_=pos_d), w=[rposi])
        V(lambda e: e.tensor_copy(out=posf, in_=posi), [rposi], [rposf])
        T(lambda e: e.matmul(psG[:, 0:16], lhsT=posf, rhs=cst[0:16, 0:16], start=True, stop=True), [rposf, rcst], [rpsG])
        post, r_post = sm("post", 80, 16)
        V(lambda e: e.tensor_copy(out=post, in_=psG[:, 0:16]), [rpsG], [r_post])
        ang = Fp[0][:, 0:256].rearrange("p (i j) -> p i j", i=NT)
        V(lambda e: e.tensor_tensor(out=ang, in0=post.unsqueeze(2).to_broadcast([128, NT, 16]),
                                    in1=cst[:, 515:531].unsqueeze(1).to_broadcast([128, NT, 16]), op=ALU.mult),
          [r_post, rcst], [rF[0]])
        ang_lo = Fp[1][:, 0:256].rearrange("p (i j) -> p i j", i=NT)
        V(lambda e: e.tensor_tensor(out=ang_lo, in0=post.unsqueeze(2).to_broadcast([128, NT, 16]),
                                    in1=cst[:, 547:563].unsqueeze(1).to_broadcast([128, NT, 16]), op=ALU.mult),
          [r_post, rcst], [rF[1]])
        V(lambda e: e.tensor_tensor(out=ang, in0=ang, in1=ang_lo, op=ALU.add), [rF[0], rF[1]], [rF[0]])
        V(lambda e: e.tensor_tensor(out=ang, in0=ang, in1=cst[:, 531:547].unsqueeze(1).to_broadcast([128, NT, 16]),
                                    op=ALU.add), [rF[0], rcst], [rF[0]])
        V(lambda e: e.tensor_scalar(out=Fp[1][:], in0=Fp[0][:], scalar1=float(1.0 / (2 * np.pi)), scalar2=None,
                                    op0=ALU.mult), [rF[0]], [rF[1]])
        V(lambda e: e.tensor_copy(out=nI[:], in_=Fp[1][:]), [rF[1]], [rnI])
        V(lambda e: e.tensor_copy(out=Fp[1][:], in_=nI[:]), [rnI], [rF[1]])
        C1 = 6.28125
        C2 = float(2 * np.pi - 6.28125)
        V(lambda e: e.scalar_tensor_tensor(out=Fp[2][:], in0=Fp[1][:], scalar=-C1, in1=Fp[0][:],
                                           op0=ALU.mult, op1=ALU.add), [rF[1], rF[0]], [rF[2]])
        V(lambda e: e.scalar_tensor_tensor(out=Fp[2][:], in0=Fp[1][:], scalar=-C2, in1=Fp[2][:],
                                           op0=ALU.mult, op1=ALU.add), [rF[1], rF[2]], [rF[2]])
        V(lambda e: e.tensor_scalar(out=Fp[2][:], in0=Fp[2][:], scalar1=float(np.pi), scalar2=float(-np.pi),
                                    op0=ALU.min, op1=ALU.max), [rF[2]], [rF[2]])
        A(lambda e: e.activation(out=Fp[3][:], in_=Fp[2][:], func=AF.Sin), [rF[2]], [rF[3]])
        sc = Fp[3][:, 0:256].rearrange("p (i j) -> p i j", i=NT)
        V(lambda e: e.tensor_copy(out=cs[:, :, 0:8], in_=sc[:, :, 8:16]), [rF[3]], [rrope])
        V(lambda e: e.tensor_copy(out=cs[:, :, 8:16], in_=sc[:, :, 8:16]), [rF[3]], [rrope])
        V(lambda e: e.tensor_scalar(out=sn[:, :, 0:8], in0=sc[:, :, 0:8], scalar1=-1.0, scalar2=None, op0=ALU.mult),
          [rF[3]], [rrope])
        V(lambda e: e.tensor_copy(out=sn[:, :, 8:16], in_=sc[:, :, 0:8]), [rF[3]], [rrope])
        def load_w(dst, rdst, src_ap):
            return DMA("gpsimd", lambda e: e.dma_start(out=dst, in_=src_ap), w=[rdst])

        def w_in_cols(l, c0, n):
            return win_d[l].rearrange("(c p) n -> p c n", p=128)[:, :, c0:c0 + n]

        psGb = psG[:].bitcast(BF16)

        def rms_to_T(src_tile, rsrc, gain, rgain, dstT, rdst, col0, ssc, r_ssc, tsc, r_tsc, rsc, r_rsc, k):
            hb = hbs[k % 2]; rhb = rhbs[k % 2]
            pst, rpst = (psT, rpsT) if k % 2 == 0 else (psGb, rpsG)
            A(lambda e: e.activation(out=hb[:], in_=src_tile, func=AF.Square, accum_out=ssc[:, k:k + 1]),
              [rsrc], [rhb, r_ssc])
            A(lambda e: e.activation(out=tsc[:, k:k + 1], in_=ssc[:, k:k + 1], func=AF.Ln, scale=1.0 / D, bias=EPS),
              [r_ssc], [r_tsc])
            A(lambda e: e.activation(out=rsc[:, k:k + 1], in_=tsc[:, k:k + 1], func=AF.Exp, scale=-0.5),
              [r_tsc], [r_rsc])
            V(lambda e: e.scalar_tensor_tensor(out=hb[:], in0=src_tile, scalar=rsc[:, k:k + 1], in1=gain,
                                               op0=ALU.mult, op1=ALU.mult), [rsrc, r_rsc, rgain], [rhb])
            for c in range(8):
                T(lambda e, c=c: e.transpose(out=pst[:, c * 128:(c + 1) * 128], in_=hb[:, c * 128:(c + 1) * 128],
                                             identity=idb[:]), [rhb, ridb], [rpst])
            src3 = pst[:, 0:1024].rearrange("p (c t) -> p c t", c=8)
            evac(lambda e: e.tensor_copy(out=dstT[:, :, col0:col0 + 128], in_=src3),
                 lambda e: e.copy(out=dstT[:, :, col0:col0 + 128], in_=src3), [rpst], [rdst])

        def proj(lhs_cols, rlhs, wt, rwt, ncols=512, lhsT_src=None):
            b = nxt("pa", 2)
            src = hT if lhsT_src is None else lhsT_src
            for c in range(8):
                T(lambda e, c=c: e.matmul(psA[b][:, 0:ncols], lhsT=src[:, c, lhs_cols:lhs_cols + 128],
                                          rhs=wt[:, c, 0:ncols], start=(c == 0), stop=(c == 7)),
                  [rlhs] + list(rwt), [rpsA[b]])
            return psA[b], rpsA[b]

        def headnorm(src_ps, rps, H, Dh, gain, outF, routF, tmpA, rtmpA, tmpB, rtmpB):
            n = H * Dh
            A(lambda e: e.activation(out=tmpA[:, 0:n], in_=src_ps, func=AF.Square), [rps], [rtmpA])
            V(lambda e: e.tensor_reduce(out=ss4[:, 0:H], in_=tmpA[:, 0:n].rearrange("p (h d) -> p h d", h=H),
                                        axis=AX.X, op=ALU.add), [rtmpA], [r_ss4])
            A(lambda e: e.activation(out=t4[:, 0:H], in_=ss4[:, 0:H], func=AF.Ln, scale=1.0 / Dh, bias=EPS),
              [r_ss4], [r_t4])
            A(lambda e: e.activation(out=rs4[:, 0:H], in_=t4[:, 0:H], func=AF.Exp, scale=-0.5), [r_t4], [r_rs4])
            V(lambda e: e.tensor_tensor(out=tmpB[:, 0:n].rearrange("p (h d) -> p h d", h=H),
                                        in0=src_ps.rearrange("p (h d) -> p h d", h=H),
                                        in1=rs4[:, 0:H].unsqueeze(2).to_broadcast([128, H, Dh]), op=ALU.mult),
              [rps, r_rs4], [rtmpB])
            (G if GOFF else V)(lambda e: e.tensor_tensor(out=outF[:, 0:n].rearrange("p (h d) -> p h d", h=H),
                                                         in0=tmpB[:, 0:n].rearrange("p (h d) -> p h d", h=H),
                                                         in1=gain.unsqueeze(1).to_broadcast([128, H, Dh]), op=ALU.mult),
                               [rtmpB, rgains], [routF])

        def rope(Fx, rFx, i, tR, rtR):
            x3 = Fx[:, 0:256].rearrange("p (h d) -> p h d", h=4)
            a3 = tR[:, 0:64].rearrange("p (h d) -> p h d", h=4)
            b3 = tR[:, 64:128].rearrange("p (h d) -> p h d", h=4)
            rtA = rtR
            rtB = rtR
            G(lambda e: e.tensor_tensor(out=a3, in0=x3[:, :, 0:16], in1=cs[:, i, :].unsqueeze(1).to_broadcast([128, 4, 16]),
                                        op=ALU.mult), [rFx, rrope], [rtA])
            G(lambda e: e.tensor_tensor(out=b3[:, :, 0:8], in0=x3[:, :, 8:16],
                                        in1=sn[:, i, 0:8].unsqueeze(1).to_broadcast([128, 4, 8]), op=ALU.mult),
              [rFx, rrope], [rtB])
            G(lambda e: e.tensor_tensor(out=b3[:, :, 8:16], in0=x3[:, :, 0:8],
                                        in1=sn[:, i, 8:16].unsqueeze(1).to_broadcast([128, 4, 8]), op=ALU.mult),
              [rFx, rrope], [rtB])
            G(lambda e: e.tensor_tensor(out=x3[:, :, 0:16], in0=a3, in1=b3, op=ALU.add), [rtA, rtB], [rFx])

        def silu_ps(dst, rdst, src_ps, rps):
            A(lambda e: e.activation(out=dst, in_=src_ps, func=AF.Exp, scale=-1.0), [rps], [rdst])
            A(lambda e: e.activation(out=dst, in_=dst, func=AF.Ln, bias=1.0), [rdst], [rdst])
            A(lambda e: e.activation(out=dst, in_=dst, func=AF.Exp, scale=-1.0), [rdst], [rdst])
            V(lambda e: e.tensor_tensor(out=dst, in0=src_ps, in1=dst, op=ALU.mult), [rps, rdst], [rdst])

        SC = [((Fp[0], rF[0]), (Fp[1], rF[1]), (Fp[2], rF[2]), (Fp[3], rF[3])),
              ((Fp[4], rF[4]), (Fp[6], rF[6]), (Fp[7], rF[7]), (Fp[8], rF[8]))]
        AUG = [(k_aug, rk_aug), (q_aug, rq_aug)]
        if _os.environ.get("NOPAR", "0") == "1":
            SC[1] = SC[0]
        if _os.environ.get("NOAUG", "0") == "1":
            AUG[1] = AUG[0]

        dec8, _r = sm("dec8", 96, 8)
        r_dec8 = [Res("dec8a"), Res("dec8b")]
        sso4, _r2 = sm("sso4", 104, 4)
        r_sso2 = [Res("ssoA"), Res("ssoB")]
        dummy = sb("dummy", [128, 8])
        kflat = kT[:].rearrange("p h t -> p (h t)")
        kf32 = kflat[:, 0:4608].bitcast(F32)
        Fq = [kf32[:, k * 256:(k + 1) * 256] for k in range(9)]
        rFq = [Res("Fq%d" % k) for k in range(9)]
        Bq = [kflat[:, 4608 + k * 256:4608 + (k + 1) * 256] for k in range(7)]
        rBq = [Res("Bq%d" % k) for k in range(7)]
        HSETS = [([t[:] for t in Fp], rF, [t[:] for t in Bp], rB), (Fq, rFq, Bq, rBq)]

        def barrier():
            G(lambda e: e.memset(dummy[:], 0.0), w=list(rkT) + rFq + rBq + list(rv) + [rstage, rg_bc])

        rrow = Res("pe_rowfence")

        out_dmas = []

        def outproj_tile(i, r, last, obanks=None):
            yb = nxt("yt", 2)
            for pp in range(2):
                T(lambda e, pp=pp: e.transpose(out=psT[:, pp * 128:(pp + 1) * 128], in_=y_tok[:, r, pp * 128:(pp + 1) * 128],
                                               identity=idb[:]), [ry[r], ridb], [rpsT])
            evac(lambda e: e.tensor_copy(out=yT[yb][:], in_=psT[:, 0:256]),
                 lambda e: e.copy(out=yT[yb][:], in_=psT[:, 0:256]), [rpsT], [ryT[yb]])
            for half in range(2):
                if obanks is None:
                    b = nxt("pa", 2)
                    pso, rpso = psA[b], rpsA[b]
                else:
                    pso, rpso = obanks[half]
                for pp in range(2):
                    T(lambda e, pp=pp, half=half, pso=pso: e.matmul(pso[:, 0:512], lhsT=yT[yb][:, pp * 128:(pp + 1) * 128],
                                                                    rhs=wo[:, pp, half * 512:(half + 1) * 512],
                                                                    start=(pp == 0), stop=(pp == 1)),
                      [ryT[yb], rwo], [rpso])
                V(lambda e, half=half, pso=pso: e.tensor_tensor(out=x_tok[:, i, half * 512:(half + 1) * 512], in0=pso[:, 0:512],
                                                                in1=x_tok[:, i, half * 512:(half + 1) * 512], op=ALU.add),
                  [rpso, rx[i]], [rx[i]])
            if last:
                out_dmas.append(DMA("sync", lambda e: e.dma_start(out=out_d[i * 128:(i + 1) * 128, :], in_=x_tok[:, i, :]),
                                    r=[rx[i]]))

        def attn_chunk(Q, H, Dh, KP, scale, key_tiles, causal, kTsrc, rkTsrc, vsrc, rvsrc, qview, rqT, sz, rsz):
            DA = Dh + 1
            for h in range(H):
                if Dh == 64:
                    o_b = h % 2
                    banks = [o_b, o_b, o_b, o_b]
                    offs = [0, DA, 2 * DA, 3 * DA]
                else:
                    banks = [0, 0, 1, 1]
                    offs = [0, DA, 0, DA]
                started = set()
                kts = key_tiles(Q)
                for kt in kts:
                    j = kt - 4 * Q if causal else -1
                    q0 = max(j, 0) * 128
                    N = 512 - q0
                    sbk = nxt("ps", PS3)
                    pss, rpss = ((psS[0], rpsS[0]), (psS[1], rpsS[1]), (psG, rpsG))[sbk]
                    T(lambda e, kt=kt, h=h, q0=q0, N=N, pss=pss: e.matmul(
                        pss[:, 0:N], lhsT=kTsrc(h, kt), rhs=qview(h)[:, q0:512], start=True, stop=True),
                      [rkTsrc(kt), rqT], [rpss])
                    pb = nxt("pt", 4)
                    A(lambda e, N=N, pss=pss, pb=pb: e.activation(out=pT[pb][:, 0:N], in_=pss[:, 0:N], func=AF.Exp,
                                                                  scale=scale), [rpss], [rpT[pb]])
                    if j >= 0:
                        (V if (MASKV and nxt("mk", 2) == 0) else G)(
                            lambda e, pb=pb: e.tensor_tensor(out=pT[pb][:, 0:128], in0=pT[pb][:, 0:128], in1=trib[:],
                                                             op=ALU.mult), [rpT[pb], rtrib], [rpT[pb]])
                    for r in range(max(j, 0), 4):
                        bk = banks[r]
                        first = bk not in started
                        started.add(bk)
                        T(lambda e, r=r, kt=kt, h=h, q0=q0, pb=pb, bk=bk, first=first: e.matmul(
                            psO[bk][:, offs[r]:offs[r] + DA], lhsT=pT[pb][:, r * 128 - q0:r * 128 - q0 + 128],
                            rhs=vsrc(h, kt), start=first, stop=False, skip_group_check=True),
                          [rpT[pb], rvsrc(kt)], [rpsO[bk]])
                for r in range(4):
                    bk = banks[r]
                    V(lambda e, r=r, bk=bk: e.reciprocal(out=rden[:, r:r + 1], in_=psO[bk][:, offs[r] + Dh:offs[r] + DA]),
                      [rpsO[bk]], [r_rden])
                    V(lambda e, r=r, bk=bk, h=h: e.scalar_tensor_tensor(
                        out=y_tok[:, r, h * Dh:(h + 1) * Dh], in0=psO[bk][:, offs[r]:offs[r] + Dh], scalar=rden[:, r:r + 1],
                        in1=sz[:, r, h * Dh:(h + 1) * Dh], op0=ALU.mult, op1=ALU.mult),
                      [rpsO[bk], r_rden, rsz[r]], [ry[r]])

        for li, l in enumerate(layers):
            last_layer = (li == len(layers) - 1)
            DMA("sync", lambda e, l=l: e.dma_start(out=g_bcv, in_=ng_d[l:l + 1, :].partition_broadcast(128)), w=[rg_bc])
            for dst, src in ((gq_bc, gq_d), (gk_bc, gk_d), (go_bc, go_d), (gmq_bc, gmq_d), (gmk_bc, gmk_d)):
                DMA("sync", lambda e, l=l, dst=dst, src=src: e.dma_start(out=dst[:], in_=src[l:l + 1, :].partition_broadcast(128)),
                    w=[rgains])
            for i in range(NT):
                if li == 0:
                    DMA(("scalar" if (XQ and i % 2 == 1) else "sync"), lambda e, i=i: e.dma_start(out=x_tok[:, i, :], in_=x_d[i * 128:(i + 1) * 128, :]), w=[rx[i]])
                rms_to_T(x_tok[:, i, :], rx[i], g_bcv, rg_bc, hT, rhT[i], i * 128, ss16, r_ss16, t16, r_t16,
                         rstd16, r_rstd16, i)
            glist = [g for g in ALL_GROUPS if g in groups]
            for gi, gname in enumerate(glist):
                last = last_layer and gi == len(glist) - 1
                kind = gname[0]
                g = int(gname[1])
                if kind == "A":
                    w1 = nxt("w", 3); w2 = nxt("w", 3)
                    load_w(wb[w1][:, :, 0:256], rwb[w1][0], w_in_cols(l, 512 + 256 * g, 256))
                    load_w(wb[w1][:, :, 256:512], rwb[w1][1], w_in_cols(l, 1024 + 256 * g, 256))
                    load_w(wb[w2][:, :, 0:256], rwb[w2][0], w_in_cols(l, 256 * g, 256))
                    load_w(wb[w2][:, :, 256:512], rwb[w2][1], w_in_cols(l, 3584 + 256 * g, 256))
                    load_w(wo[:], rwo, wout_d[l, 256 * g:256 * g + 256, :].rearrange("(c p) n -> p c n", p=128))
                    barrier()
                    G(lambda e: e.memset(v_aug[:, :, :, 64:65], 1.0), w=rv)
                    V(lambda e: e.memset(psG[:, 0:16], 0.0), w=[rpsG])
                    for i in range(NT):
                        n_blk = i // 2
                        (tA, rtA), (tB, rtB), (FO, rFO), (tR, rtR) = SC[(i % 2) * PAR1]
                        ka, rka = AUG[(i % 2) * PAR1]
                        pa, rpa = proj(i * 128, rhT[i], wb[w1], rwb[w1])
                        headnorm(pa[:, 0:256], rpa, 4, 64, gk_bc[:], FO, rFO, tA, rtA, tB, rtB)
                        evac(lambda e, i=i, pa=pa: e.tensor_copy(out=v_aug[:, i, :, 0:64],
                                                                 in_=pa[:, 256:512].rearrange("p (h d) -> p h d", h=4)),
                             lambda e, i=i, pa=pa: e.copy(out=v_aug[:, i, :, 0:64],
                                                          in_=pa[:, 256:512].rearrange("p (h d) -> p h d", h=4)),
                             [rpa], [rv[i]])
                        rope(FO, rFO, i, tR, rtR)
                        G(lambda e, ka=ka, FO=FO: e.tensor_copy(out=ka[:, :, 0:64], in_=FO[:].rearrange("p (h d) -> p h d", h=4)),
                          [rFO], [rka])
                        G(lambda e, ka=ka: e.memset(ka[:, :, 64:72], 0.0), w=[rka])
                        G(lambda e, ka=ka, n_blk=n_blk: e.memset(ka[:, :, 64 + n_blk:65 + n_blk], 1.0), w=[rka])
                        for pp in range(2):
                            T(lambda e, pp=pp, n_blk=n_blk, FO=FO: e.matmul(psG[:, pp * 8 + n_blk:pp * 8 + n_blk + 1],
                                                                            lhsT=FO[:, pp * 128:(pp + 1) * 128], rhs=cst[:, 514:515],
                                                                            start=False, stop=False, skip_group_check=True),
                              [rFO, rcst], [rpsG])
                        for h in range(4):
                            T(lambda e, h=h, ka=ka: e.transpose(out=psT[0:72, h * 128:(h + 1) * 128], in_=ka[:, h, :], identity=idb[:]),
                              [rka, ridb], [rpsT])
                        src3 = psT[0:72, 0:512].rearrange("p (h t) -> p h t", h=4)
                        evac(lambda e, i=i, src3=src3: e.tensor_copy(out=kT[0:72, :, i * 128:(i + 1) * 128], in_=src3),
                             lambda e, i=i, src3=src3: e.copy(out=kT[0:72, :, i * 128:(i + 1) * 128], in_=src3),
                             [rpsT], [rkT[i]])
                    G(lambda e: e.memset(kmT32[:], 0.0), w=[rkm])
                    A(lambda e: e.copy(out=kmT32[0:64, :, 0, :], in_=psG[0:64, 0:16].rearrange("p (a n) -> p a n", a=2)), [rpsG], [rkm])
                    A(lambda e: e.copy(out=kmT32[64:128, :, 1, :], in_=psG[64:128, 0:16].rearrange("p (a n) -> p a n", a=2)), [rpsG], [rkm])
                    for Q in range(4):
                        qT3 = qTs[Q % 2][:, 0:2048].rearrange("p (h t) -> p h t", h=4)
                        rqT = rqTs[Q % 2]
                        sz = szs[Q % 2]; rsz = rszs[Q % 2]
                        for r in range(4):
                            i = 4 * Q + r
                            own = i // 2
                            P.phase = "chain"
                            par = (i % 2) * PAR2 * (1 if (own < 4 or GPAR) else 0)
                            (tA, rtA), (tB, rtB), (FO, rFO), (tR, rtR) = SC[par]
                            qa, rqa = AUG[par]
                            pa, rpa = proj(i * 128, rhT[i], wb[w2], rwb[w2])
                            silu_ps(sz[:, r, :], rsz[r], pa[:, 256:512], rpa)
                            headnorm(pa[:, 0:256], rpa, 4, 64, gq_bc[:], FO, rFO, tA, rtA, tB, rtB)
                            rope(FO, rFO, i, tR, rtR)
                            G(lambda e, qa=qa, FO=FO: e.tensor_copy(out=qa[:, :, 0:64], in_=FO[:].rearrange("p (h d) -> p h d", h=4)),
                              [rFO], [rqa])
                            if own >= 4:
                                P.phase = "gate"
                                for pp in range(2):
                                    T(lambda e, pp=pp, FO=FO: e.matmul(psG[:, pp * 128:(pp + 1) * 128],
                                                                       lhsT=FO[:, pp * 128:(pp + 1) * 128], rhs=ident32,
                                                                       start=True, stop=True),
                                      [rFO, rcst], [rpsG])
                                A(lambda e: e.copy(out=Fp[5][:], in_=psG[:, 0:256]), [rpsG], [rF[5]])
                                for h in range(4):
                                    T(lambda e, h=h, own=own: e.matmul(
                                        psG[:, 256 + h * 8:256 + h * 8 + own],
                                        lhsT=Fp[5][:, (h // 2) * 128:(h // 2) * 128 + 128],
                                        rhs=kmT32[:, h // 2, h % 2, 0:own], start=True, stop=True),
                                      [rF[5], rkm], [rpsG])
                                V(lambda e: e.memset(gm[:], -1.0e30), w=[rgm])
                                V(lambda e, own=own: e.tensor_copy(
                                    out=gm[:, :, 0:own], in_=psG[:, 256:288].rearrange("p (h n) -> p h n", h=4)[:, :, 0:own]),
                                  [rpsG], [rgm])
                                for h in range(4):
                                    V(lambda e, h=h: e.max(out=top8[:, h, :], in_=gm[:, h, :]), [rgm], [rtop8])
                                V(lambda e: e.tensor_tensor(out=selb[:], in0=gm[:], in1=top8[:, :, 2:3].to_broadcast([128, 4, 8]),
                                                            op=ALU.is_ge), [rgm, rtop8], [rselb])
                                V(lambda e, qa=qa: e.tensor_scalar(out=qa[:, :, 64:72], in0=selb[:], scalar1=30000.0,
                                                                   scalar2=-30000.0, op0=ALU.mult, op1=ALU.add), [rselb], [rqa])
                                V(lambda e, qa=qa, own=own: e.memset(qa[:, :, 64 + own:65 + own], 0.0), w=[rqa])
                            else:
                                G(lambda e, qa=qa: e.memset(qa[:, :, 64:72], 0.0), w=[rqa])
                            P.phase = None
                            for h in range(4):
                                T(lambda e, h=h, qa=qa: e.transpose(out=psT[0:72, h * 128:(h + 1) * 128], in_=qa[:, h, :],
                                                                    identity=idb[:]), [rqa, ridb], [rpsT])
                            src3 = psT[0:72, 0:512].rearrange("p (h t) -> p h t", h=4)
                            evac(lambda e, r=r, src3=src3, qT3=qT3: e.tensor_copy(out=qT3[0:72, :, r * 128:(r + 1) * 128], in_=src3),
                                 lambda e, r=r, src3=src3, qT3=qT3: e.copy(out=qT3[0:72, :, r * 128:(r + 1) * 128], in_=src3),
                                 [rpsT], [rqT])
                        attn_chunk(Q, 4, 64, 72, 0.125, lambda Q: list(range(4 * Q + 4)), True,
                                   lambda h, kt: kT[0:72, h, kt * 128:(kt + 1) * 128], lambda kt: rkT[kt],
                                   lambda h, kt: v_aug[:, kt, h, :], lambda kt: rv[kt],
                                   lambda h, qT3=qT3: qT3[0:72, h, :], rqT, sz, rsz)
                        for r in range(4):
                            outproj_tile(4 * Q + r, r, last, obanks=([(psO[0], rpsO[0]), (psO[1], rpsO[1])] if OPB else None))
                elif kind == "M":
                    w1 = nxt("w", 3); w2 = nxt("w", 3)
                    wkvv = wkv_d[l].rearrange("(c p) n -> p c n", p=128)
                    load_w(wb[w1][:, :, 0:256], rwb[w1][0], wkvv[:, :, 256 * g:256 * g + 256])
                    load_w(wb[w1][:, :, 256:512], rwb[w1][1], wkvv[:, :, 512 + 256 * g:512 + 256 * g + 256])
                    load_w(wb[w2][:, :, 0:256], rwb[w2][0], w_in_cols(l, 3072 + 256 * g, 256))
                    load_w(wb[w2][:, :, 256:512], rwb[w2][1], w_in_cols(l, 4608 + 256 * g, 256))
                    load_w(wo[:], rwo, wout_d[l, 1024 + 256 * g:1024 + 256 * g + 256, :].rearrange("(c p) n -> p c n", p=128))
                    if g == 0 or ("M0" not in groups):
                        barrier()
                        DMA("sync", lambda e, l=l: e.dma_start(out=g_bcv, in_=mng_d[l:l + 1, :].partition_broadcast(128)),
                            w=[rg_bc])
                        for mt in range(2):
                            DMA("sync", lambda e, mt=mt: e.dma_start(out=stage, in_=mem_d[mt * 128:(mt + 1) * 128, :]),
                                w=[rstage])
                            rms_to_T(stage, rstage, g_bcv, rg_bc, memT, rmemT, mt * 128, ss16, r_ss16, t16, r_t16,
                                     rstd16, r_rstd16, mt)
                    for mt in range(2):
                        (tA, rtA), (tB, rtB), (FO, rFO), (tR, rtR) = SC[mt % 2]
                        pa, rpa = proj(mt * 128, rmemT, wb[w1], rwb[w1], lhsT_src=memT)
                        headnorm(pa[:, 0:256], rpa, 2, 128, gmk_bc[:], FO, rFO, tA, rtA, tB, rtB)
                        evac(lambda e, mt=mt, pa=pa: e.tensor_copy(out=vm_aug[:, mt, :, 0:128],
                                                                   in_=pa[:, 256:512].rearrange("p (h d) -> p h d", h=2)),
                             lambda e, mt=mt, pa=pa: e.copy(out=vm_aug[:, mt, :, 0:128],
                                                            in_=pa[:, 256:512].rearrange("p (h d) -> p h d", h=2)),
                             [rpa], [rvm])
                        G(lambda e, mt=mt, FO=FO: e.tensor_copy(out=Bp[mt % 2][:], in_=FO[:]), [rFO], [rB[mt % 2]])
                        for hh in range(2):
                            T(lambda e, hh=hh, mt=mt: e.transpose(out=psT[:, hh * 128:(hh + 1) * 128],
                                                                  in_=Bp[mt % 2][:, hh * 128:(hh + 1) * 128],
                                                                  identity=idb[:]), [rB[mt % 2], ridb], [rpsT])
                        src3 = psT[:, 0:256].rearrange("p (h t) -> p h t", h=2)
                        evac(lambda e, mt=mt, src3=src3: e.tensor_copy(out=kmT[:, :, mt * 128:(mt + 1) * 128], in_=src3),
                             lambda e, mt=mt, src3=src3: e.copy(out=kmT[:, :, mt * 128:(mt + 1) * 128], in_=src3),
                             [rpsT], [rkmT])
                    for Q in range(4):
                        qm3 = qTs[Q % 2][:, 0:1024].rearrange("p (h t) -> p h t", h=2)
                        rqT = rqTs[Q % 2]
                        sz = szs[Q % 2]; rsz = rszs[Q % 2]
                        for r in range(4):
                            i = 4 * Q + r
                            (tA, rtA), (tB, rtB), (FO, rFO), (tR, rtR) = SC[i % 2]
                            pa, rpa = proj(i * 128, rhT[i], wb[w2], rwb[w2])
                            silu_ps(sz[:, r, :], rsz[r], pa[:, 256:512], rpa)
                            headnorm(pa[:, 0:256], rpa, 2, 128, gmq_bc[:], FO, rFO, tA, rtA, tB, rtB)
                            G(lambda e, i=i, FO=FO: e.tensor_copy(out=Bp[i % 2][:], in_=FO[:]), [rFO], [rB[i % 2]])
                            for hh in range(2):
                                T(lambda e, hh=hh, i=i: e.transpose(out=psT[:, hh * 128:(hh + 1) * 128],
                                                                    in_=Bp[i % 2][:, hh * 128:(hh + 1) * 128], identity=idb[:]),
                                  [rB[i % 2], ridb], [rpsT])
                            src3 = psT[:, 0:256].rearrange("p (h t) -> p h t", h=2)
                            evac(lambda e, r=r, src3=src3, qm3=qm3: e.tensor_copy(out=qm3[:, :, r * 128:(r + 1) * 128], in_=src3),
                                 lambda e, r=r, src3=src3, qm3=qm3: e.copy(out=qm3[:, :, r * 128:(r + 1) * 128], in_=src3),
                                 [rpsT], [rqT])
                        attn_chunk(Q, 2, 128, 128, float(128 ** -0.5), lambda Q: [0, 1], False,
                                   lambda h, kt: kmT[:, h, kt * 128:(kt + 1) * 128], lambda kt: rkmT,
                                   lambda h, kt: vm_aug[:, kt, h, :], lambda kt: rvm,
                                   lambda h, qm3=qm3: qm3[:, h, :], rqT, sz, rsz)
                        for r in range(4):
                            outproj_tile(4 * Q + r, r, last, obanks=([(psO[0], rpsO[0]), (psO[1], rpsO[1])] if OPB else None))
                else:
                    w1 = nxt("w", 3); w2 = nxt("w", 3)
                    load_w(wb[w1][:, :, 0:256], rwb[w1][0], w_in_cols(l, 1536 + 256 * g, 256))
                    load_w(wb[w1][:, :, 256:512], rwb[w1][1], w_in_cols(l, 2048 + 256 * g, 256))
                    load_w(wb[w2][:, :, 0:256], rwb[w2][0], w_in_cols(l, 2560 + 256 * g, 256))
                    load_w(wb[w2][:, :, 256:512], rwb[w2][1], w_in_cols(l, 4096 + 256 * g, 256))
                    load_w(wo[:], rwo, wout_d[l, 512 + 256 * g:512 + 256 * g + 256, :].rearrange("(c p) n -> p c n", p=128))
                    barrier()
                    if l != 0:
                        DMA("sync", lambda e, g=g: e.dma_start(out=lb_g, in_=lbl_d[1:2, 256 * g:256 * g + 256].partition_broadcast(128)), w=[rlb])
                        DMA("sync", lambda e, g=g: e.dma_start(out=oml_g, in_=lbl_d[0:1, 256 * g:256 * g + 256].partition_broadcast(128)), w=[rlb])
                        V(lambda e: e.tensor_tensor(out=lb_g, in0=lb_g, in1=oml_g, op=ALU.subtract), [rlb], [rlb])
                        A(lambda e: e.activation(out=lb_g, in_=lb_g, func=AF.Exp, scale=-1.0), [rlb], [rlb])
                        A(lambda e: e.activation(out=lb_g, in_=lb_g, func=AF.Ln, bias=1.0), [rlb], [rlb])
                        A(lambda e: e.activation(out=lb_g, in_=lb_g, func=AF.Exp, scale=-1.0), [rlb], [rlb])
                        V(lambda e: e.tensor_scalar(out=oml_g, in0=lb_g, scalar1=-1.0, scalar2=1.0, op0=ALU.mult, op1=ALU.add),
                          [rlb], [rlb])
                    for hh in range(2):
                        G(lambda e, hh=hh: e.memset(S32[:, hh, :], 0.0), w=[rS32[hh]])
                        G(lambda e, hh=hh: e.memset(Sbf[:, 0, hh, :], 0.0), w=[rSbf[0][hh]])
                    Tri32 = cst[:, 256:384]
                    TriE32 = cst[:, 384:512]
                    for i in range(NT):
                        Fs, rFs, Bs, rBs = HSETS[i % 2]
                        sl = i % 4
                        sz = szs[(i // 4) % 2]; rsz = rszs[(i // 4) % 2]
                        pq, rpq = proj(i * 128, rhT[i], wb[w1], rwb[w1])
                        silu_ps(Fs[0], rFs[0], pq[:, 0:256], rpq)
                        A(lambda e, pq=pq, Fs=Fs: e.activation(out=Fs[1], in_=pq[:, 256:512], func=AF.Exp, scale=-1.0), [rpq], [rFs[1]])
                        A(lambda e, Fs=Fs: e.activation(out=Fs[1], in_=Fs[1], func=AF.Ln, bias=1.0), [rFs[1]], [rFs[1]])
                        pi_, rpi = proj(i * 128, rhT[i], wb[w2], rwb[w2])
                        silu_ps(sz[:, sl, :], rsz[sl], pi_[:, 256:512], rpi)
                        V(lambda e, pi_=pi_, Bs=Bs: e.tensor_copy(out=Bs[0], in_=pi_[:, 0:256]), [rpi], [rBs[0]])
                        if l == 0:
                            A(lambda e, Fs=Fs: e.activation(out=Fs[2], in_=Fs[1], func=AF.Copy, scale=-1.0), [rFs[1]], [rFs[2]])
                            A(lambda e, Fs=Fs: e.activation(out=Fs[1], in_=Fs[1], func=AF.Exp, scale=-1.0), [rFs[1]], [rFs[1]])
                        else:
                            A(lambda e, Fs=Fs: e.activation(out=Fs[1], in_=Fs[1], func=AF.Exp, scale=-1.0), [rFs[1]], [rFs[1]])
                            V(lambda e, g=g, Fs=Fs: e.tensor_tensor(out=Fs[1], in0=Fs[1], in1=oml_g,
                                                                    op=ALU.mult), [rFs[1], rlb], [rFs[1]])
                            V(lambda e, g=g, Fs=Fs: e.tensor_tensor(out=Fs[1], in0=Fs[1], in1=lb_g,
                                                                    op=ALU.add), [rFs[1], rlb], [rFs[1]])
                            A(lambda e, Fs=Fs: e.activation(out=Fs[2], in_=Fs[1], func=AF.Ln), [rFs[1]], [rFs[2]])
                        (G if GOFF else V)(lambda e, Fs=Fs: e.tensor_scalar(out=Fs[3], in0=Fs[1], scalar1=-1.0, scalar2=1.0, op0=ALU.mult,
                                                                            op1=ALU.add), [rFs[1]], [rFs[3]])
                        T(lambda e, Fs=Fs: e.matmul(psG[:, 0:256], lhsT=Tri32, rhs=Fs[2], start=True, stop=True),
                          [rcst, rFs[2]], [rpsG])
                        T(lambda e, Fs=Fs: e.matmul(psG[:, 256:512], lhsT=TriE32, rhs=Fs[2], start=True, stop=True),
                          [rcst, rFs[2]], [rpsG])
                        for hh in range(2):
                            T(lambda e, hh=hh, Fs=Fs: e.matmul(psS[1][:, 256 + 2 * hh:256 + 2 * hh + 2],
                                                               lhsT=Fs[2][:, hh * 128:(hh + 1) * 128],
                                                               rhs=cst[:, 512:514], start=True, stop=True), [rFs[2], rcst], [rpsS[1]])
                        dsl = dec8[:, 4 * (i % 2):4 * (i % 2) + 4]
                        rds = r_dec8[i % 2]
                        A(lambda e, dsl=dsl: e.activation(out=dsl, in_=psS[1][:, 256:260], func=AF.Exp), [rpsS[1]], [rds])
                        A(lambda e, Fs=Fs: e.activation(out=Fs[4], in_=psG[:, 0:256], func=AF.Exp), [rpsG], [rFs[4]])
                        A(lambda e, Fs=Fs: e.activation(out=Fs[5], in_=psG[:, 0:256], func=AF.Exp, scale=-1.0), [rpsG], [rFs[5]])
                        A(lambda e, Fs=Fs: e.activation(out=Fs[6], in_=psG[:, 256:512], func=AF.Exp), [rpsG], [rFs[6]])
                        V(lambda e, Fs=Fs, Bs=Bs: e.tensor_tensor(out=Bs[1], in0=Fs[0], in1=Fs[4], op=ALU.mult), [rFs[0], rFs[4]], [rBs[1]])
                        G(lambda e, Fs=Fs, Bs=Bs: e.tensor_tensor(out=Bs[2], in0=Fs[3], in1=Fs[5], op=ALU.mult), [rFs[3], rFs[5]], [rBs[2]])
                        G(lambda e, Fs=Fs, Bs=Bs: e.tensor_tensor(out=Bs[3], in0=Fs[3], in1=Fs[6], op=ALU.mult), [rFs[3], rFs[6]], [rBs[3]])
                        for hh in range(2):
                            T(lambda e, hh=hh, Bs=Bs: e.transpose(out=psT[:, hh * 128:(hh + 1) * 128], in_=Bs[1][:, hh * 128:(hh + 1) * 128],
                                                                  identity=idb[:]), [rBs[1], ridb], [rpsT])
                            T(lambda e, hh=hh, Bs=Bs: e.transpose(out=psT[:, 256 + hh * 128:256 + (hh + 1) * 128],
                                                                  in_=Bs[2][:, hh * 128:(hh + 1) * 128], identity=idb[:]),
                              [rBs[2], ridb], [rpsT])
                        pq3 = psT[:, 0:256].rearrange("p (h t) -> p h t", h=2)
                        A(lambda e, Bs=Bs: e.copy(out=Bs[4], in_=psT[:, 0:256]), [rpsT], [rBs[4]])
                        V(lambda e, Bs=Bs: e.tensor_copy(out=Bs[5], in_=psT[:, 256:512]), [rpsT], [rBs[5]])
                        A(lambda e, pq3=pq3: e.copy(out=qTA[:, :, 0:64], in_=pq3[:, :, 0:64]), [rpsT], [rqTA])
                        V(lambda e, pq3=pq3: e.tensor_copy(out=qTB[:, :, 64:128], in_=pq3[:, :, 64:128]), [rpsT], [rqTB])
                        cur = i % 2
                        nxtb = 1 - cur
                        for hh in range(2):
                            hs = slice(hh * 128, (hh + 1) * 128)
                            T(lambda e, hs=hs, Bs=Bs: e.matmul(psS[1][:, hs], lhsT=Bs[5][:, hs], rhs=Bs[4][:, hs], start=True, stop=True),
                              [rBs[5], rBs[4]], [rpsS[1]])
                        for hh in range(2):
                            hs = slice(hh * 128, (hh + 1) * 128)
                            V(lambda e, hs=hs, Bs=Bs: e.tensor_tensor(out=Bs[6][:, hs], in0=psS[1][:, hs], in1=hgmb[:], op=ALU.mult),
                              [rpsS[1], rhgmb], [rBs[6]])
                        for hh in range(2):
                            hs = slice(hh * 128, (hh + 1) * 128)
                            T(lambda e, hs=hs, hh=hh, Bs=Bs: e.matmul(psO[hh][:, 0:128], lhsT=Bs[6][:, hs], rhs=Bs[0][:, hs],
                                                                      start=True, stop=False), [rBs[6], rBs[0]], [rpsO[hh]])
                            T(lambda e, hh=hh, cur=cur: e.matmul(psO[hh][:, 0:128], lhsT=qTA[:, hh, :], rhs=Sbf[:, cur, hh, :],
                                                                 start=False, stop=False), [rqTA, rSbf[cur][hh]], [rpsO[hh]])
                        for hh in range(2):
                            hs = slice(hh * 128, (hh + 1) * 128)
                            T(lambda e, hs=hs, Bs=Bs: e.matmul(psS[0][:, hs], lhsT=Bs[3][0:64, hs], rhs=Bs[0][0:64, hs],
                                                               start=True, stop=True), [rBs[3], rBs[0]], [rpsS[0], rrow])
                        for hh in range(2):
                            hs = slice(hh * 128, (hh + 1) * 128)
                            V(lambda e, hs=hs, hh=hh, dsl=dsl: e.scalar_tensor_tensor(out=S32[:, hh, :], in0=S32[:, hh, :],
                                                                                      scalar=dsl[:, 2 * hh:2 * hh + 1], in1=psS[0][:, hs],
                                                                                      op0=ALU.mult, op1=ALU.add),
                              [rS32[hh], rds, rpsS[0]], [rS32[hh]])
                            G(lambda e, hh=hh, nxtb=nxtb: e.tensor_copy(out=Sbf[:, nxtb, hh, :], in_=S32[:, hh, :]),
                              [rS32[hh]], [rSbf[nxtb][hh]])
                        for hh in range(2):
                            T(lambda e, hh=hh, nxtb=nxtb: e.matmul(psO[hh][:, 0:128], lhsT=qTB[:, hh, :], rhs=Sbf[:, nxtb, hh, :],
                                                                   start=False, stop=True), [rqTB, rSbf[nxtb][hh]], [rpsO[hh], rrow])
                        for hh in range(2):
                            hs = slice(hh * 128, (hh + 1) * 128)
                            T(lambda e, hs=hs, Bs=Bs: e.matmul(psS[0][:, hs], lhsT=Bs[3][64:128, hs], rhs=Bs[0][64:128, hs],
                                                               start=True, stop=True), [rBs[3], rBs[0]], [rpsS[0], rrow])
                        for hh in range(2):
                            hs = slice(hh * 128, (hh + 1) * 128)
                            V(lambda e, hs=hs, hh=hh, dsl=dsl: e.scalar_tensor_tensor(out=S32[:, hh, :], in0=S32[:, hh, :],
                                                                                      scalar=dsl[:, 2 * hh + 1:2 * hh + 2], in1=psS[0][:, hs],
                                                                                      op0=ALU.mult, op1=ALU.add),
                              [rS32[hh], rds, rpsS[0]], [rS32[hh]])
                        for hh in range(2):
                            G(lambda e, hh=hh, nxtb=nxtb: e.tensor_copy(out=Sbf[:, nxtb, hh, :], in_=S32[:, hh, :]),
                              [rS32[hh]], [rSbf[nxtb][hh]])
                        ssl = sso4[:, 2 * (i % 2):2 * (i % 2) + 2]
                        r_sso = r_sso2[i % 2]
                        for hh in range(2):
                            A(lambda e, hh=hh, Fs=Fs, ssl=ssl: e.activation(out=Fs[7][:, 0:128], in_=psO[hh][:, 0:128], func=AF.Square,
                                                                            accum_out=ssl[:, hh:hh + 1]), [rpsO[hh]], [rFs[7], r_sso])
                        A(lambda e, ssl=ssl: e.activation(out=ssl, in_=ssl, func=AF.Ln, scale=1.0 / 128, bias=EPS), [r_sso], [r_sso])
                        A(lambda e, ssl=ssl: e.activation(out=ssl, in_=ssl, func=AF.Exp, scale=-0.5), [r_sso], [r_sso])
                        for hh in range(2):
                            hs = slice(hh * 128, (hh + 1) * 128)
                            V(lambda e, hh=hh, hs=hs, Fs=Fs, ssl=ssl: e.scalar_tensor_tensor(out=Fs[8][:, hs], in0=psO[hh][:, 0:128],
                                                                                             scalar=ssl[:, hh:hh + 1], in1=go_bc[:],
                                                                                             op0=ALU.mult, op1=ALU.mult),
                              [rpsO[hh], r_sso, rgains], [rFs[8]])
                        G(lambda e, Fs=Fs, sl=sl, sz=sz: e.tensor_tensor(out=y_tok[:, sl, :], in0=Fs[8], in1=sz[:, sl, :], op=ALU.mult),
                          [rFs[8], rsz[sl]], [ry[sl]])
                        outproj_tile(i, sl, last, obanks=[(psS[0], rpsS[0]), (psS[0], rpsS[0])])
            if not glist and last_layer:
                for i in range(NT):
                    out_dmas.append(DMA("sync", lambda e, i=i: e.dma_start(out=out_d[i * 128:(i + 1) * 128, :], in_=x_tok[:, i, :]),
                                        r=[rx[i]]))
        if SCHED:
            if SCHED2:
                P.schedule2(SDELTA)
            else:
                P.schedule()
        P.emit(st, out_dmas)
    build_nc.stats = P.stats
    return nc


_CACHE = {}


def _get_nc(layers, groups):
    key = (tuple(layers), tuple(groups))
    if key not in _CACHE:
        _CACHE[key] = build_nc(layers, groups)
    return _CACHE[key]


def run(inputs, layers=(0, 1), groups=ALL_GROUPS, cores=8):
    nc = _get_nc(layers, groups)
    f = lambda a: np.ascontiguousarray(np.asarray(a))
    cst = make_consts()
    shared = {k: f(inputs[k]).astype(np.float32, copy=False) for k in
              ("norm_g", "w_in", "w_out", "moba_q_norm", "moba_k_norm", "hgrn_lb_logits", "hgrn_o_norm",
               "mem_norm_g", "w_mem_kv", "mem_q_norm", "mem_k_norm")}
    x = f(inputs["x"]); mem = f(inputs["mem"]); pos = f(inputs["positions"]).astype(np.int32, copy=False)
    in_maps = []
    for b in range(cores):
        m = dict(shared)
        m["x"] = x[b]
        m["mem"] = mem[b]
        m["pos"] = pos[b].reshape(16, 128)
        m["cst"] = cst
        in_maps.append(m)
    res = run_bass_kernel_spmd(nc, in_maps, core_ids=list(range(cores)))
    return np.stack([np.asarray(r["out"]) for r in res.results], axis=0)


def kernel(x, mem, positions, norm_g, w_in, w_out, moba_q_norm, moba_k_norm, hgrn_lb_logits,
           hgrn_o_norm, mem_norm_g, w_mem_kv, mem_q_norm, mem_k_norm):
    inputs = dict(x=x, mem=mem, positions=positions, norm_g=norm_g, w_in=w_in, w_out=w_out,
                  moba_q_norm=moba_q_norm, moba_k_norm=moba_k_norm, hgrn_lb_logits=hgrn_lb_logits,
                  hgrn_o_norm=hgrn_o_norm, mem_norm_g=mem_norm_g, w_mem_kv=w_mem_kv,
                  mem_q_norm=mem_q_norm, mem_k_norm=mem_k_norm)
    return run(inputs).astype(np.float32, copy=False)
```

```python
import numpy as np
from contextlib import ExitStack
import concourse.bass as bass
import concourse.mybir as mybir
from concourse.bass_utils import run_bass_kernel_spmd

F32 = mybir.dt.float32
BF16 = mybir.dt.bfloat16
I32 = mybir.dt.int32
AF = mybir.ActivationFunctionType
ALU = mybir.AluOpType
AX = mybir.AxisListType

S = 2048
D = 1024
NT = 16
EPS = 1e-6
NCST = 576
import os as _os0
ALL_GROUPS = tuple(_os0.environ.get("ORDER", "A0,A1,H0,H1,M0,M1").split(","))
import os as _os
SCHED = _os.environ.get("SCHED", "1") == "1"
PAR1 = int(_os.environ.get("PAR1", "1"))
PAR2 = int(_os.environ.get("PAR2", "1"))
GPAR = int(_os.environ.get("GPAR", "1"))
PS3 = int(_os.environ.get("PS3", "3"))
MASKV = int(_os.environ.get("MASKV", "1"))
OPB = int(_os.environ.get("OPB", "1"))
SCHED2 = int(_os.environ.get("SCHED2", "1"))
SDELTA = float(_os.environ.get("SDELTA", "100"))
LATX = float(_os.environ.get("LATX", "180"))
PEK = float(_os.environ.get("PEK", "0.65"))
ACTK = float(_os.environ.get("ACTK", "1.0"))
DVEK = float(_os.environ.get("DVEK", "1.0"))
POOLK = float(_os.environ.get("POOLK", "1.0"))
LATS = float(_os.environ.get("LATS", "60"))
XQ = int(_os.environ.get("XQ", "0"))
GOFF = int(_os.environ.get("GOFF", "0"))
TRANS = int(_os.environ.get("TRANS", "1"))
PRUNE = int(_os.environ.get("PRUNE", "1"))


class Res:
    __slots__ = ("name", "w", "r", "excl")

    def __init__(self, name, excl=False):
        self.name = name
        self.w = None
        self.r = []
        self.excl = excl


class _Rec:
    def __init__(self):
        self.name = None
        self.args = ()
        self.kw = {}

    def __getattr__(self, name):
        def f(*a, **k):
            self.name, self.args, self.kw = name, a, k
            return self
        return f


def _free_size(ap):
    n = 1
    for d in list(ap.shape)[1:]:
        n *= int(d)
    return n


class Op:
    __slots__ = ("eng", "fn", "deps", "sdeps", "sig", "idx", "dma", "sem", "val", "i", "cost", "start")

    def __init__(self, eng, fn, deps, sdeps, dma):
        self.eng = eng
        self.fn = fn
        self.deps = deps
        self.sdeps = sdeps
        self.sig = False
        self.idx = 0
        self.dma = dma
        self.sem = None
        self.val = 0
        self.i = 0
        self.start = 0.0
        rec = _Rec()
        fn(rec)
        out = rec.kw.get("out", rec.args[0] if rec.args else None)
        n = _free_size(out) if out is not None else 64
        if dma:
            c = 2000.0 + n * int(out.shape[0]) * 4 / 120.0
        elif eng == "tensor":
            if rec.name == "transpose":
                c = 110.0
            else:
                lhsT = rec.kw.get("lhsT")
                f32 = lhsT is not None and lhsT.dtype == F32
                c = PEK * (64.0 + max(n, 64) / 2.0) * (4.0 if f32 else 1.0)
        elif eng == "scalar":
            c = ACTK * (200.0 + n / 1.2)
        elif eng == "vector":
            c = DVEK * (120.0 + n / 0.96 * (8.0 if rec.name == "reciprocal" else 1.0))
        else:
            c = POOLK * (300.0 + n / 0.5)
        self.cost = c


class Prog:
    ENGS = ["tensor", "vector", "scalar", "gpsimd", "sync"]

    def __init__(self, nc):
        self.nc = nc
        self.ops = []

    phase = None
    tok = None
    tokset = ()

    def op(self, eng, fn, reads=(), writes=(), dma=False):
        if self.phase == "gate" and self.tok is not None:
            writes = list(writes) + [self.tok]
        elif self.phase == "chain" and eng in self.tokset:
            reads = list(reads) + [self.tok]
        deps, sdeps = {}, {}

        def add(d):
            if d.dma or dma or d.eng != eng or eng != "tensor":
                deps[id(d)] = d
            else:
                sdeps[id(d)] = d
        for r in reads:
            if r.w is not None:
                add(r.w)
            if r.excl:
                for d in r.r:
                    if d.eng != eng:
                        add(d)
        for w in writes:
            if w.w is not None:
                add(w.w)
            for d in w.r:
                add(d)
        o = Op(eng, fn, list(deps.values()), list(sdeps.values()), dma)
        for r in reads:
            r.r.append(o)
        for w in writes:
            w.w = o
            w.r = []
        self.ops.append(o)
        return o

    def schedule(self):
        import heapq
        ops = self.ops
        for i, o in enumerate(ops):
            o.i = i
        succs = [[] for _ in ops]
        npred = [0] * len(ops)
        for o in ops:
            ds = o.deps + o.sdeps
            npred[o.i] = len(ds)
            for d in ds:
                succs[d.i].append(o)
        ready = [0.0] * len(ops)
        free = {e: 0.0 for e in self.ENGS}
        heap = [(0.0, o.i) for o in ops if npred[o.i] == 0]
        heapq.heapify(heap)
        done = 0
        while heap:
            t, i = heapq.heappop(heap)
            o = ops[i]
            st = max(ready[i], free[o.eng])
            if st > t + 1e-9:
                heapq.heappush(heap, (st, i))
                continue
            o.start = st
            if o.dma:
                free[o.eng] = st + 150.0
            else:
                free[o.eng] = st + o.cost
            fin = st + o.cost
            done += 1
            for sc in succs[i]:
                lat = 60.0 if (sc.eng == o.eng and not o.dma) else 180.0
                if fin + lat > ready[sc.i]:
                    ready[sc.i] = fin + lat
                npred[sc.i] -= 1
                if npred[sc.i] == 0:
                    heapq.heappush(heap, (max(ready[sc.i], free[sc.eng]), sc.i))
        assert done == len(ops), (done, len(ops))
        self.ops = sorted(ops, key=lambda o: (o.start, o.i))
        self.est_ns = max(o.start + o.cost for o in ops)

    def schedule2(self, delta=120.0):
        ops = self.ops
        n = len(ops)
        for i, o in enumerate(ops):
            o.i = i
        succs = [[] for _ in ops]
        npred = [0] * n
        for o in ops:
            ds = o.deps + o.sdeps
            npred[o.i] = len(ds)
            for d in ds:
                succs[d.i].append(o)
        blev = [0.0] * n
        for o in reversed(ops):
            b = 0.0
            for sc in succs[o.i]:
                lat = LATS if (sc.eng == o.eng and not o.dma) else LATX
                v = lat + blev[sc.i]
                if v > b:
                    b = v
            blev[o.i] = b + o.cost
        ready = [0.0] * n
        free = {e: 0.0 for e in self.ENGS}
        rsets = {e: [] for e in self.ENGS}
        for o in ops:
            if npred[o.i] == 0:
                rsets[o.eng].append(o.i)
        done = 0
        while done < n:
            best_e, best_t = None, 1e30
            for e in self.ENGS:
                rs = rsets[e]
                if not rs:
                    continue
                t = min(ready[i] for i in rs)
                if t < free[e]:
                    t = free[e]
                if t < best_t:
                    best_t, best_e = t, e
            e = best_e
            rs = rsets[e]
            lim = best_t + delta
            pick, pb = -1, -1.0
            for i in rs:
                if ready[i] <= lim and blev[i] > pb:
                    pb, pick = blev[i], i
            rs.remove(pick)
            o = ops[pick]
            st = max(ready[pick], free[e])
            o.start = st
            free[e] = st + (150.0 if o.dma else o.cost)
            fin = st + o.cost
            done += 1
            for sc in succs[pick]:
                lat = LATS if (sc.eng == o.eng and not o.dma) else LATX
                if fin + lat > ready[sc.i]:
                    ready[sc.i] = fin + lat
                npred[sc.i] -= 1
                if npred[sc.i] == 0:
                    rsets[sc.eng].append(sc.i)
        self.ops = sorted(ops, key=lambda o: (o.start, o.i))
        self.est_ns = max(o.start + o.cost for o in ops)

    def emit(self, stack, final_deps, ndma_sems=8):
        nc = self.nc
        if PRUNE:
            pos = {id(o): k for k, o in enumerate(self.ops)}
            for o in self.ops:
                best = {}
                keep = []
                for d in o.deps:
                    if d.dma:
                        keep.append(d)
                    elif d.eng not in best or pos[id(d)] > pos[id(best[d.eng])]:
                        best[d.eng] = d
                o.deps = keep + list(best.values())
        for o in self.ops:
            for d in o.deps:
                d.sig = True
        for d in final_deps:
            d.sig = True
        sems = {e: stack.enter_context(nc.semaphore("s_" + e)) for e in self.ENGS}
        cnt = {e: 0 for e in self.ENGS}
        pools, pool_i, pre_wait = {}, {}, {}
        for o in self.ops:
            if o.dma:
                if o.eng not in pools:
                    pools[o.eng] = [[stack.enter_context(nc.semaphore("d_%s_%d" % (o.eng, i))), 0]
                                    for i in range(ndma_sems)]
                    pool_i[o.eng] = 0
                p = pools[o.eng][pool_i[o.eng] % ndma_sems]
                pool_i[o.eng] += 1
                if p[1] > 0:
                    pre_wait[id(o)] = (p[0], p[1])
                p[1] += 16
                o.sem = p[0]
                o.val = p[1]
            elif o.sig:
                cnt[o.eng] += 1
                o.idx = cnt[o.eng]
        per = {e: [o for o in self.ops if o.eng == e] for e in self.ENGS}
        self.stats = {e: len(per[e]) for e in self.ENGS}
        known = {e: {} for e in self.ENGS}
        kn = {}
        plan = {}
        nw = 0

        def semkey(d):
            return (d.sem, d.val) if d.dma else (sems[d.eng], d.idx)

        for o in self.ops:
            kd = known[o.eng]
            ws = []
            for d in o.deps:
                sm, val = semkey(d)
                if kd.get(id(sm), (None, 0))[1] < val:
                    ws.append((sm, val))
                    kd[id(sm)] = (sm, val)
                if TRANS:
                    for k2, (s2, v2) in kn[id(d)].items():
                        if kd.get(k2, (None, 0))[1] < v2:
                            kd[k2] = (s2, v2)
            if o.dma:
                pw = pre_wait.get(id(o))
                if pw and kd.get(id(pw[0]), (None, 0))[1] < pw[1]:
                    ws.append(pw)
                    kd[id(pw[0])] = pw
            plan[id(o)] = ws
            nw += len(ws)
            if o.dma or o.sig:
                mine = dict(kd)
                sm, val = semkey(o)
                mine[id(sm)] = (sm, val)
                kn[id(o)] = mine
        fin_w = []
        kd = known["sync"]
        for d in final_deps:
            sm, val = semkey(d)
            if kd.get(id(sm), (None, 0))[1] < val:
                fin_w.append((sm, val))
                kd[id(sm)] = (sm, val)
        self.stats["waits"] = nw
        self.stats["sigs"] = {e: sum(1 for o in per[e] if o.sig and not o.dma) for e in self.ENGS}
        block = stack.enter_context(nc.Block())

        def mk(e):
            def body(engobj):
                for o in per[e]:
                    for sm, val in plan[id(o)]:
                        engobj.wait_ge(sm, val)
                    if o.dma:
                        o.fn(engobj).then_inc(o.sem, 16)
                    else:
                        ins = o.fn(engobj)
                        if o.sig:
                            ins.then_inc(sems[e], 1)
                if e == "sync":
                    for sm, val in fin_w:
                        engobj.wait_ge(sm, val)
            return body

        block.tensor(mk("tensor"))
        block.vector(mk("vector"))
        block.scalar(mk("scalar"))
        block.gpsimd(mk("gpsimd"))
        block.sync(mk("sync"))


def make_consts():
    c = np.zeros((128, NCST), np.float32)
    i = np.arange(128)
    c[:, 0:128] = np.eye(128)
    c[:, 128:256] = (i[None, :] >= i[:, None])
    same = (i[:, None] // 64) == (i[None, :] // 64)
    c[:, 256:384] = same & (i[:, None] <= i[None, :])
    c[:, 384:512] = same & (i[:, None] > i[None, :])
    c[:, 512] = i < 64
    c[:, 513] = i >= 64
    c[:, 514] = 1.0
    f64 = 500000.0 ** (-np.arange(8, dtype=np.float64) / 8.0)
    f = f64.astype(np.float32)
    flo = (f64 - f.astype(np.float64)).astype(np.float32)
    c[:, 515:523] = f[None, :]
    c[:, 523:531] = f[None, :]
    c[:, 547:555] = flo[None, :]
    c[:, 555:563] = flo[None, :]
    c[:, 531:539] = 0.0
    c[:, 539:547] = np.pi / 2
    return c


def build_nc(layers=(0, 1), groups=ALL_GROUPS):
    nc = bass.Bass("TRN2", target_bir_lowering=False)

    def din(name, shape, d=F32):
        return nc.dram_tensor(name, shape, d, kind="ExternalInput").ap()

    x_d = din("x", [S, D])
    mem_d = din("mem", [256, D])
    pos_d = din("pos", [16, 128], I32)
    ng_d = din("norm_g", [2, D])
    win_d = din("w_in", [2, D, 5120])
    wout_d = din("w_out", [2, 1536, D])
    gq_d = din("moba_q_norm", [2, 64])
    gk_d = din("moba_k_norm", [2, 64])
    lbl_d = din("hgrn_lb_logits", [2, 512])
    go_d = din("hgrn_o_norm", [2, 128])
    mng_d = din("mem_norm_g", [2, D])
    wkv_d = din("w_mem_kv", [2, D, 1024])
    gmq_d = din("mem_q_norm", [2, 128])
    gmk_d = din("mem_k_norm", [2, 128])
    cst_d = din("cst", [128, NCST])
    out_d = nc.dram_tensor("out", [S, D], F32, kind="ExternalOutput").ap()

    P = Prog(nc)
    if _os.environ.get("TOKR"):
        P.tok = Res("tok")
        P.tokset = tuple(_os.environ["TOKR"].split(","))
    with ExitStack() as st:
        def sb(name, shape, dt=F32):
            return st.enter_context(nc.sbuf_tensor("sb_" + name, shape, dt))

        def ps(name, shape, dt=F32):
            return st.enter_context(nc.psum_tensor("pp_" + name, shape, dt))

        def T(fn, r=(), w=()):
            return P.op("tensor", fn, r, w)

        def V(fn, r=(), w=()):
            return P.op("vector", fn, r, w)

        def A(fn, r=(), w=()):
            return P.op("scalar", fn, r, w)

        def G(fn, r=(), w=()):
            return P.op("gpsimd", fn, r, w)

        def DMA(q, fn, r=(), w=()):
            return P.op(q, fn, r, w, dma=True)

        x_tok = sb("x_tok", [128, NT, D]); rx = [Res("x%d" % i) for i in range(NT)]
        hT = sb("hT", [128, 8, S], BF16); rhT = [Res("hT%d" % i) for i in range(NT)]
        cst = sb("cst", [128, NCST]); rcst = Res("cst")
        idb = sb("idb", [128, 128], BF16); ridb = Res("idb")
        trib = sb("trib", [128, 128], BF16); rtrib = Res("trib")
        hgmb = sb("hgmb", [128, 128], BF16); rhgmb = Res("hgmb")
        gq_bc = sb("gq_bc", [128, 64]); gk_bc = sb("gk_bc", [128, 64]); go_bc = sb("go_bc", [128, 128])
        gmq_bc = sb("gmq_bc", [128, 128]); gmk_bc = sb("gmk_bc", [128, 128]); rgains = Res("gains")
        cs = sb("cs", [128, NT, 16]); sn = sb("sn", [128, NT, 16]); rrope = Res("rope")
        wb = [sb("wb%d" % i, [128, 8, 512], BF16) for i in range(3)]; rwb = [[Res("wb%da" % i), Res("wb%db" % i)] for i in range(3)]
        wo = sb("wo", [128, 2, D], BF16); rwo = Res("wo")
        Fp = [sb("F%d" % i, [128, 256]) for i in range(9)]; rF = [Res("F%d" % i) for i in range(9)]
        Bp = [sb("B%d" % i, [128, 256], BF16) for i in range(7)]; rB = [Res("B%d" % i) for i in range(7)]
        szs = [sb("sz%d" % k, [128, 4, 256]) for k in range(2)]; rszs = [[Res("sz%d_%d" % (k, i)) for i in range(4)] for k in range(2)]
        y_tok = sb("y_tok", [128, 4, 256], BF16); ry = [Res("y%d" % i) for i in range(4)]
        yT = [sb("yT%d" % i, [128, 256], BF16) for i in range(2)]; ryT = [Res("yT%d" % i) for i in range(2)]
        pT = [sb("pT%d" % i, [128, 512], BF16) for i in range(4)]; rpT = [Res("pT%d" % i) for i in range(4)]
        hbs = [sb("hb%d" % i, [128, D], BF16) for i in range(2)]; rhbs = [Res("hb0"), Res("hb1")]
        small = sb("small", [128, 128]); rsm = {}

        def sm(name, a, n):
            rsm[name] = Res("sm_" + name)
            return small[:, a:a + n], rsm[name]
        ss16, r_ss16 = sm("ss16", 0, 16)
        t16, r_t16 = sm("t16", 16, 16)
        rstd16, r_rstd16 = sm("rstd16", 32, 16)
        ss4, r_ss4 = sm("ss4", 48, 4)
        t4, r_t4 = sm("t4", 52, 4)
        rs4, r_rs4 = sm("rs4", 56, 4)
        rden, r_rden = sm("rden", 60, 4)
        dec4, r_dec4 = sm("dec4", 64, 4)
        sso, r_sso = sm("sso", 68, 2)
        to2, r_to2 = sm("to2", 70, 2)
        rso, r_rso = sm("rso", 72, 2)
        gm = sb("gm", [128, 4, 8]); rgm = Res("gm")
        top8 = sb("top8", [128, 4, 8]); rtop8 = Res("top8")
        selb = sb("selb", [128, 4, 8]); rselb = Res("selb")
        kT = sb("kT", [128, 4, S], BF16); rkT = [Res("kT%d" % i) for i in range(NT)]
        v_flat = sb("v_aug", [128, NT * 4 * 65], BF16); rv = [Res("v%d" % i) for i in range(NT)]
        v_aug = v_flat[:].rearrange("p (a b c) -> p a b c", a=NT, b=4)
        stage = v_flat[:, 0:2048].bitcast(F32); rstage = Res("stage")
        g_bcv = v_flat[:, 2048:4096].bitcast(F32); rg_bc = Res("g_bc")
        qTs = [sb("qT%d" % i, [128, 2048], BF16) for i in range(2)]; rqTs = [Res("qT0"), Res("qT1")]
        lbv = qTs[1][:, 0:2048].bitcast(F32); rlb = rqTs[1]
        lb_g = lbv[:, 0:256]; oml_g = lbv[:, 256:512]
        k_aug = sb("k_aug", [128, 4, 72], BF16); rk_aug = Res("k_aug")
        q_aug = sb("q_aug", [128, 4, 72], BF16); rq_aug = Res("q_aug")
        kmT32 = sb("kmT32", [128, 2, 2, 8]); rkm = Res("kmT32")
        S32 = sb("S32", [128, 2, 128]); rS32 = [Res("S32_0"), Res("S32_1")]
        Sbf = sb("Sbf", [128, 2, 2, 128], BF16); rSbf = [[Res("Sbf00"), Res("Sbf01")], [Res("Sbf10"), Res("Sbf11")]]
        qTA = sb("qTA", [128, 2, 128], BF16); qTB = sb("qTB", [128, 2, 128], BF16)
        rqTA = Res("qTA"); rqTB = Res("qTB")
        memT = sb("memT", [128, 8, 256], BF16); rmemT = Res("memT")
        kmT = sb("kmT", [128, 2, 256], BF16); rkmT = Res("kmT")
        vm_aug = sb("vm_aug", [128, 2, 2, 129], BF16); rvm = Res("vm")
        psA = [ps("psA%d" % i, [128, 512]) for i in range(2)]; rpsA = [Res("psA0", True), Res("psA1", True)]
        psT = ps("psT", [128, 1024], BF16); rpsT = Res("psT", True)
        psG = ps("psG", [128, 512]); rpsG = Res("psG", True)
        psS = [ps("psS%d" % i, [128, 512]) for i in range(2)]; rpsS = [Res("psS0", True), Res("psS1", True)]
        psO = [ps("psO%d" % i, [128, 512]) for i in range(2)]; rpsO = [Res("psO0", True), Res("psO1", True)]

        ctr = {"pa": 0, "w": 0, "ev": 0, "ps": 0, "pt": 0, "yt": 0, "mk": 0}

        def nxt(k, n):
            v = ctr[k] % n
            ctr[k] += 1
            return v

        def evac(fn_v, fn_a, r, w):
            if nxt("ev", 2) == 0:
                return A(fn_a, r, w)
            return V(fn_v, r, w)

        DMA("sync", lambda e: e.dma_start(out=cst[:], in_=cst_d), w=[rcst])
        V(lambda e: e.tensor_copy(out=idb[:], in_=cst[:, 0:128]), [rcst], [ridb])
        V(lambda e: e.tensor_copy(out=trib[:], in_=cst[:, 128:256]), [rcst], [rtrib])
        V(lambda e: e.tensor_copy(out=hgmb[:], in_=cst[:, 256:384]), [rcst], [rhgmb])
        ident32 = cst[:, 0:128]
        G(lambda e: e.memset(vm_aug[:, :, :, 128:129], 1.0), w=[rvm])
        G(lambda e: e.memset(qTA[:], 0.0), w=[rqTA])
        G(lambda e: e.memset(qTB[:], 0.0), w=[rqTB])
        nI = sb("nI", [128, 256], I32); rnI = Res("nI")
        posi = nI[0:16, 0:128]; rposi = rnI
        posf = Fp[8][0:16, 0:128]; rposf = rF[8]
        DMA("sync", lambda e: e.dma_start(out=posi, in_=pos_d), w=[rposi])
        V(lambda e: e.tensor_copy(out=posf, in_=posi), [rposi], [rposf])
        T(lambda e: e.matmul(psG[:, 0:16], lhsT=posf, rhs=cst[0:16, 0:16], start=True, stop=True), [rposf, rcst], [rpsG])
        post, r_post = sm("post", 80, 16)
        V(lambda e: e.tensor_copy(out=post, in_=psG[:, 0:16]), [rpsG], [r_post])
        ang = Fp[0][:, 0:256].rearrange("p (i j) -> p i j", i=NT)
        V(lambda e: e.tensor_tensor(out=ang, in0=post.unsqueeze(2).to_broadcast([128, NT, 16]),
                                    in1=cst[:, 515:531].unsqueeze(1).to_broadcast([128, NT, 16]), op=ALU.mult),
          [r_post, rcst], [rF[0]])
        ang_lo = Fp[1][:, 0:256].rearrange("p (i j) -> p i j", i=NT)
        V(lambda e: e.tensor_tensor(out=ang_lo, in0=post.unsqueeze(2).to_broadcast([128, NT, 16]),
                                    in1=cst[:, 547:563].unsqueeze(1).to_broadcast([128, NT, 16]), op=ALU.mult),
          [r_post, rcst], [rF[1]])
        V(lambda e: e.tensor_tensor(out=ang, in0=ang, in1=ang_lo, op=ALU.add), [rF[0], rF[1]], [rF[0]])
        V(lambda e: e.tensor_tensor(out=ang, in0=ang, in1=cst[:, 531:547].unsqueeze(1).to_broadcast([128, NT, 16]),
                                    op=ALU.add), [rF[0], rcst], [rF[0]])
        V(lambda e: e.tensor_scalar(out=Fp[1][:], in0=Fp[0][:], scalar1=float(1.0 / (2 * np.pi)), scalar2=None,
                                    op0=ALU.mult), [rF[0]], [rF[1]])
        V(lambda e: e.tensor_copy(out=nI[:], in_=Fp[1][:]), [rF[1]], [rnI])
        V(lambda e: e.tensor_copy(out=Fp[1][:], in_=nI[:]), [rnI], [rF[1]])
        C1 = 6.28125
        C2 = float(2 * np.pi - 6.28125)
        V(lambda e: e.scalar_tensor_tensor(out=Fp[2][:], in0=Fp[1][:], scalar=-C1, in1=Fp[0][:],
                                           op0=ALU.mult, op1=ALU.add), [rF[1], rF[0]], [rF[2]])
        V(lambda e: e.scalar_tensor_tensor(out=Fp[2][:], in0=Fp[1][:], scalar=-C2, in1=Fp[2][:],
                                           op0=ALU.mult, op1=ALU.add), [rF[1], rF[2]], [rF[2]])
        V(lambda e: e.tensor_scalar(out=Fp[2][:], in0=Fp[2][:], scalar1=float(np.pi), scalar2=float(-np.pi),
                                    op0=ALU.min, op1=ALU.max), [rF[2]], [rF[2]])
        A(lambda e: e.activation(out=Fp[3][:], in_=Fp[2][:], func=AF.Sin), [rF[2]], [rF[3]])
        sc = Fp[3][:, 0:256].rearrange("p (i j) -> p i j", i=NT)
        V(lambda e: e.tensor_copy(out=cs[:, :, 0:8], in_=sc[:, :, 8:16]), [rF[3]], [rrope])
        V(lambda e: e.tensor_copy(out=cs[:, :, 8:16], in_=sc[:, :, 8:16]), [rF[3]], [rrope])
        V(lambda e: e.tensor_scalar(out=sn[:, :, 0:8], in0=sc[:, :, 0:8], scalar1=-1.0, scalar2=None, op0=ALU.mult),
          [rF[3]], [rrope])
        V(lambda e: e.tensor_copy(out=sn[:, :, 8:16], in_=sc[:, :, 0:8]), [rF[3]], [rrope])
        def load_w(dst, rdst, src_ap):
            return DMA("gpsimd", lambda e: e.dma_start(out=dst, in_=src_ap), w=[rdst])

        def w_in_cols(l, c0, n):
            return win_d[l].rearrange("(c p) n -> p c n", p=128)[:, :, c0:c0 + n]

        psGb = psG[:].bitcast(BF16)

        def rms_to_T(src_tile, rsrc, gain, rgain, dstT, rdst, col0, ssc, r_ssc, tsc, r_tsc, rsc, r_rsc, k):
            hb = hbs[k % 2]; rhb = rhbs[k % 2]
            pst, rpst = (psT, rpsT) if k % 2 == 0 else (psGb, rpsG)
            A(lambda e: e.activation(out=hb[:], in_=src_tile, func=AF.Square, accum_out=ssc[:, k:k + 1]),
              [rsrc], [rhb, r_ssc])
            A(lambda e: e.activation(out=tsc[:, k:k + 1], in_=ssc[:, k:k + 1], func=AF.Ln, scale=1.0 / D, bias=EPS),
              [r_ssc], [r_tsc])
            A(lambda e: e.activation(out=rsc[:, k:k + 1], in_=tsc[:, k:k + 1], func=AF.Exp, scale=-0.5),
              [r_tsc], [r_rsc])
            V(lambda e: e.scalar_tensor_tensor(out=hb[:], in0=src_tile, scalar=rsc[:, k:k + 1], in1=gain,
                                               op0=ALU.mult, op1=ALU.mult), [rsrc, r_rsc, rgain], [rhb])
            for c in range(8):
                T(lambda e, c=c: e.transpose(out=pst[:, c * 128:(c + 1) * 128], in_=hb[:, c * 128:(c + 1) * 128],
                                             identity=idb[:]), [rhb, ridb], [rpst])
            src3 = pst[:, 0:1024].rearrange("p (c t) -> p c t", c=8)
            evac(lambda e: e.tensor_copy(out=dstT[:, :, col0:col0 + 128], in_=src3),
                 lambda e: e.copy(out=dstT[:, :, col0:col0 + 128], in_=src3), [rpst], [rdst])

        def proj(lhs_cols, rlhs, wt, rwt, ncols=512, lhsT_src=None):
            b = nxt("pa", 2)
            src = hT if lhsT_src is None else lhsT_src
            for c in range(8):
                T(lambda e, c=c: e.matmul(psA[b][:, 0:ncols], lhsT=src[:, c, lhs_cols:lhs_cols + 128],
                                          rhs=wt[:, c, 0:ncols], start=(c == 0), stop=(c == 7)),
                  [rlhs] + list(rwt), [rpsA[b]])
            return psA[b], rpsA[b]

        def headnorm(src_ps, rps, H, Dh, gain, outF, routF, tmpA, rtmpA, tmpB, rtmpB):
            n = H * Dh
            A(lambda e: e.activation(out=tmpA[:, 0:n], in_=src_ps, func=AF.Square), [rps], [rtmpA])
            V(lambda e: e.tensor_reduce(out=ss4[:, 0:H], in_=tmpA[:, 0:n].rearrange("p (h d) -> p h d", h=H),
                                        axis=AX.X, op=ALU.add), [rtmpA], [r_ss4])
            A(lambda e: e.activation(out=t4[:, 0:H], in_=ss4[:, 0:H], func=AF.Ln, scale=1.0 / Dh, bias=EPS),
              [r_ss4], [r_t4])
            A(lambda e: e.activation(out=rs4[:, 0:H], in_=t4[:, 0:H], func=AF.Exp, scale=-0.5), [r_t4], [r_rs4])
            V(lambda e: e.tensor_tensor(out=tmpB[:, 0:n].rearrange("p (h d) -> p h d", h=H),
                                        in0=src_ps.rearrange("p (h d) -> p h d", h=H),
                                        in1=rs4[:, 0:H].unsqueeze(2).to_broadcast([128, H, Dh]), op=ALU.mult),
              [rps, r_rs4], [rtmpB])
            (G if GOFF else V)(lambda e: e.tensor_tensor(out=outF[:, 0:n].rearrange("p (h d) -> p h d", h=H),
                                                         in0=tmpB[:, 0:n].rearrange("p (h d) -> p h d", h=H),
                                                         in1=gain.unsqueeze(1).to_broadcast([128, H, Dh]), op=ALU.mult),
                               [rtmpB, rgains], [routF])

        def rope(Fx, rFx, i, tR, rtR):
            x3 = Fx[:, 0:256].rearrange("p (h d) -> p h d", h=4)
            a3 = tR[:, 0:64].rearrange("p (h d) -> p h d", h=4)
            b3 = tR[:, 64:128].rearrange("p (h d) -> p h d", h=4)
            rtA = rtR
            rtB = rtR
            G(lambda e: e.tensor_tensor(out=a3, in0=x3[:, :, 0:16], in1=cs[:, i, :].unsqueeze(1).to_broadcast([128, 4, 16]),
                                        op=ALU.mult), [rFx, rrope], [rtA])
            G(lambda e: e.tensor_tensor(out=b3[:, :, 0:8], in0=x3[:, :, 8:16],
                                        in1=sn[:, i, 0:8].unsqueeze(1).to_broadcast([128, 4, 8]), op=ALU.mult),
              [rFx, rrope], [rtB])
            G(lambda e: e.tensor_tensor(out=b3[:, :, 8:16], in0=x3[:, :, 0:8],
                                        in1=sn[:, i, 8:16].unsqueeze(1).to_broadcast([128, 4, 8]), op=ALU.mult),
              [rFx, rrope], [rtB])
            G(lambda e: e.tensor_tensor(out=x3[:, :, 0:16], in0=a3, in1=b3, op=ALU.add), [rtA, rtB], [rFx])

        def silu_ps(dst, rdst, src_ps, rps):
            A(lambda e: e.activation(out=dst, in_=src_ps, func=AF.Exp, scale=-1.0), [rps], [rdst])
            A(lambda e: e.activation(out=dst, in_=dst, func=AF.Ln, bias=1.0), [rdst], [rdst])
            A(lambda e: e.activation(out=dst, in_=dst, func=AF.Exp, scale=-1.0), [rdst], [rdst])
            V(lambda e: e.tensor_tensor(out=dst, in0=src_ps, in1=dst, op=ALU.mult), [rps, rdst], [rdst])

        SC = [((Fp[0], rF[0]), (Fp[1], rF[1]), (Fp[2], rF[2]), (Fp[3], rF[3])),
              ((Fp[4], rF[4]), (Fp[6], rF[6]), (Fp[7], rF[7]), (Fp[8], rF[8]))]
        AUG = [(k_aug, rk_aug), (q_aug, rq_aug)]
        if _os.environ.get("NOPAR", "0") == "1":
            SC[1] = SC[0]
        if _os.environ.get("NOAUG", "0") == "1":
            AUG[1] = AUG[0]

        dec8, _r = sm("dec8", 96, 8)
        r_dec8 = [Res("dec8a"), Res("dec8b")]
        sso4, _r2 = sm("sso4", 104, 4)
        r_sso2 = [Res("ssoA"), Res("ssoB")]
        dummy = sb("dummy", [128, 8])
        kflat = kT[:].rearrange("p h t -> p (h t)")
        kf32 = kflat[:, 0:4608].bitcast(F32)
        Fq = [kf32[:, k * 256:(k + 1) * 256] for k in range(9)]
        rFq = [Res("Fq%d" % k) for k in range(9)]
        Bq = [kflat[:, 4608 + k * 256:4608 + (k + 1) * 256] for k in range(7)]
        rBq = [Res("Bq%d" % k) for k in range(7)]
        HSETS = [([t[:] for t in Fp], rF, [t[:] for t in Bp], rB), (Fq, rFq, Bq, rBq)]

        def barrier():
            G(lambda e: e.memset(dummy[:], 0.0), w=list(rkT) + rFq + rBq + list(rv) + [rstage, rg_bc])

        rrow = Res("pe_rowfence")

        out_dmas = []

        def outproj_tile(i, r, last, obanks=None):
            yb = nxt("yt", 2)
            for pp in range(2):
                T(lambda e, pp=pp: e.transpose(out=psT[:, pp * 128:(pp + 1) * 128], in_=y_tok[:, r, pp * 128:(pp + 1) * 128],
                                               identity=idb[:]), [ry[r], ridb], [rpsT])
            evac(lambda e: e.tensor_copy(out=yT[yb][:], in_=psT[:, 0:256]),
                 lambda e: e.copy(out=yT[yb][:], in_=psT[:, 0:256]), [rpsT], [ryT[yb]])
            for half in range(2):
                if obanks is None:
                    b = nxt("pa", 2)
                    pso, rpso = psA[b], rpsA[b]
                else:
                    pso, rpso = obanks[half]
                for pp in range(2):
                    T(lambda e, pp=pp, half=half, pso=pso: e.matmul(pso[:, 0:512], lhsT=yT[yb][:, pp * 128:(pp + 1) * 128],
                                                                    rhs=wo[:, pp, half * 512:(half + 1) * 512],
                                                                    start=(pp == 0), stop=(pp == 1)),
                      [ryT[yb], rwo], [rpso])
                V(lambda e, half=half, pso=pso: e.tensor_tensor(out=x_tok[:, i, half * 512:(half + 1) * 512], in0=pso[:, 0:512],
                                                                in1=x_tok[:, i, half * 512:(half + 1) * 512], op=ALU.add),
                  [rpso, rx[i]], [rx[i]])
            if last:
                out_dmas.append(DMA("sync", lambda e: e.dma_start(out=out_d[i * 128:(i + 1) * 128, :], in_=x_tok[:, i, :]),
                                    r=[rx[i]]))

        def attn_chunk(Q, H, Dh, KP, scale, key_tiles, causal, kTsrc, rkTsrc, vsrc, rvsrc, qview, rqT, sz, rsz):
            DA = Dh + 1
            for h in range(H):
                if Dh == 64:
                    o_b = h % 2
                    banks = [o_b, o_b, o_b, o_b]
                    offs = [0, DA, 2 * DA, 3 * DA]
                else:
                    banks = [0, 0, 1, 1]
                    offs = [0, DA, 0, DA]
                started = set()
                kts = key_tiles(Q)
                for kt in kts:
                    j = kt - 4 * Q if causal else -1
                    q0 = max(j, 0) * 128
                    N = 512 - q0
                    sbk = nxt("ps", PS3)
                    pss, rpss = ((psS[0], rpsS[0]), (psS[1], rpsS[1]), (psG, rpsG))[sbk]
                    T(lambda e, kt=kt, h=h, q0=q0, N=N, pss=pss: e.matmul(
                        pss[:, 0:N], lhsT=kTsrc(h, kt), rhs=qview(h)[:, q0:512], start=True, stop=True),
                      [rkTsrc(kt), rqT], [rpss])
                    pb = nxt("pt", 4)
                    A(lambda e, N=N, pss=pss, pb=pb: e.activation(out=pT[pb][:, 0:N], in_=pss[:, 0:N], func=AF.Exp,
                                                                  scale=scale), [rpss], [rpT[pb]])
                    if j >= 0:
                        (V if (MASKV and nxt("mk", 2) == 0) else G)(
                            lambda e, pb=pb: e.tensor_tensor(out=pT[pb][:, 0:128], in0=pT[pb][:, 0:128], in1=trib[:],
                                                             op=ALU.mult), [rpT[pb], rtrib], [rpT[pb]])
                    for r in range(max(j, 0), 4):
                        bk = banks[r]
                        first = bk not in started
                        started.add(bk)
                        T(lambda e, r=r, kt=kt, h=h, q0=q0, pb=pb, bk=bk, first=first: e.matmul(
                            psO[bk][:, offs[r]:offs[r] + DA], lhsT=pT[pb][:, r * 128 - q0:r * 128 - q0 + 128],
                            rhs=vsrc(h, kt), start=first, stop=False, skip_group_check=True),
                          [rpT[pb], rvsrc(kt)], [rpsO[bk]])
                for r in range(4):
                    bk = banks[r]
                    V(lambda e, r=r, bk=bk: e.reciprocal(out=rden[:, r:r + 1], in_=psO[bk][:, offs[r] + Dh:offs[r] + DA]),
                      [rpsO[bk]], [r_rden])
                    V(lambda e, r=r, bk=bk, h=h: e.scalar_tensor_tensor(
                        out=y_tok[:, r, h * Dh:(h + 1) * Dh], in0=psO[bk][:, offs[r]:offs[r] + Dh], scalar=rden[:, r:r + 1],
                        in1=sz[:, r, h * Dh:(h + 1) * Dh], op0=ALU.mult, op1=ALU.mult),
                      [rpsO[bk], r_rden, rsz[r]], [ry[r]])

        for li, l in enumerate(layers):
            last_layer = (li == len(layers) - 1)
            DMA("sync", lambda e, l=l: e.dma_start(out=g_bcv, in_=ng_d[l:l + 1, :].partition_broadcast(128)), w=[rg_bc])
            for dst, src in ((gq_bc, gq_d), (gk_bc, gk_d), (go_bc, go_d), (gmq_bc, gmq_d), (gmk_bc, gmk_d)):
                DMA("sync", lambda e, l=l, dst=dst, src=src: e.dma_start(out=dst[:], in_=src[l:l + 1, :].partition_broadcast(128)),
                    w=[rgains])
            for i in range(NT):
                if li == 0:
                    DMA(("scalar" if (XQ and i % 2 == 1) else "sync"), lambda e, i=i: e.dma_start(out=x_tok[:, i, :], in_=x_d[i * 128:(i + 1) * 128, :]), w=[rx[i]])
                rms_to_T(x_tok[:, i, :], rx[i], g_bcv, rg_bc, hT, rhT[i], i * 128, ss16, r_ss16, t16, r_t16,
                         rstd16, r_rstd16, i)
            glist = [g for g in ALL_GROUPS if g in groups]
            for gi, gname in enumerate(glist):
                last = last_layer and gi == len(glist) - 1
                kind = gname[0]
                g = int(gname[1])
                if kind == "A":
                    w1 = nxt("w", 3); w2 = nxt("w", 3)
                    load_w(wb[w1][:, :, 0:256], rwb[w1][0], w_in_cols(l, 512 + 256 * g, 256))
                    load_w(wb[w1][:, :, 256:512], rwb[w1][1], w_in_cols(l, 1024 + 256 * g, 256))
                    load_w(wb[w2][:, :, 0:256], rwb[w2][0], w_in_cols(l, 256 * g, 256))
                    load_w(wb[w2][:, :, 256:512], rwb[w2][1], w_in_cols(l, 3584 + 256 * g, 256))
                    load_w(wo[:], rwo, wout_d[l, 256 * g:256 * g + 256, :].rearrange("(c p) n -> p c n", p=128))
                    barrier()
                    G(lambda e: e.memset(v_aug[:, :, :, 64:65], 1.0), w=rv)
                    V(lambda e: e.memset(psG[:, 0:16], 0.0), w=[rpsG])
                    for i in range(NT):
                        n_blk = i // 2
                        (tA, rtA), (tB, rtB), (FO, rFO), (tR, rtR) = SC[(i % 2) * PAR1]
                        ka, rka = AUG[(i % 2) * PAR1]
                        pa, rpa = proj(i * 128, rhT[i], wb[w1], rwb[w1])
                        headnorm(pa[:, 0:256], rpa, 4, 64, gk_bc[:], FO, rFO, tA, rtA, tB, rtB)
                        evac(lambda e, i=i, pa=pa: e.tensor_copy(out=v_aug[:, i, :, 0:64],
                                                                 in_=pa[:, 256:512].rearrange("p (h d) -> p h d", h=4)),
                             lambda e, i=i, pa=pa: e.copy(out=v_aug[:, i, :, 0:64],
                                                          in_=pa[:, 256:512].rearrange("p (h d) -> p h d", h=4)),
                             [rpa], [rv[i]])
                        rope(FO, rFO, i, tR, rtR)
                        G(lambda e, ka=ka, FO=FO: e.tensor_copy(out=ka[:, :, 0:64], in_=FO[:].rearrange("p (h d) -> p h d", h=4)),
                          [rFO], [rka])
                        G(lambda e, ka=ka: e.memset(ka[:, :, 64:72], 0.0), w=[rka])
                        G(lambda e, ka=ka, n_blk=n_blk: e.memset(ka[:, :, 64 + n_blk:65 + n_blk], 1.0), w=[rka])
                        for pp in range(2):
                            T(lambda e, pp=pp, n_blk=n_blk, FO=FO: e.matmul(psG[:, pp * 8 + n_blk:pp * 8 + n_blk + 1],
                                                                            lhsT=FO[:, pp * 128:(pp + 1) * 128], rhs=cst[:, 514:515],
                                                                            start=False, stop=False, skip_group_check=True),
                              [rFO, rcst], [rpsG])
                        for h in range(4):
                            T(lambda e, h=h, ka=ka: e.transpose(out=psT[0:72, h * 128:(h + 1) * 128], in_=ka[:, h, :], identity=idb[:]),
                              [rka, ridb], [rpsT])
                        src3 = psT[0:72, 0:512].rearrange("p (h t) -> p h t", h=4)
                        evac(lambda e, i=i, src3=src3: e.tensor_copy(out=kT[0:72, :, i * 128:(i + 1) * 128], in_=src3),
                             lambda e, i=i, src3=src3: e.copy(out=kT[0:72, :, i * 128:(i + 1) * 128], in_=src3),
                             [rpsT], [rkT[i]])
                    G(lambda e: e.memset(kmT32[:], 0.0), w=[rkm])
                    A(lambda e: e.copy(out=kmT32[0:64, :, 0, :], in_=psG[0:64, 0:16].rearrange("p (a n) -> p a n", a=2)), [rpsG], [rkm])
                    A(lambda e: e.copy(out=kmT32[64:128, :, 1, :], in_=psG[64:128, 0:16].rearrange("p (a n) -> p a n", a=2)), [rpsG], [rkm])
                    for Q in range(4):
                        qT3 = qTs[Q % 2][:, 0:2048].rearrange("p (h t) -> p h t", h=4)
                        rqT = rqTs[Q % 2]
                        sz = szs[Q % 2]; rsz = rszs[Q % 2]
                        for r in range(4):
                            i = 4 * Q + r
                            own = i // 2
                            P.phase = "chain"
                            par = (i % 2) * PAR2 * (1 if (own < 4 or GPAR) else 0)
                            (tA, rtA), (tB, rtB), (FO, rFO), (tR, rtR) = SC[par]
                            qa, rqa = AUG[par]
                            pa, rpa = proj(i * 128, rhT[i], wb[w2], rwb[w2])
                            silu_ps(sz[:, r, :], rsz[r], pa[:, 256:512], rpa)
                            headnorm(pa[:, 0:256], rpa, 4, 64, gq_bc[:], FO, rFO, tA, rtA, tB, rtB)
                            rope(FO, rFO, i, tR, rtR)
                            G(lambda e, qa=qa, FO=FO: e.tensor_copy(out=qa[:, :, 0:64], in_=FO[:].rearrange("p (h d) -> p h d", h=4)),
                              [rFO], [rqa])
                            if own >= 4:
                                P.phase = "gate"
                                for pp in range(2):
                                    T(lambda e, pp=pp, FO=FO: e.matmul(psG[:, pp * 128:(pp + 1) * 128],
                                                                       lhsT=FO[:, pp * 128:(pp + 1) * 128], rhs=ident32,
                                                                       start=True, stop=True),
                                      [rFO, rcst], [rpsG])
                                A(lambda e: e.copy(out=Fp[5][:], in_=psG[:, 0:256]), [rpsG], [rF[5]])
                                for h in range(4):
                                    T(lambda e, h=h, own=own: e.matmul(
                                        psG[:, 256 + h * 8:256 + h * 8 + own],
                                        lhsT=Fp[5][:, (h // 2) * 128:(h // 2) * 128 + 128],
                                        rhs=kmT32[:, h // 2, h % 2, 0:own], start=True, stop=True),
                                      [rF[5], rkm], [rpsG])
                                V(lambda e: e.memset(gm[:], -1.0e30), w=[rgm])
                                V(lambda e, own=own: e.tensor_copy(
                                    out=gm[:, :, 0:own], in_=psG[:, 256:288].rearrange("p (h n) -> p h n", h=4)[:, :, 0:own]),
                                  [rpsG], [rgm])
                                for h in range(4):
                                    V(lambda e, h=h: e.max(out=top8[:, h, :], in_=gm[:, h, :]), [rgm], [rtop8])
                                V(lambda e: e.tensor_tensor(out=selb[:], in0=gm[:], in1=top8[:, :, 2:3].to_broadcast([128, 4, 8]),
                                                            op=ALU.is_ge), [rgm, rtop8], [rselb])
                                V(lambda e, qa=qa: e.tensor_scalar(out=qa[:, :, 64:72], in0=selb[:], scalar1=30000.0,
                                                                   scalar2=-30000.0, op0=ALU.mult, op1=ALU.add), [rselb], [rqa])
                                V(lambda e, qa=qa, own=own: e.memset(qa[:, :, 64 + own:65 + own], 0.0), w=[rqa])
                            else:
                                G(lambda e, qa=qa: e.memset(qa[:, :, 64:72], 0.0), w=[rqa])
                            P.phase = None
                            for h in range(4):
                                T(lambda e, h=h, qa=qa: e.transpose(out=psT[0:72, h * 128:(h + 1) * 128], in_=qa[:, h, :],
                                                                    identity=idb[:]), [rqa, ridb], [rpsT])
                            src3 = psT[0:72, 0:512].rearrange("p (h t) -> p h t", h=4)
                            evac(lambda e, r=r, src3=src3, qT3=qT3: e.tensor_copy(out=qT3[0:72, :, r * 128:(r + 1) * 128], in_=src3),
                                 lambda e, r=r, src3=src3, qT3=qT3: e.copy(out=qT3[0:72, :, r * 128:(r + 1) * 128], in_=src3),
                                 [rpsT], [rqT])
                        attn_chunk(Q, 4, 64, 72, 0.125, lambda Q: list(range(4 * Q + 4)), True,
                                   lambda h, kt: kT[0:72, h, kt * 128:(kt + 1) * 128], lambda kt: rkT[kt],
                                   lambda h, kt: v_aug[:, kt, h, :], lambda kt: rv[kt],
                                   lambda h, qT3=qT3: qT3[0:72, h, :], rqT, sz, rsz)
                        for r in range(4):
                            outproj_tile(4 * Q + r, r, last, obanks=([(psO[0], rpsO[0]), (psO[1], rpsO[1])] if OPB else None))
                elif kind == "M":
                    w1 = nxt("w", 3); w2 = nxt("w", 3)
                    wkvv = wkv_d[l].rearrange("(c p) n -> p c n", p=128)
                    load_w(wb[w1][:, :, 0:256], rwb[w1][0], wkvv[:, :, 256 * g:256 * g + 256])
                    load_w(wb[w1][:, :, 256:512], rwb[w1][1], wkvv[:, :, 512 + 256 * g:512 + 256 * g + 256])
                    load_w(wb[w2][:, :, 0:256], rwb[w2][0], w_in_cols(l, 3072 + 256 * g, 256))
                    load_w(wb[w2][:, :, 256:512], rwb[w2][1], w_in_cols(l, 4608 + 256 * g, 256))
                    load_w(wo[:], rwo, wout_d[l, 1024 + 256 * g:1024 + 256 * g + 256, :].rearrange("(c p) n -> p c n", p=128))
                    if g == 0 or ("M0" not in groups):
                        barrier()
                        DMA("sync", lambda e, l=l: e.dma_start(out=g_bcv, in_=mng_d[l:l + 1, :].partition_broadcast(128)),
                            w=[rg_bc])
                        for mt in range(2):
                            DMA("sync", lambda e, mt=mt: e.dma_start(out=stage, in_=mem_d[mt * 128:(mt + 1) * 128, :]),
                                w=[rstage])
                            rms_to_T(stage, rstage, g_bcv, rg_bc, memT, rmemT, mt * 128, ss16, r_ss16, t16, r_t16,
                                     rstd16, r_rstd16, mt)
                    for mt in range(2):
                        (tA, rtA), (tB, rtB), (FO, rFO), (tR, rtR) = SC[mt % 2]
                        pa, rpa = proj(mt * 128, rmemT, wb[w1], rwb[w1], lhsT_src=memT)
                        headnorm(pa[:, 0:256], rpa, 2, 128, gmk_bc[:], FO, rFO, tA, rtA, tB, rtB)
                        evac(lambda e, mt=mt, pa=pa: e.tensor_copy(out=vm_aug[:, mt, :, 0:128],
                                                                   in_=pa[:, 256:512].rearrange("p (h d) -> p h d", h=2)),
                             lambda e, mt=mt, pa=pa: e.copy(out=vm_aug[:, mt, :, 0:128],
                                                            in_=pa[:, 256:512].rearrange("p (h d) -> p h d", h=2)),
                             [rpa], [rvm])
                        G(lambda e, mt=mt, FO=FO: e.tensor_copy(out=Bp[mt % 2][:], in_=FO[:]), [rFO], [rB[mt % 2]])
                        for hh in range(2):
                            T(lambda e, hh=hh, mt=mt: e.transpose(out=psT[:, hh * 128:(hh + 1) * 128],
                                                                  in_=Bp[mt % 2][:, hh * 128:(hh + 1) * 128],
                                                                  identity=idb[:]), [rB[mt % 2], ridb], [rpsT])
                        src3 = psT[:, 0:256].rearrange("p (h t) -> p h t", h=2)
                        evac(lambda e, mt=mt, src3=src3: e.tensor_copy(out=kmT[:, :, mt * 128:(mt + 1) * 128], in_=src3),
                             lambda e, mt=mt, src3=src3: e.copy(out=kmT[:, :, mt * 128:(mt + 1) * 128], in_=src3),
                             [rpsT], [rkmT])
                    for Q in range(4):
                        qm3 = qTs[Q % 2][:, 0:1024].rearrange("p (h t) -> p h t", h=2)
                        rqT = rqTs[Q % 2]
                        sz = szs[Q % 2]; rsz = rszs[Q % 2]
                        for r in range(4):
                            i = 4 * Q + r
                            (tA, rtA), (tB, rtB), (FO, rFO), (tR, rtR) = SC[i % 2]
                            pa, rpa = proj(i * 128, rhT[i], wb[w2], rwb[w2])
                            silu_ps(sz[:, r, :], rsz[r], pa[:, 256:512], rpa)
                            headnorm(pa[:, 0:256], rpa, 2, 128, gmq_bc[:], FO, rFO, tA, rtA, tB, rtB)
                            G(lambda e, i=i, FO=FO: e.tensor_copy(out=Bp[i % 2][:], in_=FO[:]), [rFO], [rB[i % 2]])
                            for hh in range(2):
                                T(lambda e, hh=hh, i=i: e.transpose(out=psT[:, hh * 128:(hh + 1) * 128],
                                                                    in_=Bp[i % 2][:, hh * 128:(hh + 1) * 128], identity=idb[:]),
                                  [rB[i % 2], ridb], [rpsT])
                            src3 = psT[:, 0:256].rearrange("p (h t) -> p h t", h=2)
                            evac(lambda e, r=r, src3=src3, qm3=qm3: e.tensor_copy(out=qm3[:, :, r * 128:(r + 1) * 128], in_=src3),
                                 lambda e, r=r, src3=src3, qm3=qm3: e.copy(out=qm3[:, :, r * 128:(r + 1) * 128], in_=src3),
                                 [rpsT], [rqT])
                        attn_chunk(Q, 2, 128, 128, float(128 ** -0.5), lambda Q: [0, 1], False,
                                   lambda h, kt: kmT[:, h, kt * 128:(kt + 1) * 128], lambda kt: rkmT,
                                   lambda h, kt: vm_aug[:, kt, h, :], lambda kt: rvm,
                                   lambda h, qm3=qm3: qm3[:, h, :], rqT, sz, rsz)
                        for r in range(4):
                            outproj_tile(4 * Q + r, r, last, obanks=([(psO[0], rpsO[0]), (psO[1], rpsO[1])] if OPB else None))
                else:
                    w1 = nxt("w", 3); w2 = nxt("w", 3)
                    load_w(wb[w1][:, :, 0:256], rwb[w1][0], w_in_cols(l, 1536 + 256 * g, 256))
                    load_w(wb[w1][:, :, 256:512], rwb[w1][1], w_in_cols(l, 2048 + 256 * g, 256))
                    load_w(wb[w2][:, :, 0:256], rwb[w2][0], w_in_cols(l, 2560 + 256 * g, 256))
                    load_w(wb[w2][:, :, 256:512], rwb[w2][1], w_in_cols(l, 4096 + 256 * g, 256))
                    load_w(wo[:], rwo, wout_d[l, 512 + 256 * g:512 + 256 * g + 256, :].rearrange("(c p) n -> p c n", p=128))
                    barrier()
                    if l != 0:
                        DMA("sync", lambda e, g=g: e.dma_start(out=lb_g, in_=lbl_d[1:2, 256 * g:256 * g + 256].partition_broadcast(128)), w=[rlb])
                        DMA("sync", lambda e, g=g: e.dma_start(out=oml_g, in_=lbl_d[0:1, 256 * g:256 * g + 256].partition_broadcast(128)), w=[rlb])
                        V(lambda e: e.tensor_tensor(out=lb_g, in0=lb_g, in1=oml_g, op=ALU.subtract), [rlb], [rlb])
                        A(lambda e: e.activation(out=lb_g, in_=lb_g, func=AF.Exp, scale=-1.0), [rlb], [rlb])
                        A(lambda e: e.activation(out=lb_g, in_=lb_g, func=AF.Ln, bias=1.0), [rlb], [rlb])
                        A(lambda e: e.activation(out=lb_g, in_=lb_g, func=AF.Exp, scale=-1.0), [rlb], [rlb])
                        V(lambda e: e.tensor_scalar(out=oml_g, in0=lb_g, scalar1=-1.0, scalar2=1.0, op0=ALU.mult, op1=ALU.add),
                          [rlb], [rlb])
                    for hh in range(2):
                        G(lambda e, hh=hh: e.memset(S32[:, hh, :], 0.0), w=[rS32[hh]])
                        G(lambda e, hh=hh: e.memset(Sbf[:, 0, hh, :], 0.0), w=[rSbf[0][hh]])
                    Tri32 = cst[:, 256:384]
                    TriE32 = cst[:, 384:512]
                    for i in range(NT):
                        Fs, rFs, Bs, rBs = HSETS[i % 2]
                        sl = i % 4
                        sz = szs[(i // 4) % 2]; rsz = rszs[(i // 4) % 2]
                        pq, rpq = proj(i * 128, rhT[i], wb[w1], rwb[w1])
                        silu_ps(Fs[0], rFs[0], pq[:, 0:256], rpq)
                        A(lambda e, pq=pq, Fs=Fs: e.activation(out=Fs[1], in_=pq[:, 256:512], func=AF.Exp, scale=-1.0), [rpq], [rFs[1]])
                        A(lambda e, Fs=Fs: e.activation(out=Fs[1], in_=Fs[1], func=AF.Ln, bias=1.0), [rFs[1]], [rFs[1]])
                        pi_, rpi = proj(i * 128, rhT[i], wb[w2], rwb[w2])
                        silu_ps(sz[:, sl, :], rsz[sl], pi_[:, 256:512], rpi)
                        V(lambda e, pi_=pi_, Bs=Bs: e.tensor_copy(out=Bs[0], in_=pi_[:, 0:256]), [rpi], [rBs[0]])
                        if l == 0:
                            A(lambda e, Fs=Fs: e.activation(out=Fs[2], in_=Fs[1], func=AF.Copy, scale=-1.0), [rFs[1]], [rFs[2]])
                            A(lambda e, Fs=Fs: e.activation(out=Fs[1], in_=Fs[1], func=AF.Exp, scale=-1.0), [rFs[1]], [rFs[1]])
                        else:
                            A(lambda e, Fs=Fs: e.activation(out=Fs[1], in_=Fs[1], func=AF.Exp, scale=-1.0), [rFs[1]], [rFs[1]])
                            V(lambda e, g=g, Fs=Fs: e.tensor_tensor(out=Fs[1], in0=Fs[1], in1=oml_g,
                                                                    op=ALU.mult), [rFs[1], rlb], [rFs[1]])
                            V(lambda e, g=g, Fs=Fs: e.tensor_tensor(out=Fs[1], in0=Fs[1], in1=lb_g,
                                                                    op=ALU.add), [rFs[1], rlb], [rFs[1]])
                            A(lambda e, Fs=Fs: e.activation(out=Fs[2], in_=Fs[1], func=AF.Ln), [rFs[1]], [rFs[2]])
                        (G if GOFF else V)(lambda e, Fs=Fs: e.tensor_scalar(out=Fs[3], in0=Fs[1], scalar1=-1.0, scalar2=1.0, op0=ALU.mult,
                                                                            op1=ALU.add), [rFs[1]], [rFs[3]])
                        T(lambda e, Fs=Fs: e.matmul(psG[:, 0:256], lhsT=Tri32, rhs=Fs[2], start=True, stop=True),
                          [rcst, rFs[2]], [rpsG])
                        T(lambda e, Fs=Fs: e.matmul(psG[:, 256:512], lhsT=TriE32, rhs=Fs[2], start=True, stop=True),
                          [rcst, rFs[2]], [rpsG])
                        for hh in range(2):
                            T(lambda e, hh=hh, Fs=Fs: e.matmul(psS[1][:, 256 + 2 * hh:256 + 2 * hh + 2],
                                                               lhsT=Fs[2][:, hh * 128:(hh + 1) * 128],
                                                               rhs=cst[:, 512:514], start=True, stop=True), [rFs[2], rcst], [rpsS[1]])
                        dsl = dec8[:, 4 * (i % 2):4 * (i % 2) + 4]
                        rds = r_dec8[i % 2]
                        A(lambda e, dsl=dsl: e.activation(out=dsl, in_=psS[1][:, 256:260], func=AF.Exp), [rpsS[1]], [rds])
                        A(lambda e, Fs=Fs: e.activation(out=Fs[4], in_=psG[:, 0:256], func=AF.Exp), [rpsG], [rFs[4]])
                        A(lambda e, Fs=Fs: e.activation(out=Fs[5], in_=psG[:, 0:256], func=AF.Exp, scale=-1.0), [rpsG], [rFs[5]])
                        A(lambda e, Fs=Fs: e.activation(out=Fs[6], in_=psG[:, 256:512], func=AF.Exp), [rpsG], [rFs[6]])
                        V(lambda e, Fs=Fs, Bs=Bs: e.tensor_tensor(out=Bs[1], in0=Fs[0], in1=Fs[4], op=ALU.mult), [rFs[0], rFs[4]], [rBs[1]])
                        G(lambda e, Fs=Fs, Bs=Bs: e.tensor_tensor(out=Bs[2], in0=Fs[3], in1=Fs[5], op=ALU.mult), [rFs[3], rFs[5]], [rBs[2]])
                        G(lambda e, Fs=Fs, Bs=Bs: e.tensor_tensor(out=Bs[3], in0=Fs[3], in1=Fs[6], op=ALU.mult), [rFs[3], rFs[6]], [rBs[3]])
                        for hh in range(2):
                            T(lambda e, hh=hh, Bs=Bs: e.transpose(out=psT[:, hh * 128:(hh + 1) * 128], in_=Bs[1][:, hh * 128:(hh + 1) * 128],
                                                                  identity=idb[:]), [rBs[1], ridb], [rpsT])
                            T(lambda e, hh=hh, Bs=Bs: e.transpose(out=psT[:, 256 + hh * 128:256 + (hh + 1) * 128],
                                                                  in_=Bs[2][:, hh * 128:(hh + 1) * 128], identity=idb[:]),
                              [rBs[2], ridb], [rpsT])
                        pq3 = psT[:, 0:256].rearrange("p (h t) -> p h t", h=2)
                        A(lambda e, Bs=Bs: e.copy(out=Bs[4], in_=psT[:, 0:256]), [rpsT], [rBs[4]])
                        V(lambda e, Bs=Bs: e.tensor_copy(out=Bs[5], in_=psT[:, 256:512]), [rpsT], [rBs[5]])
                        A(lambda e, pq3=pq3: e.copy(out=qTA[:, :, 0:64], in_=pq3[:, :, 0:64]), [rpsT], [rqTA])
                        V(lambda e, pq3=pq3: e.tensor_copy(out=qTB[:, :, 64:128], in_=pq3[:, :, 64:128]), [rpsT], [rqTB])
                        cur = i % 2
                        nxtb = 1 - cur
                        for hh in range(2):
                            hs = slice(hh * 128, (hh + 1) * 128)
                            T(lambda e, hs=hs, Bs=Bs: e.matmul(psS[1][:, hs], lhsT=Bs[5][:, hs], rhs=Bs[4][:, hs], start=True, stop=True),
                              [rBs[5], rBs[4]], [rpsS[1]])
                        for hh in range(2):
                            hs = slice(hh * 128, (hh + 1) * 128)
                            V(lambda e, hs=hs, Bs=Bs: e.tensor_tensor(out=Bs[6][:, hs], in0=psS[1][:, hs], in1=hgmb[:], op=ALU.mult),
                              [rpsS[1], rhgmb], [rBs[6]])
                        for hh in range(2):
                            hs = slice(hh * 128, (hh + 1) * 128)
                            T(lambda e, hs=hs, hh=hh, Bs=Bs: e.matmul(psO[hh][:, 0:128], lhsT=Bs[6][:, hs], rhs=Bs[0][:, hs],
                                                                      start=True, stop=False), [rBs[6], rBs[0]], [rpsO[hh]])
                            T(lambda e, hh=hh, cur=cur: e.matmul(psO[hh][:, 0:128], lhsT=qTA[:, hh, :], rhs=Sbf[:, cur, hh, :],
                                                                 start=False, stop=False), [rqTA, rSbf[cur][hh]], [rpsO[hh]])
                        for hh in range(2):
                            hs = slice(hh * 128, (hh + 1) * 128)
                            T(lambda e, hs=hs, Bs=Bs: e.matmul(psS[0][:, hs], lhsT=Bs[3][0:64, hs], rhs=Bs[0][0:64, hs],
                                                               start=True, stop=True), [rBs[3], rBs[0]], [rpsS[0], rrow])
                        for hh in range(2):
                            hs = slice(hh * 128, (hh + 1) * 128)
                            V(lambda e, hs=hs, hh=hh, dsl=dsl: e.scalar_tensor_tensor(out=S32[:, hh, :], in0=S32[:, hh, :],
                                                                                      scalar=dsl[:, 2 * hh:2 * hh + 1], in1=psS[0][:, hs],
                                                                                      op0=ALU.mult, op1=ALU.add),
                              [rS32[hh], rds, rpsS[0]], [rS32[hh]])
                            G(lambda e, hh=hh, nxtb=nxtb: e.tensor_copy(out=Sbf[:, nxtb, hh, :], in_=S32[:, hh, :]),
                              [rS32[hh]], [rSbf[nxtb][hh]])
                        for hh in range(2):
                            T(lambda e, hh=hh, nxtb=nxtb: e.matmul(psO[hh][:, 0:128], lhsT=qTB[:, hh, :], rhs=Sbf[:, nxtb, hh, :],
                                                                   start=False, stop=True), [rqTB, rSbf[nxtb][hh]], [rpsO[hh], rrow])
                        for hh in range(2):
                            hs = slice(hh * 128, (hh + 1) * 128)
                            T(lambda e, hs=hs, Bs=Bs: e.matmul(psS[0][:, hs], lhsT=Bs[3][64:128, hs], rhs=Bs[0][64:128, hs],
                                                               start=True, stop=True), [rBs[3], rBs[0]], [rpsS[0], rrow])
                        for hh in range(2):
                            hs = slice(hh * 128, (hh + 1) * 128)
                            V(lambda e, hs=hs, hh=hh, dsl=dsl: e.scalar_tensor_tensor(out=S32[:, hh, :], in0=S32[:, hh, :],
                                                                                      scalar=dsl[:, 2 * hh + 1:2 * hh + 2], in1=psS[0][:, hs],
                                                                                      op0=ALU.mult, op1=ALU.add),
                              [rS32[hh], rds, rpsS[0]], [rS32[hh]])
                        for hh in range(2):
                            G(lambda e, hh=hh, nxtb=nxtb: e.tensor_copy(out=Sbf[:, nxtb, hh, :], in_=S32[:, hh, :]),
                              [rS32[hh]], [rSbf[nxtb][hh]])
                        ssl = sso4[:, 2 * (i % 2):2 * (i % 2) + 2]
                        r_sso = r_sso2[i % 2]
                        for hh in range(2):
                            A(lambda e, hh=hh, Fs=Fs, ssl=ssl: e.activation(out=Fs[7][:, 0:128], in_=psO[hh][:, 0:128], func=AF.Square,
                                                                            accum_out=ssl[:, hh:hh + 1]), [rpsO[hh]], [rFs[7], r_sso])
                        A(lambda e, ssl=ssl: e.activation(out=ssl, in_=ssl, func=AF.Ln, scale=1.0 / 128, bias=EPS), [r_sso], [r_sso])
                        A(lambda e, ssl=ssl: e.activation(out=ssl, in_=ssl, func=AF.Exp, scale=-0.5), [r_sso], [r_sso])
                        for hh in range(2):
                            hs = slice(hh * 128, (hh + 1) * 128)
                            V(lambda e, hh=hh, hs=hs, Fs=Fs, ssl=ssl: e.scalar_tensor_tensor(out=Fs[8][:, hs], in0=psO[hh][:, 0:128],
                                                                                             scalar=ssl[:, hh:hh + 1], in1=go_bc[:],
                                                                                             op0=ALU.mult, op1=ALU.mult),
                              [rpsO[hh], r_sso, rgains], [rFs[8]])
                        G(lambda e, Fs=Fs, sl=sl, sz=sz: e.tensor_tensor(out=y_tok[:, sl, :], in0=Fs[8], in1=sz[:, sl, :], op=ALU.mult),
                          [rFs[8], rsz[sl]], [ry[sl]])
                        outproj_tile(i, sl, last, obanks=[(psS[0], rpsS[0]), (psS[0], rpsS[0])])
            if not glist and last_layer:
                for i in range(NT):
                    out_dmas.append(DMA("sync", lambda e, i=i: e.dma_start(out=out_d[i * 128:(i + 1) * 128, :], in_=x_tok[:, i, :]),
                                        r=[rx[i]]))
        if SCHED:
            if SCHED2:
                P.schedule2(SDELTA)
            else:
                P.schedule()
        P.emit(st, out_dmas)
    build_nc.stats = P.stats
    return nc


_CACHE = {}


def _get_nc(layers, groups):
    key = (tuple(layers), tuple(groups))
    if key not in _CACHE:
        _CACHE[key] = build_nc(layers, groups)
    return _CACHE[key]


def run(inputs, layers=(0, 1), groups=ALL_GROUPS, cores=8):
    nc = _get_nc(layers, groups)
    f = lambda a: np.ascontiguousarray(np.asarray(a))
    cst = make_consts()
    shared = {k: f(inputs[k]).astype(np.float32, copy=False) for k in
              ("norm_g", "w_in", "w_out", "moba_q_norm", "moba_k_norm", "hgrn_lb_logits", "hgrn_o_norm",
               "mem_norm_g", "w_mem_kv", "mem_q_norm", "mem_k_norm")}
    x = f(inputs["x"]); mem = f(inputs["mem"]); pos = f(inputs["positions"]).astype(np.int32, copy=False)
    in_maps = []
    for b in range(cores):
        m = dict(shared)
        m["x"] = x[b]
        m["mem"] = mem[b]
        m["pos"] = pos[b].reshape(16, 128)
        m["cst"] = cst
        in_maps.append(m)
    res = run_bass_kernel_spmd(nc, in_maps, core_ids=list(range(cores)))
    return np.stack([np.asarray(r["out"]) for r in res.results], axis=0)


def kernel(x, mem, positions, norm_g, w_in, w_out, moba_q_norm, moba_k_norm, hgrn_lb_logits,
           hgrn_o_norm, mem_norm_g, w_mem_kv, mem_q_norm, mem_k_norm):
    inputs = dict(x=x, mem=mem, positions=positions, norm_g=norm_g, w_in=w_in, w_out=w_out,
                  moba_q_norm=moba_q_norm, moba_k_norm=moba_k_norm, hgrn_lb_logits=hgrn_lb_logits,
                  hgrn_o_norm=hgrn_o_norm, mem_norm_g=mem_norm_g, w_mem_kv=w_mem_kv,
                  mem_q_norm=mem_q_norm, mem_k_norm=mem_k_norm)
    return run(inputs).astype(np.float32, copy=False)
```

```python
import numpy as np
from contextlib import ExitStack
import concourse.bass as bass
import concourse.mybir as mybir
from concourse.bass_utils import run_bass_kernel_spmd

F32 = mybir.dt.float32
BF16 = mybir.dt.bfloat16
I32 = mybir.dt.int32
AF = mybir.ActivationFunctionType
ALU = mybir.AluOpType
AX = mybir.AxisListType

S = 2048
D = 1024
NT = 16
EPS = 1e-6
NCST = 576
import os as _os0
ALL_GROUPS = tuple(_os0.environ.get("ORDER", "A0,A1,H0,H1,M0,M1").split(","))
import os as _os
SCHED = _os.environ.get("SCHED", "1") == "1"
PAR1 = int(_os.environ.get("PAR1", "1"))
PAR2 = int(_os.environ.get("PAR2", "1"))
GPAR = int(_os.environ.get("GPAR", "1"))
PS3 = int(_os.environ.get("PS3", "3"))
MASKV = int(_os.environ.get("MASKV", "1"))
OPB = int(_os.environ.get("OPB", "1"))
SCHED2 = int(_os.environ.get("SCHED2", "1"))
SDELTA = float(_os.environ.get("SDELTA", "100"))
LATX = float(_os.environ.get("LATX", "180"))
PEK = float(_os.environ.get("PEK", "0.65"))
ACTK = float(_os.environ.get("ACTK", "1.0"))
DVEK = float(_os.environ.get("DVEK", "1.0"))
POOLK = float(_os.environ.get("POOLK", "1.0"))
LATS = float(_os.environ.get("LATS", "60"))
XQ = int(_os.environ.get("XQ", "0"))
GOFF = int(_os.environ.get("GOFF", "0"))
TRANS = int(_os.environ.get("TRANS", "1"))
PRUNE = int(_os.environ.get("PRUNE", "1"))


class Res:
    __slots__ = ("name", "w", "r", "excl")

    def __init__(self, name, excl=False):
        self.name = name
        self.w = None
        self.r = []
        self.excl = excl


class _Rec:
    def __init__(self):
        self.name = None
        self.args = ()
        self.kw = {}

    def __getattr__(self, name):
        def f(*a, **k):
            self.name, self.args, self.kw = name, a, k
            return self
        return f


def _free_size(ap):
    n = 1
    for d in list(ap.shape)[1:]:
        n *= int(d)
    return n


class Op:
    __slots__ = ("eng", "fn", "deps", "sdeps", "sig", "idx", "dma", "sem", "val", "i", "cost", "start")

    def __init__(self, eng, fn, deps, sdeps, dma):
        self.eng = eng
        self.fn = fn
        self.deps = deps
        self.sdeps = sdeps
        self.sig = False
        self.idx = 0
        self.dma = dma
        self.sem = None
        self.val = 0
        self.i = 0
        self.start = 0.0
        rec = _Rec()
        fn(rec)
        out = rec.kw.get("out", rec.args[0] if rec.args else None)
        n = _free_size(out) if out is not None else 64
        if dma:
            c = 2000.0 + n * int(out.shape[0]) * 4 / 120.0
        elif eng == "tensor":
            if rec.name == "transpose":
                c = 110.0
            else:
                lhsT = rec.kw.get("lhsT")
                f32 = lhsT is not None and lhsT.dtype == F32
                c = PEK * (64.0 + max(n, 64) / 2.0) * (4.0 if f32 else 1.0)
        elif eng == "scalar":
            c = ACTK * (200.0 + n / 1.2)
        elif eng == "vector":
            c = DVEK * (120.0 + n / 0.96 * (8.0 if rec.name == "reciprocal" else 1.0))
        else:
            c = POOLK * (300.0 + n / 0.5)
        self.cost = c


class Prog:
    ENGS = ["tensor", "vector", "scalar", "gpsimd", "sync"]

    def __init__(self, nc):
        self.nc = nc
        self.ops = []

    phase = None
    tok = None
    tokset = ()

    def op(self, eng, fn, reads=(), writes=(), dma=False):
        if self.phase == "gate" and self.tok is not None:
            writes = list(writes) + [self.tok]
        elif self.phase == "chain" and eng in self.tokset:
            reads = list(reads) + [self.tok]
        deps, sdeps = {}, {}

        def add(d):
            if d.dma or dma or d.eng != eng or eng != "tensor":
                deps[id(d)] = d
            else:
                sdeps[id(d)] = d
        for r in reads:
            if r.w is not None:
                add(r.w)
            if r.excl:
                for d in r.r:
                    if d.eng != eng:
                        add(d)
        for w in writes:
            if w.w is not None:
                add(w.w)
            for d in w.r:
                add(d)
        o = Op(eng, fn, list(deps.values()), list(sdeps.values()), dma)
        for r in reads:
            r.r.append(o)
        for w in writes:
            w.w = o
            w.r = []
        self.ops.append(o)
        return o

    def schedule(self):
        import heapq
        ops = self.ops
        for i, o in enumerate(ops):
            o.i = i
        succs = [[] for _ in ops]
        npred = [0] * len(ops)
        for o in ops:
            ds = o.deps + o.sdeps
            npred[o.i] = len(ds)
            for d in ds:
                succs[d.i].append(o)
        ready = [0.0] * len(ops)
        free = {e: 0.0 for e in self.ENGS}
        heap = [(0.0, o.i) for o in ops if npred[o.i] == 0]
        heapq.heapify(heap)
        done = 0
        while heap:
            t, i = heapq.heappop(heap)
            o = ops[i]
            st = max(ready[i], free[o.eng])
            if st > t + 1e-9:
                heapq.heappush(heap, (st, i))
                continue
            o.start = st
            if o.dma:
                free[o.eng] = st + 150.0
            else:
                free[o.eng] = st + o.cost
            fin = st + o.cost
            done += 1
            for sc in succs[i]:
                lat = 60.0 if (sc.eng == o.eng and not o.dma) else 180.0
                if fin + lat > ready[sc.i]:
                    ready[sc.i] = fin + lat
                npred[sc.i] -= 1
                if npred[sc.i] == 0:
                    heapq.heappush(heap, (max(ready[sc.i], free[sc.eng]), sc.i))
        assert done == len(ops), (done, len(ops))
        self.ops = sorted(ops, key=lambda o: (o.start, o.i))
        self.est_ns = max(o.start + o.cost for o in ops)

    def schedule2(self, delta=120.0):
        ops = self.ops
        n = len(ops)
        for i, o in enumerate(ops):
            o.i = i
        succs = [[] for _ in ops]
        npred = [0] * n
        for o in ops:
            ds = o.deps + o.sdeps
            npred[o.i] = len(ds)
            for d in ds:
                succs[d.i].append(o)
        blev = [0.0] * n
        for o in reversed(ops):
            b = 0.0
            for sc in succs[o.i]:
                lat = LATS if (sc.eng == o.eng and not o.dma) else LATX
                v = lat + blev[sc.i]
                if v > b:
                    b = v
            blev[o.i] = b + o.cost
        ready = [0.0] * n
        free = {e: 0.0 for e in self.ENGS}
        rsets = {e: [] for e in self.ENGS}
        for o in ops:
            if npred[o.i] == 0:
                rsets[o.eng].append(o.i)
        done = 0
        while done < n:
            best_e, best_t = None, 1e30
            for e in self.ENGS:
                rs = rsets[e]
                if not rs:
                    continue
                t = min(ready[i] for i in rs)
                if t < free[e]:
                    t = free[e]
                if t < best_t:
                    best_t, best_e = t, e
            e = best_e
            rs = rsets[e]
            lim = best_t + delta
            pick, pb = -1, -1.0
            for i in rs:
                if ready[i] <= lim and blev[i] > pb:
                    pb, pick = blev[i], i
            rs.remove(pick)
            o = ops[pick]
            st = max(ready[pick], free[e])
            o.start = st
            free[e] = st + (150.0 if o.dma else o.cost)
            fin = st + o.cost
            done += 1
            for sc in succs[pick]:
                lat = LATS if (sc.eng == o.eng and not o.dma) else LATX
                if fin + lat > ready[sc.i]:
                    ready[sc.i] = fin + lat
                npred[sc.i] -= 1
                if npred[sc.i] == 0:
                    rsets[sc.eng].append(sc.i)
        self.ops = sorted(ops, key=lambda o: (o.start, o.i))
        self.est_ns = max(o.start + o.cost for o in ops)

    def emit(self, stack, final_deps, ndma_sems=8):
        nc = self.nc
        if PRUNE:
            pos = {id(o): k for k, o in enumerate(self.ops)}
            for o in self.ops:
                best = {}
                keep = []
                for d in o.deps:
                    if d.dma:
                        keep.append(d)
                    elif d.eng not in best or pos[id(d)] > pos[id(best[d.eng])]:
                        best[d.eng] = d
                o.deps = keep + list(best.values())
        for o in self.ops:
            for d in o.deps:
                d.sig = True
        for d in final_deps:
            d.sig = True
        sems = {e: stack.enter_context(nc.semaphore("s_" + e)) for e in self.ENGS}
        cnt = {e: 0 for e in self.ENGS}
        pools, pool_i, pre_wait = {}, {}, {}
        for o in self.ops:
            if o.dma:
                if o.eng not in pools:
                    pools[o.eng] = [[stack.enter_context(nc.semaphore("d_%s_%d" % (o.eng, i))), 0]
                                    for i in range(ndma_sems)]
                    pool_i[o.eng] = 0
                p = pools[o.eng][pool_i[o.eng] % ndma_sems]
                pool_i[o.eng] += 1
                if p[1] > 0:
                    pre_wait[id(o)] = (p[0], p[1])
                p[1] += 16
                o.sem = p[0]
                o.val = p[1]
            elif o.sig:
                cnt[o.eng] += 1
                o.idx = cnt[o.eng]
        per = {e: [o for o in self.ops if o.eng == e] for e in self.ENGS}
        self.stats = {e: len(per[e]) for e in self.ENGS}
        known = {e: {} for e in self.ENGS}
        kn = {}
        plan = {}
        nw = 0

        def semkey(d):
            return (d.sem, d.val) if d.dma else (sems[d.eng], d.idx)

        for o in self.ops:
            kd = known[o.eng]
            ws = []
            for d in o.deps:
                sm, val = semkey(d)
                if kd.get(id(sm), (None, 0))[1] < val:
                    ws.append((sm, val))
                    kd[id(sm)] = (sm, val)
                if TRANS:
                    for k2, (s2, v2) in kn[id(d)].items():
                        if kd.get(k2, (None, 0))[1] < v2:
                            kd[k2] = (s2, v2)
            if o.dma:
                pw = pre_wait.get(id(o))
                if pw and kd.get(id(pw[0]), (None, 0))[1] < pw[1]:
                    ws.append(pw)
                    kd[id(pw[0])] = pw
            plan[id(o)] = ws
            nw += len(ws)
            if o.dma or o.sig:
                mine = dict(kd)
                sm, val = semkey(o)
                mine[id(sm)] = (sm, val)
                kn[id(o)] = mine
        fin_w = []
        kd = known["sync"]
        for d in final_deps:
            sm, val = semkey(d)
            if kd.get(id(sm), (None, 0))[1] < val:
                fin_w.append((sm, val))
                kd[id(sm)] = (sm, val)
        self.stats["waits"] = nw
        self.stats["sigs"] = {e: sum(1 for o in per[e] if o.sig and not o.dma) for e in self.ENGS}
        block = stack.enter_context(nc.Block())

        def mk(e):
            def body(engobj):
                for o in per[e]:
                    for sm, val in plan[id(o)]:
                        engobj.wait_ge(sm, val)
                    if o.dma:
                        o.fn(engobj).then_inc(o.sem, 16)
                    else:
                        ins = o.fn(engobj)
                        if o.sig:
                            ins.then_inc(sems[e], 1)
                if e == "sync":
                    for sm, val in fin_w:
                        engobj.wait_ge(sm, val)
            return body

        block.tensor(mk("tensor"))
        block.vector(mk("vector"))
        block.scalar(mk("scalar"))
        block.gpsimd(mk("gpsimd"))
        block.sync(mk("sync"))


def make_consts():
    c = np.zeros((128, NCST), np.float32)
    i = np.arange(128)
    c[:, 0:128] = np.eye(128)
    c[:, 128:256] = (i[None, :] >= i[:, None])
    same = (i[:, None] // 64) == (i[None, :] // 64)
    c[:, 256:384] = same & (i[:, None] <= i[None, :])
    c[:, 384:512] = same & (i[:, None] > i[None, :])
    c[:, 512] = i < 64
    c[:, 513] = i >= 64
    c[:, 514] = 1.0
    f64 = 500000.0 ** (-np.arange(8, dtype=np.float64) / 8.0)
    f = f64.astype(np.float32)
    flo = (f64 - f.astype(np.float64)).astype(np.float32)
    c[:, 515:523] = f[None, :]
    c[:, 523:531] = f[None, :]
    c[:, 547:555] = flo[None, :]
    c[:, 555:563] = flo[None, :]
    c[:, 531:539] = 0.0
    c[:, 539:547] = np.pi / 2
    return c


def build_nc(layers=(0, 1), groups=ALL_GROUPS):
    nc = bass.Bass("TRN2", target_bir_lowering=False)

    def din(name, shape, d=F32):
        return nc.dram_tensor(name, shape, d, kind="ExternalInput").ap()

    x_d = din("x", [S, D])
    mem_d = din("mem", [256, D])
    pos_d = din("pos", [16, 128], I32)
    ng_d = din("norm_g", [2, D])
    win_d = din("w_in", [2, D, 5120])
    wout_d = din("w_out", [2, 1536, D])
    gq_d = din("moba_q_norm", [2, 64])
    gk_d = din("moba_k_norm", [2, 64])
    lbl_d = din("hgrn_lb_logits", [2, 512])
    go_d = din("hgrn_o_norm", [2, 128])
    mng_d = din("mem_norm_g", [2, D])
    wkv_d = din("w_mem_kv", [2, D, 1024])
    gmq_d = din("mem_q_norm", [2, 128])
    gmk_d = din("mem_k_norm", [2, 128])
    cst_d = din("cst", [128, NCST])
    out_d = nc.dram_tensor("out", [S, D], F32, kind="ExternalOutput").ap()

    P = Prog(nc)
    if _os.environ.get("TOKR"):
        P.tok = Res("tok")
        P.tokset = tuple(_os.environ["TOKR"].split(","))
    with ExitStack() as st:
        def sb(name, shape, dt=F32):
            return st.enter_context(nc.sbuf_tensor("sb_" + name, shape, dt))

        def ps(name, shape, dt=F32):
            return st.enter_context(nc.psum_tensor("pp_" + name, shape, dt))

        def T(fn, r=(), w=()):
            return P.op("tensor", fn, r, w)

        def V(fn, r=(), w=()):
            return P.op("vector", fn, r, w)

        def A(fn, r=(), w=()):
            return P.op("scalar", fn, r, w)

        def G(fn, r=(), w=()):
            return P.op("gpsimd", fn, r, w)

        def DMA(q, fn, r=(), w=()):
            return P.op(q, fn, r, w, dma=True)

        x_tok = sb("x_tok", [128, NT, D]); rx = [Res("x%d" % i) for i in range(NT)]
        hT = sb("hT", [128, 8, S], BF16); rhT = [Res("hT%d" % i) for i in range(NT)]
        cst = sb("cst", [128, NCST]); rcst = Res("cst")
        idb = sb("idb", [128, 128], BF16); ridb = Res("idb")
        trib = sb("trib", [128, 128], BF16); rtrib = Res("trib")
        hgmb = sb("hgmb", [128, 128], BF16); rhgmb = Res("hgmb")
        gq_bc = sb("gq_bc", [128, 64]); gk_bc = sb("gk_bc", [128, 64]); go_bc = sb("go_bc", [128, 128])
        gmq_bc = sb("gmq_bc", [128, 128]); gmk_bc = sb("gmk_bc", [128, 128]); rgains = Res("gains")
        cs = sb("cs", [128, NT, 16]); sn = sb("sn", [128, NT, 16]); rrope = Res("rope")
        wb = [sb("wb%d" % i, [128, 8, 512], BF16) for i in range(3)]; rwb = [[Res("wb%da" % i), Res("wb%db" % i)] for i in range(3)]
        wo = sb("wo", [128, 2, D], BF16); rwo = Res("wo")
        Fp = [sb("F%d" % i, [128, 256]) for i in range(9)]; rF = [Res("F%d" % i) for i in range(9)]
        Bp = [sb("B%d" % i, [128, 256], BF16) for i in range(7)]; rB = [Res("B%d" % i) for i in range(7)]
        szs = [sb("sz%d" % k, [128, 4, 256]) for k in range(2)]; rszs = [[Res("sz%d_%d" % (k, i)) for i in range(4)] for k in range(2)]
        y_tok = sb("y_tok", [128, 4, 256], BF16); ry = [Res("y%d" % i) for i in range(4)]
        yT = [sb("yT%d" % i, [128, 256], BF16) for i in range(2)]; ryT = [Res("yT%d" % i) for i in range(2)]
        pT = [sb("pT%d" % i, [128, 512], BF16) for i in range(4)]; rpT = [Res("pT%d" % i) for i in range(4)]
        hbs = [sb("hb%d" % i, [128, D], BF16) for i in range(2)]; rhbs = [Res("hb0"), Res("hb1")]
        small = sb("small", [128, 128]); rsm = {}

        def sm(name, a, n):
            rsm[name] = Res("sm_" + name)
            return small[:, a:a + n], rsm[name]
        ss16, r_ss16 = sm("ss16", 0, 16)
        t16, r_t16 = sm("t16", 16, 16)
        rstd16, r_rstd16 = sm("rstd16", 32, 16)
        ss4, r_ss4 = sm("ss4", 48, 4)
        t4, r_t4 = sm("t4", 52, 4)
        rs4, r_rs4 = sm("rs4", 56, 4)
        rden, r_rden = sm("rden", 60, 4)
        dec4, r_dec4 = sm("dec4", 64, 4)
        sso, r_sso = sm("sso", 68, 2)
        to2, r_to2 = sm("to2", 70, 2)
        rso, r_rso = sm("rso", 72, 2)
        gm = sb("gm", [128, 4, 8]); rgm = Res("gm")
        top8 = sb("top8", [128, 4, 8]); rtop8 = Res("top8")
        selb = sb("selb", [128, 4, 8]); rselb = Res("selb")
        kT = sb("kT", [128, 4, S], BF16); rkT = [Res("kT%d" % i) for i in range(NT)]
        v_flat = sb("v_aug", [128, NT * 4 * 65], BF16); rv = [Res("v%d" % i) for i in range(NT)]
        v_aug = v_flat[:].rearrange("p (a b c) -> p a b c", a=NT, b=4)
        stage = v_flat[:, 0:2048].bitcast(F32); rstage = Res("stage")
        g_bcv = v_flat[:, 2048:4096].bitcast(F32); rg_bc = Res("g_bc")
        qTs = [sb("qT%d" % i, [128, 2048], BF16) for i in range(2)]; rqTs = [Res("qT0"), Res("qT1")]
        lbv = qTs[1][:, 0:2048].bitcast(F32); rlb = rqTs[1]
        lb_g = lbv[:, 0:256]; oml_g = lbv[:, 256:512]
        k_aug = sb("k_aug", [128, 4, 72], BF16); rk_aug = Res("k_aug")
        q_aug = sb("q_aug", [128, 4, 72], BF16); rq_aug = Res("q_aug")
        kmT32 = sb("kmT32", [128, 2, 2, 8]); rkm = Res("kmT32")
        S32 = sb("S32", [128, 2, 128]); rS32 = [Res("S32_0"), Res("S32_1")]
        Sbf = sb("Sbf", [128, 2, 2, 128], BF16); rSbf = [[Res("Sbf00"), Res("Sbf01")], [Res("Sbf10"), Res("Sbf11")]]
        qTA = sb("qTA", [128, 2, 128], BF16); qTB = sb("qTB", [128, 2, 128], BF16)
        rqTA = Res("qTA"); rqTB = Res("qTB")
        memT = sb("memT", [128, 8, 256], BF16); rmemT = Res("memT")
        kmT = sb("kmT", [128, 2, 256], BF16); rkmT = Res("kmT")
        vm_aug = sb("vm_aug", [128, 2, 2, 129], BF16); rvm = Res("vm")
        psA = [ps("psA%d" % i, [128, 512]) for i in range(2)]; rpsA = [Res("psA0", True), Res("psA1", True)]
        psT = ps("psT", [128, 1024], BF16); rpsT = Res("psT", True)
        psG = ps("psG", [128, 512]); rpsG = Res("psG", True)
        psS = [ps("psS%d" % i, [128, 512]) for i in range(2)]; rpsS = [Res("psS0", True), Res("psS1", True)]
        psO = [ps("psO%d" % i, [128, 512]) for i in range(2)]; rpsO = [Res("psO0", True), Res("psO1", True)]

        ctr = {"pa": 0, "w": 0, "ev": 0, "ps": 0, "pt": 0, "yt": 0, "mk": 0}

        def nxt(k, n):
            v = ctr[k] % n
            ctr[k] += 1
            return v

        def evac(fn_v, fn_a, r, w):
            if nxt("ev", 2) == 0:
                return A(fn_a, r, w)
            return V(fn_v, r, w)

        DMA("sync", lambda e: e.dma_start(out=cst[:], in_=cst_d), w=[rcst])
        V(lambda e: e.tensor_copy(out=idb[:], in_=cst[:, 0:128]), [rcst], [ridb])
        V(lambda e: e.tensor_copy(out=trib[:], in_=cst[:, 128:256]), [rcst], [rtrib])
        V(lambda e: e.tensor_copy(out=hgmb[:], in_=cst[:, 256:384]), [rcst], [rhgmb])
        ident32 = cst[:, 0:128]
        G(lambda e: e.memset(vm_aug[:, :, :, 128:129], 1.0), w=[rvm])
        G(lambda e: e.memset(qTA[:], 0.0), w=[rqTA])
        G(lambda e: e.memset(qTB[:], 0.0), w=[rqTB])
        nI = sb("nI", [128, 256], I32); rnI = Res("nI")
        posi = nI[0:16, 0:128]; rposi = rnI
        posf = Fp[8][0:16, 0:128]; rposf = rF[8]
        DMA("sync", lambda e: e.dma_start(out=posi, in_=pos_d), w=[rposi])
        V(lambda e: e.tensor_copy(out=posf, in_=posi), [rposi], [rposf])
        T(lambda e: e.matmul(psG[:, 0:16], lhsT=posf, rhs=cst[0:16, 0:16], start=True, stop=True), [rposf, rcst], [rpsG])
        post, r_post = sm("post", 80, 16)
        V(lambda e: e.tensor_copy(out=post, in_=psG[:, 0:16]), [rpsG], [r_post])
        ang = Fp[0][:, 0:256].rearrange("p (i j) -> p i j", i=NT)
        V(lambda e: e.tensor_tensor(out=ang, in0=post.unsqueeze(2).to_broadcast([128, NT, 16]),
                                    in1=cst[:, 515:531].unsqueeze(1).to_broadcast([128, NT, 16]), op=ALU.mult),
          [r_post, rcst], [rF[0]])
        ang_lo = Fp[1][:, 0:256].rearrange("p (i j) -> p i j", i=NT)
        V(lambda e: e.tensor_tensor(out=ang_lo, in0=post.unsqueeze(2).to_broadcast([128, NT, 16]),
                                    in1=cst[:, 547:563].unsqueeze(1).to_broadcast([128, NT, 16]), op=ALU.mult),
          [r_post, rcst], [rF[1]])
        V(lambda e: e.tensor_tensor(out=ang, in0=ang, in1=ang_lo, op=ALU.add), [rF[0], rF[1]], [rF[0]])
        V(lambda e: e.tensor_tensor(out=ang, in0=ang, in1=cst[:, 531:547].unsqueeze(1).to_broadcast([128, NT, 16]),
                                    op=ALU.add), [rF[0], rcst], [rF[0]])
        V(lambda e: e.tensor_scalar(out=Fp[1][:], in0=Fp[0][:], scalar1=float(1.0 / (2 * np.pi)), scalar2=None,
                                    op0=ALU.mult), [rF[0]], [rF[1]])
        V(lambda e: e.tensor_copy(out=nI[:], in_=Fp[1][:]), [rF[1]], [rnI])
        V(lambda e: e.tensor_copy(out=Fp[1][:], in_=nI[:]), [rnI], [rF[1]])
        C1 = 6.28125
        C2 = float(2 * np.pi - 6.28125)
        V(lambda e: e.scalar_tensor_tensor(out=Fp[2][:], in0=Fp[1][:], scalar=-C1, in1=Fp[0][:],
                                           op0=ALU.mult, op1=ALU.add), [rF[1], rF[0]], [rF[2]])
        V(lambda e: e.scalar_tensor_tensor(out=Fp[2][:], in0=Fp[1][:], scalar=-C2, in1=Fp[2][:],
                                           op0=ALU.mult, op1=ALU.add), [rF[1], rF[2]], [rF[2]])
        V(lambda e: e.tensor_scalar(out=Fp[2][:], in0=Fp[2][:], scalar1=float(np.pi), scalar2=float(-np.pi),
                                    op0=ALU.min, op1=ALU.max), [rF[2]], [rF[2]])
        A(lambda e: e.activation(out=Fp[3][:], in_=Fp[2][:], func=AF.Sin), [rF[2]], [rF[3]])
        sc = Fp[3][:, 0:256].rearrange("p (i j) -> p i j", i=NT)
        V(lambda e: e.tensor_copy(out=cs[:, :, 0:8], in_=sc[:, :, 8:16]), [rF[3]], [rrope])
        V(lambda e: e.tensor_copy(out=cs[:, :, 8:16], in_=sc[:, :, 8:16]), [rF[3]], [rrope])
        V(lambda e: e.tensor_scalar(out=sn[:, :, 0:8], in0=sc[:, :, 0:8], scalar1=-1.0, scalar2=None, op0=ALU.mult),
          [rF[3]], [rrope])
        V(lambda e: e.tensor_copy(out=sn[:, :, 8:16], in_=sc[:, :, 0:8]), [rF[3]], [rrope])
        def load_w(dst, rdst, src_ap):
            return DMA("gpsimd", lambda e: e.dma_start(out=dst, in_=src_ap), w=[rdst])

        def w_in_cols(l, c0, n):
            return win_d[l].rearrange("(c p) n -> p c n", p=128)[:, :, c0:c0 + n]

        psGb = psG[:].bitcast(BF16)

        def rms_to_T(src_tile, rsrc, gain, rgain, dstT, rdst, col0, ssc, r_ssc, tsc, r_tsc, rsc, r_rsc, k):
            hb = hbs[k % 2]; rhb = rhbs[k % 2]
            pst, rpst = (psT, rpsT) if k % 2 == 0 else (psGb, rpsG)
            A(lambda e: e.activation(out=hb[:], in_=src_tile, func=AF.Square, accum_out=ssc[:, k:k + 1]),
              [rsrc], [rhb, r_ssc])
            A(lambda e: e.activation(out=tsc[:, k:k + 1], in_=ssc[:, k:k + 1], func=AF.Ln, scale=1.0 / D, bias=EPS),
              [r_ssc], [r_tsc])
            A(lambda e: e.activation(out=rsc[:, k:k + 1], in_=tsc[:, k:k + 1], func=AF.Exp, scale=-0.5),
              [r_tsc], [r_rsc])
            V(lambda e: e.scalar_tensor_tensor(out=hb[:], in0=src_tile, scalar=rsc[:, k:k + 1], in1=gain,
                                               op0=ALU.mult, op1=ALU.mult), [rsrc, r_rsc, rgain], [rhb])
            for c in range(8):
                T(lambda e, c=c: e.transpose(out=pst[:, c * 128:(c + 1) * 128], in_=hb[:, c * 128:(c + 1) * 128],
                                             identity=idb[:]), [rhb, ridb], [rpst])
            src3 = pst[:, 0:1024].rearrange("p (c t) -> p c t", c=8)
            evac(lambda e: e.tensor_copy(out=dstT[:, :, col0:col0 + 128], in_=src3),
                 lambda e: e.copy(out=dstT[:, :, col0:col0 + 128], in_=src3), [rpst], [rdst])

        def proj(lhs_cols, rlhs, wt, rwt, ncols=512, lhsT_src=None):
            b = nxt("pa", 2)
            src = hT if lhsT_src is None else lhsT_src
            for c in range(8):
                T(lambda e, c=c: e.matmul(psA[b][:, 0:ncols], lhsT=src[:, c, lhs_cols:lhs_cols + 128],
                                          rhs=wt[:, c, 0:ncols], start=(c == 0), stop=(c == 7)),
                  [rlhs] + list(rwt), [rpsA[b]])
            return psA[b], rpsA[b]

        def headnorm(src_ps, rps, H, Dh, gain, outF, routF, tmpA, rtmpA, tmpB, rtmpB):
            n = H * Dh
            A(lambda e: e.activation(out=tmpA[:, 0:n], in_=src_ps, func=AF.Square), [rps], [rtmpA])
            V(lambda e: e.tensor_reduce(out=ss4[:, 0:H], in_=tmpA[:, 0:n].rearrange("p (h d) -> p h d", h=H),
                                        axis=AX.X, op=ALU.add), [rtmpA], [r_ss4])
            A(lambda e: e.activation(out=t4[:, 0:H], in_=ss4[:, 0:H], func=AF.Ln, scale=1.0 / Dh, bias=EPS),
              [r_ss4], [r_t4])
            A(lambda e: e.activation(out=rs4[:, 0:H], in_=t4[:, 0:H], func=AF.Exp, scale=-0.5), [r_t4], [r_rs4])
            V(lambda e: e.tensor_tensor(out=tmpB[:, 0:n].rearrange("p (h d) -> p h d", h=H),
                                        in0=src_ps.rearrange("p (h d) -> p h d", h=H),
                                        in1=rs4[:, 0:H].unsqueeze(2).to_broadcast([128, H, Dh]), op=ALU.mult),
              [rps, r_rs4], [rtmpB])
            (G if GOFF else V)(lambda e: e.tensor_tensor(out=outF[:, 0:n].rearrange("p (h d) -> p h d", h=H),
                                                         in0=tmpB[:, 0:n].rearrange("p (h d) -> p h d", h=H),
                                                         in1=gain.unsqueeze(1).to_broadcast([128, H, Dh]), op=ALU.mult),
                               [rtmpB, rgains], [routF])

        def rope(Fx, rFx, i, tR, rtR):
            x3 = Fx[:, 0:256].rearrange("p (h d) -> p h d", h=4)
            a3 = tR[:, 0:64].rearrange("p (h d) -> p h d", h=4)
            b3 = tR[:, 64:128].rearrange("p (h d) -> p h d", h=4)
            rtA = rtR
            rtB = rtR
            G(lambda e: e.tensor_tensor(out=a3, in0=x3[:, :, 0:16], in1=cs[:, i, :].unsqueeze(1).to_broadcast([128, 4, 16]),
                                        op=ALU.mult), [rFx, rrope], [rtA])
            G(lambda e: e.tensor_tensor(out=b3[:, :, 0:8], in0=x3[:, :, 8:16],
                                        in1=sn[:, i, 0:8].unsqueeze(1).to_broadcast([128, 4, 8]), op=ALU.mult),
              [rFx, rrope], [rtB])
            G(lambda e: e.tensor_tensor(out=b3[:, :, 8:16], in0=x3[:, :, 0:8],
                                        in1=sn[:, i, 8:16].unsqueeze(1).to_broadcast([128, 4, 8]), op=ALU.mult),
              [rFx, rrope], [rtB])
            G(lambda e: e.tensor_tensor(out=x3[:, :, 0:16], in0=a3, in1=b3, op=ALU.add), [rtA, rtB], [rFx])

        def silu_ps(dst, rdst, src_ps, rps):
            A(lambda e: e.activation(out=dst, in_=src_ps, func=AF.Exp, scale=-1.0), [rps], [rdst])
            A(lambda e: e.activation(out=dst, in_=dst, func=AF.Ln, bias=1.0), [rdst], [rdst])
            A(lambda e: e.activation(out=dst, in_=dst, func=AF.Exp, scale=-1.0), [rdst], [rdst])
            V(lambda e: e.tensor_tensor(out=dst, in0=src_ps, in1=dst, op=ALU.mult), [rps, rdst], [rdst])

        SC = [((Fp[0], rF[0]), (Fp[1], rF[1]), (Fp[2], rF[2]), (Fp[3], rF[3])),
              ((Fp[4], rF[4]), (Fp[6], rF[6]), (Fp[7], rF[7]), (Fp[8], rF[8]))]
        AUG = [(k_aug, rk_aug), (q_aug, rq_aug)]
        if _os.environ.get("NOPAR", "0") == "1":
            SC[1] = SC[0]
        if _os.environ.get("NOAUG", "0") == "1":
            AUG[1] = AUG[0]

        dec8, _r = sm("dec8", 96, 8)
        r_dec8 = [Res("dec8a"), Res("dec8b")]
        sso4, _r2 = sm("sso4", 104, 4)
        r_sso2 = [Res("ssoA"), Res("ssoB")]
        dummy = sb("dummy", [128, 8])
        kflat = kT[:].rearrange("p h t -> p (h t)")
        kf32 = kflat[:, 0:4608].bitcast(F32)
        Fq = [kf32[:, k * 256:(k + 1) * 256] for k in range(9)]
        rFq = [Res("Fq%d" % k) for k in range(9)]
        Bq = [kflat[:, 4608 + k * 256:4608 + (k + 1) * 256] for k in range(7)]
        rBq = [Res("Bq%d" % k) for k in range(7)]
        HSETS = [([t[:] for t in Fp], rF, [t[:] for t in Bp], rB), (Fq, rFq, Bq, rBq)]

        def barrier():
            G(lambda e: e.memset(dummy[:], 0.0), w=list(rkT) + rFq + rBq + list(rv) + [rstage, rg_bc])

        rrow = Res("pe_rowfence")

        out_dmas = []

        def outproj_tile(i, r, last, obanks=None):
            yb = nxt("yt", 2)
            for pp in range(2):
                T(lambda e, pp=pp: e.transpose(out=psT[:, pp * 128:(pp + 1) * 128], in_=y_tok[:, r, pp * 128:(pp + 1) * 128],
                                               identity=idb[:]), [ry[r], ridb], [rpsT])
            evac(lambda e: e.tensor_copy(out=yT[yb][:], in_=psT[:, 0:256]),
                 lambda e: e.copy(out=yT[yb][:], in_=psT[:, 0:256]), [rpsT], [ryT[yb]])
            for half in range(2):
                if obanks is None:
                    b = nxt("pa", 2)
                    pso, rpso = psA[b], rpsA[b]
                else:
                    pso, rpso = obanks[half]
                for pp in range(2):
                    T(lambda e, pp=pp, half=half, pso=pso: e.matmul(pso[:, 0:512], lhsT=yT[yb][:, pp * 128:(pp + 1) * 128],
                                                                    rhs=wo[:, pp, half * 512:(half + 1) * 512],
                                                                    start=(pp == 0), stop=(pp == 1)),
                      [ryT[yb], rwo], [rpso])
                V(lambda e, half=half, pso=pso: e.tensor_tensor(out=x_tok[:, i, half * 512:(half + 1) * 512], in0=pso[:, 0:512],
                                                                in1=x_tok[:, i, half * 512:(half + 1) * 512], op=ALU.add),
                  [rpso, rx[i]], [rx[i]])
            if last:
                out_dmas.append(DMA("sync", lambda e: e.dma_start(out=out_d[i * 128:(i + 1) * 128, :], in_=x_tok[:, i, :]),
                                    r=[rx[i]]))

        def attn_chunk(Q, H, Dh, KP, scale, key_tiles, causal, kTsrc, rkTsrc, vsrc, rvsrc, qview, rqT, sz, rsz):
            DA = Dh + 1
            for h in range(H):
                if Dh == 64:
                    o_b = h % 2
                    banks = [o_b, o_b, o_b, o_b]
                    offs = [0, DA, 2 * DA, 3 * DA]
                else:
                    banks = [0, 0, 1, 1]
                    offs = [0, DA, 0, DA]
                started = set()
                kts = key_tiles(Q)
                for kt in kts:
                    j = kt - 4 * Q if causal else -1
                    q0 = max(j, 0) * 128
                    N = 512 - q0
                    sbk = nxt("ps", PS3)
                    pss, rpss = ((psS[0], rpsS[0]), (psS[1], rpsS[1]), (psG, rpsG))[sbk]
                    T(lambda e, kt=kt, h=h, q0=q0, N=N, pss=pss: e.matmul(
                        pss[:, 0:N], lhsT=kTsrc(h, kt), rhs=qview(h)[:, q0:512], start=True, stop=True),
                      [rkTsrc(kt), rqT], [rpss])
                    pb = nxt("pt", 4)
                    A(lambda e, N=N, pss=pss, pb=pb: e.activation(out=pT[pb][:, 0:N], in_=pss[:, 0:N], func=AF.Exp,
                                                                  scale=scale), [rpss], [rpT[pb]])
                    if j >= 0:
                        (V if (MASKV and nxt("mk", 2) == 0) else G)(
                            lambda e, pb=pb: e.tensor_tensor(out=pT[pb][:, 0:128], in0=pT[pb][:, 0:128], in1=trib[:],
                                                             op=ALU.mult), [rpT[pb], rtrib], [rpT[pb]])
                    for r in range(max(j, 0), 4):
                        bk = banks[r]
                        first = bk not in started
                        started.add(bk)
                        T(lambda e, r=r, kt=kt, h=h, q0=q0, pb=pb, bk=bk, first=first: e.matmul(
                            psO[bk][:, offs[r]:offs[r] + DA], lhsT=pT[pb][:, r * 128 - q0:r * 128 - q0 + 128],
                            rhs=vsrc(h, kt), start=first, stop=False, skip_group_check=True),
                          [rpT[pb], rvsrc(kt)], [rpsO[bk]])
                for bk0 in sorted(set(banks)):
                    rs_ = [r for r in range(4) if banks[r] == bk0]
                    nr = len(rs_)
                    V(lambda e, bk0=bk0, rs_=rs_, nr=nr: e.reciprocal(
                        out=rden[:, rs_[0]:rs_[0] + nr],
                        in_=psO[bk0][:, 0:nr * DA].rearrange("p (r c) -> p r c", r=nr)[:, :, Dh:DA]),
                      [rpsO[bk0]], [r_rden])
                for r in range(4):
                    bk = banks[r]
                    V(lambda e, r=r, bk=bk, h=h: e.scalar_tensor_tensor(
                        out=y_tok[:, r, h * Dh:(h + 1) * Dh], in0=psO[bk][:, offs[r]:offs[r] + Dh], scalar=rden[:, r:r + 1],
                        in1=sz[:, r, h * Dh:(h + 1) * Dh], op0=ALU.mult, op1=ALU.mult),
                      [rpsO[bk], r_rden, rsz[r]], [ry[r]])

        for li, l in enumerate(layers):
            last_layer = (li == len(layers) - 1)
            DMA("sync", lambda e, l=l: e.dma_start(out=g_bcv, in_=ng_d[l:l + 1, :].partition_broadcast(128)), w=[rg_bc])
            for dst, src in ((gq_bc, gq_d), (gk_bc, gk_d), (go_bc, go_d), (gmq_bc, gmq_d), (gmk_bc, gmk_d)):
                DMA("sync", lambda e, l=l, dst=dst, src=src: e.dma_start(out=dst[:], in_=src[l:l + 1, :].partition_broadcast(128)),
                    w=[rgains])
            for i in range(NT):
                if li == 0:
                    DMA(("scalar" if (XQ and i % 2 == 1) else "sync"), lambda e, i=i: e.dma_start(out=x_tok[:, i, :], in_=x_d[i * 128:(i + 1) * 128, :]), w=[rx[i]])
                rms_to_T(x_tok[:, i, :], rx[i], g_bcv, rg_bc, hT, rhT[i], i * 128, ss16, r_ss16, t16, r_t16,
                         rstd16, r_rstd16, i)
            glist = [g for g in ALL_GROUPS if g in groups]
            for gi, gname in enumerate(glist):
                last = last_layer and gi == len(glist) - 1
                kind = gname[0]
                g = int(gname[1])
                if kind == "A":
                    w1 = nxt("w", 3); w2 = nxt("w", 3)
                    load_w(wb[w1][:, :, 0:256], rwb[w1][0], w_in_cols(l, 512 + 256 * g, 256))
                    load_w(wb[w1][:, :, 256:512], rwb[w1][1], w_in_cols(l, 1024 + 256 * g, 256))
                    load_w(wb[w2][:, :, 0:256], rwb[w2][0], w_in_cols(l, 256 * g, 256))
                    load_w(wb[w2][:, :, 256:512], rwb[w2][1], w_in_cols(l, 3584 + 256 * g, 256))
                    load_w(wo[:], rwo, wout_d[l, 256 * g:256 * g + 256, :].rearrange("(c p) n -> p c n", p=128))
                    barrier()
                    G(lambda e: e.memset(v_aug[:, :, :, 64:65], 1.0), w=rv)
                    V(lambda e: e.memset(psG[:, 0:16], 0.0), w=[rpsG])
                    for i in range(NT):
                        n_blk = i // 2
                        (tA, rtA), (tB, rtB), (FO, rFO), (tR, rtR) = SC[(i % 2) * PAR1]
                        ka, rka = AUG[(i % 2) * PAR1]
                        pa, rpa = proj(i * 128, rhT[i], wb[w1], rwb[w1])
                        headnorm(pa[:, 0:256], rpa, 4, 64, gk_bc[:], FO, rFO, tA, rtA, tB, rtB)
                        evac(lambda e, i=i, pa=pa: e.tensor_copy(out=v_aug[:, i, :, 0:64],
                                                                 in_=pa[:, 256:512].rearrange("p (h d) -> p h d", h=4)),
                             lambda e, i=i, pa=pa: e.copy(out=v_aug[:, i, :, 0:64],
                                                          in_=pa[:, 256:512].rearrange("p (h d) -> p h d", h=4)),
                             [rpa], [rv[i]])
                        rope(FO, rFO, i, tR, rtR)
                        G(lambda e, ka=ka, FO=FO: e.tensor_copy(out=ka[:, :, 0:64], in_=FO[:].rearrange("p (h d) -> p h d", h=4)),
                          [rFO], [rka])
                        G(lambda e, ka=ka: e.memset(ka[:, :, 64:72], 0.0), w=[rka])
                        G(lambda e, ka=ka, n_blk=n_blk: e.memset(ka[:, :, 64 + n_blk:65 + n_blk], 1.0), w=[rka])
                        for pp in range(2):
                            T(lambda e, pp=pp, n_blk=n_blk, FO=FO: e.matmul(psG[:, pp * 8 + n_blk:pp * 8 + n_blk + 1],
                                                                            lhsT=FO[:, pp * 128:(pp + 1) * 128], rhs=cst[:, 514:515],
                                                                            start=False, stop=False, skip_group_check=True),
                              [rFO, rcst], [rpsG])
                        for h in range(4):
                            T(lambda e, h=h, ka=ka: e.transpose(out=psT[0:72, h * 128:(h + 1) * 128], in_=ka[:, h, :], identity=idb[:]),
                              [rka, ridb], [rpsT])
                        src3 = psT[0:72, 0:512].rearrange("p (h t) -> p h t", h=4)
                        evac(lambda e, i=i, src3=src3: e.tensor_copy(out=kT[0:72, :, i * 128:(i + 1) * 128], in_=src3),
                             lambda e, i=i, src3=src3: e.copy(out=kT[0:72, :, i * 128:(i + 1) * 128], in_=src3),
                             [rpsT], [rkT[i]])
                    G(lambda e: e.memset(kmT32[:], 0.0), w=[rkm])
                    A(lambda e: e.copy(out=kmT32[0:64, :, 0, :], in_=psG[0:64, 0:16].rearrange("p (a n) -> p a n", a=2)), [rpsG], [rkm])
                    A(lambda e: e.copy(out=kmT32[64:128, :, 1, :], in_=psG[64:128, 0:16].rearrange("p (a n) -> p a n", a=2)), [rpsG], [rkm])
                    for Q in range(4):
                        qT3 = qTs[Q % 2][:, 0:2048].rearrange("p (h t) -> p h t", h=4)
                        rqT = rqTs[Q % 2]
                        sz = szs[Q % 2]; rsz = rszs[Q % 2]
                        for r in range(4):
                            i = 4 * Q + r
                            own = i // 2
                            P.phase = "chain"
                            par = (i % 2) * PAR2 * (1 if (own < 4 or GPAR) else 0)
                            (tA, rtA), (tB, rtB), (FO, rFO), (tR, rtR) = SC[par]
                            qa, rqa = AUG[par]
                            pa, rpa = proj(i * 128, rhT[i], wb[w2], rwb[w2])
                            silu_ps(sz[:, r, :], rsz[r], pa[:, 256:512], rpa)
                            headnorm(pa[:, 0:256], rpa, 4, 64, gq_bc[:], FO, rFO, tA, rtA, tB, rtB)
                            rope(FO, rFO, i, tR, rtR)
                            G(lambda e, qa=qa, FO=FO: e.tensor_copy(out=qa[:, :, 0:64], in_=FO[:].rearrange("p (h d) -> p h d", h=4)),
                              [rFO], [rqa])
                            if own >= 4:
                                P.phase = "gate"
                                for pp in range(2):
                                    T(lambda e, pp=pp, FO=FO: e.matmul(psG[:, pp * 128:(pp + 1) * 128],
                                                                       lhsT=FO[:, pp * 128:(pp + 1) * 128], rhs=ident32,
                                                                       start=True, stop=True),
                                      [rFO, rcst], [rpsG])
                                A(lambda e: e.copy(out=Fp[5][:], in_=psG[:, 0:256]), [rpsG], [rF[5]])
                                for h in range(4):
                                    T(lambda e, h=h, own=own: e.matmul(
                                        psG[:, 256 + h * 8:256 + h * 8 + own],
                                        lhsT=Fp[5][:, (h // 2) * 128:(h // 2) * 128 + 128],
                                        rhs=kmT32[:, h // 2, h % 2, 0:own], start=True, stop=True),
                                      [rF[5], rkm], [rpsG])
                                V(lambda e: e.memset(gm[:], -1.0e30), w=[rgm])
                                V(lambda e, own=own: e.tensor_copy(
                                    out=gm[:, :, 0:own], in_=psG[:, 256:288].rearrange("p (h n) -> p h n", h=4)[:, :, 0:own]),
                                  [rpsG], [rgm])
                                for h in range(4):
                                    V(lambda e, h=h: e.max(out=top8[:, h, :], in_=gm[:, h, :]), [rgm], [rtop8])
                                V(lambda e: e.tensor_tensor(out=selb[:], in0=gm[:], in1=top8[:, :, 2:3].to_broadcast([128, 4, 8]),
                                                            op=ALU.is_ge), [rgm, rtop8], [rselb])
                                V(lambda e, qa=qa: e.tensor_scalar(out=qa[:, :, 64:72], in0=selb[:], scalar1=30000.0,
                                                                   scalar2=-30000.0, op0=ALU.mult, op1=ALU.add), [rselb], [rqa])
                                V(lambda e, qa=qa, own=own: e.memset(qa[:, :, 64 + own:65 + own], 0.0), w=[rqa])
                            else:
                                G(lambda e, qa=qa: e.memset(qa[:, :, 64:72], 0.0), w=[rqa])
                            P.phase = None
                            for h in range(4):
                                T(lambda e, h=h, qa=qa: e.transpose(out=psT[0:72, h * 128:(h + 1) * 128], in_=qa[:, h, :],
                                                                    identity=idb[:]), [rqa, ridb], [rpsT])
                            src3 = psT[0:72, 0:512].rearrange("p (h t) -> p h t", h=4)
                            evac(lambda e, r=r, src3=src3, qT3=qT3: e.tensor_copy(out=qT3[0:72, :, r * 128:(r + 1) * 128], in_=src3),
                                 lambda e, r=r, src3=src3, qT3=qT3: e.copy(out=qT3[0:72, :, r * 128:(r + 1) * 128], in_=src3),
                                 [rpsT], [rqT])
                        attn_chunk(Q, 4, 64, 72, 0.125, lambda Q: list(range(4 * Q + 4)), True,
                                   lambda h, kt: kT[0:72, h, kt * 128:(kt + 1) * 128], lambda kt: rkT[kt],
                                   lambda h, kt: v_aug[:, kt, h, :], lambda kt: rv[kt],
                                   lambda h, qT3=qT3: qT3[0:72, h, :], rqT, sz, rsz)
                        for r in range(4):
                            outproj_tile(4 * Q + r, r, last, obanks=([(psO[0], rpsO[0]), (psO[1], rpsO[1])] if OPB else None))
                elif kind == "M":
                    w1 = nxt("w", 3); w2 = nxt("w", 3)
                    wkvv = wkv_d[l].rearrange("(c p) n -> p c n", p=128)
                    load_w(wb[w1][:, :, 0:256], rwb[w1][0], wkvv[:, :, 256 * g:256 * g + 256])
                    load_w(wb[w1][:, :, 256:512], rwb[w1][1], wkvv[:, :, 512 + 256 * g:512 + 256 * g + 256])
                    load_w(wb[w2][:, :, 0:256], rwb[w2][0], w_in_cols(l, 3072 + 256 * g, 256))
                    load_w(wb[w2][:, :, 256:512], rwb[w2][1], w_in_cols(l, 4608 + 256 * g, 256))
                    load_w(wo[:], rwo, wout_d[l, 1024 + 256 * g:1024 + 256 * g + 256, :].rearrange("(c p) n -> p c n", p=128))
                    if g == 0 or ("M0" not in groups):
                        barrier()
                        DMA("sync", lambda e, l=l: e.dma_start(out=g_bcv, in_=mng_d[l:l + 1, :].partition_broadcast(128)),
                            w=[rg_bc])
                        for mt in range(2):
                            DMA("sync", lambda e, mt=mt: e.dma_start(out=stage, in_=mem_d[mt * 128:(mt + 1) * 128, :]),
                                w=[rstage])
                            rms_to_T(stage, rstage, g_bcv, rg_bc, memT, rmemT, mt * 128, ss16, r_ss16, t16, r_t16,
                                     rstd16, r_rstd16, mt)
                    for mt in range(2):
                        (tA, rtA), (tB, rtB), (FO, rFO), (tR, rtR) = SC[mt % 2]
                        pa, rpa = proj(mt * 128, rmemT, wb[w1], rwb[w1], lhsT_src=memT)
                        headnorm(pa[:, 0:256], rpa, 2, 128, gmk_bc[:], FO, rFO, tA, rtA, tB, rtB)
                        evac(lambda e, mt=mt, pa=pa: e.tensor_copy(out=vm_aug[:, mt, :, 0:128],
                                                                   in_=pa[:, 256:512].rearrange("p (h d) -> p h d", h=2)),
                             lambda e, mt=mt, pa=pa: e.copy(out=vm_aug[:, mt, :, 0:128],
                                                            in_=pa[:, 256:512].rearrange("p (h d) -> p h d", h=2)),
                             [rpa], [rvm])
                        G(lambda e, mt=mt, FO=FO: e.tensor_copy(out=Bp[mt % 2][:], in_=FO[:]), [rFO], [rB[mt % 2]])
                        for hh in range(2):
                            T(lambda e, hh=hh, mt=mt: e.transpose(out=psT[:, hh * 128:(hh + 1) * 128],
                                                                  in_=Bp[mt % 2][:, hh * 128:(hh + 1) * 128],
                                                                  identity=idb[:]), [rB[mt % 2], ridb], [rpsT])
                        src3 = psT[:, 0:256].rearrange("p (h t) -> p h t", h=2)
                        evac(lambda e, mt=mt, src3=src3: e.tensor_copy(out=kmT[:, :, mt * 128:(mt + 1) * 128], in_=src3),
                             lambda e, mt=mt, src3=src3: e.copy(out=kmT[:, :, mt * 128:(mt + 1) * 128], in_=src3),
                             [rpsT], [rkmT])
                    for Q in range(4):
                        qm3 = qTs[Q % 2][:, 0:1024].rearrange("p (h t) -> p h t", h=2)
                        rqT = rqTs[Q % 2]
                        sz = szs[Q % 2]; rsz = rszs[Q % 2]
                        for r in range(4):
                            i = 4 * Q + r
                            (tA, rtA), (tB, rtB), (FO, rFO), (tR, rtR) = SC[i % 2]
                            pa, rpa = proj(i * 128, rhT[i], wb[w2], rwb[w2])
                            silu_ps(sz[:, r, :], rsz[r], pa[:, 256:512], rpa)
                            headnorm(pa[:, 0:256], rpa, 2, 128, gmq_bc[:], FO, rFO, tA, rtA, tB, rtB)
                            G(lambda e, i=i, FO=FO: e.tensor_copy(out=Bp[i % 2][:], in_=FO[:]), [rFO], [rB[i % 2]])
                            for hh in range(2):
                                T(lambda e, hh=hh, i=i: e.transpose(out=psT[:, hh * 128:(hh + 1) * 128],
                                                                    in_=Bp[i % 2][:, hh * 128:(hh + 1) * 128], identity=idb[:]),
                                  [rB[i % 2], ridb], [rpsT])
                            src3 = psT[:, 0:256].rearrange("p (h t) -> p h t", h=2)
                            evac(lambda e, r=r, src3=src3, qm3=qm3: e.tensor_copy(out=qm3[:, :, r * 128:(r + 1) * 128], in_=src3),
                                 lambda e, r=r, src3=src3, qm3=qm3: e.copy(out=qm3[:, :, r * 128:(r + 1) * 128], in_=src3),
                                 [rpsT], [rqT])
                        attn_chunk(Q, 2, 128, 128, float(128 ** -0.5), lambda Q: [0, 1], False,
                                   lambda h, kt: kmT[:, h, kt * 128:(kt + 1) * 128], lambda kt: rkmT,
                                   lambda h, kt: vm_aug[:, kt, h, :], lambda kt: rvm,
                                   lambda h, qm3=qm3: qm3[:, h, :], rqT, sz, rsz)
                        for r in range(4):
                            outproj_tile(4 * Q + r, r, last, obanks=([(psO[0], rpsO[0]), (psO[1], rpsO[1])] if OPB else None))
                else:
                    w1 = nxt("w", 3); w2 = nxt("w", 3)
                    load_w(wb[w1][:, :, 0:256], rwb[w1][0], w_in_cols(l, 1536 + 256 * g, 256))
                    load_w(wb[w1][:, :, 256:512], rwb[w1][1], w_in_cols(l, 2048 + 256 * g, 256))
                    load_w(wb[w2][:, :, 0:256], rwb[w2][0], w_in_cols(l, 2560 + 256 * g, 256))
                    load_w(wb[w2][:, :, 256:512], rwb[w2][1], w_in_cols(l, 4096 + 256 * g, 256))
                    load_w(wo[:], rwo, wout_d[l, 512 + 256 * g:512 + 256 * g + 256, :].rearrange("(c p) n -> p c n", p=128))
                    barrier()
                    if l != 0:
                        DMA("sync", lambda e, g=g: e.dma_start(out=lb_g, in_=lbl_d[1:2, 256 * g:256 * g + 256].partition_broadcast(128)), w=[rlb])
                        DMA("sync", lambda e, g=g: e.dma_start(out=oml_g, in_=lbl_d[0:1, 256 * g:256 * g + 256].partition_broadcast(128)), w=[rlb])
                        V(lambda e: e.tensor_tensor(out=lb_g, in0=lb_g, in1=oml_g, op=ALU.subtract), [rlb], [rlb])
                        A(lambda e: e.activation(out=lb_g, in_=lb_g, func=AF.Exp, scale=-1.0), [rlb], [rlb])
                        A(lambda e: e.activation(out=lb_g, in_=lb_g, func=AF.Ln, bias=1.0), [rlb], [rlb])
                        A(lambda e: e.activation(out=lb_g, in_=lb_g, func=AF.Exp, scale=-1.0), [rlb], [rlb])
                        V(lambda e: e.tensor_scalar(out=oml_g, in0=lb_g, scalar1=-1.0, scalar2=1.0, op0=ALU.mult, op1=ALU.add),
                          [rlb], [rlb])
                    for hh in range(2):
                        G(lambda e, hh=hh: e.memset(S32[:, hh, :], 0.0), w=[rS32[hh]])
                        G(lambda e, hh=hh: e.memset(Sbf[:, 0, hh, :], 0.0), w=[rSbf[0][hh]])
                    Tri32 = cst[:, 256:384]
                    TriE32 = cst[:, 384:512]
                    for i in range(NT):
                        Fs, rFs, Bs, rBs = HSETS[i % 2]
                        sl = i % 4
                        sz = szs[(i // 4) % 2]; rsz = rszs[(i // 4) % 2]
                        pq, rpq = proj(i * 128, rhT[i], wb[w1], rwb[w1])
                        silu_ps(Fs[0], rFs[0], pq[:, 0:256], rpq)
                        A(lambda e, pq=pq, Fs=Fs: e.activation(out=Fs[1], in_=pq[:, 256:512], func=AF.Exp, scale=-1.0), [rpq], [rFs[1]])
                        A(lambda e, Fs=Fs: e.activation(out=Fs[1], in_=Fs[1], func=AF.Ln, bias=1.0), [rFs[1]], [rFs[1]])
                        pi_, rpi = proj(i * 128, rhT[i], wb[w2], rwb[w2])
                        silu_ps(sz[:, sl, :], rsz[sl], pi_[:, 256:512], rpi)
                        V(lambda e, pi_=pi_, Bs=Bs: e.tensor_copy(out=Bs[0], in_=pi_[:, 0:256]), [rpi], [rBs[0]])
                        if l == 0:
                            A(lambda e, Fs=Fs: e.activation(out=Fs[2], in_=Fs[1], func=AF.Copy, scale=-1.0), [rFs[1]], [rFs[2]])
                            A(lambda e, Fs=Fs: e.activation(out=Fs[1], in_=Fs[1], func=AF.Exp, scale=-1.0), [rFs[1]], [rFs[1]])
                        else:
                            A(lambda e, Fs=Fs: e.activation(out=Fs[1], in_=Fs[1], func=AF.Exp, scale=-1.0), [rFs[1]], [rFs[1]])
                            V(lambda e, g=g, Fs=Fs: e.tensor_tensor(out=Fs[1], in0=Fs[1], in1=oml_g,
                                                                    op=ALU.mult), [rFs[1], rlb], [rFs[1]])
                            V(lambda e, g=g, Fs=Fs: e.tensor_tensor(out=Fs[1], in0=Fs[1], in1=lb_g,
                                                                    op=ALU.add), [rFs[1], rlb], [rFs[1]])
                            A(lambda e, Fs=Fs: e.activation(out=Fs[2], in_=Fs[1], func=AF.Ln), [rFs[1]], [rFs[2]])
                        (G if GOFF else V)(lambda e, Fs=Fs: e.tensor_scalar(out=Fs[3], in0=Fs[1], scalar1=-1.0, scalar2=1.0, op0=ALU.mult,
                                                                            op1=ALU.add), [rFs[1]], [rFs[3]])
                        T(lambda e, Fs=Fs: e.matmul(psG[:, 0:256], lhsT=Tri32, rhs=Fs[2], start=True, stop=True),
                          [rcst, rFs[2]], [rpsG])
                        T(lambda e, Fs=Fs: e.matmul(psG[:, 256:512], lhsT=TriE32, rhs=Fs[2], start=True, stop=True),
                          [rcst, rFs[2]], [rpsG])
                        for hh in range(2):
                            T(lambda e, hh=hh, Fs=Fs: e.matmul(psS[1][:, 256 + 2 * hh:256 + 2 * hh + 2],
                                                               lhsT=Fs[2][:, hh * 128:(hh + 1) * 128],
                                                               rhs=cst[:, 512:514], start=True, stop=True), [rFs[2], rcst], [rpsS[1]])
                        dsl = dec8[:, 4 * (i % 2):4 * (i % 2) + 4]
                        rds = r_dec8[i % 2]
                        A(lambda e, dsl=dsl: e.activation(out=dsl, in_=psS[1][:, 256:260], func=AF.Exp), [rpsS[1]], [rds])
                        A(lambda e, Fs=Fs: e.activation(out=Fs[4], in_=psG[:, 0:256], func=AF.Exp), [rpsG], [rFs[4]])
                        A(lambda e, Fs=Fs: e.activation(out=Fs[5], in_=psG[:, 0:256], func=AF.Exp, scale=-1.0), [rpsG], [rFs[5]])
                        A(lambda e, Fs=Fs: e.activation(out=Fs[6], in_=psG[:, 256:512], func=AF.Exp), [rpsG], [rFs[6]])
                        V(lambda e, Fs=Fs, Bs=Bs: e.tensor_tensor(out=Bs[1], in0=Fs[0], in1=Fs[4], op=ALU.mult), [rFs[0], rFs[4]], [rBs[1]])
                        G(lambda e, Fs=Fs, Bs=Bs: e.tensor_tensor(out=Bs[2], in0=Fs[3], in1=Fs[5], op=ALU.mult), [rFs[3], rFs[5]], [rBs[2]])
                        G(lambda e, Fs=Fs, Bs=Bs: e.tensor_tensor(out=Bs[3], in0=Fs[3], in1=Fs[6], op=ALU.mult), [rFs[3], rFs[6]], [rBs[3]])
                        for hh in range(2):
                            T(lambda e, hh=hh, Bs=Bs: e.transpose(out=psT[:, hh * 128:(hh + 1) * 128], in_=Bs[1][:, hh * 128:(hh + 1) * 128],
                                                                  identity=idb[:]), [rBs[1], ridb], [rpsT])
                            T(lambda e, hh=hh, Bs=Bs: e.transpose(out=psT[:, 256 + hh * 128:256 + (hh + 1) * 128],
                                                                  in_=Bs[2][:, hh * 128:(hh + 1) * 128], identity=idb[:]),
                              [rBs[2], ridb], [rpsT])
                        pq3 = psT[:, 0:256].rearrange("p (h t) -> p h t", h=2)
                        A(lambda e, Bs=Bs: e.copy(out=Bs[4], in_=psT[:, 0:256]), [rpsT], [rBs[4]])
                        V(lambda e, Bs=Bs: e.tensor_copy(out=Bs[5], in_=psT[:, 256:512]), [rpsT], [rBs[5]])
                        A(lambda e, pq3=pq3: e.copy(out=qTA[:, :, 0:64], in_=pq3[:, :, 0:64]), [rpsT], [rqTA])
                        V(lambda e, pq3=pq3: e.tensor_copy(out=qTB[:, :, 64:128], in_=pq3[:, :, 64:128]), [rpsT], [rqTB])
                        cur = i % 2
                        nxtb = 1 - cur
                        for hh in range(2):
                            hs = slice(hh * 128, (hh + 1) * 128)
                            T(lambda e, hs=hs, Bs=Bs: e.matmul(psS[1][:, hs], lhsT=Bs[5][:, hs], rhs=Bs[4][:, hs], start=True, stop=True),
                              [rBs[5], rBs[4]], [rpsS[1]])
                        for hh in range(2):
                            hs = slice(hh * 128, (hh + 1) * 128)
                            V(lambda e, hs=hs, Bs=Bs: e.tensor_tensor(out=Bs[6][:, hs], in0=psS[1][:, hs], in1=hgmb[:], op=ALU.mult),
                              [rpsS[1], rhgmb], [rBs[6]])
                        for hh in range(2):
                            hs = slice(hh * 128, (hh + 1) * 128)
                            T(lambda e, hs=hs, hh=hh, Bs=Bs: e.matmul(psO[hh][:, 0:128], lhsT=Bs[6][:, hs], rhs=Bs[0][:, hs],
                                                                      start=True, stop=False), [rBs[6], rBs[0]], [rpsO[hh]])
                            T(lambda e, hh=hh, cur=cur: e.matmul(psO[hh][:, 0:128], lhsT=qTA[:, hh, :], rhs=Sbf[:, cur, hh, :],
                                                                 start=False, stop=False), [rqTA, rSbf[cur][hh]], [rpsO[hh]])
                        for hh in range(2):
                            hs = slice(hh * 128, (hh + 1) * 128)
                            T(lambda e, hs=hs, Bs=Bs: e.matmul(psS[0][:, hs], lhsT=Bs[3][0:64, hs], rhs=Bs[0][0:64, hs],
                                                               start=True, stop=True), [rBs[3], rBs[0]], [rpsS[0], rrow])
                        for hh in range(2):
                            hs = slice(hh * 128, (hh + 1) * 128)
                            V(lambda e, hs=hs, hh=hh, dsl=dsl: e.scalar_tensor_tensor(out=S32[:, hh, :], in0=S32[:, hh, :],
                                                                                      scalar=dsl[:, 2 * hh:2 * hh + 1], in1=psS[0][:, hs],
                                                                                      op0=ALU.mult, op1=ALU.add),
                              [rS32[hh], rds, rpsS[0]], [rS32[hh]])
                            G(lambda e, hh=hh, nxtb=nxtb: e.tensor_copy(out=Sbf[:, nxtb, hh, :], in_=S32[:, hh, :]),
                              [rS32[hh]], [rSbf[nxtb][hh]])
                        for hh in range(2):
                            T(lambda e, hh=hh, nxtb=nxtb: e.matmul(psO[hh][:, 0:128], lhsT=qTB[:, hh, :], rhs=Sbf[:, nxtb, hh, :],
                                                                   start=False, stop=True), [rqTB, rSbf[nxtb][hh]], [rpsO[hh], rrow])
                        for hh in range(2):
                            hs = slice(hh * 128, (hh + 1) * 128)
                            T(lambda e, hs=hs, Bs=Bs: e.matmul(psS[0][:, hs], lhsT=Bs[3][64:128, hs], rhs=Bs[0][64:128, hs],
                                                               start=True, stop=True), [rBs[3], rBs[0]], [rpsS[0], rrow])
                        for hh in range(2):
                            hs = slice(hh * 128, (hh + 1) * 128)
                            V(lambda e, hs=hs, hh=hh, dsl=dsl: e.scalar_tensor_tensor(out=S32[:, hh, :], in0=S32[:, hh, :],
                                                                                      scalar=dsl[:, 2 * hh + 1:2 * hh + 2], in1=psS[0][:, hs],
                                                                                      op0=ALU.mult, op1=ALU.add),
                              [rS32[hh], rds, rpsS[0]], [rS32[hh]])
                        for hh in range(2):
                            G(lambda e, hh=hh, nxtb=nxtb: e.tensor_copy(out=Sbf[:, nxtb, hh, :], in_=S32[:, hh, :]),
                              [rS32[hh]], [rSbf[nxtb][hh]])
                        ssl = sso4[:, 2 * (i % 2):2 * (i % 2) + 2]
                        r_sso = r_sso2[i % 2]
                        for hh in range(2):
                            A(lambda e, hh=hh, Fs=Fs, ssl=ssl: e.activation(out=Fs[7][:, 0:128], in_=psO[hh][:, 0:128], func=AF.Square,
                                                                            accum_out=ssl[:, hh:hh + 1]), [rpsO[hh]], [rFs[7], r_sso])
                        A(lambda e, ssl=ssl: e.activation(out=ssl, in_=ssl, func=AF.Ln, scale=1.0 / 128, bias=EPS), [r_sso], [r_sso])
                        A(lambda e, ssl=ssl: e.activation(out=ssl, in_=ssl, func=AF.Exp, scale=-0.5), [r_sso], [r_sso])
                        for hh in range(2):
                            hs = slice(hh * 128, (hh + 1) * 128)
                            V(lambda e, hh=hh, hs=hs, Fs=Fs, ssl=ssl: e.scalar_tensor_tensor(out=Fs[8][:, hs], in0=psO[hh][:, 0:128],
                                                                                             scalar=ssl[:, hh:hh + 1], in1=go_bc[:],
                                                                                             op0=ALU.mult, op1=ALU.mult),
                              [rpsO[hh], r_sso, rgains], [rFs[8]])
                        G(lambda e, Fs=Fs, sl=sl, sz=sz: e.tensor_tensor(out=y_tok[:, sl, :], in0=Fs[8], in1=sz[:, sl, :], op=ALU.mult),
                          [rFs[8], rsz[sl]], [ry[sl]])
                        outproj_tile(i, sl, last, obanks=[(psS[0], rpsS[0]), (psS[0], rpsS[0])])
            if not glist and last_layer:
                for i in range(NT):
                    out_dmas.append(DMA("sync", lambda e, i=i: e.dma_start(out=out_d[i * 128:(i + 1) * 128, :], in_=x_tok[:, i, :]),
                                        r=[rx[i]]))
        if SCHED:
            if SCHED2:
                P.schedule2(SDELTA)
            else:
                P.schedule()
        P.emit(st, out_dmas)
    build_nc.stats = P.stats
    return nc


_CACHE = {}


def _get_nc(layers, groups):
    key = (tuple(layers), tuple(groups))
    if key not in _CACHE:
        _CACHE[key] = build_nc(layers, groups)
    return _CACHE[key]


def run(inputs, layers=(0, 1), groups=ALL_GROUPS, cores=8):
    nc = _get_nc(layers, groups)
    f = lambda a: np.ascontiguousarray(np.asarray(a))
    cst = make_consts()
    shared = {k: f(inputs[k]).astype(np.float32, copy=False) for k in
              ("norm_g", "w_in", "w_out", "moba_q_norm", "moba_k_norm", "hgrn_lb_logits", "hgrn_o_norm",
               "mem_norm_g", "w_mem_kv", "mem_q_norm", "mem_k_norm")}
    x = f(inputs["x"]); mem = f(inputs["mem"]); pos = f(inputs["positions"]).astype(np.int32, copy=False)
    in_maps = []
    for b in range(cores):
        m = dict(shared)
        m["x"] = x[b]
        m["mem"] = mem[b]
        m["pos"] = pos[b].reshape(16, 128)
        m["cst"] = cst
        in_maps.append(m)
    res = run_bass_kernel_spmd(nc, in_maps, core_ids=list(range(cores)))
    return np.stack([np.asarray(r["out"]) for r in res.results], axis=0)


def kernel(x, mem, positions, norm_g, w_in, w_out, moba_q_norm, moba_k_norm, hgrn_lb_logits,
           hgrn_o_norm, mem_norm_g, w_mem_kv, mem_q_norm, mem_k_norm):
    inputs = dict(x=x, mem=mem, positions=positions, norm_g=norm_g, w_in=w_in, w_out=w_out,
                  moba_q_norm=moba_q_norm, moba_k_norm=moba_k_norm, hgrn_lb_logits=hgrn_lb_logits,
                  hgrn_o_norm=hgrn_o_norm, mem_norm_g=mem_norm_g, w_mem_kv=w_mem_kv,
                  mem_q_norm=mem_q_norm, mem_k_norm=mem_k_norm)
    return run(inputs).astype(np.float32, copy=False)
```

```python
import numpy as np
from contextlib import ExitStack
import concourse.bass as bass
import concourse.mybir as mybir
from concourse.bass_utils import run_bass_kernel_spmd

F32 = mybir.dt.float32
BF16 = mybir.dt.bfloat16
I32 = mybir.dt.int32
AF = mybir.ActivationFunctionType
ALU = mybir.AluOpType
AX = mybir.AxisListType

S = 2048
D = 1024
NT = 16
EPS = 1e-6
NCST = 576
import os as _os0
ALL_GROUPS = tuple(_os0.environ.get("ORDER", "A0,A1,H0,H1,M0,M1").split(","))
import os as _os
SCHED = _os.environ.get("SCHED", "1") == "1"
PAR1 = int(_os.environ.get("PAR1", "1"))
PAR2 = int(_os.environ.get("PAR2", "1"))
GPAR = int(_os.environ.get("GPAR", "1"))
PS3 = int(_os.environ.get("PS3", "3"))
MASKV = int(_os.environ.get("MASKV", "1"))
OPB = int(_os.environ.get("OPB", "1"))
SCHED2 = int(_os.environ.get("SCHED2", "1"))
SDELTA = float(_os.environ.get("SDELTA", "100"))
LATX = float(_os.environ.get("LATX", "180"))
PEK = float(_os.environ.get("PEK", "0.65"))
ACTK = float(_os.environ.get("ACTK", "1.0"))
DVEK = float(_os.environ.get("DVEK", "1.0"))
POOLK = float(_os.environ.get("POOLK", "1.0"))
LATS = float(_os.environ.get("LATS", "60"))
XQ = int(_os.environ.get("XQ", "0"))
GOFF = int(_os.environ.get("GOFF", "0"))
TRANS = int(_os.environ.get("TRANS", "1"))
PRUNE = int(_os.environ.get("PRUNE", "1"))


class Res:
    __slots__ = ("name", "w", "r", "excl")

    def __init__(self, name, excl=False):
        self.name = name
        self.w = None
        self.r = []
        self.excl = excl


class _Rec:
    def __init__(self):
        self.name = None
        self.args = ()
        self.kw = {}

    def __getattr__(self, name):
        def f(*a, **k):
            self.name, self.args, self.kw = name, a, k
            return self
        return f


def _free_size(ap):
    n = 1
    for d in list(ap.shape)[1:]:
        n *= int(d)
    return n


class Op:
    __slots__ = ("eng", "fn", "deps", "sdeps", "sig", "idx", "dma", "sem", "val", "i", "cost", "start")

    def __init__(self, eng, fn, deps, sdeps, dma):
        self.eng = eng
        self.fn = fn
        self.deps = deps
        self.sdeps = sdeps
        self.sig = False
        self.idx = 0
        self.dma = dma
        self.sem = None
        self.val = 0
        self.i = 0
        self.start = 0.0
        rec = _Rec()
        fn(rec)
        out = rec.kw.get("out", rec.args[0] if rec.args else None)
        n = _free_size(out) if out is not None else 64
        if dma:
            c = 2000.0 + n * int(out.shape[0]) * 4 / 120.0
        elif eng == "tensor":
            if rec.name == "transpose":
                c = 110.0
            else:
                lhsT = rec.kw.get("lhsT")
                f32 = lhsT is not None and lhsT.dtype == F32
                c = PEK * (64.0 + max(n, 64) / 2.0) * (4.0 if f32 else 1.0)
        elif eng == "scalar":
            c = ACTK * (200.0 + n / 1.2)
        elif eng == "vector":
            c = DVEK * (120.0 + n / 0.96 * (8.0 if rec.name == "reciprocal" else 1.0))
        else:
            c = POOLK * (300.0 + n / 0.5)
        self.cost = c


class Prog:
    ENGS = ["tensor", "vector", "scalar", "gpsimd", "sync"]

    def __init__(self, nc):
        self.nc = nc
        self.ops = []

    phase = None
    tok = None
    tokset = ()

    def op(self, eng, fn, reads=(), writes=(), dma=False):
        if self.phase == "gate" and self.tok is not None:
            writes = list(writes) + [self.tok]
        elif self.phase == "chain" and eng in self.tokset:
            reads = list(reads) + [self.tok]
        deps, sdeps = {}, {}

        def add(d):
            if d.dma or dma or d.eng != eng or eng != "tensor":
                deps[id(d)] = d
            else:
                sdeps[id(d)] = d
        for r in reads:
            if r.w is not None:
                add(r.w)
            if r.excl:
                for d in r.r:
                    if d.eng != eng:
                        add(d)
        for w in writes:
            if w.w is not None:
                add(w.w)
            for d in w.r:
                add(d)
        o = Op(eng, fn, list(deps.values()), list(sdeps.values()), dma)
        for r in reads:
            r.r.append(o)
        for w in writes:
            w.w = o
            w.r = []
        self.ops.append(o)
        return o

    def schedule(self):
        import heapq
        ops = self.ops
        for i, o in enumerate(ops):
            o.i = i
        succs = [[] for _ in ops]
        npred = [0] * len(ops)
        for o in ops:
            ds = o.deps + o.sdeps
            npred[o.i] = len(ds)
            for d in ds:
                succs[d.i].append(o)
        ready = [0.0] * len(ops)
        free = {e: 0.0 for e in self.ENGS}
        heap = [(0.0, o.i) for o in ops if npred[o.i] == 0]
        heapq.heapify(heap)
        done = 0
        while heap:
            t, i = heapq.heappop(heap)
            o = ops[i]
            st = max(ready[i], free[o.eng])
            if st > t + 1e-9:
                heapq.heappush(heap, (st, i))
                continue
            o.start = st
            if o.dma:
                free[o.eng] = st + 150.0
            else:
                free[o.eng] = st + o.cost
            fin = st + o.cost
            done += 1
            for sc in succs[i]:
                lat = 60.0 if (sc.eng == o.eng and not o.dma) else 180.0
                if fin + lat > ready[sc.i]:
                    ready[sc.i] = fin + lat
                npred[sc.i] -= 1
                if npred[sc.i] == 0:
                    heapq.heappush(heap, (max(ready[sc.i], free[sc.eng]), sc.i))
        assert done == len(ops), (done, len(ops))
        self.ops = sorted(ops, key=lambda o: (o.start, o.i))
        self.est_ns = max(o.start + o.cost for o in ops)

    def schedule2(self, delta=120.0):
        ops = self.ops
        n = len(ops)
        for i, o in enumerate(ops):
            o.i = i
        succs = [[] for _ in ops]
        npred = [0] * n
        for o in ops:
            ds = o.deps + o.sdeps
            npred[o.i] = len(ds)
            for d in ds:
                succs[d.i].append(o)
        blev = [0.0] * n
        for o in reversed(ops):
            b = 0.0
            for sc in succs[o.i]:
                lat = LATS if (sc.eng == o.eng and not o.dma) else LATX
                v = lat + blev[sc.i]
                if v > b:
                    b = v
            blev[o.i] = b + o.cost
        ready = [0.0] * n
        free = {e: 0.0 for e in self.ENGS}
        rsets = {e: [] for e in self.ENGS}
        for o in ops:
            if npred[o.i] == 0:
                rsets[o.eng].append(o.i)
        done = 0
        while done < n:
            best_e, best_t = None, 1e30
            for e in self.ENGS:
                rs = rsets[e]
                if not rs:
                    continue
                t = min(ready[i] for i in rs)
                if t < free[e]:
                    t = free[e]
                if t < best_t:
                    best_t, best_e = t, e
            e = best_e
            rs = rsets[e]
            lim = best_t + delta
            pick, pb = -1, -1.0
            for i in rs:
                if ready[i] <= lim and blev[i] > pb:
                    pb, pick = blev[i], i
            rs.remove(pick)
            o = ops[pick]
            st = max(ready[pick], free[e])
            o.start = st
            free[e] = st + (150.0 if o.dma else o.cost)
            fin = st + o.cost
            done += 1
            for sc in succs[pick]:
                lat = LATS if (sc.eng == o.eng and not o.dma) else LATX
                if fin + lat > ready[sc.i]:
                    ready[sc.i] = fin + lat
                npred[sc.i] -= 1
                if npred[sc.i] == 0:
                    rsets[sc.eng].append(sc.i)
        self.ops = sorted(ops, key=lambda o: (o.start, o.i))
        self.est_ns = max(o.start + o.cost for o in ops)

    def emit(self, stack, final_deps, ndma_sems=8):
        nc = self.nc
        if PRUNE:
            pos = {id(o): k for k, o in enumerate(self.ops)}
            for o in self.ops:
                best = {}
                keep = []
                for d in o.deps:
                    if d.dma:
                        keep.append(d)
                    elif d.eng not in best or pos[id(d)] > pos[id(best[d.eng])]:
                        best[d.eng] = d
                o.deps = keep + list(best.values())
        for o in self.ops:
            for d in o.deps:
                d.sig = True
        for d in final_deps:
            d.sig = True
        sems = {e: stack.enter_context(nc.semaphore("s_" + e)) for e in self.ENGS}
        cnt = {e: 0 for e in self.ENGS}
        pools, pool_i, pre_wait = {}, {}, {}
        for o in self.ops:
            if o.dma:
                if o.eng not in pools:
                    pools[o.eng] = [[stack.enter_context(nc.semaphore("d_%s_%d" % (o.eng, i))), 0]
                                    for i in range(ndma_sems)]
                    pool_i[o.eng] = 0
                p = pools[o.eng][pool_i[o.eng] % ndma_sems]
                pool_i[o.eng] += 1
                if p[1] > 0:
                    pre_wait[id(o)] = (p[0], p[1])
                p[1] += 16
                o.sem = p[0]
                o.val = p[1]
            elif o.sig:
                cnt[o.eng] += 1
                o.idx = cnt[o.eng]
        per = {e: [o for o in self.ops if o.eng == e] for e in self.ENGS}
        self.stats = {e: len(per[e]) for e in self.ENGS}
        known = {e: {} for e in self.ENGS}
        kn = {}
        plan = {}
        nw = 0

        def semkey(d):
            return (d.sem, d.val) if d.dma else (sems[d.eng], d.idx)

        for o in self.ops:
            kd = known[o.eng]
            ws = []
            for d in o.deps:
                sm, val = semkey(d)
                if kd.get(id(sm), (None, 0))[1] < val:
                    ws.append((sm, val))
                    kd[id(sm)] = (sm, val)
                if TRANS:
                    for k2, (s2, v2) in kn[id(d)].items():
                        if kd.get(k2, (None, 0))[1] < v2:
                            kd[k2] = (s2, v2)
            if o.dma:
                pw = pre_wait.get(id(o))
                if pw and kd.get(id(pw[0]), (None, 0))[1] < pw[1]:
                    ws.append(pw)
                    kd[id(pw[0])] = pw
            plan[id(o)] = ws
            nw += len(ws)
            if o.dma or o.sig:
                mine = dict(kd)
                sm, val = semkey(o)
                mine[id(sm)] = (sm, val)
                kn[id(o)] = mine
        fin_w = []
        kd = known["sync"]
        for d in final_deps:
            sm, val = semkey(d)
            if kd.get(id(sm), (None, 0))[1] < val:
                fin_w.append((sm, val))
                kd[id(sm)] = (sm, val)
        self.stats["waits"] = nw
        self.stats["sigs"] = {e: sum(1 for o in per[e] if o.sig and not o.dma) for e in self.ENGS}
        block = stack.enter_context(nc.Block())

        def mk(e):
            def body(engobj):
                for o in per[e]:
                    for sm, val in plan[id(o)]:
                        engobj.wait_ge(sm, val)
                    if o.dma:
                        o.fn(engobj).then_inc(o.sem, 16)
                    else:
                        ins = o.fn(engobj)
                        if o.sig:
                            ins.then_inc(sems[e], 1)
                if e == "sync":
                    for sm, val in fin_w:
                        engobj.wait_ge(sm, val)
            return body

        block.tensor(mk("tensor"))
        block.vector(mk("vector"))
        block.scalar(mk("scalar"))
        block.gpsimd(mk("gpsimd"))
        block.sync(mk("sync"))


def make_consts():
    c = np.zeros((128, NCST), np.float32)
    i = np.arange(128)
    c[:, 0:128] = np.eye(128)
    c[:, 128:256] = (i[None, :] >= i[:, None])
    same = (i[:, None] // 64) == (i[None, :] // 64)
    c[:, 256:384] = same & (i[:, None] <= i[None, :])
    c[:, 384:512] = same & (i[:, None] > i[None, :])
    c[:, 512] = i < 64
    c[:, 513] = i >= 64
    c[:, 514] = 1.0
    f64 = 500000.0 ** (-np.arange(8, dtype=np.float64) / 8.0)
    f = f64.astype(np.float32)
    flo = (f64 - f.astype(np.float64)).astype(np.float32)
    c[:, 515:523] = f[None, :]
    c[:, 523:531] = f[None, :]
    c[:, 547:555] = flo[None, :]
    c[:, 555:563] = flo[None, :]
    c[:, 531:539] = 0.0
    c[:, 539:547] = np.pi / 2
    return c


def build_nc(layers=(0, 1), groups=ALL_GROUPS):
    nc = bass.Bass("TRN2", target_bir_lowering=False)

    def din(name, shape, d=F32):
        return nc.dram_tensor(name, shape, d, kind="ExternalInput").ap()

    x_d = din("x", [S, D])
    mem_d = din("mem", [256, D])
    pos_d = din("pos", [16, 128], I32)
    ng_d = din("norm_g", [2, D])
    win_d = din("w_in", [2, D, 5120])
    wout_d = din("w_out", [2, 1536, D])
    gq_d = din("moba_q_norm", [2, 64])
    gk_d = din("moba_k_norm", [2, 64])
    lbl_d = din("hgrn_lb_logits", [2, 512])
    go_d = din("hgrn_o_norm", [2, 128])
    mng_d = din("mem_norm_g", [2, D])
    wkv_d = din("w_mem_kv", [2, D, 1024])
    gmq_d = din("mem_q_norm", [2, 128])
    gmk_d = din("mem_k_norm", [2, 128])
    cst_d = din("cst", [128, NCST])
    out_d = nc.dram_tensor("out", [S, D], F32, kind="ExternalOutput").ap()

    P = Prog(nc)
    if _os.environ.get("TOKR"):
        P.tok = Res("tok")
        P.tokset = tuple(_os.environ["TOKR"].split(","))
    with ExitStack() as st:
        def sb(name, shape, dt=F32):
            return st.enter_context(nc.sbuf_tensor("sb_" + name, shape, dt))

        def ps(name, shape, dt=F32):
            return st.enter_context(nc.psum_tensor("pp_" + name, shape, dt))

        def T(fn, r=(), w=()):
            return P.op("tensor", fn, r, w)

        def V(fn, r=(), w=()):
            return P.op("vector", fn, r, w)

        def A(fn, r=(), w=()):
            return P.op("scalar", fn, r, w)

        def G(fn, r=(), w=()):
            return P.op("gpsimd", fn, r, w)

        def DMA(q, fn, r=(), w=()):
            return P.op(q, fn, r, w, dma=True)

        x_tok = sb("x_tok", [128, NT, D]); rx = [Res("x%d" % i) for i in range(NT)]
        hT = sb("hT", [128, 8, S], BF16); rhT = [Res("hT%d" % i) for i in range(NT)]
        cst = sb("cst", [128, NCST]); rcst = Res("cst")
        idb = sb("idb", [128, 128], BF16); ridb = Res("idb")
        trib = sb("trib", [128, 128], BF16); rtrib = Res("trib")
        hgmb = sb("hgmb", [128, 128], BF16); rhgmb = Res("hgmb")
        gq_bc = sb("gq_bc", [128, 64]); gk_bc = sb("gk_bc", [128, 64]); go_bc = sb("go_bc", [128, 128])
        gmq_bc = sb("gmq_bc", [128, 128]); gmk_bc = sb("gmk_bc", [128, 128]); rgains = Res("gains")
        cs = sb("cs", [128, NT, 16]); sn = sb("sn", [128, NT, 16]); rrope = Res("rope")
        wb = [sb("wb%d" % i, [128, 8, 512], BF16) for i in range(3)]; rwb = [[Res("wb%da" % i), Res("wb%db" % i)] for i in range(3)]
        wo = sb("wo", [128, 2, D], BF16); rwo = Res("wo")
        Fp = [sb("F%d" % i, [128, 256]) for i in range(9)]; rF = [Res("F%d" % i) for i in range(9)]
        Bp = [sb("B%d" % i, [128, 256], BF16) for i in range(7)]; rB = [Res("B%d" % i) for i in range(7)]
        szs = [sb("sz%d" % k, [128, 4, 256]) for k in range(2)]; rszs = [[Res("sz%d_%d" % (k, i)) for i in range(4)] for k in range(2)]
        y_tok = sb("y_tok", [128, 4, 256], BF16); ry = [Res("y%d" % i) for i in range(4)]
        yT = [sb("yT%d" % i, [128, 256], BF16) for i in range(2)]; ryT = [Res("yT%d" % i) for i in range(2)]
        pT = [sb("pT%d" % i, [128, 512], BF16) for i in range(4)]; rpT = [Res("pT%d" % i) for i in range(4)]
        hbs = [sb("hb%d" % i, [128, D], BF16) for i in range(2)]; rhbs = [Res("hb0"), Res("hb1")]
        small = sb("small", [128, 128]); rsm = {}

        def sm(name, a, n):
            rsm[name] = Res("sm_" + name)
            return small[:, a:a + n], rsm[name]
        ss16, r_ss16 = sm("ss16", 0, 16)
        t16, r_t16 = sm("t16", 16, 16)
        rstd16, r_rstd16 = sm("rstd16", 32, 16)
        ss4, r_ss4 = sm("ss4", 48, 4)
        t4, r_t4 = sm("t4", 52, 4)
        rs4, r_rs4 = sm("rs4", 56, 4)
        rden, r_rden = sm("rden", 60, 4)
        dec4, r_dec4 = sm("dec4", 64, 4)
        sso, r_sso = sm("sso", 68, 2)
        to2, r_to2 = sm("to2", 70, 2)
        rso, r_rso = sm("rso", 72, 2)
        gm = sb("gm", [128, 4, 8]); rgm = Res("gm")
        top8 = sb("top8", [128, 4, 8]); rtop8 = Res("top8")
        selb = sb("selb", [128, 4, 8]); rselb = Res("selb")
        kT = sb("kT", [128, 4, S], BF16); rkT = [Res("kT%d" % i) for i in range(NT)]
        v_flat = sb("v_aug", [128, NT * 4 * 65], BF16); rv = [Res("v%d" % i) for i in range(NT)]
        v_aug = v_flat[:].rearrange("p (a b c) -> p a b c", a=NT, b=4)
        stage = v_flat[:, 0:2048].bitcast(F32); rstage = Res("stage")
        g_bcv = v_flat[:, 2048:4096].bitcast(F32); rg_bc = Res("g_bc")
        qTs = [sb("qT%d" % i, [128, 2048], BF16) for i in range(2)]; rqTs = [Res("qT0"), Res("qT1")]
        lbv = qTs[1][:, 0:2048].bitcast(F32); rlb = rqTs[1]
        lb_g = lbv[:, 0:256]; oml_g = lbv[:, 256:512]
        k_aug = sb("k_aug", [128, 4, 72], BF16); rk_aug = Res("k_aug")
        q_aug = sb("q_aug", [128, 4, 72], BF16); rq_aug = Res("q_aug")
        kmT32 = sb("kmT32", [128, 2, 2, 8]); rkm = Res("kmT32")
        S32 = sb("S32", [128, 2, 128]); rS32 = [Res("S32_0"), Res("S32_1")]
        Sbf = sb("Sbf", [128, 2, 2, 128], BF16); rSbf = [[Res("Sbf00"), Res("Sbf01")], [Res("Sbf10"), Res("Sbf11")]]
        qTA = sb("qTA", [128, 2, 128], BF16); qTB = sb("qTB", [128, 2, 128], BF16)
        rqTA = Res("qTA"); rqTB = Res("qTB")
        memT = sb("memT", [128, 8, 256], BF16); rmemT = Res("memT")
        kmT = sb("kmT", [128, 2, 256], BF16); rkmT = Res("kmT")
        vm_aug = sb("vm_aug", [128, 2, 2, 129], BF16); rvm = Res("vm")
        psA = [ps("psA%d" % i, [128, 512]) for i in range(2)]; rpsA = [Res("psA0", True), Res("psA1", True)]
        psT = ps("psT", [128, 1024], BF16); rpsT = Res("psT", True)
        psG = ps("psG", [128, 512]); rpsG = Res("psG", True)
        psS = [ps("psS%d" % i, [128, 512]) for i in range(2)]; rpsS = [Res("psS0", True), Res("psS1", True)]
        psO = [ps("psO%d" % i, [128, 512]) for i in range(2)]; rpsO = [Res("psO0", True), Res("psO1", True)]

        ctr = {"pa": 0, "w": 0, "ev": 0, "ps": 0, "pt": 0, "yt": 0, "mk": 0}

        def nxt(k, n):
            v = ctr[k] % n
            ctr[k] += 1
            return v

        def evac(fn_v, fn_a, r, w):
            if nxt("ev", 2) == 0:
                return A(fn_a, r, w)
            return V(fn_v, r, w)

        DMA("sync", lambda e: e.dma_start(out=cst[:], in_=cst_d), w=[rcst])
        V(lambda e: e.tensor_copy(out=idb[:], in_=cst[:, 0:128]), [rcst], [ridb])
        V(lambda e: e.tensor_copy(out=trib[:], in_=cst[:, 128:256]), [rcst], [rtrib])
        V(lambda e: e.tensor_copy(out=hgmb[:], in_=cst[:, 256:384]), [rcst], [rhgmb])
        ident32 = cst[:, 0:128]
        G(lambda e: e.memset(vm_aug[:, :, :, 128:129], 1.0), w=[rvm])
        G(lambda e: e.memset(qTA[:], 0.0), w=[rqTA])
        G(lambda e: e.memset(qTB[:], 0.0), w=[rqTB])
        nI = sb("nI", [128, 256], I32); rnI = Res("nI")
        posi = nI[0:16, 0:128]; rposi = rnI
        posf = Fp[8][0:16, 0:128]; rposf = rF[8]
        DMA("sync", lambda e: e.dma_start(out=posi, in_=pos_d), w=[rposi])
        V(lambda e: e.tensor_copy(out=posf, in_=posi), [rposi], [rposf])
        T(lambda e: e.matmul(psG[:, 0:16], lhsT=posf, rhs=cst[0:16, 0:16], start=True, stop=True), [rposf, rcst], [rpsG])
        post, r_post = sm("post", 80, 16)
        V(lambda e: e.tensor_copy(out=post, in_=psG[:, 0:16]), [rpsG], [r_post])
        ang = Fp[0][:, 0:256].rearrange("p (i j) -> p i j", i=NT)
        V(lambda e: e.tensor_tensor(out=ang, in0=post.unsqueeze(2).to_broadcast([128, NT, 16]),
                                    in1=cst[:, 515:531].unsqueeze(1).to_broadcast([128, NT, 16]), op=ALU.mult),
          [r_post, rcst], [rF[0]])
        ang_lo = Fp[1][:, 0:256].rearrange("p (i j) -> p i j", i=NT)
        V(lambda e: e.tensor_tensor(out=ang_lo, in0=post.unsqueeze(2).to_broadcast([128, NT, 16]),
                                    in1=cst[:, 547:563].unsqueeze(1).to_broadcast([128, NT, 16]), op=ALU.mult),
          [r_post, rcst], [rF[1]])
        V(lambda e: e.tensor_tensor(out=ang, in0=ang, in1=ang_lo, op=ALU.add), [rF[0], rF[1]], [rF[0]])
        V(lambda e: e.tensor_tensor(out=ang, in0=ang, in1=cst[:, 531:547].unsqueeze(1).to_broadcast([128, NT, 16]),
                                    op=ALU.add), [rF[0], rcst], [rF[0]])
        V(lambda e: e.tensor_scalar(out=Fp[1][:], in0=Fp[0][:], scalar1=float(1.0 / (2 * np.pi)), scalar2=None,
                                    op0=ALU.mult), [rF[0]], [rF[1]])
        V(lambda e: e.tensor_copy(out=nI[:], in_=Fp[1][:]), [rF[1]], [rnI])
        V(lambda e: e.tensor_copy(out=Fp[1][:], in_=nI[:]), [rnI], [rF[1]])
        C1 = 6.28125
        C2 = float(2 * np.pi - 6.28125)
        V(lambda e: e.scalar_tensor_tensor(out=Fp[2][:], in0=Fp[1][:], scalar=-C1, in1=Fp[0][:],
                                           op0=ALU.mult, op1=ALU.add), [rF[1], rF[0]], [rF[2]])
        V(lambda e: e.scalar_tensor_tensor(out=Fp[2][:], in0=Fp[1][:], scalar=-C2, in1=Fp[2][:],
                                           op0=ALU.mult, op1=ALU.add), [rF[1], rF[2]], [rF[2]])
        V(lambda e: e.tensor_scalar(out=Fp[2][:], in0=Fp[2][:], scalar1=float(np.pi), scalar2=float(-np.pi),
                                    op0=ALU.min, op1=ALU.max), [rF[2]], [rF[2]])
        A(lambda e: e.activation(out=Fp[3][:], in_=Fp[2][:], func=AF.Sin), [rF[2]], [rF[3]])
        sc = Fp[3][:, 0:256].rearrange("p (i j) -> p i j", i=NT)
        V(lambda e: e.tensor_copy(out=cs[:, :, 0:8], in_=sc[:, :, 8:16]), [rF[3]], [rrope])
        V(lambda e: e.tensor_copy(out=cs[:, :, 8:16], in_=sc[:, :, 8:16]), [rF[3]], [rrope])
        V(lambda e: e.tensor_scalar(out=sn[:, :, 0:8], in0=sc[:, :, 0:8], scalar1=-1.0, scalar2=None, op0=ALU.mult),
          [rF[3]], [rrope])
        V(lambda e: e.tensor_copy(out=sn[:, :, 8:16], in_=sc[:, :, 0:8]), [rF[3]], [rrope])
        def load_w(dst, rdst, src_ap):
            return DMA("gpsimd", lambda e: e.dma_start(out=dst, in_=src_ap), w=[rdst])

        def w_in_cols(l, c0, n):
            return win_d[l].rearrange("(c p) n -> p c n", p=128)[:, :, c0:c0 + n]

        psGb = psG[:].bitcast(BF16)

        def rms_to_T(src_tile, rsrc, gain, rgain, dstT, rdst, col0, ssc, r_ssc, tsc, r_tsc, rsc, r_rsc, k):
            hb = hbs[k % 2]; rhb = rhbs[k % 2]
            pst, rpst = (psT, rpsT) if k % 2 == 0 else (psGb, rpsG)
            A(lambda e: e.activation(out=hb[:], in_=src_tile, func=AF.Square, accum_out=ssc[:, k:k + 1]),
              [rsrc], [rhb, r_ssc])
            A(lambda e: e.activation(out=tsc[:, k:k + 1], in_=ssc[:, k:k + 1], func=AF.Ln, scale=1.0 / D, bias=EPS),
              [r_ssc], [r_tsc])
            A(lambda e: e.activation(out=rsc[:, k:k + 1], in_=tsc[:, k:k + 1], func=AF.Exp, scale=-0.5),
              [r_tsc], [r_rsc])
            V(lambda e: e.scalar_tensor_tensor(out=hb[:], in0=src_tile, scalar=rsc[:, k:k + 1], in1=gain,
                                               op0=ALU.mult, op1=ALU.mult), [rsrc, r_rsc, rgain], [rhb])
            for c in range(8):
                T(lambda e, c=c: e.transpose(out=pst[:, c * 128:(c + 1) * 128], in_=hb[:, c * 128:(c + 1) * 128],
                                             identity=idb[:]), [rhb, ridb], [rpst])
            src3 = pst[:, 0:1024].rearrange("p (c t) -> p c t", c=8)
            evac(lambda e: e.tensor_copy(out=dstT[:, :, col0:col0 + 128], in_=src3),
                 lambda e: e.copy(out=dstT[:, :, col0:col0 + 128], in_=src3), [rpst], [rdst])

        def proj(lhs_cols, rlhs, wt, rwt, ncols=512, lhsT_src=None):
            b = nxt("pa", 2)
            src = hT if lhsT_src is None else lhsT_src
            for c in range(8):
                T(lambda e, c=c: e.matmul(psA[b][:, 0:ncols], lhsT=src[:, c, lhs_cols:lhs_cols + 128],
                                          rhs=wt[:, c, 0:ncols], start=(c == 0), stop=(c == 7)),
                  [rlhs] + list(rwt), [rpsA[b]])
            return psA[b], rpsA[b]

        def headnorm(src_ps, rps, H, Dh, gain, outF, routF, tmpA, rtmpA, tmpB, rtmpB):
            n = H * Dh
            A(lambda e: e.activation(out=tmpA[:, 0:n], in_=src_ps, func=AF.Square), [rps], [rtmpA])
            V(lambda e: e.tensor_reduce(out=ss4[:, 0:H], in_=tmpA[:, 0:n].rearrange("p (h d) -> p h d", h=H),
                                        axis=AX.X, op=ALU.add), [rtmpA], [r_ss4])
            A(lambda e: e.activation(out=t4[:, 0:H], in_=ss4[:, 0:H], func=AF.Ln, scale=1.0 / Dh, bias=EPS),
              [r_ss4], [r_t4])
            A(lambda e: e.activation(out=rs4[:, 0:H], in_=t4[:, 0:H], func=AF.Exp, scale=-0.5), [r_t4], [r_rs4])
            V(lambda e: e.tensor_tensor(out=tmpB[:, 0:n].rearrange("p (h d) -> p h d", h=H),
                                        in0=src_ps.rearrange("p (h d) -> p h d", h=H),
                                        in1=rs4[:, 0:H].unsqueeze(2).to_broadcast([128, H, Dh]), op=ALU.mult),
              [rps, r_rs4], [rtmpB])
            (G if GOFF else V)(lambda e: e.tensor_tensor(out=outF[:, 0:n].rearrange("p (h d) -> p h d", h=H),
                                                         in0=tmpB[:, 0:n].rearrange("p (h d) -> p h d", h=H),
                                                         in1=gain.unsqueeze(1).to_broadcast([128, H, Dh]), op=ALU.mult),
                               [rtmpB, rgains], [routF])

        def rope(Fx, rFx, i, tR, rtR):
            x3 = Fx[:, 0:256].rearrange("p (h d) -> p h d", h=4)
            a3 = tR[:, 0:64].rearrange("p (h d) -> p h d", h=4)
            b3 = tR[:, 64:128].rearrange("p (h d) -> p h d", h=4)
            rtA = rtR
            rtB = rtR
            G(lambda e: e.tensor_tensor(out=a3, in0=x3[:, :, 0:16], in1=cs[:, i, :].unsqueeze(1).to_broadcast([128, 4, 16]),
                                        op=ALU.mult), [rFx, rrope], [rtA])
            G(lambda e: e.tensor_tensor(out=b3[:, :, 0:8], in0=x3[:, :, 8:16],
                                        in1=sn[:, i, 0:8].unsqueeze(1).to_broadcast([128, 4, 8]), op=ALU.mult),
              [rFx, rrope], [rtB])
            G(lambda e: e.tensor_tensor(out=b3[:, :, 8:16], in0=x3[:, :, 0:8],
                                        in1=sn[:, i, 8:16].unsqueeze(1).to_broadcast([128, 4, 8]), op=ALU.mult),
              [rFx, rrope], [rtB])
            G(lambda e: e.tensor_tensor(out=x3[:, :, 0:16], in0=a3, in1=b3, op=ALU.add), [rtA, rtB], [rFx])

        def silu_ps(dst, rdst, src_ps, rps, defer=False):
            A(lambda e: e.activation(out=dst, in_=src_ps, func=AF.Exp, scale=-1.0), [rps], [rdst])
            A(lambda e: e.activation(out=dst, in_=dst, func=AF.Ln, bias=1.0), [rdst], [rdst])
            A(lambda e: e.activation(out=dst, in_=dst, func=AF.Exp, scale=-1.0), [rdst], [rdst])

            def fin():
                V(lambda e: e.tensor_tensor(out=dst, in0=src_ps, in1=dst, op=ALU.mult), [rps, rdst], [rdst])
            if defer:
                return fin
            fin()

        SC = [((Fp[0], rF[0]), (Fp[1], rF[1]), (Fp[2], rF[2]), (Fp[3], rF[3])),
              ((Fp[4], rF[4]), (Fp[6], rF[6]), (Fp[7], rF[7]), (Fp[8], rF[8]))]
        AUG = [(k_aug, rk_aug), (q_aug, rq_aug)]
        if _os.environ.get("NOPAR", "0") == "1":
            SC[1] = SC[0]
        if _os.environ.get("NOAUG", "0") == "1":
            AUG[1] = AUG[0]

        dec8, _r = sm("dec8", 96, 8)
        r_dec8 = [Res("dec8a"), Res("dec8b")]
        sso4, _r2 = sm("sso4", 104, 4)
        r_sso2 = [Res("ssoA"), Res("ssoB")]
        dummy = sb("dummy", [128, 8])
        kflat = kT[:].rearrange("p h t -> p (h t)")
        kf32 = kflat[:, 0:4608].bitcast(F32)
        Fq = [kf32[:, k * 256:(k + 1) * 256] for k in range(9)]
        rFq = [Res("Fq%d" % k) for k in range(9)]
        Bq = [kflat[:, 4608 + k * 256:4608 + (k + 1) * 256] for k in range(7)]
        rBq = [Res("Bq%d" % k) for k in range(7)]
        HSETS = [([t[:] for t in Fp], rF, [t[:] for t in Bp], rB), (Fq, rFq, Bq, rBq)]

        def barrier():
            G(lambda e: e.memset(dummy[:], 0.0), w=list(rkT) + rFq + rBq + list(rv) + [rstage, rg_bc])

        rrow = Res("pe_rowfence")

        out_dmas = []

        def outproj_tile(i, r, last, obanks=None):
            yb = nxt("yt", 2)
            for pp in range(2):
                T(lambda e, pp=pp: e.transpose(out=psT[:, pp * 128:(pp + 1) * 128], in_=y_tok[:, r, pp * 128:(pp + 1) * 128],
                                               identity=idb[:]), [ry[r], ridb], [rpsT])
            evac(lambda e: e.tensor_copy(out=yT[yb][:], in_=psT[:, 0:256]),
                 lambda e: e.copy(out=yT[yb][:], in_=psT[:, 0:256]), [rpsT], [ryT[yb]])
            for half in range(2):
                if obanks is None:
                    b = nxt("pa", 2)
                    pso, rpso = psA[b], rpsA[b]
                else:
                    pso, rpso = obanks[half]
                for pp in range(2):
                    T(lambda e, pp=pp, half=half, pso=pso: e.matmul(pso[:, 0:512], lhsT=yT[yb][:, pp * 128:(pp + 1) * 128],
                                                                    rhs=wo[:, pp, half * 512:(half + 1) * 512],
                                                                    start=(pp == 0), stop=(pp == 1)),
                      [ryT[yb], rwo], [rpso])
                V(lambda e, half=half, pso=pso: e.tensor_tensor(out=x_tok[:, i, half * 512:(half + 1) * 512], in0=pso[:, 0:512],
                                                                in1=x_tok[:, i, half * 512:(half + 1) * 512], op=ALU.add),
                  [rpso, rx[i]], [rx[i]])
            if last:
                out_dmas.append(DMA("sync", lambda e: e.dma_start(out=out_d[i * 128:(i + 1) * 128, :], in_=x_tok[:, i, :]),
                                    r=[rx[i]]))

        def attn_chunk(Q, H, Dh, KP, scale, key_tiles, causal, kTsrc, rkTsrc, vsrc, rvsrc, qview, rqT, sz, rsz):
            DA = Dh + 1
            for h in range(H):
                if Dh == 64:
                    o_b = h % 2
                    banks = [o_b, o_b, o_b, o_b]
                    offs = [0, DA, 2 * DA, 3 * DA]
                else:
                    banks = [0, 0, 1, 1]
                    offs = [0, DA, 0, DA]
                started = set()
                kts = key_tiles(Q)
                for kt in kts:
                    j = kt - 4 * Q if causal else -1
                    q0 = max(j, 0) * 128
                    N = 512 - q0
                    sbk = nxt("ps", PS3)
                    pss, rpss = ((psS[0], rpsS[0]), (psS[1], rpsS[1]), (psG, rpsG))[sbk]
                    T(lambda e, kt=kt, h=h, q0=q0, N=N, pss=pss: e.matmul(
                        pss[:, 0:N], lhsT=kTsrc(h, kt), rhs=qview(h)[:, q0:512], start=True, stop=True),
                      [rkTsrc(kt), rqT], [rpss])
                    pb = nxt("pt", 4)
                    A(lambda e, N=N, pss=pss, pb=pb: e.activation(out=pT[pb][:, 0:N], in_=pss[:, 0:N], func=AF.Exp,
                                                                  scale=scale), [rpss], [rpT[pb]])
                    if j >= 0:
                        (V if (MASKV and nxt("mk", 2) == 0) else G)(
                            lambda e, pb=pb: e.tensor_tensor(out=pT[pb][:, 0:128], in0=pT[pb][:, 0:128], in1=trib[:],
                                                             op=ALU.mult), [rpT[pb], rtrib], [rpT[pb]])
                    for r in range(max(j, 0), 4):
                        bk = banks[r]
                        first = bk not in started
                        started.add(bk)
                        T(lambda e, r=r, kt=kt, h=h, q0=q0, pb=pb, bk=bk, first=first: e.matmul(
                            psO[bk][:, offs[r]:offs[r] + DA], lhsT=pT[pb][:, r * 128 - q0:r * 128 - q0 + 128],
                            rhs=vsrc(h, kt), start=first, stop=False, skip_group_check=True),
                          [rpT[pb], rvsrc(kt)], [rpsO[bk]])
                for bk0 in sorted(set(banks)):
                    rs_ = [r for r in range(4) if banks[r] == bk0]
                    nr = len(rs_)
                    V(lambda e, bk0=bk0, rs_=rs_, nr=nr: e.reciprocal(
                        out=rden[:, rs_[0]:rs_[0] + nr],
                        in_=psO[bk0][:, 0:nr * DA].rearrange("p (r c) -> p r c", r=nr)[:, :, Dh:DA]),
                      [rpsO[bk0]], [r_rden])
                for r in range(4):
                    bk = banks[r]
                    V(lambda e, r=r, bk=bk, h=h: e.scalar_tensor_tensor(
                        out=y_tok[:, r, h * Dh:(h + 1) * Dh], in0=psO[bk][:, offs[r]:offs[r] + Dh], scalar=rden[:, r:r + 1],
                        in1=sz[:, r, h * Dh:(h + 1) * Dh], op0=ALU.mult, op1=ALU.mult),
                      [rpsO[bk], r_rden, rsz[r]], [ry[r]])

        for li, l in enumerate(layers):
            last_layer = (li == len(layers) - 1)
            DMA("sync", lambda e, l=l: e.dma_start(out=g_bcv, in_=ng_d[l:l + 1, :].partition_broadcast(128)), w=[rg_bc])
            for dst, src in ((gq_bc, gq_d), (gk_bc, gk_d), (go_bc, go_d), (gmq_bc, gmq_d), (gmk_bc, gmk_d)):
                DMA("sync", lambda e, l=l, dst=dst, src=src: e.dma_start(out=dst[:], in_=src[l:l + 1, :].partition_broadcast(128)),
                    w=[rgains])
            for i in range(NT):
                if li == 0:
                    DMA(("scalar" if (XQ and i % 2 == 1) else "sync"), lambda e, i=i: e.dma_start(out=x_tok[:, i, :], in_=x_d[i * 128:(i + 1) * 128, :]), w=[rx[i]])
                rms_to_T(x_tok[:, i, :], rx[i], g_bcv, rg_bc, hT, rhT[i], i * 128, ss16, r_ss16, t16, r_t16,
                         rstd16, r_rstd16, i)
            glist = [g for g in ALL_GROUPS if g in groups]
            for gi, gname in enumerate(glist):
                last = last_layer and gi == len(glist) - 1
                kind = gname[0]
                g = int(gname[1])
                if kind == "A":
                    w1 = nxt("w", 3); w2 = nxt("w", 3)
                    load_w(wb[w1][:, :, 0:256], rwb[w1][0], w_in_cols(l, 512 + 256 * g, 256))
                    load_w(wb[w1][:, :, 256:512], rwb[w1][1], w_in_cols(l, 1024 + 256 * g, 256))
                    load_w(wb[w2][:, :, 0:256], rwb[w2][0], w_in_cols(l, 256 * g, 256))
                    load_w(wb[w2][:, :, 256:512], rwb[w2][1], w_in_cols(l, 3584 + 256 * g, 256))
                    load_w(wo[:], rwo, wout_d[l, 256 * g:256 * g + 256, :].rearrange("(c p) n -> p c n", p=128))
                    barrier()
                    G(lambda e: e.memset(v_aug[:, :, :, 64:65], 1.0), w=rv)
                    V(lambda e: e.memset(psG[:, 0:16], 0.0), w=[rpsG])
                    for i in range(NT):
                        n_blk = i // 2
                        (tA, rtA), (tB, rtB), (FO, rFO), (tR, rtR) = SC[(i % 2) * PAR1]
                        ka, rka = AUG[(i % 2) * PAR1]
                        pa, rpa = proj(i * 128, rhT[i], wb[w1], rwb[w1])
                        headnorm(pa[:, 0:256], rpa, 4, 64, gk_bc[:], FO, rFO, tA, rtA, tB, rtB)
                        evac(lambda e, i=i, pa=pa: e.tensor_copy(out=v_aug[:, i, :, 0:64],
                                                                 in_=pa[:, 256:512].rearrange("p (h d) -> p h d", h=4)),
                             lambda e, i=i, pa=pa: e.copy(out=v_aug[:, i, :, 0:64],
                                                          in_=pa[:, 256:512].rearrange("p (h d) -> p h d", h=4)),
                             [rpa], [rv[i]])
                        rope(FO, rFO, i, tR, rtR)
                        G(lambda e, ka=ka, FO=FO: e.tensor_copy(out=ka[:, :, 0:64], in_=FO[:].rearrange("p (h d) -> p h d", h=4)),
                          [rFO], [rka])
                        G(lambda e, ka=ka: e.memset(ka[:, :, 64:72], 0.0), w=[rka])
                        G(lambda e, ka=ka, n_blk=n_blk: e.memset(ka[:, :, 64 + n_blk:65 + n_blk], 1.0), w=[rka])
                        for pp in range(2):
                            T(lambda e, pp=pp, n_blk=n_blk, FO=FO: e.matmul(psG[:, pp * 8 + n_blk:pp * 8 + n_blk + 1],
                                                                            lhsT=FO[:, pp * 128:(pp + 1) * 128], rhs=cst[:, 514:515],
                                                                            start=False, stop=False, skip_group_check=True),
                              [rFO, rcst], [rpsG])
                        for h in range(4):
                            T(lambda e, h=h, ka=ka: e.transpose(out=psT[0:72, h * 128:(h + 1) * 128], in_=ka[:, h, :], identity=idb[:]),
                              [rka, ridb], [rpsT])
                        src3 = psT[0:72, 0:512].rearrange("p (h t) -> p h t", h=4)
                        evac(lambda e, i=i, src3=src3: e.tensor_copy(out=kT[0:72, :, i * 128:(i + 1) * 128], in_=src3),
                             lambda e, i=i, src3=src3: e.copy(out=kT[0:72, :, i * 128:(i + 1) * 128], in_=src3),
                             [rpsT], [rkT[i]])
                    G(lambda e: e.memset(kmT32[:], 0.0), w=[rkm])
                    A(lambda e: e.copy(out=kmT32[0:64, :, 0, :], in_=psG[0:64, 0:16].rearrange("p (a n) -> p a n", a=2)), [rpsG], [rkm])
                    A(lambda e: e.copy(out=kmT32[64:128, :, 1, :], in_=psG[64:128, 0:16].rearrange("p (a n) -> p a n", a=2)), [rpsG], [rkm])
                    for Q in range(4):
                        qT3 = qTs[Q % 2][:, 0:2048].rearrange("p (h t) -> p h t", h=4)
                        rqT = rqTs[Q % 2]
                        sz = szs[Q % 2]; rsz = rszs[Q % 2]
                        for r in range(4):
                            i = 4 * Q + r
                            own = i // 2
                            P.phase = "chain"
                            par = (i % 2) * PAR2 * (1 if (own < 4 or GPAR) else 0)
                            (tA, rtA), (tB, rtB), (FO, rFO), (tR, rtR) = SC[par]
                            qa, rqa = AUG[par]
                            pa, rpa = proj(i * 128, rhT[i], wb[w2], rwb[w2])
                            fin = silu_ps(sz[:, r, :], rsz[r], pa[:, 256:512], rpa, defer=True)
                            headnorm(pa[:, 0:256], rpa, 4, 64, gq_bc[:], FO, rFO, tA, rtA, tB, rtB)
                            fin()
                            rope(FO, rFO, i, tR, rtR)
                            G(lambda e, qa=qa, FO=FO: e.tensor_copy(out=qa[:, :, 0:64], in_=FO[:].rearrange("p (h d) -> p h d", h=4)),
                              [rFO], [rqa])
                            if own >= 4:
                                P.phase = "gate"
                                for pp in range(2):
                                    T(lambda e, pp=pp, FO=FO: e.matmul(psG[:, pp * 128:(pp + 1) * 128],
                                                                       lhsT=FO[:, pp * 128:(pp + 1) * 128], rhs=ident32,
                                                                       start=True, stop=True),
                                      [rFO, rcst], [rpsG])
                                A(lambda e: e.copy(out=Fp[5][:], in_=psG[:, 0:256]), [rpsG], [rF[5]])
                                for h in range(4):
                                    T(lambda e, h=h, own=own: e.matmul(
                                        psG[:, 256 + h * 8:256 + h * 8 + own],
                                        lhsT=Fp[5][:, (h // 2) * 128:(h // 2) * 128 + 128],
                                        rhs=kmT32[:, h // 2, h % 2, 0:own], start=True, stop=True),
                                      [rF[5], rkm], [rpsG])
                                V(lambda e: e.memset(gm[:], -1.0e30), w=[rgm])
                                V(lambda e, own=own: e.tensor_copy(
                                    out=gm[:, :, 0:own], in_=psG[:, 256:288].rearrange("p (h n) -> p h n", h=4)[:, :, 0:own]),
                                  [rpsG], [rgm])
                                for h in range(4):
                                    V(lambda e, h=h: e.max(out=top8[:, h, :], in_=gm[:, h, :]), [rgm], [rtop8])
                                V(lambda e: e.tensor_tensor(out=selb[:], in0=gm[:], in1=top8[:, :, 2:3].to_broadcast([128, 4, 8]),
                                                            op=ALU.is_ge), [rgm, rtop8], [rselb])
                                V(lambda e, qa=qa: e.tensor_scalar(out=qa[:, :, 64:72], in0=selb[:], scalar1=30000.0,
                                                                   scalar2=-30000.0, op0=ALU.mult, op1=ALU.add), [rselb], [rqa])
                                V(lambda e, qa=qa, own=own: e.memset(qa[:, :, 64 + own:65 + own], 0.0), w=[rqa])
                            else:
                                G(lambda e, qa=qa: e.memset(qa[:, :, 64:72], 0.0), w=[rqa])
                            P.phase = None
                            for h in range(4):
                                T(lambda e, h=h, qa=qa: e.transpose(out=psT[0:72, h * 128:(h + 1) * 128], in_=qa[:, h, :],
                                                                    identity=idb[:]), [rqa, ridb], [rpsT])
                            src3 = psT[0:72, 0:512].rearrange("p (h t) -> p h t", h=4)
                            evac(lambda e, r=r, src3=src3, qT3=qT3: e.tensor_copy(out=qT3[0:72, :, r * 128:(r + 1) * 128], in_=src3),
                                 lambda e, r=r, src3=src3, qT3=qT3: e.copy(out=qT3[0:72, :, r * 128:(r + 1) * 128], in_=src3),
                                 [rpsT], [rqT])
                        attn_chunk(Q, 4, 64, 72, 0.125, lambda Q: list(range(4 * Q + 4)), True,
                                   lambda h, kt: kT[0:72, h, kt * 128:(kt + 1) * 128], lambda kt: rkT[kt],
                                   lambda h, kt: v_aug[:, kt, h, :], lambda kt: rv[kt],
                                   lambda h, qT3=qT3: qT3[0:72, h, :], rqT, sz, rsz)
                        for r in range(4):
                            outproj_tile(4 * Q + r, r, last, obanks=([(psO[0], rpsO[0]), (psO[1], rpsO[1])] if OPB else None))
                elif kind == "M":
                    w1 = nxt("w", 3); w2 = nxt("w", 3)
                    wkvv = wkv_d[l].rearrange("(c p) n -> p c n", p=128)
                    load_w(wb[w1][:, :, 0:256], rwb[w1][0], wkvv[:, :, 256 * g:256 * g + 256])
                    load_w(wb[w1][:, :, 256:512], rwb[w1][1], wkvv[:, :, 512 + 256 * g:512 + 256 * g + 256])
                    load_w(wb[w2][:, :, 0:256], rwb[w2][0], w_in_cols(l, 3072 + 256 * g, 256))
                    load_w(wb[w2][:, :, 256:512], rwb[w2][1], w_in_cols(l, 4608 + 256 * g, 256))
                    load_w(wo[:], rwo, wout_d[l, 1024 + 256 * g:1024 + 256 * g + 256, :].rearrange("(c p) n -> p c n", p=128))
                    if g == 0 or ("M0" not in groups):
                        barrier()
                        DMA("sync", lambda e, l=l: e.dma_start(out=g_bcv, in_=mng_d[l:l + 1, :].partition_broadcast(128)),
                            w=[rg_bc])
                        for mt in range(2):
                            DMA("sync", lambda e, mt=mt: e.dma_start(out=stage, in_=mem_d[mt * 128:(mt + 1) * 128, :]),
                                w=[rstage])
                            rms_to_T(stage, rstage, g_bcv, rg_bc, memT, rmemT, mt * 128, ss16, r_ss16, t16, r_t16,
                                     rstd16, r_rstd16, mt)
                    for mt in range(2):
                        (tA, rtA), (tB, rtB), (FO, rFO), (tR, rtR) = SC[mt % 2]
                        pa, rpa = proj(mt * 128, rmemT, wb[w1], rwb[w1], lhsT_src=memT)
                        headnorm(pa[:, 0:256], rpa, 2, 128, gmk_bc[:], FO, rFO, tA, rtA, tB, rtB)
                        evac(lambda e, mt=mt, pa=pa: e.tensor_copy(out=vm_aug[:, mt, :, 0:128],
                                                                   in_=pa[:, 256:512].rearrange("p (h d) -> p h d", h=2)),
                             lambda e, mt=mt, pa=pa: e.copy(out=vm_aug[:, mt, :, 0:128],
                                                            in_=pa[:, 256:512].rearrange("p (h d) -> p h d", h=2)),
                             [rpa], [rvm])
                        G(lambda e, mt=mt, FO=FO: e.tensor_copy(out=Bp[mt % 2][:], in_=FO[:]), [rFO], [rB[mt % 2]])
                        for hh in range(2):
                            T(lambda e, hh=hh, mt=mt: e.transpose(out=psT[:, hh * 128:(hh + 1) * 128],
                                                                  in_=Bp[mt % 2][:, hh * 128:(hh + 1) * 128],
                                                                  identity=idb[:]), [rB[mt % 2], ridb], [rpsT])
                        src3 = psT[:, 0:256].rearrange("p (h t) -> p h t", h=2)
                        evac(lambda e, mt=mt, src3=src3: e.tensor_copy(out=kmT[:, :, mt * 128:(mt + 1) * 128], in_=src3),
                             lambda e, mt=mt, src3=src3: e.copy(out=kmT[:, :, mt * 128:(mt + 1) * 128], in_=src3),
                             [rpsT], [rkmT])
                    for Q in range(4):
                        qm3 = qTs[Q % 2][:, 0:1024].rearrange("p (h t) -> p h t", h=2)
                        rqT = rqTs[Q % 2]
                        sz = szs[Q % 2]; rsz = rszs[Q % 2]
                        for r in range(4):
                            i = 4 * Q + r
                            (tA, rtA), (tB, rtB), (FO, rFO), (tR, rtR) = SC[i % 2]
                            pa, rpa = proj(i * 128, rhT[i], wb[w2], rwb[w2])
                            fin = silu_ps(sz[:, r, :], rsz[r], pa[:, 256:512], rpa, defer=True)
                            headnorm(pa[:, 0:256], rpa, 2, 128, gmq_bc[:], FO, rFO, tA, rtA, tB, rtB)
                            fin()
                            G(lambda e, i=i, FO=FO: e.tensor_copy(out=Bp[i % 2][:], in_=FO[:]), [rFO], [rB[i % 2]])
                            for hh in range(2):
                                T(lambda e, hh=hh, i=i: e.transpose(out=psT[:, hh * 128:(hh + 1) * 128],
                                                                    in_=Bp[i % 2][:, hh * 128:(hh + 1) * 128], identity=idb[:]),
                                  [rB[i % 2], ridb], [rpsT])
                            src3 = psT[:, 0:256].rearrange("p (h t) -> p h t", h=2)
                            evac(lambda e, r=r, src3=src3, qm3=qm3: e.tensor_copy(out=qm3[:, :, r * 128:(r + 1) * 128], in_=src3),
                                 lambda e, r=r, src3=src3, qm3=qm3: e.copy(out=qm3[:, :, r * 128:(r + 1) * 128], in_=src3),
                                 [rpsT], [rqT])
                        attn_chunk(Q, 2, 128, 128, float(128 ** -0.5), lambda Q: [0, 1], False,
                                   lambda h, kt: kmT[:, h, kt * 128:(kt + 1) * 128], lambda kt: rkmT,
                                   lambda h, kt: vm_aug[:, kt, h, :], lambda kt: rvm,
                                   lambda h, qm3=qm3: qm3[:, h, :], rqT, sz, rsz)
                        for r in range(4):
                            outproj_tile(4 * Q + r, r, last, obanks=([(psO[0], rpsO[0]), (psO[1], rpsO[1])] if OPB else None))
                else:
                    w1 = nxt("w", 3); w2 = nxt("w", 3)
                    load_w(wb[w1][:, :, 0:256], rwb[w1][0], w_in_cols(l, 1536 + 256 * g, 256))
                    load_w(wb[w1][:, :, 256:512], rwb[w1][1], w_in_cols(l, 2048 + 256 * g, 256))
                    load_w(wb[w2][:, :, 0:256], rwb[w2][0], w_in_cols(l, 2560 + 256 * g, 256))
                    load_w(wb[w2][:, :, 256:512], rwb[w2][1], w_in_cols(l, 4096 + 256 * g, 256))
                    load_w(wo[:], rwo, wout_d[l, 512 + 256 * g:512 + 256 * g + 256, :].rearrange("(c p) n -> p c n", p=128))
                    barrier()
                    if l != 0:
                        DMA("sync", lambda e, g=g: e.dma_start(out=lb_g, in_=lbl_d[1:2, 256 * g:256 * g + 256].partition_broadcast(128)), w=[rlb])
                        DMA("sync", lambda e, g=g: e.dma_start(out=oml_g, in_=lbl_d[0:1, 256 * g:256 * g + 256].partition_broadcast(128)), w=[rlb])
                        V(lambda e: e.tensor_tensor(out=lb_g, in0=lb_g, in1=oml_g, op=ALU.subtract), [rlb], [rlb])
                        A(lambda e: e.activation(out=lb_g, in_=lb_g, func=AF.Exp, scale=-1.0), [rlb], [rlb])
                        A(lambda e: e.activation(out=lb_g, in_=lb_g, func=AF.Ln, bias=1.0), [rlb], [rlb])
                        A(lambda e: e.activation(out=lb_g, in_=lb_g, func=AF.Exp, scale=-1.0), [rlb], [rlb])
                        V(lambda e: e.tensor_scalar(out=oml_g, in0=lb_g, scalar1=-1.0, scalar2=1.0, op0=ALU.mult, op1=ALU.add),
                          [rlb], [rlb])
                    for hh in range(2):
                        G(lambda e, hh=hh: e.memset(S32[:, hh, :], 0.0), w=[rS32[hh]])
                        G(lambda e, hh=hh: e.memset(Sbf[:, 0, hh, :], 0.0), w=[rSbf[0][hh]])
                    Tri32 = cst[:, 256:384]
                    TriE32 = cst[:, 384:512]
                    for i in range(NT):
                        Fs, rFs, Bs, rBs = HSETS[i % 2]
                        sl = i % 4
                        sz = szs[(i // 4) % 2]; rsz = rszs[(i // 4) % 2]
                        pq, rpq = proj(i * 128, rhT[i], wb[w1], rwb[w1])
                        finq = silu_ps(Fs[0], rFs[0], pq[:, 0:256], rpq, defer=True)
                        A(lambda e, pq=pq, Fs=Fs: e.activation(out=Fs[1], in_=pq[:, 256:512], func=AF.Exp, scale=-1.0), [rpq], [rFs[1]])
                        A(lambda e, Fs=Fs: e.activation(out=Fs[1], in_=Fs[1], func=AF.Ln, bias=1.0), [rFs[1]], [rFs[1]])
                        finq()
                        pi_, rpi = proj(i * 128, rhT[i], wb[w2], rwb[w2])
                        silu_ps(sz[:, sl, :], rsz[sl], pi_[:, 256:512], rpi)
                        V(lambda e, pi_=pi_, Bs=Bs: e.tensor_copy(out=Bs[0], in_=pi_[:, 0:256]), [rpi], [rBs[0]])
                        if l == 0:
                            A(lambda e, Fs=Fs: e.activation(out=Fs[2], in_=Fs[1], func=AF.Copy, scale=-1.0), [rFs[1]], [rFs[2]])
                            A(lambda e, Fs=Fs: e.activation(out=Fs[1], in_=Fs[1], func=AF.Exp, scale=-1.0), [rFs[1]], [rFs[1]])
                        else:
                            A(lambda e, Fs=Fs: e.activation(out=Fs[1], in_=Fs[1], func=AF.Exp, scale=-1.0), [rFs[1]], [rFs[1]])
                            V(lambda e, g=g, Fs=Fs: e.tensor_tensor(out=Fs[1], in0=Fs[1], in1=oml_g,
                                                                    op=ALU.mult), [rFs[1], rlb], [rFs[1]])
                            V(lambda e, g=g, Fs=Fs: e.tensor_tensor(out=Fs[1], in0=Fs[1], in1=lb_g,
                                                                    op=ALU.add), [rFs[1], rlb], [rFs[1]])
                            A(lambda e, Fs=Fs: e.activation(out=Fs[2], in_=Fs[1], func=AF.Ln), [rFs[1]], [rFs[2]])
                        (G if GOFF else V)(lambda e, Fs=Fs: e.tensor_scalar(out=Fs[3], in0=Fs[1], scalar1=-1.0, scalar2=1.0, op0=ALU.mult,
                                                                            op1=ALU.add), [rFs[1]], [rFs[3]])
                        T(lambda e, Fs=Fs: e.matmul(psG[:, 0:256], lhsT=Tri32, rhs=Fs[2], start=True, stop=True),
                          [rcst, rFs[2]], [rpsG])
                        T(lambda e, Fs=Fs: e.matmul(psG[:, 256:512], lhsT=TriE32, rhs=Fs[2], start=True, stop=True),
                          [rcst, rFs[2]], [rpsG])
                        for hh in range(2):
                            T(lambda e, hh=hh, Fs=Fs: e.matmul(psS[1][:, 256 + 2 * hh:256 + 2 * hh + 2],
                                                               lhsT=Fs[2][:, hh * 128:(hh + 1) * 128],
                                                               rhs=cst[:, 512:514], start=True, stop=True), [rFs[2], rcst], [rpsS[1]])
                        dsl = dec8[:, 4 * (i % 2):4 * (i % 2) + 4]
                        rds = r_dec8[i % 2]
                        A(lambda e, dsl=dsl: e.activation(out=dsl, in_=psS[1][:, 256:260], func=AF.Exp), [rpsS[1]], [rds])
                        A(lambda e, Fs=Fs: e.activation(out=Fs[4], in_=psG[:, 0:256], func=AF.Exp), [rpsG], [rFs[4]])
                        A(lambda e, Fs=Fs: e.activation(out=Fs[5], in_=psG[:, 0:256], func=AF.Exp, scale=-1.0), [rpsG], [rFs[5]])
                        A(lambda e, Fs=Fs: e.activation(out=Fs[6], in_=psG[:, 256:512], func=AF.Exp), [rpsG], [rFs[6]])
                        V(lambda e, Fs=Fs, Bs=Bs: e.tensor_tensor(out=Bs[1], in0=Fs[0], in1=Fs[4], op=ALU.mult), [rFs[0], rFs[4]], [rBs[1]])
                        G(lambda e, Fs=Fs, Bs=Bs: e.tensor_tensor(out=Bs[2], in0=Fs[3], in1=Fs[5], op=ALU.mult), [rFs[3], rFs[5]], [rBs[2]])
                        G(lambda e, Fs=Fs, Bs=Bs: e.tensor_tensor(out=Bs[3], in0=Fs[3], in1=Fs[6], op=ALU.mult), [rFs[3], rFs[6]], [rBs[3]])
                        for hh in range(2):
                            T(lambda e, hh=hh, Bs=Bs: e.transpose(out=psT[:, hh * 128:(hh + 1) * 128], in_=Bs[1][:, hh * 128:(hh + 1) * 128],
                                                                  identity=idb[:]), [rBs[1], ridb], [rpsT])
                            T(lambda e, hh=hh, Bs=Bs: e.transpose(out=psT[:, 256 + hh * 128:256 + (hh + 1) * 128],
                                                                  in_=Bs[2][:, hh * 128:(hh + 1) * 128], identity=idb[:]),
                              [rBs[2], ridb], [rpsT])
                        pq3 = psT[:, 0:256].rearrange("p (h t) -> p h t", h=2)
                        A(lambda e, Bs=Bs: e.copy(out=Bs[4], in_=psT[:, 0:256]), [rpsT], [rBs[4]])
                        V(lambda e, Bs=Bs: e.tensor_copy(out=Bs[5], in_=psT[:, 256:512]), [rpsT], [rBs[5]])
                        A(lambda e, pq3=pq3: e.copy(out=qTA[:, :, 0:64], in_=pq3[:, :, 0:64]), [rpsT], [rqTA])
                        V(lambda e, pq3=pq3: e.tensor_copy(out=qTB[:, :, 64:128], in_=pq3[:, :, 64:128]), [rpsT], [rqTB])
                        cur = i % 2
                        nxtb = 1 - cur
                        for hh in range(2):
                            hs = slice(hh * 128, (hh + 1) * 128)
                            T(lambda e, hs=hs, Bs=Bs: e.matmul(psS[1][:, hs], lhsT=Bs[5][:, hs], rhs=Bs[4][:, hs], start=True, stop=True),
                              [rBs[5], rBs[4]], [rpsS[1]])
                        for hh in range(2):
                            hs = slice(hh * 128, (hh + 1) * 128)
                            V(lambda e, hs=hs, Bs=Bs: e.tensor_tensor(out=Bs[6][:, hs], in0=psS[1][:, hs], in1=hgmb[:], op=ALU.mult),
                              [rpsS[1], rhgmb], [rBs[6]])
                        for hh in range(2):
                            hs = slice(hh * 128, (hh + 1) * 128)
                            T(lambda e, hs=hs, hh=hh, Bs=Bs: e.matmul(psO[hh][:, 0:128], lhsT=Bs[6][:, hs], rhs=Bs[0][:, hs],
                                                                      start=True, stop=False), [rBs[6], rBs[0]], [rpsO[hh]])
                            T(lambda e, hh=hh, cur=cur: e.matmul(psO[hh][:, 0:128], lhsT=qTA[:, hh, :], rhs=Sbf[:, cur, hh, :],
                                                                 start=False, stop=False), [rqTA, rSbf[cur][hh]], [rpsO[hh]])
                        for hh in range(2):
                            hs = slice(hh * 128, (hh + 1) * 128)
                            T(lambda e, hs=hs, Bs=Bs: e.matmul(psS[0][:, hs], lhsT=Bs[3][0:64, hs], rhs=Bs[0][0:64, hs],
                                                               start=True, stop=True), [rBs[3], rBs[0]], [rpsS[0], rrow])
                        for hh in range(2):
                            hs = slice(hh * 128, (hh + 1) * 128)
                            V(lambda e, hs=hs, hh=hh, dsl=dsl: e.scalar_tensor_tensor(out=S32[:, hh, :], in0=S32[:, hh, :],
                                                                                      scalar=dsl[:, 2 * hh:2 * hh + 1], in1=psS[0][:, hs],
                                                                                      op0=ALU.mult, op1=ALU.add),
                              [rS32[hh], rds, rpsS[0]], [rS32[hh]])
                            G(lambda e, hh=hh, nxtb=nxtb: e.tensor_copy(out=Sbf[:, nxtb, hh, :], in_=S32[:, hh, :]),
                              [rS32[hh]], [rSbf[nxtb][hh]])
                        for hh in range(2):
                            T(lambda e, hh=hh, nxtb=nxtb: e.matmul(psO[hh][:, 0:128], lhsT=qTB[:, hh, :], rhs=Sbf[:, nxtb, hh, :],
                                                                   start=False, stop=True), [rqTB, rSbf[nxtb][hh]], [rpsO[hh], rrow])
                        for hh in range(2):
                            hs = slice(hh * 128, (hh + 1) * 128)
                            T(lambda e, hs=hs, Bs=Bs: e.matmul(psS[0][:, hs], lhsT=Bs[3][64:128, hs], rhs=Bs[0][64:128, hs],
                                                               start=True, stop=True), [rBs[3], rBs[0]], [rpsS[0], rrow])
                        for hh in range(2):
                            hs = slice(hh * 128, (hh + 1) * 128)
                            V(lambda e, hs=hs, hh=hh, dsl=dsl: e.scalar_tensor_tensor(out=S32[:, hh, :], in0=S32[:, hh, :],
                                                                                      scalar=dsl[:, 2 * hh + 1:2 * hh + 2], in1=psS[0][:, hs],
                                                                                      op0=ALU.mult, op1=ALU.add),
                              [rS32[hh], rds, rpsS[0]], [rS32[hh]])
                        for hh in range(2):
                            G(lambda e, hh=hh, nxtb=nxtb: e.tensor_copy(out=Sbf[:, nxtb, hh, :], in_=S32[:, hh, :]),
                              [rS32[hh]], [rSbf[nxtb][hh]])
                        ssl = sso4[:, 2 * (i % 2):2 * (i % 2) + 2]
                        r_sso = r_sso2[i % 2]
                        for hh in range(2):
                            A(lambda e, hh=hh, Fs=Fs, ssl=ssl: e.activation(out=Fs[7][:, 0:128], in_=psO[hh][:, 0:128], func=AF.Square,
                                                                            accum_out=ssl[:, hh:hh + 1]), [rpsO[hh]], [rFs[7], r_sso])
                        A(lambda e, ssl=ssl: e.activation(out=ssl, in_=ssl, func=AF.Ln, scale=1.0 / 128, bias=EPS), [r_sso], [r_sso])
                        A(lambda e, ssl=ssl: e.activation(out=ssl, in_=ssl, func=AF.Exp, scale=-0.5), [r_sso], [r_sso])
                        for hh in range(2):
                            hs = slice(hh * 128, (hh + 1) * 128)
                            V(lambda e, hh=hh, hs=hs, Fs=Fs, ssl=ssl: e.scalar_tensor_tensor(out=Fs[8][:, hs], in0=psO[hh][:, 0:128],
                                                                                             scalar=ssl[:, hh:hh + 1], in1=go_bc[:],
                                                                                             op0=ALU.mult, op1=ALU.mult),
                              [rpsO[hh], r_sso, rgains], [rFs[8]])
                        G(lambda e, Fs=Fs, sl=sl, sz=sz: e.tensor_tensor(out=y_tok[:, sl, :], in0=Fs[8], in1=sz[:, sl, :], op=ALU.mult),
                          [rFs[8], rsz[sl]], [ry[sl]])
                        outproj_tile(i, sl, last, obanks=[(psS[0], rpsS[0]), (psS[0], rpsS[0])])
            if not glist and last_layer:
                for i in range(NT):
                    out_dmas.append(DMA("sync", lambda e, i=i: e.dma_start(out=out_d[i * 128:(i + 1) * 128, :], in_=x_tok[:, i, :]),
                                        r=[rx[i]]))
        if SCHED:
            if SCHED2:
                P.schedule2(SDELTA)
            else:
                P.schedule()
        P.emit(st, out_dmas)
    build_nc.stats = P.stats
    return nc


_CACHE = {}


def _get_nc(layers, groups):
    key = (tuple(layers), tuple(groups))
    if key not in _CACHE:
        _CACHE[key] = build_nc(layers, groups)
    return _CACHE[key]


def run(inputs, layers=(0, 1), groups=ALL_GROUPS, cores=8):
    nc = _get_nc(layers, groups)
    f = lambda a: np.ascontiguousarray(np.asarray(a))
    cst = make_consts()
    shared = {k: f(inputs[k]).astype(np.float32, copy=False) for k in
              ("norm_g", "w_in", "w_out", "moba_q_norm", "moba_k_norm", "hgrn_lb_logits", "hgrn_o_norm",
               "mem_norm_g", "w_mem_kv", "mem_q_norm", "mem_k_norm")}
    x = f(inputs["x"]); mem = f(inputs["mem"]); pos = f(inputs["positions"]).astype(np.int32, copy=False)
    in_maps = []
    for b in range(cores):
        m = dict(shared)
        m["x"] = x[b]
        m["mem"] = mem[b]
        m["pos"] = pos[b].reshape(16, 128)
        m["cst"] = cst
        in_maps.append(m)
    res = run_bass_kernel_spmd(nc, in_maps, core_ids=list(range(cores)))
    return np.stack([np.asarray(r["out"]) for r in res.results], axis=0)


def kernel(x, mem, positions, norm_g, w_in, w_out, moba_q_norm, moba_k_norm, hgrn_lb_logits,
           hgrn_o_norm, mem_norm_g, w_mem_kv, mem_q_norm, mem_k_norm):
    inputs = dict(x=x, mem=mem, positions=positions, norm_g=norm_g, w_in=w_in, w_out=w_out,
                  moba_q_norm=moba_q_norm, moba_k_norm=moba_k_norm, hgrn_lb_logits=hgrn_lb_logits,
                  hgrn_o_norm=hgrn_o_norm, mem_norm_g=mem_norm_g, w_mem_kv=w_mem_kv,
                  mem_q_norm=mem_q_norm, mem_k_norm=mem_k_norm)
    return run(inputs).astype(np.float32, copy=False)
```

```python
import numpy as np
from contextlib import ExitStack
import concourse.bass as bass
import concourse.mybir as mybir
from concourse.bass_utils import run_bass_kernel_spmd

F32 = mybir.dt.float32
BF16 = mybir.dt.bfloat16
I32 = mybir.dt.int32
AF = mybir.ActivationFunctionType
ALU = mybir.AluOpType
AX = mybir.AxisListType

S = 2048
D = 1024
NT = 16
EPS = 1e-6
NCST = 576
import os as _os0
ALL_GROUPS = tuple(_os0.environ.get("ORDER", "A0,A1,H0,H1,M0,M1").split(","))
import os as _os
SCHED = _os.environ.get("SCHED", "1") == "1"
PAR1 = int(_os.environ.get("PAR1", "1"))
PAR2 = int(_os.environ.get("PAR2", "1"))
GPAR = int(_os.environ.get("GPAR", "1"))
PS3 = int(_os.environ.get("PS3", "3"))
MASKV = int(_os.environ.get("MASKV", "1"))
OPB = int(_os.environ.get("OPB", "1"))
SCHED2 = int(_os.environ.get("SCHED2", "1"))
SDELTA = float(_os.environ.get("SDELTA", "100"))
LATX = float(_os.environ.get("LATX", "180"))
PEK = float(_os.environ.get("PEK", "0.65"))
ACTK = float(_os.environ.get("ACTK", "1.0"))
DVEK = float(_os.environ.get("DVEK", "1.0"))
POOLK = float(_os.environ.get("POOLK", "1.0"))
LATS = float(_os.environ.get("LATS", "60"))
XQ = int(_os.environ.get("XQ", "0"))
GOFF = int(_os.environ.get("GOFF", "0"))
TRANS = int(_os.environ.get("TRANS", "1"))
PRUNE = int(_os.environ.get("PRUNE", "1"))


class Res:
    __slots__ = ("name", "w", "r", "excl")

    def __init__(self, name, excl=False):
        self.name = name
        self.w = None
        self.r = []
        self.excl = excl


class _Rec:
    def __init__(self):
        self.name = None
        self.args = ()
        self.kw = {}

    def __getattr__(self, name):
        def f(*a, **k):
            self.name, self.args, self.kw = name, a, k
            return self
        return f


def _free_size(ap):
    n = 1
    for d in list(ap.shape)[1:]:
        n *= int(d)
    return n


class Op:
    __slots__ = ("eng", "fn", "deps", "sdeps", "sig", "idx", "dma", "sem", "val", "i", "cost", "start")

    def __init__(self, eng, fn, deps, sdeps, dma):
        self.eng = eng
        self.fn = fn
        self.deps = deps
        self.sdeps = sdeps
        self.sig = False
        self.idx = 0
        self.dma = dma
        self.sem = None
        self.val = 0
        self.i = 0
        self.start = 0.0
        rec = _Rec()
        fn(rec)
        out = rec.kw.get("out", rec.args[0] if rec.args else None)
        n = _free_size(out) if out is not None else 64
        if dma:
            c = 2000.0 + n * int(out.shape[0]) * 4 / 120.0
        elif eng == "tensor":
            if rec.name == "transpose":
                c = 110.0
            else:
                lhsT = rec.kw.get("lhsT")
                f32 = lhsT is not None and lhsT.dtype == F32
                c = PEK * (64.0 + max(n, 64) / 2.0) * (4.0 if f32 else 1.0)
        elif eng == "scalar":
            c = ACTK * (200.0 + n / 1.2)
        elif eng == "vector":
            c = DVEK * (120.0 + n / 0.96 * (8.0 if rec.name == "reciprocal" else 1.0))
        else:
            c = POOLK * (300.0 + n / 0.5)
        self.cost = c


class Prog:
    ENGS = ["tensor", "vector", "scalar", "gpsimd", "sync"]

    def __init__(self, nc):
        self.nc = nc
        self.ops = []

    phase = None
    tok = None
    tokset = ()

    def op(self, eng, fn, reads=(), writes=(), dma=False):
        if self.phase == "gate" and self.tok is not None:
            writes = list(writes) + [self.tok]
        elif self.phase == "chain" and eng in self.tokset:
            reads = list(reads) + [self.tok]
        deps, sdeps = {}, {}

        def add(d):
            if d.dma or dma or d.eng != eng or eng != "tensor":
                deps[id(d)] = d
            else:
                sdeps[id(d)] = d
        for r in reads:
            if r.w is not None:
                add(r.w)
            if r.excl:
                for d in r.r:
                    if d.eng != eng:
                        add(d)
        for w in writes:
            if w.w is not None:
                add(w.w)
            for d in w.r:
                add(d)
        o = Op(eng, fn, list(deps.values()), list(sdeps.values()), dma)
        for r in reads:
            r.r.append(o)
        for w in writes:
            w.w = o
            w.r = []
        self.ops.append(o)
        return o

    def schedule(self):
        import heapq
        ops = self.ops
        for i, o in enumerate(ops):
            o.i = i
        succs = [[] for _ in ops]
        npred = [0] * len(ops)
        for o in ops:
            ds = o.deps + o.sdeps
            npred[o.i] = len(ds)
            for d in ds:
                succs[d.i].append(o)
        ready = [0.0] * len(ops)
        free = {e: 0.0 for e in self.ENGS}
        heap = [(0.0, o.i) for o in ops if npred[o.i] == 0]
        heapq.heapify(heap)
        done = 0
        while heap:
            t, i = heapq.heappop(heap)
            o = ops[i]
            st = max(ready[i], free[o.eng])
            if st > t + 1e-9:
                heapq.heappush(heap, (st, i))
                continue
            o.start = st
            if o.dma:
                free[o.eng] = st + 150.0
            else:
                free[o.eng] = st + o.cost
            fin = st + o.cost
            done += 1
            for sc in succs[i]:
                lat = 60.0 if (sc.eng == o.eng and not o.dma) else 180.0
                if fin + lat > ready[sc.i]:
                    ready[sc.i] = fin + lat
                npred[sc.i] -= 1
                if npred[sc.i] == 0:
                    heapq.heappush(heap, (max(ready[sc.i], free[sc.eng]), sc.i))
        assert done == len(ops), (done, len(ops))
        self.ops = sorted(ops, key=lambda o: (o.start, o.i))
        self.est_ns = max(o.start + o.cost for o in ops)

    def schedule2(self, delta=120.0):
        ops = self.ops
        n = len(ops)
        for i, o in enumerate(ops):
            o.i = i
        succs = [[] for _ in ops]
        npred = [0] * n
        for o in ops:
            ds = o.deps + o.sdeps
            npred[o.i] = len(ds)
            for d in ds:
                succs[d.i].append(o)
        blev = [0.0] * n
        for o in reversed(ops):
            b = 0.0
            for sc in succs[o.i]:
                lat = LATS if (sc.eng == o.eng and not o.dma) else LATX
                v = lat + blev[sc.i]
                if v > b:
                    b = v
            blev[o.i] = b + o.cost
        ready = [0.0] * n
        free = {e: 0.0 for e in self.ENGS}
        rsets = {e: [] for e in self.ENGS}
        for o in ops:
            if npred[o.i] == 0:
                rsets[o.eng].append(o.i)
        done = 0
        while done < n:
            best_e, best_t = None, 1e30
            for e in self.ENGS:
                rs = rsets[e]
                if not rs:
                    continue
                t = min(ready[i] for i in rs)
                if t < free[e]:
                    t = free[e]
                if t < best_t:
                    best_t, best_e = t, e
            e = best_e
            rs = rsets[e]
            lim = best_t + delta
            pick, pb = -1, -1.0
            for i in rs:
                if ready[i] <= lim and blev[i] > pb:
                    pb, pick = blev[i], i
            rs.remove(pick)
            o = ops[pick]
            st = max(ready[pick], free[e])
            o.start = st
            free[e] = st + (150.0 if o.dma else o.cost)
            fin = st + o.cost
            done += 1
            for sc in succs[pick]:
                lat = LATS if (sc.eng == o.eng and not o.dma) else LATX
                if fin + lat > ready[sc.i]:
                    ready[sc.i] = fin + lat
                npred[sc.i] -= 1
                if npred[sc.i] == 0:
                    rsets[sc.eng].append(sc.i)
        self.ops = sorted(ops, key=lambda o: (o.start, o.i))
        self.est_ns = max(o.start + o.cost for o in ops)

    def emit(self, stack, final_deps, ndma_sems=8):
        nc = self.nc
        if PRUNE:
            pos = {id(o): k for k, o in enumerate(self.ops)}
            for o in self.ops:
                best = {}
                keep = []
                for d in o.deps:
                    if d.dma:
                        keep.append(d)
                    elif d.eng not in best or pos[id(d)] > pos[id(best[d.eng])]:
                        best[d.eng] = d
                o.deps = keep + list(best.values())
        for o in self.ops:
            for d in o.deps:
                d.sig = True
        for d in final_deps:
            d.sig = True
        sems = {e: stack.enter_context(nc.semaphore("s_" + e)) for e in self.ENGS}
        cnt = {e: 0 for e in self.ENGS}
        pools, pool_i, pre_wait = {}, {}, {}
        for o in self.ops:
            if o.dma:
                if o.eng not in pools:
                    pools[o.eng] = [[stack.enter_context(nc.semaphore("d_%s_%d" % (o.eng, i))), 0]
                                    for i in range(ndma_sems)]
                    pool_i[o.eng] = 0
                p = pools[o.eng][pool_i[o.eng] % ndma_sems]
                pool_i[o.eng] += 1
                if p[1] > 0:
                    pre_wait[id(o)] = (p[0], p[1])
                p[1] += 16
                o.sem = p[0]
                o.val = p[1]
            elif o.sig:
                cnt[o.eng] += 1
                o.idx = cnt[o.eng]
        per = {e: [o for o in self.ops if o.eng == e] for e in self.ENGS}
        self.stats = {e: len(per[e]) for e in self.ENGS}
        known = {e: {} for e in self.ENGS}
        kn = {}
        plan = {}
        nw = 0

        def semkey(d):
            return (d.sem, d.val) if d.dma else (sems[d.eng], d.idx)

        for o in self.ops:
            kd = known[o.eng]
            ws = []
            for d in o.deps:
                sm, val = semkey(d)
                if kd.get(id(sm), (None, 0))[1] < val:
                    ws.append((sm, val))
                    kd[id(sm)] = (sm, val)
                if TRANS:
                    for k2, (s2, v2) in kn[id(d)].items():
                        if kd.get(k2, (None, 0))[1] < v2:
                            kd[k2] = (s2, v2)
            if o.dma:
                pw = pre_wait.get(id(o))
                if pw and kd.get(id(pw[0]), (None, 0))[1] < pw[1]:
                    ws.append(pw)
                    kd[id(pw[0])] = pw
            plan[id(o)] = ws
            nw += len(ws)
            if o.dma or o.sig:
                mine = dict(kd)
                sm, val = semkey(o)
                mine[id(sm)] = (sm, val)
                kn[id(o)] = mine
        fin_w = []
        kd = known["sync"]
        for d in final_deps:
            sm, val = semkey(d)
            if kd.get(id(sm), (None, 0))[1] < val:
                fin_w.append((sm, val))
                kd[id(sm)] = (sm, val)
        self.stats["waits"] = nw
        self.stats["sigs"] = {e: sum(1 for o in per[e] if o.sig and not o.dma) for e in self.ENGS}
        block = stack.enter_context(nc.Block())

        def mk(e):
            def body(engobj):
                for o in per[e]:
                    for sm, val in plan[id(o)]:
                        engobj.wait_ge(sm, val)
                    if o.dma:
                        o.fn(engobj).then_inc(o.sem, 16)
                    else:
                        ins = o.fn(engobj)
                        if o.sig:
                            ins.then_inc(sems[e], 1)
                if e == "sync":
                    for sm, val in fin_w:
                        engobj.wait_ge(sm, val)
            return body

        block.tensor(mk("tensor"))
        block.vector(mk("vector"))
        block.scalar(mk("scalar"))
        block.gpsimd(mk("gpsimd"))
        block.sync(mk("sync"))


def make_consts():
    c = np.zeros((128, NCST), np.float32)
    i = np.arange(128)
    c[:, 0:128] = np.eye(128)
    c[:, 128:256] = (i[None, :] >= i[:, None])
    same = (i[:, None] // 64) == (i[None, :] // 64)
    c[:, 256:384] = same & (i[:, None] <= i[None, :])
    c[:, 384:512] = same & (i[:, None] > i[None, :])
    c[:, 512] = i < 64
    c[:, 513] = i >= 64
    c[:, 514] = 1.0
    f64 = 500000.0 ** (-np.arange(8, dtype=np.float64) / 8.0)
    f = f64.astype(np.float32)
    flo = (f64 - f.astype(np.float64)).astype(np.float32)
    c[:, 515:523] = f[None, :]
    c[:, 523:531] = f[None, :]
    c[:, 547:555] = flo[None, :]
    c[:, 555:563] = flo[None, :]
    c[:, 531:539] = 0.0
    c[:, 539:547] = np.pi / 2
    return c


def build_nc(layers=(0, 1), groups=ALL_GROUPS):
    nc = bass.Bass("TRN2", target_bir_lowering=False)

    def din(name, shape, d=F32):
        return nc.dram_tensor(name, shape, d, kind="ExternalInput").ap()

    x_d = din("x", [S, D])
    mem_d = din("mem", [256, D])
    pos_d = din("pos", [16, 128], I32)
    ng_d = din("norm_g", [2, D])
    win_d = din("w_in", [2, D, 5120])
    wout_d = din("w_out", [2, 1536, D])
    gq_d = din("moba_q_norm", [2, 64])
    gk_d = din("moba_k_norm", [2, 64])
    lbl_d = din("hgrn_lb_logits", [2, 512])
    go_d = din("hgrn_o_norm", [2, 128])
    mng_d = din("mem_norm_g", [2, D])
    wkv_d = din("w_mem_kv", [2, D, 1024])
    gmq_d = din("mem_q_norm", [2, 128])
    gmk_d = din("mem_k_norm", [2, 128])
    cst_d = din("cst", [128, NCST])
    out_d = nc.dram_tensor("out", [S, D], F32, kind="ExternalOutput").ap()

    P = Prog(nc)
    if _os.environ.get("TOKR"):
        P.tok = Res("tok")
        P.tokset = tuple(_os.environ["TOKR"].split(","))
    with ExitStack() as st:
        def sb(name, shape, dt=F32):
            return st.enter_context(nc.sbuf_tensor("sb_" + name, shape, dt))

        def ps(name, shape, dt=F32):
            return st.enter_context(nc.psum_tensor("pp_" + name, shape, dt))

        def T(fn, r=(), w=()):
            return P.op("tensor", fn, r, w)

        def V(fn, r=(), w=()):
            return P.op("vector", fn, r, w)

        def A(fn, r=(), w=()):
            return P.op("scalar", fn, r, w)

        def G(fn, r=(), w=()):
            return P.op("gpsimd", fn, r, w)

        def DMA(q, fn, r=(), w=()):
            return P.op(q, fn, r, w, dma=True)

        x_tok = sb("x_tok", [128, NT, D]); rx = [Res("x%d" % i) for i in range(NT)]
        hT = sb("hT", [128, 8, S], BF16); rhT = [Res("hT%d" % i) for i in range(NT)]
        cst = sb("cst", [128, NCST]); rcst = Res("cst")
        idb = sb("idb", [128, 128], BF16); ridb = Res("idb")
        trib = sb("trib", [128, 128], BF16); rtrib = Res("trib")
        hgmb = sb("hgmb", [128, 128], BF16); rhgmb = Res("hgmb")
        gq_bc = sb("gq_bc", [128, 64]); gk_bc = sb("gk_bc", [128, 64]); go_bc = sb("go_bc", [128, 128])
        gmq_bc = sb("gmq_bc", [128, 128]); gmk_bc = sb("gmk_bc", [128, 128]); rgains = Res("gains")
        cs = sb("cs", [128, NT, 16]); sn = sb("sn", [128, NT, 16]); rrope = Res("rope")
        wb = [sb("wb%d" % i, [128, 8, 512], BF16) for i in range(3)]; rwb = [[Res("wb%da" % i), Res("wb%db" % i)] for i in range(3)]
        wo = sb("wo", [128, 2, D], BF16); rwo = Res("wo")
        Fp = [sb("F%d" % i, [128, 256]) for i in range(9)]; rF = [Res("F%d" % i) for i in range(9)]
        Bp = [sb("B%d" % i, [128, 256], BF16) for i in range(7)]; rB = [Res("B%d" % i) for i in range(7)]
        szs = [sb("sz%d" % k, [128, 4, 256]) for k in range(2)]; rszs = [[Res("sz%d_%d" % (k, i)) for i in range(4)] for k in range(2)]
        y_tok = sb("y_tok", [128, 4, 256], BF16); ry = [Res("y%d" % i) for i in range(4)]
        yT = [sb("yT%d" % i, [128, 256], BF16) for i in range(2)]; ryT = [Res("yT%d" % i) for i in range(2)]
        pT = [sb("pT%d" % i, [128, 512], BF16) for i in range(4)]; rpT = [Res("pT%d" % i) for i in range(4)]
        hbs = [sb("hb%d" % i, [128, D], BF16) for i in range(2)]; rhbs = [Res("hb0"), Res("hb1")]
        small = sb("small", [128, 128]); rsm = {}

        def sm(name, a, n):
            rsm[name] = Res("sm_" + name)
            return small[:, a:a + n], rsm[name]
        ss16, r_ss16 = sm("ss16", 0, 16)
        t16, r_t16 = sm("t16", 16, 16)
        rstd16, r_rstd16 = sm("rstd16", 32, 16)
        ss4, r_ss4 = sm("ss4", 48, 4)
        t4, r_t4 = sm("t4", 52, 4)
        rs4, r_rs4 = sm("rs4", 56, 4)
        rden, r_rden = sm("rden", 60, 4)
        dec4, r_dec4 = sm("dec4", 64, 4)
        sso, r_sso = sm("sso", 68, 2)
        to2, r_to2 = sm("to2", 70, 2)
        rso, r_rso = sm("rso", 72, 2)
        gm = sb("gm", [128, 4, 8]); rgm = Res("gm")
        top8 = sb("top8", [128, 4, 8]); rtop8 = Res("top8")
        selb = sb("selb", [128, 4, 8]); rselb = Res("selb")
        kT = sb("kT", [128, 4, S], BF16); rkT = [Res("kT%d" % i) for i in range(NT)]
        v_flat = sb("v_aug", [128, NT * 4 * 65], BF16); rv = [Res("v%d" % i) for i in range(NT)]
        v_aug = v_flat[:].rearrange("p (a b c) -> p a b c", a=NT, b=4)
        stage = v_flat[:, 0:2048].bitcast(F32); rstage = Res("stage")
        g_bcv = v_flat[:, 2048:4096].bitcast(F32); rg_bc = Res("g_bc")
        qTs = [sb("qT%d" % i, [128, 2048], BF16) for i in range(2)]; rqTs = [Res("qT0"), Res("qT1")]
        lbv = qTs[1][:, 0:2048].bitcast(F32); rlb = rqTs[1]
        lb_g = lbv[:, 0:256]; oml_g = lbv[:, 256:512]
        k_aug = sb("k_aug", [128, 4, 72], BF16); rk_aug = Res("k_aug")
        q_aug = sb("q_aug", [128, 4, 72], BF16); rq_aug = Res("q_aug")
        kmT32 = sb("kmT32", [128, 2, 2, 8]); rkm = Res("kmT32")
        S32 = sb("S32", [128, 2, 128]); rS32 = [Res("S32_0"), Res("S32_1")]
        Sbf = sb("Sbf", [128, 2, 2, 128], BF16); rSbf = [[Res("Sbf00"), Res("Sbf01")], [Res("Sbf10"), Res("Sbf11")]]
        qTA = sb("qTA", [128, 2, 128], BF16); qTB = sb("qTB", [128, 2, 128], BF16)
        rqTA = Res("qTA"); rqTB = Res("qTB")
        memT = sb("memT", [128, 8, 256], BF16); rmemT = Res("memT")
        kmT = sb("kmT", [128, 2, 256], BF16); rkmT = Res("kmT")
        vm_aug = sb("vm_aug", [128, 2, 2, 129], BF16); rvm = Res("vm")
        psA = [ps("psA%d" % i, [128, 512]) for i in range(2)]; rpsA = [Res("psA0", True), Res("psA1", True)]
        psT = ps("psT", [128, 1024], BF16); rpsT = Res("psT", True)
        psG = ps("psG", [128, 512]); rpsG = Res("psG", True)
        psS = [ps("psS%d" % i, [128, 512]) for i in range(2)]; rpsS = [Res("psS0", True), Res("psS1", True)]
        psO = [ps("psO%d" % i, [128, 512]) for i in range(2)]; rpsO = [Res("psO0", True), Res("psO1", True)]

        ctr = {"pa": 0, "w": 0, "ev": 0, "ps": 0, "pt": 0, "yt": 0, "mk": 0}

        def nxt(k, n):
            v = ctr[k] % n
            ctr[k] += 1
            return v

        def evac(fn_v, fn_a, r, w):
            if nxt("ev", 2) == 0:
                return A(fn_a, r, w)
            return V(fn_v, r, w)

        DMA("sync", lambda e: e.dma_start(out=cst[:], in_=cst_d), w=[rcst])
        V(lambda e: e.tensor_copy(out=idb[:], in_=cst[:, 0:128]), [rcst], [ridb])
        V(lambda e: e.tensor_copy(out=trib[:], in_=cst[:, 128:256]), [rcst], [rtrib])
        V(lambda e: e.tensor_copy(out=hgmb[:], in_=cst[:, 256:384]), [rcst], [rhgmb])
        ident32 = cst[:, 0:128]
        G(lambda e: e.memset(vm_aug[:, :, :, 128:129], 1.0), w=[rvm])
        G(lambda e: e.memset(qTA[:], 0.0), w=[rqTA])
        G(lambda e: e.memset(qTB[:], 0.0), w=[rqTB])
        nI = sb("nI", [128, 256], I32); rnI = Res("nI")
        posi = nI[0:16, 0:128]; rposi = rnI
        posf = Fp[8][0:16, 0:128]; rposf = rF[8]
        DMA("sync", lambda e: e.dma_start(out=posi, in_=pos_d), w=[rposi])
        V(lambda e: e.tensor_copy(out=posf, in_=posi), [rposi], [rposf])
        T(lambda e: e.matmul(psG[:, 0:16], lhsT=posf, rhs=cst[0:16, 0:16], start=True, stop=True), [rposf, rcst], [rpsG])
        post, r_post = sm("post", 80, 16)
        V(lambda e: e.tensor_copy(out=post, in_=psG[:, 0:16]), [rpsG], [r_post])
        ang = Fp[0][:, 0:256].rearrange("p (i j) -> p i j", i=NT)
        V(lambda e: e.tensor_tensor(out=ang, in0=post.unsqueeze(2).to_broadcast([128, NT, 16]),
                                    in1=cst[:, 515:531].unsqueeze(1).to_broadcast([128, NT, 16]), op=ALU.mult),
          [r_post, rcst], [rF[0]])
        ang_lo = Fp[1][:, 0:256].rearrange("p (i j) -> p i j", i=NT)
        V(lambda e: e.tensor_tensor(out=ang_lo, in0=post.unsqueeze(2).to_broadcast([128, NT, 16]),
                                    in1=cst[:, 547:563].unsqueeze(1).to_broadcast([128, NT, 16]), op=ALU.mult),
          [r_post, rcst], [rF[1]])
        V(lambda e: e.tensor_tensor(out=ang, in0=ang, in1=ang_lo, op=ALU.add), [rF[0], rF[1]], [rF[0]])
        V(lambda e: e.tensor_tensor(out=ang, in0=ang, in1=cst[:, 531:547].unsqueeze(1).to_broadcast([128, NT, 16]),
                                    op=ALU.add), [rF[0], rcst], [rF[0]])
        V(lambda e: e.tensor_scalar(out=Fp[1][:], in0=Fp[0][:], scalar1=float(1.0 / (2 * np.pi)), scalar2=None,
                                    op0=ALU.mult), [rF[0]], [rF[1]])
        V(lambda e: e.tensor_copy(out=nI[:], in_=Fp[1][:]), [rF[1]], [rnI])
        V(lambda e: e.tensor_copy(out=Fp[1][:], in_=nI[:]), [rnI], [rF[1]])
        C1 = 6.28125
        C2 = float(2 * np.pi - 6.28125)
        V(lambda e: e.scalar_tensor_tensor(out=Fp[2][:], in0=Fp[1][:], scalar=-C1, in1=Fp[0][:],
                                           op0=ALU.mult, op1=ALU.add), [rF[1], rF[0]], [rF[2]])
        V(lambda e: e.scalar_tensor_tensor(out=Fp[2][:], in0=Fp[1][:], scalar=-C2, in1=Fp[2][:],
                                           op0=ALU.mult, op1=ALU.add), [rF[1], rF[2]], [rF[2]])
        V(lambda e: e.tensor_scalar(out=Fp[2][:], in0=Fp[2][:], scalar1=float(np.pi), scalar2=float(-np.pi),
                                    op0=ALU.min, op1=ALU.max), [rF[2]], [rF[2]])
        A(lambda e: e.activation(out=Fp[3][:], in_=Fp[2][:], func=AF.Sin), [rF[2]], [rF[3]])
        sc = Fp[3][:, 0:256].rearrange("p (i j) -> p i j", i=NT)
        V(lambda e: e.tensor_copy(out=cs[:, :, 0:8], in_=sc[:, :, 8:16]), [rF[3]], [rrope])
        V(lambda e: e.tensor_copy(out=cs[:, :, 8:16], in_=sc[:, :, 8:16]), [rF[3]], [rrope])
        V(lambda e: e.tensor_scalar(out=sn[:, :, 0:8], in0=sc[:, :, 0:8], scalar1=-1.0, scalar2=None, op0=ALU.mult),
          [rF[3]], [rrope])
        V(lambda e: e.tensor_copy(out=sn[:, :, 8:16], in_=sc[:, :, 0:8]), [rF[3]], [rrope])
        def load_w(dst, rdst, src_ap):
            return DMA("gpsimd", lambda e: e.dma_start(out=dst, in_=src_ap), w=[rdst])

        def w_in_cols(l, c0, n):
            return win_d[l].rearrange("(c p) n -> p c n", p=128)[:, :, c0:c0 + n]

        psGb = psG[:].bitcast(BF16)

        def rms_to_T(src_tile, rsrc, gain, rgain, dstT, rdst, col0, ssc, r_ssc, tsc, r_tsc, rsc, r_rsc, k):
            hb = hbs[k % 2]; rhb = rhbs[k % 2]
            pst, rpst = (psT, rpsT) if k % 2 == 0 else (psGb, rpsG)
            A(lambda e: e.activation(out=hb[:], in_=src_tile, func=AF.Square, accum_out=ssc[:, k:k + 1]),
              [rsrc], [rhb, r_ssc])
            A(lambda e: e.activation(out=tsc[:, k:k + 1], in_=ssc[:, k:k + 1], func=AF.Ln, scale=1.0 / D, bias=EPS),
              [r_ssc], [r_tsc])
            A(lambda e: e.activation(out=rsc[:, k:k + 1], in_=tsc[:, k:k + 1], func=AF.Exp, scale=-0.5),
              [r_tsc], [r_rsc])
            V(lambda e: e.scalar_tensor_tensor(out=hb[:], in0=src_tile, scalar=rsc[:, k:k + 1], in1=gain,
                                               op0=ALU.mult, op1=ALU.mult), [rsrc, r_rsc, rgain], [rhb])
            for c in range(8):
                T(lambda e, c=c: e.transpose(out=pst[:, c * 128:(c + 1) * 128], in_=hb[:, c * 128:(c + 1) * 128],
                                             identity=idb[:]), [rhb, ridb], [rpst])
            src3 = pst[:, 0:1024].rearrange("p (c t) -> p c t", c=8)
            evac(lambda e: e.tensor_copy(out=dstT[:, :, col0:col0 + 128], in_=src3),
                 lambda e: e.copy(out=dstT[:, :, col0:col0 + 128], in_=src3), [rpst], [rdst])

        def proj(lhs_cols, rlhs, wt, rwt, ncols=512, lhsT_src=None):
            b = nxt("pa", 2)
            src = hT if lhsT_src is None else lhsT_src
            for c in range(8):
                T(lambda e, c=c: e.matmul(psA[b][:, 0:ncols], lhsT=src[:, c, lhs_cols:lhs_cols + 128],
                                          rhs=wt[:, c, 0:ncols], start=(c == 0), stop=(c == 7)),
                  [rlhs] + list(rwt), [rpsA[b]])
            return psA[b], rpsA[b]

        def headnorm(src_ps, rps, H, Dh, gain, outF, routF, tmpA, rtmpA, tmpB, rtmpB):
            n = H * Dh
            A(lambda e: e.activation(out=tmpA[:, 0:n], in_=src_ps, func=AF.Square), [rps], [rtmpA])
            V(lambda e: e.tensor_reduce(out=ss4[:, 0:H], in_=tmpA[:, 0:n].rearrange("p (h d) -> p h d", h=H),
                                        axis=AX.X, op=ALU.add), [rtmpA], [r_ss4])
            A(lambda e: e.activation(out=t4[:, 0:H], in_=ss4[:, 0:H], func=AF.Ln, scale=1.0 / Dh, bias=EPS),
              [r_ss4], [r_t4])
            A(lambda e: e.activation(out=rs4[:, 0:H], in_=t4[:, 0:H], func=AF.Exp, scale=-0.5), [r_t4], [r_rs4])
            V(lambda e: e.tensor_tensor(out=tmpB[:, 0:n].rearrange("p (h d) -> p h d", h=H),
                                        in0=src_ps.rearrange("p (h d) -> p h d", h=H),
                                        in1=rs4[:, 0:H].unsqueeze(2).to_broadcast([128, H, Dh]), op=ALU.mult),
              [rps, r_rs4], [rtmpB])
            (G if GOFF else V)(lambda e: e.tensor_tensor(out=outF[:, 0:n].rearrange("p (h d) -> p h d", h=H),
                                                         in0=tmpB[:, 0:n].rearrange("p (h d) -> p h d", h=H),
                                                         in1=gain.unsqueeze(1).to_broadcast([128, H, Dh]), op=ALU.mult),
                               [rtmpB, rgains], [routF])

        def rope(Fx, rFx, i, tR, rtR):
            x3 = Fx[:, 0:256].rearrange("p (h d) -> p h d", h=4)
            a3 = tR[:, 0:64].rearrange("p (h d) -> p h d", h=4)
            b3 = tR[:, 64:128].rearrange("p (h d) -> p h d", h=4)
            rtA = rtR
            rtB = rtR
            G(lambda e: e.tensor_tensor(out=a3, in0=x3[:, :, 0:16], in1=cs[:, i, :].unsqueeze(1).to_broadcast([128, 4, 16]),
                                        op=ALU.mult), [rFx, rrope], [rtA])
            G(lambda e: e.tensor_tensor(out=b3[:, :, 0:8], in0=x3[:, :, 8:16],
                                        in1=sn[:, i, 0:8].unsqueeze(1).to_broadcast([128, 4, 8]), op=ALU.mult),
              [rFx, rrope], [rtB])
            G(lambda e: e.tensor_tensor(out=b3[:, :, 8:16], in0=x3[:, :, 0:8],
                                        in1=sn[:, i, 8:16].unsqueeze(1).to_broadcast([128, 4, 8]), op=ALU.mult),
              [rFx, rrope], [rtB])
            G(lambda e: e.tensor_tensor(out=x3[:, :, 0:16], in0=a3, in1=b3, op=ALU.add), [rtA, rtB], [rFx])

        def silu_ps(dst, rdst, src_ps, rps, defer=False):
            A(lambda e: e.activation(out=dst, in_=src_ps, func=AF.Exp, scale=-1.0), [rps], [rdst])
            A(lambda e: e.activation(out=dst, in_=dst, func=AF.Ln, bias=1.0), [rdst], [rdst])
            A(lambda e: e.activation(out=dst, in_=dst, func=AF.Exp, scale=-1.0), [rdst], [rdst])

            def fin():
                V(lambda e: e.tensor_tensor(out=dst, in0=src_ps, in1=dst, op=ALU.mult), [rps, rdst], [rdst])
            if defer:
                return fin
            fin()

        SC = [((Fp[0], rF[0]), (Fp[1], rF[1]), (Fp[2], rF[2]), (Fp[3], rF[3])),
              ((Fp[4], rF[4]), (Fp[6], rF[6]), (Fp[7], rF[7]), (Fp[8], rF[8]))]
        AUG = [(k_aug, rk_aug), (q_aug, rq_aug)]
        if _os.environ.get("NOPAR", "0") == "1":
            SC[1] = SC[0]
        if _os.environ.get("NOAUG", "0") == "1":
            AUG[1] = AUG[0]

        dec8, _r = sm("dec8", 96, 8)
        r_dec8 = [Res("dec8a"), Res("dec8b")]
        sso4, _r2 = sm("sso4", 104, 4)
        r_sso2 = [Res("ssoA"), Res("ssoB")]
        dummy = sb("dummy", [128, 8])
        kflat = kT[:].rearrange("p h t -> p (h t)")
        kf32 = kflat[:, 0:4608].bitcast(F32)
        Fq = [kf32[:, k * 256:(k + 1) * 256] for k in range(9)]
        rFq = [Res("Fq%d" % k) for k in range(9)]
        Bq = [kflat[:, 4608 + k * 256:4608 + (k + 1) * 256] for k in range(7)]
        rBq = [Res("Bq%d" % k) for k in range(7)]
        HSETS = [([t[:] for t in Fp], rF, [t[:] for t in Bp], rB), (Fq, rFq, Bq, rBq)]

        def barrier():
            G(lambda e: e.memset(dummy[:], 0.0), w=list(rkT) + rFq + rBq + list(rv) + [rstage, rg_bc])

        rrow = Res("pe_rowfence")

        out_dmas = []

        def outproj_tile(i, r, last, obanks=None):
            yb = nxt("yt", 2)
            for pp in range(2):
                T(lambda e, pp=pp: e.transpose(out=psT[:, pp * 128:(pp + 1) * 128], in_=y_tok[:, r, pp * 128:(pp + 1) * 128],
                                               identity=idb[:]), [ry[r], ridb], [rpsT])
            evac(lambda e: e.tensor_copy(out=yT[yb][:], in_=psT[:, 0:256]),
                 lambda e: e.copy(out=yT[yb][:], in_=psT[:, 0:256]), [rpsT], [ryT[yb]])
            for half in range(2):
                if obanks is None:
                    b = nxt("pa", 2)
                    pso, rpso = psA[b], rpsA[b]
                else:
                    pso, rpso = obanks[half]
                for pp in range(2):
                    T(lambda e, pp=pp, half=half, pso=pso: e.matmul(pso[:, 0:512], lhsT=yT[yb][:, pp * 128:(pp + 1) * 128],
                                                                    rhs=wo[:, pp, half * 512:(half + 1) * 512],
                                                                    start=(pp == 0), stop=(pp == 1)),
                      [ryT[yb], rwo], [rpso])
                V(lambda e, half=half, pso=pso: e.tensor_tensor(out=x_tok[:, i, half * 512:(half + 1) * 512], in0=pso[:, 0:512],
                                                                in1=x_tok[:, i, half * 512:(half + 1) * 512], op=ALU.add),
                  [rpso, rx[i]], [rx[i]])
            if last:
                out_dmas.append(DMA("sync", lambda e: e.dma_start(out=out_d[i * 128:(i + 1) * 128, :], in_=x_tok[:, i, :]),
                                    r=[rx[i]]))

        def attn_chunk(Q, H, Dh, KP, scale, key_tiles, causal, kTsrc, rkTsrc, vsrc, rvsrc, qview, rqT, sz, rsz):
            DA = Dh + 1
            for h in range(H):
                if Dh == 64:
                    o_b = h % 2
                    banks = [o_b, o_b, o_b, o_b]
                    offs = [0, DA, 2 * DA, 3 * DA]
                else:
                    banks = [0, 0, 1, 1]
                    offs = [0, DA, 0, DA]
                started = set()
                kts = key_tiles(Q)
                for kt in kts:
                    j = kt - 4 * Q if causal else -1
                    q0 = max(j, 0) * 128
                    N = 512 - q0
                    sbk = nxt("ps", PS3)
                    pss, rpss = ((psS[0], rpsS[0]), (psS[1], rpsS[1]), (psG, rpsG))[sbk]
                    T(lambda e, kt=kt, h=h, q0=q0, N=N, pss=pss: e.matmul(
                        pss[:, 0:N], lhsT=kTsrc(h, kt), rhs=qview(h)[:, q0:512], start=True, stop=True),
                      [rkTsrc(kt), rqT], [rpss])
                    pb = nxt("pt", 4)
                    A(lambda e, N=N, pss=pss, pb=pb: e.activation(out=pT[pb][:, 0:N], in_=pss[:, 0:N], func=AF.Exp,
                                                                  scale=scale), [rpss], [rpT[pb]])
                    if j >= 0:
                        (V if (MASKV and nxt("mk", 2) == 0) else G)(
                            lambda e, pb=pb: e.tensor_tensor(out=pT[pb][:, 0:128], in0=pT[pb][:, 0:128], in1=trib[:],
                                                             op=ALU.mult), [rpT[pb], rtrib], [rpT[pb]])
                    for r in range(max(j, 0), 4):
                        bk = banks[r]
                        first = bk not in started
                        started.add(bk)
                        T(lambda e, r=r, kt=kt, h=h, q0=q0, pb=pb, bk=bk, first=first: e.matmul(
                            psO[bk][:, offs[r]:offs[r] + DA], lhsT=pT[pb][:, r * 128 - q0:r * 128 - q0 + 128],
                            rhs=vsrc(h, kt), start=first, stop=False, skip_group_check=True),
                          [rpT[pb], rvsrc(kt)], [rpsO[bk]])
                for bk0 in sorted(set(banks)):
                    rs_ = [r for r in range(4) if banks[r] == bk0]
                    nr = len(rs_)
                    V(lambda e, bk0=bk0, rs_=rs_, nr=nr: e.reciprocal(
                        out=rden[:, rs_[0]:rs_[0] + nr],
                        in_=psO[bk0][:, 0:nr * DA].rearrange("p (r c) -> p r c", r=nr)[:, :, Dh:DA]),
                      [rpsO[bk0]], [r_rden])
                for r in range(4):
                    bk = banks[r]
                    V(lambda e, r=r, bk=bk, h=h: e.scalar_tensor_tensor(
                        out=y_tok[:, r, h * Dh:(h + 1) * Dh], in0=psO[bk][:, offs[r]:offs[r] + Dh], scalar=rden[:, r:r + 1],
                        in1=sz[:, r, h * Dh:(h + 1) * Dh], op0=ALU.mult, op1=ALU.mult),
                      [rpsO[bk], r_rden, rsz[r]], [ry[r]])

        for li, l in enumerate(layers):
            last_layer = (li == len(layers) - 1)
            DMA("sync", lambda e, l=l: e.dma_start(out=g_bcv, in_=ng_d[l:l + 1, :].partition_broadcast(128)), w=[rg_bc])
            for dst, src in ((gq_bc, gq_d), (gk_bc, gk_d), (go_bc, go_d), (gmq_bc, gmq_d), (gmk_bc, gmk_d)):
                DMA("sync", lambda e, l=l, dst=dst, src=src: e.dma_start(out=dst[:], in_=src[l:l + 1, :].partition_broadcast(128)),
                    w=[rgains])
            for i in range(NT):
                if li == 0:
                    DMA(("scalar" if (XQ and i % 2 == 1) else "sync"), lambda e, i=i: e.dma_start(out=x_tok[:, i, :], in_=x_d[i * 128:(i + 1) * 128, :]), w=[rx[i]])
                rms_to_T(x_tok[:, i, :], rx[i], g_bcv, rg_bc, hT, rhT[i], i * 128, ss16, r_ss16, t16, r_t16,
                         rstd16, r_rstd16, i)
            glist = [g for g in ALL_GROUPS if g in groups]
            for gi, gname in enumerate(glist):
                last = last_layer and gi == len(glist) - 1
                kind = gname[0]
                g = int(gname[1])
                if kind == "A":
                    w1 = nxt("w", 3); w2 = nxt("w", 3)
                    load_w(wb[w1][:, :, 0:256], rwb[w1][0], w_in_cols(l, 512 + 256 * g, 256))
                    load_w(wb[w1][:, :, 256:512], rwb[w1][1], w_in_cols(l, 1024 + 256 * g, 256))
                    load_w(wb[w2][:, :, 0:256], rwb[w2][0], w_in_cols(l, 256 * g, 256))
                    load_w(wb[w2][:, :, 256:512], rwb[w2][1], w_in_cols(l, 3584 + 256 * g, 256))
                    load_w(wo[:], rwo, wout_d[l, 256 * g:256 * g + 256, :].rearrange("(c p) n -> p c n", p=128))
                    barrier()
                    G(lambda e: e.memset(v_aug[:, :, :, 64:65], 1.0), w=rv)
                    V(lambda e: e.memset(psG[:, 0:16], 0.0), w=[rpsG])
                    for i in range(NT):
                        n_blk = i // 2
                        (tA, rtA), (tB, rtB), (FO, rFO), (tR, rtR) = SC[(i % 2) * PAR1]
                        ka, rka = AUG[(i % 2) * PAR1]
                        pa, rpa = proj(i * 128, rhT[i], wb[w1], rwb[w1])
                        A(lambda e, i=i, pa=pa: e.copy(out=v_aug[:, i, :, 0:64],
                                                       in_=pa[:, 256:512].rearrange("p (h d) -> p h d", h=4)),
                          [rpa], [rv[i]])
                        headnorm(pa[:, 0:256], rpa, 4, 64, gk_bc[:], FO, rFO, tA, rtA, tB, rtB)
                        rope(FO, rFO, i, tR, rtR)
                        G(lambda e, ka=ka, FO=FO: e.tensor_copy(out=ka[:, :, 0:64], in_=FO[:].rearrange("p (h d) -> p h d", h=4)),
                          [rFO], [rka])
                        G(lambda e, ka=ka: e.memset(ka[:, :, 64:72], 0.0), w=[rka])
                        G(lambda e, ka=ka, n_blk=n_blk: e.memset(ka[:, :, 64 + n_blk:65 + n_blk], 1.0), w=[rka])
                        for pp in range(2):
                            T(lambda e, pp=pp, n_blk=n_blk, FO=FO: e.matmul(psG[:, pp * 8 + n_blk:pp * 8 + n_blk + 1],
                                                                            lhsT=FO[:, pp * 128:(pp + 1) * 128], rhs=cst[:, 514:515],
                                                                            start=False, stop=False, skip_group_check=True),
                              [rFO, rcst], [rpsG])
                        for h in range(4):
                            T(lambda e, h=h, ka=ka: e.transpose(out=psT[0:72, h * 128:(h + 1) * 128], in_=ka[:, h, :], identity=idb[:]),
                              [rka, ridb], [rpsT])
                        src3 = psT[0:72, 0:512].rearrange("p (h t) -> p h t", h=4)
                        evac(lambda e, i=i, src3=src3: e.tensor_copy(out=kT[0:72, :, i * 128:(i + 1) * 128], in_=src3),
                             lambda e, i=i, src3=src3: e.copy(out=kT[0:72, :, i * 128:(i + 1) * 128], in_=src3),
                             [rpsT], [rkT[i]])
                    G(lambda e: e.memset(kmT32[:], 0.0), w=[rkm])
                    A(lambda e: e.copy(out=kmT32[0:64, :, 0, :], in_=psG[0:64, 0:16].rearrange("p (a n) -> p a n", a=2)), [rpsG], [rkm])
                    A(lambda e: e.copy(out=kmT32[64:128, :, 1, :], in_=psG[64:128, 0:16].rearrange("p (a n) -> p a n", a=2)), [rpsG], [rkm])
                    for Q in range(4):
                        qT3 = qTs[Q % 2][:, 0:2048].rearrange("p (h t) -> p h t", h=4)
                        rqT = rqTs[Q % 2]
                        sz = szs[Q % 2]; rsz = rszs[Q % 2]
                        for r in range(4):
                            i = 4 * Q + r
                            own = i // 2
                            P.phase = "chain"
                            par = (i % 2) * PAR2 * (1 if (own < 4 or GPAR) else 0)
                            (tA, rtA), (tB, rtB), (FO, rFO), (tR, rtR) = SC[par]
                            qa, rqa = AUG[par]
                            pa, rpa = proj(i * 128, rhT[i], wb[w2], rwb[w2])
                            fin = silu_ps(sz[:, r, :], rsz[r], pa[:, 256:512], rpa, defer=True)
                            headnorm(pa[:, 0:256], rpa, 4, 64, gq_bc[:], FO, rFO, tA, rtA, tB, rtB)
                            fin()
                            rope(FO, rFO, i, tR, rtR)
                            G(lambda e, qa=qa, FO=FO: e.tensor_copy(out=qa[:, :, 0:64], in_=FO[:].rearrange("p (h d) -> p h d", h=4)),
                              [rFO], [rqa])
                            if own >= 4:
                                P.phase = "gate"
                                for pp in range(2):
                                    T(lambda e, pp=pp, FO=FO: e.matmul(psG[:, pp * 128:(pp + 1) * 128],
                                                                       lhsT=FO[:, pp * 128:(pp + 1) * 128], rhs=ident32,
                                                                       start=True, stop=True),
                                      [rFO, rcst], [rpsG])
                                A(lambda e: e.copy(out=Fp[5][:], in_=psG[:, 0:256]), [rpsG], [rF[5]])
                                for h in range(4):
                                    T(lambda e, h=h, own=own: e.matmul(
                                        psG[:, 256 + h * 8:256 + h * 8 + own],
                                        lhsT=Fp[5][:, (h // 2) * 128:(h // 2) * 128 + 128],
                                        rhs=kmT32[:, h // 2, h % 2, 0:own], start=True, stop=True),
                                      [rF[5], rkm], [rpsG])
                                V(lambda e: e.memset(gm[:], -1.0e30), w=[rgm])
                                V(lambda e, own=own: e.tensor_copy(
                                    out=gm[:, :, 0:own], in_=psG[:, 256:288].rearrange("p (h n) -> p h n", h=4)[:, :, 0:own]),
                                  [rpsG], [rgm])
                                for h in range(4):
                                    V(lambda e, h=h: e.max(out=top8[:, h, :], in_=gm[:, h, :]), [rgm], [rtop8])
                                V(lambda e: e.tensor_tensor(out=selb[:], in0=gm[:], in1=top8[:, :, 2:3].to_broadcast([128, 4, 8]),
                                                            op=ALU.is_ge), [rgm, rtop8], [rselb])
                                V(lambda e, qa=qa: e.tensor_scalar(out=qa[:, :, 64:72], in0=selb[:], scalar1=30000.0,
                                                                   scalar2=-30000.0, op0=ALU.mult, op1=ALU.add), [rselb], [rqa])
                                V(lambda e, qa=qa, own=own: e.memset(qa[:, :, 64 + own:65 + own], 0.0), w=[rqa])
                            else:
                                G(lambda e, qa=qa: e.memset(qa[:, :, 64:72], 0.0), w=[rqa])
                            P.phase = None
                            for h in range(4):
                                T(lambda e, h=h, qa=qa: e.transpose(out=psT[0:72, h * 128:(h + 1) * 128], in_=qa[:, h, :],
                                                                    identity=idb[:]), [rqa, ridb], [rpsT])
                            src3 = psT[0:72, 0:512].rearrange("p (h t) -> p h t", h=4)
                            evac(lambda e, r=r, src3=src3, qT3=qT3: e.tensor_copy(out=qT3[0:72, :, r * 128:(r + 1) * 128], in_=src3),
                                 lambda e, r=r, src3=src3, qT3=qT3: e.copy(out=qT3[0:72, :, r * 128:(r + 1) * 128], in_=src3),
                                 [rpsT], [rqT])
                        attn_chunk(Q, 4, 64, 72, 0.125, lambda Q: list(range(4 * Q + 4)), True,
                                   lambda h, kt: kT[0:72, h, kt * 128:(kt + 1) * 128], lambda kt: rkT[kt],
                                   lambda h, kt: v_aug[:, kt, h, :], lambda kt: rv[kt],
                                   lambda h, qT3=qT3: qT3[0:72, h, :], rqT, sz, rsz)
                        for r in range(4):
                            outproj_tile(4 * Q + r, r, last, obanks=([(psO[0], rpsO[0]), (psO[1], rpsO[1])] if OPB else None))
                elif kind == "M":
                    w1 = nxt("w", 3); w2 = nxt("w", 3)
                    wkvv = wkv_d[l].rearrange("(c p) n -> p c n", p=128)
                    load_w(wb[w1][:, :, 0:256], rwb[w1][0], wkvv[:, :, 256 * g:256 * g + 256])
                    load_w(wb[w1][:, :, 256:512], rwb[w1][1], wkvv[:, :, 512 + 256 * g:512 + 256 * g + 256])
                    load_w(wb[w2][:, :, 0:256], rwb[w2][0], w_in_cols(l, 3072 + 256 * g, 256))
                    load_w(wb[w2][:, :, 256:512], rwb[w2][1], w_in_cols(l, 4608 + 256 * g, 256))
                    load_w(wo[:], rwo, wout_d[l, 1024 + 256 * g:1024 + 256 * g + 256, :].rearrange("(c p) n -> p c n", p=128))
                    if g == 0 or ("M0" not in groups):
                        barrier()
                        DMA("sync", lambda e, l=l: e.dma_start(out=g_bcv, in_=mng_d[l:l + 1, :].partition_broadcast(128)),
                            w=[rg_bc])
                        for mt in range(2):
                            DMA("sync", lambda e, mt=mt: e.dma_start(out=stage, in_=mem_d[mt * 128:(mt + 1) * 128, :]),
                                w=[rstage])
                            rms_to_T(stage, rstage, g_bcv, rg_bc, memT, rmemT, mt * 128, ss16, r_ss16, t16, r_t16,
                                     rstd16, r_rstd16, mt)
                    for mt in range(2):
                        (tA, rtA), (tB, rtB), (FO, rFO), (tR, rtR) = SC[mt % 2]
                        pa, rpa = proj(mt * 128, rmemT, wb[w1], rwb[w1], lhsT_src=memT)
                        A(lambda e, mt=mt, pa=pa: e.copy(out=vm_aug[:, mt, :, 0:128],
                                                         in_=pa[:, 256:512].rearrange("p (h d) -> p h d", h=2)),
                          [rpa], [rvm])
                        headnorm(pa[:, 0:256], rpa, 2, 128, gmk_bc[:], FO, rFO, tA, rtA, tB, rtB)
                        G(lambda e, mt=mt, FO=FO: e.tensor_copy(out=Bp[mt % 2][:], in_=FO[:]), [rFO], [rB[mt % 2]])
                        for hh in range(2):
                            T(lambda e, hh=hh, mt=mt: e.transpose(out=psT[:, hh * 128:(hh + 1) * 128],
                                                                  in_=Bp[mt % 2][:, hh * 128:(hh + 1) * 128],
                                                                  identity=idb[:]), [rB[mt % 2], ridb], [rpsT])
                        src3 = psT[:, 0:256].rearrange("p (h t) -> p h t", h=2)
                        evac(lambda e, mt=mt, src3=src3: e.tensor_copy(out=kmT[:, :, mt * 128:(mt + 1) * 128], in_=src3),
                             lambda e, mt=mt, src3=src3: e.copy(out=kmT[:, :, mt * 128:(mt + 1) * 128], in_=src3),
                             [rpsT], [rkmT])
                    for Q in range(4):
                        qm3 = qTs[Q % 2][:, 0:1024].rearrange("p (h t) -> p h t", h=2)
                        rqT = rqTs[Q % 2]
                        sz = szs[Q % 2]; rsz = rszs[Q % 2]
                        for r in range(4):
                            i = 4 * Q + r
                            (tA, rtA), (tB, rtB), (FO, rFO), (tR, rtR) = SC[i % 2]
                            pa, rpa = proj(i * 128, rhT[i], wb[w2], rwb[w2])
                            fin = silu_ps(sz[:, r, :], rsz[r], pa[:, 256:512], rpa, defer=True)
                            headnorm(pa[:, 0:256], rpa, 2, 128, gmq_bc[:], FO, rFO, tA, rtA, tB, rtB)
                            fin()
                            G(lambda e, i=i, FO=FO: e.tensor_copy(out=Bp[i % 2][:], in_=FO[:]), [rFO], [rB[i % 2]])
                            for hh in range(2):
                                T(lambda e, hh=hh, i=i: e.transpose(out=psT[:, hh * 128:(hh + 1) * 128],
                                                                    in_=Bp[i % 2][:, hh * 128:(hh + 1) * 128], identity=idb[:]),
                                  [rB[i % 2], ridb], [rpsT])
                            src3 = psT[:, 0:256].rearrange("p (h t) -> p h t", h=2)
                            evac(lambda e, r=r, src3=src3, qm3=qm3: e.tensor_copy(out=qm3[:, :, r * 128:(r + 1) * 128], in_=src3),
                                 lambda e, r=r, src3=src3, qm3=qm3: e.copy(out=qm3[:, :, r * 128:(r + 1) * 128], in_=src3),
                                 [rpsT], [rqT])
                        attn_chunk(Q, 2, 128, 128, float(128 ** -0.5), lambda Q: [0, 1], False,
                                   lambda h, kt: kmT[:, h, kt * 128:(kt + 1) * 128], lambda kt: rkmT,
                                   lambda h, kt: vm_aug[:, kt, h, :], lambda kt: rvm,
                                   lambda h, qm3=qm3: qm3[:, h, :], rqT, sz, rsz)
                        for r in range(4):
                            outproj_tile(4 * Q + r, r, last, obanks=([(psO[0], rpsO[0]), (psO[1], rpsO[1])] if OPB else None))
                else:
                    w1 = nxt("w", 3); w2 = nxt("w", 3)
                    load_w(wb[w1][:, :, 0:256], rwb[w1][0], w_in_cols(l, 1536 + 256 * g, 256))
                    load_w(wb[w1][:, :, 256:512], rwb[w1][1], w_in_cols(l, 2048 + 256 * g, 256))
                    load_w(wb[w2][:, :, 0:256], rwb[w2][0], w_in_cols(l, 2560 + 256 * g, 256))
                    load_w(wb[w2][:, :, 256:512], rwb[w2][1], w_in_cols(l, 4096 + 256 * g, 256))
                    load_w(wo[:], rwo, wout_d[l, 512 + 256 * g:512 + 256 * g + 256, :].rearrange("(c p) n -> p c n", p=128))
                    barrier()
                    if l != 0:
                        DMA("sync", lambda e, g=g: e.dma_start(out=lb_g, in_=lbl_d[1:2, 256 * g:256 * g + 256].partition_broadcast(128)), w=[rlb])
                        DMA("sync", lambda e, g=g: e.dma_start(out=oml_g, in_=lbl_d[0:1, 256 * g:256 * g + 256].partition_broadcast(128)), w=[rlb])
                        V(lambda e: e.tensor_tensor(out=lb_g, in0=lb_g, in1=oml_g, op=ALU.subtract), [rlb], [rlb])
                        A(lambda e: e.activation(out=lb_g, in_=lb_g, func=AF.Exp, scale=-1.0), [rlb], [rlb])
                        A(lambda e: e.activation(out=lb_g, in_=lb_g, func=AF.Ln, bias=1.0), [rlb], [rlb])
                        A(lambda e: e.activation(out=lb_g, in_=lb_g, func=AF.Exp, scale=-1.0), [rlb], [rlb])
                        V(lambda e: e.tensor_scalar(out=oml_g, in0=lb_g, scalar1=-1.0, scalar2=1.0, op0=ALU.mult, op1=ALU.add),
                          [rlb], [rlb])
                    for hh in range(2):
                        G(lambda e, hh=hh: e.memset(S32[:, hh, :], 0.0), w=[rS32[hh]])
                        G(lambda e, hh=hh: e.memset(Sbf[:, 0, hh, :], 0.0), w=[rSbf[0][hh]])
                    Tri32 = cst[:, 256:384]
                    TriE32 = cst[:, 384:512]
                    for i in range(NT):
                        Fs, rFs, Bs, rBs = HSETS[i % 2]
                        sl = i % 4
                        sz = szs[(i // 4) % 2]; rsz = rszs[(i // 4) % 2]
                        pq, rpq = proj(i * 128, rhT[i], wb[w1], rwb[w1])
                        finq = silu_ps(Fs[0], rFs[0], pq[:, 0:256], rpq, defer=True)
                        A(lambda e, pq=pq, Fs=Fs: e.activation(out=Fs[1], in_=pq[:, 256:512], func=AF.Exp, scale=-1.0), [rpq], [rFs[1]])
                        A(lambda e, Fs=Fs: e.activation(out=Fs[1], in_=Fs[1], func=AF.Ln, bias=1.0), [rFs[1]], [rFs[1]])
                        finq()
                        pi_, rpi = proj(i * 128, rhT[i], wb[w2], rwb[w2])
                        silu_ps(sz[:, sl, :], rsz[sl], pi_[:, 256:512], rpi)
                        V(lambda e, pi_=pi_, Bs=Bs: e.tensor_copy(out=Bs[0], in_=pi_[:, 0:256]), [rpi], [rBs[0]])
                        if l == 0:
                            A(lambda e, Fs=Fs: e.activation(out=Fs[2], in_=Fs[1], func=AF.Copy, scale=-1.0), [rFs[1]], [rFs[2]])
                            A(lambda e, Fs=Fs: e.activation(out=Fs[1], in_=Fs[1], func=AF.Exp, scale=-1.0), [rFs[1]], [rFs[1]])
                        else:
                            A(lambda e, Fs=Fs: e.activation(out=Fs[1], in_=Fs[1], func=AF.Exp, scale=-1.0), [rFs[1]], [rFs[1]])
                            V(lambda e, g=g, Fs=Fs: e.tensor_tensor(out=Fs[1], in0=Fs[1], in1=oml_g,
                                                                    op=ALU.mult), [rFs[1], rlb], [rFs[1]])
                            V(lambda e, g=g, Fs=Fs: e.tensor_tensor(out=Fs[1], in0=Fs[1], in1=lb_g,
                                                                    op=ALU.add), [rFs[1], rlb], [rFs[1]])
                            A(lambda e, Fs=Fs: e.activation(out=Fs[2], in_=Fs[1], func=AF.Ln), [rFs[1]], [rFs[2]])
                        (G if GOFF else V)(lambda e, Fs=Fs: e.tensor_scalar(out=Fs[3], in0=Fs[1], scalar1=-1.0, scalar2=1.0, op0=ALU.mult,
                                                                            op1=ALU.add), [rFs[1]], [rFs[3]])
                        T(lambda e, Fs=Fs: e.matmul(psG[:, 0:256], lhsT=Tri32, rhs=Fs[2], start=True, stop=True),
                          [rcst, rFs[2]], [rpsG])
                        T(lambda e, Fs=Fs: e.matmul(psG[:, 256:512], lhsT=TriE32, rhs=Fs[2], start=True, stop=True),
                          [rcst, rFs[2]], [rpsG])
                        for hh in range(2):
                            T(lambda e, hh=hh, Fs=Fs: e.matmul(psS[1][:, 256 + 2 * hh:256 + 2 * hh + 2],
                                                               lhsT=Fs[2][:, hh * 128:(hh + 1) * 128],
                                                               rhs=cst[:, 512:514], start=True, stop=True), [rFs[2], rcst], [rpsS[1]])
                        dsl = dec8[:, 4 * (i % 2):4 * (i % 2) + 4]
                        rds = r_dec8[i % 2]
                        A(lambda e, dsl=dsl: e.activation(out=dsl, in_=psS[1][:, 256:260], func=AF.Exp), [rpsS[1]], [rds])
                        A(lambda e, Fs=Fs: e.activation(out=Fs[4], in_=psG[:, 0:256], func=AF.Exp), [rpsG], [rFs[4]])
                        A(lambda e, Fs=Fs: e.activation(out=Fs[5], in_=psG[:, 0:256], func=AF.Exp, scale=-1.0), [rpsG], [rFs[5]])
                        A(lambda e, Fs=Fs: e.activation(out=Fs[6], in_=psG[:, 256:512], func=AF.Exp), [rpsG], [rFs[6]])
                        V(lambda e, Fs=Fs, Bs=Bs: e.tensor_tensor(out=Bs[1], in0=Fs[0], in1=Fs[4], op=ALU.mult), [rFs[0], rFs[4]], [rBs[1]])
                        G(lambda e, Fs=Fs, Bs=Bs: e.tensor_tensor(out=Bs[2], in0=Fs[3], in1=Fs[5], op=ALU.mult), [rFs[3], rFs[5]], [rBs[2]])
                        G(lambda e, Fs=Fs, Bs=Bs: e.tensor_tensor(out=Bs[3], in0=Fs[3], in1=Fs[6], op=ALU.mult), [rFs[3], rFs[6]], [rBs[3]])
                        for hh in range(2):
                            T(lambda e, hh=hh, Bs=Bs: e.transpose(out=psT[:, hh * 128:(hh + 1) * 128], in_=Bs[1][:, hh * 128:(hh + 1) * 128],
                                                                  identity=idb[:]), [rBs[1], ridb], [rpsT])
                            T(lambda e, hh=hh, Bs=Bs: e.transpose(out=psT[:, 256 + hh * 128:256 + (hh + 1) * 128],
                                                                  in_=Bs[2][:, hh * 128:(hh + 1) * 128], identity=idb[:]),
                              [rBs[2], ridb], [rpsT])
                        pq3 = psT[:, 0:256].rearrange("p (h t) -> p h t", h=2)
                        A(lambda e, Bs=Bs: e.copy(out=Bs[4], in_=psT[:, 0:256]), [rpsT], [rBs[4]])
                        A(lambda e, pq3=pq3: e.copy(out=qTA[:, :, 0:64], in_=pq3[:, :, 0:64]), [rpsT], [rqTA])
                        V(lambda e, Bs=Bs: e.tensor_copy(out=Bs[5], in_=psT[:, 256:512]), [rpsT], [rBs[5]])
                        V(lambda e, pq3=pq3: e.tensor_copy(out=qTB[:, :, 64:128], in_=pq3[:, :, 64:128]), [rpsT], [rqTB])
                        cur = i % 2
                        nxtb = 1 - cur
                        for hh in range(2):
                            hs = slice(hh * 128, (hh + 1) * 128)
                            T(lambda e, hs=hs, Bs=Bs: e.matmul(psS[1][:, hs], lhsT=Bs[5][:, hs], rhs=Bs[4][:, hs], start=True, stop=True),
                              [rBs[5], rBs[4]], [rpsS[1]])
                        for hh in range(2):
                            hs = slice(hh * 128, (hh + 1) * 128)
                            V(lambda e, hs=hs, Bs=Bs: e.tensor_tensor(out=Bs[6][:, hs], in0=psS[1][:, hs], in1=hgmb[:], op=ALU.mult),
                              [rpsS[1], rhgmb], [rBs[6]])
                        for hh in range(2):
                            hs = slice(hh * 128, (hh + 1) * 128)
                            T(lambda e, hs=hs, hh=hh, Bs=Bs: e.matmul(psO[hh][:, 0:128], lhsT=Bs[6][:, hs], rhs=Bs[0][:, hs],
                                                                      start=True, stop=False), [rBs[6], rBs[0]], [rpsO[hh]])
                            T(lambda e, hh=hh, cur=cur: e.matmul(psO[hh][:, 0:128], lhsT=qTA[:, hh, :], rhs=Sbf[:, cur, hh, :],
                                                                 start=False, stop=False), [rqTA, rSbf[cur][hh]], [rpsO[hh]])
                        for hh in range(2):
                            hs = slice(hh * 128, (hh + 1) * 128)
                            T(lambda e, hs=hs, Bs=Bs: e.matmul(psS[0][:, hs], lhsT=Bs[3][0:64, hs], rhs=Bs[0][0:64, hs],
                                                               start=True, stop=True), [rBs[3], rBs[0]], [rpsS[0], rrow])
                        for hh in range(2):
                            hs = slice(hh * 128, (hh + 1) * 128)
                            V(lambda e, hs=hs, hh=hh, dsl=dsl: e.scalar_tensor_tensor(out=S32[:, hh, :], in0=S32[:, hh, :],
                                                                                      scalar=dsl[:, 2 * hh:2 * hh + 1], in1=psS[0][:, hs],
                                                                                      op0=ALU.mult, op1=ALU.add),
                              [rS32[hh], rds, rpsS[0]], [rS32[hh]])
                            G(lambda e, hh=hh, nxtb=nxtb: e.tensor_copy(out=Sbf[:, nxtb, hh, :], in_=S32[:, hh, :]),
                              [rS32[hh]], [rSbf[nxtb][hh]])
                        for hh in range(2):
                            T(lambda e, hh=hh, nxtb=nxtb: e.matmul(psO[hh][:, 0:128], lhsT=qTB[:, hh, :], rhs=Sbf[:, nxtb, hh, :],
                                                                   start=False, stop=True), [rqTB, rSbf[nxtb][hh]], [rpsO[hh], rrow])
                        for hh in range(2):
                            hs = slice(hh * 128, (hh + 1) * 128)
                            T(lambda e, hs=hs, Bs=Bs: e.matmul(psS[0][:, hs], lhsT=Bs[3][64:128, hs], rhs=Bs[0][64:128, hs],
                                                               start=True, stop=True), [rBs[3], rBs[0]], [rpsS[0], rrow])
                        for hh in range(2):
                            hs = slice(hh * 128, (hh + 1) * 128)
                            V(lambda e, hs=hs, hh=hh, dsl=dsl: e.scalar_tensor_tensor(out=S32[:, hh, :], in0=S32[:, hh, :],
                                                                                      scalar=dsl[:, 2 * hh + 1:2 * hh + 2], in1=psS[0][:, hs],
                                                                                      op0=ALU.mult, op1=ALU.add),
                              [rS32[hh], rds, rpsS[0]], [rS32[hh]])
                        for hh in range(2):
                            G(lambda e, hh=hh, nxtb=nxtb: e.tensor_copy(out=Sbf[:, nxtb, hh, :], in_=S32[:, hh, :]),
                              [rS32[hh]], [rSbf[nxtb][hh]])
                        ssl = sso4[:, 2 * (i % 2):2 * (i % 2) + 2]
                        r_sso = r_sso2[i % 2]
                        for hh in range(2):
                            A(lambda e, hh=hh, Fs=Fs, ssl=ssl: e.activation(out=Fs[7][:, 0:128], in_=psO[hh][:, 0:128], func=AF.Square,
                                                                            accum_out=ssl[:, hh:hh + 1]), [rpsO[hh]], [rFs[7], r_sso])
                        A(lambda e, ssl=ssl: e.activation(out=ssl, in_=ssl, func=AF.Ln, scale=1.0 / 128, bias=EPS), [r_sso], [r_sso])
                        A(lambda e, ssl=ssl: e.activation(out=ssl, in_=ssl, func=AF.Exp, scale=-0.5), [r_sso], [r_sso])
                        for hh in range(2):
                            hs = slice(hh * 128, (hh + 1) * 128)
                            V(lambda e, hh=hh, hs=hs, Fs=Fs, ssl=ssl: e.scalar_tensor_tensor(out=Fs[8][:, hs], in0=psO[hh][:, 0:128],
                                                                                             scalar=ssl[:, hh:hh + 1], in1=go_bc[:],
                                                                                             op0=ALU.mult, op1=ALU.mult),
                              [rpsO[hh], r_sso, rgains], [rFs[8]])
                        G(lambda e, Fs=Fs, sl=sl, sz=sz: e.tensor_tensor(out=y_tok[:, sl, :], in0=Fs[8], in1=sz[:, sl, :], op=ALU.mult),
                          [rFs[8], rsz[sl]], [ry[sl]])
                        outproj_tile(i, sl, last, obanks=[(psS[0], rpsS[0]), (psS[0], rpsS[0])])
            if not glist and last_layer:
                for i in range(NT):
                    out_dmas.append(DMA("sync", lambda e, i=i: e.dma_start(out=out_d[i * 128:(i + 1) * 128, :], in_=x_tok[:, i, :]),
                                        r=[rx[i]]))
        if SCHED:
            if SCHED2:
                P.schedule2(SDELTA)
            else:
                P.schedule()
        P.emit(st, out_dmas)
    build_nc.stats = P.stats
    return nc


_CACHE = {}


def _get_nc(layers, groups):
    key = (tuple(layers), tuple(groups))
    if key not in _CACHE:
        _CACHE[key] = build_nc(layers, groups)
    return _CACHE[key]


def run(inputs, layers=(0, 1), groups=ALL_GROUPS, cores=8):
    nc = _get_nc(layers, groups)
    f = lambda a: np.ascontiguousarray(np.asarray(a))
    cst = make_consts()
    shared = {k: f(inputs[k]).astype(np.float32, copy=False) for k in
              ("norm_g", "w_in", "w_out", "moba_q_norm", "moba_k_norm", "hgrn_lb_logits", "hgrn_o_norm",
               "mem_norm_g", "w_mem_kv", "mem_q_norm", "mem_k_norm")}
    x = f(inputs["x"]); mem = f(inputs["mem"]); pos = f(inputs["positions"]).astype(np.int32, copy=False)
    in_maps = []
    for b in range(cores):
        m = dict(shared)
        m["x"] = x[b]
        m["mem"] = mem[b]
        m["pos"] = pos[b].reshape(16, 128)
        m["cst"] = cst
        in_maps.append(m)
    res = run_bass_kernel_spmd(nc, in_maps, core_ids=list(range(cores)))
    return np.stack([np.asarray(r["out"]) for r in res.results], axis=0)


def kernel(x, mem, positions, norm_g, w_in, w_out, moba_q_norm, moba_k_norm, hgrn_lb_logits,
           hgrn_o_norm, mem_norm_g, w_mem_kv, mem_q_norm, mem_k_norm):
    inputs = dict(x=x, mem=mem, positions=positions, norm_g=norm_g, w_in=w_in, w_out=w_out,
                  moba_q_norm=moba_q_norm, moba_k_norm=moba_k_norm, hgrn_lb_logits=hgrn_lb_logits,
                  hgrn_o_norm=hgrn_o_norm, mem_norm_g=mem_norm_g, w_mem_kv=w_mem_kv,
                  mem_q_norm=mem_q_norm, mem_k_norm=mem_k_norm)
    return run(inputs).astype(np.float32, copy=False)
```

```python
import numpy as np
from contextlib import ExitStack
import concourse.bass as bass
import concourse.mybir as mybir
from concourse.bass_utils import run_bass_kernel_spmd

F32 = mybir.dt.float32
BF16 = mybir.dt.bfloat16
I32 = mybir.dt.int32
AF = mybir.ActivationFunctionType
ALU = mybir.AluOpType
AX = mybir.AxisListType

S = 2048
D = 1024
NT = 16
EPS = 1e-6
NCST = 576
import os as _os0
ALL_GROUPS = tuple(_os0.environ.get("ORDER", "A0,A1,H0,H1,M0,M1").split(","))
import os as _os
SCHED = _os.environ.get("SCHED", "1") == "1"
PAR1 = int(_os.environ.get("PAR1", "1"))
PAR2 = int(_os.environ.get("PAR2", "1"))
GPAR = int(_os.environ.get("GPAR", "1"))
PS3 = int(_os.environ.get("PS3", "3"))
MASKV = int(_os.environ.get("MASKV", "1"))
OPB = int(_os.environ.get("OPB", "1"))
SCHED2 = int(_os.environ.get("SCHED2", "1"))
SDELTA = float(_os.environ.get("SDELTA", "100"))
LATX = float(_os.environ.get("LATX", "180"))
PEK = float(_os.environ.get("PEK", "0.65"))
ACTK = float(_os.environ.get("ACTK", "1.0"))
DVEK = float(_os.environ.get("DVEK", "1.0"))
POOLK = float(_os.environ.get("POOLK", "1.0"))
LATS = float(_os.environ.get("LATS", "60"))
XQ = int(_os.environ.get("XQ", "0"))
GOFF = int(_os.environ.get("GOFF", "0"))
TRANS = int(_os.environ.get("TRANS", "1"))
PRUNE = int(_os.environ.get("PRUNE", "1"))


class Res:
    __slots__ = ("name", "w", "r", "excl")

    def __init__(self, name, excl=False):
        self.name = name
        self.w = None
        self.r = []
        self.excl = excl


class _Rec:
    def __init__(self):
        self.name = None
        self.args = ()
        self.kw = {}

    def __getattr__(self, name):
        def f(*a, **k):
            self.name, self.args, self.kw = name, a, k
            return self
        return f


def _free_size(ap):
    n = 1
    for d in list(ap.shape)[1:]:
        n *= int(d)
    return n


class Op:
    __slots__ = ("eng", "fn", "deps", "sdeps", "sig", "idx", "dma", "sem", "val", "i", "cost", "start")

    def __init__(self, eng, fn, deps, sdeps, dma):
        self.eng = eng
        self.fn = fn
        self.deps = deps
        self.sdeps = sdeps
        self.sig = False
        self.idx = 0
        self.dma = dma
        self.sem = None
        self.val = 0
        self.i = 0
        self.start = 0.0
        rec = _Rec()
        fn(rec)
        out = rec.kw.get("out", rec.args[0] if rec.args else None)
        n = _free_size(out) if out is not None else 64
        if dma:
            c = 2000.0 + n * int(out.shape[0]) * 4 / 120.0
        elif eng == "tensor":
            if rec.name == "transpose":
                c = 110.0
            else:
                lhsT = rec.kw.get("lhsT")
                f32 = lhsT is not None and lhsT.dtype == F32
                c = PEK * (64.0 + max(n, 64) / 2.0) * (4.0 if f32 else 1.0)
        elif eng == "scalar":
            c = ACTK * (200.0 + n / 1.2)
        elif eng == "vector":
            c = DVEK * (120.0 + n / 0.96 * (8.0 if rec.name == "reciprocal" else 1.0))
        else:
            c = POOLK * (300.0 + n / 0.5)
        self.cost = c


class Prog:
    ENGS = ["tensor", "vector", "scalar", "gpsimd", "sync"]

    def __init__(self, nc):
        self.nc = nc
        self.ops = []

    phase = None
    tok = None
    tokset = ()

    def op(self, eng, fn, reads=(), writes=(), dma=False):
        if self.phase == "gate" and self.tok is not None:
            writes = list(writes) + [self.tok]
        elif self.phase == "chain" and eng in self.tokset:
            reads = list(reads) + [self.tok]
        deps, sdeps = {}, {}

        def add(d):
            if d.dma or dma or d.eng != eng or eng != "tensor":
                deps[id(d)] = d
            else:
                sdeps[id(d)] = d
        for r in reads:
            if r.w is not None:
                add(r.w)
            if r.excl:
                for d in r.r:
                    if d.eng != eng:
                        add(d)
        for w in writes:
            if w.w is not None:
                add(w.w)
            for d in w.r:
                add(d)
        o = Op(eng, fn, list(deps.values()), list(sdeps.values()), dma)
        for r in reads:
            r.r.append(o)
        for w in writes:
            w.w = o
            w.r = []
        self.ops.append(o)
        return o

    def schedule(self):
        import heapq
        ops = self.ops
        for i, o in enumerate(ops):
            o.i = i
        succs = [[] for _ in ops]
        npred = [0] * len(ops)
        for o in ops:
            ds = o.deps + o.sdeps
            npred[o.i] = len(ds)
            for d in ds:
                succs[d.i].append(o)
        ready = [0.0] * len(ops)
        free = {e: 0.0 for e in self.ENGS}
        heap = [(0.0, o.i) for o in ops if npred[o.i] == 0]
        heapq.heapify(heap)
        done = 0
        while heap:
            t, i = heapq.heappop(heap)
            o = ops[i]
            st = max(ready[i], free[o.eng])
            if st > t + 1e-9:
                heapq.heappush(heap, (st, i))
                continue
            o.start = st
            if o.dma:
                free[o.eng] = st + 150.0
            else:
                free[o.eng] = st + o.cost
            fin = st + o.cost
            done += 1
            for sc in succs[i]:
                lat = 60.0 if (sc.eng == o.eng and not o.dma) else 180.0
                if fin + lat > ready[sc.i]:
                    ready[sc.i] = fin + lat
                npred[sc.i] -= 1
                if npred[sc.i] == 0:
                    heapq.heappush(heap, (max(ready[sc.i], free[sc.eng]), sc.i))
        assert done == len(ops), (done, len(ops))
        self.ops = sorted(ops, key=lambda o: (o.start, o.i))
        self.est_ns = max(o.start + o.cost for o in ops)

    def schedule2(self, delta=120.0):
        ops = self.ops
        n = len(ops)
        for i, o in enumerate(ops):
            o.i = i
        succs = [[] for _ in ops]
        npred = [0] * n
        for o in ops:
            ds = o.deps + o.sdeps
            npred[o.i] = len(ds)
            for d in ds:
                succs[d.i].append(o)
        blev = [0.0] * n
        for o in reversed(ops):
            b = 0.0
            for sc in succs[o.i]:
                lat = LATS if (sc.eng == o.eng and not o.dma) else LATX
                v = lat + blev[sc.i]
                if v > b:
                    b = v
            blev[o.i] = b + o.cost
        ready = [0.0] * n
        free = {e: 0.0 for e in self.ENGS}
        rsets = {e: [] for e in self.ENGS}
        for o in ops:
            if npred[o.i] == 0:
                rsets[o.eng].append(o.i)
        done = 0
        while done < n:
            best_e, best_t = None, 1e30
            for e in self.ENGS:
                rs = rsets[e]
                if not rs:
                    continue
                t = min(ready[i] for i in rs)
                if t < free[e]:
                    t = free[e]
                if t < best_t:
                    best_t, best_e = t, e
            e = best_e
            rs = rsets[e]
            lim = best_t + delta
            pick, pb = -1, -1.0
            for i in rs:
                if ready[i] <= lim and blev[i] > pb:
                    pb, pick = blev[i], i
            rs.remove(pick)
            o = ops[pick]
            st = max(ready[pick], free[e])
            o.start = st
            free[e] = st + (150.0 if o.dma else o.cost)
            fin = st + o.cost
            done += 1
            for sc in succs[pick]:
                lat = LATS if (sc.eng == o.eng and not o.dma) else LATX
                if fin + lat > ready[sc.i]:
                    ready[sc.i] = fin + lat
                npred[sc.i] -= 1
                if npred[sc.i] == 0:
                    rsets[sc.eng].append(sc.i)
        self.ops = sorted(ops, key=lambda o: (o.start, o.i))
        self.est_ns = max(o.start + o.cost for o in ops)

    def emit(self, stack, final_deps, ndma_sems=8):
        nc = self.nc
        if PRUNE:
            pos = {id(o): k for k, o in enumerate(self.ops)}
            for o in self.ops:
                best = {}
                keep = []
                for d in o.deps:
                    if d.dma:
                        keep.append(d)
                    elif d.eng not in best or pos[id(d)] > pos[id(best[d.eng])]:
                        best[d.eng] = d
                o.deps = keep + list(best.values())
        for o in self.ops:
            for d in o.deps:
                d.sig = True
        for d in final_deps:
            d.sig = True
        sems = {e: stack.enter_context(nc.semaphore("s_" + e)) for e in self.ENGS}
        cnt = {e: 0 for e in self.ENGS}
        pools, pool_i, pre_wait = {}, {}, {}
        for o in self.ops:
            if o.dma:
                if o.eng not in pools:
                    pools[o.eng] = [[stack.enter_context(nc.semaphore("d_%s_%d" % (o.eng, i))), 0]
                                    for i in range(ndma_sems)]
                    pool_i[o.eng] = 0
                p = pools[o.eng][pool_i[o.eng] % ndma_sems]
                pool_i[o.eng] += 1
                if p[1] > 0:
                    pre_wait[id(o)] = (p[0], p[1])
                p[1] += 16
                o.sem = p[0]
                o.val = p[1]
            elif o.sig:
                cnt[o.eng] += 1
                o.idx = cnt[o.eng]
        per = {e: [o for o in self.ops if o.eng == e] for e in self.ENGS}
        self.stats = {e: len(per[e]) for e in self.ENGS}
        known = {e: {} for e in self.ENGS}
        kn = {}
        plan = {}
        nw = 0

        def semkey(d):
            return (d.sem, d.val) if d.dma else (sems[d.eng], d.idx)

        for o in self.ops:
            kd = known[o.eng]
            ws = []
            for d in o.deps:
                sm, val = semkey(d)
                if kd.get(id(sm), (None, 0))[1] < val:
                    ws.append((sm, val))
                    kd[id(sm)] = (sm, val)
                if TRANS:
                    for k2, (s2, v2) in kn[id(d)].items():
                        if kd.get(k2, (None, 0))[1] < v2:
                            kd[k2] = (s2, v2)
            if o.dma:
                pw = pre_wait.get(id(o))
                if pw and kd.get(id(pw[0]), (None, 0))[1] < pw[1]:
                    ws.append(pw)
                    kd[id(pw[0])] = pw
            plan[id(o)] = ws
            nw += len(ws)
            if o.dma or o.sig:
                mine = dict(kd)
                sm, val = semkey(o)
                mine[id(sm)] = (sm, val)
                kn[id(o)] = mine
        fin_w = []
        kd = known["sync"]
        for d in final_deps:
            sm, val = semkey(d)
            if kd.get(id(sm), (None, 0))[1] < val:
                fin_w.append((sm, val))
                kd[id(sm)] = (sm, val)
        self.stats["waits"] = nw
        self.stats["sigs"] = {e: sum(1 for o in per[e] if o.sig and not o.dma) for e in self.ENGS}
        block = stack.enter_context(nc.Block())

        def mk(e):
            def body(engobj):
                for o in per[e]:
                    for sm, val in plan[id(o)]:
                        engobj.wait_ge(sm, val)
                    if o.dma:
                        o.fn(engobj).then_inc(o.sem, 16)
                    else:
                        ins = o.fn(engobj)
                        if o.sig:
                            ins.then_inc(sems[e], 1)
                if e == "sync":
                    for sm, val in fin_w:
                        engobj.wait_ge(sm, val)
            return body

        block.tensor(mk("tensor"))
        block.vector(mk("vector"))
        block.scalar(mk("scalar"))
        block.gpsimd(mk("gpsimd"))
        block.sync(mk("sync"))


def make_consts():
    c = np.zeros((128, NCST), np.float32)
    i = np.arange(128)
    c[:, 0:128] = np.eye(128)
    c[:, 128:256] = (i[None, :] >= i[:, None])
    same = (i[:, None] // 64) == (i[None, :] // 64)
    c[:, 256:384] = same & (i[:, None] <= i[None, :])
    c[:, 384:512] = same & (i[:, None] > i[None, :])
    c[:, 512] = i < 64
    c[:, 513] = i >= 64
    c[:, 514] = 1.0
    f64 = 500000.0 ** (-np.arange(8, dtype=np.float64) / 8.0)
    f = f64.astype(np.float32)
    flo = (f64 - f.astype(np.float64)).astype(np.float32)
    c[:, 515:523] = f[None, :]
    c[:, 523:531] = f[None, :]
    c[:, 547:555] = flo[None, :]
    c[:, 555:563] = flo[None, :]
    c[:, 531:539] = 0.0
    c[:, 539:547] = np.pi / 2
    return c


def build_nc(layers=(0, 1), groups=ALL_GROUPS):
    nc = bass.Bass("TRN2", target_bir_lowering=False)

    def din(name, shape, d=F32):
        return nc.dram_tensor(name, shape, d, kind="ExternalInput").ap()

    x_d = din("x", [S, D])
    mem_d = din("mem", [256, D])
    pos_d = din("pos", [16, 128], I32)
    ng_d = din("norm_g", [2, D])
    win_d = din("w_in", [2, D, 5120])
    wout_d = din("w_out", [2, 1536, D])
    gq_d = din("moba_q_norm", [2, 64])
    gk_d = din("moba_k_norm", [2, 64])
    lbl_d = din("hgrn_lb_logits", [2, 512])
    go_d = din("hgrn_o_norm", [2, 128])
    mng_d = din("mem_norm_g", [2, D])
    wkv_d = din("w_mem_kv", [2, D, 1024])
    gmq_d = din("mem_q_norm", [2, 128])
    gmk_d = din("mem_k_norm", [2, 128])
    cst_d = din("cst", [128, NCST])
    out_d = nc.dram_tensor("out", [S, D], F32, kind="ExternalOutput").ap()

    P = Prog(nc)
    if _os.environ.get("TOKR"):
        P.tok = Res("tok")
        P.tokset = tuple(_os.environ["TOKR"].split(","))
    with ExitStack() as st:
        def sb(name, shape, dt=F32):
            return st.enter_context(nc.sbuf_tensor("sb_" + name, shape, dt))

        def ps(name, shape, dt=F32):
            return st.enter_context(nc.psum_tensor("pp_" + name, shape, dt))

        def T(fn, r=(), w=()):
            return P.op("tensor", fn, r, w)

        def V(fn, r=(), w=()):
            return P.op("vector", fn, r, w)

        def A(fn, r=(), w=()):
            return P.op("scalar", fn, r, w)

        def G(fn, r=(), w=()):
            return P.op("gpsimd", fn, r, w)

        def DMA(q, fn, r=(), w=()):
            return P.op(q, fn, r, w, dma=True)

        x_tok = sb("x_tok", [128, NT, D]); rx = [Res("x%d" % i) for i in range(NT)]
        hT = sb("hT", [128, 8, S], BF16); rhT = [Res("hT%d" % i) for i in range(NT)]
        cst = sb("cst", [128, NCST]); rcst = Res("cst")
        idb = sb("idb", [128, 128], BF16); ridb = Res("idb")
        trib = sb("trib", [128, 128], BF16); rtrib = Res("trib")
        hgmb = sb("hgmb", [128, 128], BF16); rhgmb = Res("hgmb")
        gq_bc = sb("gq_bc", [128, 64]); gk_bc = sb("gk_bc", [128, 64]); go_bc = sb("go_bc", [128, 128])
        gmq_bc = sb("gmq_bc", [128, 128]); gmk_bc = sb("gmk_bc", [128, 128]); rgains = Res("gains")
        cs = sb("cs", [128, NT, 16]); sn = sb("sn", [128, NT, 16]); rrope = Res("rope")
        wb = [sb("wb%d" % i, [128, 8, 512], BF16) for i in range(3)]; rwb = [[Res("wb%da" % i), Res("wb%db" % i)] for i in range(3)]
        wo = sb("wo", [128, 2, D], BF16); rwo = Res("wo")
        Fp = [sb("F%d" % i, [128, 256]) for i in range(9)]; rF = [Res("F%d" % i) for i in range(9)]
        Bp = [sb("B%d" % i, [128, 256], BF16) for i in range(7)]; rB = [Res("B%d" % i) for i in range(7)]
        szs = [sb("sz%d" % k, [128, 4, 256]) for k in range(2)]; rszs = [[Res("sz%d_%d" % (k, i)) for i in range(4)] for k in range(2)]
        y_tok = sb("y_tok", [128, 4, 256], BF16); ry = [Res("y%d" % i) for i in range(4)]
        yT = [sb("yT%d" % i, [128, 256], BF16) for i in range(2)]; ryT = [Res("yT%d" % i) for i in range(2)]
        pT = [sb("pT%d" % i, [128, 512], BF16) for i in range(4)]; rpT = [Res("pT%d" % i) for i in range(4)]
        hbs = [sb("hb%d" % i, [128, D], BF16) for i in range(2)]; rhbs = [Res("hb0"), Res("hb1")]
        small = sb("small", [128, 128]); rsm = {}

        def sm(name, a, n):
            rsm[name] = Res("sm_" + name)
            return small[:, a:a + n], rsm[name]
        ss16, r_ss16 = sm("ss16", 0, 16)
        t16, r_t16 = sm("t16", 16, 16)
        rstd16, r_rstd16 = sm("rstd16", 32, 16)
        ss4, r_ss4 = sm("ss4", 48, 4)
        t4, r_t4 = sm("t4", 52, 4)
        rs4, r_rs4 = sm("rs4", 56, 4)
        rden, r_rden = sm("rden", 60, 4)
        rdenB, r_rdenB = sm("rdenB", 108, 4)
        dec4, r_dec4 = sm("dec4", 64, 4)
        sso, r_sso = sm("sso", 68, 2)
        to2, r_to2 = sm("to2", 70, 2)
        rso, r_rso = sm("rso", 72, 2)
        gm = sb("gm", [128, 4, 8]); rgm = Res("gm")
        top8 = sb("top8", [128, 4, 8]); rtop8 = Res("top8")
        selb = sb("selb", [128, 4, 8]); rselb = Res("selb")
        kT = sb("kT", [128, 4, S], BF16); rkT = [Res("kT%d" % i) for i in range(NT)]
        v_flat = sb("v_aug", [128, NT * 4 * 65], BF16); rv = [Res("v%d" % i) for i in range(NT)]
        v_aug = v_flat[:].rearrange("p (a b c) -> p a b c", a=NT, b=4)
        stage = v_flat[:, 0:2048].bitcast(F32); rstage = Res("stage")
        g_bcv = v_flat[:, 2048:4096].bitcast(F32); rg_bc = Res("g_bc")
        qTs = [sb("qT%d" % i, [128, 2048], BF16) for i in range(2)]; rqTs = [Res("qT0"), Res("qT1")]
        lbv = qTs[1][:, 0:2048].bitcast(F32); rlb = rqTs[1]
        lb_g = lbv[:, 0:256]; oml_g = lbv[:, 256:512]
        k_aug = sb("k_aug", [128, 4, 72], BF16); rk_aug = Res("k_aug")
        q_aug = sb("q_aug", [128, 4, 72], BF16); rq_aug = Res("q_aug")
        kmT32 = sb("kmT32", [128, 2, 2, 8]); rkm = Res("kmT32")
        S32 = sb("S32", [128, 2, 128]); rS32 = [Res("S32_0"), Res("S32_1")]
        Sbf = sb("Sbf", [128, 2, 2, 128], BF16); rSbf = [[Res("Sbf00"), Res("Sbf01")], [Res("Sbf10"), Res("Sbf11")]]
        qTA = sb("qTA", [128, 2, 128], BF16); qTB = sb("qTB", [128, 2, 128], BF16)
        rqTA = Res("qTA"); rqTB = Res("qTB")
        memT = sb("memT", [128, 8, 256], BF16); rmemT = Res("memT")
        kmT = sb("kmT", [128, 2, 256], BF16); rkmT = Res("kmT")
        vm_aug = sb("vm_aug", [128, 2, 2, 129], BF16); rvm = Res("vm")
        psA = [ps("psA%d" % i, [128, 512]) for i in range(2)]; rpsA = [Res("psA0", True), Res("psA1", True)]
        psT = ps("psT", [128, 1024], BF16); rpsT = Res("psT", True)
        psG = ps("psG", [128, 512]); rpsG = Res("psG", True)
        psS = [ps("psS%d" % i, [128, 512]) for i in range(2)]; rpsS = [Res("psS0", True), Res("psS1", True)]
        psO = [ps("psO%d" % i, [128, 512]) for i in range(2)]; rpsO = [Res("psO0", True), Res("psO1", True)]

        ctr = {"pa": 0, "w": 0, "ev": 0, "ps": 0, "pt": 0, "yt": 0, "mk": 0}

        def nxt(k, n):
            v = ctr[k] % n
            ctr[k] += 1
            return v

        def evac(fn_v, fn_a, r, w):
            if nxt("ev", 2) == 0:
                return A(fn_a, r, w)
            return V(fn_v, r, w)

        DMA("sync", lambda e: e.dma_start(out=cst[:], in_=cst_d), w=[rcst])
        V(lambda e: e.tensor_copy(out=idb[:], in_=cst[:, 0:128]), [rcst], [ridb])
        V(lambda e: e.tensor_copy(out=trib[:], in_=cst[:, 128:256]), [rcst], [rtrib])
        V(lambda e: e.tensor_copy(out=hgmb[:], in_=cst[:, 256:384]), [rcst], [rhgmb])
        ident32 = cst[:, 0:128]
        G(lambda e: e.memset(vm_aug[:, :, :, 128:129], 1.0), w=[rvm])
        G(lambda e: e.memset(qTA[:], 0.0), w=[rqTA])
        G(lambda e: e.memset(qTB[:], 0.0), w=[rqTB])
        nI = sb("nI", [128, 256], I32); rnI = Res("nI")
        posi = nI[0:16, 0:128]; rposi = rnI
        posf = Fp[8][0:16, 0:128]; rposf = rF[8]
        DMA("sync", lambda e: e.dma_start(out=posi, in_=pos_d), w=[rposi])
        V(lambda e: e.tensor_copy(out=posf, in_=posi), [rposi], [rposf])
        T(lambda e: e.matmul(psG[:, 0:16], lhsT=posf, rhs=cst[0:16, 0:16], start=True, stop=True), [rposf, rcst], [rpsG])
        post, r_post = sm("post", 80, 16)
        V(lambda e: e.tensor_copy(out=post, in_=psG[:, 0:16]), [rpsG], [r_post])
        ang = Fp[0][:, 0:256].rearrange("p (i j) -> p i j", i=NT)
        V(lambda e: e.tensor_tensor(out=ang, in0=post.unsqueeze(2).to_broadcast([128, NT, 16]),
                                    in1=cst[:, 515:531].unsqueeze(1).to_broadcast([128, NT, 16]), op=ALU.mult),
          [r_post, rcst], [rF[0]])
        ang_lo = Fp[1][:, 0:256].rearrange("p (i j) -> p i j", i=NT)
        V(lambda e: e.tensor_tensor(out=ang_lo, in0=post.unsqueeze(2).to_broadcast([128, NT, 16]),
                                    in1=cst[:, 547:563].unsqueeze(1).to_broadcast([128, NT, 16]), op=ALU.mult),
          [r_post, rcst], [rF[1]])
        V(lambda e: e.tensor_tensor(out=ang, in0=ang, in1=ang_lo, op=ALU.add), [rF[0], rF[1]], [rF[0]])
        V(lambda e: e.tensor_tensor(out=ang, in0=ang, in1=cst[:, 531:547].unsqueeze(1).to_broadcast([128, NT, 16]),
                                    op=ALU.add), [rF[0], rcst], [rF[0]])
        V(lambda e: e.tensor_scalar(out=Fp[1][:], in0=Fp[0][:], scalar1=float(1.0 / (2 * np.pi)), scalar2=None,
                                    op0=ALU.mult), [rF[0]], [rF[1]])
        V(lambda e: e.tensor_copy(out=nI[:], in_=Fp[1][:]), [rF[1]], [rnI])
        V(lambda e: e.tensor_copy(out=Fp[1][:], in_=nI[:]), [rnI], [rF[1]])
        C1 = 6.28125
        C2 = float(2 * np.pi - 6.28125)
        V(lambda e: e.scalar_tensor_tensor(out=Fp[2][:], in0=Fp[1][:], scalar=-C1, in1=Fp[0][:],
                                           op0=ALU.mult, op1=ALU.add), [rF[1], rF[0]], [rF[2]])
        V(lambda e: e.scalar_tensor_tensor(out=Fp[2][:], in0=Fp[1][:], scalar=-C2, in1=Fp[2][:],
                                           op0=ALU.mult, op1=ALU.add), [rF[1], rF[2]], [rF[2]])
        V(lambda e: e.tensor_scalar(out=Fp[2][:], in0=Fp[2][:], scalar1=float(np.pi), scalar2=float(-np.pi),
                                    op0=ALU.min, op1=ALU.max), [rF[2]], [rF[2]])
        A(lambda e: e.activation(out=Fp[3][:], in_=Fp[2][:], func=AF.Sin), [rF[2]], [rF[3]])
        sc = Fp[3][:, 0:256].rearrange("p (i j) -> p i j", i=NT)
        V(lambda e: e.tensor_copy(out=cs[:, :, 0:8], in_=sc[:, :, 8:16]), [rF[3]], [rrope])
        V(lambda e: e.tensor_copy(out=cs[:, :, 8:16], in_=sc[:, :, 8:16]), [rF[3]], [rrope])
        V(lambda e: e.tensor_scalar(out=sn[:, :, 0:8], in0=sc[:, :, 0:8], scalar1=-1.0, scalar2=None, op0=ALU.mult),
          [rF[3]], [rrope])
        V(lambda e: e.tensor_copy(out=sn[:, :, 8:16], in_=sc[:, :, 0:8]), [rF[3]], [rrope])
        def load_w(dst, rdst, src_ap):
            return DMA("gpsimd", lambda e: e.dma_start(out=dst, in_=src_ap), w=[rdst])

        def w_in_cols(l, c0, n):
            return win_d[l].rearrange("(c p) n -> p c n", p=128)[:, :, c0:c0 + n]

        psGb = psG[:].bitcast(BF16)

        def rms_to_T(src_tile, rsrc, gain, rgain, dstT, rdst, col0, ssc, r_ssc, tsc, r_tsc, rsc, r_rsc, k):
            hb = hbs[k % 2]; rhb = rhbs[k % 2]
            pst, rpst = (psT, rpsT) if k % 2 == 0 else (psGb, rpsG)
            A(lambda e: e.activation(out=hb[:], in_=src_tile, func=AF.Square, accum_out=ssc[:, k:k + 1]),
              [rsrc], [rhb, r_ssc])
            A(lambda e: e.activation(out=tsc[:, k:k + 1], in_=ssc[:, k:k + 1], func=AF.Ln, scale=1.0 / D, bias=EPS),
              [r_ssc], [r_tsc])
            A(lambda e: e.activation(out=rsc[:, k:k + 1], in_=tsc[:, k:k + 1], func=AF.Exp, scale=-0.5),
              [r_tsc], [r_rsc])
            V(lambda e: e.scalar_tensor_tensor(out=hb[:], in0=src_tile, scalar=rsc[:, k:k + 1], in1=gain,
                                               op0=ALU.mult, op1=ALU.mult), [rsrc, r_rsc, rgain], [rhb])
            for c in range(8):
                T(lambda e, c=c: e.transpose(out=pst[:, c * 128:(c + 1) * 128], in_=hb[:, c * 128:(c + 1) * 128],
                                             identity=idb[:]), [rhb, ridb], [rpst])
            src3 = pst[:, 0:1024].rearrange("p (c t) -> p c t", c=8)
            evac(lambda e: e.tensor_copy(out=dstT[:, :, col0:col0 + 128], in_=src3),
                 lambda e: e.copy(out=dstT[:, :, col0:col0 + 128], in_=src3), [rpst], [rdst])

        def proj(lhs_cols, rlhs, wt, rwt, ncols=512, lhsT_src=None):
            b = nxt("pa", 2)
            src = hT if lhsT_src is None else lhsT_src
            for c in range(8):
                T(lambda e, c=c: e.matmul(psA[b][:, 0:ncols], lhsT=src[:, c, lhs_cols:lhs_cols + 128],
                                          rhs=wt[:, c, 0:ncols], start=(c == 0), stop=(c == 7)),
                  [rlhs] + list(rwt), [rpsA[b]])
            return psA[b], rpsA[b]

        def headnorm(src_ps, rps, H, Dh, gain, outF, routF, tmpA, rtmpA, tmpB, rtmpB):
            n = H * Dh
            A(lambda e: e.activation(out=tmpA[:, 0:n], in_=src_ps, func=AF.Square), [rps], [rtmpA])
            V(lambda e: e.tensor_reduce(out=ss4[:, 0:H], in_=tmpA[:, 0:n].rearrange("p (h d) -> p h d", h=H),
                                        axis=AX.X, op=ALU.add), [rtmpA], [r_ss4])
            A(lambda e: e.activation(out=t4[:, 0:H], in_=ss4[:, 0:H], func=AF.Ln, scale=1.0 / Dh, bias=EPS),
              [r_ss4], [r_t4])
            A(lambda e: e.activation(out=rs4[:, 0:H], in_=t4[:, 0:H], func=AF.Exp, scale=-0.5), [r_t4], [r_rs4])
            V(lambda e: e.tensor_tensor(out=tmpB[:, 0:n].rearrange("p (h d) -> p h d", h=H),
                                        in0=src_ps.rearrange("p (h d) -> p h d", h=H),
                                        in1=rs4[:, 0:H].unsqueeze(2).to_broadcast([128, H, Dh]), op=ALU.mult),
              [rps, r_rs4], [rtmpB])
            (G if GOFF else V)(lambda e: e.tensor_tensor(out=outF[:, 0:n].rearrange("p (h d) -> p h d", h=H),
                                                         in0=tmpB[:, 0:n].rearrange("p (h d) -> p h d", h=H),
                                                         in1=gain.unsqueeze(1).to_broadcast([128, H, Dh]), op=ALU.mult),
                               [rtmpB, rgains], [routF])

        def rope(Fx, rFx, i, tR, rtR):
            x3 = Fx[:, 0:256].rearrange("p (h d) -> p h d", h=4)
            a3 = tR[:, 0:64].rearrange("p (h d) -> p h d", h=4)
            b3 = tR[:, 64:128].rearrange("p (h d) -> p h d", h=4)
            rtA = rtR
            rtB = rtR
            G(lambda e: e.tensor_tensor(out=a3, in0=x3[:, :, 0:16], in1=cs[:, i, :].unsqueeze(1).to_broadcast([128, 4, 16]),
                                        op=ALU.mult), [rFx, rrope], [rtA])
            G(lambda e: e.tensor_tensor(out=b3[:, :, 0:8], in0=x3[:, :, 8:16],
                                        in1=sn[:, i, 0:8].unsqueeze(1).to_broadcast([128, 4, 8]), op=ALU.mult),
              [rFx, rrope], [rtB])
            G(lambda e: e.tensor_tensor(out=b3[:, :, 8:16], in0=x3[:, :, 0:8],
                                        in1=sn[:, i, 8:16].unsqueeze(1).to_broadcast([128, 4, 8]), op=ALU.mult),
              [rFx, rrope], [rtB])
            G(lambda e: e.tensor_tensor(out=x3[:, :, 0:16], in0=a3, in1=b3, op=ALU.add), [rtA, rtB], [rFx])

        def silu_ps(dst, rdst, src_ps, rps, defer=False):
            A(lambda e: e.activation(out=dst, in_=src_ps, func=AF.Exp, scale=-1.0), [rps], [rdst])
            A(lambda e: e.activation(out=dst, in_=dst, func=AF.Ln, bias=1.0), [rdst], [rdst])
            A(lambda e: e.activation(out=dst, in_=dst, func=AF.Exp, scale=-1.0), [rdst], [rdst])

            def fin():
                V(lambda e: e.tensor_tensor(out=dst, in0=src_ps, in1=dst, op=ALU.mult), [rps, rdst], [rdst])
            if defer:
                return fin
            fin()

        SC = [((Fp[0], rF[0]), (Fp[1], rF[1]), (Fp[2], rF[2]), (Fp[3], rF[3])),
              ((Fp[4], rF[4]), (Fp[6], rF[6]), (Fp[7], rF[7]), (Fp[8], rF[8]))]
        AUG = [(k_aug, rk_aug), (q_aug, rq_aug)]
        if _os.environ.get("NOPAR", "0") == "1":
            SC[1] = SC[0]
        if _os.environ.get("NOAUG", "0") == "1":
            AUG[1] = AUG[0]

        dec8, _r = sm("dec8", 96, 8)
        r_dec8 = [Res("dec8a"), Res("dec8b")]
        sso4, _r2 = sm("sso4", 104, 4)
        r_sso2 = [Res("ssoA"), Res("ssoB")]
        dummy = sb("dummy", [128, 8])
        kflat = kT[:].rearrange("p h t -> p (h t)")
        kf32 = kflat[:, 0:4608].bitcast(F32)
        Fq = [kf32[:, k * 256:(k + 1) * 256] for k in range(9)]
        rFq = [Res("Fq%d" % k) for k in range(9)]
        Bq = [kflat[:, 4608 + k * 256:4608 + (k + 1) * 256] for k in range(7)]
        rBq = [Res("Bq%d" % k) for k in range(7)]
        HSETS = [([t[:] for t in Fp], rF, [t[:] for t in Bp], rB), (Fq, rFq, Bq, rBq)]

        def barrier():
            G(lambda e: e.memset(dummy[:], 0.0), w=list(rkT) + rFq + rBq + list(rv) + [rstage, rg_bc])

        rrow = Res("pe_rowfence")

        out_dmas = []

        def outproj_tile(i, r, last, obanks=None):
            yb = nxt("yt", 2)
            for pp in range(2):
                T(lambda e, pp=pp: e.transpose(out=psT[:, pp * 128:(pp + 1) * 128], in_=y_tok[:, r, pp * 128:(pp + 1) * 128],
                                               identity=idb[:]), [ry[r], ridb], [rpsT])
            evac(lambda e: e.tensor_copy(out=yT[yb][:], in_=psT[:, 0:256]),
                 lambda e: e.copy(out=yT[yb][:], in_=psT[:, 0:256]), [rpsT], [ryT[yb]])
            for half in range(2):
                if obanks is None:
                    b = nxt("pa", 2)
                    pso, rpso = psA[b], rpsA[b]
                else:
                    pso, rpso = obanks[half]
                for pp in range(2):
                    T(lambda e, pp=pp, half=half, pso=pso: e.matmul(pso[:, 0:512], lhsT=yT[yb][:, pp * 128:(pp + 1) * 128],
                                                                    rhs=wo[:, pp, half * 512:(half + 1) * 512],
                                                                    start=(pp == 0), stop=(pp == 1)),
                      [ryT[yb], rwo], [rpso])
                V(lambda e, half=half, pso=pso: e.tensor_tensor(out=x_tok[:, i, half * 512:(half + 1) * 512], in0=pso[:, 0:512],
                                                                in1=x_tok[:, i, half * 512:(half + 1) * 512], op=ALU.add),
                  [rpso, rx[i]], [rx[i]])
            if last:
                out_dmas.append(DMA("sync", lambda e: e.dma_start(out=out_d[i * 128:(i + 1) * 128, :], in_=x_tok[:, i, :]),
                                    r=[rx[i]]))

        def attn_chunk(Q, H, Dh, KP, scale, key_tiles, causal, kTsrc, rkTsrc, vsrc, rvsrc, qview, rqT, sz, rsz):
            DA = Dh + 1
            for h in range(H):
                if Dh == 64:
                    o_b = h % 2
                    banks = [o_b, o_b, o_b, o_b]
                    offs = [0, DA, 2 * DA, 3 * DA]
                else:
                    banks = [0, 0, 1, 1]
                    offs = [0, DA, 0, DA]
                started = set()
                kts = key_tiles(Q)
                for kt in kts:
                    j = kt - 4 * Q if causal else -1
                    q0 = max(j, 0) * 128
                    N = 512 - q0
                    sbk = nxt("ps", PS3)
                    pss, rpss = ((psS[0], rpsS[0]), (psS[1], rpsS[1]), (psG, rpsG))[sbk]
                    T(lambda e, kt=kt, h=h, q0=q0, N=N, pss=pss: e.matmul(
                        pss[:, 0:N], lhsT=kTsrc(h, kt), rhs=qview(h)[:, q0:512], start=True, stop=True),
                      [rkTsrc(kt), rqT], [rpss])
                    pb = nxt("pt", 4)
                    A(lambda e, N=N, pss=pss, pb=pb: e.activation(out=pT[pb][:, 0:N], in_=pss[:, 0:N], func=AF.Exp,
                                                                  scale=scale), [rpss], [rpT[pb]])
                    if j >= 0:
                        (V if (MASKV and nxt("mk", 2) == 0) else G)(
                            lambda e, pb=pb: e.tensor_tensor(out=pT[pb][:, 0:128], in0=pT[pb][:, 0:128], in1=trib[:],
                                                             op=ALU.mult), [rpT[pb], rtrib], [rpT[pb]])
                    for r in range(max(j, 0), 4):
                        bk = banks[r]
                        first = bk not in started
                        started.add(bk)
                        T(lambda e, r=r, kt=kt, h=h, q0=q0, pb=pb, bk=bk, first=first: e.matmul(
                            psO[bk][:, offs[r]:offs[r] + DA], lhsT=pT[pb][:, r * 128 - q0:r * 128 - q0 + 128],
                            rhs=vsrc(h, kt), start=first, stop=False, skip_group_check=True),
                          [rpT[pb], rvsrc(kt)], [rpsO[bk]])
                rd, r_rd = (rden, r_rden) if h % 2 == 0 else (rdenB, r_rdenB)
                for bk0 in sorted(set(banks)):
                    rs_ = [r for r in range(4) if banks[r] == bk0]
                    nr = len(rs_)
                    V(lambda e, bk0=bk0, rs_=rs_, nr=nr, rd=rd: e.reciprocal(
                        out=rd[:, rs_[0]:rs_[0] + nr],
                        in_=psO[bk0][:, 0:nr * DA].rearrange("p (r c) -> p r c", r=nr)[:, :, Dh:DA]),
                      [rpsO[bk0]], [r_rd])
                    V(lambda e, bk0=bk0, rs_=rs_, nr=nr, h=h: e.tensor_tensor(
                        out=sz[:, rs_[0]:rs_[0] + nr, h * Dh:(h + 1) * Dh],
                        in0=psO[bk0][:, 0:nr * DA].rearrange("p (r c) -> p r c", r=nr)[:, :, 0:Dh],
                        in1=sz[:, rs_[0]:rs_[0] + nr, h * Dh:(h + 1) * Dh], op=ALU.mult),
                      [rpsO[bk0]] + [rsz[r] for r in rs_], [rsz[r] for r in rs_])
                G(lambda e, h=h, rd=rd: e.tensor_tensor(out=y_tok[:, :, h * Dh:(h + 1) * Dh], in0=sz[:, :, h * Dh:(h + 1) * Dh],
                                                        in1=rd[:, 0:4].unsqueeze(2).to_broadcast([128, 4, Dh]), op=ALU.mult),
                  list(rsz) + [r_rd], list(ry))

        for li, l in enumerate(layers):
            last_layer = (li == len(layers) - 1)
            DMA("sync", lambda e, l=l: e.dma_start(out=g_bcv, in_=ng_d[l:l + 1, :].partition_broadcast(128)), w=[rg_bc])
            for dst, src in ((gq_bc, gq_d), (gk_bc, gk_d), (go_bc, go_d), (gmq_bc, gmq_d), (gmk_bc, gmk_d)):
                DMA("sync", lambda e, l=l, dst=dst, src=src: e.dma_start(out=dst[:], in_=src[l:l + 1, :].partition_broadcast(128)),
                    w=[rgains])
            for i in range(NT):
                if li == 0:
                    DMA(("scalar" if (XQ and i % 2 == 1) else "sync"), lambda e, i=i: e.dma_start(out=x_tok[:, i, :], in_=x_d[i * 128:(i + 1) * 128, :]), w=[rx[i]])
                rms_to_T(x_tok[:, i, :], rx[i], g_bcv, rg_bc, hT, rhT[i], i * 128, ss16, r_ss16, t16, r_t16,
                         rstd16, r_rstd16, i)
            glist = [g for g in ALL_GROUPS if g in groups]
            for gi, gname in enumerate(glist):
                last = last_layer and gi == len(glist) - 1
                kind = gname[0]
                g = int(gname[1])
                if kind == "A":
                    w1 = nxt("w", 3); w2 = nxt("w", 3)
                    load_w(wb[w1][:, :, 0:256], rwb[w1][0], w_in_cols(l, 512 + 256 * g, 256))
                    load_w(wb[w1][:, :, 256:512], rwb[w1][1], w_in_cols(l, 1024 + 256 * g, 256))
                    load_w(wb[w2][:, :, 0:256], rwb[w2][0], w_in_cols(l, 256 * g, 256))
                    load_w(wb[w2][:, :, 256:512], rwb[w2][1], w_in_cols(l, 3584 + 256 * g, 256))
                    load_w(wo[:], rwo, wout_d[l, 256 * g:256 * g + 256, :].rearrange("(c p) n -> p c n", p=128))
                    barrier()
                    G(lambda e: e.memset(v_aug[:, :, :, 64:65], 1.0), w=rv)
                    V(lambda e: e.memset(psG[:, 0:16], 0.0), w=[rpsG])
                    for i in range(NT):
                        n_blk = i // 2
                        (tA, rtA), (tB, rtB), (FO, rFO), (tR, rtR) = SC[(i % 2) * PAR1]
                        ka, rka = AUG[(i % 2) * PAR1]
                        pa, rpa = proj(i * 128, rhT[i], wb[w1], rwb[w1])
                        A(lambda e, i=i, pa=pa: e.copy(out=v_aug[:, i, :, 0:64],
                                                       in_=pa[:, 256:512].rearrange("p (h d) -> p h d", h=4)),
                          [rpa], [rv[i]])
                        headnorm(pa[:, 0:256], rpa, 4, 64, gk_bc[:], FO, rFO, tA, rtA, tB, rtB)
                        rope(FO, rFO, i, tR, rtR)
                        G(lambda e, ka=ka, FO=FO: e.tensor_copy(out=ka[:, :, 0:64], in_=FO[:].rearrange("p (h d) -> p h d", h=4)),
                          [rFO], [rka])
                        G(lambda e, ka=ka: e.memset(ka[:, :, 64:72], 0.0), w=[rka])
                        G(lambda e, ka=ka, n_blk=n_blk: e.memset(ka[:, :, 64 + n_blk:65 + n_blk], 1.0), w=[rka])
                        for pp in range(2):
                            T(lambda e, pp=pp, n_blk=n_blk, FO=FO: e.matmul(psG[:, pp * 8 + n_blk:pp * 8 + n_blk + 1],
                                                                            lhsT=FO[:, pp * 128:(pp + 1) * 128], rhs=cst[:, 514:515],
                                                                            start=False, stop=False, skip_group_check=True),
                              [rFO, rcst], [rpsG])
                        for h in range(4):
                            T(lambda e, h=h, ka=ka: e.transpose(out=psT[0:72, h * 128:(h + 1) * 128], in_=ka[:, h, :], identity=idb[:]),
                              [rka, ridb], [rpsT])
                        src3 = psT[0:72, 0:512].rearrange("p (h t) -> p h t", h=4)
                        evac(lambda e, i=i, src3=src3: e.tensor_copy(out=kT[0:72, :, i * 128:(i + 1) * 128], in_=src3),
                             lambda e, i=i, src3=src3: e.copy(out=kT[0:72, :, i * 128:(i + 1) * 128], in_=src3),
                             [rpsT], [rkT[i]])
                    G(lambda e: e.memset(kmT32[:], 0.0), w=[rkm])
                    A(lambda e: e.copy(out=kmT32[0:64, :, 0, :], in_=psG[0:64, 0:16].rearrange("p (a n) -> p a n", a=2)), [rpsG], [rkm])
                    A(lambda e: e.copy(out=kmT32[64:128, :, 1, :], in_=psG[64:128, 0:16].rearrange("p (a n) -> p a n", a=2)), [rpsG], [rkm])
                    for Q in range(4):
                        qT3 = qTs[Q % 2][:, 0:2048].rearrange("p (h t) -> p h t", h=4)
                        rqT = rqTs[Q % 2]
                        sz = szs[Q % 2]; rsz = rszs[Q % 2]
                        for r in range(4):
                            i = 4 * Q + r
                            own = i // 2
                            P.phase = "chain"
                            par = (i % 2) * PAR2 * (1 if (own < 4 or GPAR) else 0)
                            (tA, rtA), (tB, rtB), (FO, rFO), (tR, rtR) = SC[par]
                            qa, rqa = AUG[par]
                            pa, rpa = proj(i * 128, rhT[i], wb[w2], rwb[w2])
                            fin = silu_ps(sz[:, r, :], rsz[r], pa[:, 256:512], rpa, defer=True)
                            headnorm(pa[:, 0:256], rpa, 4, 64, gq_bc[:], FO, rFO, tA, rtA, tB, rtB)
                            fin()
                            rope(FO, rFO, i, tR, rtR)
                            G(lambda e, qa=qa, FO=FO: e.tensor_copy(out=qa[:, :, 0:64], in_=FO[:].rearrange("p (h d) -> p h d", h=4)),
                              [rFO], [rqa])
                            if own >= 4:
                                P.phase = "gate"
                                for pp in range(2):
                                    T(lambda e, pp=pp, FO=FO: e.matmul(psG[:, pp * 128:(pp + 1) * 128],
                                                                       lhsT=FO[:, pp * 128:(pp + 1) * 128], rhs=ident32,
                                                                       start=True, stop=True),
                                      [rFO, rcst], [rpsG])
                                A(lambda e: e.copy(out=Fp[5][:], in_=psG[:, 0:256]), [rpsG], [rF[5]])
                                for h in range(4):
                                    T(lambda e, h=h, own=own: e.matmul(
                                        psG[:, 256 + h * 8:256 + h * 8 + own],
                                        lhsT=Fp[5][:, (h // 2) * 128:(h // 2) * 128 + 128],
                                        rhs=kmT32[:, h // 2, h % 2, 0:own], start=True, stop=True),
                                      [rF[5], rkm], [rpsG])
                                V(lambda e: e.memset(gm[:], -1.0e30), w=[rgm])
                                V(lambda e, own=own: e.tensor_copy(
                                    out=gm[:, :, 0:own], in_=psG[:, 256:288].rearrange("p (h n) -> p h n", h=4)[:, :, 0:own]),
                                  [rpsG], [rgm])
                                for h in range(4):
                                    V(lambda e, h=h: e.max(out=top8[:, h, :], in_=gm[:, h, :]), [rgm], [rtop8])
                                V(lambda e: e.tensor_tensor(out=selb[:], in0=gm[:], in1=top8[:, :, 2:3].to_broadcast([128, 4, 8]),
                                                            op=ALU.is_ge), [rgm, rtop8], [rselb])
                                V(lambda e, qa=qa: e.tensor_scalar(out=qa[:, :, 64:72], in0=selb[:], scalar1=30000.0,
                                                                   scalar2=-30000.0, op0=ALU.mult, op1=ALU.add), [rselb], [rqa])
                                V(lambda e, qa=qa, own=own: e.memset(qa[:, :, 64 + own:65 + own], 0.0), w=[rqa])
                            else:
                                G(lambda e, qa=qa: e.memset(qa[:, :, 64:72], 0.0), w=[rqa])
                            P.phase = None
                            for h in range(4):
                                T(lambda e, h=h, qa=qa: e.transpose(out=psT[0:72, h * 128:(h + 1) * 128], in_=qa[:, h, :],
                                                                    identity=idb[:]), [rqa, ridb], [rpsT])
                            src3 = psT[0:72, 0:512].rearrange("p (h t) -> p h t", h=4)
                            evac(lambda e, r=r, src3=src3, qT3=qT3: e.tensor_copy(out=qT3[0:72, :, r * 128:(r + 1) * 128], in_=src3),
                                 lambda e, r=r, src3=src3, qT3=qT3: e.copy(out=qT3[0:72, :, r * 128:(r + 1) * 128], in_=src3),
                                 [rpsT], [rqT])
                        attn_chunk(Q, 4, 64, 72, 0.125, lambda Q: list(range(4 * Q + 4)), True,
                                   lambda h, kt: kT[0:72, h, kt * 128:(kt + 1) * 128], lambda kt: rkT[kt],
                                   lambda h, kt: v_aug[:, kt, h, :], lambda kt: rv[kt],
                                   lambda h, qT3=qT3: qT3[0:72, h, :], rqT, sz, rsz)
                        for r in range(4):
                            outproj_tile(4 * Q + r, r, last, obanks=([(psO[0], rpsO[0]), (psO[1], rpsO[1])] if OPB else None))
                elif kind == "M":
                    w1 = nxt("w", 3); w2 = nxt("w", 3)
                    wkvv = wkv_d[l].rearrange("(c p) n -> p c n", p=128)
                    load_w(wb[w1][:, :, 0:256], rwb[w1][0], wkvv[:, :, 256 * g:256 * g + 256])
                    load_w(wb[w1][:, :, 256:512], rwb[w1][1], wkvv[:, :, 512 + 256 * g:512 + 256 * g + 256])
                    load_w(wb[w2][:, :, 0:256], rwb[w2][0], w_in_cols(l, 3072 + 256 * g, 256))
                    load_w(wb[w2][:, :, 256:512], rwb[w2][1], w_in_cols(l, 4608 + 256 * g, 256))
                    load_w(wo[:], rwo, wout_d[l, 1024 + 256 * g:1024 + 256 * g + 256, :].rearrange("(c p) n -> p c n", p=128))
                    if g == 0 or ("M0" not in groups):
                        barrier()
                        DMA("sync", lambda e, l=l: e.dma_start(out=g_bcv, in_=mng_d[l:l + 1, :].partition_broadcast(128)),
                            w=[rg_bc])
                        for mt in range(2):
                            DMA("sync", lambda e, mt=mt: e.dma_start(out=stage, in_=mem_d[mt * 128:(mt + 1) * 128, :]),
                                w=[rstage])
                            rms_to_T(stage, rstage, g_bcv, rg_bc, memT, rmemT, mt * 128, ss16, r_ss16, t16, r_t16,
                                     rstd16, r_rstd16, mt)
                    for mt in range(2):
                        (tA, rtA), (tB, rtB), (FO, rFO), (tR, rtR) = SC[mt % 2]
                        pa, rpa = proj(mt * 128, rmemT, wb[w1], rwb[w1], lhsT_src=memT)
                        A(lambda e, mt=mt, pa=pa: e.copy(out=vm_aug[:, mt, :, 0:128],
                                                         in_=pa[:, 256:512].rearrange("p (h d) -> p h d", h=2)),
                          [rpa], [rvm])
                        headnorm(pa[:, 0:256], rpa, 2, 128, gmk_bc[:], FO, rFO, tA, rtA, tB, rtB)
                        G(lambda e, mt=mt, FO=FO: e.tensor_copy(out=Bp[mt % 2][:], in_=FO[:]), [rFO], [rB[mt % 2]])
                        for hh in range(2):
                            T(lambda e, hh=hh, mt=mt: e.transpose(out=psT[:, hh * 128:(hh + 1) * 128],
                                                                  in_=Bp[mt % 2][:, hh * 128:(hh + 1) * 128],
                                                                  identity=idb[:]), [rB[mt % 2], ridb], [rpsT])
                        src3 = psT[:, 0:256].rearrange("p (h t) -> p h t", h=2)
                        evac(lambda e, mt=mt, src3=src3: e.tensor_copy(out=kmT[:, :, mt * 128:(mt + 1) * 128], in_=src3),
                             lambda e, mt=mt, src3=src3: e.copy(out=kmT[:, :, mt * 128:(mt + 1) * 128], in_=src3),
                             [rpsT], [rkmT])
                    for Q in range(4):
                        qm3 = qTs[Q % 2][:, 0:1024].rearrange("p (h t) -> p h t", h=2)
                        rqT = rqTs[Q % 2]
                        sz = szs[Q % 2]; rsz = rszs[Q % 2]
                        for r in range(4):
                            i = 4 * Q + r
                            (tA, rtA), (tB, rtB), (FO, rFO), (tR, rtR) = SC[i % 2]
                            pa, rpa = proj(i * 128, rhT[i], wb[w2], rwb[w2])
                            fin = silu_ps(sz[:, r, :], rsz[r], pa[:, 256:512], rpa, defer=True)
                            headnorm(pa[:, 0:256], rpa, 2, 128, gmq_bc[:], FO, rFO, tA, rtA, tB, rtB)
                            fin()
                            G(lambda e, i=i, FO=FO: e.tensor_copy(out=Bp[i % 2][:], in_=FO[:]), [rFO], [rB[i % 2]])
                            for hh in range(2):
                                T(lambda e, hh=hh, i=i: e.transpose(out=psT[:, hh * 128:(hh + 1) * 128],
                                                                    in_=Bp[i % 2][:, hh * 128:(hh + 1) * 128], identity=idb[:]),
                                  [rB[i % 2], ridb], [rpsT])
                            src3 = psT[:, 0:256].rearrange("p (h t) -> p h t", h=2)
                            evac(lambda e, r=r, src3=src3, qm3=qm3: e.tensor_copy(out=qm3[:, :, r * 128:(r + 1) * 128], in_=src3),
                                 lambda e, r=r, src3=src3, qm3=qm3: e.copy(out=qm3[:, :, r * 128:(r + 1) * 128], in_=src3),
                                 [rpsT], [rqT])
                        attn_chunk(Q, 2, 128, 128, float(128 ** -0.5), lambda Q: [0, 1], False,
                                   lambda h, kt: kmT[:, h, kt * 128:(kt + 1) * 128], lambda kt: rkmT,
                                   lambda h, kt: vm_aug[:, kt, h, :], lambda kt: rvm,
                                   lambda h, qm3=qm3: qm3[:, h, :], rqT, sz, rsz)
                        for r in range(4):
                            outproj_tile(4 * Q + r, r, last, obanks=([(psO[0], rpsO[0]), (psO[1], rpsO[1])] if OPB else None))
                else:
                    w1 = nxt("w", 3); w2 = nxt("w", 3)
                    load_w(wb[w1][:, :, 0:256], rwb[w1][0], w_in_cols(l, 1536 + 256 * g, 256))
                    load_w(wb[w1][:, :, 256:512], rwb[w1][1], w_in_cols(l, 2048 + 256 * g, 256))
                    load_w(wb[w2][:, :, 0:256], rwb[w2][0], w_in_cols(l, 2560 + 256 * g, 256))
                    load_w(wb[w2][:, :, 256:512], rwb[w2][1], w_in_cols(l, 4096 + 256 * g, 256))
                    load_w(wo[:], rwo, wout_d[l, 512 + 256 * g:512 + 256 * g + 256, :].rearrange("(c p) n -> p c n", p=128))
                    barrier()
                    if l != 0:
                        DMA("sync", lambda e, g=g: e.dma_start(out=lb_g, in_=lbl_d[1:2, 256 * g:256 * g + 256].partition_broadcast(128)), w=[rlb])
                        DMA("sync", lambda e, g=g: e.dma_start(out=oml_g, in_=lbl_d[0:1, 256 * g:256 * g + 256].partition_broadcast(128)), w=[rlb])
                        V(lambda e: e.tensor_tensor(out=lb_g, in0=lb_g, in1=oml_g, op=ALU.subtract), [rlb], [rlb])
                        A(lambda e: e.activation(out=lb_g, in_=lb_g, func=AF.Exp, scale=-1.0), [rlb], [rlb])
                        A(lambda e: e.activation(out=lb_g, in_=lb_g, func=AF.Ln, bias=1.0), [rlb], [rlb])
                        A(lambda e: e.activation(out=lb_g, in_=lb_g, func=AF.Exp, scale=-1.0), [rlb], [rlb])
                        V(lambda e: e.tensor_scalar(out=oml_g, in0=lb_g, scalar1=-1.0, scalar2=1.0, op0=ALU.mult, op1=ALU.add),
                          [rlb], [rlb])
                    for hh in range(2):
                        G(lambda e, hh=hh: e.memset(S32[:, hh, :], 0.0), w=[rS32[hh]])
                        G(lambda e, hh=hh: e.memset(Sbf[:, 0, hh, :], 0.0), w=[rSbf[0][hh]])
                    Tri32 = cst[:, 256:384]
                    TriE32 = cst[:, 384:512]
                    for i in range(NT):
                        Fs, rFs, Bs, rBs = HSETS[i % 2]
                        sl = i % 4
                        sz = szs[(i // 4) % 2]; rsz = rszs[(i // 4) % 2]
                        pq, rpq = proj(i * 128, rhT[i], wb[w1], rwb[w1])
                        finq = silu_ps(Fs[0], rFs[0], pq[:, 0:256], rpq, defer=True)
                        A(lambda e, pq=pq, Fs=Fs: e.activation(out=Fs[1], in_=pq[:, 256:512], func=AF.Exp, scale=-1.0), [rpq], [rFs[1]])
                        A(lambda e, Fs=Fs: e.activation(out=Fs[1], in_=Fs[1], func=AF.Ln, bias=1.0), [rFs[1]], [rFs[1]])
                        finq()
                        pi_, rpi = proj(i * 128, rhT[i], wb[w2], rwb[w2])
                        silu_ps(sz[:, sl, :], rsz[sl], pi_[:, 256:512], rpi)
                        V(lambda e, pi_=pi_, Bs=Bs: e.tensor_copy(out=Bs[0], in_=pi_[:, 0:256]), [rpi], [rBs[0]])
                        if l == 0:
                            A(lambda e, Fs=Fs: e.activation(out=Fs[2], in_=Fs[1], func=AF.Copy, scale=-1.0), [rFs[1]], [rFs[2]])
                            A(lambda e, Fs=Fs: e.activation(out=Fs[1], in_=Fs[1], func=AF.Exp, scale=-1.0), [rFs[1]], [rFs[1]])
                        else:
                            A(lambda e, Fs=Fs: e.activation(out=Fs[1], in_=Fs[1], func=AF.Exp, scale=-1.0), [rFs[1]], [rFs[1]])
                            V(lambda e, g=g, Fs=Fs: e.tensor_tensor(out=Fs[1], in0=Fs[1], in1=oml_g,
                                                                    op=ALU.mult), [rFs[1], rlb], [rFs[1]])
                            V(lambda e, g=g, Fs=Fs: e.tensor_tensor(out=Fs[1], in0=Fs[1], in1=lb_g,
                                                                    op=ALU.add), [rFs[1], rlb], [rFs[1]])
                            A(lambda e, Fs=Fs: e.activation(out=Fs[2], in_=Fs[1], func=AF.Ln), [rFs[1]], [rFs[2]])
                        (G if GOFF else V)(lambda e, Fs=Fs: e.tensor_scalar(out=Fs[3], in0=Fs[1], scalar1=-1.0, scalar2=1.0, op0=ALU.mult,
                                                                            op1=ALU.add), [rFs[1]], [rFs[3]])
                        T(lambda e, Fs=Fs: e.matmul(psG[:, 0:256], lhsT=Tri32, rhs=Fs[2], start=True, stop=True),
                          [rcst, rFs[2]], [rpsG])
                        T(lambda e, Fs=Fs: e.matmul(psG[:, 256:512], lhsT=TriE32, rhs=Fs[2], start=True, stop=True),
                          [rcst, rFs[2]], [rpsG])
                        for hh in range(2):
                            T(lambda e, hh=hh, Fs=Fs: e.matmul(psS[1][:, 256 + 2 * hh:256 + 2 * hh + 2],
                                                               lhsT=Fs[2][:, hh * 128:(hh + 1) * 128],
                                                               rhs=cst[:, 512:514], start=True, stop=True), [rFs[2], rcst], [rpsS[1]])
                        dsl = dec8[:, 4 * (i % 2):4 * (i % 2) + 4]
                        rds = r_dec8[i % 2]
                        A(lambda e, dsl=dsl: e.activation(out=dsl, in_=psS[1][:, 256:260], func=AF.Exp), [rpsS[1]], [rds])
                        A(lambda e, Fs=Fs: e.activation(out=Fs[4], in_=psG[:, 0:256], func=AF.Exp), [rpsG], [rFs[4]])
                        A(lambda e, Fs=Fs: e.activation(out=Fs[5], in_=psG[:, 0:256], func=AF.Exp, scale=-1.0), [rpsG], [rFs[5]])
                        A(lambda e, Fs=Fs: e.activation(out=Fs[6], in_=psG[:, 256:512], func=AF.Exp), [rpsG], [rFs[6]])
                        V(lambda e, Fs=Fs, Bs=Bs: e.tensor_tensor(out=Bs[1], in0=Fs[0], in1=Fs[4], op=ALU.mult), [rFs[0], rFs[4]], [rBs[1]])
                        G(lambda e, Fs=Fs, Bs=Bs: e.tensor_tensor(out=Bs[2], in0=Fs[3], in1=Fs[5], op=ALU.mult), [rFs[3], rFs[5]], [rBs[2]])
                        G(lambda e, Fs=Fs, Bs=Bs: e.tensor_tensor(out=Bs[3], in0=Fs[3], in1=Fs[6], op=ALU.mult), [rFs[3], rFs[6]], [rBs[3]])
                        for hh in range(2):
                            T(lambda e, hh=hh, Bs=Bs: e.transpose(out=psT[:, hh * 128:(hh + 1) * 128], in_=Bs[1][:, hh * 128:(hh + 1) * 128],
                                                                  identity=idb[:]), [rBs[1], ridb], [rpsT])
                            T(lambda e, hh=hh, Bs=Bs: e.transpose(out=psT[:, 256 + hh * 128:256 + (hh + 1) * 128],
                                                                  in_=Bs[2][:, hh * 128:(hh + 1) * 128], identity=idb[:]),
                              [rBs[2], ridb], [rpsT])
                        pq3 = psT[:, 0:256].rearrange("p (h t) -> p h t", h=2)
                        A(lambda e, Bs=Bs: e.copy(out=Bs[4], in_=psT[:, 0:256]), [rpsT], [rBs[4]])
                        A(lambda e, pq3=pq3: e.copy(out=qTA[:, :, 0:64], in_=pq3[:, :, 0:64]), [rpsT], [rqTA])
                        V(lambda e, Bs=Bs: e.tensor_copy(out=Bs[5], in_=psT[:, 256:512]), [rpsT], [rBs[5]])
                        V(lambda e, pq3=pq3: e.tensor_copy(out=qTB[:, :, 64:128], in_=pq3[:, :, 64:128]), [rpsT], [rqTB])
                        cur = i % 2
                        nxtb = 1 - cur
                        for hh in range(2):
                            hs = slice(hh * 128, (hh + 1) * 128)
                            T(lambda e, hs=hs, Bs=Bs: e.matmul(psS[1][:, hs], lhsT=Bs[5][:, hs], rhs=Bs[4][:, hs], start=True, stop=True),
                              [rBs[5], rBs[4]], [rpsS[1]])
                        for hh in range(2):
                            hs = slice(hh * 128, (hh + 1) * 128)
                            V(lambda e, hs=hs, Bs=Bs: e.tensor_tensor(out=Bs[6][:, hs], in0=psS[1][:, hs], in1=hgmb[:], op=ALU.mult),
                              [rpsS[1], rhgmb], [rBs[6]])
                        for hh in range(2):
                            hs = slice(hh * 128, (hh + 1) * 128)
                            T(lambda e, hs=hs, hh=hh, Bs=Bs: e.matmul(psO[hh][:, 0:128], lhsT=Bs[6][:, hs], rhs=Bs[0][:, hs],
                                                                      start=True, stop=False), [rBs[6], rBs[0]], [rpsO[hh]])
                            T(lambda e, hh=hh, cur=cur: e.matmul(psO[hh][:, 0:128], lhsT=qTA[:, hh, :], rhs=Sbf[:, cur, hh, :],
                                                                 start=False, stop=False), [rqTA, rSbf[cur][hh]], [rpsO[hh]])
                        for hh in range(2):
                            hs = slice(hh * 128, (hh + 1) * 128)
                            T(lambda e, hs=hs, Bs=Bs: e.matmul(psS[0][:, hs], lhsT=Bs[3][0:64, hs], rhs=Bs[0][0:64, hs],
                                                               start=True, stop=True), [rBs[3], rBs[0]], [rpsS[0], rrow])
                        for hh in range(2):
                            hs = slice(hh * 128, (hh + 1) * 128)
                            V(lambda e, hs=hs, hh=hh, dsl=dsl: e.scalar_tensor_tensor(out=S32[:, hh, :], in0=S32[:, hh, :],
                                                                                      scalar=dsl[:, 2 * hh:2 * hh + 1], in1=psS[0][:, hs],
                                                                                      op0=ALU.mult, op1=ALU.add),
                              [rS32[hh], rds, rpsS[0]], [rS32[hh]])
                            G(lambda e, hh=hh, nxtb=nxtb: e.tensor_copy(out=Sbf[:, nxtb, hh, :], in_=S32[:, hh, :]),
                              [rS32[hh]], [rSbf[nxtb][hh]])
                        for hh in range(2):
                            T(lambda e, hh=hh, nxtb=nxtb: e.matmul(psO[hh][:, 0:128], lhsT=qTB[:, hh, :], rhs=Sbf[:, nxtb, hh, :],
                                                                   start=False, stop=True), [rqTB, rSbf[nxtb][hh]], [rpsO[hh], rrow])
                        for hh in range(2):
                            hs = slice(hh * 128, (hh + 1) * 128)
                            T(lambda e, hs=hs, Bs=Bs: e.matmul(psS[0][:, hs], lhsT=Bs[3][64:128, hs], rhs=Bs[0][64:128, hs],
                                                               start=True, stop=True), [rBs[3], rBs[0]], [rpsS[0], rrow])
                        for hh in range(2):
                            hs = slice(hh * 128, (hh + 1) * 128)
                            V(lambda e, hs=hs, hh=hh, dsl=dsl: e.scalar_tensor_tensor(out=S32[:, hh, :], in0=S32[:, hh, :],
                                                                                      scalar=dsl[:, 2 * hh + 1:2 * hh + 2], in1=psS[0][:, hs],
                                                                                      op0=ALU.mult, op1=ALU.add),
                              [rS32[hh], rds, rpsS[0]], [rS32[hh]])
                        for hh in range(2):
                            G(lambda e, hh=hh, nxtb=nxtb: e.tensor_copy(out=Sbf[:, nxtb, hh, :], in_=S32[:, hh, :]),
                              [rS32[hh]], [rSbf[nxtb][hh]])
                        ssl = sso4[:, 2 * (i % 2):2 * (i % 2) + 2]
                        r_sso = r_sso2[i % 2]
                        for hh in range(2):
                            A(lambda e, hh=hh, Fs=Fs, ssl=ssl: e.activation(out=Fs[7][:, 0:128], in_=psO[hh][:, 0:128], func=AF.Square,
                                                                            accum_out=ssl[:, hh:hh + 1]), [rpsO[hh]], [rFs[7], r_sso])
                        A(lambda e, ssl=ssl: e.activation(out=ssl, in_=ssl, func=AF.Ln, scale=1.0 / 128, bias=EPS), [r_sso], [r_sso])
                        A(lambda e, ssl=ssl: e.activation(out=ssl, in_=ssl, func=AF.Exp, scale=-0.5), [r_sso], [r_sso])
                        for hh in range(2):
                            hs = slice(hh * 128, (hh + 1) * 128)
                            V(lambda e, hh=hh, hs=hs, Fs=Fs, ssl=ssl: e.scalar_tensor_tensor(out=Fs[8][:, hs], in0=psO[hh][:, 0:128],
                                                                                             scalar=ssl[:, hh:hh + 1], in1=go_bc[:],
                                                                                             op0=ALU.mult, op1=ALU.mult),
                              [rpsO[hh], r_sso, rgains], [rFs[8]])
                        G(lambda e, Fs=Fs, sl=sl, sz=sz: e.tensor_tensor(out=y_tok[:, sl, :], in0=Fs[8], in1=sz[:, sl, :], op=ALU.mult),
                          [rFs[8], rsz[sl]], [ry[sl]])
                        outproj_tile(i, sl, last, obanks=[(psS[0], rpsS[0]), (psS[0], rpsS[0])])
            if not glist and last_layer:
                for i in range(NT):
                    out_dmas.append(DMA("sync", lambda e, i=i: e.dma_start(out=out_d[i * 128:(i + 1) * 128, :], in_=x_tok[:, i, :]),
                                        r=[rx[i]]))
        if SCHED:
            if SCHED2:
                P.schedule2(SDELTA)
            else:
                P.schedule()
        P.emit(st, out_dmas)
    build_nc.stats = P.stats
    return nc


_CACHE = {}


def _get_nc(layers, groups):
    key = (tuple(layers), tuple(groups))
    if key not in _CACHE:
        _CACHE[key] = build_nc(layers, groups)
    return _CACHE[key]


def run(inputs, layers=(0, 1), groups=ALL_GROUPS, cores=8):
    nc = _get_nc(layers, groups)
    f = lambda a: np.ascontiguousarray(np.asarray(a))
    cst = make_consts()
    shared = {k: f(inputs[k]).astype(np.float32, copy=False) for k in
              ("norm_g", "w_in", "w_out", "moba_q_norm", "moba_k_norm", "hgrn_lb_logits", "hgrn_o_norm",
               "mem_norm_g", "w_mem_kv", "mem_q_norm", "mem_k_norm")}
    x = f(inputs["x"]); mem = f(inputs["mem"]); pos = f(inputs["positions"]).astype(np.int32, copy=False)
    in_maps = []
    for b in range(cores):
        m = dict(shared)
        m["x"] = x[b]
        m["mem"] = mem[b]
        m["pos"] = pos[b].reshape(16, 128)
        m["cst"] = cst
        in_maps.append(m)
    res = run_bass_kernel_spmd(nc, in_maps, core_ids=list(range(cores)))
    return np.stack([np.asarray(r["out"]) for r in res.results], axis=0)


def kernel(x, mem, positions, norm_g, w_in, w_out, moba_q_norm, moba_k_norm, hgrn_lb_logits,
           hgrn_o_norm, mem_norm_g, w_mem_kv, mem_q_norm, mem_k_norm):
    inputs = dict(x=x, mem=mem, positions=positions, norm_g=norm_g, w_in=w_in, w_out=w_out,
                  moba_q_norm=moba_q_norm, moba_k_norm=moba_k_norm, hgrn_lb_logits=hgrn_lb_logits,
                  hgrn_o_norm=hgrn_o_norm, mem_norm_g=mem_norm_g, w_mem_kv=w_mem_kv,
                  mem_q_norm=mem_q_norm, mem_k_norm=mem_k_norm)
    return run(inputs).astype(np.float32, copy=False)
```

```python
import numpy as np
from contextlib import ExitStack
import concourse.bass as bass
import concourse.mybir as mybir
from concourse.bass_utils import run_bass_kernel_spmd

F32 = mybir.dt.float32
BF16 = mybir.dt.bfloat16
I32 = mybir.dt.int32
AF = mybir.ActivationFunctionType
ALU = mybir.AluOpType
AX = mybir.AxisListType

S = 2048
D = 1024
NT = 16
EPS = 1e-6
NCST = 576
import os as _os0
ALL_GROUPS = tuple(_os0.environ.get("ORDER", "A0,A1,H0,H1,M0,M1").split(","))
import os as _os
SCHED = _os.environ.get("SCHED", "1") == "1"
PAR1 = int(_os.environ.get("PAR1", "1"))
PAR2 = int(_os.environ.get("PAR2", "1"))
GPAR = int(_os.environ.get("GPAR", "1"))
PS3 = int(_os.environ.get("PS3", "3"))
MASKV = int(_os.environ.get("MASKV", "1"))
OPB = int(_os.environ.get("OPB", "1"))
SCHED2 = int(_os.environ.get("SCHED2", "1"))
SDELTA = float(_os.environ.get("SDELTA", "100"))
LATX = float(_os.environ.get("LATX", "180"))
PEK = float(_os.environ.get("PEK", "0.65"))
ACTK = float(_os.environ.get("ACTK", "1.0"))
DVEK = float(_os.environ.get("DVEK", "1.0"))
POOLK = float(_os.environ.get("POOLK", "1.0"))
LATS = float(_os.environ.get("LATS", "60"))
XQ = int(_os.environ.get("XQ", "0"))
GOFF = int(_os.environ.get("GOFF", "0"))
TRANS = int(_os.environ.get("TRANS", "1"))
PRUNE = int(_os.environ.get("PRUNE", "1"))
WIDE = int(_os.environ.get("WIDE", "1"))


class Res:
    __slots__ = ("name", "w", "r", "excl")

    def __init__(self, name, excl=False):
        self.name = name
        self.w = None
        self.r = []
        self.excl = excl


class _Rec:
    def __init__(self):
        self.name = None
        self.args = ()
        self.kw = {}

    def __getattr__(self, name):
        def f(*a, **k):
            self.name, self.args, self.kw = name, a, k
            return self
        return f


def _free_size(ap):
    n = 1
    for d in list(ap.shape)[1:]:
        n *= int(d)
    return n


class Op:
    __slots__ = ("eng", "fn", "deps", "sdeps", "sig", "idx", "dma", "sem", "val", "i", "cost", "start")

    def __init__(self, eng, fn, deps, sdeps, dma):
        self.eng = eng
        self.fn = fn
        self.deps = deps
        self.sdeps = sdeps
        self.sig = False
        self.idx = 0
        self.dma = dma
        self.sem = None
        self.val = 0
        self.i = 0
        self.start = 0.0
        rec = _Rec()
        fn(rec)
        out = rec.kw.get("out", rec.args[0] if rec.args else None)
        n = _free_size(out) if out is not None else 64
        if dma:
            c = 2000.0 + n * int(out.shape[0]) * 4 / 120.0
        elif eng == "tensor":
            if rec.name == "transpose":
                c = 110.0
            else:
                lhsT = rec.kw.get("lhsT")
                f32 = lhsT is not None and lhsT.dtype == F32
                c = PEK * (64.0 + max(n, 64) / 2.0) * (4.0 if f32 else 1.0)
        elif eng == "scalar":
            c = ACTK * (200.0 + n / 1.2)
        elif eng == "vector":
            c = DVEK * (120.0 + n / 0.96 * (8.0 if rec.name == "reciprocal" else 1.0))
        else:
            c = POOLK * (300.0 + n / 0.5)
        self.cost = c


class Prog:
    ENGS = ["tensor", "vector", "scalar", "gpsimd", "sync"]

    def __init__(self, nc):
        self.nc = nc
        self.ops = []

    phase = None
    tok = None
    tokset = ()

    def op(self, eng, fn, reads=(), writes=(), dma=False):
        if self.phase == "gate" and self.tok is not None:
            writes = list(writes) + [self.tok]
        elif self.phase == "chain" and eng in self.tokset:
            reads = list(reads) + [self.tok]
        deps, sdeps = {}, {}

        def add(d):
            if d.dma or dma or d.eng != eng or eng != "tensor":
                deps[id(d)] = d
            else:
                sdeps[id(d)] = d
        for r in reads:
            if r.w is not None:
                add(r.w)
            if r.excl:
                for d in r.r:
                    if d.eng != eng:
                        add(d)
        for w in writes:
            if w.w is not None:
                add(w.w)
            for d in w.r:
                add(d)
        o = Op(eng, fn, list(deps.values()), list(sdeps.values()), dma)
        for r in reads:
            r.r.append(o)
        for w in writes:
            w.w = o
            w.r = []
        self.ops.append(o)
        return o

    def schedule(self):
        import heapq
        ops = self.ops
        for i, o in enumerate(ops):
            o.i = i
        succs = [[] for _ in ops]
        npred = [0] * len(ops)
        for o in ops:
            ds = o.deps + o.sdeps
            npred[o.i] = len(ds)
            for d in ds:
                succs[d.i].append(o)
        ready = [0.0] * len(ops)
        free = {e: 0.0 for e in self.ENGS}
        heap = [(0.0, o.i) for o in ops if npred[o.i] == 0]
        heapq.heapify(heap)
        done = 0
        while heap:
            t, i = heapq.heappop(heap)
            o = ops[i]
            st = max(ready[i], free[o.eng])
            if st > t + 1e-9:
                heapq.heappush(heap, (st, i))
                continue
            o.start = st
            if o.dma:
                free[o.eng] = st + 150.0
            else:
                free[o.eng] = st + o.cost
            fin = st + o.cost
            done += 1
            for sc in succs[i]:
                lat = 60.0 if (sc.eng == o.eng and not o.dma) else 180.0
                if fin + lat > ready[sc.i]:
                    ready[sc.i] = fin + lat
                npred[sc.i] -= 1
                if npred[sc.i] == 0:
                    heapq.heappush(heap, (max(ready[sc.i], free[sc.eng]), sc.i))
        assert done == len(ops), (done, len(ops))
        self.ops = sorted(ops, key=lambda o: (o.start, o.i))
        self.est_ns = max(o.start + o.cost for o in ops)

    def schedule2(self, delta=120.0):
        ops = self.ops
        n = len(ops)
        for i, o in enumerate(ops):
            o.i = i
        succs = [[] for _ in ops]
        npred = [0] * n
        for o in ops:
            ds = o.deps + o.sdeps
            npred[o.i] = len(ds)
            for d in ds:
                succs[d.i].append(o)
        blev = [0.0] * n
        for o in reversed(ops):
            b = 0.0
            for sc in succs[o.i]:
                lat = LATS if (sc.eng == o.eng and not o.dma) else LATX
                v = lat + blev[sc.i]
                if v > b:
                    b = v
            blev[o.i] = b + o.cost
        ready = [0.0] * n
        free = {e: 0.0 for e in self.ENGS}
        rsets = {e: [] for e in self.ENGS}
        for o in ops:
            if npred[o.i] == 0:
                rsets[o.eng].append(o.i)
        done = 0
        while done < n:
            best_e, best_t = None, 1e30
            for e in self.ENGS:
                rs = rsets[e]
                if not rs:
                    continue
                t = min(ready[i] for i in rs)
                if t < free[e]:
                    t = free[e]
                if t < best_t:
                    best_t, best_e = t, e
            e = best_e
            rs = rsets[e]
            lim = best_t + delta
            pick, pb = -1, -1.0
            for i in rs:
                if ready[i] <= lim and blev[i] > pb:
                    pb, pick = blev[i], i
            rs.remove(pick)
            o = ops[pick]
            st = max(ready[pick], free[e])
            o.start = st
            free[e] = st + (150.0 if o.dma else o.cost)
            fin = st + o.cost
            done += 1
            for sc in succs[pick]:
                lat = LATS if (sc.eng == o.eng and not o.dma) else LATX
                if fin + lat > ready[sc.i]:
                    ready[sc.i] = fin + lat
                npred[sc.i] -= 1
                if npred[sc.i] == 0:
                    rsets[sc.eng].append(sc.i)
        self.ops = sorted(ops, key=lambda o: (o.start, o.i))
        self.est_ns = max(o.start + o.cost for o in ops)

    def emit(self, stack, final_deps, ndma_sems=8):
        nc = self.nc
        if PRUNE:
            pos = {id(o): k for k, o in enumerate(self.ops)}
            for o in self.ops:
                best = {}
                keep = []
                for d in o.deps:
                    if d.dma:
                        keep.append(d)
                    elif d.eng not in best or pos[id(d)] > pos[id(best[d.eng])]:
                        best[d.eng] = d
                o.deps = keep + list(best.values())
        for o in self.ops:
            for d in o.deps:
                d.sig = True
        for d in final_deps:
            d.sig = True
        sems = {e: stack.enter_context(nc.semaphore("s_" + e)) for e in self.ENGS}
        cnt = {e: 0 for e in self.ENGS}
        pools, pool_i, pre_wait = {}, {}, {}
        for o in self.ops:
            if o.dma:
                if o.eng not in pools:
                    pools[o.eng] = [[stack.enter_context(nc.semaphore("d_%s_%d" % (o.eng, i))), 0]
                                    for i in range(ndma_sems)]
                    pool_i[o.eng] = 0
                p = pools[o.eng][pool_i[o.eng] % ndma_sems]
                pool_i[o.eng] += 1
                if p[1] > 0:
                    pre_wait[id(o)] = (p[0], p[1])
                p[1] += 16
                o.sem = p[0]
                o.val = p[1]
            elif o.sig:
                cnt[o.eng] += 1
                o.idx = cnt[o.eng]
        per = {e: [o for o in self.ops if o.eng == e] for e in self.ENGS}
        self.stats = {e: len(per[e]) for e in self.ENGS}
        known = {e: {} for e in self.ENGS}
        kn = {}
        plan = {}
        nw = 0

        def semkey(d):
            return (d.sem, d.val) if d.dma else (sems[d.eng], d.idx)

        for o in self.ops:
            kd = known[o.eng]
            ws = []
            for d in o.deps:
                sm, val = semkey(d)
                if kd.get(id(sm), (None, 0))[1] < val:
                    ws.append((sm, val))
                    kd[id(sm)] = (sm, val)
                if TRANS:
                    for k2, (s2, v2) in kn[id(d)].items():
                        if kd.get(k2, (None, 0))[1] < v2:
                            kd[k2] = (s2, v2)
            if o.dma:
                pw = pre_wait.get(id(o))
                if pw and kd.get(id(pw[0]), (None, 0))[1] < pw[1]:
                    ws.append(pw)
                    kd[id(pw[0])] = pw
            plan[id(o)] = ws
            nw += len(ws)
            if o.dma or o.sig:
                mine = dict(kd)
                sm, val = semkey(o)
                mine[id(sm)] = (sm, val)
                kn[id(o)] = mine
        fin_w = []
        kd = known["sync"]
        for d in final_deps:
            sm, val = semkey(d)
            if kd.get(id(sm), (None, 0))[1] < val:
                fin_w.append((sm, val))
                kd[id(sm)] = (sm, val)
        self.stats["waits"] = nw
        self.stats["sigs"] = {e: sum(1 for o in per[e] if o.sig and not o.dma) for e in self.ENGS}
        block = stack.enter_context(nc.Block())

        def mk(e):
            def body(engobj):
                for o in per[e]:
                    for sm, val in plan[id(o)]:
                        engobj.wait_ge(sm, val)
                    if o.dma:
                        o.fn(engobj).then_inc(o.sem, 16)
                    else:
                        ins = o.fn(engobj)
                        if o.sig:
                            ins.then_inc(sems[e], 1)
                if e == "sync":
                    for sm, val in fin_w:
                        engobj.wait_ge(sm, val)
            return body

        block.tensor(mk("tensor"))
        block.vector(mk("vector"))
        block.scalar(mk("scalar"))
        block.gpsimd(mk("gpsimd"))
        block.sync(mk("sync"))


def make_consts():
    c = np.zeros((128, NCST), np.float32)
    i = np.arange(128)
    c[:, 0:128] = np.eye(128)
    c[:, 128:256] = (i[None, :] >= i[:, None])
    same = (i[:, None] // 64) == (i[None, :] // 64)
    c[:, 256:384] = same & (i[:, None] <= i[None, :])
    c[:, 384:512] = same & (i[:, None] > i[None, :])
    c[:, 512] = i < 64
    c[:, 513] = i >= 64
    c[:, 514] = 1.0
    f64 = 500000.0 ** (-np.arange(8, dtype=np.float64) / 8.0)
    f = f64.astype(np.float32)
    flo = (f64 - f.astype(np.float64)).astype(np.float32)
    c[:, 515:523] = f[None, :]
    c[:, 523:531] = f[None, :]
    c[:, 547:555] = flo[None, :]
    c[:, 555:563] = flo[None, :]
    c[:, 531:539] = 0.0
    c[:, 539:547] = np.pi / 2
    return c


def build_nc(layers=(0, 1), groups=ALL_GROUPS):
    nc = bass.Bass("TRN2", target_bir_lowering=False)

    def din(name, shape, d=F32):
        return nc.dram_tensor(name, shape, d, kind="ExternalInput").ap()

    x_d = din("x", [S, D])
    mem_d = din("mem", [256, D])
    pos_d = din("pos", [16, 128], I32)
    ng_d = din("norm_g", [2, D])
    win_d = din("w_in", [2, D, 5120])
    wout_d = din("w_out", [2, 1536, D])
    gq_d = din("moba_q_norm", [2, 64])
    gk_d = din("moba_k_norm", [2, 64])
    lbl_d = din("hgrn_lb_logits", [2, 512])
    go_d = din("hgrn_o_norm", [2, 128])
    mng_d = din("mem_norm_g", [2, D])
    wkv_d = din("w_mem_kv", [2, D, 1024])
    gmq_d = din("mem_q_norm", [2, 128])
    gmk_d = din("mem_k_norm", [2, 128])
    cst_d = din("cst", [128, NCST])
    out_d = nc.dram_tensor("out", [S, D], F32, kind="ExternalOutput").ap()

    P = Prog(nc)
    if _os.environ.get("TOKR"):
        P.tok = Res("tok")
        P.tokset = tuple(_os.environ["TOKR"].split(","))
    with ExitStack() as st:
        def sb(name, shape, dt=F32):
            return st.enter_context(nc.sbuf_tensor("sb_" + name, shape, dt))

        def ps(name, shape, dt=F32):
            return st.enter_context(nc.psum_tensor("pp_" + name, shape, dt))

        def T(fn, r=(), w=()):
            return P.op("tensor", fn, r, w)

        def V(fn, r=(), w=()):
            return P.op("vector", fn, r, w)

        def A(fn, r=(), w=()):
            return P.op("scalar", fn, r, w)

        def G(fn, r=(), w=()):
            return P.op("gpsimd", fn, r, w)

        def DMA(q, fn, r=(), w=()):
            return P.op(q, fn, r, w, dma=True)

        x_tok = sb("x_tok", [128, NT, D]); rx = [Res("x%d" % i) for i in range(NT)]
        hT = sb("hT", [128, 8, S], BF16); rhT = [Res("hT%d" % i) for i in range(NT)]
        cst = sb("cst", [128, NCST]); rcst = Res("cst")
        idb = sb("idb", [128, 128], BF16); ridb = Res("idb")
        trib = sb("trib", [128, 128], BF16); rtrib = Res("trib")
        hgmb = sb("hgmb", [128, 128], BF16); rhgmb = Res("hgmb")
        gq_bc = sb("gq_bc", [128, 64]); gk_bc = sb("gk_bc", [128, 64]); go_bc = sb("go_bc", [128, 128])
        gmq_bc = sb("gmq_bc", [128, 128]); gmk_bc = sb("gmk_bc", [128, 128]); rgains = Res("gains")
        cs = sb("cs", [128, NT, 16]); sn = sb("sn", [128, NT, 16]); rrope = Res("rope")
        wb = [sb("wb%d" % i, [128, 8, 512], BF16) for i in range(3)]; rwb = [[Res("wb%da" % i), Res("wb%db" % i)] for i in range(3)]
        wo = sb("wo", [128, 2, D], BF16); rwo = Res("wo")
        Fp = [sb("F%d" % i, [128, 256]) for i in range(9)]; rF = [Res("F%d" % i) for i in range(9)]
        Bp = [sb("B%d" % i, [128, 256], BF16) for i in range(7)]; rB = [Res("B%d" % i) for i in range(7)]
        szs = [sb("sz%d" % k, [128, 4, 256]) for k in range(2)]; rszs = [[Res("sz%d_%d" % (k, i)) for i in range(4)] for k in range(2)]
        y_tok = sb("y_tok", [128, 4, 256], BF16); ry = [Res("y%d" % i) for i in range(4)]
        yT = [sb("yT%d" % i, [128, 256], BF16) for i in range(2)]; ryT = [Res("yT%d" % i) for i in range(2)]
        pT = [sb("pT%d" % i, [128, 512], BF16) for i in range(4)]; rpT = [Res("pT%d" % i) for i in range(4)]
        hbs = [sb("hb%d" % i, [128, D], BF16) for i in range(2)]; rhbs = [Res("hb0"), Res("hb1")]
        small = sb("small", [128, 128]); rsm = {}

        def sm(name, a, n):
            rsm[name] = Res("sm_" + name)
            return small[:, a:a + n], rsm[name]
        ss16, r_ss16 = sm("ss16", 0, 16)
        t16, r_t16 = sm("t16", 16, 16)
        rstd16, r_rstd16 = sm("rstd16", 32, 16)
        ss4, r_ss4 = sm("ss4", 48, 4)
        t4, r_t4 = sm("t4", 52, 4)
        rs4, r_rs4 = sm("rs4", 56, 4)
        rden, r_rden = sm("rden", 60, 4)
        rdenB, r_rdenB = sm("rdenB", 108, 4)
        dec4, r_dec4 = sm("dec4", 64, 4)
        sso, r_sso = sm("sso", 68, 2)
        to2, r_to2 = sm("to2", 70, 2)
        rso, r_rso = sm("rso", 72, 2)
        gm = sb("gm", [128, 4, 8]); rgm = Res("gm")
        top8 = sb("top8", [128, 4, 8]); rtop8 = Res("top8")
        selb = sb("selb", [128, 4, 8]); rselb = Res("selb")
        kT = sb("kT", [128, 4, S], BF16); rkT = [Res("kT%d" % i) for i in range(NT)]
        v_flat = sb("v_aug", [128, NT * 4 * 65], BF16); rv = [Res("v%d" % i) for i in range(NT)]
        v_aug = v_flat[:].rearrange("p (a b c) -> p a b c", a=NT, b=4)
        stage = v_flat[:, 0:2048].bitcast(F32); rstage = Res("stage")
        g_bcv = v_flat[:, 2048:4096].bitcast(F32); rg_bc = Res("g_bc")
        qTs = [sb("qT%d" % i, [128, 2048], BF16) for i in range(2)]; rqTs = [Res("qT0"), Res("qT1")]
        lbv = qTs[1][:, 0:2048].bitcast(F32); rlb = rqTs[1]
        lb_g = lbv[:, 0:256]; oml_g = lbv[:, 256:512]
        k_aug = sb("k_aug", [128, 4, 72], BF16); rk_aug = Res("k_aug")
        q_aug = sb("q_aug", [128, 4, 72], BF16); rq_aug = Res("q_aug")
        kmT32 = sb("kmT32", [128, 2, 2, 8]); rkm = Res("kmT32")
        S32 = sb("S32", [128, 2, 128]); rS32 = [Res("S32_0"), Res("S32_1")]
        Sbf = sb("Sbf", [128, 2, 2, 128], BF16); rSbf = [[Res("Sbf00"), Res("Sbf01")], [Res("Sbf10"), Res("Sbf11")]]
        qTA = sb("qTA", [128, 2, 128], BF16); qTB = sb("qTB", [128, 2, 128], BF16)
        rqTA = Res("qTA"); rqTB = Res("qTB")
        memT = sb("memT", [128, 8, 256], BF16); rmemT = Res("memT")
        kmT = sb("kmT", [128, 2, 256], BF16); rkmT = Res("kmT")
        vm_aug = sb("vm_aug", [128, 2, 2, 129], BF16); rvm = Res("vm")
        psA = [ps("psA%d" % i, [128, 512]) for i in range(2)]; rpsA = [Res("psA0", True), Res("psA1", True)]
        psT = ps("psT", [128, 1024], BF16); rpsT = Res("psT", True)
        psG = ps("psG", [128, 512]); rpsG = Res("psG", True)
        psS = [ps("psS%d" % i, [128, 512]) for i in range(2)]; rpsS = [Res("psS0", True), Res("psS1", True)]
        psO = [ps("psO%d" % i, [128, 512]) for i in range(2)]; rpsO = [Res("psO0", True), Res("psO1", True)]

        ctr = {"pa": 0, "w": 0, "ev": 0, "ps": 0, "pt": 0, "yt": 0, "mk": 0, "pw": 0}

        def nxt(k, n):
            v = ctr[k] % n
            ctr[k] += 1
            return v

        def evac(fn_v, fn_a, r, w):
            if nxt("ev", 2) == 0:
                return A(fn_a, r, w)
            return V(fn_v, r, w)

        DMA("sync", lambda e: e.dma_start(out=cst[:], in_=cst_d), w=[rcst])
        V(lambda e: e.tensor_copy(out=idb[:], in_=cst[:, 0:128]), [rcst], [ridb])
        V(lambda e: e.tensor_copy(out=trib[:], in_=cst[:, 128:256]), [rcst], [rtrib])
        V(lambda e: e.tensor_copy(out=hgmb[:], in_=cst[:, 256:384]), [rcst], [rhgmb])
        ident32 = cst[:, 0:128]
        G(lambda e: e.memset(vm_aug[:, :, :, 128:129], 1.0), w=[rvm])
        G(lambda e: e.memset(qTA[:], 0.0), w=[rqTA])
        G(lambda e: e.memset(qTB[:], 0.0), w=[rqTB])
        nI = sb("nI", [128, 256], I32); rnI = Res("nI")
        posi = nI[0:16, 0:128]; rposi = rnI
        posf = Fp[8][0:16, 0:128]; rposf = rF[8]
        DMA("sync", lambda e: e.dma_start(out=posi, in_=pos_d), w=[rposi])
        V(lambda e: e.tensor_copy(out=posf, in_=posi), [rposi], [rposf])
        T(lambda e: e.matmul(psG[:, 0:16], lhsT=posf, rhs=cst[0:16, 0:16], start=True, stop=True), [rposf, rcst], [rpsG])
        post, r_post = sm("post", 80, 16)
        V(lambda e: e.tensor_copy(out=post, in_=psG[:, 0:16]), [rpsG], [r_post])
        ang = Fp[0][:, 0:256].rearrange("p (i j) -> p i j", i=NT)
        V(lambda e: e.tensor_tensor(out=ang, in0=post.unsqueeze(2).to_broadcast([128, NT, 16]),
                                    in1=cst[:, 515:531].unsqueeze(1).to_broadcast([128, NT, 16]), op=ALU.mult),
          [r_post, rcst], [rF[0]])
        ang_lo = Fp[1][:, 0:256].rearrange("p (i j) -> p i j", i=NT)
        V(lambda e: e.tensor_tensor(out=ang_lo, in0=post.unsqueeze(2).to_broadcast([128, NT, 16]),
                                    in1=cst[:, 547:563].unsqueeze(1).to_broadcast([128, NT, 16]), op=ALU.mult),
          [r_post, rcst], [rF[1]])
        V(lambda e: e.tensor_tensor(out=ang, in0=ang, in1=ang_lo, op=ALU.add), [rF[0], rF[1]], [rF[0]])
        V(lambda e: e.tensor_tensor(out=ang, in0=ang, in1=cst[:, 531:547].unsqueeze(1).to_broadcast([128, NT, 16]),
                                    op=ALU.add), [rF[0], rcst], [rF[0]])
        V(lambda e: e.tensor_scalar(out=Fp[1][:], in0=Fp[0][:], scalar1=float(1.0 / (2 * np.pi)), scalar2=None,
                                    op0=ALU.mult), [rF[0]], [rF[1]])
        V(lambda e: e.tensor_copy(out=nI[:], in_=Fp[1][:]), [rF[1]], [rnI])
        V(lambda e: e.tensor_copy(out=Fp[1][:], in_=nI[:]), [rnI], [rF[1]])
        C1 = 6.28125
        C2 = float(2 * np.pi - 6.28125)
        V(lambda e: e.scalar_tensor_tensor(out=Fp[2][:], in0=Fp[1][:], scalar=-C1, in1=Fp[0][:],
                                           op0=ALU.mult, op1=ALU.add), [rF[1], rF[0]], [rF[2]])
        V(lambda e: e.scalar_tensor_tensor(out=Fp[2][:], in0=Fp[1][:], scalar=-C2, in1=Fp[2][:],
                                           op0=ALU.mult, op1=ALU.add), [rF[1], rF[2]], [rF[2]])
        V(lambda e: e.tensor_scalar(out=Fp[2][:], in0=Fp[2][:], scalar1=float(np.pi), scalar2=float(-np.pi),
                                    op0=ALU.min, op1=ALU.max), [rF[2]], [rF[2]])
        A(lambda e: e.activation(out=Fp[3][:], in_=Fp[2][:], func=AF.Sin), [rF[2]], [rF[3]])
        sc = Fp[3][:, 0:256].rearrange("p (i j) -> p i j", i=NT)
        V(lambda e: e.tensor_copy(out=cs[:, :, 0:8], in_=sc[:, :, 8:16]), [rF[3]], [rrope])
        V(lambda e: e.tensor_copy(out=cs[:, :, 8:16], in_=sc[:, :, 8:16]), [rF[3]], [rrope])
        V(lambda e: e.tensor_scalar(out=sn[:, :, 0:8], in0=sc[:, :, 0:8], scalar1=-1.0, scalar2=None, op0=ALU.mult),
          [rF[3]], [rrope])
        V(lambda e: e.tensor_copy(out=sn[:, :, 8:16], in_=sc[:, :, 0:8]), [rF[3]], [rrope])
        def load_w(dst, rdst, src_ap):
            return DMA("gpsimd", lambda e: e.dma_start(out=dst, in_=src_ap), w=[rdst])

        def w_in_cols(l, c0, n):
            return win_d[l].rearrange("(c p) n -> p c n", p=128)[:, :, c0:c0 + n]

        psGb = psG[:].bitcast(BF16)

        def rms_to_T(src_tile, rsrc, gain, rgain, dstT, rdst, col0, ssc, r_ssc, tsc, r_tsc, rsc, r_rsc, k):
            hb = hbs[k % 2]; rhb = rhbs[k % 2]
            pst, rpst = (psT, rpsT) if k % 2 == 0 else (psGb, rpsG)
            A(lambda e: e.activation(out=hb[:], in_=src_tile, func=AF.Square, accum_out=ssc[:, k:k + 1]),
              [rsrc], [rhb, r_ssc])
            A(lambda e: e.activation(out=tsc[:, k:k + 1], in_=ssc[:, k:k + 1], func=AF.Ln, scale=1.0 / D, bias=EPS),
              [r_ssc], [r_tsc])
            A(lambda e: e.activation(out=rsc[:, k:k + 1], in_=tsc[:, k:k + 1], func=AF.Exp, scale=-0.5),
              [r_tsc], [r_rsc])
            V(lambda e: e.scalar_tensor_tensor(out=hb[:], in0=src_tile, scalar=rsc[:, k:k + 1], in1=gain,
                                               op0=ALU.mult, op1=ALU.mult), [rsrc, r_rsc, rgain], [rhb])
            for c in range(8):
                T(lambda e, c=c: e.transpose(out=pst[:, c * 128:(c + 1) * 128], in_=hb[:, c * 128:(c + 1) * 128],
                                             identity=idb[:]), [rhb, ridb], [rpst])
            src3 = pst[:, 0:1024].rearrange("p (c t) -> p c t", c=8)
            evac(lambda e: e.tensor_copy(out=dstT[:, :, col0:col0 + 128], in_=src3),
                 lambda e: e.copy(out=dstT[:, :, col0:col0 + 128], in_=src3), [rpst], [rdst])

        def proj(lhs_cols, rlhs, wt, rwt, ncols=512, lhsT_src=None, wide=False):
            if wide:
                pt, rpt = ((psA[0], rpsA[0]), (psA[1], rpsA[1]), (psO[0], rpsO[0]), (psO[1], rpsO[1]))[nxt("pw", 4)]
            else:
                b = nxt("pa", 2)
                pt, rpt = psA[b], rpsA[b]
            src = hT if lhsT_src is None else lhsT_src
            for c in range(8):
                T(lambda e, c=c: e.matmul(pt[:, 0:ncols], lhsT=src[:, c, lhs_cols:lhs_cols + 128],
                                          rhs=wt[:, c, 0:ncols], start=(c == 0), stop=(c == 7)),
                  [rlhs] + list(rwt), [rpt])
            return pt, rpt

        def headnorm(src_ps, rps, H, Dh, gain, outF, routF, tmpA, rtmpA, tmpB, rtmpB):
            n = H * Dh
            A(lambda e: e.activation(out=tmpA[:, 0:n], in_=src_ps, func=AF.Square), [rps], [rtmpA])
            V(lambda e: e.tensor_reduce(out=ss4[:, 0:H], in_=tmpA[:, 0:n].rearrange("p (h d) -> p h d", h=H),
                                        axis=AX.X, op=ALU.add), [rtmpA], [r_ss4])
            A(lambda e: e.activation(out=t4[:, 0:H], in_=ss4[:, 0:H], func=AF.Ln, scale=1.0 / Dh, bias=EPS),
              [r_ss4], [r_t4])
            A(lambda e: e.activation(out=rs4[:, 0:H], in_=t4[:, 0:H], func=AF.Exp, scale=-0.5), [r_t4], [r_rs4])
            V(lambda e: e.tensor_tensor(out=tmpB[:, 0:n].rearrange("p (h d) -> p h d", h=H),
                                        in0=src_ps.rearrange("p (h d) -> p h d", h=H),
                                        in1=rs4[:, 0:H].unsqueeze(2).to_broadcast([128, H, Dh]), op=ALU.mult),
              [rps, r_rs4], [rtmpB])
            (G if GOFF else V)(lambda e: e.tensor_tensor(out=outF[:, 0:n].rearrange("p (h d) -> p h d", h=H),
                                                         in0=tmpB[:, 0:n].rearrange("p (h d) -> p h d", h=H),
                                                         in1=gain.unsqueeze(1).to_broadcast([128, H, Dh]), op=ALU.mult),
                               [rtmpB, rgains], [routF])

        def rope(Fx, rFx, i, tR, rtR):
            x3 = Fx[:, 0:256].rearrange("p (h d) -> p h d", h=4)
            a3 = tR[:, 0:64].rearrange("p (h d) -> p h d", h=4)
            b3 = tR[:, 64:128].rearrange("p (h d) -> p h d", h=4)
            rtA = rtR
            rtB = rtR
            G(lambda e: e.tensor_tensor(out=a3, in0=x3[:, :, 0:16], in1=cs[:, i, :].unsqueeze(1).to_broadcast([128, 4, 16]),
                                        op=ALU.mult), [rFx, rrope], [rtA])
            G(lambda e: e.tensor_tensor(out=b3[:, :, 0:8], in0=x3[:, :, 8:16],
                                        in1=sn[:, i, 0:8].unsqueeze(1).to_broadcast([128, 4, 8]), op=ALU.mult),
              [rFx, rrope], [rtB])
            G(lambda e: e.tensor_tensor(out=b3[:, :, 8:16], in0=x3[:, :, 0:8],
                                        in1=sn[:, i, 8:16].unsqueeze(1).to_broadcast([128, 4, 8]), op=ALU.mult),
              [rFx, rrope], [rtB])
            G(lambda e: e.tensor_tensor(out=x3[:, :, 0:16], in0=a3, in1=b3, op=ALU.add), [rtA, rtB], [rFx])

        def silu_ps(dst, rdst, src_ps, rps, defer=False):
            A(lambda e: e.activation(out=dst, in_=src_ps, func=AF.Exp, scale=-1.0), [rps], [rdst])
            A(lambda e: e.activation(out=dst, in_=dst, func=AF.Ln, bias=1.0), [rdst], [rdst])
            A(lambda e: e.activation(out=dst, in_=dst, func=AF.Exp, scale=-1.0), [rdst], [rdst])

            def fin():
                V(lambda e: e.tensor_tensor(out=dst, in0=src_ps, in1=dst, op=ALU.mult), [rps, rdst], [rdst])
            if defer:
                return fin
            fin()

        SC = [((Fp[0], rF[0]), (Fp[1], rF[1]), (Fp[2], rF[2]), (Fp[3], rF[3])),
              ((Fp[4], rF[4]), (Fp[6], rF[6]), (Fp[7], rF[7]), (Fp[8], rF[8]))]
        AUG = [(k_aug, rk_aug), (q_aug, rq_aug)]
        if _os.environ.get("NOPAR", "0") == "1":
            SC[1] = SC[0]
        if _os.environ.get("NOAUG", "0") == "1":
            AUG[1] = AUG[0]

        dec8, _r = sm("dec8", 96, 8)
        r_dec8 = [Res("dec8a"), Res("dec8b")]
        sso4, _r2 = sm("sso4", 104, 4)
        r_sso2 = [Res("ssoA"), Res("ssoB")]
        dummy = sb("dummy", [128, 8])
        kflat = kT[:].rearrange("p h t -> p (h t)")
        kf32 = kflat[:, 0:4608].bitcast(F32)
        Fq = [kf32[:, k * 256:(k + 1) * 256] for k in range(9)]
        rFq = [Res("Fq%d" % k) for k in range(9)]
        Bq = [kflat[:, 4608 + k * 256:4608 + (k + 1) * 256] for k in range(7)]
        rBq = [Res("Bq%d" % k) for k in range(7)]
        HSETS = [([t[:] for t in Fp], rF, [t[:] for t in Bp], rB), (Fq, rFq, Bq, rBq)]

        def barrier():
            G(lambda e: e.memset(dummy[:], 0.0), w=list(rkT) + rFq + rBq + list(rv) + [rstage, rg_bc])

        rrow = Res("pe_rowfence")

        out_dmas = []

        def outproj_tile(i, r, last, obanks=None):
            yb = nxt("yt", 2)
            for pp in range(2):
                T(lambda e, pp=pp: e.transpose(out=psT[:, pp * 128:(pp + 1) * 128], in_=y_tok[:, r, pp * 128:(pp + 1) * 128],
                                               identity=idb[:]), [ry[r], ridb], [rpsT])
            evac(lambda e: e.tensor_copy(out=yT[yb][:], in_=psT[:, 0:256]),
                 lambda e: e.copy(out=yT[yb][:], in_=psT[:, 0:256]), [rpsT], [ryT[yb]])
            for half in range(2):
                if obanks is None:
                    b = nxt("pa", 2)
                    pso, rpso = psA[b], rpsA[b]
                else:
                    pso, rpso = obanks[half]
                for pp in range(2):
                    T(lambda e, pp=pp, half=half, pso=pso: e.matmul(pso[:, 0:512], lhsT=yT[yb][:, pp * 128:(pp + 1) * 128],
                                                                    rhs=wo[:, pp, half * 512:(half + 1) * 512],
                                                                    start=(pp == 0), stop=(pp == 1)),
                      [ryT[yb], rwo], [rpso])
                V(lambda e, half=half, pso=pso: e.tensor_tensor(out=x_tok[:, i, half * 512:(half + 1) * 512], in0=pso[:, 0:512],
                                                                in1=x_tok[:, i, half * 512:(half + 1) * 512], op=ALU.add),
                  [rpso, rx[i]], [rx[i]])
            if last:
                out_dmas.append(DMA("sync", lambda e: e.dma_start(out=out_d[i * 128:(i + 1) * 128, :], in_=x_tok[:, i, :]),
                                    r=[rx[i]]))

        def attn_chunk(Q, H, Dh, KP, scale, key_tiles, causal, kTsrc, rkTsrc, vsrc, rvsrc, qview, rqT, sz, rsz):
            DA = Dh + 1
            for h in range(H):
                if Dh == 64:
                    o_b = h % 2
                    banks = [o_b, o_b, o_b, o_b]
                    offs = [0, DA, 2 * DA, 3 * DA]
                else:
                    banks = [0, 0, 1, 1]
                    offs = [0, DA, 0, DA]
                started = set()
                kts = key_tiles(Q)
                for kt in kts:
                    j = kt - 4 * Q if causal else -1
                    q0 = max(j, 0) * 128
                    N = 512 - q0
                    sbk = nxt("ps", PS3)
                    pss, rpss = ((psS[0], rpsS[0]), (psS[1], rpsS[1]), (psG, rpsG))[sbk]
                    T(lambda e, kt=kt, h=h, q0=q0, N=N, pss=pss: e.matmul(
                        pss[:, 0:N], lhsT=kTsrc(h, kt), rhs=qview(h)[:, q0:512], start=True, stop=True),
                      [rkTsrc(kt), rqT], [rpss])
                    pb = nxt("pt", 4)
                    A(lambda e, N=N, pss=pss, pb=pb: e.activation(out=pT[pb][:, 0:N], in_=pss[:, 0:N], func=AF.Exp,
                                                                  scale=scale), [rpss], [rpT[pb]])
                    if j >= 0:
                        (V if (MASKV and nxt("mk", 2) == 0) else G)(
                            lambda e, pb=pb: e.tensor_tensor(out=pT[pb][:, 0:128], in0=pT[pb][:, 0:128], in1=trib[:],
                                                             op=ALU.mult), [rpT[pb], rtrib], [rpT[pb]])
                    for r in range(max(j, 0), 4):
                        bk = banks[r]
                        first = bk not in started
                        started.add(bk)
                        T(lambda e, r=r, kt=kt, h=h, q0=q0, pb=pb, bk=bk, first=first: e.matmul(
                            psO[bk][:, offs[r]:offs[r] + DA], lhsT=pT[pb][:, r * 128 - q0:r * 128 - q0 + 128],
                            rhs=vsrc(h, kt), start=first, stop=False, skip_group_check=True),
                          [rpT[pb], rvsrc(kt)], [rpsO[bk]])
                rd, r_rd = (rden, r_rden) if h % 2 == 0 else (rdenB, r_rdenB)
                for bk0 in sorted(set(banks)):
                    rs_ = [r for r in range(4) if banks[r] == bk0]
                    nr = len(rs_)
                    V(lambda e, bk0=bk0, rs_=rs_, nr=nr, rd=rd: e.reciprocal(
                        out=rd[:, rs_[0]:rs_[0] + nr],
                        in_=psO[bk0][:, 0:nr * DA].rearrange("p (r c) -> p r c", r=nr)[:, :, Dh:DA]),
                      [rpsO[bk0]], [r_rd])
                    V(lambda e, bk0=bk0, rs_=rs_, nr=nr, h=h: e.tensor_tensor(
                        out=sz[:, rs_[0]:rs_[0] + nr, h * Dh:(h + 1) * Dh],
                        in0=psO[bk0][:, 0:nr * DA].rearrange("p (r c) -> p r c", r=nr)[:, :, 0:Dh],
                        in1=sz[:, rs_[0]:rs_[0] + nr, h * Dh:(h + 1) * Dh], op=ALU.mult),
                      [rpsO[bk0]] + [rsz[r] for r in rs_], [rsz[r] for r in rs_])
                G(lambda e, h=h, rd=rd: e.tensor_tensor(out=y_tok[:, :, h * Dh:(h + 1) * Dh], in0=sz[:, :, h * Dh:(h + 1) * Dh],
                                                        in1=rd[:, 0:4].unsqueeze(2).to_broadcast([128, 4, Dh]), op=ALU.mult),
                  list(rsz) + [r_rd], list(ry))

        for li, l in enumerate(layers):
            last_layer = (li == len(layers) - 1)
            DMA("sync", lambda e, l=l: e.dma_start(out=g_bcv, in_=ng_d[l:l + 1, :].partition_broadcast(128)), w=[rg_bc])
            for dst, src in ((gq_bc, gq_d), (gk_bc, gk_d), (go_bc, go_d), (gmq_bc, gmq_d), (gmk_bc, gmk_d)):
                DMA("sync", lambda e, l=l, dst=dst, src=src: e.dma_start(out=dst[:], in_=src[l:l + 1, :].partition_broadcast(128)),
                    w=[rgains])
            for i in range(NT):
                if li == 0:
                    DMA(("scalar" if (XQ and i % 2 == 1) else "sync"), lambda e, i=i: e.dma_start(out=x_tok[:, i, :], in_=x_d[i * 128:(i + 1) * 128, :]), w=[rx[i]])
                rms_to_T(x_tok[:, i, :], rx[i], g_bcv, rg_bc, hT, rhT[i], i * 128, ss16, r_ss16, t16, r_t16,
                         rstd16, r_rstd16, i)
            glist = [g for g in ALL_GROUPS if g in groups]
            for gi, gname in enumerate(glist):
                last = last_layer and gi == len(glist) - 1
                kind = gname[0]
                g = int(gname[1])
                if kind == "A":
                    w1 = nxt("w", 3); w2 = nxt("w", 3)
                    load_w(wb[w1][:, :, 0:256], rwb[w1][0], w_in_cols(l, 512 + 256 * g, 256))
                    load_w(wb[w1][:, :, 256:512], rwb[w1][1], w_in_cols(l, 1024 + 256 * g, 256))
                    load_w(wb[w2][:, :, 0:256], rwb[w2][0], w_in_cols(l, 256 * g, 256))
                    load_w(wb[w2][:, :, 256:512], rwb[w2][1], w_in_cols(l, 3584 + 256 * g, 256))
                    load_w(wo[:], rwo, wout_d[l, 256 * g:256 * g + 256, :].rearrange("(c p) n -> p c n", p=128))
                    barrier()
                    G(lambda e: e.memset(v_aug[:, :, :, 64:65], 1.0), w=rv)
                    V(lambda e: e.memset(psG[:, 0:16], 0.0), w=[rpsG])
                    for i in range(NT):
                        n_blk = i // 2
                        (tA, rtA), (tB, rtB), (FO, rFO), (tR, rtR) = SC[(i % 2) * PAR1]
                        ka, rka = AUG[(i % 2) * PAR1]
                        pa, rpa = proj(i * 128, rhT[i], wb[w1], rwb[w1], wide=bool(WIDE))
                        A(lambda e, i=i, pa=pa: e.copy(out=v_aug[:, i, :, 0:64],
                                                       in_=pa[:, 256:512].rearrange("p (h d) -> p h d", h=4)),
                          [rpa], [rv[i]])
                        headnorm(pa[:, 0:256], rpa, 4, 64, gk_bc[:], FO, rFO, tA, rtA, tB, rtB)
                        rope(FO, rFO, i, tR, rtR)
                        G(lambda e, ka=ka, FO=FO: e.tensor_copy(out=ka[:, :, 0:64], in_=FO[:].rearrange("p (h d) -> p h d", h=4)),
                          [rFO], [rka])
                        G(lambda e, ka=ka: e.memset(ka[:, :, 64:72], 0.0), w=[rka])
                        G(lambda e, ka=ka, n_blk=n_blk: e.memset(ka[:, :, 64 + n_blk:65 + n_blk], 1.0), w=[rka])
                        for pp in range(2):
                            T(lambda e, pp=pp, n_blk=n_blk, FO=FO: e.matmul(psG[:, pp * 8 + n_blk:pp * 8 + n_blk + 1],
                                                                            lhsT=FO[:, pp * 128:(pp + 1) * 128], rhs=cst[:, 514:515],
                                                                            start=False, stop=False, skip_group_check=True),
                              [rFO, rcst], [rpsG])
                        for h in range(4):
                            T(lambda e, h=h, ka=ka: e.transpose(out=psT[0:72, h * 128:(h + 1) * 128], in_=ka[:, h, :], identity=idb[:]),
                              [rka, ridb], [rpsT])
                        src3 = psT[0:72, 0:512].rearrange("p (h t) -> p h t", h=4)
                        evac(lambda e, i=i, src3=src3: e.tensor_copy(out=kT[0:72, :, i * 128:(i + 1) * 128], in_=src3),
                             lambda e, i=i, src3=src3: e.copy(out=kT[0:72, :, i * 128:(i + 1) * 128], in_=src3),
                             [rpsT], [rkT[i]])
                    G(lambda e: e.memset(kmT32[:], 0.0), w=[rkm])
                    A(lambda e: e.copy(out=kmT32[0:64, :, 0, :], in_=psG[0:64, 0:16].rearrange("p (a n) -> p a n", a=2)), [rpsG], [rkm])
                    A(lambda e: e.copy(out=kmT32[64:128, :, 1, :], in_=psG[64:128, 0:16].rearrange("p (a n) -> p a n", a=2)), [rpsG], [rkm])
                    for Q in range(4):
                        qT3 = qTs[Q % 2][:, 0:2048].rearrange("p (h t) -> p h t", h=4)
                        rqT = rqTs[Q % 2]
                        sz = szs[Q % 2]; rsz = rszs[Q % 2]
                        for r in range(4):
                            i = 4 * Q + r
                            own = i // 2
                            P.phase = "chain"
                            par = (i % 2) * PAR2 * (1 if (own < 4 or GPAR) else 0)
                            (tA, rtA), (tB, rtB), (FO, rFO), (tR, rtR) = SC[par]
                            qa, rqa = AUG[par]
                            pa, rpa = proj(i * 128, rhT[i], wb[w2], rwb[w2])
                            fin = silu_ps(sz[:, r, :], rsz[r], pa[:, 256:512], rpa, defer=True)
                            headnorm(pa[:, 0:256], rpa, 4, 64, gq_bc[:], FO, rFO, tA, rtA, tB, rtB)
                            fin()
                            rope(FO, rFO, i, tR, rtR)
                            G(lambda e, qa=qa, FO=FO: e.tensor_copy(out=qa[:, :, 0:64], in_=FO[:].rearrange("p (h d) -> p h d", h=4)),
                              [rFO], [rqa])
                            if own >= 4:
                                P.phase = "gate"
                                for pp in range(2):
                                    T(lambda e, pp=pp, FO=FO: e.matmul(psG[:, pp * 128:(pp + 1) * 128],
                                                                       lhsT=FO[:, pp * 128:(pp + 1) * 128], rhs=ident32,
                                                                       start=True, stop=True),
                                      [rFO, rcst], [rpsG])
                                A(lambda e: e.copy(out=Fp[5][:], in_=psG[:, 0:256]), [rpsG], [rF[5]])
                                for h in range(4):
                                    T(lambda e, h=h, own=own: e.matmul(
                                        psG[:, 256 + h * 8:256 + h * 8 + own],
                                        lhsT=Fp[5][:, (h // 2) * 128:(h // 2) * 128 + 128],
                                        rhs=kmT32[:, h // 2, h % 2, 0:own], start=True, stop=True),
                                      [rF[5], rkm], [rpsG])
                                V(lambda e: e.memset(gm[:], -1.0e30), w=[rgm])
                                V(lambda e, own=own: e.tensor_copy(
                                    out=gm[:, :, 0:own], in_=psG[:, 256:288].rearrange("p (h n) -> p h n", h=4)[:, :, 0:own]),
                                  [rpsG], [rgm])
                                for h in range(4):
                                    V(lambda e, h=h: e.max(out=top8[:, h, :], in_=gm[:, h, :]), [rgm], [rtop8])
                                V(lambda e: e.tensor_tensor(out=selb[:], in0=gm[:], in1=top8[:, :, 2:3].to_broadcast([128, 4, 8]),
                                                            op=ALU.is_ge), [rgm, rtop8], [rselb])
                                V(lambda e, qa=qa: e.tensor_scalar(out=qa[:, :, 64:72], in0=selb[:], scalar1=30000.0,
                                                                   scalar2=-30000.0, op0=ALU.mult, op1=ALU.add), [rselb], [rqa])
                                V(lambda e, qa=qa, own=own: e.memset(qa[:, :, 64 + own:65 + own], 0.0), w=[rqa])
                            else:
                                G(lambda e, qa=qa: e.memset(qa[:, :, 64:72], 0.0), w=[rqa])
                            P.phase = None
                            for h in range(4):
                                T(lambda e, h=h, qa=qa: e.transpose(out=psT[0:72, h * 128:(h + 1) * 128], in_=qa[:, h, :],
                                                                    identity=idb[:]), [rqa, ridb], [rpsT])
                            src3 = psT[0:72, 0:512].rearrange("p (h t) -> p h t", h=4)
                            evac(lambda e, r=r, src3=src3, qT3=qT3: e.tensor_copy(out=qT3[0:72, :, r * 128:(r + 1) * 128], in_=src3),
                                 lambda e, r=r, src3=src3, qT3=qT3: e.copy(out=qT3[0:72, :, r * 128:(r + 1) * 128], in_=src3),
                                 [rpsT], [rqT])
                        attn_chunk(Q, 4, 64, 72, 0.125, lambda Q: list(range(4 * Q + 4)), True,
                                   lambda h, kt: kT[0:72, h, kt * 128:(kt + 1) * 128], lambda kt: rkT[kt],
                                   lambda h, kt: v_aug[:, kt, h, :], lambda kt: rv[kt],
                                   lambda h, qT3=qT3: qT3[0:72, h, :], rqT, sz, rsz)
                        for r in range(4):
                            outproj_tile(4 * Q + r, r, last, obanks=([(psO[0], rpsO[0]), (psO[1], rpsO[1])] if OPB else None))
                elif kind == "M":
                    w1 = nxt("w", 3); w2 = nxt("w", 3)
                    wkvv = wkv_d[l].rearrange("(c p) n -> p c n", p=128)
                    load_w(wb[w1][:, :, 0:256], rwb[w1][0], wkvv[:, :, 256 * g:256 * g + 256])
                    load_w(wb[w1][:, :, 256:512], rwb[w1][1], wkvv[:, :, 512 + 256 * g:512 + 256 * g + 256])
                    load_w(wb[w2][:, :, 0:256], rwb[w2][0], w_in_cols(l, 3072 + 256 * g, 256))
                    load_w(wb[w2][:, :, 256:512], rwb[w2][1], w_in_cols(l, 4608 + 256 * g, 256))
                    load_w(wo[:], rwo, wout_d[l, 1024 + 256 * g:1024 + 256 * g + 256, :].rearrange("(c p) n -> p c n", p=128))
                    if g == 0 or ("M0" not in groups):
                        barrier()
                        DMA("sync", lambda e, l=l: e.dma_start(out=g_bcv, in_=mng_d[l:l + 1, :].partition_broadcast(128)),
                            w=[rg_bc])
                        for mt in range(2):
                            DMA("sync", lambda e, mt=mt: e.dma_start(out=stage, in_=mem_d[mt * 128:(mt + 1) * 128, :]),
                                w=[rstage])
                            rms_to_T(stage, rstage, g_bcv, rg_bc, memT, rmemT, mt * 128, ss16, r_ss16, t16, r_t16,
                                     rstd16, r_rstd16, mt)
                    for mt in range(2):
                        (tA, rtA), (tB, rtB), (FO, rFO), (tR, rtR) = SC[mt % 2]
                        pa, rpa = proj(mt * 128, rmemT, wb[w1], rwb[w1], lhsT_src=memT)
                        A(lambda e, mt=mt, pa=pa: e.copy(out=vm_aug[:, mt, :, 0:128],
                                                         in_=pa[:, 256:512].rearrange("p (h d) -> p h d", h=2)),
                          [rpa], [rvm])
                        headnorm(pa[:, 0:256], rpa, 2, 128, gmk_bc[:], FO, rFO, tA, rtA, tB, rtB)
                        G(lambda e, mt=mt, FO=FO: e.tensor_copy(out=Bp[mt % 2][:], in_=FO[:]), [rFO], [rB[mt % 2]])
                        for hh in range(2):
                            T(lambda e, hh=hh, mt=mt: e.transpose(out=psT[:, hh * 128:(hh + 1) * 128],
                                                                  in_=Bp[mt % 2][:, hh * 128:(hh + 1) * 128],
                                                                  identity=idb[:]), [rB[mt % 2], ridb], [rpsT])
                        src3 = psT[:, 0:256].rearrange("p (h t) -> p h t", h=2)
                        evac(lambda e, mt=mt, src3=src3: e.tensor_copy(out=kmT[:, :, mt * 128:(mt + 1) * 128], in_=src3),
                             lambda e, mt=mt, src3=src3: e.copy(out=kmT[:, :, mt * 128:(mt + 1) * 128], in_=src3),
                             [rpsT], [rkmT])
                    for Q in range(4):
                        qm3 = qTs[Q % 2][:, 0:1024].rearrange("p (h t) -> p h t", h=2)
                        rqT = rqTs[Q % 2]
                        sz = szs[Q % 2]; rsz = rszs[Q % 2]
                        for r in range(4):
                            i = 4 * Q + r
                            (tA, rtA), (tB, rtB), (FO, rFO), (tR, rtR) = SC[i % 2]
                            pa, rpa = proj(i * 128, rhT[i], wb[w2], rwb[w2])
                            fin = silu_ps(sz[:, r, :], rsz[r], pa[:, 256:512], rpa, defer=True)
                            headnorm(pa[:, 0:256], rpa, 2, 128, gmq_bc[:], FO, rFO, tA, rtA, tB, rtB)
                            fin()
                            G(lambda e, i=i, FO=FO: e.tensor_copy(out=Bp[i % 2][:], in_=FO[:]), [rFO], [rB[i % 2]])
                            for hh in range(2):
                                T(lambda e, hh=hh, i=i: e.transpose(out=psT[:, hh * 128:(hh + 1) * 128],
                                                                    in_=Bp[i % 2][:, hh * 128:(hh + 1) * 128], identity=idb[:]),
                                  [rB[i % 2], ridb], [rpsT])
                            src3 = psT[:, 0:256].rearrange("p (h t) -> p h t", h=2)
                            evac(lambda e, r=r, src3=src3, qm3=qm3: e.tensor_copy(out=qm3[:, :, r * 128:(r + 1) * 128], in_=src3),
                                 lambda e, r=r, src3=src3, qm3=qm3: e.copy(out=qm3[:, :, r * 128:(r + 1) * 128], in_=src3),
                                 [rpsT], [rqT])
                        attn_chunk(Q, 2, 128, 128, float(128 ** -0.5), lambda Q: [0, 1], False,
                                   lambda h, kt: kmT[:, h, kt * 128:(kt + 1) * 128], lambda kt: rkmT,
                                   lambda h, kt: vm_aug[:, kt, h, :], lambda kt: rvm,
                                   lambda h, qm3=qm3: qm3[:, h, :], rqT, sz, rsz)
                        for r in range(4):
                            outproj_tile(4 * Q + r, r, last, obanks=([(psO[0], rpsO[0]), (psO[1], rpsO[1])] if OPB else None))
                else:
                    w1 = nxt("w", 3); w2 = nxt("w", 3)
                    load_w(wb[w1][:, :, 0:256], rwb[w1][0], w_in_cols(l, 1536 + 256 * g, 256))
                    load_w(wb[w1][:, :, 256:512], rwb[w1][1], w_in_cols(l, 2048 + 256 * g, 256))
                    load_w(wb[w2][:, :, 0:256], rwb[w2][0], w_in_cols(l, 2560 + 256 * g, 256))
                    load_w(wb[w2][:, :, 256:512], rwb[w2][1], w_in_cols(l, 4096 + 256 * g, 256))
                    load_w(wo[:], rwo, wout_d[l, 512 + 256 * g:512 + 256 * g + 256, :].rearrange("(c p) n -> p c n", p=128))
                    barrier()
                    if l != 0:
                        DMA("sync", lambda e, g=g: e.dma_start(out=lb_g, in_=lbl_d[1:2, 256 * g:256 * g + 256].partition_broadcast(128)), w=[rlb])
                        DMA("sync", lambda e, g=g: e.dma_start(out=oml_g, in_=lbl_d[0:1, 256 * g:256 * g + 256].partition_broadcast(128)), w=[rlb])
                        V(lambda e: e.tensor_tensor(out=lb_g, in0=lb_g, in1=oml_g, op=ALU.subtract), [rlb], [rlb])
                        A(lambda e: e.activation(out=lb_g, in_=lb_g, func=AF.Exp, scale=-1.0), [rlb], [rlb])
                        A(lambda e: e.activation(out=lb_g, in_=lb_g, func=AF.Ln, bias=1.0), [rlb], [rlb])
                        A(lambda e: e.activation(out=lb_g, in_=lb_g, func=AF.Exp, scale=-1.0), [rlb], [rlb])
                        V(lambda e: e.tensor_scalar(out=oml_g, in0=lb_g, scalar1=-1.0, scalar2=1.0, op0=ALU.mult, op1=ALU.add),
                          [rlb], [rlb])
                    for hh in range(2):
                        G(lambda e, hh=hh: e.memset(S32[:, hh, :], 0.0), w=[rS32[hh]])
                        G(lambda e, hh=hh: e.memset(Sbf[:, 0, hh, :], 0.0), w=[rSbf[0][hh]])
                    Tri32 = cst[:, 256:384]
                    TriE32 = cst[:, 384:512]
                    for i in range(NT):
                        Fs, rFs, Bs, rBs = HSETS[i % 2]
                        sl = i % 4
                        sz = szs[(i // 4) % 2]; rsz = rszs[(i // 4) % 2]
                        pq, rpq = proj(i * 128, rhT[i], wb[w1], rwb[w1])
                        finq = silu_ps(Fs[0], rFs[0], pq[:, 0:256], rpq, defer=True)
                        A(lambda e, pq=pq, Fs=Fs: e.activation(out=Fs[1], in_=pq[:, 256:512], func=AF.Exp, scale=-1.0), [rpq], [rFs[1]])
                        A(lambda e, Fs=Fs: e.activation(out=Fs[1], in_=Fs[1], func=AF.Ln, bias=1.0), [rFs[1]], [rFs[1]])
                        finq()
                        pi_, rpi = proj(i * 128, rhT[i], wb[w2], rwb[w2])
                        silu_ps(sz[:, sl, :], rsz[sl], pi_[:, 256:512], rpi)
                        V(lambda e, pi_=pi_, Bs=Bs: e.tensor_copy(out=Bs[0], in_=pi_[:, 0:256]), [rpi], [rBs[0]])
                        if l == 0:
                            A(lambda e, Fs=Fs: e.activation(out=Fs[2], in_=Fs[1], func=AF.Copy, scale=-1.0), [rFs[1]], [rFs[2]])
                            A(lambda e, Fs=Fs: e.activation(out=Fs[1], in_=Fs[1], func=AF.Exp, scale=-1.0), [rFs[1]], [rFs[1]])
                        else:
                            A(lambda e, Fs=Fs: e.activation(out=Fs[1], in_=Fs[1], func=AF.Exp, scale=-1.0), [rFs[1]], [rFs[1]])
                            V(lambda e, g=g, Fs=Fs: e.tensor_tensor(out=Fs[1], in0=Fs[1], in1=oml_g,
                                                                    op=ALU.mult), [rFs[1], rlb], [rFs[1]])
                            V(lambda e, g=g, Fs=Fs: e.tensor_tensor(out=Fs[1], in0=Fs[1], in1=lb_g,
                                                                    op=ALU.add), [rFs[1], rlb], [rFs[1]])
                            A(lambda e, Fs=Fs: e.activation(out=Fs[2], in_=Fs[1], func=AF.Ln), [rFs[1]], [rFs[2]])
                        (G if GOFF else V)(lambda e, Fs=Fs: e.tensor_scalar(out=Fs[3], in0=Fs[1], scalar1=-1.0, scalar2=1.0, op0=ALU.mult,
                                                                            op1=ALU.add), [rFs[1]], [rFs[3]])
                        T(lambda e, Fs=Fs: e.matmul(psG[:, 0:256], lhsT=Tri32, rhs=Fs[2], start=True, stop=True),
                          [rcst, rFs[2]], [rpsG])
                        T(lambda e, Fs=Fs: e.matmul(psG[:, 256:512], lhsT=TriE32, rhs=Fs[2], start=True, stop=True),
                          [rcst, rFs[2]], [rpsG])
                        for hh in range(2):
                            T(lambda e, hh=hh, Fs=Fs: e.matmul(psS[1][:, 256 + 2 * hh:256 + 2 * hh + 2],
                                                               lhsT=Fs[2][:, hh * 128:(hh + 1) * 128],
                                                               rhs=cst[:, 512:514], start=True, stop=True), [rFs[2], rcst], [rpsS[1]])
                        dsl = dec8[:, 4 * (i % 2):4 * (i % 2) + 4]
                        rds = r_dec8[i % 2]
                        A(lambda e, dsl=dsl: e.activation(out=dsl, in_=psS[1][:, 256:260], func=AF.Exp), [rpsS[1]], [rds])
                        A(lambda e, Fs=Fs: e.activation(out=Fs[4], in_=psG[:, 0:256], func=AF.Exp), [rpsG], [rFs[4]])
                        A(lambda e, Fs=Fs: e.activation(out=Fs[5], in_=psG[:, 0:256], func=AF.Exp, scale=-1.0), [rpsG], [rFs[5]])
                        A(lambda e, Fs=Fs: e.activation(out=Fs[6], in_=psG[:, 256:512], func=AF.Exp), [rpsG], [rFs[6]])
                        V(lambda e, Fs=Fs, Bs=Bs: e.tensor_tensor(out=Bs[1], in0=Fs[0], in1=Fs[4], op=ALU.mult), [rFs[0], rFs[4]], [rBs[1]])
                        G(lambda e, Fs=Fs, Bs=Bs: e.tensor_tensor(out=Bs[2], in0=Fs[3], in1=Fs[5], op=ALU.mult), [rFs[3], rFs[5]], [rBs[2]])
                        G(lambda e, Fs=Fs, Bs=Bs: e.tensor_tensor(out=Bs[3], in0=Fs[3], in1=Fs[6], op=ALU.mult), [rFs[3], rFs[6]], [rBs[3]])
                        for hh in range(2):
                            T(lambda e, hh=hh, Bs=Bs: e.transpose(out=psT[:, hh * 128:(hh + 1) * 128], in_=Bs[1][:, hh * 128:(hh + 1) * 128],
                                                                  identity=idb[:]), [rBs[1], ridb], [rpsT])
                            T(lambda e, hh=hh, Bs=Bs: e.transpose(out=psT[:, 256 + hh * 128:256 + (hh + 1) * 128],
                                                                  in_=Bs[2][:, hh * 128:(hh + 1) * 128], identity=idb[:]),
                              [rBs[2], ridb], [rpsT])
                        pq3 = psT[:, 0:256].rearrange("p (h t) -> p h t", h=2)
                        A(lambda e, Bs=Bs: e.copy(out=Bs[4], in_=psT[:, 0:256]), [rpsT], [rBs[4]])
                        A(lambda e, pq3=pq3: e.copy(out=qTA[:, :, 0:64], in_=pq3[:, :, 0:64]), [rpsT], [rqTA])
                        V(lambda e, Bs=Bs: e.tensor_copy(out=Bs[5], in_=psT[:, 256:512]), [rpsT], [rBs[5]])
                        V(lambda e, pq3=pq3: e.tensor_copy(out=qTB[:, :, 64:128], in_=pq3[:, :, 64:128]), [rpsT], [rqTB])
                        cur = i % 2
                        nxtb = 1 - cur
                        for hh in range(2):
                            hs = slice(hh * 128, (hh + 1) * 128)
                            T(lambda e, hs=hs, Bs=Bs: e.matmul(psS[1][:, hs], lhsT=Bs[5][:, hs], rhs=Bs[4][:, hs], start=True, stop=True),
                              [rBs[5], rBs[4]], [rpsS[1]])
                        for hh in range(2):
                            hs = slice(hh * 128, (hh + 1) * 128)
                            V(lambda e, hs=hs, Bs=Bs: e.tensor_tensor(out=Bs[6][:, hs], in0=psS[1][:, hs], in1=hgmb[:], op=ALU.mult),
                              [rpsS[1], rhgmb], [rBs[6]])
                        for hh in range(2):
                            hs = slice(hh * 128, (hh + 1) * 128)
                            T(lambda e, hs=hs, hh=hh, Bs=Bs: e.matmul(psO[hh][:, 0:128], lhsT=Bs[6][:, hs], rhs=Bs[0][:, hs],
                                                                      start=True, stop=False), [rBs[6], rBs[0]], [rpsO[hh]])
                            T(lambda e, hh=hh, cur=cur: e.matmul(psO[hh][:, 0:128], lhsT=qTA[:, hh, :], rhs=Sbf[:, cur, hh, :],
                                                                 start=False, stop=False), [rqTA, rSbf[cur][hh]], [rpsO[hh]])
                        for hh in range(2):
                            hs = slice(hh * 128, (hh + 1) * 128)
                            T(lambda e, hs=hs, Bs=Bs: e.matmul(psS[0][:, hs], lhsT=Bs[3][0:64, hs], rhs=Bs[0][0:64, hs],
                                                               start=True, stop=True), [rBs[3], rBs[0]], [rpsS[0], rrow])
                        for hh in range(2):
                            hs = slice(hh * 128, (hh + 1) * 128)
                            V(lambda e, hs=hs, hh=hh, dsl=dsl: e.scalar_tensor_tensor(out=S32[:, hh, :], in0=S32[:, hh, :],
                                                                                      scalar=dsl[:, 2 * hh:2 * hh + 1], in1=psS[0][:, hs],
                                                                                      op0=ALU.mult, op1=ALU.add),
                              [rS32[hh], rds, rpsS[0]], [rS32[hh]])
                            G(lambda e, hh=hh, nxtb=nxtb: e.tensor_copy(out=Sbf[:, nxtb, hh, :], in_=S32[:, hh, :]),
                              [rS32[hh]], [rSbf[nxtb][hh]])
                        for hh in range(2):
                            T(lambda e, hh=hh, nxtb=nxtb: e.matmul(psO[hh][:, 0:128], lhsT=qTB[:, hh, :], rhs=Sbf[:, nxtb, hh, :],
                                                                   start=False, stop=True), [rqTB, rSbf[nxtb][hh]], [rpsO[hh], rrow])
                        for hh in range(2):
                            hs = slice(hh * 128, (hh + 1) * 128)
                            T(lambda e, hs=hs, Bs=Bs: e.matmul(psS[0][:, hs], lhsT=Bs[3][64:128, hs], rhs=Bs[0][64:128, hs],
                                                               start=True, stop=True), [rBs[3], rBs[0]], [rpsS[0], rrow])
                        for hh in range(2):
                            hs = slice(hh * 128, (hh + 1) * 128)
                            V(lambda e, hs=hs, hh=hh, dsl=dsl: e.scalar_tensor_tensor(out=S32[:, hh, :], in0=S32[:, hh, :],
                                                                                      scalar=dsl[:, 2 * hh + 1:2 * hh + 2], in1=psS[0][:, hs],
                                                                                      op0=ALU.mult, op1=ALU.add),
                              [rS32[hh], rds, rpsS[0]], [rS32[hh]])
                        for hh in range(2):
                            G(lambda e, hh=hh, nxtb=nxtb: e.tensor_copy(out=Sbf[:, nxtb, hh, :], in_=S32[:, hh, :]),
                              [rS32[hh]], [rSbf[nxtb][hh]])
                        ssl = sso4[:, 2 * (i % 2):2 * (i % 2) + 2]
                        r_sso = r_sso2[i % 2]
                        for hh in range(2):
                            A(lambda e, hh=hh, Fs=Fs, ssl=ssl: e.activation(out=Fs[7][:, 0:128], in_=psO[hh][:, 0:128], func=AF.Square,
                                                                            accum_out=ssl[:, hh:hh + 1]), [rpsO[hh]], [rFs[7], r_sso])
                        A(lambda e, ssl=ssl: e.activation(out=ssl, in_=ssl, func=AF.Ln, scale=1.0 / 128, bias=EPS), [r_sso], [r_sso])
                        A(lambda e, ssl=ssl: e.activation(out=ssl, in_=ssl, func=AF.Exp, scale=-0.5), [r_sso], [r_sso])
                        for hh in range(2):
                            hs = slice(hh * 128, (hh + 1) * 128)
                            V(lambda e, hh=hh, hs=hs, Fs=Fs, ssl=ssl: e.scalar_tensor_tensor(out=Fs[8][:, hs], in0=psO[hh][:, 0:128],
                                                                                             scalar=ssl[:, hh:hh + 1], in1=go_bc[:],
                                                                                             op0=ALU.mult, op1=ALU.mult),
                              [rpsO[hh], r_sso, rgains], [rFs[8]])
                        G(lambda e, Fs=Fs, sl=sl, sz=sz: e.tensor_tensor(out=y_tok[:, sl, :], in0=Fs[8], in1=sz[:, sl, :], op=ALU.mult),
                          [rFs[8], rsz[sl]], [ry[sl]])
                        outproj_tile(i, sl, last, obanks=[(psS[0], rpsS[0]), (psS[0], rpsS[0])])
            if not glist and last_layer:
                for i in range(NT):
                    out_dmas.append(DMA("sync", lambda e, i=i: e.dma_start(out=out_d[i * 128:(i + 1) * 128, :], in_=x_tok[:, i, :]),
                                        r=[rx[i]]))
        if SCHED:
            if SCHED2:
                P.schedule2(SDELTA)
            else:
                P.schedule()
        P.emit(st, out_dmas)
    build_nc.stats = P.stats
    return nc


_CACHE = {}


def _get_nc(layers, groups):
    key = (tuple(layers), tuple(groups))
    if key not in _CACHE:
        _CACHE[key] = build_nc(layers, groups)
    return _CACHE[key]


def run(inputs, layers=(0, 1), groups=ALL_GROUPS, cores=8):
    nc = _get_nc(layers, groups)
    f = lambda a: np.ascontiguousarray(np.asarray(a))
    cst = make_consts()
    shared = {k: f(inputs[k]).astype(np.float32, copy=False) for k in
              ("norm_g", "w_in", "w_out", "moba_q_norm", "moba_k_norm", "hgrn_lb_logits", "hgrn_o_norm",
               "mem_norm_g", "w_mem_kv", "mem_q_norm", "mem_k_norm")}
    x = f(inputs["x"]); mem = f(inputs["mem"]); pos = f(inputs["positions"]).astype(np.int32, copy=False)
    in_maps = []
    for b in range(cores):
        m = dict(shared)
        m["x"] = x[b]
        m["mem"] = mem[b]
        m["pos"] = pos[b].reshape(16, 128)
        m["cst"] = cst
        in_maps.append(m)
    res = run_bass_kernel_spmd(nc, in_maps, core_ids=list(range(cores)))
    return np.stack([np.asarray(r["out"]) for r in res.results], axis=0)


def kernel(x, mem, positions, norm_g, w_in, w_out, moba_q_norm, moba_k_norm, hgrn_lb_logits,
           hgrn_o_norm, mem_norm_g, w_mem_kv, mem_q_norm, mem_k_norm):
    inputs = dict(x=x, mem=mem, positions=positions, norm_g=norm_g, w_in=w_in, w_out=w_out,
                  moba_q_norm=moba_q_norm, moba_k_norm=moba_k_norm, hgrn_lb_logits=hgrn_lb_logits,
                  hgrn_o_norm=hgrn_o_norm, mem_norm_g=mem_norm_g, w_mem_kv=w_mem_kv,
                  mem_q_norm=mem_q_norm, mem_k_norm=mem_k_norm)
    return run(inputs).astype(np.float32, copy=False)
```

```python
import numpy as np
from contextlib import ExitStack
import concourse.bass as bass
import concourse.mybir as mybir
from concourse.bass_utils import run_bass_kernel_spmd

F32 = mybir.dt.float32
BF16 = mybir.dt.bfloat16
I32 = mybir.dt.int32
AF = mybir.ActivationFunctionType
ALU = mybir.AluOpType
AX = mybir.AxisListType

S = 2048
D = 1024
NT = 16
EPS = 1e-6
NCST = 576
import os as _os0
ALL_GROUPS = tuple(_os0.environ.get("ORDER", "A0,A1,H0,H1,M0,M1").split(","))
import os as _os
SCHED = _os.environ.get("SCHED", "1") == "1"
PAR1 = int(_os.environ.get("PAR1", "1"))
PAR2 = int(_os.environ.get("PAR2", "1"))
GPAR = int(_os.environ.get("GPAR", "1"))
PS3 = int(_os.environ.get("PS3", "3"))
MASKV = int(_os.environ.get("MASKV", "1"))
OPB = int(_os.environ.get("OPB", "1"))
SCHED2 = int(_os.environ.get("SCHED2", "1"))
SDELTA = float(_os.environ.get("SDELTA", "100"))
LATX = float(_os.environ.get("LATX", "180"))
PEK = float(_os.environ.get("PEK", "0.65"))
ACTK = float(_os.environ.get("ACTK", "1.0"))
DVEK = float(_os.environ.get("DVEK", "1.0"))
POOLK = float(_os.environ.get("POOLK", "1.0"))
LATS = float(_os.environ.get("LATS", "60"))
XQ = int(_os.environ.get("XQ", "0"))
GOFF = int(_os.environ.get("GOFF", "0"))
TRANS = int(_os.environ.get("TRANS", "1"))
PRUNE = int(_os.environ.get("PRUNE", "1"))
WIDE = int(_os.environ.get("WIDE", "1"))
WIDEM = int(_os.environ.get("WIDEM", "1"))


class Res:
    __slots__ = ("name", "w", "r", "excl")

    def __init__(self, name, excl=False):
        self.name = name
        self.w = None
        self.r = []
        self.excl = excl


class _Rec:
    def __init__(self):
        self.name = None
        self.args = ()
        self.kw = {}

    def __getattr__(self, name):
        def f(*a, **k):
            self.name, self.args, self.kw = name, a, k
            return self
        return f


def _free_size(ap):
    n = 1
    for d in list(ap.shape)[1:]:
        n *= int(d)
    return n


class Op:
    __slots__ = ("eng", "fn", "deps", "sdeps", "sig", "idx", "dma", "sem", "val", "i", "cost", "start")

    def __init__(self, eng, fn, deps, sdeps, dma):
        self.eng = eng
        self.fn = fn
        self.deps = deps
        self.sdeps = sdeps
        self.sig = False
        self.idx = 0
        self.dma = dma
        self.sem = None
        self.val = 0
        self.i = 0
        self.start = 0.0
        rec = _Rec()
        fn(rec)
        out = rec.kw.get("out", rec.args[0] if rec.args else None)
        n = _free_size(out) if out is not None else 64
        if dma:
            c = 2000.0 + n * int(out.shape[0]) * 4 / 120.0
        elif eng == "tensor":
            if rec.name == "transpose":
                c = 110.0
            else:
                lhsT = rec.kw.get("lhsT")
                f32 = lhsT is not None and lhsT.dtype == F32
                c = PEK * (64.0 + max(n, 64) / 2.0) * (4.0 if f32 else 1.0)
        elif eng == "scalar":
            c = ACTK * (200.0 + n / 1.2)
        elif eng == "vector":
            c = DVEK * (120.0 + n / 0.96 * (8.0 if rec.name == "reciprocal" else 1.0))
        else:
            c = POOLK * (300.0 + n / 0.5)
        self.cost = c


class Prog:
    ENGS = ["tensor", "vector", "scalar", "gpsimd", "sync"]

    def __init__(self, nc):
        self.nc = nc
        self.ops = []

    phase = None
    tok = None
    tokset = ()

    def op(self, eng, fn, reads=(), writes=(), dma=False):
        if self.phase == "gate" and self.tok is not None:
            writes = list(writes) + [self.tok]
        elif self.phase == "chain" and eng in self.tokset:
            reads = list(reads) + [self.tok]
        deps, sdeps = {}, {}

        def add(d):
            if d.dma or dma or d.eng != eng or eng != "tensor":
                deps[id(d)] = d
            else:
                sdeps[id(d)] = d
        for r in reads:
            if r.w is not None:
                add(r.w)
            if r.excl:
                for d in r.r:
                    if d.eng != eng:
                        add(d)
        for w in writes:
            if w.w is not None:
                add(w.w)
            for d in w.r:
                add(d)
        o = Op(eng, fn, list(deps.values()), list(sdeps.values()), dma)
        for r in reads:
            r.r.append(o)
        for w in writes:
            w.w = o
            w.r = []
        self.ops.append(o)
        return o

    def schedule(self):
        import heapq
        ops = self.ops
        for i, o in enumerate(ops):
            o.i = i
        succs = [[] for _ in ops]
        npred = [0] * len(ops)
        for o in ops:
            ds = o.deps + o.sdeps
            npred[o.i] = len(ds)
            for d in ds:
                succs[d.i].append(o)
        ready = [0.0] * len(ops)
        free = {e: 0.0 for e in self.ENGS}
        heap = [(0.0, o.i) for o in ops if npred[o.i] == 0]
        heapq.heapify(heap)
        done = 0
        while heap:
            t, i = heapq.heappop(heap)
            o = ops[i]
            st = max(ready[i], free[o.eng])
            if st > t + 1e-9:
                heapq.heappush(heap, (st, i))
                continue
            o.start = st
            if o.dma:
                free[o.eng] = st + 150.0
            else:
                free[o.eng] = st + o.cost
            fin = st + o.cost
            done += 1
            for sc in succs[i]:
                lat = 60.0 if (sc.eng == o.eng and not o.dma) else 180.0
                if fin + lat > ready[sc.i]:
                    ready[sc.i] = fin + lat
                npred[sc.i] -= 1
                if npred[sc.i] == 0:
                    heapq.heappush(heap, (max(ready[sc.i], free[sc.eng]), sc.i))
        assert done == len(ops), (done, len(ops))
        self.ops = sorted(ops, key=lambda o: (o.start, o.i))
        self.est_ns = max(o.start + o.cost for o in ops)

    def schedule2(self, delta=120.0):
        ops = self.ops
        n = len(ops)
        for i, o in enumerate(ops):
            o.i = i
        succs = [[] for _ in ops]
        npred = [0] * n
        for o in ops:
            ds = o.deps + o.sdeps
            npred[o.i] = len(ds)
            for d in ds:
                succs[d.i].append(o)
        blev = [0.0] * n
        for o in reversed(ops):
            b = 0.0
            for sc in succs[o.i]:
                lat = LATS if (sc.eng == o.eng and not o.dma) else LATX
                v = lat + blev[sc.i]
                if v > b:
                    b = v
            blev[o.i] = b + o.cost
        ready = [0.0] * n
        free = {e: 0.0 for e in self.ENGS}
        rsets = {e: [] for e in self.ENGS}
        for o in ops:
            if npred[o.i] == 0:
                rsets[o.eng].append(o.i)
        done = 0
        while done < n:
            best_e, best_t = None, 1e30
            for e in self.ENGS:
                rs = rsets[e]
                if not rs:
                    continue
                t = min(ready[i] for i in rs)
                if t < free[e]:
                    t = free[e]
                if t < best_t:
                    best_t, best_e = t, e
            e = best_e
            rs = rsets[e]
            lim = best_t + delta
            pick, pb = -1, -1.0
            for i in rs:
                if ready[i] <= lim and blev[i] > pb:
                    pb, pick = blev[i], i
            rs.remove(pick)
            o = ops[pick]
            st = max(ready[pick], free[e])
            o.start = st
            free[e] = st + (150.0 if o.dma else o.cost)
            fin = st + o.cost
            done += 1
            for sc in succs[pick]:
                lat = LATS if (sc.eng == o.eng and not o.dma) else LATX
                if fin + lat > ready[sc.i]:
                    ready[sc.i] = fin + lat
                npred[sc.i] -= 1
                if npred[sc.i] == 0:
                    rsets[sc.eng].append(sc.i)
        self.ops = sorted(ops, key=lambda o: (o.start, o.i))
        self.est_ns = max(o.start + o.cost for o in ops)

    def emit(self, stack, final_deps, ndma_sems=8):
        nc = self.nc
        if PRUNE:
            pos = {id(o): k for k, o in enumerate(self.ops)}
            for o in self.ops:
                best = {}
                keep = []
                for d in o.deps:
                    if d.dma:
                        keep.append(d)
                    elif d.eng not in best or pos[id(d)] > pos[id(best[d.eng])]:
                        best[d.eng] = d
                o.deps = keep + list(best.values())
        for o in self.ops:
            for d in o.deps:
                d.sig = True
        for d in final_deps:
            d.sig = True
        sems = {e: stack.enter_context(nc.semaphore("s_" + e)) for e in self.ENGS}
        cnt = {e: 0 for e in self.ENGS}
        pools, pool_i, pre_wait = {}, {}, {}
        for o in self.ops:
            if o.dma:
                if o.eng not in pools:
                    pools[o.eng] = [[stack.enter_context(nc.semaphore("d_%s_%d" % (o.eng, i))), 0]
                                    for i in range(ndma_sems)]
                    pool_i[o.eng] = 0
                p = pools[o.eng][pool_i[o.eng] % ndma_sems]
                pool_i[o.eng] += 1
                if p[1] > 0:
                    pre_wait[id(o)] = (p[0], p[1])
                p[1] += 16
                o.sem = p[0]
                o.val = p[1]
            elif o.sig:
                cnt[o.eng] += 1
                o.idx = cnt[o.eng]
        per = {e: [o for o in self.ops if o.eng == e] for e in self.ENGS}
        self.stats = {e: len(per[e]) for e in self.ENGS}
        known = {e: {} for e in self.ENGS}
        kn = {}
        plan = {}
        nw = 0

        def semkey(d):
            return (d.sem, d.val) if d.dma else (sems[d.eng], d.idx)

        for o in self.ops:
            kd = known[o.eng]
            ws = []
            for d in o.deps:
                sm, val = semkey(d)
                if kd.get(id(sm), (None, 0))[1] < val:
                    ws.append((sm, val))
                    kd[id(sm)] = (sm, val)
                if TRANS:
                    for k2, (s2, v2) in kn[id(d)].items():
                        if kd.get(k2, (None, 0))[1] < v2:
                            kd[k2] = (s2, v2)
            if o.dma:
                pw = pre_wait.get(id(o))
                if pw and kd.get(id(pw[0]), (None, 0))[1] < pw[1]:
                    ws.append(pw)
                    kd[id(pw[0])] = pw
            plan[id(o)] = ws
            nw += len(ws)
            if o.dma or o.sig:
                mine = dict(kd)
                sm, val = semkey(o)
                mine[id(sm)] = (sm, val)
                kn[id(o)] = mine
        fin_w = []
        kd = known["sync"]
        for d in final_deps:
            sm, val = semkey(d)
            if kd.get(id(sm), (None, 0))[1] < val:
                fin_w.append((sm, val))
                kd[id(sm)] = (sm, val)
        self.stats["waits"] = nw
        self.stats["sigs"] = {e: sum(1 for o in per[e] if o.sig and not o.dma) for e in self.ENGS}
        block = stack.enter_context(nc.Block())

        def mk(e):
            def body(engobj):
                for o in per[e]:
                    for sm, val in plan[id(o)]:
                        engobj.wait_ge(sm, val)
                    if o.dma:
                        o.fn(engobj).then_inc(o.sem, 16)
                    else:
                        ins = o.fn(engobj)
                        if o.sig:
                            ins.then_inc(sems[e], 1)
                if e == "sync":
                    for sm, val in fin_w:
                        engobj.wait_ge(sm, val)
            return body

        block.tensor(mk("tensor"))
        block.vector(mk("vector"))
        block.scalar(mk("scalar"))
        block.gpsimd(mk("gpsimd"))
        block.sync(mk("sync"))


def make_consts():
    c = np.zeros((128, NCST), np.float32)
    i = np.arange(128)
    c[:, 0:128] = np.eye(128)
    c[:, 128:256] = (i[None, :] >= i[:, None])
    same = (i[:, None] // 64) == (i[None, :] // 64)
    c[:, 256:384] = same & (i[:, None] <= i[None, :])
    c[:, 384:512] = same & (i[:, None] > i[None, :])
    c[:, 512] = i < 64
    c[:, 513] = i >= 64
    c[:, 514] = 1.0
    f64 = 500000.0 ** (-np.arange(8, dtype=np.float64) / 8.0)
    f = f64.astype(np.float32)
    flo = (f64 - f.astype(np.float64)).astype(np.float32)
    c[:, 515:523] = f[None, :]
    c[:, 523:531] = f[None, :]
    c[:, 547:555] = flo[None, :]
    c[:, 555:563] = flo[None, :]
    c[:, 531:539] = 0.0
    c[:, 539:547] = np.pi / 2
    return c


def build_nc(layers=(0, 1), groups=ALL_GROUPS):
    nc = bass.Bass("TRN2", target_bir_lowering=False)

    def din(name, shape, d=F32):
        return nc.dram_tensor(name, shape, d, kind="ExternalInput").ap()

    x_d = din("x", [S, D])
    mem_d = din("mem", [256, D])
    pos_d = din("pos", [16, 128], I32)
    ng_d = din("norm_g", [2, D])
    win_d = din("w_in", [2, D, 5120])
    wout_d = din("w_out", [2, 1536, D])
    gq_d = din("moba_q_norm", [2, 64])
    gk_d = din("moba_k_norm", [2, 64])
    lbl_d = din("hgrn_lb_logits", [2, 512])
    go_d = din("hgrn_o_norm", [2, 128])
    mng_d = din("mem_norm_g", [2, D])
    wkv_d = din("w_mem_kv", [2, D, 1024])
    gmq_d = din("mem_q_norm", [2, 128])
    gmk_d = din("mem_k_norm", [2, 128])
    cst_d = din("cst", [128, NCST])
    out_d = nc.dram_tensor("out", [S, D], F32, kind="ExternalOutput").ap()

    P = Prog(nc)
    if _os.environ.get("TOKR"):
        P.tok = Res("tok")
        P.tokset = tuple(_os.environ["TOKR"].split(","))
    with ExitStack() as st:
        def sb(name, shape, dt=F32):
            return st.enter_context(nc.sbuf_tensor("sb_" + name, shape, dt))

        def ps(name, shape, dt=F32):
            return st.enter_context(nc.psum_tensor("pp_" + name, shape, dt))

        def T(fn, r=(), w=()):
            return P.op("tensor", fn, r, w)

        def V(fn, r=(), w=()):
            return P.op("vector", fn, r, w)

        def A(fn, r=(), w=()):
            return P.op("scalar", fn, r, w)

        def G(fn, r=(), w=()):
            return P.op("gpsimd", fn, r, w)

        def DMA(q, fn, r=(), w=()):
            return P.op(q, fn, r, w, dma=True)

        x_tok = sb("x_tok", [128, NT, D]); rx = [Res("x%d" % i) for i in range(NT)]
        hT = sb("hT", [128, 8, S], BF16); rhT = [Res("hT%d" % i) for i in range(NT)]
        cst = sb("cst", [128, NCST]); rcst = Res("cst")
        idb = sb("idb", [128, 128], BF16); ridb = Res("idb")
        trib = sb("trib", [128, 128], BF16); rtrib = Res("trib")
        hgmb = sb("hgmb", [128, 128], BF16); rhgmb = Res("hgmb")
        gq_bc = sb("gq_bc", [128, 64]); gk_bc = sb("gk_bc", [128, 64]); go_bc = sb("go_bc", [128, 128])
        gmq_bc = sb("gmq_bc", [128, 128]); gmk_bc = sb("gmk_bc", [128, 128]); rgains = Res("gains")
        cs = sb("cs", [128, NT, 16]); sn = sb("sn", [128, NT, 16]); rrope = Res("rope")
        wb = [sb("wb%d" % i, [128, 8, 512], BF16) for i in range(3)]; rwb = [[Res("wb%da" % i), Res("wb%db" % i)] for i in range(3)]
        wo = sb("wo", [128, 2, D], BF16); rwo = Res("wo")
        Fp = [sb("F%d" % i, [128, 256]) for i in range(9)]; rF = [Res("F%d" % i) for i in range(9)]
        Bp = [sb("B%d" % i, [128, 256], BF16) for i in range(7)]; rB = [Res("B%d" % i) for i in range(7)]
        szs = [sb("sz%d" % k, [128, 4, 256]) for k in range(2)]; rszs = [[Res("sz%d_%d" % (k, i)) for i in range(4)] for k in range(2)]
        y_tok = sb("y_tok", [128, 4, 256], BF16); ry = [Res("y%d" % i) for i in range(4)]
        yT = [sb("yT%d" % i, [128, 256], BF16) for i in range(2)]; ryT = [Res("yT%d" % i) for i in range(2)]
        pT = [sb("pT%d" % i, [128, 512], BF16) for i in range(4)]; rpT = [Res("pT%d" % i) for i in range(4)]
        hbs = [sb("hb%d" % i, [128, D], BF16) for i in range(2)]; rhbs = [Res("hb0"), Res("hb1")]
        small = sb("small", [128, 128]); rsm = {}

        def sm(name, a, n):
            rsm[name] = Res("sm_" + name)
            return small[:, a:a + n], rsm[name]
        ss16, r_ss16 = sm("ss16", 0, 16)
        t16, r_t16 = sm("t16", 16, 16)
        rstd16, r_rstd16 = sm("rstd16", 32, 16)
        ss4, r_ss4 = sm("ss4", 48, 4)
        t4, r_t4 = sm("t4", 52, 4)
        rs4, r_rs4 = sm("rs4", 56, 4)
        rden, r_rden = sm("rden", 60, 4)
        rdenB, r_rdenB = sm("rdenB", 108, 4)
        dec4, r_dec4 = sm("dec4", 64, 4)
        sso, r_sso = sm("sso", 68, 2)
        to2, r_to2 = sm("to2", 70, 2)
        rso, r_rso = sm("rso", 72, 2)
        gm = sb("gm", [128, 4, 8]); rgm = Res("gm")
        top8 = sb("top8", [128, 4, 8]); rtop8 = Res("top8")
        selb = sb("selb", [128, 4, 8]); rselb = Res("selb")
        kT = sb("kT", [128, 4, S], BF16); rkT = [Res("kT%d" % i) for i in range(NT)]
        v_flat = sb("v_aug", [128, NT * 4 * 65], BF16); rv = [Res("v%d" % i) for i in range(NT)]
        v_aug = v_flat[:].rearrange("p (a b c) -> p a b c", a=NT, b=4)
        stage = v_flat[:, 0:2048].bitcast(F32); rstage = Res("stage")
        g_bcv = v_flat[:, 2048:4096].bitcast(F32); rg_bc = Res("g_bc")
        qTs = [sb("qT%d" % i, [128, 2048], BF16) for i in range(2)]; rqTs = [Res("qT0"), Res("qT1")]
        lbv = qTs[1][:, 0:2048].bitcast(F32); rlb = rqTs[1]
        lb_g = lbv[:, 0:256]; oml_g = lbv[:, 256:512]
        k_aug = sb("k_aug", [128, 4, 72], BF16); rk_aug = Res("k_aug")
        q_aug = sb("q_aug", [128, 4, 72], BF16); rq_aug = Res("q_aug")
        kmT32 = sb("kmT32", [128, 2, 2, 8]); rkm = Res("kmT32")
        S32 = sb("S32", [128, 2, 128]); rS32 = [Res("S32_0"), Res("S32_1")]
        Sbf = sb("Sbf", [128, 2, 2, 128], BF16); rSbf = [[Res("Sbf00"), Res("Sbf01")], [Res("Sbf10"), Res("Sbf11")]]
        qTA = sb("qTA", [128, 2, 128], BF16); qTB = sb("qTB", [128, 2, 128], BF16)
        rqTA = Res("qTA"); rqTB = Res("qTB")
        memT = sb("memT", [128, 8, 256], BF16); rmemT = Res("memT")
        kmT = sb("kmT", [128, 2, 256], BF16); rkmT = Res("kmT")
        vm_aug = sb("vm_aug", [128, 2, 2, 129], BF16); rvm = Res("vm")
        psA = [ps("psA%d" % i, [128, 512]) for i in range(2)]; rpsA = [Res("psA0", True), Res("psA1", True)]
        psT = ps("psT", [128, 1024], BF16); rpsT = Res("psT", True)
        psG = ps("psG", [128, 512]); rpsG = Res("psG", True)
        psS = [ps("psS%d" % i, [128, 512]) for i in range(2)]; rpsS = [Res("psS0", True), Res("psS1", True)]
        psO = [ps("psO%d" % i, [128, 512]) for i in range(2)]; rpsO = [Res("psO0", True), Res("psO1", True)]

        ctr = {"pa": 0, "w": 0, "ev": 0, "ps": 0, "pt": 0, "yt": 0, "mk": 0, "pw": 0, "pw3": 0, "ps2": 0}

        def nxt(k, n):
            v = ctr[k] % n
            ctr[k] += 1
            return v

        def evac(fn_v, fn_a, r, w):
            if nxt("ev", 2) == 0:
                return A(fn_a, r, w)
            return V(fn_v, r, w)

        DMA("sync", lambda e: e.dma_start(out=cst[:], in_=cst_d), w=[rcst])
        V(lambda e: e.tensor_copy(out=idb[:], in_=cst[:, 0:128]), [rcst], [ridb])
        V(lambda e: e.tensor_copy(out=trib[:], in_=cst[:, 128:256]), [rcst], [rtrib])
        V(lambda e: e.tensor_copy(out=hgmb[:], in_=cst[:, 256:384]), [rcst], [rhgmb])
        ident32 = cst[:, 0:128]
        G(lambda e: e.memset(vm_aug[:, :, :, 128:129], 1.0), w=[rvm])
        G(lambda e: e.memset(qTA[:], 0.0), w=[rqTA])
        G(lambda e: e.memset(qTB[:], 0.0), w=[rqTB])
        nI = sb("nI", [128, 256], I32); rnI = Res("nI")
        posi = nI[0:16, 0:128]; rposi = rnI
        posf = Fp[8][0:16, 0:128]; rposf = rF[8]
        DMA("sync", lambda e: e.dma_start(out=posi, in_=pos_d), w=[rposi])
        V(lambda e: e.tensor_copy(out=posf, in_=posi), [rposi], [rposf])
        T(lambda e: e.matmul(psG[:, 0:16], lhsT=posf, rhs=cst[0:16, 0:16], start=True, stop=True), [rposf, rcst], [rpsG])
        post, r_post = sm("post", 80, 16)
        V(lambda e: e.tensor_copy(out=post, in_=psG[:, 0:16]), [rpsG], [r_post])
        ang = Fp[0][:, 0:256].rearrange("p (i j) -> p i j", i=NT)
        V(lambda e: e.tensor_tensor(out=ang, in0=post.unsqueeze(2).to_broadcast([128, NT, 16]),
                                    in1=cst[:, 515:531].unsqueeze(1).to_broadcast([128, NT, 16]), op=ALU.mult),
          [r_post, rcst], [rF[0]])
        ang_lo = Fp[1][:, 0:256].rearrange("p (i j) -> p i j", i=NT)
        V(lambda e: e.tensor_tensor(out=ang_lo, in0=post.unsqueeze(2).to_broadcast([128, NT, 16]),
                                    in1=cst[:, 547:563].unsqueeze(1).to_broadcast([128, NT, 16]), op=ALU.mult),
          [r_post, rcst], [rF[1]])
        V(lambda e: e.tensor_tensor(out=ang, in0=ang, in1=ang_lo, op=ALU.add), [rF[0], rF[1]], [rF[0]])
        V(lambda e: e.tensor_tensor(out=ang, in0=ang, in1=cst[:, 531:547].unsqueeze(1).to_broadcast([128, NT, 16]),
                                    op=ALU.add), [rF[0], rcst], [rF[0]])
        V(lambda e: e.tensor_scalar(out=Fp[1][:], in0=Fp[0][:], scalar1=float(1.0 / (2 * np.pi)), scalar2=None,
                                    op0=ALU.mult), [rF[0]], [rF[1]])
        V(lambda e: e.tensor_copy(out=nI[:], in_=Fp[1][:]), [rF[1]], [rnI])
        V(lambda e: e.tensor_copy(out=Fp[1][:], in_=nI[:]), [rnI], [rF[1]])
        C1 = 6.28125
        C2 = float(2 * np.pi - 6.28125)
        V(lambda e: e.scalar_tensor_tensor(out=Fp[2][:], in0=Fp[1][:], scalar=-C1, in1=Fp[0][:],
                                           op0=ALU.mult, op1=ALU.add), [rF[1], rF[0]], [rF[2]])
        V(lambda e: e.scalar_tensor_tensor(out=Fp[2][:], in0=Fp[1][:], scalar=-C2, in1=Fp[2][:],
                                           op0=ALU.mult, op1=ALU.add), [rF[1], rF[2]], [rF[2]])
        V(lambda e: e.tensor_scalar(out=Fp[2][:], in0=Fp[2][:], scalar1=float(np.pi), scalar2=float(-np.pi),
                                    op0=ALU.min, op1=ALU.max), [rF[2]], [rF[2]])
        A(lambda e: e.activation(out=Fp[3][:], in_=Fp[2][:], func=AF.Sin), [rF[2]], [rF[3]])
        sc = Fp[3][:, 0:256].rearrange("p (i j) -> p i j", i=NT)
        V(lambda e: e.tensor_copy(out=cs[:, :, 0:8], in_=sc[:, :, 8:16]), [rF[3]], [rrope])
        V(lambda e: e.tensor_copy(out=cs[:, :, 8:16], in_=sc[:, :, 8:16]), [rF[3]], [rrope])
        V(lambda e: e.tensor_scalar(out=sn[:, :, 0:8], in0=sc[:, :, 0:8], scalar1=-1.0, scalar2=None, op0=ALU.mult),
          [rF[3]], [rrope])
        V(lambda e: e.tensor_copy(out=sn[:, :, 8:16], in_=sc[:, :, 0:8]), [rF[3]], [rrope])
        def load_w(dst, rdst, src_ap):
            return DMA("gpsimd", lambda e: e.dma_start(out=dst, in_=src_ap), w=[rdst])

        def w_in_cols(l, c0, n):
            return win_d[l].rearrange("(c p) n -> p c n", p=128)[:, :, c0:c0 + n]

        psGb = psG[:].bitcast(BF16)

        def rms_to_T(src_tile, rsrc, gain, rgain, dstT, rdst, col0, ssc, r_ssc, tsc, r_tsc, rsc, r_rsc, k):
            hb = hbs[k % 2]; rhb = rhbs[k % 2]
            pst, rpst = (psT, rpsT) if k % 2 == 0 else (psGb, rpsG)
            A(lambda e: e.activation(out=hb[:], in_=src_tile, func=AF.Square, accum_out=ssc[:, k:k + 1]),
              [rsrc], [rhb, r_ssc])
            A(lambda e: e.activation(out=tsc[:, k:k + 1], in_=ssc[:, k:k + 1], func=AF.Ln, scale=1.0 / D, bias=EPS),
              [r_ssc], [r_tsc])
            A(lambda e: e.activation(out=rsc[:, k:k + 1], in_=tsc[:, k:k + 1], func=AF.Exp, scale=-0.5),
              [r_tsc], [r_rsc])
            V(lambda e: e.scalar_tensor_tensor(out=hb[:], in0=src_tile, scalar=rsc[:, k:k + 1], in1=gain,
                                               op0=ALU.mult, op1=ALU.mult), [rsrc, r_rsc, rgain], [rhb])
            for c in range(8):
                T(lambda e, c=c: e.transpose(out=pst[:, c * 128:(c + 1) * 128], in_=hb[:, c * 128:(c + 1) * 128],
                                             identity=idb[:]), [rhb, ridb], [rpst])
            src3 = pst[:, 0:1024].rearrange("p (c t) -> p c t", c=8)
            evac(lambda e: e.tensor_copy(out=dstT[:, :, col0:col0 + 128], in_=src3),
                 lambda e: e.copy(out=dstT[:, :, col0:col0 + 128], in_=src3), [rpst], [rdst])

        def proj(lhs_cols, rlhs, wt, rwt, ncols=512, lhsT_src=None, wide=False):
            if wide == 3:
                pt, rpt = ((psA[0], rpsA[0]), (psA[1], rpsA[1]), (psG, rpsG))[nxt("pw3", 3)]
            elif wide:
                pt, rpt = ((psA[0], rpsA[0]), (psA[1], rpsA[1]), (psO[0], rpsO[0]), (psO[1], rpsO[1]))[nxt("pw", 4)]
            else:
                b = nxt("pa", 2)
                pt, rpt = psA[b], rpsA[b]
            src = hT if lhsT_src is None else lhsT_src
            for c in range(8):
                T(lambda e, c=c: e.matmul(pt[:, 0:ncols], lhsT=src[:, c, lhs_cols:lhs_cols + 128],
                                          rhs=wt[:, c, 0:ncols], start=(c == 0), stop=(c == 7)),
                  [rlhs] + list(rwt), [rpt])
            return pt, rpt

        def headnorm(src_ps, rps, H, Dh, gain, outF, routF, tmpA, rtmpA, tmpB, rtmpB):
            n = H * Dh
            A(lambda e: e.activation(out=tmpA[:, 0:n], in_=src_ps, func=AF.Square), [rps], [rtmpA])
            V(lambda e: e.tensor_reduce(out=ss4[:, 0:H], in_=tmpA[:, 0:n].rearrange("p (h d) -> p h d", h=H),
                                        axis=AX.X, op=ALU.add), [rtmpA], [r_ss4])
            A(lambda e: e.activation(out=t4[:, 0:H], in_=ss4[:, 0:H], func=AF.Ln, scale=1.0 / Dh, bias=EPS),
              [r_ss4], [r_t4])
            A(lambda e: e.activation(out=rs4[:, 0:H], in_=t4[:, 0:H], func=AF.Exp, scale=-0.5), [r_t4], [r_rs4])
            V(lambda e: e.tensor_tensor(out=tmpB[:, 0:n].rearrange("p (h d) -> p h d", h=H),
                                        in0=src_ps.rearrange("p (h d) -> p h d", h=H),
                                        in1=rs4[:, 0:H].unsqueeze(2).to_broadcast([128, H, Dh]), op=ALU.mult),
              [rps, r_rs4], [rtmpB])
            (G if GOFF else V)(lambda e: e.tensor_tensor(out=outF[:, 0:n].rearrange("p (h d) -> p h d", h=H),
                                                         in0=tmpB[:, 0:n].rearrange("p (h d) -> p h d", h=H),
                                                         in1=gain.unsqueeze(1).to_broadcast([128, H, Dh]), op=ALU.mult),
                               [rtmpB, rgains], [routF])

        def rope(Fx, rFx, i, tR, rtR):
            x3 = Fx[:, 0:256].rearrange("p (h d) -> p h d", h=4)
            a3 = tR[:, 0:64].rearrange("p (h d) -> p h d", h=4)
            b3 = tR[:, 64:128].rearrange("p (h d) -> p h d", h=4)
            rtA = rtR
            rtB = rtR
            G(lambda e: e.tensor_tensor(out=a3, in0=x3[:, :, 0:16], in1=cs[:, i, :].unsqueeze(1).to_broadcast([128, 4, 16]),
                                        op=ALU.mult), [rFx, rrope], [rtA])
            G(lambda e: e.tensor_tensor(out=b3[:, :, 0:8], in0=x3[:, :, 8:16],
                                        in1=sn[:, i, 0:8].unsqueeze(1).to_broadcast([128, 4, 8]), op=ALU.mult),
              [rFx, rrope], [rtB])
            G(lambda e: e.tensor_tensor(out=b3[:, :, 8:16], in0=x3[:, :, 0:8],
                                        in1=sn[:, i, 8:16].unsqueeze(1).to_broadcast([128, 4, 8]), op=ALU.mult),
              [rFx, rrope], [rtB])
            G(lambda e: e.tensor_tensor(out=x3[:, :, 0:16], in0=a3, in1=b3, op=ALU.add), [rtA, rtB], [rFx])

        def silu_ps(dst, rdst, src_ps, rps, defer=False):
            A(lambda e: e.activation(out=dst, in_=src_ps, func=AF.Exp, scale=-1.0), [rps], [rdst])
            A(lambda e: e.activation(out=dst, in_=dst, func=AF.Ln, bias=1.0), [rdst], [rdst])
            A(lambda e: e.activation(out=dst, in_=dst, func=AF.Exp, scale=-1.0), [rdst], [rdst])

            def fin():
                V(lambda e: e.tensor_tensor(out=dst, in0=src_ps, in1=dst, op=ALU.mult), [rps, rdst], [rdst])
            if defer:
                return fin
            fin()

        SC = [((Fp[0], rF[0]), (Fp[1], rF[1]), (Fp[2], rF[2]), (Fp[3], rF[3])),
              ((Fp[4], rF[4]), (Fp[6], rF[6]), (Fp[7], rF[7]), (Fp[8], rF[8]))]
        AUG = [(k_aug, rk_aug), (q_aug, rq_aug)]
        if _os.environ.get("NOPAR", "0") == "1":
            SC[1] = SC[0]
        if _os.environ.get("NOAUG", "0") == "1":
            AUG[1] = AUG[0]

        dec8, _r = sm("dec8", 96, 8)
        r_dec8 = [Res("dec8a"), Res("dec8b")]
        sso4, _r2 = sm("sso4", 104, 4)
        r_sso2 = [Res("ssoA"), Res("ssoB")]
        dummy = sb("dummy", [128, 8])
        kflat = kT[:].rearrange("p h t -> p (h t)")
        kf32 = kflat[:, 0:4608].bitcast(F32)
        Fq = [kf32[:, k * 256:(k + 1) * 256] for k in range(9)]
        rFq = [Res("Fq%d" % k) for k in range(9)]
        Bq = [kflat[:, 4608 + k * 256:4608 + (k + 1) * 256] for k in range(7)]
        rBq = [Res("Bq%d" % k) for k in range(7)]
        HSETS = [([t[:] for t in Fp], rF, [t[:] for t in Bp], rB), (Fq, rFq, Bq, rBq)]

        def barrier():
            G(lambda e: e.memset(dummy[:], 0.0), w=list(rkT) + rFq + rBq + list(rv) + [rstage, rg_bc])

        rrow = Res("pe_rowfence")

        out_dmas = []

        def outproj_tile(i, r, last, obanks=None):
            yb = nxt("yt", 2)
            for pp in range(2):
                T(lambda e, pp=pp: e.transpose(out=psT[:, pp * 128:(pp + 1) * 128], in_=y_tok[:, r, pp * 128:(pp + 1) * 128],
                                               identity=idb[:]), [ry[r], ridb], [rpsT])
            evac(lambda e: e.tensor_copy(out=yT[yb][:], in_=psT[:, 0:256]),
                 lambda e: e.copy(out=yT[yb][:], in_=psT[:, 0:256]), [rpsT], [ryT[yb]])
            for half in range(2):
                if obanks is None:
                    b = nxt("pa", 2)
                    pso, rpso = psA[b], rpsA[b]
                else:
                    pso, rpso = obanks[half]
                for pp in range(2):
                    T(lambda e, pp=pp, half=half, pso=pso: e.matmul(pso[:, 0:512], lhsT=yT[yb][:, pp * 128:(pp + 1) * 128],
                                                                    rhs=wo[:, pp, half * 512:(half + 1) * 512],
                                                                    start=(pp == 0), stop=(pp == 1)),
                      [ryT[yb], rwo], [rpso])
                V(lambda e, half=half, pso=pso: e.tensor_tensor(out=x_tok[:, i, half * 512:(half + 1) * 512], in0=pso[:, 0:512],
                                                                in1=x_tok[:, i, half * 512:(half + 1) * 512], op=ALU.add),
                  [rpso, rx[i]], [rx[i]])
            if last:
                out_dmas.append(DMA("sync", lambda e: e.dma_start(out=out_d[i * 128:(i + 1) * 128, :], in_=x_tok[:, i, :]),
                                    r=[rx[i]]))

        def attn_chunk(Q, H, Dh, KP, scale, key_tiles, causal, kTsrc, rkTsrc, vsrc, rvsrc, qview, rqT, sz, rsz, nsb=None):
            DA = Dh + 1
            for h in range(H):
                if Dh == 64:
                    o_b = h % 2
                    banks = [o_b, o_b, o_b, o_b]
                    offs = [0, DA, 2 * DA, 3 * DA]
                else:
                    banks = [0, 0, 1, 1]
                    offs = [0, DA, 0, DA]
                started = set()
                kts = key_tiles(Q)
                for kt in kts:
                    j = kt - 4 * Q if causal else -1
                    q0 = max(j, 0) * 128
                    N = 512 - q0
                    sbk = nxt("ps", PS3) if nsb is None else nxt("ps2", nsb)
                    pss, rpss = ((psS[0], rpsS[0]), (psS[1], rpsS[1]), (psG, rpsG))[sbk]
                    T(lambda e, kt=kt, h=h, q0=q0, N=N, pss=pss: e.matmul(
                        pss[:, 0:N], lhsT=kTsrc(h, kt), rhs=qview(h)[:, q0:512], start=True, stop=True),
                      [rkTsrc(kt), rqT], [rpss])
                    pb = nxt("pt", 4)
                    A(lambda e, N=N, pss=pss, pb=pb: e.activation(out=pT[pb][:, 0:N], in_=pss[:, 0:N], func=AF.Exp,
                                                                  scale=scale), [rpss], [rpT[pb]])
                    if j >= 0:
                        (V if (MASKV and nxt("mk", 2) == 0) else G)(
                            lambda e, pb=pb: e.tensor_tensor(out=pT[pb][:, 0:128], in0=pT[pb][:, 0:128], in1=trib[:],
                                                             op=ALU.mult), [rpT[pb], rtrib], [rpT[pb]])
                    for r in range(max(j, 0), 4):
                        bk = banks[r]
                        first = bk not in started
                        started.add(bk)
                        T(lambda e, r=r, kt=kt, h=h, q0=q0, pb=pb, bk=bk, first=first: e.matmul(
                            psO[bk][:, offs[r]:offs[r] + DA], lhsT=pT[pb][:, r * 128 - q0:r * 128 - q0 + 128],
                            rhs=vsrc(h, kt), start=first, stop=False, skip_group_check=True),
                          [rpT[pb], rvsrc(kt)], [rpsO[bk]])
                rd, r_rd = (rden, r_rden) if h % 2 == 0 else (rdenB, r_rdenB)
                for bk0 in sorted(set(banks)):
                    rs_ = [r for r in range(4) if banks[r] == bk0]
                    nr = len(rs_)
                    V(lambda e, bk0=bk0, rs_=rs_, nr=nr, rd=rd: e.reciprocal(
                        out=rd[:, rs_[0]:rs_[0] + nr],
                        in_=psO[bk0][:, 0:nr * DA].rearrange("p (r c) -> p r c", r=nr)[:, :, Dh:DA]),
                      [rpsO[bk0]], [r_rd])
                    V(lambda e, bk0=bk0, rs_=rs_, nr=nr, h=h: e.tensor_tensor(
                        out=sz[:, rs_[0]:rs_[0] + nr, h * Dh:(h + 1) * Dh],
                        in0=psO[bk0][:, 0:nr * DA].rearrange("p (r c) -> p r c", r=nr)[:, :, 0:Dh],
                        in1=sz[:, rs_[0]:rs_[0] + nr, h * Dh:(h + 1) * Dh], op=ALU.mult),
                      [rpsO[bk0]] + [rsz[r] for r in rs_], [rsz[r] for r in rs_])
                G(lambda e, h=h, rd=rd: e.tensor_tensor(out=y_tok[:, :, h * Dh:(h + 1) * Dh], in0=sz[:, :, h * Dh:(h + 1) * Dh],
                                                        in1=rd[:, 0:4].unsqueeze(2).to_broadcast([128, 4, Dh]), op=ALU.mult),
                  list(rsz) + [r_rd], list(ry))

        for li, l in enumerate(layers):
            last_layer = (li == len(layers) - 1)
            DMA("sync", lambda e, l=l: e.dma_start(out=g_bcv, in_=ng_d[l:l + 1, :].partition_broadcast(128)), w=[rg_bc])
            for dst, src in ((gq_bc, gq_d), (gk_bc, gk_d), (go_bc, go_d), (gmq_bc, gmq_d), (gmk_bc, gmk_d)):
                DMA("sync", lambda e, l=l, dst=dst, src=src: e.dma_start(out=dst[:], in_=src[l:l + 1, :].partition_broadcast(128)),
                    w=[rgains])
            for i in range(NT):
                if li == 0:
                    DMA(("scalar" if (XQ and i % 2 == 1) else "sync"), lambda e, i=i: e.dma_start(out=x_tok[:, i, :], in_=x_d[i * 128:(i + 1) * 128, :]), w=[rx[i]])
                rms_to_T(x_tok[:, i, :], rx[i], g_bcv, rg_bc, hT, rhT[i], i * 128, ss16, r_ss16, t16, r_t16,
                         rstd16, r_rstd16, i)
            glist = [g for g in ALL_GROUPS if g in groups]
            for gi, gname in enumerate(glist):
                last = last_layer and gi == len(glist) - 1
                kind = gname[0]
                g = int(gname[1])
                if kind == "A":
                    w1 = nxt("w", 3); w2 = nxt("w", 3)
                    load_w(wb[w1][:, :, 0:256], rwb[w1][0], w_in_cols(l, 512 + 256 * g, 256))
                    load_w(wb[w1][:, :, 256:512], rwb[w1][1], w_in_cols(l, 1024 + 256 * g, 256))
                    load_w(wb[w2][:, :, 0:256], rwb[w2][0], w_in_cols(l, 256 * g, 256))
                    load_w(wb[w2][:, :, 256:512], rwb[w2][1], w_in_cols(l, 3584 + 256 * g, 256))
                    load_w(wo[:], rwo, wout_d[l, 256 * g:256 * g + 256, :].rearrange("(c p) n -> p c n", p=128))
                    barrier()
                    G(lambda e: e.memset(v_aug[:, :, :, 64:65], 1.0), w=rv)
                    V(lambda e: e.memset(psG[:, 0:16], 0.0), w=[rpsG])
                    for i in range(NT):
                        n_blk = i // 2
                        (tA, rtA), (tB, rtB), (FO, rFO), (tR, rtR) = SC[(i % 2) * PAR1]
                        ka, rka = AUG[(i % 2) * PAR1]
                        pa, rpa = proj(i * 128, rhT[i], wb[w1], rwb[w1], wide=bool(WIDE))
                        A(lambda e, i=i, pa=pa: e.copy(out=v_aug[:, i, :, 0:64],
                                                       in_=pa[:, 256:512].rearrange("p (h d) -> p h d", h=4)),
                          [rpa], [rv[i]])
                        headnorm(pa[:, 0:256], rpa, 4, 64, gk_bc[:], FO, rFO, tA, rtA, tB, rtB)
                        rope(FO, rFO, i, tR, rtR)
                        G(lambda e, ka=ka, FO=FO: e.tensor_copy(out=ka[:, :, 0:64], in_=FO[:].rearrange("p (h d) -> p h d", h=4)),
                          [rFO], [rka])
                        G(lambda e, ka=ka: e.memset(ka[:, :, 64:72], 0.0), w=[rka])
                        G(lambda e, ka=ka, n_blk=n_blk: e.memset(ka[:, :, 64 + n_blk:65 + n_blk], 1.0), w=[rka])
                        for pp in range(2):
                            T(lambda e, pp=pp, n_blk=n_blk, FO=FO: e.matmul(psG[:, pp * 8 + n_blk:pp * 8 + n_blk + 1],
                                                                            lhsT=FO[:, pp * 128:(pp + 1) * 128], rhs=cst[:, 514:515],
                                                                            start=False, stop=False, skip_group_check=True),
                              [rFO, rcst], [rpsG])
                        for h in range(4):
                            T(lambda e, h=h, ka=ka: e.transpose(out=psT[0:72, h * 128:(h + 1) * 128], in_=ka[:, h, :], identity=idb[:]),
                              [rka, ridb], [rpsT])
                        src3 = psT[0:72, 0:512].rearrange("p (h t) -> p h t", h=4)
                        evac(lambda e, i=i, src3=src3: e.tensor_copy(out=kT[0:72, :, i * 128:(i + 1) * 128], in_=src3),
                             lambda e, i=i, src3=src3: e.copy(out=kT[0:72, :, i * 128:(i + 1) * 128], in_=src3),
                             [rpsT], [rkT[i]])
                    G(lambda e: e.memset(kmT32[:], 0.0), w=[rkm])
                    A(lambda e: e.copy(out=kmT32[0:64, :, 0, :], in_=psG[0:64, 0:16].rearrange("p (a n) -> p a n", a=2)), [rpsG], [rkm])
                    A(lambda e: e.copy(out=kmT32[64:128, :, 1, :], in_=psG[64:128, 0:16].rearrange("p (a n) -> p a n", a=2)), [rpsG], [rkm])
                    for Q in range(4):
                        qT3 = qTs[Q % 2][:, 0:2048].rearrange("p (h t) -> p h t", h=4)
                        rqT = rqTs[Q % 2]
                        sz = szs[Q % 2]; rsz = rszs[Q % 2]
                        for r in range(4):
                            i = 4 * Q + r
                            own = i // 2
                            P.phase = "chain"
                            par = (i % 2) * PAR2 * (1 if (own < 4 or GPAR) else 0)
                            (tA, rtA), (tB, rtB), (FO, rFO), (tR, rtR) = SC[par]
                            qa, rqa = AUG[par]
                            pa, rpa = proj(i * 128, rhT[i], wb[w2], rwb[w2])
                            fin = silu_ps(sz[:, r, :], rsz[r], pa[:, 256:512], rpa, defer=True)
                            headnorm(pa[:, 0:256], rpa, 4, 64, gq_bc[:], FO, rFO, tA, rtA, tB, rtB)
                            fin()
                            rope(FO, rFO, i, tR, rtR)
                            G(lambda e, qa=qa, FO=FO: e.tensor_copy(out=qa[:, :, 0:64], in_=FO[:].rearrange("p (h d) -> p h d", h=4)),
                              [rFO], [rqa])
                            if own >= 4:
                                P.phase = "gate"
                                for pp in range(2):
                                    T(lambda e, pp=pp, FO=FO: e.matmul(psG[:, pp * 128:(pp + 1) * 128],
                                                                       lhsT=FO[:, pp * 128:(pp + 1) * 128], rhs=ident32,
                                                                       start=True, stop=True),
                                      [rFO, rcst], [rpsG])
                                A(lambda e: e.copy(out=Fp[5][:], in_=psG[:, 0:256]), [rpsG], [rF[5]])
                                for h in range(4):
                                    T(lambda e, h=h, own=own: e.matmul(
                                        psG[:, 256 + h * 8:256 + h * 8 + own],
                                        lhsT=Fp[5][:, (h // 2) * 128:(h // 2) * 128 + 128],
                                        rhs=kmT32[:, h // 2, h % 2, 0:own], start=True, stop=True),
                                      [rF[5], rkm], [rpsG])
                                V(lambda e: e.memset(gm[:], -1.0e30), w=[rgm])
                                V(lambda e, own=own: e.tensor_copy(
                                    out=gm[:, :, 0:own], in_=psG[:, 256:288].rearrange("p (h n) -> p h n", h=4)[:, :, 0:own]),
                                  [rpsG], [rgm])
                                for h in range(4):
                                    V(lambda e, h=h: e.max(out=top8[:, h, :], in_=gm[:, h, :]), [rgm], [rtop8])
                                V(lambda e: e.tensor_tensor(out=selb[:], in0=gm[:], in1=top8[:, :, 2:3].to_broadcast([128, 4, 8]),
                                                            op=ALU.is_ge), [rgm, rtop8], [rselb])
                                V(lambda e, qa=qa: e.tensor_scalar(out=qa[:, :, 64:72], in0=selb[:], scalar1=30000.0,
                                                                   scalar2=-30000.0, op0=ALU.mult, op1=ALU.add), [rselb], [rqa])
                                V(lambda e, qa=qa, own=own: e.memset(qa[:, :, 64 + own:65 + own], 0.0), w=[rqa])
                            else:
                                G(lambda e, qa=qa: e.memset(qa[:, :, 64:72], 0.0), w=[rqa])
                            P.phase = None
                            for h in range(4):
                                T(lambda e, h=h, qa=qa: e.transpose(out=psT[0:72, h * 128:(h + 1) * 128], in_=qa[:, h, :],
                                                                    identity=idb[:]), [rqa, ridb], [rpsT])
                            src3 = psT[0:72, 0:512].rearrange("p (h t) -> p h t", h=4)
                            evac(lambda e, r=r, src3=src3, qT3=qT3: e.tensor_copy(out=qT3[0:72, :, r * 128:(r + 1) * 128], in_=src3),
                                 lambda e, r=r, src3=src3, qT3=qT3: e.copy(out=qT3[0:72, :, r * 128:(r + 1) * 128], in_=src3),
                                 [rpsT], [rqT])
                        attn_chunk(Q, 4, 64, 72, 0.125, lambda Q: list(range(4 * Q + 4)), True,
                                   lambda h, kt: kT[0:72, h, kt * 128:(kt + 1) * 128], lambda kt: rkT[kt],
                                   lambda h, kt: v_aug[:, kt, h, :], lambda kt: rv[kt],
                                   lambda h, qT3=qT3: qT3[0:72, h, :], rqT, sz, rsz)
                        for r in range(4):
                            outproj_tile(4 * Q + r, r, last, obanks=([(psO[0], rpsO[0]), (psO[1], rpsO[1])] if OPB else None))
                elif kind == "M":
                    w1 = nxt("w", 3); w2 = nxt("w", 3)
                    wkvv = wkv_d[l].rearrange("(c p) n -> p c n", p=128)
                    load_w(wb[w1][:, :, 0:256], rwb[w1][0], wkvv[:, :, 256 * g:256 * g + 256])
                    load_w(wb[w1][:, :, 256:512], rwb[w1][1], wkvv[:, :, 512 + 256 * g:512 + 256 * g + 256])
                    load_w(wb[w2][:, :, 0:256], rwb[w2][0], w_in_cols(l, 3072 + 256 * g, 256))
                    load_w(wb[w2][:, :, 256:512], rwb[w2][1], w_in_cols(l, 4608 + 256 * g, 256))
                    load_w(wo[:], rwo, wout_d[l, 1024 + 256 * g:1024 + 256 * g + 256, :].rearrange("(c p) n -> p c n", p=128))
                    if g == 0 or ("M0" not in groups):
                        barrier()
                        DMA("sync", lambda e, l=l: e.dma_start(out=g_bcv, in_=mng_d[l:l + 1, :].partition_broadcast(128)),
                            w=[rg_bc])
                        for mt in range(2):
                            DMA("sync", lambda e, mt=mt: e.dma_start(out=stage, in_=mem_d[mt * 128:(mt + 1) * 128, :]),
                                w=[rstage])
                            rms_to_T(stage, rstage, g_bcv, rg_bc, memT, rmemT, mt * 128, ss16, r_ss16, t16, r_t16,
                                     rstd16, r_rstd16, mt)
                    for mt in range(2):
                        (tA, rtA), (tB, rtB), (FO, rFO), (tR, rtR) = SC[mt % 2]
                        pa, rpa = proj(mt * 128, rmemT, wb[w1], rwb[w1], lhsT_src=memT)
                        A(lambda e, mt=mt, pa=pa: e.copy(out=vm_aug[:, mt, :, 0:128],
                                                         in_=pa[:, 256:512].rearrange("p (h d) -> p h d", h=2)),
                          [rpa], [rvm])
                        headnorm(pa[:, 0:256], rpa, 2, 128, gmk_bc[:], FO, rFO, tA, rtA, tB, rtB)
                        G(lambda e, mt=mt, FO=FO: e.tensor_copy(out=Bp[mt % 2][:], in_=FO[:]), [rFO], [rB[mt % 2]])
                        for hh in range(2):
                            T(lambda e, hh=hh, mt=mt: e.transpose(out=psT[:, hh * 128:(hh + 1) * 128],
                                                                  in_=Bp[mt % 2][:, hh * 128:(hh + 1) * 128],
                                                                  identity=idb[:]), [rB[mt % 2], ridb], [rpsT])
                        src3 = psT[:, 0:256].rearrange("p (h t) -> p h t", h=2)
                        evac(lambda e, mt=mt, src3=src3: e.tensor_copy(out=kmT[:, :, mt * 128:(mt + 1) * 128], in_=src3),
                             lambda e, mt=mt, src3=src3: e.copy(out=kmT[:, :, mt * 128:(mt + 1) * 128], in_=src3),
                             [rpsT], [rkmT])
                    for Q in range(4):
                        qm3 = qTs[Q % 2][:, 0:1024].rearrange("p (h t) -> p h t", h=2)
                        rqT = rqTs[Q % 2]
                        sz = szs[Q % 2]; rsz = rszs[Q % 2]
                        for r in range(4):
                            i = 4 * Q + r
                            (tA, rtA), (tB, rtB), (FO, rFO), (tR, rtR) = SC[i % 2]
                            pa, rpa = proj(i * 128, rhT[i], wb[w2], rwb[w2], wide=(3 if WIDEM else False))
                            fin = silu_ps(sz[:, r, :], rsz[r], pa[:, 256:512], rpa, defer=True)
                            headnorm(pa[:, 0:256], rpa, 2, 128, gmq_bc[:], FO, rFO, tA, rtA, tB, rtB)
                            fin()
                            G(lambda e, i=i, FO=FO: e.tensor_copy(out=Bp[i % 2][:], in_=FO[:]), [rFO], [rB[i % 2]])
                            for hh in range(2):
                                T(lambda e, hh=hh, i=i: e.transpose(out=psT[:, hh * 128:(hh + 1) * 128],
                                                                    in_=Bp[i % 2][:, hh * 128:(hh + 1) * 128], identity=idb[:]),
                                  [rB[i % 2], ridb], [rpsT])
                            src3 = psT[:, 0:256].rearrange("p (h t) -> p h t", h=2)
                            evac(lambda e, r=r, src3=src3, qm3=qm3: e.tensor_copy(out=qm3[:, :, r * 128:(r + 1) * 128], in_=src3),
                                 lambda e, r=r, src3=src3, qm3=qm3: e.copy(out=qm3[:, :, r * 128:(r + 1) * 128], in_=src3),
                                 [rpsT], [rqT])
                        attn_chunk(Q, 2, 128, 128, float(128 ** -0.5), lambda Q: [0, 1], False,
                                   lambda h, kt: kmT[:, h, kt * 128:(kt + 1) * 128], lambda kt: rkmT,
                                   lambda h, kt: vm_aug[:, kt, h, :], lambda kt: rvm,
                                   lambda h, qm3=qm3: qm3[:, h, :], rqT, sz, rsz, nsb=(2 if WIDEM else None))
                        for r in range(4):
                            outproj_tile(4 * Q + r, r, last, obanks=([(psO[0], rpsO[0]), (psO[1], rpsO[1])] if OPB else None))
                else:
                    w1 = nxt("w", 3); w2 = nxt("w", 3)
                    load_w(wb[w1][:, :, 0:256], rwb[w1][0], w_in_cols(l, 1536 + 256 * g, 256))
                    load_w(wb[w1][:, :, 256:512], rwb[w1][1], w_in_cols(l, 2048 + 256 * g, 256))
                    load_w(wb[w2][:, :, 0:256], rwb[w2][0], w_in_cols(l, 2560 + 256 * g, 256))
                    load_w(wb[w2][:, :, 256:512], rwb[w2][1], w_in_cols(l, 4096 + 256 * g, 256))
                    load_w(wo[:], rwo, wout_d[l, 512 + 256 * g:512 + 256 * g + 256, :].rearrange("(c p) n -> p c n", p=128))
                    barrier()
                    if l != 0:
                        DMA("sync", lambda e, g=g: e.dma_start(out=lb_g, in_=lbl_d[1:2, 256 * g:256 * g + 256].partition_broadcast(128)), w=[rlb])
                        DMA("sync", lambda e, g=g: e.dma_start(out=oml_g, in_=lbl_d[0:1, 256 * g:256 * g + 256].partition_broadcast(128)), w=[rlb])
                        V(lambda e: e.tensor_tensor(out=lb_g, in0=lb_g, in1=oml_g, op=ALU.subtract), [rlb], [rlb])
                        A(lambda e: e.activation(out=lb_g, in_=lb_g, func=AF.Exp, scale=-1.0), [rlb], [rlb])
                        A(lambda e: e.activation(out=lb_g, in_=lb_g, func=AF.Ln, bias=1.0), [rlb], [rlb])
                        A(lambda e: e.activation(out=lb_g, in_=lb_g, func=AF.Exp, scale=-1.0), [rlb], [rlb])
                        V(lambda e: e.tensor_scalar(out=oml_g, in0=lb_g, scalar1=-1.0, scalar2=1.0, op0=ALU.mult, op1=ALU.add),
                          [rlb], [rlb])
                    for hh in range(2):
                        G(lambda e, hh=hh: e.memset(S32[:, hh, :], 0.0), w=[rS32[hh]])
                        G(lambda e, hh=hh: e.memset(Sbf[:, 0, hh, :], 0.0), w=[rSbf[0][hh]])
                    Tri32 = cst[:, 256:384]
                    TriE32 = cst[:, 384:512]
                    for i in range(NT):
                        Fs, rFs, Bs, rBs = HSETS[i % 2]
                        sl = i % 4
                        sz = szs[(i // 4) % 2]; rsz = rszs[(i // 4) % 2]
                        pq, rpq = proj(i * 128, rhT[i], wb[w1], rwb[w1])
                        finq = silu_ps(Fs[0], rFs[0], pq[:, 0:256], rpq, defer=True)
                        A(lambda e, pq=pq, Fs=Fs: e.activation(out=Fs[1], in_=pq[:, 256:512], func=AF.Exp, scale=-1.0), [rpq], [rFs[1]])
                        A(lambda e, Fs=Fs: e.activation(out=Fs[1], in_=Fs[1], func=AF.Ln, bias=1.0), [rFs[1]], [rFs[1]])
                        finq()
                        pi_, rpi = proj(i * 128, rhT[i], wb[w2], rwb[w2])
                        silu_ps(sz[:, sl, :], rsz[sl], pi_[:, 256:512], rpi)
                        V(lambda e, pi_=pi_, Bs=Bs: e.tensor_copy(out=Bs[0], in_=pi_[:, 0:256]), [rpi], [rBs[0]])
                        if l == 0:
                            A(lambda e, Fs=Fs: e.activation(out=Fs[2], in_=Fs[1], func=AF.Copy, scale=-1.0), [rFs[1]], [rFs[2]])
                            A(lambda e, Fs=Fs: e.activation(out=Fs[1], in_=Fs[1], func=AF.Exp, scale=-1.0), [rFs[1]], [rFs[1]])
                        else:
                            A(lambda e, Fs=Fs: e.activation(out=Fs[1], in_=Fs[1], func=AF.Exp, scale=-1.0), [rFs[1]], [rFs[1]])
                            V(lambda e, g=g, Fs=Fs: e.tensor_tensor(out=Fs[1], in0=Fs[1], in1=oml_g,
                                                                    op=ALU.mult), [rFs[1], rlb], [rFs[1]])
                            V(lambda e, g=g, Fs=Fs: e.tensor_tensor(out=Fs[1], in0=Fs[1], in1=lb_g,
                                                                    op=ALU.add), [rFs[1], rlb], [rFs[1]])
                            A(lambda e, Fs=Fs: e.activation(out=Fs[2], in_=Fs[1], func=AF.Ln), [rFs[1]], [rFs[2]])
                        (G if GOFF else V)(lambda e, Fs=Fs: e.tensor_scalar(out=Fs[3], in0=Fs[1], scalar1=-1.0, scalar2=1.0, op0=ALU.mult,
                                                                            op1=ALU.add), [rFs[1]], [rFs[3]])
                        T(lambda e, Fs=Fs: e.matmul(psG[:, 0:256], lhsT=Tri32, rhs=Fs[2], start=True, stop=True),
                          [rcst, rFs[2]], [rpsG])
                        T(lambda e, Fs=Fs: e.matmul(psG[:, 256:512], lhsT=TriE32, rhs=Fs[2], start=True, stop=True),
                          [rcst, rFs[2]], [rpsG])
                        for hh in range(2):
                            T(lambda e, hh=hh, Fs=Fs: e.matmul(psS[1][:, 256 + 2 * hh:256 + 2 * hh + 2],
                                                               lhsT=Fs[2][:, hh * 128:(hh + 1) * 128],
                                                               rhs=cst[:, 512:514], start=True, stop=True), [rFs[2], rcst], [rpsS[1]])
                        dsl = dec8[:, 4 * (i % 2):4 * (i % 2) + 4]
                        rds = r_dec8[i % 2]
                        A(lambda e, dsl=dsl: e.activation(out=dsl, in_=psS[1][:, 256:260], func=AF.Exp), [rpsS[1]], [rds])
                        A(lambda e, Fs=Fs: e.activation(out=Fs[4], in_=psG[:, 0:256], func=AF.Exp), [rpsG], [rFs[4]])
                        A(lambda e, Fs=Fs: e.activation(out=Fs[5], in_=psG[:, 0:256], func=AF.Exp, scale=-1.0), [rpsG], [rFs[5]])
                        A(lambda e, Fs=Fs: e.activation(out=Fs[6], in_=psG[:, 256:512], func=AF.Exp), [rpsG], [rFs[6]])
                        V(lambda e, Fs=Fs, Bs=Bs: e.tensor_tensor(out=Bs[1], in0=Fs[0], in1=Fs[4], op=ALU.mult), [rFs[0], rFs[4]], [rBs[1]])
                        G(lambda e, Fs=Fs, Bs=Bs: e.tensor_tensor(out=Bs[2], in0=Fs[3], in1=Fs[5], op=ALU.mult), [rFs[3], rFs[5]], [rBs[2]])
                        G(lambda e, Fs=Fs, Bs=Bs: e.tensor_tensor(out=Bs[3], in0=Fs[3], in1=Fs[6], op=ALU.mult), [rFs[3], rFs[6]], [rBs[3]])
                        for hh in range(2):
                            T(lambda e, hh=hh, Bs=Bs: e.transpose(out=psT[:, hh * 128:(hh + 1) * 128], in_=Bs[1][:, hh * 128:(hh + 1) * 128],
                                                                  identity=idb[:]), [rBs[1], ridb], [rpsT])
                            T(lambda e, hh=hh, Bs=Bs: e.transpose(out=psT[:, 256 + hh * 128:256 + (hh + 1) * 128],
                                                                  in_=Bs[2][:, hh * 128:(hh + 1) * 128], identity=idb[:]),
                              [rBs[2], ridb], [rpsT])
                        pq3 = psT[:, 0:256].rearrange("p (h t) -> p h t", h=2)
                        A(lambda e, Bs=Bs: e.copy(out=Bs[4], in_=psT[:, 0:256]), [rpsT], [rBs[4]])
                        A(lambda e, pq3=pq3: e.copy(out=qTA[:, :, 0:64], in_=pq3[:, :, 0:64]), [rpsT], [rqTA])
                        V(lambda e, Bs=Bs: e.tensor_copy(out=Bs[5], in_=psT[:, 256:512]), [rpsT], [rBs[5]])
                        V(lambda e, pq3=pq3: e.tensor_copy(out=qTB[:, :, 64:128], in_=pq3[:, :, 64:128]), [rpsT], [rqTB])
                        cur = i % 2
                        nxtb = 1 - cur
                        for hh in range(2):
                            hs = slice(hh * 128, (hh + 1) * 128)
                            T(lambda e, hs=hs, Bs=Bs: e.matmul(psS[1][:, hs], lhsT=Bs[5][:, hs], rhs=Bs[4][:, hs], start=True, stop=True),
                              [rBs[5], rBs[4]], [rpsS[1]])
                        for hh in range(2):
                            hs = slice(hh * 128, (hh + 1) * 128)
                            V(lambda e, hs=hs, Bs=Bs: e.tensor_tensor(out=Bs[6][:, hs], in0=psS[1][:, hs], in1=hgmb[:], op=ALU.mult),
                              [rpsS[1], rhgmb], [rBs[6]])
                        for hh in range(2):
                            hs = slice(hh * 128, (hh + 1) * 128)
                            T(lambda e, hs=hs, hh=hh, Bs=Bs: e.matmul(psO[hh][:, 0:128], lhsT=Bs[6][:, hs], rhs=Bs[0][:, hs],
                                                                      start=True, stop=False), [rBs[6], rBs[0]], [rpsO[hh]])
                            T(lambda e, hh=hh, cur=cur: e.matmul(psO[hh][:, 0:128], lhsT=qTA[:, hh, :], rhs=Sbf[:, cur, hh, :],
                                                                 start=False, stop=False), [rqTA, rSbf[cur][hh]], [rpsO[hh]])
                        for hh in range(2):
                            hs = slice(hh * 128, (hh + 1) * 128)
                            T(lambda e, hs=hs, Bs=Bs: e.matmul(psS[0][:, hs], lhsT=Bs[3][0:64, hs], rhs=Bs[0][0:64, hs],
                                                               start=True, stop=True), [rBs[3], rBs[0]], [rpsS[0], rrow])
                        for hh in range(2):
                            hs = slice(hh * 128, (hh + 1) * 128)
                            V(lambda e, hs=hs, hh=hh, dsl=dsl: e.scalar_tensor_tensor(out=S32[:, hh, :], in0=S32[:, hh, :],
                                                                                      scalar=dsl[:, 2 * hh:2 * hh + 1], in1=psS[0][:, hs],
                                                                                      op0=ALU.mult, op1=ALU.add),
                              [rS32[hh], rds, rpsS[0]], [rS32[hh]])
                            G(lambda e, hh=hh, nxtb=nxtb: e.tensor_copy(out=Sbf[:, nxtb, hh, :], in_=S32[:, hh, :]),
                              [rS32[hh]], [rSbf[nxtb][hh]])
                        for hh in range(2):
                            T(lambda e, hh=hh, nxtb=nxtb: e.matmul(psO[hh][:, 0:128], lhsT=qTB[:, hh, :], rhs=Sbf[:, nxtb, hh, :],
                                                                   start=False, stop=True), [rqTB, rSbf[nxtb][hh]], [rpsO[hh], rrow])
                        for hh in range(2):
                            hs = slice(hh * 128, (hh + 1) * 128)
                            T(lambda e, hs=hs, Bs=Bs: e.matmul(psS[0][:, hs], lhsT=Bs[3][64:128, hs], rhs=Bs[0][64:128, hs],
                                                               start=True, stop=True), [rBs[3], rBs[0]], [rpsS[0], rrow])
                        for hh in range(2):
                            hs = slice(hh * 128, (hh + 1) * 128)
                            V(lambda e, hs=hs, hh=hh, dsl=dsl: e.scalar_tensor_tensor(out=S32[:, hh, :], in0=S32[:, hh, :],
                                                                                      scalar=dsl[:, 2 * hh + 1:2 * hh + 2], in1=psS[0][:, hs],
                                                                                      op0=ALU.mult, op1=ALU.add),
                              [rS32[hh], rds, rpsS[0]], [rS32[hh]])
                        for hh in range(2):
                            G(lambda e, hh=hh, nxtb=nxtb: e.tensor_copy(out=Sbf[:, nxtb, hh, :], in_=S32[:, hh, :]),
                              [rS32[hh]], [rSbf[nxtb][hh]])
                        ssl = sso4[:, 2 * (i % 2):2 * (i % 2) + 2]
                        r_sso = r_sso2[i % 2]
                        for hh in range(2):
                            A(lambda e, hh=hh, Fs=Fs, ssl=ssl: e.activation(out=Fs[7][:, 0:128], in_=psO[hh][:, 0:128], func=AF.Square,
                                                                            accum_out=ssl[:, hh:hh + 1]), [rpsO[hh]], [rFs[7], r_sso])
                        A(lambda e, ssl=ssl: e.activation(out=ssl, in_=ssl, func=AF.Ln, scale=1.0 / 128, bias=EPS), [r_sso], [r_sso])
                        A(lambda e, ssl=ssl: e.activation(out=ssl, in_=ssl, func=AF.Exp, scale=-0.5), [r_sso], [r_sso])
                        for hh in range(2):
                            hs = slice(hh * 128, (hh + 1) * 128)
                            V(lambda e, hh=hh, hs=hs, Fs=Fs, ssl=ssl: e.scalar_tensor_tensor(out=Fs[8][:, hs], in0=psO[hh][:, 0:128],
                                                                                             scalar=ssl[:, hh:hh + 1], in1=go_bc[:],
                                                                                             op0=ALU.mult, op1=ALU.mult),
                              [rpsO[hh], r_sso, rgains], [rFs[8]])
                        G(lambda e, Fs=Fs, sl=sl, sz=sz: e.tensor_tensor(out=y_tok[:, sl, :], in0=Fs[8], in1=sz[:, sl, :], op=ALU.mult),
                          [rFs[8], rsz[sl]], [ry[sl]])
                        outproj_tile(i, sl, last, obanks=[(psS[0], rpsS[0]), (psS[0], rpsS[0])])
            if not glist and last_layer:
                for i in range(NT):
                    out_dmas.append(DMA("sync", lambda e, i=i: e.dma_start(out=out_d[i * 128:(i + 1) * 128, :], in_=x_tok[:, i, :]),
                                        r=[rx[i]]))
        if SCHED:
            if SCHED2:
                P.schedule2(SDELTA)
            else:
                P.schedule()
        P.emit(st, out_dmas)
    build_nc.stats = P.stats
    return nc


_CACHE = {}


def _get_nc(layers, groups):
    key = (tuple(layers), tuple(groups))
    if key not in _CACHE:
        _CACHE[key] = build_nc(layers, groups)
    return _CACHE[key]


def run(inputs, layers=(0, 1), groups=ALL_GROUPS, cores=8):
    nc = _get_nc(layers, groups)
    f = lambda a: np.ascontiguousarray(np.asarray(a))
    cst = make_consts()
    shared = {k: f(inputs[k]).astype(np.float32, copy=False) for k in
              ("norm_g", "w_in", "w_out", "moba_q_norm", "moba_k_norm", "hgrn_lb_logits", "hgrn_o_norm",
               "mem_norm_g", "w_mem_kv", "mem_q_norm", "mem_k_norm")}
    x = f(inputs["x"]); mem = f(inputs["mem"]); pos = f(inputs["positions"]).astype(np.int32, copy=False)
    in_maps = []
    for b in range(cores):
        m = dict(shared)
        m["x"] = x[b]
        m["mem"] = mem[b]
        m["pos"] = pos[b].reshape(16, 128)
        m["cst"] = cst
        in_maps.append(m)
    res = run_bass_kernel_spmd(nc, in_maps, core_ids=list(range(cores)))
    return np.stack([np.asarray(r["out"]) for r in res.results], axis=0)


def kernel(x, mem, positions, norm_g, w_in, w_out, moba_q_norm, moba_k_norm, hgrn_lb_logits,
           hgrn_o_norm, mem_norm_g, w_mem_kv, mem_q_norm, mem_k_norm):
    inputs = dict(x=x, mem=mem, positions=positions, norm_g=norm_g, w_in=w_in, w_out=w_out,
                  moba_q_norm=moba_q_norm, moba_k_norm=moba_k_norm, hgrn_lb_logits=hgrn_lb_logits,
                  hgrn_o_norm=hgrn_o_norm, mem_norm_g=mem_norm_g, w_mem_kv=w_mem_kv,
                  mem_q_norm=mem_q_norm, mem_k_norm=mem_k_norm)
    return run(inputs).astype(np.float32, copy=False)
```

```python
import numpy as np
from contextlib import ExitStack
import concourse.bass as bass
import concourse.mybir as mybir
from concourse.bass_utils import run_bass_kernel_spmd

F32 = mybir.dt.float32
BF16 = mybir.dt.bfloat16
I32 = mybir.dt.int32
AF = mybir.ActivationFunctionType
ALU = mybir.AluOpType
AX = mybir.AxisListType

S = 2048
D = 1024
NT = 16
EPS = 1e-6
NCST = 576
import os as _os0
ALL_GROUPS = tuple(_os0.environ.get("ORDER", "A0,A1,H0,H1,M0,M1").split(","))
import os as _os
SCHED = _os.environ.get("SCHED", "1") == "1"
PAR1 = int(_os.environ.get("PAR1", "1"))
PAR2 = int(_os.environ.get("PAR2", "1"))
GPAR = int(_os.environ.get("GPAR", "1"))
PS3 = int(_os.environ.get("PS3", "3"))
MASKV = int(_os.environ.get("MASKV", "1"))
OPB = int(_os.environ.get("OPB", "1"))
SCHED2 = int(_os.environ.get("SCHED2", "1"))
SDELTA = float(_os.environ.get("SDELTA", "100"))
LATX = float(_os.environ.get("LATX", "180"))
PEK = float(_os.environ.get("PEK", "0.65"))
ACTK = float(_os.environ.get("ACTK", "1.0"))
DVEK = float(_os.environ.get("DVEK", "1.0"))
POOLK = float(_os.environ.get("POOLK", "1.0"))
LATS = float(_os.environ.get("LATS", "60"))
XQ = int(_os.environ.get("XQ", "0"))
GOFF = int(_os.environ.get("GOFF", "0"))
TRANS = int(_os.environ.get("TRANS", "1"))
PRUNE = int(_os.environ.get("PRUNE", "1"))
WIDE = int(_os.environ.get("WIDE", "1"))
WIDEM = int(_os.environ.get("WIDEM", "1"))


class Res:
    __slots__ = ("name", "w", "r", "excl")

    def __init__(self, name, excl=False):
        self.name = name
        self.w = None
        self.r = []
        self.excl = excl


class _Rec:
    def __init__(self):
        self.name = None
        self.args = ()
        self.kw = {}

    def __getattr__(self, name):
        def f(*a, **k):
            self.name, self.args, self.kw = name, a, k
            return self
        return f


def _free_size(ap):
    n = 1
    for d in list(ap.shape)[1:]:
        n *= int(d)
    return n


class Op:
    __slots__ = ("eng", "fn", "deps", "sdeps", "sig", "idx", "dma", "sem", "val", "i", "cost", "start")

    def __init__(self, eng, fn, deps, sdeps, dma):
        self.eng = eng
        self.fn = fn
        self.deps = deps
        self.sdeps = sdeps
        self.sig = False
        self.idx = 0
        self.dma = dma
        self.sem = None
        self.val = 0
        self.i = 0
        self.start = 0.0
        rec = _Rec()
        fn(rec)
        out = rec.kw.get("out", rec.args[0] if rec.args else None)
        n = _free_size(out) if out is not None else 64
        if dma:
            c = 2000.0 + n * int(out.shape[0]) * 4 / 120.0
        elif eng == "tensor":
            if rec.name == "transpose":
                c = 110.0
            else:
                lhsT = rec.kw.get("lhsT")
                f32 = lhsT is not None and lhsT.dtype == F32
                c = PEK * (64.0 + max(n, 64) / 2.0) * (4.0 if f32 else 1.0)
        elif eng == "scalar":
            c = ACTK * (200.0 + n / 1.2)
        elif eng == "vector":
            c = DVEK * (120.0 + n / 0.96 * (8.0 if rec.name == "reciprocal" else 1.0))
        else:
            c = POOLK * (300.0 + n / 0.5)
        self.cost = c


class Prog:
    ENGS = ["tensor", "vector", "scalar", "gpsimd", "sync"]

    def __init__(self, nc):
        self.nc = nc
        self.ops = []

    phase = None
    tok = None
    tokset = ()

    def op(self, eng, fn, reads=(), writes=(), dma=False):
        if self.phase == "gate" and self.tok is not None:
            writes = list(writes) + [self.tok]
        elif self.phase == "chain" and eng in self.tokset:
            reads = list(reads) + [self.tok]
        deps, sdeps = {}, {}

        def add(d):
            if d.dma or dma or d.eng != eng or eng != "tensor":
                deps[id(d)] = d
            else:
                sdeps[id(d)] = d
        for r in reads:
            if r.w is not None:
                add(r.w)
            if r.excl:
                for d in r.r:
                    if d.eng != eng:
                        add(d)
        for w in writes:
            if w.w is not None:
                add(w.w)
            for d in w.r:
                add(d)
        o = Op(eng, fn, list(deps.values()), list(sdeps.values()), dma)
        for r in reads:
            r.r.append(o)
        for w in writes:
            w.w = o
            w.r = []
        self.ops.append(o)
        return o

    def schedule(self):
        import heapq
        ops = self.ops
        for i, o in enumerate(ops):
            o.i = i
        succs = [[] for _ in ops]
        npred = [0] * len(ops)
        for o in ops:
            ds = o.deps + o.sdeps
            npred[o.i] = len(ds)
            for d in ds:
                succs[d.i].append(o)
        ready = [0.0] * len(ops)
        free = {e: 0.0 for e in self.ENGS}
        heap = [(0.0, o.i) for o in ops if npred[o.i] == 0]
        heapq.heapify(heap)
        done = 0
        while heap:
            t, i = heapq.heappop(heap)
            o = ops[i]
            st = max(ready[i], free[o.eng])
            if st > t + 1e-9:
                heapq.heappush(heap, (st, i))
                continue
            o.start = st
            if o.dma:
                free[o.eng] = st + 150.0
            else:
                free[o.eng] = st + o.cost
            fin = st + o.cost
            done += 1
            for sc in succs[i]:
                lat = 60.0 if (sc.eng == o.eng and not o.dma) else 180.0
                if fin + lat > ready[sc.i]:
                    ready[sc.i] = fin + lat
                npred[sc.i] -= 1
                if npred[sc.i] == 0:
                    heapq.heappush(heap, (max(ready[sc.i], free[sc.eng]), sc.i))
        assert done == len(ops), (done, len(ops))
        self.ops = sorted(ops, key=lambda o: (o.start, o.i))
        self.est_ns = max(o.start + o.cost for o in ops)

    def schedule2(self, delta=120.0):
        ops = self.ops
        n = len(ops)
        for i, o in enumerate(ops):
            o.i = i
        succs = [[] for _ in ops]
        npred = [0] * n
        for o in ops:
            ds = o.deps + o.sdeps
            npred[o.i] = len(ds)
            for d in ds:
                succs[d.i].append(o)
        blev = [0.0] * n
        for o in reversed(ops):
            b = 0.0
            for sc in succs[o.i]:
                lat = LATS if (sc.eng == o.eng and not o.dma) else LATX
                v = lat + blev[sc.i]
                if v > b:
                    b = v
            blev[o.i] = b + o.cost
        ready = [0.0] * n
        free = {e: 0.0 for e in self.ENGS}
        rsets = {e: [] for e in self.ENGS}
        for o in ops:
            if npred[o.i] == 0:
                rsets[o.eng].append(o.i)
        done = 0
        while done < n:
            best_e, best_t = None, 1e30
            for e in self.ENGS:
                rs = rsets[e]
                if not rs:
                    continue
                t = min(ready[i] for i in rs)
                if t < free[e]:
                    t = free[e]
                if t < best_t:
                    best_t, best_e = t, e
            e = best_e
            rs = rsets[e]
            lim = best_t + delta
            pick, pb = -1, -1.0
            for i in rs:
                if ready[i] <= lim and blev[i] > pb:
                    pb, pick = blev[i], i
            rs.remove(pick)
            o = ops[pick]
            st = max(ready[pick], free[e])
            o.start = st
            free[e] = st + (150.0 if o.dma else o.cost)
            fin = st + o.cost
            done += 1
            for sc in succs[pick]:
                lat = LATS if (sc.eng == o.eng and not o.dma) else LATX
                if fin + lat > ready[sc.i]:
                    ready[sc.i] = fin + lat
                npred[sc.i] -= 1
                if npred[sc.i] == 0:
                    rsets[sc.eng].append(sc.i)
        self.ops = sorted(ops, key=lambda o: (o.start, o.i))
        self.est_ns = max(o.start + o.cost for o in ops)

    def emit(self, stack, final_deps, ndma_sems=8):
        nc = self.nc
        if PRUNE:
            pos = {id(o): k for k, o in enumerate(self.ops)}
            for o in self.ops:
                best = {}
                keep = []
                for d in o.deps:
                    if d.dma:
                        keep.append(d)
                    elif d.eng not in best or pos[id(d)] > pos[id(best[d.eng])]:
                        best[d.eng] = d
                o.deps = keep + list(best.values())
        for o in self.ops:
            for d in o.deps:
                d.sig = True
        for d in final_deps:
            d.sig = True
        sems = {e: stack.enter_context(nc.semaphore("s_" + e)) for e in self.ENGS}
        cnt = {e: 0 for e in self.ENGS}
        pools, pool_i, pre_wait = {}, {}, {}
        for o in self.ops:
            if o.dma:
                if o.eng not in pools:
                    pools[o.eng] = [[stack.enter_context(nc.semaphore("d_%s_%d" % (o.eng, i))), 0]
                                    for i in range(ndma_sems)]
                    pool_i[o.eng] = 0
                p = pools[o.eng][pool_i[o.eng] % ndma_sems]
                pool_i[o.eng] += 1
                if p[1] > 0:
                    pre_wait[id(o)] = (p[0], p[1])
                p[1] += 16
                o.sem = p[0]
                o.val = p[1]
            elif o.sig:
                cnt[o.eng] += 1
                o.idx = cnt[o.eng]
        per = {e: [o for o in self.ops if o.eng == e] for e in self.ENGS}
        self.stats = {e: len(per[e]) for e in self.ENGS}
        known = {e: {} for e in self.ENGS}
        kn = {}
        plan = {}
        nw = 0

        def semkey(d):
            return (d.sem, d.val) if d.dma else (sems[d.eng], d.idx)

        for o in self.ops:
            kd = known[o.eng]
            ws = []
            for d in o.deps:
                sm, val = semkey(d)
                if kd.get(id(sm), (None, 0))[1] < val:
                    ws.append((sm, val))
                    kd[id(sm)] = (sm, val)
                if TRANS:
                    for k2, (s2, v2) in kn[id(d)].items():
                        if kd.get(k2, (None, 0))[1] < v2:
                            kd[k2] = (s2, v2)
            if o.dma:
                pw = pre_wait.get(id(o))
                if pw and kd.get(id(pw[0]), (None, 0))[1] < pw[1]:
                    ws.append(pw)
                    kd[id(pw[0])] = pw
            plan[id(o)] = ws
            nw += len(ws)
            if o.dma or o.sig:
                mine = dict(kd)
                sm, val = semkey(o)
                mine[id(sm)] = (sm, val)
                kn[id(o)] = mine
        fin_w = []
        kd = known["sync"]
        for d in final_deps:
            sm, val = semkey(d)
            if kd.get(id(sm), (None, 0))[1] < val:
                fin_w.append((sm, val))
                kd[id(sm)] = (sm, val)
        self.stats["waits"] = nw
        self.stats["sigs"] = {e: sum(1 for o in per[e] if o.sig and not o.dma) for e in self.ENGS}
        block = stack.enter_context(nc.Block())

        def mk(e):
            def body(engobj):
                for o in per[e]:
                    for sm, val in plan[id(o)]:
                        engobj.wait_ge(sm, val)
                    if o.dma:
                        o.fn(engobj).then_inc(o.sem, 16)
                    else:
                        ins = o.fn(engobj)
                        if o.sig:
                            ins.then_inc(sems[e], 1)
                if e == "sync":
                    for sm, val in fin_w:
                        engobj.wait_ge(sm, val)
            return body

        block.tensor(mk("tensor"))
        block.vector(mk("vector"))
        block.scalar(mk("scalar"))
        block.gpsimd(mk("gpsimd"))
        block.sync(mk("sync"))


def make_consts():
    c = np.zeros((128, NCST), np.float32)
    i = np.arange(128)
    c[:, 0:128] = np.eye(128)
    c[:, 128:256] = (i[None, :] >= i[:, None])
    same = (i[:, None] // 64) == (i[None, :] // 64)
    c[:, 256:384] = same & (i[:, None] <= i[None, :])
    c[:, 384:512] = same & (i[:, None] > i[None, :])
    c[:, 512] = i < 64
    c[:, 513] = i >= 64
    c[:, 514] = 1.0
    f64 = 500000.0 ** (-np.arange(8, dtype=np.float64) / 8.0)
    f = f64.astype(np.float32)
    flo = (f64 - f.astype(np.float64)).astype(np.float32)
    c[:, 515:523] = f[None, :]
    c[:, 523:531] = f[None, :]
    c[:, 547:555] = flo[None, :]
    c[:, 555:563] = flo[None, :]
    c[:, 531:539] = 0.0
    c[:, 539:547] = np.pi / 2
    return c


def build_nc(layers=(0, 1), groups=ALL_GROUPS):
    nc = bass.Bass("TRN2", target_bir_lowering=False)

    def din(name, shape, d=F32):
        return nc.dram_tensor(name, shape, d, kind="ExternalInput").ap()

    x_d = din("x", [S, D])
    mem_d = din("mem", [256, D])
    pos_d = din("pos", [16, 128], I32)
    ng_d = din("norm_g", [2, D])
    win_d = din("w_in", [2, D, 5120])
    wout_d = din("w_out", [2, 1536, D])
    gq_d = din("moba_q_norm", [2, 64])
    gk_d = din("moba_k_norm", [2, 64])
    lbl_d = din("hgrn_lb_logits", [2, 512])
    go_d = din("hgrn_o_norm", [2, 128])
    mng_d = din("mem_norm_g", [2, D])
    wkv_d = din("w_mem_kv", [2, D, 1024])
    gmq_d = din("mem_q_norm", [2, 128])
    gmk_d = din("mem_k_norm", [2, 128])
    cst_d = din("cst", [128, NCST])
    out_d = nc.dram_tensor("out", [S, D], F32, kind="ExternalOutput").ap()

    P = Prog(nc)
    if _os.environ.get("TOKR"):
        P.tok = Res("tok")
        P.tokset = tuple(_os.environ["TOKR"].split(","))
    with ExitStack() as st:
        def sb(name, shape, dt=F32):
            return st.enter_context(nc.sbuf_tensor("sb_" + name, shape, dt))

        def ps(name, shape, dt=F32):
            return st.enter_context(nc.psum_tensor("pp_" + name, shape, dt))

        def T(fn, r=(), w=()):
            return P.op("tensor", fn, r, w)

        def V(fn, r=(), w=()):
            return P.op("vector", fn, r, w)

        def A(fn, r=(), w=()):
            return P.op("scalar", fn, r, w)

        def G(fn, r=(), w=()):
            return P.op("gpsimd", fn, r, w)

        def DMA(q, fn, r=(), w=()):
            return P.op(q, fn, r, w, dma=True)

        x_tok = sb("x_tok", [128, NT, D]); rx = [Res("x%d" % i) for i in range(NT)]
        hT = sb("hT", [128, 8, S], BF16); rhT = [Res("hT%d" % i) for i in range(NT)]
        cst = sb("cst", [128, NCST]); rcst = Res("cst")
        idb = sb("idb", [128, 128], BF16); ridb = Res("idb")
        trib = sb("trib", [128, 128], BF16); rtrib = Res("trib")
        hgmb = sb("hgmb", [128, 128], BF16); rhgmb = Res("hgmb")
        gq_bc = sb("gq_bc", [128, 64]); gk_bc = sb("gk_bc", [128, 64]); go_bc = sb("go_bc", [128, 128])
        gmq_bc = sb("gmq_bc", [128, 128]); gmk_bc = sb("gmk_bc", [128, 128]); rgains = Res("gains")
        cs = sb("cs", [128, NT, 16]); sn = sb("sn", [128, NT, 16]); rrope = Res("rope")
        wb = [sb("wb%d" % i, [128, 8, 512], BF16) for i in range(3)]; rwb = [[Res("wb%da" % i), Res("wb%db" % i)] for i in range(3)]
        wo = sb("wo", [128, 2, D], BF16); rwo = Res("wo")
        Fp = [sb("F%d" % i, [128, 256]) for i in range(9)]; rF = [Res("F%d" % i) for i in range(9)]
        Bp = [sb("B%d" % i, [128, 256], BF16) for i in range(7)]; rB = [Res("B%d" % i) for i in range(7)]
        szs = [sb("sz%d" % k, [128, 4, 256]) for k in range(2)]; rszs = [[Res("sz%d_%d" % (k, i)) for i in range(4)] for k in range(2)]
        y_tok = sb("y_tok", [128, 4, 256], BF16); ry = [Res("y%d" % i) for i in range(4)]
        yT = [sb("yT%d" % i, [128, 256], BF16) for i in range(2)]; ryT = [Res("yT%d" % i) for i in range(2)]
        pT = [sb("pT%d" % i, [128, 512], BF16) for i in range(4)]; rpT = [Res("pT%d" % i) for i in range(4)]
        hbs = [sb("hb%d" % i, [128, D], BF16) for i in range(2)]; rhbs = [Res("hb0"), Res("hb1")]
        small = sb("small", [128, 128]); rsm = {}

        def sm(name, a, n):
            rsm[name] = Res("sm_" + name)
            return small[:, a:a + n], rsm[name]
        ss16, r_ss16 = sm("ss16", 0, 16)
        t16, r_t16 = sm("t16", 16, 16)
        rstd16, r_rstd16 = sm("rstd16", 32, 16)
        ss4, r_ss4 = sm("ss4", 48, 4)
        t4, r_t4 = sm("t4", 52, 4)
        rs4, r_rs4 = sm("rs4", 56, 4)
        rden, r_rden = sm("rden", 60, 4)
        rdenB, r_rdenB = sm("rdenB", 108, 4)
        dec4, r_dec4 = sm("dec4", 64, 4)
        sso, r_sso = sm("sso", 68, 2)
        to2, r_to2 = sm("to2", 70, 2)
        rso, r_rso = sm("rso", 72, 2)
        gm = sb("gm", [128, 4, 8]); rgm = Res("gm")
        top8 = sb("top8", [128, 4, 8]); rtop8 = Res("top8")
        selb = sb("selb", [128, 4, 8]); rselb = Res("selb")
        kT = sb("kT", [128, 4, S], BF16); rkT = [Res("kT%d" % i) for i in range(NT)]
        v_flat = sb("v_aug", [128, NT * 4 * 65], BF16); rv = [Res("v%d" % i) for i in range(NT)]
        v_aug = v_flat[:].rearrange("p (a b c) -> p a b c", a=NT, b=4)
        stage = v_flat[:, 0:2048].bitcast(F32); rstage = Res("stage")
        g_bcv = v_flat[:, 2048:4096].bitcast(F32); rg_bc = Res("g_bc")
        qTs = [sb("qT%d" % i, [128, 2048], BF16) for i in range(2)]; rqTs = [Res("qT0"), Res("qT1")]
        lbv = qTs[1][:, 0:2048].bitcast(F32); rlb = rqTs[1]
        lb_g = lbv[:, 0:256]; oml_g = lbv[:, 256:512]
        k_aug = sb("k_aug", [128, 4, 72], BF16); rk_aug = Res("k_aug")
        q_aug = sb("q_aug", [128, 4, 72], BF16); rq_aug = Res("q_aug")
        kmT32 = sb("kmT32", [128, 2, 2, 8]); rkm = Res("kmT32")
        S32 = sb("S32", [128, 2, 128]); rS32 = [Res("S32_0"), Res("S32_1")]
        Sbf = sb("Sbf", [128, 2, 2, 128], BF16); rSbf = [[Res("Sbf00"), Res("Sbf01")], [Res("Sbf10"), Res("Sbf11")]]
        qTA = sb("qTA", [128, 2, 128], BF16); qTB = sb("qTB", [128, 2, 128], BF16)
        rqTA = Res("qTA"); rqTB = Res("qTB")
        memT = sb("memT", [128, 8, 256], BF16); rmemT = Res("memT")
        kmT = sb("kmT", [128, 2, 256], BF16); rkmT = Res("kmT")
        vm_aug = sb("vm_aug", [128, 2, 2, 129], BF16); rvm = Res("vm")
        psA = [ps("psA%d" % i, [128, 512]) for i in range(2)]; rpsA = [Res("psA0", True), Res("psA1", True)]
        psT = ps("psT", [128, 1024], BF16); rpsT = Res("psT", True)
        psG = ps("psG", [128, 512]); rpsG = Res("psG", True)
        psS = [ps("psS%d" % i, [128, 512]) for i in range(2)]; rpsS = [Res("psS0", True), Res("psS1", True)]
        psO = [ps("psO%d" % i, [128, 512]) for i in range(2)]; rpsO = [Res("psO0", True), Res("psO1", True)]

        ctr = {"pa": 0, "w": 0, "ev": 0, "ps": 0, "pt": 0, "yt": 0, "mk": 0, "pw": 0, "pw3": 0, "ps2": 0}

        def nxt(k, n):
            v = ctr[k] % n
            ctr[k] += 1
            return v

        def evac(fn_v, fn_a, r, w):
            if nxt("ev", 2) == 0:
                return A(fn_a, r, w)
            return V(fn_v, r, w)

        DMA("sync", lambda e: e.dma_start(out=cst[:], in_=cst_d), w=[rcst])
        V(lambda e: e.tensor_copy(out=idb[:], in_=cst[:, 0:128]), [rcst], [ridb])
        V(lambda e: e.tensor_copy(out=trib[:], in_=cst[:, 128:256]), [rcst], [rtrib])
        V(lambda e: e.tensor_copy(out=hgmb[:], in_=cst[:, 256:384]), [rcst], [rhgmb])
        ident32 = cst[:, 0:128]
        G(lambda e: e.memset(vm_aug[:, :, :, 128:129], 1.0), w=[rvm])
        G(lambda e: e.memset(qTA[:], 0.0), w=[rqTA])
        G(lambda e: e.memset(qTB[:], 0.0), w=[rqTB])
        nI = sb("nI", [128, 256], I32); rnI = Res("nI")
        posi = nI[0:16, 0:128]; rposi = rnI
        posf = Fp[8][0:16, 0:128]; rposf = rF[8]
        DMA("sync", lambda e: e.dma_start(out=posi, in_=pos_d), w=[rposi])
        V(lambda e: e.tensor_copy(out=posf, in_=posi), [rposi], [rposf])
        T(lambda e: e.matmul(psG[:, 0:16], lhsT=posf, rhs=cst[0:16, 0:16], start=True, stop=True), [rposf, rcst], [rpsG])
        post, r_post = sm("post", 80, 16)
        V(lambda e: e.tensor_copy(out=post, in_=psG[:, 0:16]), [rpsG], [r_post])
        ang = Fp[0][:, 0:256].rearrange("p (i j) -> p i j", i=NT)
        V(lambda e: e.tensor_tensor(out=ang, in0=post.unsqueeze(2).to_broadcast([128, NT, 16]),
                                    in1=cst[:, 515:531].unsqueeze(1).to_broadcast([128, NT, 16]), op=ALU.mult),
          [r_post, rcst], [rF[0]])
        ang_lo = Fp[1][:, 0:256].rearrange("p (i j) -> p i j", i=NT)
        V(lambda e: e.tensor_tensor(out=ang_lo, in0=post.unsqueeze(2).to_broadcast([128, NT, 16]),
                                    in1=cst[:, 547:563].unsqueeze(1).to_broadcast([128, NT, 16]), op=ALU.mult),
          [r_post, rcst], [rF[1]])
        V(lambda e: e.tensor_tensor(out=ang, in0=ang, in1=ang_lo, op=ALU.add), [rF[0], rF[1]], [rF[0]])
        V(lambda e: e.tensor_tensor(out=ang, in0=ang, in1=cst[:, 531:547].unsqueeze(1).to_broadcast([128, NT, 16]),
                                    op=ALU.add), [rF[0], rcst], [rF[0]])
        V(lambda e: e.tensor_scalar(out=Fp[1][:], in0=Fp[0][:], scalar1=float(1.0 / (2 * np.pi)), scalar2=None,
                                    op0=ALU.mult), [rF[0]], [rF[1]])
        V(lambda e: e.tensor_copy(out=nI[:], in_=Fp[1][:]), [rF[1]], [rnI])
        V(lambda e: e.tensor_copy(out=Fp[1][:], in_=nI[:]), [rnI], [rF[1]])
        C1 = 6.28125
        C2 = float(2 * np.pi - 6.28125)
        V(lambda e: e.scalar_tensor_tensor(out=Fp[2][:], in0=Fp[1][:], scalar=-C1, in1=Fp[0][:],
                                           op0=ALU.mult, op1=ALU.add), [rF[1], rF[0]], [rF[2]])
        V(lambda e: e.scalar_tensor_tensor(out=Fp[2][:], in0=Fp[1][:], scalar=-C2, in1=Fp[2][:],
                                           op0=ALU.mult, op1=ALU.add), [rF[1], rF[2]], [rF[2]])
        V(lambda e: e.tensor_scalar(out=Fp[2][:], in0=Fp[2][:], scalar1=float(np.pi), scalar2=float(-np.pi),
                                    op0=ALU.min, op1=ALU.max), [rF[2]], [rF[2]])
        A(lambda e: e.activation(out=Fp[3][:], in_=Fp[2][:], func=AF.Sin), [rF[2]], [rF[3]])
        sc = Fp[3][:, 0:256].rearrange("p (i j) -> p i j", i=NT)
        V(lambda e: e.tensor_copy(out=cs[:, :, 0:8], in_=sc[:, :, 8:16]), [rF[3]], [rrope])
        V(lambda e: e.tensor_copy(out=cs[:, :, 8:16], in_=sc[:, :, 8:16]), [rF[3]], [rrope])
        V(lambda e: e.tensor_scalar(out=sn[:, :, 0:8], in0=sc[:, :, 0:8], scalar1=-1.0, scalar2=None, op0=ALU.mult),
          [rF[3]], [rrope])
        V(lambda e: e.tensor_copy(out=sn[:, :, 8:16], in_=sc[:, :, 0:8]), [rF[3]], [rrope])
        def load_w(dst, rdst, src_ap):
            return DMA("gpsimd", lambda e: e.dma_start(out=dst, in_=src_ap), w=[rdst])

        def w_in_cols(l, c0, n):
            return win_d[l].rearrange("(c p) n -> p c n", p=128)[:, :, c0:c0 + n]

        psGb = psG[:].bitcast(BF16)

        def rms_to_T(src_tile, rsrc, gain, rgain, dstT, rdst, col0, ssc, r_ssc, tsc, r_tsc, rsc, r_rsc, k):
            hb = hbs[k % 2]; rhb = rhbs[k % 2]
            pst, rpst = (psT, rpsT) if k % 2 == 0 else (psGb, rpsG)
            A(lambda e: e.activation(out=hb[:], in_=src_tile, func=AF.Square, accum_out=ssc[:, k:k + 1]),
              [rsrc], [rhb, r_ssc])
            A(lambda e: e.activation(out=tsc[:, k:k + 1], in_=ssc[:, k:k + 1], func=AF.Ln, scale=1.0 / D, bias=EPS),
              [r_ssc], [r_tsc])
            A(lambda e: e.activation(out=rsc[:, k:k + 1], in_=tsc[:, k:k + 1], func=AF.Exp, scale=-0.5),
              [r_tsc], [r_rsc])
            V(lambda e: e.scalar_tensor_tensor(out=hb[:], in0=src_tile, scalar=rsc[:, k:k + 1], in1=gain,
                                               op0=ALU.mult, op1=ALU.mult), [rsrc, r_rsc, rgain], [rhb])
            for c in range(8):
                T(lambda e, c=c: e.transpose(out=pst[:, c * 128:(c + 1) * 128], in_=hb[:, c * 128:(c + 1) * 128],
                                             identity=idb[:]), [rhb, ridb], [rpst])
            src3 = pst[:, 0:1024].rearrange("p (c t) -> p c t", c=8)
            evac(lambda e: e.tensor_copy(out=dstT[:, :, col0:col0 + 128], in_=src3),
                 lambda e: e.copy(out=dstT[:, :, col0:col0 + 128], in_=src3), [rpst], [rdst])

        def proj(lhs_cols, rlhs, wt, rwt, ncols=512, lhsT_src=None, wide=False):
            if wide == 3:
                pt, rpt = ((psA[0], rpsA[0]), (psA[1], rpsA[1]), (psG, rpsG))[nxt("pw3", 3)]
            elif wide:
                pt, rpt = ((psA[0], rpsA[0]), (psA[1], rpsA[1]), (psO[0], rpsO[0]), (psO[1], rpsO[1]))[nxt("pw", 4)]
            else:
                b = nxt("pa", 2)
                pt, rpt = psA[b], rpsA[b]
            src = hT if lhsT_src is None else lhsT_src
            for c in range(8):
                T(lambda e, c=c: e.matmul(pt[:, 0:ncols], lhsT=src[:, c, lhs_cols:lhs_cols + 128],
                                          rhs=wt[:, c, 0:ncols], start=(c == 0), stop=(c == 7)),
                  [rlhs] + list(rwt), [rpt])
            return pt, rpt

        def headnorm(src_ps, rps, H, Dh, gain, outF, routF, tmpA, rtmpA, tmpB, rtmpB):
            n = H * Dh
            A(lambda e: e.activation(out=tmpA[:, 0:n], in_=src_ps, func=AF.Square), [rps], [rtmpA])
            V(lambda e: e.tensor_reduce(out=ss4[:, 0:H], in_=tmpA[:, 0:n].rearrange("p (h d) -> p h d", h=H),
                                        axis=AX.X, op=ALU.add), [rtmpA], [r_ss4])
            A(lambda e: e.activation(out=t4[:, 0:H], in_=ss4[:, 0:H], func=AF.Ln, scale=1.0 / Dh, bias=EPS),
              [r_ss4], [r_t4])
            A(lambda e: e.activation(out=rs4[:, 0:H], in_=t4[:, 0:H], func=AF.Exp, scale=-0.5), [r_t4], [r_rs4])
            V(lambda e: e.tensor_tensor(out=tmpB[:, 0:n].rearrange("p (h d) -> p h d", h=H),
                                        in0=src_ps.rearrange("p (h d) -> p h d", h=H),
                                        in1=rs4[:, 0:H].unsqueeze(2).to_broadcast([128, H, Dh]), op=ALU.mult),
              [rps, r_rs4], [rtmpB])
            (G if GOFF else V)(lambda e: e.tensor_tensor(out=outF[:, 0:n].rearrange("p (h d) -> p h d", h=H),
                                                         in0=tmpB[:, 0:n].rearrange("p (h d) -> p h d", h=H),
                                                         in1=gain.unsqueeze(1).to_broadcast([128, H, Dh]), op=ALU.mult),
                               [rtmpB, rgains], [routF])

        def rope(Fx, rFx, i, tR, rtR):
            x3 = Fx[:, 0:256].rearrange("p (h d) -> p h d", h=4)
            a3 = tR[:, 0:64].rearrange("p (h d) -> p h d", h=4)
            b3 = tR[:, 64:128].rearrange("p (h d) -> p h d", h=4)
            rtA = rtR
            rtB = rtR
            G(lambda e: e.tensor_tensor(out=a3, in0=x3[:, :, 0:16], in1=cs[:, i, :].unsqueeze(1).to_broadcast([128, 4, 16]),
                                        op=ALU.mult), [rFx, rrope], [rtA])
            G(lambda e: e.tensor_tensor(out=b3[:, :, 0:8], in0=x3[:, :, 8:16],
                                        in1=sn[:, i, 0:8].unsqueeze(1).to_broadcast([128, 4, 8]), op=ALU.mult),
              [rFx, rrope], [rtB])
            G(lambda e: e.tensor_tensor(out=b3[:, :, 8:16], in0=x3[:, :, 0:8],
                                        in1=sn[:, i, 8:16].unsqueeze(1).to_broadcast([128, 4, 8]), op=ALU.mult),
              [rFx, rrope], [rtB])
            G(lambda e: e.tensor_tensor(out=x3[:, :, 0:16], in0=a3, in1=b3, op=ALU.add), [rtA, rtB], [rFx])

        def silu_ps(dst, rdst, src_ps, rps, defer=False):
            A(lambda e: e.activation(out=dst, in_=src_ps, func=AF.Exp, scale=-1.0), [rps], [rdst])
            A(lambda e: e.activation(out=dst, in_=dst, func=AF.Ln, bias=1.0), [rdst], [rdst])
            A(lambda e: e.activation(out=dst, in_=dst, func=AF.Exp, scale=-1.0), [rdst], [rdst])

            def fin():
                V(lambda e: e.tensor_tensor(out=dst, in0=src_ps, in1=dst, op=ALU.mult), [rps, rdst], [rdst])
            if defer:
                return fin
            fin()

        SC = [((Fp[0], rF[0]), (Fp[1], rF[1]), (Fp[2], rF[2]), (Fp[3], rF[3])),
              ((Fp[4], rF[4]), (Fp[6], rF[6]), (Fp[7], rF[7]), (Fp[8], rF[8]))]
        AUG = [(k_aug, rk_aug), (q_aug, rq_aug)]
        if _os.environ.get("NOPAR", "0") == "1":
            SC[1] = SC[0]
        if _os.environ.get("NOAUG", "0") == "1":
            AUG[1] = AUG[0]

        dec8, _r = sm("dec8", 96, 8)
        r_dec8 = [Res("dec8a"), Res("dec8b")]
        sso4, _r2 = sm("sso4", 104, 4)
        r_sso2 = [Res("ssoA"), Res("ssoB")]
        dummy = sb("dummy", [128, 8])
        kflat = kT[:].rearrange("p h t -> p (h t)")
        kf32 = kflat[:, 0:4608].bitcast(F32)
        Fq = [kf32[:, k * 256:(k + 1) * 256] for k in range(9)]
        rFq = [Res("Fq%d" % k) for k in range(9)]
        Bq = [kflat[:, 4608 + k * 256:4608 + (k + 1) * 256] for k in range(7)]
        rBq = [Res("Bq%d" % k) for k in range(7)]
        HSETS = [([t[:] for t in Fp], rF, [t[:] for t in Bp], rB), (Fq, rFq, Bq, rBq)]

        def barrier():
            G(lambda e: e.memset(dummy[:], 0.0), w=list(rkT) + rFq + rBq + list(rv) + [rstage, rg_bc])

        rrow = Res("pe_rowfence")

        out_dmas = []

        def outproj_tile(i, r, last, obanks=None):
            yb = nxt("yt", 2)
            for pp in range(2):
                T(lambda e, pp=pp: e.transpose(out=psT[:, pp * 128:(pp + 1) * 128], in_=y_tok[:, r, pp * 128:(pp + 1) * 128],
                                               identity=idb[:]), [ry[r], ridb], [rpsT])
            evac(lambda e: e.tensor_copy(out=yT[yb][:], in_=psT[:, 0:256]),
                 lambda e: e.copy(out=yT[yb][:], in_=psT[:, 0:256]), [rpsT], [ryT[yb]])
            for half in range(2):
                if obanks is None:
                    b = nxt("pa", 2)
                    pso, rpso = psA[b], rpsA[b]
                else:
                    pso, rpso = obanks[half]
                for pp in range(2):
                    T(lambda e, pp=pp, half=half, pso=pso: e.matmul(pso[:, 0:512], lhsT=yT[yb][:, pp * 128:(pp + 1) * 128],
                                                                    rhs=wo[:, pp, half * 512:(half + 1) * 512],
                                                                    start=(pp == 0), stop=(pp == 1)),
                      [ryT[yb], rwo], [rpso])
                V(lambda e, half=half, pso=pso: e.tensor_tensor(out=x_tok[:, i, half * 512:(half + 1) * 512], in0=pso[:, 0:512],
                                                                in1=x_tok[:, i, half * 512:(half + 1) * 512], op=ALU.add),
                  [rpso, rx[i]], [rx[i]])
            if last:
                out_dmas.append(DMA("sync", lambda e: e.dma_start(out=out_d[i * 128:(i + 1) * 128, :], in_=x_tok[:, i, :]),
                                    r=[rx[i]]))

        def attn_chunk(Q, H, Dh, KP, scale, key_tiles, causal, kTsrc, rkTsrc, vsrc, rvsrc, qview, rqT, sz, rsz, nsb=None):
            DA = Dh + 1
            for h in range(H):
                if Dh == 64:
                    o_b = h % 2
                    banks = [o_b, o_b, o_b, o_b]
                    offs = [0, DA, 2 * DA, 3 * DA]
                else:
                    banks = [0, 0, 1, 1]
                    offs = [0, DA, 0, DA]
                started = set()
                kts = key_tiles(Q)
                for kt in kts:
                    j = kt - 4 * Q if causal else -1
                    q0 = max(j, 0) * 128
                    N = 512 - q0
                    sbk = nxt("ps", PS3) if nsb is None else nxt("ps2", nsb)
                    pss, rpss = ((psS[0], rpsS[0]), (psS[1], rpsS[1]), (psG, rpsG))[sbk]
                    T(lambda e, kt=kt, h=h, q0=q0, N=N, pss=pss: e.matmul(
                        pss[:, 0:N], lhsT=kTsrc(h, kt), rhs=qview(h)[:, q0:512], start=True, stop=True),
                      [rkTsrc(kt), rqT], [rpss])
                    pb = nxt("pt", 4)
                    A(lambda e, N=N, pss=pss, pb=pb: e.activation(out=pT[pb][:, 0:N], in_=pss[:, 0:N], func=AF.Exp,
                                                                  scale=scale), [rpss], [rpT[pb]])
                    if j >= 0:
                        (V if (MASKV and nxt("mk", 2) == 0) else G)(
                            lambda e, pb=pb: e.tensor_tensor(out=pT[pb][:, 0:128], in0=pT[pb][:, 0:128], in1=trib[:],
                                                             op=ALU.mult), [rpT[pb], rtrib], [rpT[pb]])
                    for r in range(max(j, 0), 4):
                        bk = banks[r]
                        first = bk not in started
                        started.add(bk)
                        T(lambda e, r=r, kt=kt, h=h, q0=q0, pb=pb, bk=bk, first=first: e.matmul(
                            psO[bk][:, offs[r]:offs[r] + DA], lhsT=pT[pb][:, r * 128 - q0:r * 128 - q0 + 128],
                            rhs=vsrc(h, kt), start=first, stop=False, skip_group_check=True),
                          [rpT[pb], rvsrc(kt)], [rpsO[bk]])
                rd, r_rd = (rden, r_rden) if h % 2 == 0 else (rdenB, r_rdenB)
                for bk0 in sorted(set(banks)):
                    rs_ = [r for r in range(4) if banks[r] == bk0]
                    nr = len(rs_)
                    V(lambda e, bk0=bk0, rs_=rs_, nr=nr, rd=rd: e.reciprocal(
                        out=rd[:, rs_[0]:rs_[0] + nr],
                        in_=psO[bk0][:, 0:nr * DA].rearrange("p (r c) -> p r c", r=nr)[:, :, Dh:DA]),
                      [rpsO[bk0]], [r_rd])
                    V(lambda e, bk0=bk0, rs_=rs_, nr=nr, h=h: e.tensor_tensor(
                        out=sz[:, rs_[0]:rs_[0] + nr, h * Dh:(h + 1) * Dh],
                        in0=psO[bk0][:, 0:nr * DA].rearrange("p (r c) -> p r c", r=nr)[:, :, 0:Dh],
                        in1=sz[:, rs_[0]:rs_[0] + nr, h * Dh:(h + 1) * Dh], op=ALU.mult),
                      [rpsO[bk0]] + [rsz[r] for r in rs_], [rsz[r] for r in rs_])
                G(lambda e, h=h, rd=rd: e.tensor_tensor(out=y_tok[:, :, h * Dh:(h + 1) * Dh], in0=sz[:, :, h * Dh:(h + 1) * Dh],
                                                        in1=rd[:, 0:4].unsqueeze(2).to_broadcast([128, 4, Dh]), op=ALU.mult),
                  list(rsz) + [r_rd], list(ry))

        for li, l in enumerate(layers):
            last_layer = (li == len(layers) - 1)
            DMA("sync", lambda e, l=l: e.dma_start(out=g_bcv, in_=ng_d[l:l + 1, :].partition_broadcast(128)), w=[rg_bc])
            for dst, src in ((gq_bc, gq_d), (gk_bc, gk_d), (go_bc, go_d), (gmq_bc, gmq_d), (gmk_bc, gmk_d)):
                DMA("sync", lambda e, l=l, dst=dst, src=src: e.dma_start(out=dst[:], in_=src[l:l + 1, :].partition_broadcast(128)),
                    w=[rgains])
            for i in range(NT):
                if li == 0:
                    DMA(("scalar" if (XQ and i % 2 == 1) else "sync"), lambda e, i=i: e.dma_start(out=x_tok[:, i, :], in_=x_d[i * 128:(i + 1) * 128, :]), w=[rx[i]])
                rms_to_T(x_tok[:, i, :], rx[i], g_bcv, rg_bc, hT, rhT[i], i * 128, ss16, r_ss16, t16, r_t16,
                         rstd16, r_rstd16, i)
            glist = [g for g in ALL_GROUPS if g in groups]
            for gi, gname in enumerate(glist):
                last = last_layer and gi == len(glist) - 1
                kind = gname[0]
                g = int(gname[1])
                if kind == "A":
                    w1 = nxt("w", 3); w2 = nxt("w", 3)
                    load_w(wb[w1][:, :, 0:256], rwb[w1][0], w_in_cols(l, 512 + 256 * g, 256))
                    load_w(wb[w1][:, :, 256:512], rwb[w1][1], w_in_cols(l, 1024 + 256 * g, 256))
                    load_w(wb[w2][:, :, 0:256], rwb[w2][0], w_in_cols(l, 256 * g, 256))
                    load_w(wb[w2][:, :, 256:512], rwb[w2][1], w_in_cols(l, 3584 + 256 * g, 256))
                    load_w(wo[:], rwo, wout_d[l, 256 * g:256 * g + 256, :].rearrange("(c p) n -> p c n", p=128))
                    barrier()
                    G(lambda e: e.memset(v_aug[:, :, :, 64:65], 1.0), w=rv)
                    V(lambda e: e.memset(psG[:, 0:16], 0.0), w=[rpsG])
                    for i in range(NT):
                        n_blk = i // 2
                        (tA, rtA), (tB, rtB), (FO, rFO), (tR, rtR) = SC[(i % 2) * PAR1]
                        ka, rka = AUG[(i % 2) * PAR1]
                        pa, rpa = proj(i * 128, rhT[i], wb[w1], rwb[w1], wide=bool(WIDE))
                        A(lambda e, i=i, pa=pa: e.copy(out=v_aug[:, i, :, 0:64],
                                                       in_=pa[:, 256:512].rearrange("p (h d) -> p h d", h=4)),
                          [rpa], [rv[i]])
                        headnorm(pa[:, 0:256], rpa, 4, 64, gk_bc[:], FO, rFO, tA, rtA, tB, rtB)
                        rope(FO, rFO, i, tR, rtR)
                        G(lambda e, ka=ka, FO=FO: e.tensor_copy(out=ka[:, :, 0:64], in_=FO[:].rearrange("p (h d) -> p h d", h=4)),
                          [rFO], [rka])
                        G(lambda e, ka=ka: e.memset(ka[:, :, 64:72], 0.0), w=[rka])
                        G(lambda e, ka=ka, n_blk=n_blk: e.memset(ka[:, :, 64 + n_blk:65 + n_blk], 1.0), w=[rka])
                        for pp in range(2):
                            T(lambda e, pp=pp, n_blk=n_blk, FO=FO: e.matmul(psG[:, pp * 8 + n_blk:pp * 8 + n_blk + 1],
                                                                            lhsT=FO[:, pp * 128:(pp + 1) * 128], rhs=cst[:, 514:515],
                                                                            start=False, stop=False, skip_group_check=True),
                              [rFO, rcst], [rpsG])
                        for h in range(4):
                            T(lambda e, h=h, ka=ka: e.transpose(out=psT[0:72, h * 128:(h + 1) * 128], in_=ka[:, h, :], identity=idb[:]),
                              [rka, ridb], [rpsT])
                        src3 = psT[0:72, 0:512].rearrange("p (h t) -> p h t", h=4)
                        evac(lambda e, i=i, src3=src3: e.tensor_copy(out=kT[0:72, :, i * 128:(i + 1) * 128], in_=src3),
                             lambda e, i=i, src3=src3: e.copy(out=kT[0:72, :, i * 128:(i + 1) * 128], in_=src3),
                             [rpsT], [rkT[i]])
                    G(lambda e: e.memset(kmT32[:], 0.0), w=[rkm])
                    A(lambda e: e.copy(out=kmT32[0:64, :, 0, :], in_=psG[0:64, 0:16].rearrange("p (a n) -> p a n", a=2)), [rpsG], [rkm])
                    A(lambda e: e.copy(out=kmT32[64:128, :, 1, :], in_=psG[64:128, 0:16].rearrange("p (a n) -> p a n", a=2)), [rpsG], [rkm])
                    for Q in range(4):
                        qT3 = qTs[Q % 2][:, 0:2048].rearrange("p (h t) -> p h t", h=4)
                        rqT = rqTs[Q % 2]
                        sz = szs[Q % 2]; rsz = rszs[Q % 2]
                        for r in range(4):
                            i = 4 * Q + r
                            own = i // 2
                            P.phase = "chain"
                            par = (i % 2) * PAR2 * (1 if (own < 4 or GPAR) else 0)
                            (tA, rtA), (tB, rtB), (FO, rFO), (tR, rtR) = SC[par]
                            qa, rqa = AUG[par]
                            pa, rpa = proj(i * 128, rhT[i], wb[w2], rwb[w2])
                            fin = silu_ps(sz[:, r, :], rsz[r], pa[:, 256:512], rpa, defer=True)
                            headnorm(pa[:, 0:256], rpa, 4, 64, gq_bc[:], FO, rFO, tA, rtA, tB, rtB)
                            fin()
                            rope(FO, rFO, i, tR, rtR)
                            G(lambda e, qa=qa, FO=FO: e.tensor_copy(out=qa[:, :, 0:64], in_=FO[:].rearrange("p (h d) -> p h d", h=4)),
                              [rFO], [rqa])
                            if own >= 4:
                                P.phase = "gate"
                                for pp in range(2):
                                    T(lambda e, pp=pp, FO=FO: e.matmul(psG[:, pp * 128:(pp + 1) * 128],
                                                                       lhsT=FO[:, pp * 128:(pp + 1) * 128], rhs=ident32,
                                                                       start=True, stop=True),
                                      [rFO, rcst], [rpsG])
                                A(lambda e: e.copy(out=Fp[5][:], in_=psG[:, 0:256]), [rpsG], [rF[5]])
                                for h in range(4):
                                    T(lambda e, h=h, own=own: e.matmul(
                                        psG[:, 256 + h * 8:256 + h * 8 + own],
                                        lhsT=Fp[5][:, (h // 2) * 128:(h // 2) * 128 + 128],
                                        rhs=kmT32[:, h // 2, h % 2, 0:own], start=True, stop=True),
                                      [rF[5], rkm], [rpsG])
                                V(lambda e: e.memset(gm[:], -1.0e30), w=[rgm])
                                V(lambda e, own=own: e.tensor_copy(
                                    out=gm[:, :, 0:own], in_=psG[:, 256:288].rearrange("p (h n) -> p h n", h=4)[:, :, 0:own]),
                                  [rpsG], [rgm])
                                for h in range(4):
                                    V(lambda e, h=h: e.max(out=top8[:, h, :], in_=gm[:, h, :]), [rgm], [rtop8])
                                V(lambda e: e.tensor_tensor(out=selb[:], in0=gm[:], in1=top8[:, :, 2:3].to_broadcast([128, 4, 8]),
                                                            op=ALU.is_ge), [rgm, rtop8], [rselb])
                                V(lambda e, qa=qa: e.tensor_scalar(out=qa[:, :, 64:72], in0=selb[:], scalar1=30000.0,
                                                                   scalar2=-30000.0, op0=ALU.mult, op1=ALU.add), [rselb], [rqa])
                                V(lambda e, qa=qa, own=own: e.memset(qa[:, :, 64 + own:65 + own], 0.0), w=[rqa])
                            else:
                                G(lambda e, qa=qa: e.memset(qa[:, :, 64:72], 0.0), w=[rqa])
                            P.phase = None
                            for h in range(4):
                                T(lambda e, h=h, qa=qa: e.transpose(out=psT[0:72, h * 128:(h + 1) * 128], in_=qa[:, h, :],
                                                                    identity=idb[:]), [rqa, ridb], [rpsT])
                            src3 = psT[0:72, 0:512].rearrange("p (h t) -> p h t", h=4)
                            evac(lambda e, r=r, src3=src3, qT3=qT3: e.tensor_copy(out=qT3[0:72, :, r * 128:(r + 1) * 128], in_=src3),
                                 lambda e, r=r, src3=src3, qT3=qT3: e.copy(out=qT3[0:72, :, r * 128:(r + 1) * 128], in_=src3),
                                 [rpsT], [rqT])
                        attn_chunk(Q, 4, 64, 72, 0.125, lambda Q: list(range(4 * Q + 4)), True,
                                   lambda h, kt: kT[0:72, h, kt * 128:(kt + 1) * 128], lambda kt: rkT[kt],
                                   lambda h, kt: v_aug[:, kt, h, :], lambda kt: rv[kt],
                                   lambda h, qT3=qT3: qT3[0:72, h, :], rqT, sz, rsz)
                        for r in range(4):
                            outproj_tile(4 * Q + r, r, last, obanks=([(psO[0], rpsO[0]), (psO[1], rpsO[1])] if OPB else None))
                elif kind == "M":
                    w1 = nxt("w", 3); w2 = nxt("w", 3)
                    wkvv = wkv_d[l].rearrange("(c p) n -> p c n", p=128)
                    load_w(wb[w1][:, :, 0:256], rwb[w1][0], wkvv[:, :, 256 * g:256 * g + 256])
                    load_w(wb[w1][:, :, 256:512], rwb[w1][1], wkvv[:, :, 512 + 256 * g:512 + 256 * g + 256])
                    load_w(wb[w2][:, :, 0:256], rwb[w2][0], w_in_cols(l, 3072 + 256 * g, 256))
                    load_w(wb[w2][:, :, 256:512], rwb[w2][1], w_in_cols(l, 4608 + 256 * g, 256))
                    load_w(wo[:], rwo, wout_d[l, 1024 + 256 * g:1024 + 256 * g + 256, :].rearrange("(c p) n -> p c n", p=128))
                    if g == 0 or ("M0" not in groups):
                        barrier()
                        DMA("sync", lambda e, l=l: e.dma_start(out=g_bcv, in_=mng_d[l:l + 1, :].partition_broadcast(128)),
                            w=[rg_bc])
                        for mt in range(2):
                            DMA("sync", lambda e, mt=mt: e.dma_start(out=stage, in_=mem_d[mt * 128:(mt + 1) * 128, :]),
                                w=[rstage])
                            rms_to_T(stage, rstage, g_bcv, rg_bc, memT, rmemT, mt * 128, ss16, r_ss16, t16, r_t16,
                                     rstd16, r_rstd16, mt)
                    for mt in range(2):
                        (tA, rtA), (tB, rtB), (FO, rFO), (tR, rtR) = SC[mt % 2]
                        pa, rpa = proj(mt * 128, rmemT, wb[w1], rwb[w1], lhsT_src=memT)
                        A(lambda e, mt=mt, pa=pa: e.copy(out=vm_aug[:, mt, :, 0:128],
                                                         in_=pa[:, 256:512].rearrange("p (h d) -> p h d", h=2)),
                          [rpa], [rvm])
                        headnorm(pa[:, 0:256], rpa, 2, 128, gmk_bc[:], FO, rFO, tA, rtA, tB, rtB)
                        G(lambda e, mt=mt, FO=FO: e.tensor_copy(out=Bp[mt % 2][:], in_=FO[:]), [rFO], [rB[mt % 2]])
                        for hh in range(2):
                            T(lambda e, hh=hh, mt=mt: e.transpose(out=psT[:, hh * 128:(hh + 1) * 128],
                                                                  in_=Bp[mt % 2][:, hh * 128:(hh + 1) * 128],
                                                                  identity=idb[:]), [rB[mt % 2], ridb], [rpsT])
                        src3 = psT[:, 0:256].rearrange("p (h t) -> p h t", h=2)
                        evac(lambda e, mt=mt, src3=src3: e.tensor_copy(out=kmT[:, :, mt * 128:(mt + 1) * 128], in_=src3),
                             lambda e, mt=mt, src3=src3: e.copy(out=kmT[:, :, mt * 128:(mt + 1) * 128], in_=src3),
                             [rpsT], [rkmT])
                    for Q in range(4):
                        qm3 = qTs[Q % 2][:, 0:1024].rearrange("p (h t) -> p h t", h=2)
                        rqT = rqTs[Q % 2]
                        sz = szs[Q % 2]; rsz = rszs[Q % 2]
                        for r in range(4):
                            i = 4 * Q + r
                            (tA, rtA), (tB, rtB), (FO, rFO), (tR, rtR) = SC[i % 2]
                            pa, rpa = proj(i * 128, rhT[i], wb[w2], rwb[w2], wide=(3 if WIDEM else False))
                            fin = silu_ps(sz[:, r, :], rsz[r], pa[:, 256:512], rpa, defer=True)
                            headnorm(pa[:, 0:256], rpa, 2, 128, gmq_bc[:], FO, rFO, tA, rtA, tB, rtB)
                            fin()
                            G(lambda e, i=i, FO=FO: e.tensor_copy(out=Bp[i % 2][:], in_=FO[:]), [rFO], [rB[i % 2]])
                            for hh in range(2):
                                T(lambda e, hh=hh, i=i: e.transpose(out=psT[:, hh * 128:(hh + 1) * 128],
                                                                    in_=Bp[i % 2][:, hh * 128:(hh + 1) * 128], identity=idb[:]),
                                  [rB[i % 2], ridb], [rpsT])
                            src3 = psT[:, 0:256].rearrange("p (h t) -> p h t", h=2)
                            evac(lambda e, r=r, src3=src3, qm3=qm3: e.tensor_copy(out=qm3[:, :, r * 128:(r + 1) * 128], in_=src3),
                                 lambda e, r=r, src3=src3, qm3=qm3: e.copy(out=qm3[:, :, r * 128:(r + 1) * 128], in_=src3),
                                 [rpsT], [rqT])
                        attn_chunk(Q, 2, 128, 128, float(128 ** -0.5), lambda Q: [0, 1], False,
                                   lambda h, kt: kmT[:, h, kt * 128:(kt + 1) * 128], lambda kt: rkmT,
                                   lambda h, kt: vm_aug[:, kt, h, :], lambda kt: rvm,
                                   lambda h, qm3=qm3: qm3[:, h, :], rqT, sz, rsz, nsb=(2 if WIDEM else None))
                        for r in range(4):
                            outproj_tile(4 * Q + r, r, last, obanks=([(psO[0], rpsO[0]), (psO[1], rpsO[1])] if OPB else None))
                else:
                    w1 = nxt("w", 3); w2 = nxt("w", 3)
                    load_w(wb[w1][:, :, 0:256], rwb[w1][0], w_in_cols(l, 1536 + 256 * g, 256))
                    load_w(wb[w1][:, :, 256:512], rwb[w1][1], w_in_cols(l, 2048 + 256 * g, 256))
                    load_w(wb[w2][:, :, 0:256], rwb[w2][0], w_in_cols(l, 2560 + 256 * g, 256))
                    load_w(wb[w2][:, :, 256:512], rwb[w2][1], w_in_cols(l, 4096 + 256 * g, 256))
                    load_w(wo[:], rwo, wout_d[l, 512 + 256 * g:512 + 256 * g + 256, :].rearrange("(c p) n -> p c n", p=128))
                    barrier()
                    if l != 0:
                        DMA("sync", lambda e, g=g: e.dma_start(out=lb_g, in_=lbl_d[1:2, 256 * g:256 * g + 256].partition_broadcast(128)), w=[rlb])
                        DMA("sync", lambda e, g=g: e.dma_start(out=oml_g, in_=lbl_d[0:1, 256 * g:256 * g + 256].partition_broadcast(128)), w=[rlb])
                        V(lambda e: e.tensor_tensor(out=lb_g, in0=lb_g, in1=oml_g, op=ALU.subtract), [rlb], [rlb])
                        A(lambda e: e.activation(out=lb_g, in_=lb_g, func=AF.Exp, scale=-1.0), [rlb], [rlb])
                        A(lambda e: e.activation(out=lb_g, in_=lb_g, func=AF.Ln, bias=1.0), [rlb], [rlb])
                        A(lambda e: e.activation(out=lb_g, in_=lb_g, func=AF.Exp, scale=-1.0), [rlb], [rlb])
                        V(lambda e: e.tensor_scalar(out=oml_g, in0=lb_g, scalar1=-1.0, scalar2=1.0, op0=ALU.mult, op1=ALU.add),
                          [rlb], [rlb])
                    for hh in range(2):
                        G(lambda e, hh=hh: e.memset(S32[:, hh, :], 0.0), w=[rS32[hh]])
                        G(lambda e, hh=hh: e.memset(Sbf[:, 0, hh, :], 0.0), w=[rSbf[0][hh]])
                    Tri32 = cst[:, 256:384]
                    TriE32 = cst[:, 384:512]
                    for i in range(NT):
                        Fs, rFs, Bs, rBs = HSETS[i % 2]
                        sl = i % 4
                        sz = szs[(i // 4) % 2]; rsz = rszs[(i // 4) % 2]
                        pq, rpq = proj(i * 128, rhT[i], wb[w1], rwb[w1])
                        finq = silu_ps(Fs[0], rFs[0], pq[:, 0:256], rpq, defer=True)
                        A(lambda e, pq=pq, Fs=Fs: e.activation(out=Fs[1], in_=pq[:, 256:512], func=AF.Exp, scale=-1.0), [rpq], [rFs[1]])
                        A(lambda e, Fs=Fs: e.activation(out=Fs[1], in_=Fs[1], func=AF.Ln, bias=1.0), [rFs[1]], [rFs[1]])
                        finq()
                        pi_, rpi = proj(i * 128, rhT[i], wb[w2], rwb[w2])
                        silu_ps(sz[:, sl, :], rsz[sl], pi_[:, 256:512], rpi)
                        V(lambda e, pi_=pi_, Bs=Bs: e.tensor_copy(out=Bs[0], in_=pi_[:, 0:256]), [rpi], [rBs[0]])
                        if l == 0:
                            A(lambda e, Fs=Fs: e.activation(out=Fs[2], in_=Fs[1], func=AF.Copy, scale=-1.0), [rFs[1]], [rFs[2]])
                            A(lambda e, Fs=Fs: e.activation(out=Fs[1], in_=Fs[1], func=AF.Exp, scale=-1.0), [rFs[1]], [rFs[1]])
                        else:
                            A(lambda e, Fs=Fs: e.activation(out=Fs[1], in_=Fs[1], func=AF.Exp, scale=-1.0), [rFs[1]], [rFs[1]])
                            V(lambda e, g=g, Fs=Fs: e.tensor_tensor(out=Fs[1], in0=Fs[1], in1=oml_g,
                                                                    op=ALU.mult), [rFs[1], rlb], [rFs[1]])
                            V(lambda e, g=g, Fs=Fs: e.tensor_tensor(out=Fs[1], in0=Fs[1], in1=lb_g,
                                                                    op=ALU.add), [rFs[1], rlb], [rFs[1]])
                            A(lambda e, Fs=Fs: e.activation(out=Fs[2], in_=Fs[1], func=AF.Ln), [rFs[1]], [rFs[2]])
                        (G if GOFF else V)(lambda e, Fs=Fs: e.tensor_scalar(out=Fs[3], in0=Fs[1], scalar1=-1.0, scalar2=1.0, op0=ALU.mult,
                                                                            op1=ALU.add), [rFs[1]], [rFs[3]])
                        T(lambda e, Fs=Fs: e.matmul(psG[:, 0:256], lhsT=Tri32, rhs=Fs[2], start=True, stop=True),
                          [rcst, rFs[2]], [rpsG])
                        T(lambda e, Fs=Fs: e.matmul(psG[:, 256:512], lhsT=TriE32, rhs=Fs[2], start=True, stop=True),
                          [rcst, rFs[2]], [rpsG])
                        for hh in range(2):
                            T(lambda e, hh=hh, Fs=Fs: e.matmul(psS[1][:, 256 + 2 * hh:256 + 2 * hh + 2],
                                                               lhsT=Fs[2][:, hh * 128:(hh + 1) * 128],
                                                               rhs=cst[:, 512:514], start=True, stop=True), [rFs[2], rcst], [rpsS[1]])
                        dsl = dec8[:, 4 * (i % 2):4 * (i % 2) + 4]
                        rds = r_dec8[i % 2]
                        A(lambda e, dsl=dsl: e.activation(out=dsl, in_=psS[1][:, 256:260], func=AF.Exp), [rpsS[1]], [rds])
                        A(lambda e, Fs=Fs: e.activation(out=Fs[4], in_=psG[:, 0:256], func=AF.Exp), [rpsG], [rFs[4]])
                        A(lambda e, Fs=Fs: e.activation(out=Fs[5], in_=psG[:, 0:256], func=AF.Exp, scale=-1.0), [rpsG], [rFs[5]])
                        A(lambda e, Fs=Fs: e.activation(out=Fs[6], in_=psG[:, 256:512], func=AF.Exp), [rpsG], [rFs[6]])
                        V(lambda e, Fs=Fs, Bs=Bs: e.tensor_tensor(out=Bs[1], in0=Fs[0], in1=Fs[4], op=ALU.mult), [rFs[0], rFs[4]], [rBs[1]])
                        G(lambda e, Fs=Fs, Bs=Bs: e.tensor_tensor(out=Bs[2], in0=Fs[3], in1=Fs[5], op=ALU.mult), [rFs[3], rFs[5]], [rBs[2]])
                        G(lambda e, Fs=Fs, Bs=Bs: e.tensor_tensor(out=Bs[3], in0=Fs[3], in1=Fs[6], op=ALU.mult), [rFs[3], rFs[6]], [rBs[3]])
                        for hh in range(2):
                            T(lambda e, hh=hh, Bs=Bs: e.transpose(out=psT[:, hh * 128:(hh + 1) * 128], in_=Bs[1][:, hh * 128:(hh + 1) * 128],
                                                                  identity=idb[:]), [rBs[1], ridb], [rpsT])
                            T(lambda e, hh=hh, Bs=Bs: e.transpose(out=psT[:, 256 + hh * 128:256 + (hh + 1) * 128],
                                                                  in_=Bs[2][:, hh * 128:(hh + 1) * 128], identity=idb[:]),
                              [rBs[2], ridb], [rpsT])
                        pq3 = psT[:, 0:256].rearrange("p (h t) -> p h t", h=2)
                        A(lambda e, Bs=Bs: e.copy(out=Bs[4], in_=psT[:, 0:256]), [rpsT], [rBs[4]])
                        A(lambda e, pq3=pq3: e.copy(out=qTA[:, :, 0:64], in_=pq3[:, :, 0:64]), [rpsT], [rqTA])
                        V(lambda e, Bs=Bs: e.tensor_copy(out=Bs[5], in_=psT[:, 256:512]), [rpsT], [rBs[5]])
                        V(lambda e, pq3=pq3: e.tensor_copy(out=qTB[:, :, 64:128], in_=pq3[:, :, 64:128]), [rpsT], [rqTB])
                        cur = i % 2
                        nxtb = 1 - cur
                        for hh in range(2):
                            hs = slice(hh * 128, (hh + 1) * 128)
                            T(lambda e, hs=hs, Bs=Bs: e.matmul(psS[1][:, hs], lhsT=Bs[5][:, hs], rhs=Bs[4][:, hs], start=True, stop=True),
                              [rBs[5], rBs[4]], [rpsS[1]])
                        V(lambda e, Bs=Bs: e.tensor_tensor(out=Bs[6].rearrange("p (h t) -> p h t", h=2),
                                                           in0=psS[1][:, 0:256].rearrange("p (h t) -> p h t", h=2),
                                                           in1=hgmb[:].unsqueeze(1).to_broadcast([128, 2, 128]), op=ALU.mult),
                          [rpsS[1], rhgmb], [rBs[6]])
                        for hh in range(2):
                            hs = slice(hh * 128, (hh + 1) * 128)
                            T(lambda e, hs=hs, hh=hh, Bs=Bs: e.matmul(psO[hh][:, 0:128], lhsT=Bs[6][:, hs], rhs=Bs[0][:, hs],
                                                                      start=True, stop=False), [rBs[6], rBs[0]], [rpsO[hh]])
                            T(lambda e, hh=hh, cur=cur: e.matmul(psO[hh][:, 0:128], lhsT=qTA[:, hh, :], rhs=Sbf[:, cur, hh, :],
                                                                 start=False, stop=False), [rqTA, rSbf[cur][hh]], [rpsO[hh]])
                        for hh in range(2):
                            hs = slice(hh * 128, (hh + 1) * 128)
                            T(lambda e, hs=hs, Bs=Bs: e.matmul(psS[0][:, hs], lhsT=Bs[3][0:64, hs], rhs=Bs[0][0:64, hs],
                                                               start=True, stop=True), [rBs[3], rBs[0]], [rpsS[0], rrow])
                        for hh in range(2):
                            hs = slice(hh * 128, (hh + 1) * 128)
                            V(lambda e, hs=hs, hh=hh, dsl=dsl: e.scalar_tensor_tensor(out=S32[:, hh, :], in0=S32[:, hh, :],
                                                                                      scalar=dsl[:, 2 * hh:2 * hh + 1], in1=psS[0][:, hs],
                                                                                      op0=ALU.mult, op1=ALU.add),
                              [rS32[hh], rds, rpsS[0]], [rS32[hh]])
                            G(lambda e, hh=hh, nxtb=nxtb: e.tensor_copy(out=Sbf[:, nxtb, hh, :], in_=S32[:, hh, :]),
                              [rS32[hh]], [rSbf[nxtb][hh]])
                        for hh in range(2):
                            T(lambda e, hh=hh, nxtb=nxtb: e.matmul(psO[hh][:, 0:128], lhsT=qTB[:, hh, :], rhs=Sbf[:, nxtb, hh, :],
                                                                   start=False, stop=True), [rqTB, rSbf[nxtb][hh]], [rpsO[hh], rrow])
                        for hh in range(2):
                            hs = slice(hh * 128, (hh + 1) * 128)
                            T(lambda e, hs=hs, Bs=Bs: e.matmul(psS[0][:, hs], lhsT=Bs[3][64:128, hs], rhs=Bs[0][64:128, hs],
                                                               start=True, stop=True), [rBs[3], rBs[0]], [rpsS[0], rrow])
                        for hh in range(2):
                            hs = slice(hh * 128, (hh + 1) * 128)
                            V(lambda e, hs=hs, hh=hh, dsl=dsl: e.scalar_tensor_tensor(out=S32[:, hh, :], in0=S32[:, hh, :],
                                                                                      scalar=dsl[:, 2 * hh + 1:2 * hh + 2], in1=psS[0][:, hs],
                                                                                      op0=ALU.mult, op1=ALU.add),
                              [rS32[hh], rds, rpsS[0]], [rS32[hh]])
                        for hh in range(2):
                            G(lambda e, hh=hh, nxtb=nxtb: e.tensor_copy(out=Sbf[:, nxtb, hh, :], in_=S32[:, hh, :]),
                              [rS32[hh]], [rSbf[nxtb][hh]])
                        ssl = sso4[:, 2 * (i % 2):2 * (i % 2) + 2]
                        r_sso = r_sso2[i % 2]
                        for hh in range(2):
                            A(lambda e, hh=hh, Fs=Fs, ssl=ssl: e.activation(out=Fs[7][:, 0:128], in_=psO[hh][:, 0:128], func=AF.Square,
                                                                            accum_out=ssl[:, hh:hh + 1]), [rpsO[hh]], [rFs[7], r_sso])
                        A(lambda e, ssl=ssl: e.activation(out=ssl, in_=ssl, func=AF.Ln, scale=1.0 / 128, bias=EPS), [r_sso], [r_sso])
                        A(lambda e, ssl=ssl: e.activation(out=ssl, in_=ssl, func=AF.Exp, scale=-0.5), [r_sso], [r_sso])
                        for hh in range(2):
                            hs = slice(hh * 128, (hh + 1) * 128)
                            V(lambda e, hh=hh, hs=hs, Fs=Fs, ssl=ssl: e.scalar_tensor_tensor(out=Fs[8][:, hs], in0=psO[hh][:, 0:128],
                                                                                             scalar=ssl[:, hh:hh + 1], in1=go_bc[:],
                                                                                             op0=ALU.mult, op1=ALU.mult),
                              [rpsO[hh], r_sso, rgains], [rFs[8]])
                        G(lambda e, Fs=Fs, sl=sl, sz=sz: e.tensor_tensor(out=y_tok[:, sl, :], in0=Fs[8], in1=sz[:, sl, :], op=ALU.mult),
                          [rFs[8], rsz[sl]], [ry[sl]])
                        outproj_tile(i, sl, last, obanks=[(psS[0], rpsS[0]), (psS[0], rpsS[0])])
            if not glist and last_layer:
                for i in range(NT):
                    out_dmas.append(DMA("sync", lambda e, i=i: e.dma_start(out=out_d[i * 128:(i + 1) * 128, :], in_=x_tok[:, i, :]),
                                        r=[rx[i]]))
        if SCHED:
            if SCHED2:
                P.schedule2(SDELTA)
            else:
                P.schedule()
        P.emit(st, out_dmas)
    build_nc.stats = P.stats
    return nc


_CACHE = {}


def _get_nc(layers, groups):
    key = (tuple(layers), tuple(groups))
    if key not in _CACHE:
        _CACHE[key] = build_nc(layers, groups)
    return _CACHE[key]


def run(inputs, layers=(0, 1), groups=ALL_GROUPS, cores=8):
    nc = _get_nc(layers, groups)
    f = lambda a: np.ascontiguousarray(np.asarray(a))
    cst = make_consts()
    shared = {k: f(inputs[k]).astype(np.float32, copy=False) for k in
              ("norm_g", "w_in", "w_out", "moba_q_norm", "moba_k_norm", "hgrn_lb_logits", "hgrn_o_norm",
               "mem_norm_g", "w_mem_kv", "mem_q_norm", "mem_k_norm")}
    x = f(inputs["x"]); mem = f(inputs["mem"]); pos = f(inputs["positions"]).astype(np.int32, copy=False)
    in_maps = []
    for b in range(cores):
        m = dict(shared)
        m["x"] = x[b]
        m["mem"] = mem[b]
        m["pos"] = pos[b].reshape(16, 128)
        m["cst"] = cst
        in_maps.append(m)
    res = run_bass_kernel_spmd(nc, in_maps, core_ids=list(range(cores)))
    return np.stack([np.asarray(r["out"]) for r in res.results], axis=0)


def kernel(x, mem, positions, norm_g, w_in, w_out, moba_q_norm, moba_k_norm, hgrn_lb_logits,
           hgrn_o_norm, mem_norm_g, w_mem_kv, mem_q_norm, mem_k_norm):
    inputs = dict(x=x, mem=mem, positions=positions, norm_g=norm_g, w_in=w_in, w_out=w_out,
                  moba_q_norm=moba_q_norm, moba_k_norm=moba_k_norm, hgrn_lb_logits=hgrn_lb_logits,
                  hgrn_o_norm=hgrn_o_norm, mem_norm_g=mem_norm_g, w_mem_kv=w_mem_kv,
                  mem_q_norm=mem_q_norm, mem_k_norm=mem_k_norm)
    return run(inputs).astype(np.float32, copy=False)
```

```python
import numpy as np
from contextlib import ExitStack
import concourse.bass as bass
import concourse.mybir as mybir
from concourse.bass_utils import run_bass_kernel_spmd

F32 = mybir.dt.float32
BF16 = mybir.dt.bfloat16
I32 = mybir.dt.int32
AF = mybir.ActivationFunctionType
ALU = mybir.AluOpType
AX = mybir.AxisListType

S = 2048
D = 1024
NT = 16
EPS = 1e-6
NCST = 576
import os as _os0
ALL_GROUPS = tuple(_os0.environ.get("ORDER", "A0,A1,H0,H1,M0,M1").split(","))
import os as _os
SCHED = _os.environ.get("SCHED", "1") == "1"
PAR1 = int(_os.environ.get("PAR1", "1"))
PAR2 = int(_os.environ.get("PAR2", "1"))
GPAR = int(_os.environ.get("GPAR", "1"))
PS3 = int(_os.environ.get("PS3", "3"))
MASKV = int(_os.environ.get("MASKV", "1"))
OPB = int(_os.environ.get("OPB", "1"))
SCHED2 = int(_os.environ.get("SCHED2", "1"))
SDELTA = float(_os.environ.get("SDELTA", "120"))
LATX = float(_os.environ.get("LATX", "180"))
PEK = float(_os.environ.get("PEK", "0.65"))
ACTK = float(_os.environ.get("ACTK", "1.0"))
DVEK = float(_os.environ.get("DVEK", "1.0"))
POOLK = float(_os.environ.get("POOLK", "1.0"))
LATS = float(_os.environ.get("LATS", "60"))
XQ = int(_os.environ.get("XQ", "0"))
GOFF = int(_os.environ.get("GOFF", "0"))
TRANS = int(_os.environ.get("TRANS", "1"))
PRUNE = int(_os.environ.get("PRUNE", "1"))
WIDE = int(_os.environ.get("WIDE", "1"))
WIDEM = int(_os.environ.get("WIDEM", "1"))


class Res:
    __slots__ = ("name", "w", "r", "excl")

    def __init__(self, name, excl=False):
        self.name = name
        self.w = None
        self.r = []
        self.excl = excl


class _Rec:
    def __init__(self):
        self.name = None
        self.args = ()
        self.kw = {}

    def __getattr__(self, name):
        def f(*a, **k):
            self.name, self.args, self.kw = name, a, k
            return self
        return f


def _free_size(ap):
    n = 1
    for d in list(ap.shape)[1:]:
        n *= int(d)
    return n


class Op:
    __slots__ = ("eng", "fn", "deps", "sdeps", "sig", "idx", "dma", "sem", "val", "i", "cost", "start")

    def __init__(self, eng, fn, deps, sdeps, dma):
        self.eng = eng
        self.fn = fn
        self.deps = deps
        self.sdeps = sdeps
        self.sig = False
        self.idx = 0
        self.dma = dma
        self.sem = None
        self.val = 0
        self.i = 0
        self.start = 0.0
        rec = _Rec()
        fn(rec)
        out = rec.kw.get("out", rec.args[0] if rec.args else None)
        n = _free_size(out) if out is not None else 64
        if dma:
            c = 2000.0 + n * int(out.shape[0]) * 4 / 120.0
        elif eng == "tensor":
            if rec.name == "transpose":
                c = 110.0
            else:
                lhsT = rec.kw.get("lhsT")
                f32 = lhsT is not None and lhsT.dtype == F32
                c = PEK * (64.0 + max(n, 64) / 2.0) * (4.0 if f32 else 1.0)
        elif eng == "scalar":
            c = ACTK * (200.0 + n / 1.2)
        elif eng == "vector":
            c = DVEK * (120.0 + n / 0.96 * (8.0 if rec.name == "reciprocal" else 1.0))
        else:
            c = POOLK * (300.0 + n / 0.5)
        self.cost = c


class Prog:
    ENGS = ["tensor", "vector", "scalar", "gpsimd", "sync"]

    def __init__(self, nc):
        self.nc = nc
        self.ops = []

    phase = None
    tok = None
    tokset = ()

    def op(self, eng, fn, reads=(), writes=(), dma=False):
        if self.phase == "gate" and self.tok is not None:
            writes = list(writes) + [self.tok]
        elif self.phase == "chain" and eng in self.tokset:
            reads = list(reads) + [self.tok]
        deps, sdeps = {}, {}

        def add(d):
            if d.dma or dma or d.eng != eng or eng != "tensor":
                deps[id(d)] = d
            else:
                sdeps[id(d)] = d
        for r in reads:
            if r.w is not None:
                add(r.w)
            if r.excl:
                for d in r.r:
                    if d.eng != eng:
                        add(d)
        for w in writes:
            if w.w is not None:
                add(w.w)
            for d in w.r:
                add(d)
        o = Op(eng, fn, list(deps.values()), list(sdeps.values()), dma)
        for r in reads:
            r.r.append(o)
        for w in writes:
            w.w = o
            w.r = []
        self.ops.append(o)
        return o

    def schedule(self):
        import heapq
        ops = self.ops
        for i, o in enumerate(ops):
            o.i = i
        succs = [[] for _ in ops]
        npred = [0] * len(ops)
        for o in ops:
            ds = o.deps + o.sdeps
            npred[o.i] = len(ds)
            for d in ds:
                succs[d.i].append(o)
        ready = [0.0] * len(ops)
        free = {e: 0.0 for e in self.ENGS}
        heap = [(0.0, o.i) for o in ops if npred[o.i] == 0]
        heapq.heapify(heap)
        done = 0
        while heap:
            t, i = heapq.heappop(heap)
            o = ops[i]
            st = max(ready[i], free[o.eng])
            if st > t + 1e-9:
                heapq.heappush(heap, (st, i))
                continue
            o.start = st
            if o.dma:
                free[o.eng] = st + 150.0
            else:
                free[o.eng] = st + o.cost
            fin = st + o.cost
            done += 1
            for sc in succs[i]:
                lat = 60.0 if (sc.eng == o.eng and not o.dma) else 180.0
                if fin + lat > ready[sc.i]:
                    ready[sc.i] = fin + lat
                npred[sc.i] -= 1
                if npred[sc.i] == 0:
                    heapq.heappush(heap, (max(ready[sc.i], free[sc.eng]), sc.i))
        assert done == len(ops), (done, len(ops))
        self.ops = sorted(ops, key=lambda o: (o.start, o.i))
        self.est_ns = max(o.start + o.cost for o in ops)

    def schedule2(self, delta=120.0):
        ops = self.ops
        n = len(ops)
        for i, o in enumerate(ops):
            o.i = i
        succs = [[] for _ in ops]
        npred = [0] * n
        for o in ops:
            ds = o.deps + o.sdeps
            npred[o.i] = len(ds)
            for d in ds:
                succs[d.i].append(o)
        blev = [0.0] * n
        for o in reversed(ops):
            b = 0.0
            for sc in succs[o.i]:
                lat = LATS if (sc.eng == o.eng and not o.dma) else LATX
                v = lat + blev[sc.i]
                if v > b:
                    b = v
            blev[o.i] = b + o.cost
        ready = [0.0] * n
        free = {e: 0.0 for e in self.ENGS}
        rsets = {e: [] for e in self.ENGS}
        for o in ops:
            if npred[o.i] == 0:
                rsets[o.eng].append(o.i)
        done = 0
        while done < n:
            best_e, best_t = None, 1e30
            for e in self.ENGS:
                rs = rsets[e]
                if not rs:
                    continue
                t = min(ready[i] for i in rs)
                if t < free[e]:
                    t = free[e]
                if t < best_t:
                    best_t, best_e = t, e
            e = best_e
            rs = rsets[e]
            lim = best_t + delta
            pick, pb = -1, -1.0
            for i in rs:
                if ready[i] <= lim and blev[i] > pb:
                    pb, pick = blev[i], i
            rs.remove(pick)
            o = ops[pick]
            st = max(ready[pick], free[e])
            o.start = st
            free[e] = st + (150.0 if o.dma else o.cost)
            fin = st + o.cost
            done += 1
            for sc in succs[pick]:
                lat = LATS if (sc.eng == o.eng and not o.dma) else LATX
                if fin + lat > ready[sc.i]:
                    ready[sc.i] = fin + lat
                npred[sc.i] -= 1
                if npred[sc.i] == 0:
                    rsets[sc.eng].append(sc.i)
        self.ops = sorted(ops, key=lambda o: (o.start, o.i))
        self.est_ns = max(o.start + o.cost for o in ops)

    def emit(self, stack, final_deps, ndma_sems=8):
        nc = self.nc
        if PRUNE:
            pos = {id(o): k for k, o in enumerate(self.ops)}
            for o in self.ops:
                best = {}
                keep = []
                for d in o.deps:
                    if d.dma:
                        keep.append(d)
                    elif d.eng not in best or pos[id(d)] > pos[id(best[d.eng])]:
                        best[d.eng] = d
                o.deps = keep + list(best.values())
        for o in self.ops:
            for d in o.deps:
                d.sig = True
        for d in final_deps:
            d.sig = True
        sems = {e: stack.enter_context(nc.semaphore("s_" + e)) for e in self.ENGS}
        cnt = {e: 0 for e in self.ENGS}
        pools, pool_i, pre_wait = {}, {}, {}
        for o in self.ops:
            if o.dma:
                if o.eng not in pools:
                    pools[o.eng] = [[stack.enter_context(nc.semaphore("d_%s_%d" % (o.eng, i))), 0]
                                    for i in range(ndma_sems)]
                    pool_i[o.eng] = 0
                p = pools[o.eng][pool_i[o.eng] % ndma_sems]
                pool_i[o.eng] += 1
                if p[1] > 0:
                    pre_wait[id(o)] = (p[0], p[1])
                p[1] += 16
                o.sem = p[0]
                o.val = p[1]
            elif o.sig:
                cnt[o.eng] += 1
                o.idx = cnt[o.eng]
        per = {e: [o for o in self.ops if o.eng == e] for e in self.ENGS}
        self.stats = {e: len(per[e]) for e in self.ENGS}
        known = {e: {} for e in self.ENGS}
        kn = {}
        plan = {}
        nw = 0

        def semkey(d):
            return (d.sem, d.val) if d.dma else (sems[d.eng], d.idx)

        for o in self.ops:
            kd = known[o.eng]
            ws = []
            for d in o.deps:
                sm, val = semkey(d)
                if kd.get(id(sm), (None, 0))[1] < val:
                    ws.append((sm, val))
                    kd[id(sm)] = (sm, val)
                if TRANS:
                    for k2, (s2, v2) in kn[id(d)].items():
                        if kd.get(k2, (None, 0))[1] < v2:
                            kd[k2] = (s2, v2)
            if o.dma:
                pw = pre_wait.get(id(o))
                if pw and kd.get(id(pw[0]), (None, 0))[1] < pw[1]:
                    ws.append(pw)
                    kd[id(pw[0])] = pw
            plan[id(o)] = ws
            nw += len(ws)
            if o.dma or o.sig:
                mine = dict(kd)
                sm, val = semkey(o)
                mine[id(sm)] = (sm, val)
                kn[id(o)] = mine
        fin_w = []
        kd = known["sync"]
        for d in final_deps:
            sm, val = semkey(d)
            if kd.get(id(sm), (None, 0))[1] < val:
                fin_w.append((sm, val))
                kd[id(sm)] = (sm, val)
        self.stats["waits"] = nw
        self.stats["sigs"] = {e: sum(1 for o in per[e] if o.sig and not o.dma) for e in self.ENGS}
        block = stack.enter_context(nc.Block())

        def mk(e):
            def body(engobj):
                for o in per[e]:
                    for sm, val in plan[id(o)]:
                        engobj.wait_ge(sm, val)
                    if o.dma:
                        o.fn(engobj).then_inc(o.sem, 16)
                    else:
                        ins = o.fn(engobj)
                        if o.sig:
                            ins.then_inc(sems[e], 1)
                if e == "sync":
                    for sm, val in fin_w:
                        engobj.wait_ge(sm, val)
            return body

        block.tensor(mk("tensor"))
        block.vector(mk("vector"))
        block.scalar(mk("scalar"))
        block.gpsimd(mk("gpsimd"))
        block.sync(mk("sync"))


def make_consts():
    c = np.zeros((128, NCST), np.float32)
    i = np.arange(128)
    c[:, 0:128] = np.eye(128)
    c[:, 128:256] = (i[None, :] >= i[:, None])
    same = (i[:, None] // 64) == (i[None, :] // 64)
    c[:, 256:384] = same & (i[:, None] <= i[None, :])
    c[:, 384:512] = same & (i[:, None] > i[None, :])
    c[:, 512] = i < 64
    c[:, 513] = i >= 64
    c[:, 514] = 1.0
    f64 = 500000.0 ** (-np.arange(8, dtype=np.float64) / 8.0)
    f = f64.astype(np.float32)
    flo = (f64 - f.astype(np.float64)).astype(np.float32)
    c[:, 515:523] = f[None, :]
    c[:, 523:531] = f[None, :]
    c[:, 547:555] = flo[None, :]
    c[:, 555:563] = flo[None, :]
    c[:, 531:539] = 0.0
    c[:, 539:547] = np.pi / 2
    return c


def build_nc(layers=(0, 1), groups=ALL_GROUPS):
    nc = bass.Bass("TRN2", target_bir_lowering=False)

    def din(name, shape, d=F32):
        return nc.dram_tensor(name, shape, d, kind="ExternalInput").ap()

    x_d = din("x", [S, D])
    mem_d = din("mem", [256, D])
    pos_d = din("pos", [16, 128], I32)
    ng_d = din("norm_g", [2, D])
    win_d = din("w_in", [2, D, 5120])
    wout_d = din("w_out", [2, 1536, D])
    gq_d = din("moba_q_norm", [2, 64])
    gk_d = din("moba_k_norm", [2, 64])
    lbl_d = din("hgrn_lb_logits", [2, 512])
    go_d = din("hgrn_o_norm", [2, 128])
    mng_d = din("mem_norm_g", [2, D])
    wkv_d = din("w_mem_kv", [2, D, 1024])
    gmq_d = din("mem_q_norm", [2, 128])
    gmk_d = din("mem_k_norm", [2, 128])
    cst_d = din("cst", [128, NCST])
    out_d = nc.dram_tensor("out", [S, D], F32, kind="ExternalOutput").ap()

    P = Prog(nc)
    if _os.environ.get("TOKR"):
        P.tok = Res("tok")
        P.tokset = tuple(_os.environ["TOKR"].split(","))
    with ExitStack() as st:
        def sb(name, shape, dt=F32):
            return st.enter_context(nc.sbuf_tensor("sb_" + name, shape, dt))

        def ps(name, shape, dt=F32):
            return st.enter_context(nc.psum_tensor("pp_" + name, shape, dt))

        def T(fn, r=(), w=()):
            return P.op("tensor", fn, r, w)

        def V(fn, r=(), w=()):
            return P.op("vector", fn, r, w)

        def A(fn, r=(), w=()):
            return P.op("scalar", fn, r, w)

        def G(fn, r=(), w=()):
            return P.op("gpsimd", fn, r, w)

        def DMA(q, fn, r=(), w=()):
            return P.op(q, fn, r, w, dma=True)

        x_tok = sb("x_tok", [128, NT, D]); rx = [Res("x%d" % i) for i in range(NT)]
        hT = sb("hT", [128, 8, S], BF16); rhT = [Res("hT%d" % i) for i in range(NT)]
        cst = sb("cst", [128, NCST]); rcst = Res("cst")
        idb = sb("idb", [128, 128], BF16); ridb = Res("idb")
        trib = sb("trib", [128, 128], BF16); rtrib = Res("trib")
        hgmb = sb("hgmb", [128, 128], BF16); rhgmb = Res("hgmb")
        gq_bc = sb("gq_bc", [128, 64]); gk_bc = sb("gk_bc", [128, 64]); go_bc = sb("go_bc", [128, 128])
        gmq_bc = sb("gmq_bc", [128, 128]); gmk_bc = sb("gmk_bc", [128, 128]); rgains = Res("gains")
        cs = sb("cs", [128, NT, 16]); sn = sb("sn", [128, NT, 16]); rrope = Res("rope")
        wb = [sb("wb%d" % i, [128, 8, 512], BF16) for i in range(3)]; rwb = [[Res("wb%da" % i), Res("wb%db" % i)] for i in range(3)]
        wo = sb("wo", [128, 2, D], BF16); rwo = Res("wo")
        Fp = [sb("F%d" % i, [128, 256]) for i in range(9)]; rF = [Res("F%d" % i) for i in range(9)]
        Bp = [sb("B%d" % i, [128, 256], BF16) for i in range(7)]; rB = [Res("B%d" % i) for i in range(7)]
        szs = [sb("sz%d" % k, [128, 4, 256]) for k in range(2)]; rszs = [[Res("sz%d_%d" % (k, i)) for i in range(4)] for k in range(2)]
        y_tok = sb("y_tok", [128, 4, 256], BF16); ry = [Res("y%d" % i) for i in range(4)]
        yT = [sb("yT%d" % i, [128, 256], BF16) for i in range(2)]; ryT = [Res("yT%d" % i) for i in range(2)]
        pT = [sb("pT%d" % i, [128, 512], BF16) for i in range(4)]; rpT = [Res("pT%d" % i) for i in range(4)]
        hbs = [sb("hb%d" % i, [128, D], BF16) for i in range(2)]; rhbs = [Res("hb0"), Res("hb1")]
        small = sb("small", [128, 128]); rsm = {}

        def sm(name, a, n):
            rsm[name] = Res("sm_" + name)
            return small[:, a:a + n], rsm[name]
        ss16, r_ss16 = sm("ss16", 0, 16)
        t16, r_t16 = sm("t16", 16, 16)
        rstd16, r_rstd16 = sm("rstd16", 32, 16)
        ss4, r_ss4 = sm("ss4", 48, 4)
        t4, r_t4 = sm("t4", 52, 4)
        rs4, r_rs4 = sm("rs4", 56, 4)
        rden, r_rden = sm("rden", 60, 4)
        rdenB, r_rdenB = sm("rdenB", 108, 4)
        dec4, r_dec4 = sm("dec4", 64, 4)
        sso, r_sso = sm("sso", 68, 2)
        to2, r_to2 = sm("to2", 70, 2)
        rso, r_rso = sm("rso", 72, 2)
        gm = sb("gm", [128, 4, 8]); rgm = Res("gm")
        top8 = sb("top8", [128, 4, 8]); rtop8 = Res("top8")
        selb = sb("selb", [128, 4, 8]); rselb = Res("selb")
        kT = sb("kT", [128, 4, S], BF16); rkT = [Res("kT%d" % i) for i in range(NT)]
        v_flat = sb("v_aug", [128, NT * 4 * 65], BF16); rv = [Res("v%d" % i) for i in range(NT)]
        v_aug = v_flat[:].rearrange("p (a b c) -> p a b c", a=NT, b=4)
        stage = v_flat[:, 0:2048].bitcast(F32); rstage = Res("stage")
        g_bcv = v_flat[:, 2048:4096].bitcast(F32); rg_bc = Res("g_bc")
        qTs = [sb("qT%d" % i, [128, 2048], BF16) for i in range(2)]; rqTs = [Res("qT0"), Res("qT1")]
        lbv = qTs[1][:, 0:2048].bitcast(F32); rlb = rqTs[1]
        lb_g = lbv[:, 0:256]; oml_g = lbv[:, 256:512]
        k_aug = sb("k_aug", [128, 4, 72], BF16); rk_aug = Res("k_aug")
        q_aug = sb("q_aug", [128, 4, 72], BF16); rq_aug = Res("q_aug")
        kmT32 = sb("kmT32", [128, 2, 2, 8]); rkm = Res("kmT32")
        S32 = sb("S32", [128, 2, 128]); rS32 = [Res("S32_0"), Res("S32_1")]
        Sbf = sb("Sbf", [128, 2, 2, 128], BF16); rSbf = [[Res("Sbf00"), Res("Sbf01")], [Res("Sbf10"), Res("Sbf11")]]
        qTA = sb("qTA", [128, 2, 128], BF16); qTB = sb("qTB", [128, 2, 128], BF16)
        rqTA = Res("qTA"); rqTB = Res("qTB")
        memT = sb("memT", [128, 8, 256], BF16); rmemT = Res("memT")
        kmT = sb("kmT", [128, 2, 256], BF16); rkmT = Res("kmT")
        vm_aug = sb("vm_aug", [128, 2, 2, 129], BF16); rvm = Res("vm")
        psA = [ps("psA%d" % i, [128, 512]) for i in range(2)]; rpsA = [Res("psA0", True), Res("psA1", True)]
        psT = ps("psT", [128, 1024], BF16); rpsT = Res("psT", True)
        psG = ps("psG", [128, 512]); rpsG = Res("psG", True)
        psS = [ps("psS%d" % i, [128, 512]) for i in range(2)]; rpsS = [Res("psS0", True), Res("psS1", True)]
        psO = [ps("psO%d" % i, [128, 512]) for i in range(2)]; rpsO = [Res("psO0", True), Res("psO1", True)]

        ctr = {"pa": 0, "w": 0, "ev": 0, "ps": 0, "pt": 0, "yt": 0, "mk": 0, "pw": 0, "pw3": 0, "ps2": 0}

        def nxt(k, n):
            v = ctr[k] % n
            ctr[k] += 1
            return v

        def evac(fn_v, fn_a, r, w):
            if nxt("ev", 2) == 0:
                return A(fn_a, r, w)
            return V(fn_v, r, w)

        DMA("sync", lambda e: e.dma_start(out=cst[:], in_=cst_d), w=[rcst])
        V(lambda e: e.tensor_copy(out=idb[:], in_=cst[:, 0:128]), [rcst], [ridb])
        V(lambda e: e.tensor_copy(out=trib[:], in_=cst[:, 128:256]), [rcst], [rtrib])
        V(lambda e: e.tensor_copy(out=hgmb[:], in_=cst[:, 256:384]), [rcst], [rhgmb])
        ident32 = cst[:, 0:128]
        G(lambda e: e.memset(vm_aug[:, :, :, 128:129], 1.0), w=[rvm])
        G(lambda e: e.memset(qTA[:], 0.0), w=[rqTA])
        G(lambda e: e.memset(qTB[:], 0.0), w=[rqTB])
        nI = sb("nI", [128, 256], I32); rnI = Res("nI")
        posi = nI[0:16, 0:128]; rposi = rnI
        posf = Fp[8][0:16, 0:128]; rposf = rF[8]
        DMA("sync", lambda e: e.dma_start(out=posi, in_=pos_d), w=[rposi])
        V(lambda e: e.tensor_copy(out=posf, in_=posi), [rposi], [rposf])
        T(lambda e: e.matmul(psG[:, 0:16], lhsT=posf, rhs=cst[0:16, 0:16], start=True, stop=True), [rposf, rcst], [rpsG])
        post, r_post = sm("post", 80, 16)
        V(lambda e: e.tensor_copy(out=post, in_=psG[:, 0:16]), [rpsG], [r_post])
        ang = Fp[0][:, 0:256].rearrange("p (i j) -> p i j", i=NT)
        V(lambda e: e.tensor_tensor(out=ang, in0=post.unsqueeze(2).to_broadcast([128, NT, 16]),
                                    in1=cst[:, 515:531].unsqueeze(1).to_broadcast([128, NT, 16]), op=ALU.mult),
          [r_post, rcst], [rF[0]])
        ang_lo = Fp[1][:, 0:256].rearrange("p (i j) -> p i j", i=NT)
        V(lambda e: e.tensor_tensor(out=ang_lo, in0=post.unsqueeze(2).to_broadcast([128, NT, 16]),
                                    in1=cst[:, 547:563].unsqueeze(1).to_broadcast([128, NT, 16]), op=ALU.mult),
          [r_post, rcst], [rF[1]])
        V(lambda e: e.tensor_tensor(out=ang, in0=ang, in1=ang_lo, op=ALU.add), [rF[0], rF[1]], [rF[0]])
        V(lambda e: e.tensor_tensor(out=ang, in0=ang, in1=cst[:, 531:547].unsqueeze(1).to_broadcast([128, NT, 16]),
                                    op=ALU.add), [rF[0], rcst], [rF[0]])
        V(lambda e: e.tensor_scalar(out=Fp[1][:], in0=Fp[0][:], scalar1=float(1.0 / (2 * np.pi)), scalar2=None,
                                    op0=ALU.mult), [rF[0]], [rF[1]])
        V(lambda e: e.tensor_copy(out=nI[:], in_=Fp[1][:]), [rF[1]], [rnI])
        V(lambda e: e.tensor_copy(out=Fp[1][:], in_=nI[:]), [rnI], [rF[1]])
        C1 = 6.28125
        C2 = float(2 * np.pi - 6.28125)
        V(lambda e: e.scalar_tensor_tensor(out=Fp[2][:], in0=Fp[1][:], scalar=-C1, in1=Fp[0][:],
                                           op0=ALU.mult, op1=ALU.add), [rF[1], rF[0]], [rF[2]])
        V(lambda e: e.scalar_tensor_tensor(out=Fp[2][:], in0=Fp[1][:], scalar=-C2, in1=Fp[2][:],
                                           op0=ALU.mult, op1=ALU.add), [rF[1], rF[2]], [rF[2]])
        V(lambda e: e.tensor_scalar(out=Fp[2][:], in0=Fp[2][:], scalar1=float(np.pi), scalar2=float(-np.pi),
                                    op0=ALU.min, op1=ALU.max), [rF[2]], [rF[2]])
        A(lambda e: e.activation(out=Fp[3][:], in_=Fp[2][:], func=AF.Sin), [rF[2]], [rF[3]])
        sc = Fp[3][:, 0:256].rearrange("p (i j) -> p i j", i=NT)
        V(lambda e: e.tensor_copy(out=cs[:, :, 0:8], in_=sc[:, :, 8:16]), [rF[3]], [rrope])
        V(lambda e: e.tensor_copy(out=cs[:, :, 8:16], in_=sc[:, :, 8:16]), [rF[3]], [rrope])
        V(lambda e: e.tensor_scalar(out=sn[:, :, 0:8], in0=sc[:, :, 0:8], scalar1=-1.0, scalar2=None, op0=ALU.mult),
          [rF[3]], [rrope])
        V(lambda e: e.tensor_copy(out=sn[:, :, 8:16], in_=sc[:, :, 0:8]), [rF[3]], [rrope])
        def load_w(dst, rdst, src_ap):
            return DMA("gpsimd", lambda e: e.dma_start(out=dst, in_=src_ap), w=[rdst])

        def w_in_cols(l, c0, n):
            return win_d[l].rearrange("(c p) n -> p c n", p=128)[:, :, c0:c0 + n]

        psGb = psG[:].bitcast(BF16)

        def rms_to_T(src_tile, rsrc, gain, rgain, dstT, rdst, col0, ssc, r_ssc, tsc, r_tsc, rsc, r_rsc, k):
            hb = hbs[k % 2]; rhb = rhbs[k % 2]
            pst, rpst = (psT, rpsT) if k % 2 == 0 else (psGb, rpsG)
            A(lambda e: e.activation(out=hb[:], in_=src_tile, func=AF.Square, accum_out=ssc[:, k:k + 1]),
              [rsrc], [rhb, r_ssc])
            A(lambda e: e.activation(out=tsc[:, k:k + 1], in_=ssc[:, k:k + 1], func=AF.Ln, scale=1.0 / D, bias=EPS),
              [r_ssc], [r_tsc])
            A(lambda e: e.activation(out=rsc[:, k:k + 1], in_=tsc[:, k:k + 1], func=AF.Exp, scale=-0.5),
              [r_tsc], [r_rsc])
            V(lambda e: e.scalar_tensor_tensor(out=hb[:], in0=src_tile, scalar=rsc[:, k:k + 1], in1=gain,
                                               op0=ALU.mult, op1=ALU.mult), [rsrc, r_rsc, rgain], [rhb])
            for c in range(8):
                T(lambda e, c=c: e.transpose(out=pst[:, c * 128:(c + 1) * 128], in_=hb[:, c * 128:(c + 1) * 128],
                                             identity=idb[:]), [rhb, ridb], [rpst])
            src3 = pst[:, 0:1024].rearrange("p (c t) -> p c t", c=8)
            evac(lambda e: e.tensor_copy(out=dstT[:, :, col0:col0 + 128], in_=src3),
                 lambda e: e.copy(out=dstT[:, :, col0:col0 + 128], in_=src3), [rpst], [rdst])

        def proj(lhs_cols, rlhs, wt, rwt, ncols=512, lhsT_src=None, wide=False):
            if wide == 3:
                pt, rpt = ((psA[0], rpsA[0]), (psA[1], rpsA[1]), (psG, rpsG))[nxt("pw3", 3)]
            elif wide:
                pt, rpt = ((psA[0], rpsA[0]), (psA[1], rpsA[1]), (psO[0], rpsO[0]), (psO[1], rpsO[1]))[nxt("pw", 4)]
            else:
                b = nxt("pa", 2)
                pt, rpt = psA[b], rpsA[b]
            src = hT if lhsT_src is None else lhsT_src
            for c in range(8):
                T(lambda e, c=c: e.matmul(pt[:, 0:ncols], lhsT=src[:, c, lhs_cols:lhs_cols + 128],
                                          rhs=wt[:, c, 0:ncols], start=(c == 0), stop=(c == 7)),
                  [rlhs] + list(rwt), [rpt])
            return pt, rpt

        def headnorm(src_ps, rps, H, Dh, gain, outF, routF, tmpA, rtmpA, tmpB, rtmpB):
            n = H * Dh
            A(lambda e: e.activation(out=tmpA[:, 0:n], in_=src_ps, func=AF.Square), [rps], [rtmpA])
            V(lambda e: e.tensor_reduce(out=ss4[:, 0:H], in_=tmpA[:, 0:n].rearrange("p (h d) -> p h d", h=H),
                                        axis=AX.X, op=ALU.add), [rtmpA], [r_ss4])
            A(lambda e: e.activation(out=t4[:, 0:H], in_=ss4[:, 0:H], func=AF.Ln, scale=1.0 / Dh, bias=EPS),
              [r_ss4], [r_t4])
            A(lambda e: e.activation(out=rs4[:, 0:H], in_=t4[:, 0:H], func=AF.Exp, scale=-0.5), [r_t4], [r_rs4])
            V(lambda e: e.tensor_tensor(out=tmpB[:, 0:n].rearrange("p (h d) -> p h d", h=H),
                                        in0=src_ps.rearrange("p (h d) -> p h d", h=H),
                                        in1=rs4[:, 0:H].unsqueeze(2).to_broadcast([128, H, Dh]), op=ALU.mult),
              [rps, r_rs4], [rtmpB])
            (G if GOFF else V)(lambda e: e.tensor_tensor(out=outF[:, 0:n].rearrange("p (h d) -> p h d", h=H),
                                                         in0=tmpB[:, 0:n].rearrange("p (h d) -> p h d", h=H),
                                                         in1=gain.unsqueeze(1).to_broadcast([128, H, Dh]), op=ALU.mult),
                               [rtmpB, rgains], [routF])

        def rope(Fx, rFx, i, tR, rtR):
            x3 = Fx[:, 0:256].rearrange("p (h d) -> p h d", h=4)
            a3 = tR[:, 0:64].rearrange("p (h d) -> p h d", h=4)
            b3 = tR[:, 64:128].rearrange("p (h d) -> p h d", h=4)
            rtA = rtR
            rtB = rtR
            G(lambda e: e.tensor_tensor(out=a3, in0=x3[:, :, 0:16], in1=cs[:, i, :].unsqueeze(1).to_broadcast([128, 4, 16]),
                                        op=ALU.mult), [rFx, rrope], [rtA])
            G(lambda e: e.tensor_tensor(out=b3[:, :, 0:8], in0=x3[:, :, 8:16],
                                        in1=sn[:, i, 0:8].unsqueeze(1).to_broadcast([128, 4, 8]), op=ALU.mult),
              [rFx, rrope], [rtB])
            G(lambda e: e.tensor_tensor(out=b3[:, :, 8:16], in0=x3[:, :, 0:8],
                                        in1=sn[:, i, 8:16].unsqueeze(1).to_broadcast([128, 4, 8]), op=ALU.mult),
              [rFx, rrope], [rtB])
            G(lambda e: e.tensor_tensor(out=x3[:, :, 0:16], in0=a3, in1=b3, op=ALU.add), [rtA, rtB], [rFx])

        def silu_ps(dst, rdst, src_ps, rps, defer=False):
            A(lambda e: e.activation(out=dst, in_=src_ps, func=AF.Exp, scale=-1.0), [rps], [rdst])
            A(lambda e: e.activation(out=dst, in_=dst, func=AF.Ln, bias=1.0), [rdst], [rdst])
            A(lambda e: e.activation(out=dst, in_=dst, func=AF.Exp, scale=-1.0), [rdst], [rdst])

            def fin():
                V(lambda e: e.tensor_tensor(out=dst, in0=src_ps, in1=dst, op=ALU.mult), [rps, rdst], [rdst])
            if defer:
                return fin
            fin()

        SC = [((Fp[0], rF[0]), (Fp[1], rF[1]), (Fp[2], rF[2]), (Fp[3], rF[3])),
              ((Fp[4], rF[4]), (Fp[6], rF[6]), (Fp[7], rF[7]), (Fp[8], rF[8]))]
        AUG = [(k_aug, rk_aug), (q_aug, rq_aug)]
        if _os.environ.get("NOPAR", "0") == "1":
            SC[1] = SC[0]
        if _os.environ.get("NOAUG", "0") == "1":
            AUG[1] = AUG[0]

        dec8, _r = sm("dec8", 96, 8)
        r_dec8 = [Res("dec8a"), Res("dec8b")]
        sso4, _r2 = sm("sso4", 104, 4)
        r_sso2 = [Res("ssoA"), Res("ssoB")]
        dummy = sb("dummy", [128, 8])
        kflat = kT[:].rearrange("p h t -> p (h t)")
        kf32 = kflat[:, 0:4608].bitcast(F32)
        Fq = [kf32[:, k * 256:(k + 1) * 256] for k in range(9)]
        rFq = [Res("Fq%d" % k) for k in range(9)]
        Bq = [kflat[:, 4608 + k * 256:4608 + (k + 1) * 256] for k in range(7)]
        rBq = [Res("Bq%d" % k) for k in range(7)]
        HSETS = [([t[:] for t in Fp], rF, [t[:] for t in Bp], rB), (Fq, rFq, Bq, rBq)]

        def barrier():
            G(lambda e: e.memset(dummy[:], 0.0), w=list(rkT) + rFq + rBq + list(rv) + [rstage, rg_bc])

        rrow = Res("pe_rowfence")

        out_dmas = []

        def outproj_tile(i, r, last, obanks=None):
            yb = nxt("yt", 2)
            for pp in range(2):
                T(lambda e, pp=pp: e.transpose(out=psT[:, pp * 128:(pp + 1) * 128], in_=y_tok[:, r, pp * 128:(pp + 1) * 128],
                                               identity=idb[:]), [ry[r], ridb], [rpsT])
            evac(lambda e: e.tensor_copy(out=yT[yb][:], in_=psT[:, 0:256]),
                 lambda e: e.copy(out=yT[yb][:], in_=psT[:, 0:256]), [rpsT], [ryT[yb]])
            for half in range(2):
                if obanks is None:
                    b = nxt("pa", 2)
                    pso, rpso = psA[b], rpsA[b]
                else:
                    pso, rpso = obanks[half]
                for pp in range(2):
                    T(lambda e, pp=pp, half=half, pso=pso: e.matmul(pso[:, 0:512], lhsT=yT[yb][:, pp * 128:(pp + 1) * 128],
                                                                    rhs=wo[:, pp, half * 512:(half + 1) * 512],
                                                                    start=(pp == 0), stop=(pp == 1)),
                      [ryT[yb], rwo], [rpso])
                V(lambda e, half=half, pso=pso: e.tensor_tensor(out=x_tok[:, i, half * 512:(half + 1) * 512], in0=pso[:, 0:512],
                                                                in1=x_tok[:, i, half * 512:(half + 1) * 512], op=ALU.add),
                  [rpso, rx[i]], [rx[i]])
            if last:
                out_dmas.append(DMA("sync", lambda e: e.dma_start(out=out_d[i * 128:(i + 1) * 128, :], in_=x_tok[:, i, :]),
                                    r=[rx[i]]))

        def attn_chunk(Q, H, Dh, KP, scale, key_tiles, causal, kTsrc, rkTsrc, vsrc, rvsrc, qview, rqT, sz, rsz, nsb=None):
            DA = Dh + 1
            for h in range(H):
                if Dh == 64:
                    o_b = h % 2
                    banks = [o_b, o_b, o_b, o_b]
                    offs = [0, DA, 2 * DA, 3 * DA]
                else:
                    banks = [0, 0, 1, 1]
                    offs = [0, DA, 0, DA]
                started = set()
                kts = key_tiles(Q)
                for kt in kts:
                    j = kt - 4 * Q if causal else -1
                    q0 = max(j, 0) * 128
                    N = 512 - q0
                    sbk = nxt("ps", PS3) if nsb is None else nxt("ps2", nsb)
                    pss, rpss = ((psS[0], rpsS[0]), (psS[1], rpsS[1]), (psG, rpsG))[sbk]
                    T(lambda e, kt=kt, h=h, q0=q0, N=N, pss=pss: e.matmul(
                        pss[:, 0:N], lhsT=kTsrc(h, kt), rhs=qview(h)[:, q0:512], start=True, stop=True),
                      [rkTsrc(kt), rqT], [rpss])
                    pb = nxt("pt", 4)
                    A(lambda e, N=N, pss=pss, pb=pb: e.activation(out=pT[pb][:, 0:N], in_=pss[:, 0:N], func=AF.Exp,
                                                                  scale=scale), [rpss], [rpT[pb]])
                    if j >= 0:
                        (V if (MASKV and nxt("mk", 2) == 0) else G)(
                            lambda e, pb=pb: e.tensor_tensor(out=pT[pb][:, 0:128], in0=pT[pb][:, 0:128], in1=trib[:],
                                                             op=ALU.mult), [rpT[pb], rtrib], [rpT[pb]])
                    for r in range(max(j, 0), 4):
                        bk = banks[r]
                        first = bk not in started
                        started.add(bk)
                        T(lambda e, r=r, kt=kt, h=h, q0=q0, pb=pb, bk=bk, first=first: e.matmul(
                            psO[bk][:, offs[r]:offs[r] + DA], lhsT=pT[pb][:, r * 128 - q0:r * 128 - q0 + 128],
                            rhs=vsrc(h, kt), start=first, stop=False, skip_group_check=True),
                          [rpT[pb], rvsrc(kt)], [rpsO[bk]])
                rd, r_rd = (rden, r_rden) if h % 2 == 0 else (rdenB, r_rdenB)
                for bk0 in sorted(set(banks)):
                    rs_ = [r for r in range(4) if banks[r] == bk0]
                    nr = len(rs_)
                    V(lambda e, bk0=bk0, rs_=rs_, nr=nr, rd=rd: e.reciprocal(
                        out=rd[:, rs_[0]:rs_[0] + nr],
                        in_=psO[bk0][:, 0:nr * DA].rearrange("p (r c) -> p r c", r=nr)[:, :, Dh:DA]),
                      [rpsO[bk0]], [r_rd])
                    V(lambda e, bk0=bk0, rs_=rs_, nr=nr, h=h: e.tensor_tensor(
                        out=sz[:, rs_[0]:rs_[0] + nr, h * Dh:(h + 1) * Dh],
                        in0=psO[bk0][:, 0:nr * DA].rearrange("p (r c) -> p r c", r=nr)[:, :, 0:Dh],
                        in1=sz[:, rs_[0]:rs_[0] + nr, h * Dh:(h + 1) * Dh], op=ALU.mult),
                      [rpsO[bk0]] + [rsz[r] for r in rs_], [rsz[r] for r in rs_])
                G(lambda e, h=h, rd=rd: e.tensor_tensor(out=y_tok[:, :, h * Dh:(h + 1) * Dh], in0=sz[:, :, h * Dh:(h + 1) * Dh],
                                                        in1=rd[:, 0:4].unsqueeze(2).to_broadcast([128, 4, Dh]), op=ALU.mult),
                  list(rsz) + [r_rd], list(ry))

        for li, l in enumerate(layers):
            last_layer = (li == len(layers) - 1)
            DMA("sync", lambda e, l=l: e.dma_start(out=g_bcv, in_=ng_d[l:l + 1, :].partition_broadcast(128)), w=[rg_bc])
            for dst, src in ((gq_bc, gq_d), (gk_bc, gk_d), (go_bc, go_d), (gmq_bc, gmq_d), (gmk_bc, gmk_d)):
                DMA("sync", lambda e, l=l, dst=dst, src=src: e.dma_start(out=dst[:], in_=src[l:l + 1, :].partition_broadcast(128)),
                    w=[rgains])
            for i in range(NT):
                if li == 0:
                    DMA(("scalar" if (XQ and i % 2 == 1) else "sync"), lambda e, i=i: e.dma_start(out=x_tok[:, i, :], in_=x_d[i * 128:(i + 1) * 128, :]), w=[rx[i]])
                rms_to_T(x_tok[:, i, :], rx[i], g_bcv, rg_bc, hT, rhT[i], i * 128, ss16, r_ss16, t16, r_t16,
                         rstd16, r_rstd16, i)
            glist = [g for g in ALL_GROUPS if g in groups]
            for gi, gname in enumerate(glist):
                last = last_layer and gi == len(glist) - 1
                kind = gname[0]
                g = int(gname[1])
                if kind == "A":
                    w1 = nxt("w", 3); w2 = nxt("w", 3)
                    load_w(wb[w1][:, :, 0:256], rwb[w1][0], w_in_cols(l, 512 + 256 * g, 256))
                    load_w(wb[w1][:, :, 256:512], rwb[w1][1], w_in_cols(l, 1024 + 256 * g, 256))
                    load_w(wb[w2][:, :, 0:256], rwb[w2][0], w_in_cols(l, 256 * g, 256))
                    load_w(wb[w2][:, :, 256:512], rwb[w2][1], w_in_cols(l, 3584 + 256 * g, 256))
                    load_w(wo[:], rwo, wout_d[l, 256 * g:256 * g + 256, :].rearrange("(c p) n -> p c n", p=128))
                    barrier()
                    G(lambda e: e.memset(v_aug[:, :, :, 64:65], 1.0), w=rv)
                    V(lambda e: e.memset(psG[:, 0:16], 0.0), w=[rpsG])
                    for i in range(NT):
                        n_blk = i // 2
                        (tA, rtA), (tB, rtB), (FO, rFO), (tR, rtR) = SC[(i % 2) * PAR1]
                        ka, rka = AUG[(i % 2) * PAR1]
                        pa, rpa = proj(i * 128, rhT[i], wb[w1], rwb[w1], wide=bool(WIDE))
                        A(lambda e, i=i, pa=pa: e.copy(out=v_aug[:, i, :, 0:64],
                                                       in_=pa[:, 256:512].rearrange("p (h d) -> p h d", h=4)),
                          [rpa], [rv[i]])
                        headnorm(pa[:, 0:256], rpa, 4, 64, gk_bc[:], FO, rFO, tA, rtA, tB, rtB)
                        rope(FO, rFO, i, tR, rtR)
                        G(lambda e, ka=ka, FO=FO: e.tensor_copy(out=ka[:, :, 0:64], in_=FO[:].rearrange("p (h d) -> p h d", h=4)),
                          [rFO], [rka])
                        G(lambda e, ka=ka: e.memset(ka[:, :, 64:72], 0.0), w=[rka])
                        G(lambda e, ka=ka, n_blk=n_blk: e.memset(ka[:, :, 64 + n_blk:65 + n_blk], 1.0), w=[rka])
                        for pp in range(2):
                            T(lambda e, pp=pp, n_blk=n_blk, FO=FO: e.matmul(psG[:, pp * 8 + n_blk:pp * 8 + n_blk + 1],
                                                                            lhsT=FO[:, pp * 128:(pp + 1) * 128], rhs=cst[:, 514:515],
                                                                            start=False, stop=False, skip_group_check=True),
                              [rFO, rcst], [rpsG])
                        for h in range(4):
                            T(lambda e, h=h, ka=ka: e.transpose(out=psT[0:72, h * 128:(h + 1) * 128], in_=ka[:, h, :], identity=idb[:]),
                              [rka, ridb], [rpsT])
                        src3 = psT[0:72, 0:512].rearrange("p (h t) -> p h t", h=4)
                        evac(lambda e, i=i, src3=src3: e.tensor_copy(out=kT[0:72, :, i * 128:(i + 1) * 128], in_=src3),
                             lambda e, i=i, src3=src3: e.copy(out=kT[0:72, :, i * 128:(i + 1) * 128], in_=src3),
                             [rpsT], [rkT[i]])
                    G(lambda e: e.memset(kmT32[:], 0.0), w=[rkm])
                    A(lambda e: e.copy(out=kmT32[0:64, :, 0, :], in_=psG[0:64, 0:16].rearrange("p (a n) -> p a n", a=2)), [rpsG], [rkm])
                    A(lambda e: e.copy(out=kmT32[64:128, :, 1, :], in_=psG[64:128, 0:16].rearrange("p (a n) -> p a n", a=2)), [rpsG], [rkm])
                    for Q in range(4):
                        qT3 = qTs[Q % 2][:, 0:2048].rearrange("p (h t) -> p h t", h=4)
                        rqT = rqTs[Q % 2]
                        sz = szs[Q % 2]; rsz = rszs[Q % 2]
                        for r in range(4):
                            i = 4 * Q + r
                            own = i // 2
                            P.phase = "chain"
                            par = (i % 2) * PAR2 * (1 if (own < 4 or GPAR) else 0)
                            (tA, rtA), (tB, rtB), (FO, rFO), (tR, rtR) = SC[par]
                            qa, rqa = AUG[par]
                            pa, rpa = proj(i * 128, rhT[i], wb[w2], rwb[w2])
                            fin = silu_ps(sz[:, r, :], rsz[r], pa[:, 256:512], rpa, defer=True)
                            headnorm(pa[:, 0:256], rpa, 4, 64, gq_bc[:], FO, rFO, tA, rtA, tB, rtB)
                            fin()
                            rope(FO, rFO, i, tR, rtR)
                            G(lambda e, qa=qa, FO=FO: e.tensor_copy(out=qa[:, :, 0:64], in_=FO[:].rearrange("p (h d) -> p h d", h=4)),
                              [rFO], [rqa])
                            if own >= 4:
                                P.phase = "gate"
                                for pp in range(2):
                                    T(lambda e, pp=pp, FO=FO: e.matmul(psG[:, pp * 128:(pp + 1) * 128],
                                                                       lhsT=FO[:, pp * 128:(pp + 1) * 128], rhs=ident32,
                                                                       start=True, stop=True),
                                      [rFO, rcst], [rpsG])
                                A(lambda e: e.copy(out=Fp[5][:], in_=psG[:, 0:256]), [rpsG], [rF[5]])
                                for h in range(4):
                                    T(lambda e, h=h, own=own: e.matmul(
                                        psG[:, 256 + h * 8:256 + h * 8 + own],
                                        lhsT=Fp[5][:, (h // 2) * 128:(h // 2) * 128 + 128],
                                        rhs=kmT32[:, h // 2, h % 2, 0:own], start=True, stop=True),
                                      [rF[5], rkm], [rpsG])
                                V(lambda e: e.memset(gm[:], -1.0e30), w=[rgm])
                                V(lambda e, own=own: e.tensor_copy(
                                    out=gm[:, :, 0:own], in_=psG[:, 256:288].rearrange("p (h n) -> p h n", h=4)[:, :, 0:own]),
                                  [rpsG], [rgm])
                                for h in range(4):
                                    V(lambda e, h=h: e.max(out=top8[:, h, :], in_=gm[:, h, :]), [rgm], [rtop8])
                                V(lambda e: e.tensor_tensor(out=selb[:], in0=gm[:], in1=top8[:, :, 2:3].to_broadcast([128, 4, 8]),
                                                            op=ALU.is_ge), [rgm, rtop8], [rselb])
                                V(lambda e, qa=qa: e.tensor_scalar(out=qa[:, :, 64:72], in0=selb[:], scalar1=30000.0,
                                                                   scalar2=-30000.0, op0=ALU.mult, op1=ALU.add), [rselb], [rqa])
                                V(lambda e, qa=qa, own=own: e.memset(qa[:, :, 64 + own:65 + own], 0.0), w=[rqa])
                            else:
                                G(lambda e, qa=qa: e.memset(qa[:, :, 64:72], 0.0), w=[rqa])
                            P.phase = None
                            for h in range(4):
                                T(lambda e, h=h, qa=qa: e.transpose(out=psT[0:72, h * 128:(h + 1) * 128], in_=qa[:, h, :],
                                                                    identity=idb[:]), [rqa, ridb], [rpsT])
                            src3 = psT[0:72, 0:512].rearrange("p (h t) -> p h t", h=4)
                            evac(lambda e, r=r, src3=src3, qT3=qT3: e.tensor_copy(out=qT3[0:72, :, r * 128:(r + 1) * 128], in_=src3),
                                 lambda e, r=r, src3=src3, qT3=qT3: e.copy(out=qT3[0:72, :, r * 128:(r + 1) * 128], in_=src3),
                                 [rpsT], [rqT])
                        attn_chunk(Q, 4, 64, 72, 0.125, lambda Q: list(range(4 * Q + 4)), True,
                                   lambda h, kt: kT[0:72, h, kt * 128:(kt + 1) * 128], lambda kt: rkT[kt],
                                   lambda h, kt: v_aug[:, kt, h, :], lambda kt: rv[kt],
                                   lambda h, qT3=qT3: qT3[0:72, h, :], rqT, sz, rsz)
                        for r in range(4):
                            outproj_tile(4 * Q + r, r, last, obanks=([(psO[0], rpsO[0]), (psO[1], rpsO[1])] if OPB else None))
                elif kind == "M":
                    w1 = nxt("w", 3); w2 = nxt("w", 3)
                    wkvv = wkv_d[l].rearrange("(c p) n -> p c n", p=128)
                    load_w(wb[w1][:, :, 0:256], rwb[w1][0], wkvv[:, :, 256 * g:256 * g + 256])
                    load_w(wb[w1][:, :, 256:512], rwb[w1][1], wkvv[:, :, 512 + 256 * g:512 + 256 * g + 256])
                    load_w(wb[w2][:, :, 0:256], rwb[w2][0], w_in_cols(l, 3072 + 256 * g, 256))
                    load_w(wb[w2][:, :, 256:512], rwb[w2][1], w_in_cols(l, 4608 + 256 * g, 256))
                    load_w(wo[:], rwo, wout_d[l, 1024 + 256 * g:1024 + 256 * g + 256, :].rearrange("(c p) n -> p c n", p=128))
                    if g == 0 or ("M0" not in groups):
                        barrier()
                        DMA("sync", lambda e, l=l: e.dma_start(out=g_bcv, in_=mng_d[l:l + 1, :].partition_broadcast(128)),
                            w=[rg_bc])
                        for mt in range(2):
                            DMA("sync", lambda e, mt=mt: e.dma_start(out=stage, in_=mem_d[mt * 128:(mt + 1) * 128, :]),
                                w=[rstage])
                            rms_to_T(stage, rstage, g_bcv, rg_bc, memT, rmemT, mt * 128, ss16, r_ss16, t16, r_t16,
                                     rstd16, r_rstd16, mt)
                    for mt in range(2):
                        (tA, rtA), (tB, rtB), (FO, rFO), (tR, rtR) = SC[mt % 2]
                        pa, rpa = proj(mt * 128, rmemT, wb[w1], rwb[w1], lhsT_src=memT)
                        A(lambda e, mt=mt, pa=pa: e.copy(out=vm_aug[:, mt, :, 0:128],
                                                         in_=pa[:, 256:512].rearrange("p (h d) -> p h d", h=2)),
                          [rpa], [rvm])
                        headnorm(pa[:, 0:256], rpa, 2, 128, gmk_bc[:], FO, rFO, tA, rtA, tB, rtB)
                        G(lambda e, mt=mt, FO=FO: e.tensor_copy(out=Bp[mt % 2][:], in_=FO[:]), [rFO], [rB[mt % 2]])
                        for hh in range(2):
                            T(lambda e, hh=hh, mt=mt: e.transpose(out=psT[:, hh * 128:(hh + 1) * 128],
                                                                  in_=Bp[mt % 2][:, hh * 128:(hh + 1) * 128],
                                                                  identity=idb[:]), [rB[mt % 2], ridb], [rpsT])
                        src3 = psT[:, 0:256].rearrange("p (h t) -> p h t", h=2)
                        evac(lambda e, mt=mt, src3=src3: e.tensor_copy(out=kmT[:, :, mt * 128:(mt + 1) * 128], in_=src3),
                             lambda e, mt=mt, src3=src3: e.copy(out=kmT[:, :, mt * 128:(mt + 1) * 128], in_=src3),
                             [rpsT], [rkmT])
                    for Q in range(4):
                        qm3 = qTs[Q % 2][:, 0:1024].rearrange("p (h t) -> p h t", h=2)
                        rqT = rqTs[Q % 2]
                        sz = szs[Q % 2]; rsz = rszs[Q % 2]
                        for r in range(4):
                            i = 4 * Q + r
                            (tA, rtA), (tB, rtB), (FO, rFO), (tR, rtR) = SC[i % 2]
                            pa, rpa = proj(i * 128, rhT[i], wb[w2], rwb[w2], wide=(3 if WIDEM else False))
                            fin = silu_ps(sz[:, r, :], rsz[r], pa[:, 256:512], rpa, defer=True)
                            headnorm(pa[:, 0:256], rpa, 2, 128, gmq_bc[:], FO, rFO, tA, rtA, tB, rtB)
                            fin()
                            G(lambda e, i=i, FO=FO: e.tensor_copy(out=Bp[i % 2][:], in_=FO[:]), [rFO], [rB[i % 2]])
                            for hh in range(2):
                                T(lambda e, hh=hh, i=i: e.transpose(out=psT[:, hh * 128:(hh + 1) * 128],
                                                                    in_=Bp[i % 2][:, hh * 128:(hh + 1) * 128], identity=idb[:]),
                                  [rB[i % 2], ridb], [rpsT])
                            src3 = psT[:, 0:256].rearrange("p (h t) -> p h t", h=2)
                            evac(lambda e, r=r, src3=src3, qm3=qm3: e.tensor_copy(out=qm3[:, :, r * 128:(r + 1) * 128], in_=src3),
                                 lambda e, r=r, src3=src3, qm3=qm3: e.copy(out=qm3[:, :, r * 128:(r + 1) * 128], in_=src3),
                                 [rpsT], [rqT])
                        attn_chunk(Q, 2, 128, 128, float(128 ** -0.5), lambda Q: [0, 1], False,
                                   lambda h, kt: kmT[:, h, kt * 128:(kt + 1) * 128], lambda kt: rkmT,
                                   lambda h, kt: vm_aug[:, kt, h, :], lambda kt: rvm,
                                   lambda h, qm3=qm3: qm3[:, h, :], rqT, sz, rsz, nsb=(2 if WIDEM else None))
                        for r in range(4):
                            outproj_tile(4 * Q + r, r, last, obanks=([(psO[0], rpsO[0]), (psO[1], rpsO[1])] if OPB else None))
                else:
                    w1 = nxt("w", 3); w2 = nxt("w", 3)
                    load_w(wb[w1][:, :, 0:256], rwb[w1][0], w_in_cols(l, 1536 + 256 * g, 256))
                    load_w(wb[w1][:, :, 256:512], rwb[w1][1], w_in_cols(l, 2048 + 256 * g, 256))
                    load_w(wb[w2][:, :, 0:256], rwb[w2][0], w_in_cols(l, 2560 + 256 * g, 256))
                    load_w(wb[w2][:, :, 256:512], rwb[w2][1], w_in_cols(l, 4096 + 256 * g, 256))
                    load_w(wo[:], rwo, wout_d[l, 512 + 256 * g:512 + 256 * g + 256, :].rearrange("(c p) n -> p c n", p=128))
                    barrier()
                    if l != 0:
                        DMA("sync", lambda e, g=g: e.dma_start(out=lb_g, in_=lbl_d[1:2, 256 * g:256 * g + 256].partition_broadcast(128)), w=[rlb])
                        DMA("sync", lambda e, g=g: e.dma_start(out=oml_g, in_=lbl_d[0:1, 256 * g:256 * g + 256].partition_broadcast(128)), w=[rlb])
                        V(lambda e: e.tensor_tensor(out=lb_g, in0=lb_g, in1=oml_g, op=ALU.subtract), [rlb], [rlb])
                        A(lambda e: e.activation(out=lb_g, in_=lb_g, func=AF.Exp, scale=-1.0), [rlb], [rlb])
                        A(lambda e: e.activation(out=lb_g, in_=lb_g, func=AF.Ln, bias=1.0), [rlb], [rlb])
                        A(lambda e: e.activation(out=lb_g, in_=lb_g, func=AF.Exp, scale=-1.0), [rlb], [rlb])
                        V(lambda e: e.tensor_scalar(out=oml_g, in0=lb_g, scalar1=-1.0, scalar2=1.0, op0=ALU.mult, op1=ALU.add),
                          [rlb], [rlb])
                    for hh in range(2):
                        G(lambda e, hh=hh: e.memset(S32[:, hh, :], 0.0), w=[rS32[hh]])
                        G(lambda e, hh=hh: e.memset(Sbf[:, 0, hh, :], 0.0), w=[rSbf[0][hh]])
                    Tri32 = cst[:, 256:384]
                    TriE32 = cst[:, 384:512]
                    for i in range(NT):
                        Fs, rFs, Bs, rBs = HSETS[i % 2]
                        sl = i % 4
                        sz = szs[(i // 4) % 2]; rsz = rszs[(i // 4) % 2]
                        pq, rpq = proj(i * 128, rhT[i], wb[w1], rwb[w1])
                        finq = silu_ps(Fs[0], rFs[0], pq[:, 0:256], rpq, defer=True)
                        A(lambda e, pq=pq, Fs=Fs: e.activation(out=Fs[1], in_=pq[:, 256:512], func=AF.Exp, scale=-1.0), [rpq], [rFs[1]])
                        A(lambda e, Fs=Fs: e.activation(out=Fs[1], in_=Fs[1], func=AF.Ln, bias=1.0), [rFs[1]], [rFs[1]])
                        finq()
                        pi_, rpi = proj(i * 128, rhT[i], wb[w2], rwb[w2])
                        silu_ps(sz[:, sl, :], rsz[sl], pi_[:, 256:512], rpi)
                        V(lambda e, pi_=pi_, Bs=Bs: e.tensor_copy(out=Bs[0], in_=pi_[:, 0:256]), [rpi], [rBs[0]])
                        if l == 0:
                            A(lambda e, Fs=Fs: e.activation(out=Fs[2], in_=Fs[1], func=AF.Copy, scale=-1.0), [rFs[1]], [rFs[2]])
                            A(lambda e, Fs=Fs: e.activation(out=Fs[1], in_=Fs[1], func=AF.Exp, scale=-1.0), [rFs[1]], [rFs[1]])
                        else:
                            A(lambda e, Fs=Fs: e.activation(out=Fs[1], in_=Fs[1], func=AF.Exp, scale=-1.0), [rFs[1]], [rFs[1]])
                            V(lambda e, g=g, Fs=Fs: e.tensor_tensor(out=Fs[1], in0=Fs[1], in1=oml_g,
                                                                    op=ALU.mult), [rFs[1], rlb], [rFs[1]])
                            V(lambda e, g=g, Fs=Fs: e.tensor_tensor(out=Fs[1], in0=Fs[1], in1=lb_g,
                                                                    op=ALU.add), [rFs[1], rlb], [rFs[1]])
                            A(lambda e, Fs=Fs: e.activation(out=Fs[2], in_=Fs[1], func=AF.Ln), [rFs[1]], [rFs[2]])
                        (G if GOFF else V)(lambda e, Fs=Fs: e.tensor_scalar(out=Fs[3], in0=Fs[1], scalar1=-1.0, scalar2=1.0, op0=ALU.mult,
                                                                            op1=ALU.add), [rFs[1]], [rFs[3]])
                        T(lambda e, Fs=Fs: e.matmul(psG[:, 0:256], lhsT=Tri32, rhs=Fs[2], start=True, stop=True),
                          [rcst, rFs[2]], [rpsG])
                        T(lambda e, Fs=Fs: e.matmul(psG[:, 256:512], lhsT=TriE32, rhs=Fs[2], start=True, stop=True),
                          [rcst, rFs[2]], [rpsG])
                        for hh in range(2):
                            T(lambda e, hh=hh, Fs=Fs: e.matmul(psS[1][:, 256 + 2 * hh:256 + 2 * hh + 2],
                                                               lhsT=Fs[2][:, hh * 128:(hh + 1) * 128],
                                                               rhs=cst[:, 512:514], start=True, stop=True), [rFs[2], rcst], [rpsS[1]])
                        dsl = dec8[:, 4 * (i % 2):4 * (i % 2) + 4]
                        rds = r_dec8[i % 2]
                        A(lambda e, dsl=dsl: e.activation(out=dsl, in_=psS[1][:, 256:260], func=AF.Exp), [rpsS[1]], [rds])
                        A(lambda e, Fs=Fs: e.activation(out=Fs[4], in_=psG[:, 0:256], func=AF.Exp), [rpsG], [rFs[4]])
                        A(lambda e, Fs=Fs: e.activation(out=Fs[5], in_=psG[:, 0:256], func=AF.Exp, scale=-1.0), [rpsG], [rFs[5]])
                        A(lambda e, Fs=Fs: e.activation(out=Fs[6], in_=psG[:, 256:512], func=AF.Exp), [rpsG], [rFs[6]])
                        V(lambda e, Fs=Fs, Bs=Bs: e.tensor_tensor(out=Bs[1], in0=Fs[0], in1=Fs[4], op=ALU.mult), [rFs[0], rFs[4]], [rBs[1]])
                        G(lambda e, Fs=Fs, Bs=Bs: e.tensor_tensor(out=Bs[2], in0=Fs[3], in1=Fs[5], op=ALU.mult), [rFs[3], rFs[5]], [rBs[2]])
                        G(lambda e, Fs=Fs, Bs=Bs: e.tensor_tensor(out=Bs[3], in0=Fs[3], in1=Fs[6], op=ALU.mult), [rFs[3], rFs[6]], [rBs[3]])
                        for hh in range(2):
                            T(lambda e, hh=hh, Bs=Bs: e.transpose(out=psT[:, hh * 128:(hh + 1) * 128], in_=Bs[1][:, hh * 128:(hh + 1) * 128],
                                                                  identity=idb[:]), [rBs[1], ridb], [rpsT])
                            T(lambda e, hh=hh, Bs=Bs: e.transpose(out=psT[:, 256 + hh * 128:256 + (hh + 1) * 128],
                                                                  in_=Bs[2][:, hh * 128:(hh + 1) * 128], identity=idb[:]),
                              [rBs[2], ridb], [rpsT])
                        pq3 = psT[:, 0:256].rearrange("p (h t) -> p h t", h=2)
                        A(lambda e, Bs=Bs: e.copy(out=Bs[4], in_=psT[:, 0:256]), [rpsT], [rBs[4]])
                        A(lambda e, pq3=pq3: e.copy(out=qTA[:, :, 0:64], in_=pq3[:, :, 0:64]), [rpsT], [rqTA])
                        V(lambda e, Bs=Bs: e.tensor_copy(out=Bs[5], in_=psT[:, 256:512]), [rpsT], [rBs[5]])
                        V(lambda e, pq3=pq3: e.tensor_copy(out=qTB[:, :, 64:128], in_=pq3[:, :, 64:128]), [rpsT], [rqTB])
                        cur = i % 2
                        nxtb = 1 - cur
                        for hh in range(2):
                            hs = slice(hh * 128, (hh + 1) * 128)
                            T(lambda e, hs=hs, Bs=Bs: e.matmul(psS[1][:, hs], lhsT=Bs[5][:, hs], rhs=Bs[4][:, hs], start=True, stop=True),
                              [rBs[5], rBs[4]], [rpsS[1]])
                        V(lambda e, Bs=Bs: e.tensor_tensor(out=Bs[6].rearrange("p (h t) -> p h t", h=2),
                                                           in0=psS[1][:, 0:256].rearrange("p (h t) -> p h t", h=2),
                                                           in1=hgmb[:].unsqueeze(1).to_broadcast([128, 2, 128]), op=ALU.mult),
                          [rpsS[1], rhgmb], [rBs[6]])
                        for hh in range(2):
                            hs = slice(hh * 128, (hh + 1) * 128)
                            T(lambda e, hs=hs, hh=hh, Bs=Bs: e.matmul(psO[hh][:, 0:128], lhsT=Bs[6][:, hs], rhs=Bs[0][:, hs],
                                                                      start=True, stop=False), [rBs[6], rBs[0]], [rpsO[hh]])
                            T(lambda e, hh=hh, cur=cur: e.matmul(psO[hh][:, 0:128], lhsT=qTA[:, hh, :], rhs=Sbf[:, cur, hh, :],
                                                                 start=False, stop=False), [rqTA, rSbf[cur][hh]], [rpsO[hh]])
                        for hh in range(2):
                            hs = slice(hh * 128, (hh + 1) * 128)
                            T(lambda e, hs=hs, Bs=Bs: e.matmul(psS[0][:, hs], lhsT=Bs[3][0:64, hs], rhs=Bs[0][0:64, hs],
                                                               start=True, stop=True), [rBs[3], rBs[0]], [rpsS[0], rrow])
                        for hh in range(2):
                            hs = slice(hh * 128, (hh + 1) * 128)
                            V(lambda e, hs=hs, hh=hh, dsl=dsl: e.scalar_tensor_tensor(out=S32[:, hh, :], in0=S32[:, hh, :],
                                                                                      scalar=dsl[:, 2 * hh:2 * hh + 1], in1=psS[0][:, hs],
                                                                                      op0=ALU.mult, op1=ALU.add),
                              [rS32[hh], rds, rpsS[0]], [rS32[hh]])
                            G(lambda e, hh=hh, nxtb=nxtb: e.tensor_copy(out=Sbf[:, nxtb, hh, :], in_=S32[:, hh, :]),
                              [rS32[hh]], [rSbf[nxtb][hh]])
                        for hh in range(2):
                            T(lambda e, hh=hh, nxtb=nxtb: e.matmul(psO[hh][:, 0:128], lhsT=qTB[:, hh, :], rhs=Sbf[:, nxtb, hh, :],
                                                                   start=False, stop=True), [rqTB, rSbf[nxtb][hh]], [rpsO[hh], rrow])
                        for hh in range(2):
                            hs = slice(hh * 128, (hh + 1) * 128)
                            T(lambda e, hs=hs, Bs=Bs: e.matmul(psS[0][:, hs], lhsT=Bs[3][64:128, hs], rhs=Bs[0][64:128, hs],
                                                               start=True, stop=True), [rBs[3], rBs[0]], [rpsS[0], rrow])
                        for hh in range(2):
                            hs = slice(hh * 128, (hh + 1) * 128)
                            V(lambda e, hs=hs, hh=hh, dsl=dsl: e.scalar_tensor_tensor(out=S32[:, hh, :], in0=S32[:, hh, :],
                                                                                      scalar=dsl[:, 2 * hh + 1:2 * hh + 2], in1=psS[0][:, hs],
                                                                                      op0=ALU.mult, op1=ALU.add),
                              [rS32[hh], rds, rpsS[0]], [rS32[hh]])
                        for hh in range(2):
                            G(lambda e, hh=hh, nxtb=nxtb: e.tensor_copy(out=Sbf[:, nxtb, hh, :], in_=S32[:, hh, :]),
                              [rS32[hh]], [rSbf[nxtb][hh]])
                        ssl = sso4[:, 2 * (i % 2):2 * (i % 2) + 2]
                        r_sso = r_sso2[i % 2]
                        for hh in range(2):
                            A(lambda e, hh=hh, Fs=Fs, ssl=ssl: e.activation(out=Fs[7][:, 0:128], in_=psO[hh][:, 0:128], func=AF.Square,
                                                                            accum_out=ssl[:, hh:hh + 1]), [rpsO[hh]], [rFs[7], r_sso])
                        A(lambda e, ssl=ssl: e.activation(out=ssl, in_=ssl, func=AF.Ln, scale=1.0 / 128, bias=EPS), [r_sso], [r_sso])
                        A(lambda e, ssl=ssl: e.activation(out=ssl, in_=ssl, func=AF.Exp, scale=-0.5), [r_sso], [r_sso])
                        for hh in range(2):
                            hs = slice(hh * 128, (hh + 1) * 128)
                            V(lambda e, hh=hh, hs=hs, Fs=Fs, ssl=ssl: e.scalar_tensor_tensor(out=Fs[8][:, hs], in0=psO[hh][:, 0:128],
                                                                                             scalar=ssl[:, hh:hh + 1], in1=go_bc[:],
                                                                                             op0=ALU.mult, op1=ALU.mult),
                              [rpsO[hh], r_sso, rgains], [rFs[8]])
                        G(lambda e, Fs=Fs, sl=sl, sz=sz: e.tensor_tensor(out=y_tok[:, sl, :], in0=Fs[8], in1=sz[:, sl, :], op=ALU.mult),
                          [rFs[8], rsz[sl]], [ry[sl]])
                        outproj_tile(i, sl, last, obanks=[(psS[0], rpsS[0]), (psS[0], rpsS[0])])
            if not glist and last_layer:
                for i in range(NT):
                    out_dmas.append(DMA("sync", lambda e, i=i: e.dma_start(out=out_d[i * 128:(i + 1) * 128, :], in_=x_tok[:, i, :]),
                                        r=[rx[i]]))
        if SCHED:
            if SCHED2:
                P.schedule2(SDELTA)
            else:
                P.schedule()
        P.emit(st, out_dmas)
    build_nc.stats = P.stats
    return nc


_CACHE = {}


def _get_nc(layers, groups):
    key = (tuple(layers), tuple(groups))
    if key not in _CACHE:
        _CACHE[key] = build_nc(layers, groups)
    return _CACHE[key]


def run(inputs, layers=(0, 1), groups=ALL_GROUPS, cores=8):
    nc = _get_nc(layers, groups)
    f = lambda a: np.ascontiguousarray(np.asarray(a))
    cst = make_consts()
    shared = {k: f(inputs[k]).astype(np.float32, copy=False) for k in
              ("norm_g", "w_in", "w_out", "moba_q_norm", "moba_k_norm", "hgrn_lb_logits", "hgrn_o_norm",
               "mem_norm_g", "w_mem_kv", "mem_q_norm", "mem_k_norm")}
    x = f(inputs["x"]); mem = f(inputs["mem"]); pos = f(inputs["positions"]).astype(np.int32, copy=False)
    in_maps = []
    for b in range(cores):
        m = dict(shared)
        m["x"] = x[b]
        m["mem"] = mem[b]
        m["pos"] = pos[b].reshape(16, 128)
        m["cst"] = cst
        in_maps.append(m)
    res = run_bass_kernel_spmd(nc, in_maps, core_ids=list(range(cores)))
    return np.stack([np.asarray(r["out"]) for r in res.results], axis=0)


def kernel(x, mem, positions, norm_g, w_in, w_out, moba_q_norm, moba_k_norm, hgrn_lb_logits,
           hgrn_o_norm, mem_norm_g, w_mem_kv, mem_q_norm, mem_k_norm):
    inputs = dict(x=x, mem=mem, positions=positions, norm_g=norm_g, w_in=w_in, w_out=w_out,
                  moba_q_norm=moba_q_norm, moba_k_norm=moba_k_norm, hgrn_lb_logits=hgrn_lb_logits,
                  hgrn_o_norm=hgrn_o_norm, mem_norm_g=mem_norm_g, w_mem_kv=w_mem_kv,
                  mem_q_norm=mem_q_norm, mem_k_norm=mem_k_norm)
    return run(inputs).astype(np.float32, copy=False)
```

```python
import numpy as np
from contextlib import ExitStack
import concourse.bass as bass
import concourse.mybir as mybir
from concourse.bass_utils import run_bass_kernel_spmd

F32 = mybir.dt.float32
BF16 = mybir.dt.bfloat16
I32 = mybir.dt.int32
AF = mybir.ActivationFunctionType
ALU = mybir.AluOpType
AX = mybir.AxisListType

S = 2048
D = 1024
NT = 16
EPS = 1e-6
NCST = 576
import os as _os0
ALL_GROUPS = tuple(_os0.environ.get("ORDER", "A0,A1,H0,H1,M0,M1").split(","))
import os as _os
SCHED = _os.environ.get("SCHED", "1") == "1"
PAR1 = int(_os.environ.get("PAR1", "1"))
PAR2 = int(_os.environ.get("PAR2", "1"))
GPAR = int(_os.environ.get("GPAR", "1"))
PS3 = int(_os.environ.get("PS3", "3"))
MASKV = int(_os.environ.get("MASKV", "1"))
OPB = int(_os.environ.get("OPB", "1"))
SCHED2 = int(_os.environ.get("SCHED2", "1"))
SDELTA = float(_os.environ.get("SDELTA", "120"))
LATX = float(_os.environ.get("LATX", "180"))
PEK = float(_os.environ.get("PEK", "0.65"))
ACTK = float(_os.environ.get("ACTK", "1.0"))
DVEK = float(_os.environ.get("DVEK", "1.0"))
POOLK = float(_os.environ.get("POOLK", "1.0"))
LATS = float(_os.environ.get("LATS", "60"))
XQ = int(_os.environ.get("XQ", "0"))
GOFF = int(_os.environ.get("GOFF", "0"))
TRANS = int(_os.environ.get("TRANS", "1"))
PRUNE = int(_os.environ.get("PRUNE", "1"))
WIDE = int(_os.environ.get("WIDE", "1"))
WIDEM = int(_os.environ.get("WIDEM", "1"))


class Res:
    __slots__ = ("name", "w", "r", "excl")

    def __init__(self, name, excl=False):
        self.name = name
        self.w = None
        self.r = []
        self.excl = excl


class _Rec:
    def __init__(self):
        self.name = None
        self.args = ()
        self.kw = {}

    def __getattr__(self, name):
        def f(*a, **k):
            self.name, self.args, self.kw = name, a, k
            return self
        return f


def _free_size(ap):
    n = 1
    for d in list(ap.shape)[1:]:
        n *= int(d)
    return n


class Op:
    __slots__ = ("eng", "fn", "deps", "sdeps", "sig", "idx", "dma", "sem", "val", "i", "cost", "start")

    def __init__(self, eng, fn, deps, sdeps, dma):
        self.eng = eng
        self.fn = fn
        self.deps = deps
        self.sdeps = sdeps
        self.sig = False
        self.idx = 0
        self.dma = dma
        self.sem = None
        self.val = 0
        self.i = 0
        self.start = 0.0
        rec = _Rec()
        fn(rec)
        out = rec.kw.get("out", rec.args[0] if rec.args else None)
        n = _free_size(out) if out is not None else 64
        if dma:
            c = 2000.0 + n * int(out.shape[0]) * 4 / 120.0
        elif eng == "tensor":
            if rec.name == "transpose":
                c = 110.0
            else:
                lhsT = rec.kw.get("lhsT")
                f32 = lhsT is not None and lhsT.dtype == F32
                c = PEK * (64.0 + max(n, 64) / 2.0) * (4.0 if f32 else 1.0)
        elif eng == "scalar":
            c = ACTK * (200.0 + n / 1.2)
        elif eng == "vector":
            c = DVEK * (120.0 + n / 0.96 * (8.0 if rec.name == "reciprocal" else 1.0))
        else:
            c = POOLK * (300.0 + n / 0.5)
        self.cost = c


class Prog:
    ENGS = ["tensor", "vector", "scalar", "gpsimd", "sync"]

    def __init__(self, nc):
        self.nc = nc
        self.ops = []

    phase = None
    tok = None
    tokset = ()

    def op(self, eng, fn, reads=(), writes=(), dma=False):
        if self.phase == "gate" and self.tok is not None:
            writes = list(writes) + [self.tok]
        elif self.phase == "chain" and eng in self.tokset:
            reads = list(reads) + [self.tok]
        deps, sdeps = {}, {}

        def add(d):
            if d.dma or dma or d.eng != eng or eng != "tensor":
                deps[id(d)] = d
            else:
                sdeps[id(d)] = d
        for r in reads:
            if r.w is not None:
                add(r.w)
            if r.excl:
                for d in r.r:
                    if d.eng != eng:
                        add(d)
        for w in writes:
            if w.w is not None:
                add(w.w)
            for d in w.r:
                add(d)
        o = Op(eng, fn, list(deps.values()), list(sdeps.values()), dma)
        for r in reads:
            r.r.append(o)
        for w in writes:
            w.w = o
            w.r = []
        self.ops.append(o)
        return o

    def schedule(self):
        import heapq
        ops = self.ops
        for i, o in enumerate(ops):
            o.i = i
        succs = [[] for _ in ops]
        npred = [0] * len(ops)
        for o in ops:
            ds = o.deps + o.sdeps
            npred[o.i] = len(ds)
            for d in ds:
                succs[d.i].append(o)
        ready = [0.0] * len(ops)
        free = {e: 0.0 for e in self.ENGS}
        heap = [(0.0, o.i) for o in ops if npred[o.i] == 0]
        heapq.heapify(heap)
        done = 0
        while heap:
            t, i = heapq.heappop(heap)
            o = ops[i]
            st = max(ready[i], free[o.eng])
            if st > t + 1e-9:
                heapq.heappush(heap, (st, i))
                continue
            o.start = st
            if o.dma:
                free[o.eng] = st + 150.0
            else:
                free[o.eng] = st + o.cost
            fin = st + o.cost
            done += 1
            for sc in succs[i]:
                lat = 60.0 if (sc.eng == o.eng and not o.dma) else 180.0
                if fin + lat > ready[sc.i]:
                    ready[sc.i] = fin + lat
                npred[sc.i] -= 1
                if npred[sc.i] == 0:
                    heapq.heappush(heap, (max(ready[sc.i], free[sc.eng]), sc.i))
        assert done == len(ops), (done, len(ops))
        self.ops = sorted(ops, key=lambda o: (o.start, o.i))
        self.est_ns = max(o.start + o.cost for o in ops)

    def schedule2(self, delta=120.0):
        ops = self.ops
        n = len(ops)
        for i, o in enumerate(ops):
            o.i = i
        succs = [[] for _ in ops]
        npred = [0] * n
        for o in ops:
            ds = o.deps + o.sdeps
            npred[o.i] = len(ds)
            for d in ds:
                succs[d.i].append(o)
        blev = [0.0] * n
        for o in reversed(ops):
            b = 0.0
            for sc in succs[o.i]:
                lat = LATS if (sc.eng == o.eng and not o.dma) else LATX
                v = lat + blev[sc.i]
                if v > b:
                    b = v
            blev[o.i] = b + o.cost
        ready = [0.0] * n
        free = {e: 0.0 for e in self.ENGS}
        rsets = {e: [] for e in self.ENGS}
        for o in ops:
            if npred[o.i] == 0:
                rsets[o.eng].append(o.i)
        done = 0
        while done < n:
            best_e, best_t = None, 1e30
            for e in self.ENGS:
                rs = rsets[e]
                if not rs:
                    continue
                t = min(ready[i] for i in rs)
                if t < free[e]:
                    t = free[e]
                if t < best_t:
                    best_t, best_e = t, e
            e = best_e
            rs = rsets[e]
            lim = best_t + delta
            pick, pb = -1, -1.0
            for i in rs:
                if ready[i] <= lim and blev[i] > pb:
                    pb, pick = blev[i], i
            rs.remove(pick)
            o = ops[pick]
            st = max(ready[pick], free[e])
            o.start = st
            free[e] = st + (150.0 if o.dma else o.cost)
            fin = st + o.cost
            done += 1
            for sc in succs[pick]:
                lat = LATS if (sc.eng == o.eng and not o.dma) else LATX
                if fin + lat > ready[sc.i]:
                    ready[sc.i] = fin + lat
                npred[sc.i] -= 1
                if npred[sc.i] == 0:
                    rsets[sc.eng].append(sc.i)
        self.ops = sorted(ops, key=lambda o: (o.start, o.i))
        self.est_ns = max(o.start + o.cost for o in ops)

    def emit(self, stack, final_deps, ndma_sems=8):
        nc = self.nc
        if PRUNE:
            pos = {id(o): k for k, o in enumerate(self.ops)}
            for o in self.ops:
                best = {}
                keep = []
                for d in o.deps:
                    if d.dma:
                        keep.append(d)
                    elif d.eng not in best or pos[id(d)] > pos[id(best[d.eng])]:
                        best[d.eng] = d
                o.deps = keep + list(best.values())
        for o in self.ops:
            for d in o.deps:
                d.sig = True
        for d in final_deps:
            d.sig = True
        sems = {e: stack.enter_context(nc.semaphore("s_" + e)) for e in self.ENGS}
        cnt = {e: 0 for e in self.ENGS}
        pools, pool_i, pre_wait = {}, {}, {}
        for o in self.ops:
            if o.dma:
                if o.eng not in pools:
                    pools[o.eng] = [[stack.enter_context(nc.semaphore("d_%s_%d" % (o.eng, i))), 0]
                                    for i in range(ndma_sems)]
                    pool_i[o.eng] = 0
                p = pools[o.eng][pool_i[o.eng] % ndma_sems]
                pool_i[o.eng] += 1
                if p[1] > 0:
                    pre_wait[id(o)] = (p[0], p[1])
                p[1] += 16
                o.sem = p[0]
                o.val = p[1]
            elif o.sig:
                cnt[o.eng] += 1
                o.idx = cnt[o.eng]
        per = {e: [o for o in self.ops if o.eng == e] for e in self.ENGS}
        self.stats = {e: len(per[e]) for e in self.ENGS}
        known = {e: {} for e in self.ENGS}
        kn = {}
        plan = {}
        nw = 0

        def semkey(d):
            return (d.sem, d.val) if d.dma else (sems[d.eng], d.idx)

        for o in self.ops:
            kd = known[o.eng]
            ws = []
            for d in o.deps:
                sm, val = semkey(d)
                if kd.get(id(sm), (None, 0))[1] < val:
                    ws.append((sm, val))
                    kd[id(sm)] = (sm, val)
                if TRANS:
                    for k2, (s2, v2) in kn[id(d)].items():
                        if kd.get(k2, (None, 0))[1] < v2:
                            kd[k2] = (s2, v2)
            if o.dma:
                pw = pre_wait.get(id(o))
                if pw and kd.get(id(pw[0]), (None, 0))[1] < pw[1]:
                    ws.append(pw)
                    kd[id(pw[0])] = pw
            plan[id(o)] = ws
            nw += len(ws)
            if o.dma or o.sig:
                mine = dict(kd)
                sm, val = semkey(o)
                mine[id(sm)] = (sm, val)
                kn[id(o)] = mine
        fin_w = []
        kd = known["sync"]
        for d in final_deps:
            sm, val = semkey(d)
            if kd.get(id(sm), (None, 0))[1] < val:
                fin_w.append((sm, val))
                kd[id(sm)] = (sm, val)
        self.stats["waits"] = nw
        self.stats["sigs"] = {e: sum(1 for o in per[e] if o.sig and not o.dma) for e in self.ENGS}
        block = stack.enter_context(nc.Block())

        def mk(e):
            def body(engobj):
                for o in per[e]:
                    for sm, val in plan[id(o)]:
                        engobj.wait_ge(sm, val)
                    if o.dma:
                        o.fn(engobj).then_inc(o.sem, 16)
                    else:
                        ins = o.fn(engobj)
                        if o.sig:
                            ins.then_inc(sems[e], 1)
                if e == "sync":
                    for sm, val in fin_w:
                        engobj.wait_ge(sm, val)
            return body

        block.tensor(mk("tensor"))
        block.vector(mk("vector"))
        block.scalar(mk("scalar"))
        block.gpsimd(mk("gpsimd"))
        block.sync(mk("sync"))


def make_consts():
    c = np.zeros((128, NCST), np.float32)
    i = np.arange(128)
    c[:, 0:128] = np.eye(128)
    c[:, 128:256] = (i[None, :] >= i[:, None])
    same = (i[:, None] // 64) == (i[None, :] // 64)
    c[:, 256:384] = same & (i[:, None] <= i[None, :])
    c[:, 384:512] = same & (i[:, None] > i[None, :])
    c[:, 512] = i < 64
    c[:, 513] = i >= 64
    c[:, 514] = 1.0
    f64 = 500000.0 ** (-np.arange(8, dtype=np.float64) / 8.0)
    f = f64.astype(np.float32)
    flo = (f64 - f.astype(np.float64)).astype(np.float32)
    c[:, 515:523] = f[None, :]
    c[:, 523:531] = f[None, :]
    c[:, 547:555] = flo[None, :]
    c[:, 555:563] = flo[None, :]
    c[:, 531:539] = 0.0
    c[:, 539:547] = np.pi / 2
    return c


def build_nc(layers=(0, 1), groups=ALL_GROUPS):
    nc = bass.Bass("TRN2", target_bir_lowering=False)

    def din(name, shape, d=F32):
        return nc.dram_tensor(name, shape, d, kind="ExternalInput").ap()

    x_d = din("x", [S, D])
    mem_d = din("mem", [256, D])
    pos_d = din("pos", [16, 128], I32)
    ng_d = din("norm_g", [2, D])
    win_d = din("w_in", [2, D, 5120])
    wout_d = din("w_out", [2, 1536, D])
    gq_d = din("moba_q_norm", [2, 64])
    gk_d = din("moba_k_norm", [2, 64])
    lbl_d = din("hgrn_lb_logits", [2, 512])
    go_d = din("hgrn_o_norm", [2, 128])
    mng_d = din("mem_norm_g", [2, D])
    wkv_d = din("w_mem_kv", [2, D, 1024])
    gmq_d = din("mem_q_norm", [2, 128])
    gmk_d = din("mem_k_norm", [2, 128])
    cst_d = din("cst", [128, NCST])
    out_d = nc.dram_tensor("out", [S, D], F32, kind="ExternalOutput").ap()

    P = Prog(nc)
    if _os.environ.get("TOKR"):
        P.tok = Res("tok")
        P.tokset = tuple(_os.environ["TOKR"].split(","))
    with ExitStack() as st:
        def sb(name, shape, dt=F32):
            return st.enter_context(nc.sbuf_tensor("sb_" + name, shape, dt))

        def ps(name, shape, dt=F32):
            return st.enter_context(nc.psum_tensor("pp_" + name, shape, dt))

        def T(fn, r=(), w=()):
            return P.op("tensor", fn, r, w)

        def V(fn, r=(), w=()):
            return P.op("vector", fn, r, w)

        def A(fn, r=(), w=()):
            return P.op("scalar", fn, r, w)

        def G(fn, r=(), w=()):
            return P.op("gpsimd", fn, r, w)

        def DMA(q, fn, r=(), w=()):
            return P.op(q, fn, r, w, dma=True)

        x_tok = sb("x_tok", [128, NT, D]); rx = [Res("x%d" % i) for i in range(NT)]
        hT = sb("hT", [128, 8, S], BF16); rhT = [Res("hT%d" % i) for i in range(NT)]
        cst = sb("cst", [128, NCST]); rcst = Res("cst")
        idb = sb("idb", [128, 128], BF16); ridb = Res("idb")
        trib = sb("trib", [128, 128], BF16); rtrib = Res("trib")
        hgmb = sb("hgmb", [128, 128], BF16); rhgmb = Res("hgmb")
        gq_bc = sb("gq_bc", [128, 64]); gk_bc = sb("gk_bc", [128, 64]); go_bc = sb("go_bc", [128, 128])
        gmq_bc = sb("gmq_bc", [128, 128]); gmk_bc = sb("gmk_bc", [128, 128]); rgains = Res("gains")
        cs = sb("cs", [128, NT, 16]); sn = sb("sn", [128, NT, 16]); rrope = Res("rope")
        wb = [sb("wb%d" % i, [128, 8, 512], BF16) for i in range(3)]; rwb = [[Res("wb%da" % i), Res("wb%db" % i)] for i in range(3)]
        wo = sb("wo", [128, 2, D], BF16); rwo = Res("wo")
        Fp = [sb("F%d" % i, [128, 256]) for i in range(9)]; rF = [Res("F%d" % i) for i in range(9)]
        Bp = [sb("B%d" % i, [128, 256], BF16) for i in range(7)]; rB = [Res("B%d" % i) for i in range(7)]
        szs = [sb("sz%d" % k, [128, 4, 256]) for k in range(2)]; rszs = [[Res("sz%d_%d" % (k, i)) for i in range(4)] for k in range(2)]
        y_tok = sb("y_tok", [128, 4, 256], BF16); ry = [Res("y%d" % i) for i in range(4)]
        yT = [sb("yT%d" % i, [128, 256], BF16) for i in range(2)]; ryT = [Res("yT%d" % i) for i in range(2)]
        pT = [sb("pT%d" % i, [128, 512], BF16) for i in range(4)]; rpT = [Res("pT%d" % i) for i in range(4)]
        hbs = [sb("hb%d" % i, [128, D], BF16) for i in range(2)]; rhbs = [Res("hb0"), Res("hb1")]
        small = sb("small", [128, 128]); rsm = {}

        def sm(name, a, n):
            rsm[name] = Res("sm_" + name)
            return small[:, a:a + n], rsm[name]
        ss16, r_ss16 = sm("ss16", 0, 16)
        t16, r_t16 = sm("t16", 16, 16)
        rstd16, r_rstd16 = sm("rstd16", 32, 16)
        ss4, r_ss4 = sm("ss4", 48, 4)
        t4, r_t4 = sm("t4", 52, 4)
        rs4, r_rs4 = sm("rs4", 56, 4)
        rden, r_rden = sm("rden", 60, 4)
        rdenB, r_rdenB = sm("rdenB", 108, 4)
        dec4, r_dec4 = sm("dec4", 64, 4)
        sso, r_sso = sm("sso", 68, 2)
        to2, r_to2 = sm("to2", 70, 2)
        rso, r_rso = sm("rso", 72, 2)
        gm = sb("gm", [128, 4, 8]); rgm = Res("gm")
        top8 = sb("top8", [128, 4, 8]); rtop8 = Res("top8")
        selb = sb("selb", [128, 4, 8]); rselb = Res("selb")
        kT = sb("kT", [128, 4, S], BF16); rkT = [Res("kT%d" % i) for i in range(NT)]
        v_flat = sb("v_aug", [128, NT * 4 * 65], BF16); rv = [Res("v%d" % i) for i in range(NT)]
        v_aug = v_flat[:].rearrange("p (a b c) -> p a b c", a=NT, b=4)
        stage = v_flat[:, 0:2048].bitcast(F32); rstage = Res("stage")
        g_bcv = v_flat[:, 2048:4096].bitcast(F32); rg_bc = Res("g_bc")
        qTs = [sb("qT%d" % i, [128, 2048], BF16) for i in range(2)]; rqTs = [Res("qT0"), Res("qT1")]
        lbv = qTs[1][:, 0:2048].bitcast(F32); rlb = rqTs[1]
        lb_g = lbv[:, 0:256]; oml_g = lbv[:, 256:512]
        k_aug = sb("k_aug", [128, 4, 72], BF16); rk_aug = Res("k_aug")
        q_aug = sb("q_aug", [128, 4, 72], BF16); rq_aug = Res("q_aug")
        kmT32 = sb("kmT32", [128, 2, 2, 8]); rkm = Res("kmT32")
        S32 = sb("S32", [128, 2, 128]); rS32 = [Res("S32_0"), Res("S32_1")]
        Sbf = sb("Sbf", [128, 2, 2, 128], BF16); rSbf = [[Res("Sbf00"), Res("Sbf01")], [Res("Sbf10"), Res("Sbf11")]]
        qTA = sb("qTA", [128, 2, 128], BF16); qTB = sb("qTB", [128, 2, 128], BF16)
        rqTA = Res("qTA"); rqTB = Res("qTB")
        memT = sb("memT", [128, 8, 256], BF16); rmemT = Res("memT")
        kmT = sb("kmT", [128, 2, 256], BF16); rkmT = Res("kmT")
        vm_aug = sb("vm_aug", [128, 2, 2, 129], BF16); rvm = Res("vm")
        psA = [ps("psA%d" % i, [128, 512]) for i in range(2)]; rpsA = [Res("psA0", True), Res("psA1", True)]
        psT = ps("psT", [128, 1024], BF16); rpsT = Res("psT", True)
        psG = ps("psG", [128, 512]); rpsG = Res("psG", True)
        psS = [ps("psS%d" % i, [128, 512]) for i in range(2)]; rpsS = [Res("psS0", True), Res("psS1", True)]
        psO = [ps("psO%d" % i, [128, 512]) for i in range(2)]; rpsO = [Res("psO0", True), Res("psO1", True)]

        ctr = {"pa": 0, "w": 0, "ev": 0, "ps": 0, "pt": 0, "yt": 0, "mk": 0, "pw": 0, "pw3": 0, "ps2": 0}

        def nxt(k, n):
            v = ctr[k] % n
            ctr[k] += 1
            return v

        def evac(fn_v, fn_a, r, w):
            if nxt("ev", 2) == 0:
                return A(fn_a, r, w)
            return V(fn_v, r, w)

        DMA("sync", lambda e: e.dma_start(out=cst[:], in_=cst_d), w=[rcst])
        V(lambda e: e.tensor_copy(out=idb[:], in_=cst[:, 0:128]), [rcst], [ridb])
        V(lambda e: e.tensor_copy(out=trib[:], in_=cst[:, 128:256]), [rcst], [rtrib])
        V(lambda e: e.tensor_copy(out=hgmb[:], in_=cst[:, 256:384]), [rcst], [rhgmb])
        ident32 = cst[:, 0:128]
        G(lambda e: e.memset(vm_aug[:, :, :, 128:129], 1.0), w=[rvm])
        G(lambda e: e.memset(qTA[:], 0.0), w=[rqTA])
        G(lambda e: e.memset(qTB[:], 0.0), w=[rqTB])
        nI = sb("nI", [128, 256], I32); rnI = Res("nI")
        posi = nI[0:16, 0:128]; rposi = rnI
        posf = Fp[8][0:16, 0:128]; rposf = rF[8]
        DMA("sync", lambda e: e.dma_start(out=posi, in_=pos_d), w=[rposi])
        V(lambda e: e.tensor_copy(out=posf, in_=posi), [rposi], [rposf])
        T(lambda e: e.matmul(psG[:, 0:16], lhsT=posf, rhs=cst[0:16, 0:16], start=True, stop=True), [rposf, rcst], [rpsG])
        post, r_post = sm("post", 80, 16)
        V(lambda e: e.tensor_copy(out=post, in_=psG[:, 0:16]), [rpsG], [r_post])
        ang = Fp[0][:, 0:256].rearrange("p (i j) -> p i j", i=NT)
        V(lambda e: e.tensor_tensor(out=ang, in0=post.unsqueeze(2).to_broadcast([128, NT, 16]),
                                    in1=cst[:, 515:531].unsqueeze(1).to_broadcast([128, NT, 16]), op=ALU.mult),
          [r_post, rcst], [rF[0]])
        ang_lo = Fp[1][:, 0:256].rearrange("p (i j) -> p i j", i=NT)
        V(lambda e: e.tensor_tensor(out=ang_lo, in0=post.unsqueeze(2).to_broadcast([128, NT, 16]),
                                    in1=cst[:, 547:563].unsqueeze(1).to_broadcast([128, NT, 16]), op=ALU.mult),
          [r_post, rcst], [rF[1]])
        V(lambda e: e.tensor_tensor(out=ang, in0=ang, in1=ang_lo, op=ALU.add), [rF[0], rF[1]], [rF[0]])
        V(lambda e: e.tensor_tensor(out=ang, in0=ang, in1=cst[:, 531:547].unsqueeze(1).to_broadcast([128, NT, 16]),
                                    op=ALU.add), [rF[0], rcst], [rF[0]])
        V(lambda e: e.tensor_scalar(out=Fp[1][:], in0=Fp[0][:], scalar1=float(1.0 / (2 * np.pi)), scalar2=None,
                                    op0=ALU.mult), [rF[0]], [rF[1]])
        V(lambda e: e.tensor_copy(out=nI[:], in_=Fp[1][:]), [rF[1]], [rnI])
        V(lambda e: e.tensor_copy(out=Fp[1][:], in_=nI[:]), [rnI], [rF[1]])
        C1 = 6.28125
        C2 = float(2 * np.pi - 6.28125)
        V(lambda e: e.scalar_tensor_tensor(out=Fp[2][:], in0=Fp[1][:], scalar=-C1, in1=Fp[0][:],
                                           op0=ALU.mult, op1=ALU.add), [rF[1], rF[0]], [rF[2]])
        V(lambda e: e.scalar_tensor_tensor(out=Fp[2][:], in0=Fp[1][:], scalar=-C2, in1=Fp[2][:],
                                           op0=ALU.mult, op1=ALU.add), [rF[1], rF[2]], [rF[2]])
        V(lambda e: e.tensor_scalar(out=Fp[2][:], in0=Fp[2][:], scalar1=float(np.pi), scalar2=float(-np.pi),
                                    op0=ALU.min, op1=ALU.max), [rF[2]], [rF[2]])
        A(lambda e: e.activation(out=Fp[3][:], in_=Fp[2][:], func=AF.Sin), [rF[2]], [rF[3]])
        sc = Fp[3][:, 0:256].rearrange("p (i j) -> p i j", i=NT)
        V(lambda e: e.tensor_copy(out=cs[:, :, 0:8], in_=sc[:, :, 8:16]), [rF[3]], [rrope])
        V(lambda e: e.tensor_copy(out=cs[:, :, 8:16], in_=sc[:, :, 8:16]), [rF[3]], [rrope])
        V(lambda e: e.tensor_scalar(out=sn[:, :, 0:8], in0=sc[:, :, 0:8], scalar1=-1.0, scalar2=None, op0=ALU.mult),
          [rF[3]], [rrope])
        V(lambda e: e.tensor_copy(out=sn[:, :, 8:16], in_=sc[:, :, 0:8]), [rF[3]], [rrope])
        def load_w(dst, rdst, src_ap):
            return DMA("gpsimd", lambda e: e.dma_start(out=dst, in_=src_ap), w=[rdst])

        def w_in_cols(l, c0, n):
            return win_d[l].rearrange("(c p) n -> p c n", p=128)[:, :, c0:c0 + n]

        psGb = psG[:].bitcast(BF16)

        def rms_to_T(src_tile, rsrc, gain, rgain, dstT, rdst, col0, ssc, r_ssc, tsc, r_tsc, rsc, r_rsc, k):
            hb = hbs[k % 2]; rhb = rhbs[k % 2]
            pst, rpst = (psT, rpsT) if k % 2 == 0 else (psGb, rpsG)
            A(lambda e: e.activation(out=hb[:], in_=src_tile, func=AF.Square, accum_out=ssc[:, k:k + 1]),
              [rsrc], [rhb, r_ssc])
            A(lambda e: e.activation(out=tsc[:, k:k + 1], in_=ssc[:, k:k + 1], func=AF.Ln, scale=1.0 / D, bias=EPS),
              [r_ssc], [r_tsc])
            A(lambda e: e.activation(out=rsc[:, k:k + 1], in_=tsc[:, k:k + 1], func=AF.Exp, scale=-0.5),
              [r_tsc], [r_rsc])
            V(lambda e: e.scalar_tensor_tensor(out=hb[:], in0=src_tile, scalar=rsc[:, k:k + 1], in1=gain,
                                               op0=ALU.mult, op1=ALU.mult), [rsrc, r_rsc, rgain], [rhb])
            for c in range(8):
                T(lambda e, c=c: e.transpose(out=pst[:, c * 128:(c + 1) * 128], in_=hb[:, c * 128:(c + 1) * 128],
                                             identity=idb[:]), [rhb, ridb], [rpst])
            src3 = pst[:, 0:1024].rearrange("p (c t) -> p c t", c=8)
            evac(lambda e: e.tensor_copy(out=dstT[:, :, col0:col0 + 128], in_=src3),
                 lambda e: e.copy(out=dstT[:, :, col0:col0 + 128], in_=src3), [rpst], [rdst])

        def proj(lhs_cols, rlhs, wt, rwt, ncols=512, lhsT_src=None, wide=False):
            if wide == 3:
                pt, rpt = ((psA[0], rpsA[0]), (psA[1], rpsA[1]), (psG, rpsG))[nxt("pw3", 3)]
            elif wide:
                pt, rpt = ((psA[0], rpsA[0]), (psA[1], rpsA[1]), (psO[0], rpsO[0]), (psO[1], rpsO[1]))[nxt("pw", 4)]
            else:
                b = nxt("pa", 2)
                pt, rpt = psA[b], rpsA[b]
            src = hT if lhsT_src is None else lhsT_src
            for c in range(8):
                T(lambda e, c=c: e.matmul(pt[:, 0:ncols], lhsT=src[:, c, lhs_cols:lhs_cols + 128],
                                          rhs=wt[:, c, 0:ncols], start=(c == 0), stop=(c == 7)),
                  [rlhs] + list(rwt), [rpt])
            return pt, rpt

        def headnorm(src_ps, rps, H, Dh, gain, outF, routF, tmpA, rtmpA, tmpB, rtmpB):
            n = H * Dh
            A(lambda e: e.activation(out=tmpA[:, 0:n], in_=src_ps, func=AF.Square), [rps], [rtmpA])
            V(lambda e: e.tensor_reduce(out=ss4[:, 0:H], in_=tmpA[:, 0:n].rearrange("p (h d) -> p h d", h=H),
                                        axis=AX.X, op=ALU.add), [rtmpA], [r_ss4])
            A(lambda e: e.activation(out=t4[:, 0:H], in_=ss4[:, 0:H], func=AF.Ln, scale=1.0 / Dh, bias=EPS),
              [r_ss4], [r_t4])
            A(lambda e: e.activation(out=rs4[:, 0:H], in_=t4[:, 0:H], func=AF.Exp, scale=-0.5), [r_t4], [r_rs4])
            V(lambda e: e.tensor_tensor(out=tmpB[:, 0:n].rearrange("p (h d) -> p h d", h=H),
                                        in0=src_ps.rearrange("p (h d) -> p h d", h=H),
                                        in1=rs4[:, 0:H].unsqueeze(2).to_broadcast([128, H, Dh]), op=ALU.mult),
              [rps, r_rs4], [rtmpB])
            (G if GOFF else V)(lambda e: e.tensor_tensor(out=outF[:, 0:n].rearrange("p (h d) -> p h d", h=H),
                                                         in0=tmpB[:, 0:n].rearrange("p (h d) -> p h d", h=H),
                                                         in1=gain.unsqueeze(1).to_broadcast([128, H, Dh]), op=ALU.mult),
                               [rtmpB, rgains], [routF])

        def rope(Fx, rFx, i, tR, rtR):
            x3 = Fx[:, 0:256].rearrange("p (h d) -> p h d", h=4)
            a3 = tR[:, 0:64].rearrange("p (h d) -> p h d", h=4)
            b3 = tR[:, 64:128].rearrange("p (h d) -> p h d", h=4)
            rtA = rtR
            rtB = rtR
            G(lambda e: e.tensor_tensor(out=a3, in0=x3[:, :, 0:16], in1=cs[:, i, :].unsqueeze(1).to_broadcast([128, 4, 16]),
                                        op=ALU.mult), [rFx, rrope], [rtA])
            G(lambda e: e.tensor_tensor(out=b3[:, :, 0:8], in0=x3[:, :, 8:16],
                                        in1=sn[:, i, 0:8].unsqueeze(1).to_broadcast([128, 4, 8]), op=ALU.mult),
              [rFx, rrope], [rtB])
            G(lambda e: e.tensor_tensor(out=b3[:, :, 8:16], in0=x3[:, :, 0:8],
                                        in1=sn[:, i, 8:16].unsqueeze(1).to_broadcast([128, 4, 8]), op=ALU.mult),
              [rFx, rrope], [rtB])
            G(lambda e: e.tensor_tensor(out=x3[:, :, 0:16], in0=a3, in1=b3, op=ALU.add), [rtA, rtB], [rFx])

        def silu_ps(dst, rdst, src_ps, rps, defer=False):
            A(lambda e: e.activation(out=dst, in_=src_ps, func=AF.Exp, scale=-1.0), [rps], [rdst])
            A(lambda e: e.activation(out=dst, in_=dst, func=AF.Ln, bias=1.0), [rdst], [rdst])
            A(lambda e: e.activation(out=dst, in_=dst, func=AF.Exp, scale=-1.0), [rdst], [rdst])

            def fin():
                V(lambda e: e.tensor_tensor(out=dst, in0=src_ps, in1=dst, op=ALU.mult), [rps, rdst], [rdst])
            if defer:
                return fin
            fin()

        SC = [((Fp[0], rF[0]), (Fp[1], rF[1]), (Fp[2], rF[2]), (Fp[3], rF[3])),
              ((Fp[4], rF[4]), (Fp[6], rF[6]), (Fp[7], rF[7]), (Fp[8], rF[8]))]
        AUG = [(k_aug, rk_aug), (q_aug, rq_aug)]
        if _os.environ.get("NOPAR", "0") == "1":
            SC[1] = SC[0]
        if _os.environ.get("NOAUG", "0") == "1":
            AUG[1] = AUG[0]

        dec8, _r = sm("dec8", 96, 8)
        r_dec8 = [Res("dec8a"), Res("dec8b")]
        sso4, _r2 = sm("sso4", 104, 4)
        r_sso2 = [Res("ssoA"), Res("ssoB")]
        dummy = sb("dummy", [128, 8])
        kflat = kT[:].rearrange("p h t -> p (h t)")
        kf32 = kflat[:, 0:4608].bitcast(F32)
        Fq = [kf32[:, k * 256:(k + 1) * 256] for k in range(9)]
        rFq = [Res("Fq%d" % k) for k in range(9)]
        Bq = [kflat[:, 4608 + k * 256:4608 + (k + 1) * 256] for k in range(7)]
        rBq = [Res("Bq%d" % k) for k in range(7)]
        HSETS = [([t[:] for t in Fp], rF, [t[:] for t in Bp], rB), (Fq, rFq, Bq, rBq)]

        def barrier():
            G(lambda e: e.memset(dummy[:], 0.0), w=list(rkT) + rFq + rBq + list(rv) + [rstage, rg_bc])

        rrow = Res("pe_rowfence")

        out_dmas = []

        def outproj_tile(i, r, last, obanks=None):
            yb = nxt("yt", 2)
            for pp in range(2):
                T(lambda e, pp=pp: e.transpose(out=psT[:, pp * 128:(pp + 1) * 128], in_=y_tok[:, r, pp * 128:(pp + 1) * 128],
                                               identity=idb[:]), [ry[r], ridb], [rpsT])
            evac(lambda e: e.tensor_copy(out=yT[yb][:], in_=psT[:, 0:256]),
                 lambda e: e.copy(out=yT[yb][:], in_=psT[:, 0:256]), [rpsT], [ryT[yb]])
            for half in range(2):
                if obanks is None:
                    b = nxt("pa", 2)
                    pso, rpso = psA[b], rpsA[b]
                else:
                    pso, rpso = obanks[half]
                for pp in range(2):
                    T(lambda e, pp=pp, half=half, pso=pso: e.matmul(pso[:, 0:512], lhsT=yT[yb][:, pp * 128:(pp + 1) * 128],
                                                                    rhs=wo[:, pp, half * 512:(half + 1) * 512],
                                                                    start=(pp == 0), stop=(pp == 1)),
                      [ryT[yb], rwo], [rpso])
                V(lambda e, half=half, pso=pso: e.tensor_tensor(out=x_tok[:, i, half * 512:(half + 1) * 512], in0=pso[:, 0:512],
                                                                in1=x_tok[:, i, half * 512:(half + 1) * 512], op=ALU.add),
                  [rpso, rx[i]], [rx[i]])
            if last:
                out_dmas.append(DMA("sync", lambda e: e.dma_start(out=out_d[i * 128:(i + 1) * 128, :], in_=x_tok[:, i, :]),
                                    r=[rx[i]]))

        def attn_chunk(Q, H, Dh, KP, scale, key_tiles, causal, kTsrc, rkTsrc, vsrc, rvsrc, qview, rqT, sz, rsz, nsb=None):
            DA = Dh + 1
            for h in range(H):
                if Dh == 64:
                    o_b = h % 2
                    banks = [o_b, o_b, o_b, o_b]
                    offs = [0, DA, 2 * DA, 3 * DA]
                else:
                    banks = [0, 0, 1, 1]
                    offs = [0, DA, 0, DA]
                started = set()
                kts = key_tiles(Q)
                for kt in kts:
                    j = kt - 4 * Q if causal else -1
                    q0 = max(j, 0) * 128
                    N = 512 - q0
                    sbk = nxt("ps", PS3) if nsb is None else nxt("ps2", nsb)
                    pss, rpss = ((psS[0], rpsS[0]), (psS[1], rpsS[1]), (psG, rpsG))[sbk]
                    T(lambda e, kt=kt, h=h, q0=q0, N=N, pss=pss: e.matmul(
                        pss[:, 0:N], lhsT=kTsrc(h, kt), rhs=qview(h)[:, q0:512], start=True, stop=True),
                      [rkTsrc(kt), rqT], [rpss])
                    pb = nxt("pt", 4)
                    A(lambda e, N=N, pss=pss, pb=pb: e.activation(out=pT[pb][:, 0:N], in_=pss[:, 0:N], func=AF.Exp,
                                                                  scale=scale), [rpss], [rpT[pb]])
                    if j >= 0:
                        (V if (MASKV and nxt("mk", 2) == 0) else G)(
                            lambda e, pb=pb: e.tensor_tensor(out=pT[pb][:, 0:128], in0=pT[pb][:, 0:128], in1=trib[:],
                                                             op=ALU.mult), [rpT[pb], rtrib], [rpT[pb]])
                    for r in range(max(j, 0), 4):
                        bk = banks[r]
                        first = bk not in started
                        started.add(bk)
                        T(lambda e, r=r, kt=kt, h=h, q0=q0, pb=pb, bk=bk, first=first: e.matmul(
                            psO[bk][:, offs[r]:offs[r] + DA], lhsT=pT[pb][:, r * 128 - q0:r * 128 - q0 + 128],
                            rhs=vsrc(h, kt), start=first, stop=False, skip_group_check=True),
                          [rpT[pb], rvsrc(kt)], [rpsO[bk]])
                rd, r_rd = (rden, r_rden) if h % 2 == 0 else (rdenB, r_rdenB)
                for bk0 in sorted(set(banks)):
                    rs_ = [r for r in range(4) if banks[r] == bk0]
                    nr = len(rs_)
                    V(lambda e, bk0=bk0, rs_=rs_, nr=nr, rd=rd: e.reciprocal(
                        out=rd[:, rs_[0]:rs_[0] + nr],
                        in_=psO[bk0][:, 0:nr * DA].rearrange("p (r c) -> p r c", r=nr)[:, :, Dh:DA]),
                      [rpsO[bk0]], [r_rd])
                    V(lambda e, bk0=bk0, rs_=rs_, nr=nr, h=h: e.tensor_tensor(
                        out=sz[:, rs_[0]:rs_[0] + nr, h * Dh:(h + 1) * Dh],
                        in0=psO[bk0][:, 0:nr * DA].rearrange("p (r c) -> p r c", r=nr)[:, :, 0:Dh],
                        in1=sz[:, rs_[0]:rs_[0] + nr, h * Dh:(h + 1) * Dh], op=ALU.mult),
                      [rpsO[bk0]] + [rsz[r] for r in rs_], [rsz[r] for r in rs_])
                G(lambda e, h=h, rd=rd: e.tensor_tensor(out=y_tok[:, :, h * Dh:(h + 1) * Dh], in0=sz[:, :, h * Dh:(h + 1) * Dh],
                                                        in1=rd[:, 0:4].unsqueeze(2).to_broadcast([128, 4, Dh]), op=ALU.mult),
                  list(rsz) + [r_rd], list(ry))

        for li, l in enumerate(layers):
            last_layer = (li == len(layers) - 1)
            DMA("sync", lambda e, l=l: e.dma_start(out=g_bcv, in_=ng_d[l:l + 1, :].partition_broadcast(128)), w=[rg_bc])
            for dst, src in ((gq_bc, gq_d), (gk_bc, gk_d), (go_bc, go_d), (gmq_bc, gmq_d), (gmk_bc, gmk_d)):
                DMA("sync", lambda e, l=l, dst=dst, src=src: e.dma_start(out=dst[:], in_=src[l:l + 1, :].partition_broadcast(128)),
                    w=[rgains])
            for i in range(NT):
                if li == 0:
                    DMA(("scalar" if (XQ and i % 2 == 1) else "sync"), lambda e, i=i: e.dma_start(out=x_tok[:, i, :], in_=x_d[i * 128:(i + 1) * 128, :]), w=[rx[i]])
                rms_to_T(x_tok[:, i, :], rx[i], g_bcv, rg_bc, hT, rhT[i], i * 128, ss16, r_ss16, t16, r_t16,
                         rstd16, r_rstd16, i)
            glist = [g for g in ALL_GROUPS if g in groups]
            for gi, gname in enumerate(glist):
                last = last_layer and gi == len(glist) - 1
                kind = gname[0]
                g = int(gname[1])
                if kind == "A":
                    w1 = nxt("w", 3); w2 = nxt("w", 3)
                    load_w(wb[w1][:, :, 0:256], rwb[w1][0], w_in_cols(l, 512 + 256 * g, 256))
                    load_w(wb[w1][:, :, 256:512], rwb[w1][1], w_in_cols(l, 1024 + 256 * g, 256))
                    load_w(wb[w2][:, :, 0:256], rwb[w2][0], w_in_cols(l, 256 * g, 256))
                    load_w(wb[w2][:, :, 256:512], rwb[w2][1], w_in_cols(l, 3584 + 256 * g, 256))
                    load_w(wo[:], rwo, wout_d[l, 256 * g:256 * g + 256, :].rearrange("(c p) n -> p c n", p=128))
                    barrier()
                    G(lambda e: e.memset(v_aug[:, :, :, 64:65], 1.0), w=rv)
                    V(lambda e: e.memset(psG[:, 0:16], 0.0), w=[rpsG])
                    for i in range(NT):
                        n_blk = i // 2
                        (tA, rtA), (tB, rtB), (FO, rFO), (tR, rtR) = SC[(i % 2) * PAR1]
                        ka, rka = AUG[(i % 2) * PAR1]
                        pa, rpa = proj(i * 128, rhT[i], wb[w1], rwb[w1], wide=bool(WIDE))
                        A(lambda e, i=i, pa=pa: e.copy(out=v_aug[:, i, :, 0:64],
                                                       in_=pa[:, 256:512].rearrange("p (h d) -> p h d", h=4)),
                          [rpa], [rv[i]])
                        headnorm(pa[:, 0:256], rpa, 4, 64, gk_bc[:], FO, rFO, tA, rtA, tB, rtB)
                        rope(FO, rFO, i, tR, rtR)
                        G(lambda e, ka=ka, FO=FO: e.tensor_copy(out=ka[:, :, 0:64], in_=FO[:].rearrange("p (h d) -> p h d", h=4)),
                          [rFO], [rka])
                        G(lambda e, ka=ka: e.memset(ka[:, :, 64:72], 0.0), w=[rka])
                        G(lambda e, ka=ka, n_blk=n_blk: e.memset(ka[:, :, 64 + n_blk:65 + n_blk], 1.0), w=[rka])
                        for pp in range(2):
                            T(lambda e, pp=pp, n_blk=n_blk, FO=FO: e.matmul(psG[:, pp * 8 + n_blk:pp * 8 + n_blk + 1],
                                                                            lhsT=FO[:, pp * 128:(pp + 1) * 128], rhs=cst[:, 514:515],
                                                                            start=False, stop=False, skip_group_check=True),
                              [rFO, rcst], [rpsG])
                        for h in range(4):
                            T(lambda e, h=h, ka=ka: e.transpose(out=psT[0:72, h * 128:(h + 1) * 128], in_=ka[:, h, :], identity=idb[:]),
                              [rka, ridb], [rpsT])
                        src3 = psT[0:72, 0:512].rearrange("p (h t) -> p h t", h=4)
                        evac(lambda e, i=i, src3=src3: e.tensor_copy(out=kT[0:72, :, i * 128:(i + 1) * 128], in_=src3),
                             lambda e, i=i, src3=src3: e.copy(out=kT[0:72, :, i * 128:(i + 1) * 128], in_=src3),
                             [rpsT], [rkT[i]])
                    G(lambda e: e.memset(kmT32[:], 0.0), w=[rkm])
                    A(lambda e: e.copy(out=kmT32[0:64, :, 0, :], in_=psG[0:64, 0:16].rearrange("p (a n) -> p a n", a=2)), [rpsG], [rkm])
                    A(lambda e: e.copy(out=kmT32[64:128, :, 1, :], in_=psG[64:128, 0:16].rearrange("p (a n) -> p a n", a=2)), [rpsG], [rkm])
                    for Q in range(4):
                        qT3 = qTs[Q % 2][:, 0:2048].rearrange("p (h t) -> p h t", h=4)
                        rqT = rqTs[Q % 2]
                        sz = szs[Q % 2]; rsz = rszs[Q % 2]
                        for r in range(4):
                            i = 4 * Q + r
                            own = i // 2
                            P.phase = "chain"
                            par = (i % 2) * PAR2 * (1 if (own < 4 or GPAR) else 0)
                            (tA, rtA), (tB, rtB), (FO, rFO), (tR, rtR) = SC[par]
                            qa, rqa = AUG[par]
                            pa, rpa = proj(i * 128, rhT[i], wb[w2], rwb[w2], wide=(bool(WIDE) and Q == 0))
                            fin = silu_ps(sz[:, r, :], rsz[r], pa[:, 256:512], rpa, defer=True)
                            headnorm(pa[:, 0:256], rpa, 4, 64, gq_bc[:], FO, rFO, tA, rtA, tB, rtB)
                            fin()
                            rope(FO, rFO, i, tR, rtR)
                            G(lambda e, qa=qa, FO=FO: e.tensor_copy(out=qa[:, :, 0:64], in_=FO[:].rearrange("p (h d) -> p h d", h=4)),
                              [rFO], [rqa])
                            if own >= 4:
                                P.phase = "gate"
                                for pp in range(2):
                                    T(lambda e, pp=pp, FO=FO: e.matmul(psG[:, pp * 128:(pp + 1) * 128],
                                                                       lhsT=FO[:, pp * 128:(pp + 1) * 128], rhs=ident32,
                                                                       start=True, stop=True),
                                      [rFO, rcst], [rpsG])
                                A(lambda e: e.copy(out=Fp[5][:], in_=psG[:, 0:256]), [rpsG], [rF[5]])
                                for h in range(4):
                                    T(lambda e, h=h, own=own: e.matmul(
                                        psG[:, 256 + h * 8:256 + h * 8 + own],
                                        lhsT=Fp[5][:, (h // 2) * 128:(h // 2) * 128 + 128],
                                        rhs=kmT32[:, h // 2, h % 2, 0:own], start=True, stop=True),
                                      [rF[5], rkm], [rpsG])
                                V(lambda e: e.memset(gm[:], -1.0e30), w=[rgm])
                                V(lambda e, own=own: e.tensor_copy(
                                    out=gm[:, :, 0:own], in_=psG[:, 256:288].rearrange("p (h n) -> p h n", h=4)[:, :, 0:own]),
                                  [rpsG], [rgm])
                                for h in range(4):
                                    V(lambda e, h=h: e.max(out=top8[:, h, :], in_=gm[:, h, :]), [rgm], [rtop8])
                                V(lambda e: e.tensor_tensor(out=selb[:], in0=gm[:], in1=top8[:, :, 2:3].to_broadcast([128, 4, 8]),
                                                            op=ALU.is_ge), [rgm, rtop8], [rselb])
                                V(lambda e, qa=qa: e.tensor_scalar(out=qa[:, :, 64:72], in0=selb[:], scalar1=30000.0,
                                                                   scalar2=-30000.0, op0=ALU.mult, op1=ALU.add), [rselb], [rqa])
                                V(lambda e, qa=qa, own=own: e.memset(qa[:, :, 64 + own:65 + own], 0.0), w=[rqa])
                            else:
                                G(lambda e, qa=qa: e.memset(qa[:, :, 64:72], 0.0), w=[rqa])
                            P.phase = None
                            for h in range(4):
                                T(lambda e, h=h, qa=qa: e.transpose(out=psT[0:72, h * 128:(h + 1) * 128], in_=qa[:, h, :],
                                                                    identity=idb[:]), [rqa, ridb], [rpsT])
                            src3 = psT[0:72, 0:512].rearrange("p (h t) -> p h t", h=4)
                            evac(lambda e, r=r, src3=src3, qT3=qT3: e.tensor_copy(out=qT3[0:72, :, r * 128:(r + 1) * 128], in_=src3),
                                 lambda e, r=r, src3=src3, qT3=qT3: e.copy(out=qT3[0:72, :, r * 128:(r + 1) * 128], in_=src3),
                                 [rpsT], [rqT])
                        attn_chunk(Q, 4, 64, 72, 0.125, lambda Q: list(range(4 * Q + 4)), True,
                                   lambda h, kt: kT[0:72, h, kt * 128:(kt + 1) * 128], lambda kt: rkT[kt],
                                   lambda h, kt: v_aug[:, kt, h, :], lambda kt: rv[kt],
                                   lambda h, qT3=qT3: qT3[0:72, h, :], rqT, sz, rsz)
                        for r in range(4):
                            outproj_tile(4 * Q + r, r, last, obanks=([(psO[0], rpsO[0]), (psO[1], rpsO[1])] if OPB else None))
                elif kind == "M":
                    w1 = nxt("w", 3); w2 = nxt("w", 3)
                    wkvv = wkv_d[l].rearrange("(c p) n -> p c n", p=128)
                    load_w(wb[w1][:, :, 0:256], rwb[w1][0], wkvv[:, :, 256 * g:256 * g + 256])
                    load_w(wb[w1][:, :, 256:512], rwb[w1][1], wkvv[:, :, 512 + 256 * g:512 + 256 * g + 256])
                    load_w(wb[w2][:, :, 0:256], rwb[w2][0], w_in_cols(l, 3072 + 256 * g, 256))
                    load_w(wb[w2][:, :, 256:512], rwb[w2][1], w_in_cols(l, 4608 + 256 * g, 256))
                    load_w(wo[:], rwo, wout_d[l, 1024 + 256 * g:1024 + 256 * g + 256, :].rearrange("(c p) n -> p c n", p=128))
                    if g == 0 or ("M0" not in groups):
                        barrier()
                        DMA("sync", lambda e, l=l: e.dma_start(out=g_bcv, in_=mng_d[l:l + 1, :].partition_broadcast(128)),
                            w=[rg_bc])
                        for mt in range(2):
                            DMA("sync", lambda e, mt=mt: e.dma_start(out=stage, in_=mem_d[mt * 128:(mt + 1) * 128, :]),
                                w=[rstage])
                            rms_to_T(stage, rstage, g_bcv, rg_bc, memT, rmemT, mt * 128, ss16, r_ss16, t16, r_t16,
                                     rstd16, r_rstd16, mt)
                    for mt in range(2):
                        (tA, rtA), (tB, rtB), (FO, rFO), (tR, rtR) = SC[mt % 2]
                        pa, rpa = proj(mt * 128, rmemT, wb[w1], rwb[w1], lhsT_src=memT)
                        A(lambda e, mt=mt, pa=pa: e.copy(out=vm_aug[:, mt, :, 0:128],
                                                         in_=pa[:, 256:512].rearrange("p (h d) -> p h d", h=2)),
                          [rpa], [rvm])
                        headnorm(pa[:, 0:256], rpa, 2, 128, gmk_bc[:], FO, rFO, tA, rtA, tB, rtB)
                        G(lambda e, mt=mt, FO=FO: e.tensor_copy(out=Bp[mt % 2][:], in_=FO[:]), [rFO], [rB[mt % 2]])
                        for hh in range(2):
                            T(lambda e, hh=hh, mt=mt: e.transpose(out=psT[:, hh * 128:(hh + 1) * 128],
                                                                  in_=Bp[mt % 2][:, hh * 128:(hh + 1) * 128],
                                                                  identity=idb[:]), [rB[mt % 2], ridb], [rpsT])
                        src3 = psT[:, 0:256].rearrange("p (h t) -> p h t", h=2)
                        evac(lambda e, mt=mt, src3=src3: e.tensor_copy(out=kmT[:, :, mt * 128:(mt + 1) * 128], in_=src3),
                             lambda e, mt=mt, src3=src3: e.copy(out=kmT[:, :, mt * 128:(mt + 1) * 128], in_=src3),
                             [rpsT], [rkmT])
                    for Q in range(4):
                        qm3 = qTs[Q % 2][:, 0:1024].rearrange("p (h t) -> p h t", h=2)
                        rqT = rqTs[Q % 2]
                        sz = szs[Q % 2]; rsz = rszs[Q % 2]
                        for r in range(4):
                            i = 4 * Q + r
                            (tA, rtA), (tB, rtB), (FO, rFO), (tR, rtR) = SC[i % 2]
                            pa, rpa = proj(i * 128, rhT[i], wb[w2], rwb[w2], wide=(3 if WIDEM else False))
                            fin = silu_ps(sz[:, r, :], rsz[r], pa[:, 256:512], rpa, defer=True)
                            headnorm(pa[:, 0:256], rpa, 2, 128, gmq_bc[:], FO, rFO, tA, rtA, tB, rtB)
                            fin()
                            G(lambda e, i=i, FO=FO: e.tensor_copy(out=Bp[i % 2][:], in_=FO[:]), [rFO], [rB[i % 2]])
                            for hh in range(2):
                                T(lambda e, hh=hh, i=i: e.transpose(out=psT[:, hh * 128:(hh + 1) * 128],
                                                                    in_=Bp[i % 2][:, hh * 128:(hh + 1) * 128], identity=idb[:]),
                                  [rB[i % 2], ridb], [rpsT])
                            src3 = psT[:, 0:256].rearrange("p (h t) -> p h t", h=2)
                            evac(lambda e, r=r, src3=src3, qm3=qm3: e.tensor_copy(out=qm3[:, :, r * 128:(r + 1) * 128], in_=src3),
                                 lambda e, r=r, src3=src3, qm3=qm3: e.copy(out=qm3[:, :, r * 128:(r + 1) * 128], in_=src3),
                                 [rpsT], [rqT])
                        attn_chunk(Q, 2, 128, 128, float(128 ** -0.5), lambda Q: [0, 1], False,
                                   lambda h, kt: kmT[:, h, kt * 128:(kt + 1) * 128], lambda kt: rkmT,
                                   lambda h, kt: vm_aug[:, kt, h, :], lambda kt: rvm,
                                   lambda h, qm3=qm3: qm3[:, h, :], rqT, sz, rsz, nsb=(2 if WIDEM else None))
                        for r in range(4):
                            outproj_tile(4 * Q + r, r, last, obanks=([(psO[0], rpsO[0]), (psO[1], rpsO[1])] if OPB else None))
                else:
                    w1 = nxt("w", 3); w2 = nxt("w", 3)
                    load_w(wb[w1][:, :, 0:256], rwb[w1][0], w_in_cols(l, 1536 + 256 * g, 256))
                    load_w(wb[w1][:, :, 256:512], rwb[w1][1], w_in_cols(l, 2048 + 256 * g, 256))
                    load_w(wb[w2][:, :, 0:256], rwb[w2][0], w_in_cols(l, 2560 + 256 * g, 256))
                    load_w(wb[w2][:, :, 256:512], rwb[w2][1], w_in_cols(l, 4096 + 256 * g, 256))
                    load_w(wo[:], rwo, wout_d[l, 512 + 256 * g:512 + 256 * g + 256, :].rearrange("(c p) n -> p c n", p=128))
                    barrier()
                    if l != 0:
                        DMA("sync", lambda e, g=g: e.dma_start(out=lb_g, in_=lbl_d[1:2, 256 * g:256 * g + 256].partition_broadcast(128)), w=[rlb])
                        DMA("sync", lambda e, g=g: e.dma_start(out=oml_g, in_=lbl_d[0:1, 256 * g:256 * g + 256].partition_broadcast(128)), w=[rlb])
                        V(lambda e: e.tensor_tensor(out=lb_g, in0=lb_g, in1=oml_g, op=ALU.subtract), [rlb], [rlb])
                        A(lambda e: e.activation(out=lb_g, in_=lb_g, func=AF.Exp, scale=-1.0), [rlb], [rlb])
                        A(lambda e: e.activation(out=lb_g, in_=lb_g, func=AF.Ln, bias=1.0), [rlb], [rlb])
                        A(lambda e: e.activation(out=lb_g, in_=lb_g, func=AF.Exp, scale=-1.0), [rlb], [rlb])
                        V(lambda e: e.tensor_scalar(out=oml_g, in0=lb_g, scalar1=-1.0, scalar2=1.0, op0=ALU.mult, op1=ALU.add),
                          [rlb], [rlb])
                    for hh in range(2):
                        G(lambda e, hh=hh: e.memset(S32[:, hh, :], 0.0), w=[rS32[hh]])
                        G(lambda e, hh=hh: e.memset(Sbf[:, 0, hh, :], 0.0), w=[rSbf[0][hh]])
                    Tri32 = cst[:, 256:384]
                    TriE32 = cst[:, 384:512]
                    for i in range(NT):
                        Fs, rFs, Bs, rBs = HSETS[i % 2]
                        sl = i % 4
                        sz = szs[(i // 4) % 2]; rsz = rszs[(i // 4) % 2]
                        pq, rpq = proj(i * 128, rhT[i], wb[w1], rwb[w1])
                        finq = silu_ps(Fs[0], rFs[0], pq[:, 0:256], rpq, defer=True)
                        A(lambda e, pq=pq, Fs=Fs: e.activation(out=Fs[1], in_=pq[:, 256:512], func=AF.Exp, scale=-1.0), [rpq], [rFs[1]])
                        A(lambda e, Fs=Fs: e.activation(out=Fs[1], in_=Fs[1], func=AF.Ln, bias=1.0), [rFs[1]], [rFs[1]])
                        finq()
                        pi_, rpi = proj(i * 128, rhT[i], wb[w2], rwb[w2])
                        silu_ps(sz[:, sl, :], rsz[sl], pi_[:, 256:512], rpi)
                        V(lambda e, pi_=pi_, Bs=Bs: e.tensor_copy(out=Bs[0], in_=pi_[:, 0:256]), [rpi], [rBs[0]])
                        if l == 0:
                            A(lambda e, Fs=Fs: e.activation(out=Fs[2], in_=Fs[1], func=AF.Copy, scale=-1.0), [rFs[1]], [rFs[2]])
                            A(lambda e, Fs=Fs: e.activation(out=Fs[1], in_=Fs[1], func=AF.Exp, scale=-1.0), [rFs[1]], [rFs[1]])
                        else:
                            A(lambda e, Fs=Fs: e.activation(out=Fs[1], in_=Fs[1], func=AF.Exp, scale=-1.0), [rFs[1]], [rFs[1]])
                            V(lambda e, g=g, Fs=Fs: e.tensor_tensor(out=Fs[1], in0=Fs[1], in1=oml_g,
                                                                    op=ALU.mult), [rFs[1], rlb], [rFs[1]])
                            V(lambda e, g=g, Fs=Fs: e.tensor_tensor(out=Fs[1], in0=Fs[1], in1=lb_g,
                                                                    op=ALU.add), [rFs[1], rlb], [rFs[1]])
                            A(lambda e, Fs=Fs: e.activation(out=Fs[2], in_=Fs[1], func=AF.Ln), [rFs[1]], [rFs[2]])
                        (G if GOFF else V)(lambda e, Fs=Fs: e.tensor_scalar(out=Fs[3], in0=Fs[1], scalar1=-1.0, scalar2=1.0, op0=ALU.mult,
                                                                            op1=ALU.add), [rFs[1]], [rFs[3]])
                        T(lambda e, Fs=Fs: e.matmul(psG[:, 0:256], lhsT=Tri32, rhs=Fs[2], start=True, stop=True),
                          [rcst, rFs[2]], [rpsG])
                        T(lambda e, Fs=Fs: e.matmul(psG[:, 256:512], lhsT=TriE32, rhs=Fs[2], start=True, stop=True),
                          [rcst, rFs[2]], [rpsG])
                        for hh in range(2):
                            T(lambda e, hh=hh, Fs=Fs: e.matmul(psS[1][:, 256 + 2 * hh:256 + 2 * hh + 2],
                                                               lhsT=Fs[2][:, hh * 128:(hh + 1) * 128],
                                                               rhs=cst[:, 512:514], start=True, stop=True), [rFs[2], rcst], [rpsS[1]])
                        dsl = dec8[:, 4 * (i % 2):4 * (i % 2) + 4]
                        rds = r_dec8[i % 2]
                        A(lambda e, dsl=dsl: e.activation(out=dsl, in_=psS[1][:, 256:260], func=AF.Exp), [rpsS[1]], [rds])
                        A(lambda e, Fs=Fs: e.activation(out=Fs[4], in_=psG[:, 0:256], func=AF.Exp), [rpsG], [rFs[4]])
                        A(lambda e, Fs=Fs: e.activation(out=Fs[5], in_=psG[:, 0:256], func=AF.Exp, scale=-1.0), [rpsG], [rFs[5]])
                        A(lambda e, Fs=Fs: e.activation(out=Fs[6], in_=psG[:, 256:512], func=AF.Exp), [rpsG], [rFs[6]])
                        V(lambda e, Fs=Fs, Bs=Bs: e.tensor_tensor(out=Bs[1], in0=Fs[0], in1=Fs[4], op=ALU.mult), [rFs[0], rFs[4]], [rBs[1]])
                        G(lambda e, Fs=Fs, Bs=Bs: e.tensor_tensor(out=Bs[2], in0=Fs[3], in1=Fs[5], op=ALU.mult), [rFs[3], rFs[5]], [rBs[2]])
                        G(lambda e, Fs=Fs, Bs=Bs: e.tensor_tensor(out=Bs[3], in0=Fs[3], in1=Fs[6], op=ALU.mult), [rFs[3], rFs[6]], [rBs[3]])
                        for hh in range(2):
                            T(lambda e, hh=hh, Bs=Bs: e.transpose(out=psT[:, hh * 128:(hh + 1) * 128], in_=Bs[1][:, hh * 128:(hh + 1) * 128],
                                                                  identity=idb[:]), [rBs[1], ridb], [rpsT])
                            T(lambda e, hh=hh, Bs=Bs: e.transpose(out=psT[:, 256 + hh * 128:256 + (hh + 1) * 128],
                                                                  in_=Bs[2][:, hh * 128:(hh + 1) * 128], identity=idb[:]),
                              [rBs[2], ridb], [rpsT])
                        pq3 = psT[:, 0:256].rearrange("p (h t) -> p h t", h=2)
                        A(lambda e, Bs=Bs: e.copy(out=Bs[4], in_=psT[:, 0:256]), [rpsT], [rBs[4]])
                        A(lambda e, pq3=pq3: e.copy(out=qTA[:, :, 0:64], in_=pq3[:, :, 0:64]), [rpsT], [rqTA])
                        V(lambda e, Bs=Bs: e.tensor_copy(out=Bs[5], in_=psT[:, 256:512]), [rpsT], [rBs[5]])
                        V(lambda e, pq3=pq3: e.tensor_copy(out=qTB[:, :, 64:128], in_=pq3[:, :, 64:128]), [rpsT], [rqTB])
                        cur = i % 2
                        nxtb = 1 - cur
                        for hh in range(2):
                            hs = slice(hh * 128, (hh + 1) * 128)
                            T(lambda e, hs=hs, Bs=Bs: e.matmul(psS[1][:, hs], lhsT=Bs[5][:, hs], rhs=Bs[4][:, hs], start=True, stop=True),
                              [rBs[5], rBs[4]], [rpsS[1]])
                        V(lambda e, Bs=Bs: e.tensor_tensor(out=Bs[6].rearrange("p (h t) -> p h t", h=2),
                                                           in0=psS[1][:, 0:256].rearrange("p (h t) -> p h t", h=2),
                                                           in1=hgmb[:].unsqueeze(1).to_broadcast([128, 2, 128]), op=ALU.mult),
                          [rpsS[1], rhgmb], [rBs[6]])
                        for hh in range(2):
                            hs = slice(hh * 128, (hh + 1) * 128)
                            T(lambda e, hs=hs, hh=hh, Bs=Bs: e.matmul(psO[hh][:, 0:128], lhsT=Bs[6][:, hs], rhs=Bs[0][:, hs],
                                                                      start=True, stop=False), [rBs[6], rBs[0]], [rpsO[hh]])
                            T(lambda e, hh=hh, cur=cur: e.matmul(psO[hh][:, 0:128], lhsT=qTA[:, hh, :], rhs=Sbf[:, cur, hh, :],
                                                                 start=False, stop=False), [rqTA, rSbf[cur][hh]], [rpsO[hh]])
                        for hh in range(2):
                            hs = slice(hh * 128, (hh + 1) * 128)
                            T(lambda e, hs=hs, Bs=Bs: e.matmul(psS[0][:, hs], lhsT=Bs[3][0:64, hs], rhs=Bs[0][0:64, hs],
                                                               start=True, stop=True), [rBs[3], rBs[0]], [rpsS[0], rrow])
                        for hh in range(2):
                            hs = slice(hh * 128, (hh + 1) * 128)
                            V(lambda e, hs=hs, hh=hh, dsl=dsl: e.scalar_tensor_tensor(out=S32[:, hh, :], in0=S32[:, hh, :],
                                                                                      scalar=dsl[:, 2 * hh:2 * hh + 1], in1=psS[0][:, hs],
                                                                                      op0=ALU.mult, op1=ALU.add),
                              [rS32[hh], rds, rpsS[0]], [rS32[hh]])
                            G(lambda e, hh=hh, nxtb=nxtb: e.tensor_copy(out=Sbf[:, nxtb, hh, :], in_=S32[:, hh, :]),
                              [rS32[hh]], [rSbf[nxtb][hh]])
                        for hh in range(2):
                            T(lambda e, hh=hh, nxtb=nxtb: e.matmul(psO[hh][:, 0:128], lhsT=qTB[:, hh, :], rhs=Sbf[:, nxtb, hh, :],
                                                                   start=False, stop=True), [rqTB, rSbf[nxtb][hh]], [rpsO[hh], rrow])
                        for hh in range(2):
                            hs = slice(hh * 128, (hh + 1) * 128)
                            T(lambda e, hs=hs, Bs=Bs: e.matmul(psS[0][:, hs], lhsT=Bs[3][64:128, hs], rhs=Bs[0][64:128, hs],
                                                               start=True, stop=True), [rBs[3], rBs[0]], [rpsS[0], rrow])
                        for hh in range(2):
                            hs = slice(hh * 128, (hh + 1) * 128)
                            V(lambda e, hs=hs, hh=hh, dsl=dsl: e.scalar_tensor_tensor(out=S32[:, hh, :], in0=S32[:, hh, :],
                                                                                      scalar=dsl[:, 2 * hh + 1:2 * hh + 2], in1=psS[0][:, hs],
                                                                                      op0=ALU.mult, op1=ALU.add),
                              [rS32[hh], rds, rpsS[0]], [rS32[hh]])
                        for hh in range(2):
                            G(lambda e, hh=hh, nxtb=nxtb: e.tensor_copy(out=Sbf[:, nxtb, hh, :], in_=S32[:, hh, :]),
                              [rS32[hh]], [rSbf[nxtb][hh]])
                        ssl = sso4[:, 2 * (i % 2):2 * (i % 2) + 2]
                        r_sso = r_sso2[i % 2]
                        for hh in range(2):
                            A(lambda e, hh=hh, Fs=Fs, ssl=ssl: e.activation(out=Fs[7][:, 0:128], in_=psO[hh][:, 0:128], func=AF.Square,
                                                                            accum_out=ssl[:, hh:hh + 1]), [rpsO[hh]], [rFs[7], r_sso])
                        A(lambda e, ssl=ssl: e.activation(out=ssl, in_=ssl, func=AF.Ln, scale=1.0 / 128, bias=EPS), [r_sso], [r_sso])
                        A(lambda e, ssl=ssl: e.activation(out=ssl, in_=ssl, func=AF.Exp, scale=-0.5), [r_sso], [r_sso])
                        for hh in range(2):
                            hs = slice(hh * 128, (hh + 1) * 128)
                            V(lambda e, hh=hh, hs=hs, Fs=Fs, ssl=ssl: e.scalar_tensor_tensor(out=Fs[8][:, hs], in0=psO[hh][:, 0:128],
                                                                                             scalar=ssl[:, hh:hh + 1], in1=go_bc[:],
                                                                                             op0=ALU.mult, op1=ALU.mult),
                              [rpsO[hh], r_sso, rgains], [rFs[8]])
                        G(lambda e, Fs=Fs, sl=sl, sz=sz: e.tensor_tensor(out=y_tok[:, sl, :], in0=Fs[8], in1=sz[:, sl, :], op=ALU.mult),
                          [rFs[8], rsz[sl]], [ry[sl]])
                        outproj_tile(i, sl, last, obanks=[(psS[0], rpsS[0]), (psS[0], rpsS[0])])
            if not glist and last_layer:
                for i in range(NT):
                    out_dmas.append(DMA("sync", lambda e, i=i: e.dma_start(out=out_d[i * 128:(i + 1) * 128, :], in_=x_tok[:, i, :]),
                                        r=[rx[i]]))
        if SCHED:
            if SCHED2:
                P.schedule2(SDELTA)
            else:
                P.schedule()
        P.emit(st, out_dmas)
    build_nc.stats = P.stats
    return nc


_CACHE = {}


def _get_nc(layers, groups):
    key = (tuple(layers), tuple(groups))
    if key not in _CACHE:
        _CACHE[key] = build_nc(layers, groups)
    return _CACHE[key]


def run(inputs, layers=(0, 1), groups=ALL_GROUPS, cores=8):
    nc = _get_nc(layers, groups)
    f = lambda a: np.ascontiguousarray(np.asarray(a))
    cst = make_consts()
    shared = {k: f(inputs[k]).astype(np.float32, copy=False) for k in
              ("norm_g", "w_in", "w_out", "moba_q_norm", "moba_k_norm", "hgrn_lb_logits", "hgrn_o_norm",
               "mem_norm_g", "w_mem_kv", "mem_q_norm", "mem_k_norm")}
    x = f(inputs["x"]); mem = f(inputs["mem"]); pos = f(inputs["positions"]).astype(np.int32, copy=False)
    in_maps = []
    for b in range(cores):
        m = dict(shared)
        m["x"] = x[b]
        m["mem"] = mem[b]
        m["pos"] = pos[b].reshape(16, 128)
        m["cst"] = cst
        in_maps.append(m)
    res = run_bass_kernel_spmd(nc, in_maps, core_ids=list(range(cores)))
    return np.stack([np.asarray(r["out"]) for r in res.results], axis=0)


def kernel(x, mem, positions, norm_g, w_in, w_out, moba_q_norm, moba_k_norm, hgrn_lb_logits,
           hgrn_o_norm, mem_norm_g, w_mem_kv, mem_q_norm, mem_k_norm):
    inputs = dict(x=x, mem=mem, positions=positions, norm_g=norm_g, w_in=w_in, w_out=w_out,
                  moba_q_norm=moba_q_norm, moba_k_norm=moba_k_norm, hgrn_lb_logits=hgrn_lb_logits,
                  hgrn_o_norm=hgrn_o_norm, mem_norm_g=mem_norm_g, w_mem_kv=w_mem_kv,
                  mem_q_norm=mem_q_norm, mem_k_norm=mem_k_norm)
    return run(inputs).astype(np.float32, copy=False)
```
